# Optimizing a Trainium2 kernel written in Bass

```python
import numpy as np
import jax
import jax.numpy as jnp
from jax import lax

D_MODEL = 1024
BATCH = 8
SEQ = 2048
DEPTH = 1

NSA_HEADS = 8
NSA_KV_GROUPS = 2
NSA_HEAD_DIM = 64
NSA_REP = NSA_HEADS // NSA_KV_GROUPS
NSA_BRANCHES = 3
CMP_STRIDE = 16
CMP_BLOCK = 2 * CMP_STRIDE
CMP_HIDDEN = 2 * NSA_HEAD_DIM
SLC_BLOCK = 64
SLC_TOPK = 8
WINDOW = 512
Q_BLOCK = 128
RET_HEADS = 4
RET_HEAD_DIM = 128
RET_CHUNK = 128
N_BRANCH = 2
BRANCH_WIDTH = NSA_HEADS * NSA_HEAD_DIM
D_FF = -(-8 * D_MODEL // (3 * 256)) * 256
EPS = 1e-6
NEG_INF = -1e9
FORCE_BONUS = 1e4
Q_W = NSA_HEADS * NSA_HEAD_DIM
KV_W = NSA_BRANCHES * 2 * NSA_KV_GROUPS * NSA_HEAD_DIM
NSA_GATE_W = NSA_HEADS * NSA_BRANCHES
RET_W = 4 * RET_HEADS * RET_HEAD_DIM
MERGE_W = N_BRANCH * D_MODEL
N_IN = Q_W + KV_W + NSA_GATE_W + RET_W + MERGE_W

kernel_name = 'hybrid_nsa_retention_block'


def rms_norm(x, g):
    xf = x.astype(jnp.float32)
    y = xf * lax.rsqrt(jnp.mean(xf * xf, axis=-1, keepdims=True) + EPS)
    return (y * g.astype(jnp.float32)).astype(x.dtype)


def alibi_slopes(n):
    return jnp.asarray(2.0 ** (-8.0 * (np.arange(n) + 1) / n), dtype=jnp.float32)


def masked_softmax(s, mask):
    s = jnp.where(mask, s, NEG_INF)
    p = jax.nn.softmax(s, axis=-1)
    return jnp.where(mask, p, 0.0)


def compress_blocks(t, pe, w1, w2):
    b, s, g, d = t.shape
    chunks = t.reshape(b, s // CMP_STRIDE, CMP_STRIDE, g, d)
    blocks = jnp.concatenate([chunks[:, :-1], chunks[:, 1:]], axis=2)
    blocks = blocks + pe[None, None, :, None, :]
    hid = jax.nn.gelu(jnp.einsum('bnlgd,ldf->bngf', blocks, w1))
    return jnp.einsum('bngf,fd->bngd', hid, w2)


def nsa_attention(q, kv, gate_logits, q_g, k_g, cmp_pe, cmp_w1, cmp_w2):
    b, s, h, d = q.shape
    G, R = NSA_KV_GROUPS, NSA_REP
    f32 = jnp.float32
    scale = d ** -0.5
    slopes = alibi_slopes(h).reshape(G, R)
    qg = rms_norm(q, q_g).reshape(b, s, G, R, d)
    k_cmp, v_cmp = kv[:, :, 0, 0], kv[:, :, 0, 1]
    k_slc, v_slc = kv[:, :, 1, 0], kv[:, :, 1, 1]
    k_win, v_win = kv[:, :, 2, 0], kv[:, :, 2, 1]
    t_pos = jnp.arange(s)

    kc = rms_norm(compress_blocks(k_cmp, cmp_pe[0], cmp_w1[0], cmp_w2[0]), k_g[0])
    vc = compress_blocks(v_cmp, cmp_pe[1], cmp_w1[1], cmp_w2[1])
    n_cmp = kc.shape[1]
    c_end = jnp.arange(n_cmp) * CMP_STRIDE + CMP_BLOCK - 1
    c_dist = (t_pos[:, None] - c_end[None, :]).astype(f32)
    sc = jnp.einsum('btgrd,bngd->bgrtn', qg, kc).astype(f32) * scale - slopes[:, :, None, None] * c_dist
    p_cmp = masked_softmax(sc, c_dist >= 0)
    o_cmp = jnp.einsum('bgrtn,bngd->btgrd', p_cmp.astype(vc.dtype), vc)

    n_slc = s // SLC_BLOCK
    cs = np.arange(n_cmp) * CMP_STRIDE
    ss = np.arange(n_slc) * SLC_BLOCK
    overlap = (cs[:, None] <= ss[None, :] + SLC_BLOCK - 1) & (cs[:, None] + CMP_BLOCK - 1 >= ss[None, :])
    overlap = jnp.asarray(overlap, f32)
    imp = jnp.einsum('bgtn,nj->bgtj', p_cmp.sum(axis=2), overlap)
    j = jnp.arange(n_slc)[None, :]
    cur = (t_pos // SLC_BLOCK)[:, None]
    forced = (j == 0) | (j == cur) | (j == cur - 1)
    blk_valid = j * SLC_BLOCK <= t_pos[:, None]
    imp = jnp.where(blk_valid, imp + jnp.where(forced, FORCE_BONUS, 0.0), NEG_INF)
    n_top = min(SLC_TOPK, n_slc)
    _, sel_idx = lax.top_k(imp, n_top)
    n_sel = n_top * SLC_BLOCK

    ks_blk = rms_norm(k_slc, k_g[1]).reshape(b, n_slc, SLC_BLOCK, G, d).transpose(0, 3, 1, 2, 4)
    vs_blk = v_slc.reshape(b, n_slc, SLC_BLOCK, G, d).transpose(0, 3, 1, 2, 4)
    pad = ((0, 0), (WINDOW, 0), (0, 0), (0, 0))
    kw_pad = jnp.pad(rms_norm(k_win, k_g[2]), pad)
    vw_pad = jnp.pad(v_win, pad)
    gather_blocks = jax.vmap(jax.vmap(lambda blk, idx: blk[idx]))
    offs = jnp.arange(SLC_BLOCK)

    def query_block(c):
        s0 = c * Q_BLOCK
        tq = s0 + jnp.arange(Q_BLOCK)
        qc = lax.dynamic_slice_in_dim(qg, s0, Q_BLOCK, axis=1)
        idx = lax.dynamic_slice_in_dim(sel_idx, s0, Q_BLOCK, axis=2)
        ksel = gather_blocks(ks_blk, idx).reshape(b, G, Q_BLOCK, n_sel, d)
        vsel = gather_blocks(vs_blk, idx).reshape(b, G, Q_BLOCK, n_sel, d)
        kpos = (idx[..., None] * SLC_BLOCK + offs).reshape(b, G, Q_BLOCK, n_sel)
        sdist = (tq[None, None, :, None] - kpos).astype(f32)
        s_sel = (jnp.einsum('btgrd,bgtld->bgrtl', qc, ksel).astype(f32) * scale
                 - slopes[None, :, :, None, None] * sdist[:, :, None])
        p = masked_softmax(s_sel, (sdist >= 0)[:, :, None])
        o_slc = jnp.einsum('bgrtl,bgtld->btgrd', p.astype(vsel.dtype), vsel)
        kwin = lax.dynamic_slice_in_dim(kw_pad, s0, Q_BLOCK + WINDOW, axis=1)
        vwin = lax.dynamic_slice_in_dim(vw_pad, s0, Q_BLOCK + WINDOW, axis=1)
        wpos = s0 - WINDOW + jnp.arange(Q_BLOCK + WINDOW)
        wdist = tq[:, None] - wpos[None, :]
        wvalid = (wdist >= 0) & (wdist < WINDOW) & (wpos[None, :] >= 0)
        s_win = (jnp.einsum('btgrd,blgd->bgrtl', qc, kwin).astype(f32) * scale
                 - slopes[:, :, None, None] * wdist.astype(f32))
        p = masked_softmax(s_win, wvalid)
        o_win = jnp.einsum('bgrtl,blgd->btgrd', p.astype(vwin.dtype), vwin)
        return o_slc, o_win

    o_slc, o_win = lax.map(query_block, jnp.arange(s // Q_BLOCK))
    o_slc = jnp.moveaxis(o_slc, 0, 1).reshape(b, s, G, R, d)
    o_win = jnp.moveaxis(o_win, 0, 1).reshape(b, s, G, R, d)
    gate = jax.nn.sigmoid(gate_logits.astype(f32)).reshape(b, s, G, R, NSA_BRANCHES, 1)
    o = gate[..., 0, :] * o_cmp + gate[..., 1, :] * o_slc + gate[..., 2, :] * o_win
    return o.reshape(b, s, h * d).astype(q.dtype)


def retention(q, k, v, g, gn_g):
    b, s, h, d = q.shape
    f32 = jnp.float32
    C = RET_CHUNK
    n_ch = s // C
    log_gamma = jnp.asarray(np.log(1.0 - 2.0 ** (-5.0 - np.arange(h))), f32)
    i = jnp.arange(C, dtype=f32)
    rel = i[:, None] - i[None, :]
    inner_decay = jnp.where(rel >= 0, jnp.exp(jnp.maximum(rel, 0.0)[None] * log_gamma[:, None, None]), 0.0)
    q_decay = jnp.exp((i[:, None] + 1.0) * log_gamma[None, :])
    k_decay = jnp.exp((C - 1.0 - i[:, None]) * log_gamma[None, :])
    chunk_decay = jnp.exp(C * log_gamma)

    def to_chunks(t):
        return t.astype(f32).reshape(b, n_ch, C, h, t.shape[-1]).swapaxes(0, 1)

    qs, ks, vs = to_chunks(q), to_chunks(k * d ** -0.5), to_chunks(v)

    def step(state, inp):
        qc, kc, vc = inp
        att = jnp.einsum('bihd,bjhd->bhij', qc, kc) * inner_decay[None]
        o = (jnp.einsum('bhij,bjhe->bihe', att, vc)
             + jnp.einsum('bihd,bhde->bihe', qc * q_decay[None, :, :, None], state))
        state = (state * chunk_decay[None, :, None, None]
                 + jnp.einsum('bjhd,bjhe->bhde', kc * k_decay[None, :, :, None], vc))
        return state, o

    state0 = jnp.zeros((b, h, d, v.shape[-1]), f32)
    _, o = lax.scan(step, state0, (qs, ks, vs))
    o = o.swapaxes(0, 1).reshape(b, s, h, v.shape[-1])
    mu = jnp.mean(o, axis=-1, keepdims=True)
    var = jnp.mean(jnp.square(o - mu), axis=-1, keepdims=True)
    on = (o - mu) * lax.rsqrt(var + EPS) * gn_g.astype(f32)
    out = jax.nn.silu(g.astype(f32)).reshape(b, s, -1) * on.reshape(b, s, -1)
    return out.astype(q.dtype)


def setup_inputs(seed: int = 0) -> dict:
    key = jax.random.key(seed)
    ks = jax.random.split(key, 16)
    f32 = jnp.float32

    def nrm(k, shape, scale):
        return jax.random.normal(k, shape, f32) * scale

    def gain(k, shape):
        return 1.0 + 0.01 * jax.random.normal(k, shape, f32)

    L, dh = DEPTH, NSA_HEAD_DIM
    return {
        'x': jax.random.normal(ks[0], (BATCH, SEQ, D_MODEL), f32),
        'norm1_g': gain(ks[1], (L, D_MODEL)),
        'w_in': nrm(ks[2], (L, D_MODEL, N_IN), D_MODEL ** -0.5),
        'nsa_q_norm': gain(ks[3], (L, dh)),
        'nsa_k_norm': gain(ks[4], (L, NSA_BRANCHES, dh)),
        'cmp_pe': nrm(ks[5], (L, 2, CMP_BLOCK, dh), 0.1),
        'cmp_w1': nrm(ks[6], (L, 2, CMP_BLOCK, dh, CMP_HIDDEN), (CMP_BLOCK * dh) ** -0.5),
        'cmp_w2': nrm(ks[7], (L, 2, CMP_HIDDEN, dh), CMP_HIDDEN ** -0.5),
        'ret_gn_g': gain(ks[8], (L, RET_HEADS, RET_HEAD_DIM)),
        'w_branch': nrm(ks[9], (L, N_BRANCH, BRANCH_WIDTH, D_MODEL), BRANCH_WIDTH ** -0.5),
        'w_out': nrm(ks[10], (L, D_MODEL, D_MODEL), D_MODEL ** -0.5),
        'norm2_g': gain(ks[11], (L, D_MODEL)),
        'ffn_w_gate': nrm(ks[12], (L, D_MODEL, D_FF), D_MODEL ** -0.5),
        'ffn_w_up': nrm(ks[13], (L, D_MODEL, D_FF), D_MODEL ** -0.5),
        'ffn_w_down': nrm(ks[14], (L, D_FF, D_MODEL), D_FF ** -0.5),
    }


def reference(x, norm1_g, w_in, nsa_q_norm, nsa_k_norm, cmp_pe, cmp_w1, cmp_w2, ret_gn_g,
              w_branch, w_out, norm2_g, ffn_w_gate, ffn_w_up, ffn_w_down):
    b, s, _ = x.shape
    splits = [Q_W, Q_W + KV_W, Q_W + KV_W + NSA_GATE_W, Q_W + KV_W + NSA_GATE_W + RET_W]
    for l in range(DEPTH):
        h = rms_norm(x, norm1_g[l])
        z = jnp.einsum('bsd,dn->bsn', h, w_in[l])
        zq, zkv, zg, zr, zm = jnp.split(z, splits, axis=-1)
        y_nsa = nsa_attention(
            zq.reshape(b, s, NSA_HEADS, NSA_HEAD_DIM),
            zkv.reshape(b, s, NSA_BRANCHES, 2, NSA_KV_GROUPS, NSA_HEAD_DIM),
            zg.reshape(b, s, NSA_HEADS, NSA_BRANCHES),
            nsa_q_norm[l], nsa_k_norm[l], cmp_pe[l], cmp_w1[l], cmp_w2[l])
        zr = zr.reshape(b, s, 4, RET_HEADS, RET_HEAD_DIM)
        y_ret = retention(zr[:, :, 0], zr[:, :, 1], zr[:, :, 2], zr[:, :, 3], ret_gn_g[l])
        u = jnp.einsum('nbsc,ncd->bsnd', jnp.stack([y_nsa, y_ret]), w_branch[l])
        gates = jax.nn.sigmoid(zm.reshape(b, s, N_BRANCH, D_MODEL).astype(jnp.float32))
        mixed = jnp.sum(gates * u, axis=2).astype(x.dtype)
        x = x + jnp.einsum('bsd,de->bse', mixed, w_out[l])
        h = rms_norm(x, norm2_g[l])
        ff = jax.nn.silu(h @ ffn_w_gate[l]) * (h @ ffn_w_up[l])
        x = x + ff @ ffn_w_down[l]
    return x
```

```python
import numpy as np
from contextlib import ExitStack
import concourse.bass as bass
import concourse.mybir as mybir
from concourse.bass_utils import run_bass_kernel_spmd

F32 = mybir.dt.float32
BF16 = mybir.dt.bfloat16
AF = mybir.ActivationFunctionType
ALU = mybir.AluOpType
AX = mybir.AxisListType

S = 2048
D = 1024
NT = 16
KC = 8
N_IN = 5400
DFF = 2816
NFB = 22
EPS = 1e-6
SEM_LIMIT = 24000
import os as _os
CUT = int(_os.environ.get('P1B_CUT', '99'))
CUTC = int(_os.environ.get('P1C_CUT', '99'))
SUBC = int(_os.environ.get('P1C_SUB', '99'))
NEG = -30000.0

C_Q = 0
C_KV = 512
C_G = 1280
C_R = 1304
C_M = 3352


class Buf:
    __slots__ = ("name", "w", "r", "excl")

    def __init__(self, name):
        self.name = name
        self.w = None
        self.r = []
        self.excl = False


class SemW:
    __slots__ = ("h",)

    def __init__(self, h):
        self.h = h


class Slot:
    __slots__ = ("sem", "val")

    def __init__(self, sem):
        self.sem = sem
        self.val = 0


class Q:
    def __init__(self, name, eng):
        self.name = name
        self.eng = eng
        self.sem = None
        self.count = 0
        self.waited = {}
        self.ring = []
        self.ri = 0
        self.pending = False


class T:
    def __init__(self, t, name, nb=1):
        self.t = t
        self.bs = [Buf(f"{name}{i}") for i in range(nb)]

    @property
    def b(self):
        return self.bs[0]


class KB:
    def __init__(self, nc, es):
        self.nc = nc
        self.es = es
        self.nsem = 0
        self.pe = self.mkq("pe", nc.tensor)
        self.act = self.mkq("act", nc.scalar)
        self.dve = self.mkq("dve", nc.vector)
        self.pool = self.mkq("pool", nc.gpsimd)
        self.sp = self.mkq("sp", nc.sync)
        self.qs = [self.pe, self.act, self.dve, self.pool, self.sp]
        for q, n in ((self.sp, 16), (self.pool, 8), (self.act, 4)):
            q.ring = [Slot(self.new_sem(f"{q.name}_d{i}")) for i in range(n)]
        self.out_toks = []

    def new_sem(self, name):
        self.nsem += 1
        return SemW(self.es.enter_context(self.nc.semaphore(f"{name}_{self.nsem}")))

    def mkq(self, name, eng):
        q = Q(name, eng)
        q.sem = self.new_sem(name)
        return q

    def wait(self, q, tok):
        sw, val = tok[0], tok[1]
        if q.waited.get(sw, 0) >= val:
            return
        q.eng.wait_ge(sw.h, val)
        q.waited[sw] = val

    def _dep(self, q, tok, raw, force=False):
        if tok[2] is q and q is self.pe and not force:
            return
        self.wait(q, tok)

    def _deps(self, q, reads, writes, force=False):
        for b in reads:
            if b.w is not None:
                self._dep(q, b.w, True, force)
            if b.excl:
                for t in b.r:
                    if t[2] is not q:
                        self._dep(q, t, False, force)
        for b in writes:
            if b.w is not None:
                self._dep(q, b.w, False, force)
            for t in b.r:
                self._dep(q, t, False, force)

    def _record(self, tok, reads, writes):
        for b in reads:
            if tok[2] is not None:
                b.r = [t for t in b.r if t[2] is not tok[2]]
            b.r.append(tok)
        for b in writes:
            b.w = tok
            b.r = []

    def op(self, q, fn, reads=(), writes=(), inc=True):
        self._deps(q, reads, writes)
        ins = fn()
        if inc:
            if q.count >= SEM_LIMIT and not q.pending:
                q.sem = self.new_sem(q.name)
                q.count = 0
            ins.then_inc(q.sem.h, 1)
            q.count += 1
            q.pending = False
            tok = (q.sem, q.count, q)
        else:
            q.pending = True
            tok = (q.sem, q.count + 1, q)
        self._record(tok, reads, writes)
        return ins

    def dma(self, q, out, in_, reads=(), writes=(), is_out=False):
        self._deps(q, reads, writes, force=True)
        slot = q.ring[q.ri % len(q.ring)]
        q.ri += 1
        if slot.val > 0:
            self.wait(q, (slot.sem, slot.val))
        if slot.val >= SEM_LIMIT:
            slot.sem = self.new_sem(q.name + "_d")
            slot.val = 0
        ins = q.eng.dma_start(out=out, in_=in_)
        ins.then_inc(slot.sem.h, 16)
        slot.val += 16
        tok = (slot.sem, slot.val, None)
        self._record(tok, reads, writes)
        if is_out:
            self.out_toks.append(tok)
        return tok

    def barrier(self):
        toks = []
        for o in self.qs:
            if o.count > 0:
                toks.append((o.sem, o.count, o))
            for sl in o.ring:
                if sl.val > 0:
                    toks.append((sl.sem, sl.val, None))
        for q in self.qs:
            for t in toks:
                if t[2] is q:
                    continue
                self.wait(q, t)

    def finish(self):
        for t in self.out_toks:
            self.wait(self.sp, t)


class _Stop(Exception):
    pass


def build_nc(debug=None, stop=None):
    nc = bass.Bass("TRN2", target_bir_lowering=False)

    def din(name, shape):
        return nc.dram_tensor(name, list(shape), F32, kind="ExternalInput").ap()

    x = din("x", [S, D])
    norm1_g = din("norm1_g", [1, D])
    w_in = din("w_in", [1, D, N_IN])
    nsa_q_norm = din("nsa_q_norm", [1, 64])
    nsa_k_norm = din("nsa_k_norm", [1, 3, 64])
    cmp_pe = din("cmp_pe", [1, 2, 32, 64])
    cmp_w1 = din("cmp_w1", [1, 2, 32, 64, 128])
    cmp_w2 = din("cmp_w2", [1, 2, 128, 64])
    ret_gn_g = din("ret_gn_g", [1, 4, 128])
    w_branch = din("w_branch", [1, 2, 512, D])
    w_out = din("w_out", [1, D, D])
    norm2_g = din("norm2_g", [1, D])
    ffn_w_gate = din("ffn_w_gate", [1, D, DFF])
    ffn_w_up = din("ffn_w_up", [1, D, DFF])
    ffn_w_down = din("ffn_w_down", [1, DFF, D])
    out = nc.dram_tensor("out", [S, D], F32, kind="ExternalOutput").ap()
    xmid = nc.dram_tensor("xmid", [S, D], F32, kind="Internal").ap()
    dbg = {}
    if debug:
        for name, shape in debug.items():
            dbg[name] = nc.dram_tensor("dbg_" + name, list(shape), F32, kind="ExternalOutput").ap()

    w_in_v = w_in[0].rearrange("(k p) n -> p k n", p=128)

    try:
        with ExitStack() as es:
            kb = KB(nc, es)

            def chk(n):
                if stop is not None and n >= stop:
                    kb.barrier()
                    kb.finish()
                    raise _Stop()
            PE, ACT, DVE, POOL, SP = kb.pe, kb.act, kb.dve, kb.pool, kb.sp
            V, A, G, TE = nc.vector, nc.scalar, nc.gpsimd, nc.tensor

            def sb(scope, name, shape, dt, nb=1):
                return T(scope.enter_context(nc.sbuf_tensor(name, list(shape), dt)), name, nb)

            PS = [T(es.enter_context(nc.psum_tensor(f"ps{i}", [128, 512], F32)), f"ps{i}") for i in range(8)]
            for p_ in PS:
                p_.b.excl = True

            def pf(i):
                return PS[i].t[:]

            def pb(i):
                return PS[i].t[:].bitcast(BF16)

            ident_f = sb(es, "ident_f", [128, 128], F32)
            ident_b = sb(es, "ident_b", [128, 128], BF16)
            ones_f = sb(es, "ones_f", [128, 512], F32)
            hT = sb(es, "hT", [128, KC, S], BF16, NT)
            g2bc = sb(es, "g2bc", [128, D], F32)
            xt = [sb(es, f"xt{i}", [128, D], F32) for i in range(2)]
            stat = sb(es, "stat", [128, NT, 4], F32, NT)
            hb = [sb(es, f"hb{i}", [128, D], BF16) for i in range(2)]
            junk = sb(es, "junk", [128, D], BF16)

            kb.op(POOL, lambda: G.memset(ones_f.t[:], 1.0), writes=[ones_f.b])
            kb.op(POOL, lambda: G.affine_select(out=ident_f.t[:], in_=ones_f.t[:, 0:128], pattern=[[1, 128]],
                                                compare_op=ALU.is_equal, fill=0.0, base=0, channel_multiplier=-1),
                  reads=[ones_f.b], writes=[ident_f.b])
            kb.op(DVE, lambda: V.tensor_copy(out=ident_b.t[:], in_=ident_f.t[:]), reads=[ident_f.b], writes=[ident_b.b])
            kb.dma(SP, g2bc.t[:], norm2_g[0:1, :].broadcast_to([128, D]), writes=[g2bc.b])

            def rstd_from_ss(ss_ap, ms_ap, sd_ap, rs_ap, n, bufs):
                kb.op(DVE, lambda: V.tensor_scalar(out=ms_ap, in0=ss_ap, scalar1=1.0 / n, scalar2=EPS,
                                                   op0=ALU.mult, op1=ALU.add), reads=bufs, writes=bufs)
                kb.op(ACT, lambda: A.activation(out=sd_ap, in_=ms_ap, func=AF.Sqrt), reads=bufs, writes=bufs)
                kb.op(DVE, lambda: V.reciprocal(out=rs_ap, in_=sd_ap), reads=bufs, writes=bufs)

            def transposes(src_aps, bank, reads):
                pbv = pb(bank)
                n = len(src_aps)
                for i, ap in enumerate(src_aps):
                    kb.op(PE, lambda ap=ap, i=i: TE.transpose(out=pbv[:, i * 128:(i + 1) * 128], in_=ap, identity=ident_b.t[:]),
                          reads=list(reads) + [ident_b.b], writes=[PS[bank].b], inc=(i == n - 1))

            def norm_to_hT(src, c, gbc, sidx):
                sbuf_ = [stat.bs[c]]
                st = stat.t
                kb.op(ACT, lambda: A.activation(out=junk.t[:], in_=src.t[:], func=AF.Square, accum_out=st[:, c, 0:1]),
                      reads=[src.b], writes=[junk.b] + sbuf_)
                rstd_from_ss(st[:, c, 0:1], st[:, c, 1:2], st[:, c, 2:3], st[:, c, 3:4], D, sbuf_)
                h = hb[sidx % 2]
                kb.op(DVE, lambda: V.scalar_tensor_tensor(out=h.t[:], in0=src.t[:], scalar=st[:, c, 3:4], in1=gbc.t[:],
                                                          op0=ALU.mult, op1=ALU.mult),
                      reads=[src.b, gbc.b] + sbuf_, writes=[h.b])
                transposes([h.t[:, k * 128:(k + 1) * 128] for k in range(KC)], 7, [h.b])
                kb.op(ACT, lambda: A.copy(out=hT.t[:, :, c * 128:(c + 1) * 128],
                                          in_=pb(7).rearrange("p (a b) -> p a b", b=128)),
                      reads=[PS[7].b], writes=[hT.bs[c]])

            def load_w(dst, src_ap, q=None):
                kb.dma(q or POOL, dst.t[:], src_ap, writes=[dst.b])

            def dbg_store(name, src_ap, rows, reads):
                if name in dbg:
                    kb.dma(SP, dbg[name][rows], src_ap, reads=reads, is_out=True)

            def phase_1c():
                with ExitStack() as s3:
                    wq = sb(s3, "wq", [128, KC, 512], BF16)
                    load_w(wq, w_in_v[:, :, C_Q:C_Q + 512])
                    wg = sb(s3, "wg", [128, KC, 24], BF16)
                    load_w(wg, w_in_v[:, :, C_G:C_G + 24])
                    sqq = [sb(s3, f"sqq{i}", [128, 512], F32) for i in range(2)]
                    qst = sb(s3, "qst", [128, NT, 32], F32, NT)
                    tmpq = [sb(s3, f"tmpq{i}", [128, 8, 64], F32) for i in range(2)]
                    qaug = [sb(s3, f"qaug{i}", [128, 8, 128], BF16) for i in range(2)]
                    qT = [sb(s3, f"qT{i}", [128, 8, 128], BF16) for i in range(2)]
                    qT2 = [sb(s3, f"qT2{i}", [128, 8, 128], BF16) for i in range(2)]
                    gate = [sb(s3, f"gate{i}", [128, 24], F32) for i in range(2)]
                    PcT = [sb(s3, f"PcT{i}", [128, 512], BF16) for i in range(2)]
                    scl = [sb(s3, f"scl{i}", [128, 512], F32) for i in range(2)]
                    NPT = 24
                    PT = [sb(s3, f"PT{i}", [128, 512], BF16) for i in range(NPT)]
                    num = [sb(s3, f"num{i}", [128, 3, 8, 64], F32) for i in range(2)]
                    den = [sb(s3, f"den{i}", [128, 3, 8], F32) for i in range(2)]
                    rdc = sb(s3, "rdc", [128, 8], F32)
                    impn = sb(s3, "impn", [128, 8, 32], F32)
                    imp = sb(s3, "imp", [128, 2, 32], F32)
                    top8 = sb(s3, "top8", [128, 2, 8], F32)
                    rd = sb(s3, "rd", [128, 3, 8], F32)
                    coef = sb(s3, "coef", [128, 3, 8], F32)
                    oacc = sb(s3, "oacc", [128, 8, 64], F32)
                    otmp = sb(s3, "otmp", [128, 8, 64], F32)
                    ytok = [sb(s3, f"ytok{i}", [128, 512], BF16) for i in range(2)]
                    ydbg = sb(s3, "ydbg", [128, 512], F32) if "y_nsa" in dbg else None
                    for i in range(2):
                        kb.op(POOL, lambda i=i: G.memset(qaug[i].t[:], 0.0), writes=[qaug[i].b])
                    cnt = [0, 0]

                    def pvv(g):
                        return pf(6 + g).rearrange("p (h c) -> p h c", h=4)

                    def branch(c, g, i2, br, KT, VV, kts, qsrc):
                        nu, de = num[i2], den[i2]
                        pts = []
                        for kt in kts:
                            scb = 4 + (cnt[0] % 2)
                            cnt[0] += 1
                            kb.op(PE, lambda kt=kt, scb=scb: TE.matmul(pf(scb), lhsT=KT.t[:, g, kt * 128:(kt + 1) * 128],
                                                                      rhs=qsrc.t[:, 4 * g:4 * g + 4, :], start=True, stop=True),
                                  reads=[KT.bs[kt], qsrc.b], writes=[PS[scb].b])
                            pt = PT[cnt[1] % NPT]
                            cnt[1] += 1
                            kb.op(ACT, lambda pt=pt, scb=scb: A.activation(out=pt.t[:], in_=pf(scb), func=AF.Exp),
                                  reads=[PS[scb].b], writes=[pt.b])
                            if kt == c:
                                kb.op(POOL, lambda pt=pt: G.tensor_tensor(out=pt.t[:], in0=pt.t[:], in1=dmask.t[:].rearrange("p a b -> p (a b)"), op=ALU.mult),
                                      reads=[pt.b, dmask.b], writes=[pt.b])
                            elif br == 2 and kt == c - 4:
                                kb.op(POOL, lambda pt=pt: G.tensor_tensor(out=pt.t[:], in0=pt.t[:], in1=tmask.t[:].rearrange("p a b -> p (a b)"), op=ALU.mult),
                                      reads=[pt.b, tmask.b], writes=[pt.b])
                            pts.append(pt)
                        n = len(kts)
                        for hh in range(4):
                            for j, kt in enumerate(kts):
                                kb.op(PE, lambda hh=hh, j=j, kt=kt: TE.matmul(pvv(g)[:, hh, 0:65], lhsT=pts[j].t[:, hh * 128:(hh + 1) * 128],
                                                                              rhs=VV.t[:, kt, g, :], start=(j == 0), stop=(j == n - 1)),
                                      reads=[pts[j].b, VV.bs[kt]], writes=[PS[6 + g].b], inc=(j == n - 1))
                        kb.op(ACT, lambda: A.copy(out=nu.t[:, br, 4 * g:4 * g + 4, :], in_=pvv(g)[:, :, 0:64]),
                              reads=[PS[6 + g].b], writes=[nu.b])
                        kb.op(DVE, lambda: V.tensor_scalar(out=de.t[:, br, 4 * g:4 * g + 4], in0=pvv(g)[:, :, 64], scalar1=1e-30, scalar2=None, op0=ALU.max),
                              reads=[PS[6 + g].b], writes=[de.b])

                    for c in range(NT):
                        i2 = c % 2
                        tok = slice(c * 128, (c + 1) * 128)
                        bZ = i2
                        for k in range(KC):
                            kb.op(PE, lambda k=k: TE.matmul(pf(bZ), lhsT=hT.t[:, k, tok], rhs=wq.t[:, k, :], start=(k == 0), stop=(k == KC - 1)),
                                  reads=[hT.bs[c], wq.b], writes=[PS[bZ].b], inc=(k == KC - 1))
                        for k in range(KC):
                            kb.op(PE, lambda k=k: TE.matmul(pf(2)[:, 0:24], lhsT=hT.t[:, k, tok], rhs=wg.t[:, k, :], start=(k == 0), stop=(k == KC - 1)),
                                  reads=[hT.bs[c], wg.b], writes=[PS[2].b], inc=(k == KC - 1))
                        sq = sqq[i2]
                        kb.op(ACT, lambda: A.activation(out=sq.t[:], in_=pf(bZ), func=AF.Square), reads=[PS[bZ].b], writes=[sq.b])
                        qs = qst.t
                        qsb = [qst.bs[c]]
                        kb.op(DVE, lambda: V.tensor_reduce(out=qs[:, c, 0:8], in_=sq.t[:].rearrange("p (a b) -> p a b", b=64), axis=AX.X, op=ALU.add),
                              reads=[sq.b], writes=qsb)
                        rstd_from_ss(qs[:, c, 0:8], qs[:, c, 8:16], qs[:, c, 16:24], qs[:, c, 24:32], 64, qsb)
                        tq = tmpq[i2]
                        qa = qaug[i2]
                        kb.op(DVE, lambda: V.tensor_tensor(out=tq.t[:], in0=pf(bZ).rearrange("p (a b) -> p a b", b=64),
                                                           in1=qs[:, c, 24:32].unsqueeze(2).broadcast_to([128, 8, 64]), op=ALU.mult),
                              reads=[PS[bZ].b] + qsb, writes=[tq.b])
                        kb.op(DVE, lambda: V.tensor_tensor(out=qa.t[:, :, 0:64], in0=tq.t[:], in1=gq.t[:].unsqueeze(1).broadcast_to([128, 8, 64]), op=ALU.mult),
                              reads=[tq.b, gq.b], writes=[qa.b])
                        gt = gate[i2]
                        kb.op(ACT, lambda: A.activation(out=gt.t[:], in_=pf(2)[:, 0:24], func=AF.Sigmoid), reads=[PS[2].b], writes=[gt.b])
                        kb.op(POOL, lambda: G.tensor_copy(out=qa.t[:, :, 96:100], in_=QAL.t[:, c, :, :]), reads=[QAL.b], writes=[qa.b])
                        transposes([qa.t[:, h, :] for h in range(8)], 3, [qa.b])
                        q1 = qT[i2]
                        kb.op(ACT, lambda: A.copy(out=q1.t[:], in_=pb(3).rearrange("p (a b) -> p a b", b=128)), reads=[PS[3].b], writes=[q1.b])
                        nu, de = num[i2], den[i2]
                        if CUTC >= 2:
                            for g in range(2):
                                scb = 4 + (cnt[0] % 2)
                                cnt[0] += 1
                                pc = PcT[g]
                                sc_ = scl[g]
                                kb.op(PE, lambda g=g, scb=scb: TE.matmul(pf(scb)[0:127, :], lhsT=KcT.t[:, g, 0:127], rhs=q1.t[:, 4 * g:4 * g + 4, :], start=True, stop=True),
                                      reads=[KcT.b, q1.b], writes=[PS[scb].b])
                                if SUBC >= 2:
                                    kb.op(DVE, lambda sc_=sc_, scb=scb: V.scalar_tensor_tensor(out=sc_.t[0:127, :].rearrange("p (a b) -> p a b", b=128),
                                                                                               in0=pf(scb)[0:127, :].rearrange("p (a b) -> p a b", b=128), scalar=60.0,
                                                                                               in1=cmask.t[0:127, tok].unsqueeze(1).broadcast_to([127, 4, 128]),
                                                                                               op0=ALU.min, op1=ALU.add),
                                          reads=[PS[scb].b, cmask.b], writes=[sc_.b])
                                if SUBC >= 3:
                                    kb.op(ACT, lambda pc=pc, sc_=sc_: A.activation(out=pc.t[0:127, :], in_=sc_.t[0:127, :], func=AF.Exp),
                                          reads=[sc_.b], writes=[pc.b])
                                if SUBC >= 4:
                                    for hh in range(4):
                                        kb.op(PE, lambda g=g, hh=hh, pc=pc: TE.matmul(pvv(g)[:, hh, 0:97], lhsT=pc.t[0:127, hh * 128:(hh + 1) * 128],
                                                                                      rhs=Vc.t[0:127, g, :], start=True, stop=True),
                                              reads=[pc.b, Vc.b], writes=[PS[6 + g].b], inc=(hh == 3))
                                if SUBC >= 5:
                                    kb.op(ACT, lambda g=g: A.copy(out=nu.t[:, 0, 4 * g:4 * g + 4, :], in_=pvv(g)[:, :, 0:64]), reads=[PS[6 + g].b], writes=[nu.b])
                                if SUBC >= 6:
                                    kb.op(DVE, lambda g=g: V.tensor_scalar(out=de.t[:, 0, 4 * g:4 * g + 4], in0=pvv(g)[:, :, 64], scalar1=1e-30, scalar2=None, op0=ALU.max), reads=[PS[6 + g].b], writes=[de.b])
                                if SUBC >= 7:
                                    kb.op(DVE, lambda g=g: V.tensor_scalar(out=rdc.t[:, 4 * g:4 * g + 4], in0=de.t[:, 0, 4 * g:4 * g + 4], scalar1=1e-30, scalar2=None, op0=ALU.max),
                                          reads=[de.b], writes=[rdc.b])
                                    kb.op(DVE, lambda g=g: V.reciprocal(out=rdc.t[:, 4 * g:4 * g + 4], in_=rdc.t[:, 4 * g:4 * g + 4]), reads=[rdc.b], writes=[rdc.b])
                                if SUBC >= 8:
                                    kb.op(DVE, lambda g=g: V.tensor_tensor(out=impn.t[:, 4 * g:4 * g + 4, :], in0=pvv(g)[:, :, 65:97],
                                                                           in1=rdc.t[:, 4 * g:4 * g + 4].unsqueeze(2).broadcast_to([128, 4, 32]), op=ALU.mult),
                                          reads=[PS[6 + g].b, rdc.b], writes=[impn.b])
                        if CUTC >= 3:
                            kb.op(DVE, lambda: V.tensor_reduce(out=imp.t[:], in_=impn.t[:].rearrange("p (g r) j -> p g j r", g=2), axis=AX.X, op=ALU.add),
                                  reads=[impn.b], writes=[imp.b])
                            kb.op(DVE, lambda: V.tensor_tensor(out=imp.t[:], in0=imp.t[:], in1=addc.t[:, c:c + 1, :].broadcast_to([128, 2, 32]), op=ALU.add),
                                  reads=[imp.b, addc.b], writes=[imp.b])
                            for g in range(2):
                                kb.op(DVE, lambda g=g: V.max(out=top8.t[:, g, :], in_=imp.t[:, g, :]), reads=[imp.b], writes=[top8.b])
                            for g in range(2):
                                kb.op(DVE, lambda g=g: V.tensor_scalar(out=qa.t[:, 4 * g:4 * g + 4, 64:96],
                                                                       in0=imp.t[:, g:g + 1, :].broadcast_to([128, 4, 32]),
                                                                       scalar1=top8.t[:, g, 7:8], scalar2=NEG, op0=ALU.is_lt, op1=ALU.mult),
                                      reads=[imp.b, top8.b], writes=[qa.b])
                            transposes([qa.t[:, h, :] for h in range(8)], 3, [qa.b])
                            q2 = qT2[i2]
                            kb.op(ACT, lambda: A.copy(out=q2.t[:], in_=pb(3).rearrange("p (a b) -> p a b", b=128)), reads=[PS[3].b], writes=[q2.b])
                        if CUTC >= 4:
                            for g in range(2):
                                branch(c, g, i2, 1, KT_slc, V_slc, list(range(0, c + 1)), q2)
                                branch(c, g, i2, 2, KT_win, V_win, list(range(max(0, c - 4), c + 1)), q2)
                        if CUTC >= 5:
                            kb.op(DVE, lambda: V.tensor_scalar(out=rd.t[:], in0=de.t[:], scalar1=1e-30, scalar2=None, op0=ALU.max), reads=[de.b], writes=[rd.b])
                            kb.op(DVE, lambda: V.reciprocal(out=rd.t[:], in_=rd.t[:]), reads=[rd.b], writes=[rd.b])
                            kb.op(DVE, lambda: V.tensor_tensor(out=coef.t[:], in0=gt.t[:].rearrange("p (h b) -> p b h", b=3), in1=rd.t[:], op=ALU.mult),
                                  reads=[gt.b, rd.b], writes=[coef.b])
                            kb.op(DVE, lambda: V.tensor_tensor(out=oacc.t[:], in0=nu.t[:, 0], in1=coef.t[:, 0, :].unsqueeze(2).broadcast_to([128, 8, 64]), op=ALU.mult),
                                  reads=[nu.b, coef.b], writes=[oacc.b])
                            kb.op(DVE, lambda: V.tensor_tensor(out=otmp.t[:], in0=nu.t[:, 1], in1=coef.t[:, 1, :].unsqueeze(2).broadcast_to([128, 8, 64]), op=ALU.mult),
                                  reads=[nu.b, coef.b], writes=[otmp.b])
                            kb.op(POOL, lambda: G.tensor_tensor(out=oacc.t[:], in0=oacc.t[:], in1=otmp.t[:], op=ALU.add), reads=[oacc.b, otmp.b], writes=[oacc.b])
                            kb.op(DVE, lambda: V.tensor_tensor(out=otmp.t[:], in0=nu.t[:, 2], in1=coef.t[:, 2, :].unsqueeze(2).broadcast_to([128, 8, 64]), op=ALU.mult),
                                  reads=[nu.b, coef.b], writes=[otmp.b])
                            yt = ytok[i2]
                            kb.op(POOL, lambda: G.tensor_tensor(out=yt.t[:], in0=oacc.t[:].rearrange("p a b -> p (a b)"), in1=otmp.t[:].rearrange("p a b -> p (a b)"), op=ALU.add),
                                  reads=[oacc.b, otmp.b], writes=[yt.b])
                            if ydbg is not None:
                                kb.op(POOL, lambda: G.tensor_tensor(out=ydbg.t[:], in0=oacc.t[:].rearrange("p a b -> p (a b)"), in1=otmp.t[:].rearrange("p a b -> p (a b)"), op=ALU.add),
                                      reads=[oacc.b, otmp.b], writes=[ydbg.b])
                                dbg_store("y_nsa", ydbg.t[:], tok, [ydbg.b])
                            transposes([yt.t[:, k * 128:(k + 1) * 128] for k in range(4)], 3, [yt.b])
                            kb.op(ACT, lambda: A.copy(out=ynsaT.t[:, :, tok], in_=pb(3)[:, 0:512].rearrange("p (a b) -> p a b", b=128)),
                                  reads=[PS[3].b], writes=[ynsaT.bs[c]])

            def phase_1d():
                with ExitStack() as s4:
                    wr = sb(s4, "wr", [128, KC, 2048], BF16, 4)
                    for j in range(4):
                        kb.dma(POOL, wr.t[:, :, j * 512:(j + 1) * 512], w_in_v[:, :, C_R + j * 512:C_R + (j + 1) * 512], writes=[wr.bs[j]])
                    idT = sb(s4, "idT", [128, 4, 128], F32)
                    qdec = sb(s4, "qdec", [128, 4, 128], F32)
                    kdec = sb(s4, "kdec", [128, 4, 128], F32)
                    gn = sb(s4, "gn", [128, 512], F32)
                    eij = sb(s4, "eij", [128, 128], F32)
                    rowq = sb(s4, "rowq", [128, 128], F32)
                    rowk = sb(s4, "rowk", [128, 128], F32)
                    kb.dma(SP, gn.t[:], ret_gn_g.rearrange("o a b -> o (a b)").broadcast_to([128, 512]), writes=[gn.b])
                    kb.op(POOL, lambda: G.iota(eij.t[:], pattern=[[1, 128]], base=0, channel_multiplier=-1, allow_small_or_imprecise_dtypes=True), writes=[eij.b])
                    kb.op(POOL, lambda: G.iota(rowq.t[:], pattern=[[1, 128]], base=1, channel_multiplier=0, allow_small_or_imprecise_dtypes=True), writes=[rowq.b])
                    kb.op(POOL, lambda: G.iota(rowk.t[:], pattern=[[-1, 128]], base=127, channel_multiplier=0, allow_small_or_imprecise_dtypes=True), writes=[rowk.b])
                    lgs = [float(np.log(1.0 - 2.0 ** (-5.0 - h))) for h in range(4)]
                    cds = [float(np.exp(128.0 * np.float32(lg))) for lg in lgs]
                    for h in range(4):
                        kb.op(ACT, lambda h=h: A.activation(out=idT.t[:, h, :], in_=eij.t[:], func=AF.Exp, scale=lgs[h]), reads=[eij.b], writes=[idT.b])
                        kb.op(POOL, lambda h=h: G.affine_select(out=idT.t[:, h, :], in_=idT.t[:, h, :], pattern=[[1, 128]], compare_op=ALU.is_ge, fill=0.0,
                                                                base=0, channel_multiplier=-1), reads=[idT.b], writes=[idT.b])
                        kb.op(ACT, lambda h=h: A.activation(out=qdec.t[:, h, :], in_=rowq.t[:], func=AF.Exp, scale=lgs[h]), reads=[rowq.b], writes=[qdec.b])
                        kb.op(ACT, lambda h=h: A.activation(out=kdec.t[:, h, :], in_=rowk.t[:], func=AF.Exp, scale=lgs[h]), reads=[rowk.b], writes=[kdec.b])
                    qTr = sb(s4, "qTr", [128, 4, 512], BF16)
                    qdT = sb(s4, "qdT", [128, 4, 512], BF16)
                    kTr = sb(s4, "kTr", [128, 4, 512], BF16)
                    kdT = sb(s4, "kdT", [128, 4, 512], BF16)
                    v_sb = [sb(s4, f"v_sb{i}", [128, 4, 128], BF16) for i in range(2)]
                    sgl = [sb(s4, f"sgl{i}", [128, 512], F32) for i in range(2)]
                    kd = [sb(s4, f"kd{i}", [128, 4, 128], BF16) for i in range(2)]
                    attb = [sb(s4, f"attb{i}", [128, 4, 128], BF16) for i in range(2)]
                    state_f = sb(s4, "state_f", [128, 4, 128], F32)
                    state_b = sb(s4, "state_b", [128, 4, 128], BF16)
                    bst = sb(s4, "bst", [128, 4, 6], F32)
                    mv = sb(s4, "mv", [128, 4, 2], F32)
                    rs4 = sb(s4, "rs4", [128, 12], F32)
                    on = sb(s4, "on", [128, 512], F32)
                    yr = [sb(s4, f"yr{i}", [128, 512], BF16) for i in range(2)]
                    yrdbg = sb(s4, "yrdbg", [128, 512], F32) if "y_ret" in dbg else None
                    kb.op(POOL, lambda: G.memset(state_f.t[:], 0.0), writes=[state_f.b])
                    kb.op(POOL, lambda: G.memset(state_b.t[:], 0.0), writes=[state_b.b])
                    KS = float(128.0 ** -0.5)
                    pcnt = [0]
                    for tg in range(4):
                        tks = slice(tg * 512, (tg + 1) * 512)
                        hbs = [hT.bs[4 * tg + i] for i in range(4)]
                        for qk in range(2):
                            for h in range(4):
                                bk = pcnt[0] % 2
                                pcnt[0] += 1
                                for k in range(KC):
                                    kb.op(PE, lambda k=k, qk=qk, h=h, bk=bk: TE.matmul(pf(bk), lhsT=wr.t[:, k, qk * 512 + h * 128:qk * 512 + (h + 1) * 128],
                                                                                      rhs=hT.t[:, k, tks], start=(k == 0), stop=(k == KC - 1)),
                                          reads=hbs + [wr.bs[qk]], writes=[PS[bk].b], inc=(k == KC - 1))
                                pv4 = pf(bk).rearrange("p (a b) -> p a b", b=128)
                                if qk == 0:
                                    kb.op(ACT, lambda h=h, bk=bk: A.copy(out=qTr.t[:, h, :], in_=pf(bk)), reads=[PS[bk].b], writes=[qTr.b])
                                    kb.op(DVE, lambda h=h, pv4=pv4: V.tensor_tensor(out=qdT.t[:, h, :].rearrange("p (a b) -> p a b", b=128), in0=pv4,
                                                                                    in1=qdec.t[:, h:h + 1, :].broadcast_to([128, 4, 128]), op=ALU.mult),
                                          reads=[PS[bk].b, qdec.b], writes=[qdT.b])
                                else:
                                    kb.op(ACT, lambda h=h, bk=bk: A.mul(out=kTr.t[:, h, :], in_=pf(bk), mul=KS), reads=[PS[bk].b], writes=[kTr.b])
                                    kb.op(DVE, lambda h=h, pv4=pv4: V.scalar_tensor_tensor(out=kdT.t[:, h, :].rearrange("p (a b) -> p a b", b=128), in0=pv4, scalar=KS,
                                                                                           in1=kdec.t[:, h:h + 1, :].broadcast_to([128, 4, 128]),
                                                                                           op0=ALU.mult, op1=ALU.mult),
                                          reads=[PS[bk].b, kdec.b], writes=[kdT.b])
                        for cl in range(4):
                            c = 4 * tg + cl
                            i2 = c % 2
                            tok = slice(c * 128, (c + 1) * 128)
                            cs = slice(cl * 128, (cl + 1) * 128)
                            for k in range(KC):
                                kb.op(PE, lambda k=k: TE.matmul(pf(2), lhsT=hT.t[:, k, tok], rhs=wr.t[:, k, 1024:1536], start=(k == 0), stop=(k == KC - 1)),
                                      reads=[hT.bs[c], wr.bs[2]], writes=[PS[2].b], inc=(k == KC - 1))
                            for k in range(KC):
                                kb.op(PE, lambda k=k: TE.matmul(pf(3), lhsT=hT.t[:, k, tok], rhs=wr.t[:, k, 1536:2048], start=(k == 0), stop=(k == KC - 1)),
                                      reads=[hT.bs[c], wr.bs[3]], writes=[PS[3].b], inc=(k == KC - 1))
                            vs, sg_, kd_, ab = v_sb[i2], sgl[i2], kd[i2], attb[i2]
                            kb.op(ACT, lambda: A.copy(out=vs.t[:].rearrange("p a b -> p (a b)"), in_=pf(2)), reads=[PS[2].b], writes=[vs.b])
                            kb.op(ACT, lambda: A.activation(out=sg_.t[:], in_=pf(3), func=AF.Silu), reads=[PS[3].b], writes=[sg_.b])
                            transposes([kdT.t[:, h, cs] for h in range(4)], 4, [kdT.b])
                            kb.op(ACT, lambda: A.copy(out=kd_.t[:].rearrange("p a b -> p (a b)"), in_=pb(4)[:, 0:512]), reads=[PS[4].b], writes=[kd_.b])
                            for h in range(4):
                                kb.op(PE, lambda h=h: TE.matmul(pf(5)[:, h * 128:(h + 1) * 128], lhsT=kTr.t[:, h, cs], rhs=qTr.t[:, h, cs], start=True, stop=True),
                                      reads=[kTr.b, qTr.b], writes=[PS[5].b], inc=(h == 3))
                            kb.op(DVE, lambda: V.tensor_tensor(out=ab.t[:], in0=pf(5).rearrange("p (a b) -> p a b", b=128), in1=idT.t[:], op=ALU.mult),
                                  reads=[PS[5].b, idT.b], writes=[ab.b])
                            for h in range(4):
                                kb.op(PE, lambda h=h: TE.matmul(pf(6)[:, h * 128:(h + 1) * 128], lhsT=ab.t[:, h, :], rhs=vs.t[:, h, :], start=True, stop=(c == 0)),
                                      reads=[ab.b, vs.b], writes=[PS[6].b], inc=(c == 0 and h == 3))
                                if c > 0:
                                    kb.op(PE, lambda h=h: TE.matmul(pf(6)[:, h * 128:(h + 1) * 128], lhsT=qdT.t[:, h, cs], rhs=state_b.t[:, h, :], start=False, stop=True),
                                          reads=[qdT.b, state_b.b], writes=[PS[6].b], inc=(h == 3))
                            if c < NT - 1:
                                for h in range(4):
                                    kb.op(PE, lambda h=h: TE.matmul(pf(7)[:, h * 128:(h + 1) * 128], lhsT=kd_.t[:, h, :], rhs=vs.t[:, h, :], start=True, stop=True),
                                          reads=[kd_.b, vs.b], writes=[PS[7].b], inc=(h == 3))
                                for h in range(4):
                                    kb.op(DVE, lambda h=h: V.scalar_tensor_tensor(out=state_f.t[:, h, :], in0=state_f.t[:, h, :], scalar=cds[h],
                                                                                  in1=pf(7)[:, h * 128:(h + 1) * 128], op0=ALU.mult, op1=ALU.add),
                                          reads=[state_f.b, PS[7].b], writes=[state_f.b])
                                kb.op(POOL, lambda: G.tensor_copy(out=state_b.t[:], in_=state_f.t[:]), reads=[state_f.b], writes=[state_b.b])
                            for h in range(4):
                                kb.op(DVE, lambda h=h: V.bn_stats(out=bst.t[:, h, :], in_=pf(6)[:, h * 128:(h + 1) * 128]), reads=[PS[6].b], writes=[bst.b])
                                kb.op(DVE, lambda h=h: V.bn_aggr(out=mv.t[:, h, :], in_=bst.t[:, h, :]), reads=[bst.b], writes=[mv.b])
                            kb.op(DVE, lambda: V.tensor_scalar(out=rs4.t[:, 0:4], in0=mv.t[:, :, 1], scalar1=EPS, scalar2=None, op0=ALU.add), reads=[mv.b], writes=[rs4.b])
                            kb.op(ACT, lambda: A.activation(out=rs4.t[:, 4:8], in_=rs4.t[:, 0:4], func=AF.Sqrt), reads=[rs4.b], writes=[rs4.b])
                            kb.op(DVE, lambda: V.reciprocal(out=rs4.t[:, 8:12], in_=rs4.t[:, 4:8]), reads=[rs4.b], writes=[rs4.b])
                            for h in range(4):
                                kb.op(DVE, lambda h=h: V.tensor_scalar(out=on.t[:, h * 128:(h + 1) * 128], in0=pf(6)[:, h * 128:(h + 1) * 128],
                                                                       scalar1=mv.t[:, h, 0:1], scalar2=rs4.t[:, 8 + h:9 + h], op0=ALU.subtract, op1=ALU.mult),
                                      reads=[PS[6].b, mv.b, rs4.b], writes=[on.b])
                            kb.op(POOL, lambda: G.tensor_tensor(out=on.t[:], in0=on.t[:], in1=gn.t[:], op=ALU.mult), reads=[on.b, gn.b], writes=[on.b])
                            y_ = yr[i2]
                            kb.op(DVE, lambda: V.tensor_tensor(out=y_.t[:], in0=on.t[:], in1=sg_.t[:], op=ALU.mult), reads=[on.b, sg_.b], writes=[y_.b])
                            if yrdbg is not None:
                                kb.op(DVE, lambda: V.tensor_tensor(out=yrdbg.t[:], in0=on.t[:], in1=sg_.t[:], op=ALU.mult), reads=[on.b, sg_.b], writes=[yrdbg.b])
                                dbg_store("y_ret", yrdbg.t[:], tok, [yrdbg.b])
                            transposes([y_.t[:, k * 128:(k + 1) * 128] for k in range(4)], 4, [y_.b])
                            kb.op(ACT, lambda: A.copy(out=yretT.t[:, :, tok], in_=pb(4)[:, 0:512].rearrange("p (a b) -> p a b", b=128)),
                                  reads=[PS[4].b], writes=[yretT.bs[c]])

            def phase_1e():
                with ExitStack() as s5:
                    wm = sb(s5, "wm", [128, KC, 2048], BF16, 4)
                    for j in range(4):
                        kb.dma(POOL, wm.t[:, :, j * 512:(j + 1) * 512], w_in_v[:, :, C_M + j * 512:C_M + (j + 1) * 512], writes=[wm.bs[j]])
                    wbr = sb(s5, "wbr", [128, 8, D], BF16)
                    load_w(wbr, w_branch[0].rearrange("n (k p) d -> p (n k) d", p=128))
                    wo = sb(s5, "wo", [128, KC, D], BF16)
                    load_w(wo, w_out[0].rearrange("(k p) d -> p k d", p=128))
                    gates = sb(s5, "gates", [128, 2048], F32, 4)
                    tmix = sb(s5, "tmix", [128, D], F32)
                    tmix2 = sb(s5, "tmix2", [128, D], F32)
                    mixed = sb(s5, "mixed", [128, D], BF16)
                    mixT = sb(s5, "mixT", [128, KC, 128], BF16)
                    x1t = [sb(s5, f"x1t{i}", [128, D], F32) for i in range(2)]
                    for c in range(NT):
                        tok = slice(c * 128, (c + 1) * 128)
                        xx = xt[c % 2]
                        kb.dma(SP, xx.t[:], x[tok, :], writes=[xx.b])
                        for j in range(4):
                            for k in range(KC):
                                kb.op(PE, lambda j=j, k=k: TE.matmul(pf(j), lhsT=hT.t[:, k, tok], rhs=wm.t[:, k, j * 512:(j + 1) * 512], start=(k == 0), stop=(k == KC - 1)),
                                      reads=[hT.bs[c], wm.bs[j]], writes=[PS[j].b], inc=(k == KC - 1))
                            kb.op(ACT, lambda j=j: A.activation(out=gates.t[:, j * 512:(j + 1) * 512], in_=pf(j), func=AF.Sigmoid),
                                  reads=[PS[j].b], writes=[gates.bs[j]])
                        for n, yT in ((0, ynsaT), (1, yretT)):
                            for half in range(2):
                                bk = 4 + 2 * n + half
                                for k in range(4):
                                    kb.op(PE, lambda n=n, half=half, k=k, bk=bk, yT=yT: TE.matmul(pf(bk), lhsT=yT.t[:, k, tok], rhs=wbr.t[:, n * 4 + k, half * 512:(half + 1) * 512],
                                                                                                  start=(k == 0), stop=(k == 3)),
                                          reads=[yT.bs[c], wbr.b], writes=[PS[bk].b], inc=(k == 3))
                        for half in range(2):
                            hs = slice(half * 512, (half + 1) * 512)
                            kb.op(DVE, lambda half=half, hs=hs: V.tensor_tensor(out=tmix.t[:, hs], in0=gates.t[:, half * 512:(half + 1) * 512], in1=pf(4 + half), op=ALU.mult),
                                  reads=[gates.bs[half], PS[4 + half].b], writes=[tmix.b])
                            kb.op(DVE, lambda half=half, hs=hs: V.tensor_tensor(out=tmix2.t[:, hs], in0=gates.t[:, 1024 + half * 512:1024 + (half + 1) * 512], in1=pf(6 + half), op=ALU.mult),
                                  reads=[gates.bs[2 + half], PS[6 + half].b], writes=[tmix2.b])
                        kb.op(POOL, lambda: G.tensor_tensor(out=mixed.t[:], in0=tmix.t[:], in1=tmix2.t[:], op=ALU.add), reads=[tmix.b, tmix2.b], writes=[mixed.b])
                        transposes([mixed.t[:, k * 128:(k + 1) * 128] for k in range(KC)], 0, [mixed.b])
                        kb.op(ACT, lambda: A.copy(out=mixT.t[:], in_=pb(0).rearrange("p (a b) -> p a b", b=128)), reads=[PS[0].b], writes=[mixT.b])
                        for half in range(2):
                            for k in range(KC):
                                kb.op(PE, lambda half=half, k=k: TE.matmul(pf(1 + half), lhsT=mixT.t[:, k, :], rhs=wo.t[:, k, half * 512:(half + 1) * 512],
                                                                           start=(k == 0), stop=(k == KC - 1)),
                                      reads=[mixT.b, wo.b], writes=[PS[1 + half].b], inc=(k == KC - 1))
                        x1 = x1t[c % 2]
                        for half in range(2):
                            hs = slice(half * 512, (half + 1) * 512)
                            kb.op(DVE, lambda half=half, hs=hs: V.tensor_tensor(out=x1.t[:, hs], in0=xx.t[:, hs], in1=pf(1 + half), op=ALU.add),
                                  reads=[xx.b, PS[1 + half].b], writes=[x1.b])
                        kb.dma(SP, xmid[tok, :], x1.t[:], reads=[x1.b])
                        dbg_store("x1", x1.t[:], tok, [x1.b])
                        norm_to_hT(x1, c, g2bc, c)

            def phase_2():
                with ExitStack() as s6:
                    wd = sb(s6, "wd", [128, NFB, D], BF16, 2)
                    wd_v = ffn_w_down[0].rearrange("(fb p) d -> p fb d", p=128)
                    kb.dma(POOL, wd.t[:, 0:11, :], wd_v[:, 0:11, :], writes=[wd.bs[0]])
                    kb.dma(POOL, wd.t[:, 11:22, :], wd_v[:, 11:22, :], writes=[wd.bs[1]])
                    wgs = [sb(s6, f"wgs{i}", [128, KC, 256], BF16) for i in range(2)]
                    wus = [sb(s6, f"wus{i}", [128, KC, 256], BF16) for i in range(2)]
                    act = sb(s6, "act", [128, NFB, 1024], BF16, NFB)
                    sgs = [sb(s6, f"sgs{i}", [128, 512], F32) for i in range(2)]
                    outt = [sb(s6, f"outt{i}", [128, D], F32) for i in range(2)]
                    wg_v = ffn_w_gate[0].rearrange("(k p) f -> p k f", p=128)
                    wu_v = ffn_w_up[0].rearrange("(k p) f -> p k f", p=128)
                    cn = [0, 0]
                    for hf in range(2):
                        for fg in range(11):
                            cols = slice(fg * 256, (fg + 1) * 256)
                            wg_, wu_ = wgs[fg % 2], wus[fg % 2]
                            kb.dma(POOL, wg_.t[:], wg_v[:, :, cols], writes=[wg_.b])
                            kb.dma(POOL, wu_.t[:], wu_v[:, :, cols], writes=[wu_.b])
                            for fl in range(2):
                                fb = fg * 2 + fl
                                for t2 in range(2):
                                    tokc = slice(hf * 1024 + t2 * 512, hf * 1024 + (t2 + 1) * 512)
                                    hbs = [hT.bs[hf * 8 + t2 * 4 + i] for i in range(4)]
                                    gb, ub = (0, 1) if cn[0] % 2 == 0 else (2, 3)
                                    cn[0] += 1
                                    for k in range(KC):
                                        kb.op(PE, lambda k=k, fl=fl, gb=gb, wg_=wg_, tokc=tokc: TE.matmul(pf(gb), lhsT=wg_.t[:, k, fl * 128:(fl + 1) * 128], rhs=hT.t[:, k, tokc],
                                                                                                          start=(k == 0), stop=(k == KC - 1)),
                                              reads=hbs + [wg_.b], writes=[PS[gb].b], inc=(k == KC - 1))
                                    for k in range(KC):
                                        kb.op(PE, lambda k=k, fl=fl, ub=ub, wu_=wu_, tokc=tokc: TE.matmul(pf(ub), lhsT=wu_.t[:, k, fl * 128:(fl + 1) * 128], rhs=hT.t[:, k, tokc],
                                                                                                          start=(k == 0), stop=(k == KC - 1)),
                                              reads=hbs + [wu_.b], writes=[PS[ub].b], inc=(k == KC - 1))
                                    sg_ = sgs[cn[0] % 2]
                                    kb.op(ACT, lambda sg_=sg_, gb=gb: A.activation(out=sg_.t[:], in_=pf(gb), func=AF.Silu), reads=[PS[gb].b], writes=[sg_.b])
                                    kb.op(DVE, lambda sg_=sg_, ub=ub, fb=fb, t2=t2: V.tensor_tensor(out=act.t[:, fb, t2 * 512:(t2 + 1) * 512], in0=sg_.t[:], in1=pf(ub), op=ALU.mult),
                                          reads=[sg_.b, PS[ub].b], writes=[act.bs[fb]])
                        for tl in range(8):
                            c = hf * 8 + tl
                            tok = slice(c * 128, (c + 1) * 128)
                            xx = xt[c % 2]
                            kb.dma(SP, xx.t[:], xmid[tok, :], writes=[xx.b])
                            ob = (4, 5) if cn[1] % 2 == 0 else (6, 7)
                            cn[1] += 1
                            for half in range(2):
                                for fb in range(NFB):
                                    kb.op(PE, lambda half=half, fb=fb, ob=ob, tl=tl: TE.matmul(pf(ob[half]), lhsT=act.t[:, fb, tl * 128:(tl + 1) * 128],
                                                                                               rhs=wd.t[:, fb, half * 512:(half + 1) * 512],
                                                                                               start=(fb == 0), stop=(fb == NFB - 1)),
                                          reads=[act.bs[fb], wd.bs[0 if fb < 11 else 1]], writes=[PS[ob[half]].b], inc=(fb == NFB - 1))
                            ot = outt[c % 2]
                            for half in range(2):
                                hs = slice(half * 512, (half + 1) * 512)
                                kb.op(DVE, lambda half=half, hs=hs, ob=ob, ot=ot, xx=xx: V.tensor_tensor(out=ot.t[:, hs], in0=xx.t[:, hs], in1=pf(ob[half]), op=ALU.add),
                                      reads=[xx.b, PS[ob[half]].b], writes=[ot.b])
                            kb.dma(SP, out[tok, :], ot.t[:], reads=[ot.b], is_out=True)

            with ExitStack() as sB:
                ynsaT = sb(sB, "ynsaT", [128, 4, S], BF16, NT)
                yretT = sb(sB, "yretT", [128, 4, S], BF16, NT)

                with ExitStack() as sA:
                    gq = sb(sA, "gq", [128, 64], F32)
                    gk = sb(sA, "gk", [128, 3, 64], F32)
                    QAL = sb(sA, "QAL", [128, NT, 8, 4], BF16)
                    KAL = sb(sA, "KAL", [128, NT, 4], BF16)
                    KCAL = sb(sA, "KCAL", [128, 4], BF16)
                    OH = sb(sA, "OH", [128, NT, 32], BF16)
                    dmask = sb(sA, "dmask", [128, 4, 128], BF16)
                    tmask = sb(sA, "tmask", [128, 4, 128], BF16)
                    cmask = sb(sA, "cmask", [128, S], BF16)
                    addc = sb(sA, "addc", [128, NT, 32], F32)
                    ov = sb(sA, "ov", [128, 32], BF16)
                    KT_slc = sb(sA, "KT_slc", [128, 2, S], BF16, NT)
                    KT_win = sb(sA, "KT_win", [128, 2, S], BF16, NT)
                    V_slc = sb(sA, "V_slc", [128, NT, 2, 65], BF16, NT)
                    V_win = sb(sA, "V_win", [128, NT, 2, 65], BF16, NT)
                    KcT = sb(sA, "KcT", [128, 2, 128], BF16)
                    Vc = sb(sA, "Vc", [128, 2, 97], BF16)

                    with ExitStack() as s0:
                        SL = sb(s0, "SL", [128, 8], F32)
                        th128 = sb(s0, "th128", [128, NT], F32)
                        pidx = sb(s0, "pidx", [128, 1], F32)
                        QALf = sb(s0, "QALf", [128, NT, 8, 4], F32)
                        KALf = sb(s0, "KALf", [128, NT, 4], F32)
                        KCALf = sb(s0, "KCALf", [128, 4], F32)
                        rel = sb(s0, "rel", [128, NT, 32], F32)
                        f0 = sb(s0, "f0", [128, NT, 32], F32)
                        f1 = sb(s0, "f1", [128, NT, 32], F32)
                        t1 = sb(s0, "t1", [128, NT, 32], F32)
                        hp = sb(s0, "hp", [128, 1], F32)
                        ovf = sb(s0, "ovf", [128, 32], F32)
                        ova = sb(s0, "ova", [128, 32], F32)
                        ones_b = sb(s0, "ones_b", [128, 512], BF16)
                        zeros_b = sb(s0, "zeros_b", [128, 512], BF16)

                        kb.dma(SP, gq.t[:], nsa_q_norm[0:1, :].broadcast_to([128, 64]), writes=[gq.b])
                        kb.dma(SP, gk.t[:].rearrange("p a b -> p (a b)"),
                               nsa_k_norm.rearrange("o a b -> o (a b)").broadcast_to([128, 192]), writes=[gk.b])
                        kb.op(DVE, lambda: V.tensor_scalar(out=gq.t[:], in0=gq.t[:], scalar1=0.125, scalar2=None, op0=ALU.mult),
                              reads=[gq.b], writes=[gq.b])
                        for h in range(8):
                            kb.op(POOL, lambda h=h: G.memset(SL.t[:, h:h + 1], 2.0 ** (-(h + 1))), writes=[SL.b])
                        kb.op(POOL, lambda: G.iota(th128.t[:], pattern=[[128, NT]], base=0, channel_multiplier=0,
                                                   allow_small_or_imprecise_dtypes=True), writes=[th128.b])
                        kb.op(POOL, lambda: G.iota(pidx.t[:], pattern=[[0, 1]], base=0, channel_multiplier=1,
                                                   allow_small_or_imprecise_dtypes=True), writes=[pidx.b])
                        SLb = SL.t[:].unsqueeze(1).broadcast_to([128, NT, 8])
                        THb = th128.t[:].unsqueeze(2).broadcast_to([128, NT, 8])
                        kb.op(DVE, lambda: V.scalar_tensor_tensor(out=QALf.t[:, :, :, 0], in0=THb, scalar=-1.0, in1=SLb,
                                                                  op0=ALU.mult, op1=ALU.mult),
                              reads=[SL.b, th128.b], writes=[QALf.b])
                        kb.op(DVE, lambda: V.tensor_scalar(out=QALf.t[:, :, :, 1], in0=SLb, scalar1=pidx.t[:, 0:1], scalar2=-1.0,
                                                           op0=ALU.mult, op1=ALU.mult),
                              reads=[SL.b, pidx.b], writes=[QALf.b])
                        kb.op(DVE, lambda: V.tensor_copy(out=QALf.t[:, :, :, 2], in_=SLb), reads=[SL.b], writes=[QALf.b])
                        kb.op(DVE, lambda: V.tensor_copy(out=QALf.t[:, :, :, 3], in_=SLb), reads=[SL.b], writes=[QALf.b])
                        kb.op(DVE, lambda: V.tensor_copy(out=QAL.t[:], in_=QALf.t[:]), reads=[QALf.b], writes=[QAL.b])
                        kb.op(POOL, lambda: G.memset(KALf.t[:, :, 0:2], 1.0), writes=[KALf.b])
                        kb.op(DVE, lambda: V.tensor_copy(out=KALf.t[:, :, 2], in_=th128.t[:]), reads=[th128.b], writes=[KALf.b])
                        kb.op(DVE, lambda: V.tensor_copy(out=KALf.t[:, :, 3], in_=pidx.t[:, 0:1].broadcast_to([128, NT])),
                              reads=[pidx.b], writes=[KALf.b])
                        kb.op(DVE, lambda: V.tensor_copy(out=KAL.t[:], in_=KALf.t[:]), reads=[KALf.b], writes=[KAL.b])
                        kb.op(POOL, lambda: G.memset(KCALf.t[:, 0:2], 1.0), writes=[KCALf.b])
                        kb.op(POOL, lambda: G.memset(KCALf.t[:, 3:4], 31.0), reads=[], writes=[KCALf.b])
                        kb.op(DVE, lambda: V.tensor_scalar(out=KCALf.t[:, 2:3], in0=pidx.t[:, 0:1], scalar1=16.0, scalar2=None,
                                                           op0=ALU.mult), reads=[pidx.b], writes=[KCALf.b])
                        kb.op(DVE, lambda: V.tensor_copy(out=KCAL.t[:], in_=KCALf.t[:]), reads=[KCALf.b], writes=[KCAL.b])
                        kb.op(POOL, lambda: G.memset(OH.t[:], 0.0), writes=[OH.b])
                        for kt in range(NT):
                            kb.op(POOL, lambda kt=kt: G.memset(OH.t[0:64, kt, 2 * kt:2 * kt + 1], 1.0), writes=[OH.b])
                            kb.op(POOL, lambda kt=kt: G.memset(OH.t[64:128, kt, 2 * kt + 1:2 * kt + 2], 1.0), writes=[OH.b])
                        kb.op(POOL, lambda: G.memset(ones_b.t[:], 1.0), writes=[ones_b.b])
                        kb.op(POOL, lambda: G.memset(zeros_b.t[:], 0.0), writes=[zeros_b.b])
                        ob4 = ones_b.t[:].rearrange("p (a b) -> p a b", b=128)
                        kb.op(POOL, lambda: G.affine_select(out=dmask.t[:], in_=ob4, pattern=[[0, 4], [1, 128]],
                                                            compare_op=ALU.is_ge, fill=0.0, base=0, channel_multiplier=-1),
                              reads=[ones_b.b], writes=[dmask.b])
                        kb.op(POOL, lambda: G.affine_select(out=tmask.t[:], in_=ob4, pattern=[[0, 4], [-1, 128]],
                                                            compare_op=ALU.is_gt, fill=0.0, base=0, channel_multiplier=1),
                              reads=[ones_b.b], writes=[tmask.b])
                        for i in range(4):
                            kb.op(POOL, lambda i=i: G.affine_select(out=cmask.t[:, i * 512:(i + 1) * 512], in_=zeros_b.t[:],
                                                                    pattern=[[1, 512]], compare_op=ALU.is_ge, fill=NEG,
                                                                    base=-31 + 512 * i, channel_multiplier=-16),
                                  reads=[zeros_b.b], writes=[cmask.b])
                        kb.op(POOL, lambda: G.iota(rel.t[:], pattern=[[-2, NT], [1, 32]], base=0, channel_multiplier=0,
                                                   allow_small_or_imprecise_dtypes=True), writes=[rel.b])
                        kb.op(DVE, lambda: V.tensor_scalar(out=hp.t[:], in0=pidx.t[:], scalar1=64.0, scalar2=None, op0=ALU.is_ge),
                              reads=[pidx.b], writes=[hp.b])
                        kb.op(DVE, lambda: V.tensor_scalar(out=rel.t[:], in0=rel.t[:], scalar1=hp.t[:, 0:1], scalar2=None,
                                                           op0=ALU.subtract), reads=[rel.b, hp.b], writes=[rel.b])
                        kb.op(DVE, lambda: V.tensor_scalar(out=t1.t[:], in0=rel.t[:], scalar1=0.0, scalar2=-1e9,
                                                           op0=ALU.is_gt, op1=ALU.mult), reads=[rel.b], writes=[t1.b])
                        kb.op(DVE, lambda: V.tensor_scalar(out=f0.t[:], in0=rel.t[:], scalar1=0.0, scalar2=None, op0=ALU.is_equal),
                              reads=[rel.b], writes=[f0.b])
                        kb.op(DVE, lambda: V.tensor_scalar(out=f1.t[:], in0=rel.t[:], scalar1=-1.0, scalar2=None, op0=ALU.is_equal),
                              reads=[rel.b], writes=[f1.b])
                        kb.op(DVE, lambda: V.tensor_tensor(out=f0.t[:], in0=f0.t[:], in1=f1.t[:], op=ALU.max),
                              reads=[f0.b, f1.b], writes=[f0.b])
                        kb.op(DVE, lambda: V.memset(f0.t[:, :, 0:1], 1.0), reads=[], writes=[f0.b])
                        kb.op(DVE, lambda: V.scalar_tensor_tensor(out=addc.t[:], in0=f0.t[:], scalar=1e4, in1=t1.t[:],
                                                                  op0=ALU.mult, op1=ALU.add), reads=[f0.b, t1.b], writes=[addc.b])
                        kb.op(POOL, lambda: G.iota(ovf.t[:], pattern=[[-64, 32]], base=0, channel_multiplier=16,
                                                   allow_small_or_imprecise_dtypes=True), writes=[ovf.b])
                        kb.op(DVE, lambda: V.tensor_scalar(out=ova.t[:], in0=ovf.t[:], scalar1=63.0, scalar2=None, op0=ALU.is_le),
                              reads=[ovf.b], writes=[ova.b])
                        kb.op(DVE, lambda: V.tensor_scalar(out=ovf.t[:], in0=ovf.t[:], scalar1=-31.0, scalar2=None, op0=ALU.is_ge),
                              reads=[ovf.b], writes=[ovf.b])
                        kb.op(DVE, lambda: V.tensor_tensor(out=ov.t[:], in0=ova.t[:], in1=ovf.t[:], op=ALU.mult),
                              reads=[ova.b, ovf.b], writes=[ov.b])
                        kb.op(POOL, lambda: G.memset(V_slc.t[:, :, :, 64:65], 1.0), writes=V_slc.bs)
                        kb.op(POOL, lambda: G.memset(V_win.t[:, :, :, 64:65], 1.0), writes=V_win.bs)
                        kb.barrier()
                        chk(1)

                    with ExitStack() as s1:
                        g1bc = sb(s1, "g1bc", [128, D], F32)
                        kb.dma(SP, g1bc.t[:], norm1_g[0:1, :].broadcast_to([128, D]), writes=[g1bc.b])
                        for c in range(NT):
                            xx = xt[c % 2]
                            kb.dma(SP, xx.t[:], x[c * 128:(c + 1) * 128, :], writes=[xx.b])
                            norm_to_hT(xx, c, g1bc, c)
                        kb.barrier()
                        chk(2)

                    with ExitStack() as s2:
                        cmpT = sb(s2, "cmpT", [128, 2, S], BF16, NT)
                        with ExitStack() as s2a:
                            wkv = sb(s2a, "wkv", [128, KC, 768], BF16)
                            load_w(wkv, w_in_v[:, :, C_KV:C_KV + 768])
                            cmp_tok = [sb(s2a, f"cmp_tok{i}", [128, 256], BF16) for i in range(2)]
                            sqk = [sb(s2a, f"sqk{i}", [128, 256], F32) for i in range(2)]
                            kst = sb(s2a, "kst", [128, NT, 16], F32, NT)
                            tmpk = [sb(s2a, f"tmpk{i}", [128, 4, 64], F32) for i in range(2)]
                            ka_slc = [sb(s2a, f"ka_slc{i}", [128, 2, 128], BF16) for i in range(2)]
                            ka_win = [sb(s2a, f"ka_win{i}", [128, 2, 128], BF16) for i in range(2)]
                            for i in range(2):
                                kb.op(POOL, lambda i=i: G.memset(ka_slc[i].t[:], 0.0), writes=[ka_slc[i].b])
                                kb.op(POOL, lambda i=i: G.memset(ka_win[i].t[:], 0.0), writes=[ka_win[i].b])
                            for c in range(NT):
                                i2 = c % 2
                                bA, bB = (0, 1) if i2 == 0 else (2, 3)
                                tok = slice(c * 128, (c + 1) * 128)
                                if CUT >= 1:
                                    for k in range(KC):
                                        kb.op(PE, lambda k=k: TE.matmul(pf(bA), lhsT=hT.t[:, k, tok], rhs=wkv.t[:, k, 0:512],
                                                                        start=(k == 0), stop=(k == KC - 1)),
                                              reads=[hT.bs[c], wkv.b], writes=[PS[bA].b], inc=(k == KC - 1))
                                    for k in range(KC):
                                        kb.op(PE, lambda k=k: TE.matmul(pf(bB)[:, 0:256], lhsT=hT.t[:, k, tok], rhs=wkv.t[:, k, 512:768],
                                                                        start=(k == 0), stop=(k == KC - 1)),
                                              reads=[hT.bs[c], wkv.b], writes=[PS[bB].b], inc=(k == KC - 1))
                                if CUT >= 2:
                                    ct = cmp_tok[i2]
                                    kb.op(ACT, lambda: A.copy(out=ct.t[:], in_=pf(bA)[:, 0:256]), reads=[PS[bA].b], writes=[ct.b])
                                    sq = sqk[i2]
                                    kb.op(ACT, lambda: A.activation(out=sq.t[:, 0:128], in_=pf(bA)[:, 256:384], func=AF.Square),
                                          reads=[PS[bA].b], writes=[sq.b])
                                    kb.op(ACT, lambda: A.activation(out=sq.t[:, 128:256], in_=pf(bB)[:, 0:128], func=AF.Square),
                                          reads=[PS[bB].b], writes=[sq.b])
                                if CUT >= 3:
                                    ks = kst.t
                                    ksb = [kst.bs[c]]
                                    kb.op(DVE, lambda: V.tensor_reduce(out=ks[:, c, 0:4], in_=sq.t[:].rearrange("p (a b) -> p a b", b=64),
                                                                       axis=AX.X, op=ALU.add), reads=[sq.b], writes=ksb)
                                    rstd_from_ss(ks[:, c, 0:4], ks[:, c, 4:8], ks[:, c, 8:12], ks[:, c, 12:16], 64, ksb)
                                    tk = tmpk[i2]
                                    kb.op(DVE, lambda: V.tensor_tensor(out=tk.t[:, 0:2, :], in0=pf(bA)[:, 256:384].rearrange("p (a b) -> p a b", b=64),
                                                                       in1=ks[:, c, 12:14].unsqueeze(2).broadcast_to([128, 2, 64]), op=ALU.mult),
                                          reads=[PS[bA].b] + ksb, writes=[tk.b])
                                    kb.op(DVE, lambda: V.tensor_tensor(out=tk.t[:, 2:4, :], in0=pf(bB)[:, 0:128].rearrange("p (a b) -> p a b", b=64),
                                                                       in1=ks[:, c, 14:16].unsqueeze(2).broadcast_to([128, 2, 64]), op=ALU.mult),
                                          reads=[PS[bB].b] + ksb, writes=[tk.b])
                                    ksl, kwn = ka_slc[i2], ka_win[i2]
                                    kb.op(DVE, lambda: V.tensor_tensor(out=ksl.t[:, :, 0:64], in0=tk.t[:, 0:2, :],
                                                                       in1=gk.t[:, 1:2, :].broadcast_to([128, 2, 64]), op=ALU.mult),
                                          reads=[tk.b, gk.b], writes=[ksl.b])
                                    kb.op(DVE, lambda: V.tensor_tensor(out=kwn.t[:, :, 0:64], in0=tk.t[:, 2:4, :],
                                                                       in1=gk.t[:, 2:3, :].broadcast_to([128, 2, 64]), op=ALU.mult),
                                          reads=[tk.b, gk.b], writes=[kwn.b])
                                if CUT >= 4:
                                    kb.op(POOL, lambda: G.tensor_copy(out=ksl.t[:, :, 64:96], in_=OH.t[:, c:c + 1, :].broadcast_to([128, 2, 32])),
                                          reads=[OH.b], writes=[ksl.b])
                                    kb.op(POOL, lambda: G.tensor_copy(out=ksl.t[:, :, 96:100], in_=KAL.t[:, c:c + 1, :].broadcast_to([128, 2, 4])),
                                          reads=[KAL.b], writes=[ksl.b])
                                    kb.op(POOL, lambda: G.tensor_copy(out=kwn.t[:, :, 96:100], in_=KAL.t[:, c:c + 1, :].broadcast_to([128, 2, 4])),
                                          reads=[KAL.b], writes=[kwn.b])
                                if CUT >= 5:
                                    kb.op(ACT, lambda: A.copy(out=V_slc.t[:, c, :, 0:64], in_=pf(bA)[:, 384:512].rearrange("p (a b) -> p a b", b=64)),
                                          reads=[PS[bA].b], writes=[V_slc.bs[c]])
                                    kb.op(ACT, lambda: A.copy(out=V_win.t[:, c, :, 0:64], in_=pf(bB)[:, 128:256].rearrange("p (a b) -> p a b", b=64)),
                                          reads=[PS[bB].b], writes=[V_win.bs[c]])
                                if CUT >= 6:
                                    tb = 4 + i2
                                    transposes([ksl.t[:, 0, :], ksl.t[:, 1, :], kwn.t[:, 0, :], kwn.t[:, 1, :], ct.t[:, 0:128], ct.t[:, 128:256]],
                                               tb, [ksl.b, kwn.b, ct.b])
                                    pv3 = pb(tb).rearrange("p (a b) -> p a b", b=128)
                                    kb.op(ACT, lambda: A.copy(out=KT_slc.t[:, :, tok], in_=pv3[:, 0:2, :]), reads=[PS[tb].b], writes=[KT_slc.bs[c]])
                                    kb.op(ACT, lambda: A.copy(out=KT_win.t[:, :, tok], in_=pv3[:, 2:4, :]), reads=[PS[tb].b], writes=[KT_win.bs[c]])
                                    kb.op(ACT, lambda: A.copy(out=cmpT.t[:, :, tok], in_=pv3[:, 4:6, :]), reads=[PS[tb].b], writes=[cmpT.bs[c]])
                            if "KT_slc" in dbg:
                                kdb = sb(s2a, "kdb", [128, 2, S], F32)
                                kb.op(DVE, lambda: V.tensor_copy(out=kdb.t[:], in_=KT_slc.t[:]), reads=KT_slc.bs, writes=[kdb.b])
                                kb.dma(SP, dbg["KT_slc"].rearrange("p (a b) -> p a b", b=S), kdb.t[:], reads=[kdb.b], is_out=True)
                            kb.barrier()
                            chk(3)

                        with ExitStack() as s2b:
                            w1sb = sb(s2b, "w1sb", [128, 2, 32, 128], BF16)
                            w2sb = sb(s2b, "w2sb", [128, 2, 64], BF16)
                            pe_sb = sb(s2b, "pe_sb", [32, 2, 64], F32)
                            peT = sb(s2b, "peT", [64, 2, 32], BF16)
                            bias_c = sb(s2b, "bias_c", [128, 2], F32)
                            xh = sb(s2b, "xh", [128, 128], F32)
                            x2 = sb(s2b, "x2", [128, 128], F32)
                            sg = sb(s2b, "sgc", [128, 128], F32)
                            HTb = sb(s2b, "HTb", [128, 128], BF16)
                            kca = sb(s2b, "kca", [128, 2, 128], BF16)
                            cst = sb(s2b, "cst", [128, 8], F32)
                            tmpc = sb(s2b, "tmpc", [128, 64], F32)
                            for kv in range(2):
                                src = cmp_w1[0, kv].rearrange("l d f -> d l f")
                                kb.dma(POOL, w1sb.t[0:64, kv], src, writes=[w1sb.b])
                                kb.dma(POOL, w1sb.t[64:128, kv], src, writes=[w1sb.b])
                            kb.dma(POOL, w2sb.t[:], cmp_w2[0].rearrange("k f d -> f k d"), writes=[w2sb.b])
                            kb.dma(SP, pe_sb.t[:], cmp_pe[0].rearrange("k l d -> l k d"), writes=[pe_sb.b])
                            kb.op(POOL, lambda: G.memset(kca.t[:], 0.0), writes=[kca.b])
                            kb.op(POOL, lambda: G.memset(Vc.t[:], 0.0), writes=[Vc.b])
                            kb.op(POOL, lambda: G.tensor_copy(out=kca.t[:, :, 96:100], in_=KCAL.t[:].unsqueeze(1).broadcast_to([128, 2, 4])),
                                  reads=[KCAL.b], writes=[kca.b])
                            kb.op(POOL, lambda: G.memset(Vc.t[:, :, 64:65], 1.0), writes=[Vc.b])
                            kb.op(POOL, lambda: G.tensor_copy(out=Vc.t[:, :, 65:97], in_=ov.t[:].unsqueeze(1).broadcast_to([128, 2, 32])),
                                  reads=[ov.b], writes=[Vc.b])
                            for kv in range(2):
                                kb.op(PE, lambda kv=kv: TE.transpose(out=pf(0)[0:64, kv * 32:(kv + 1) * 32], in_=pe_sb.t[0:32, kv, :],
                                                                     identity=ident_f.t[0:32, 0:32]),
                                      reads=[pe_sb.b, ident_f.b], writes=[PS[0].b])
                            kb.op(DVE, lambda: V.tensor_copy(out=peT.t[:], in_=pf(0)[0:64, 0:64].rearrange("p (a b) -> p a b", b=32)),
                                  reads=[PS[0].b], writes=[peT.b])
                            for kv in range(2):
                                for l in range(32):
                                    kb.op(PE, lambda kv=kv, l=l: TE.matmul(pf(1)[:, kv:kv + 1], lhsT=w1sb.t[0:64, kv, l, :], rhs=peT.t[0:64, kv, l:l + 1],
                                                                           start=(l == 0), stop=(l == 31)),
                                          reads=[w1sb.b, peT.b], writes=[PS[1].b], inc=(l == 31))
                            kb.op(DVE, lambda: V.tensor_copy(out=bias_c.t[:], in_=pf(1)[:, 0:2]), reads=[PS[1].b], writes=[bias_c.b])
                            it = 0
                            for kv in range(2):
                                for g in range(2):
                                    bH = 2 + (it % 2)
                                    bO = 4 + (it % 2)
                                    it += 1
                                    for l in range(32):
                                        kb.op(PE, lambda kv=kv, g=g, l=l: TE.matmul(
                                            pf(bH)[:, 0:127], lhsT=w1sb.t[g * 64:(g + 1) * 64, kv, l, :],
                                            rhs=cmpT.t[g * 64:(g + 1) * 64, kv, l:l + 16 * 126 + 1:16],
                                            start=(l == 0), stop=(l == 31)),
                                            reads=[w1sb.b] + cmpT.bs, writes=[PS[bH].b], inc=(l == 31))
                                    kb.op(ACT, lambda kv=kv: A.activation(out=xh.t[:, 0:127], in_=pf(bH)[:, 0:127], func=AF.Identity,
                                                                          bias=bias_c.t[:, kv:kv + 1], scale=1.0),
                                          reads=[PS[bH].b, bias_c.b], writes=[xh.b])
                                    kb.op(DVE, lambda: V.tensor_tensor(out=x2.t[:, 0:127], in0=xh.t[:, 0:127], in1=xh.t[:, 0:127], op=ALU.mult),
                                          reads=[xh.b], writes=[x2.b])
                                    kb.op(DVE, lambda: V.tensor_scalar(out=x2.t[:, 0:127], in0=x2.t[:, 0:127], scalar1=0.044715, scalar2=1.0,
                                                                       op0=ALU.mult, op1=ALU.add), reads=[x2.b], writes=[x2.b])
                                    kb.op(DVE, lambda: V.tensor_tensor(out=x2.t[:, 0:127], in0=x2.t[:, 0:127], in1=xh.t[:, 0:127], op=ALU.mult),
                                          reads=[x2.b, xh.b], writes=[x2.b])
                                    kb.op(ACT, lambda: A.activation(out=sg.t[:, 0:127], in_=x2.t[:, 0:127], func=AF.Sigmoid, scale=1.5957691216057308),
                                          reads=[x2.b], writes=[sg.b])
                                    kb.op(DVE, lambda: V.tensor_tensor(out=HTb.t[:, 0:127], in0=xh.t[:, 0:127], in1=sg.t[:, 0:127], op=ALU.mult),
                                          reads=[xh.b, sg.b], writes=[HTb.b])
                                    kb.op(PE, lambda kv=kv: TE.matmul(pf(bO)[0:127, 0:64], lhsT=HTb.t[:, 0:127], rhs=w2sb.t[:, kv, :], start=True, stop=True),
                                          reads=[HTb.b, w2sb.b], writes=[PS[bO].b])
                                    if kv == 0:
                                        kb.op(ACT, lambda g=g: A.activation(out=tmpc.t[0:127, :], in_=pf(bO)[0:127, 0:64], func=AF.Square,
                                                                            accum_out=cst.t[0:127, g:g + 1]),
                                              reads=[PS[bO].b], writes=[tmpc.b, cst.b])
                                        rstd_from_ss(cst.t[0:127, g:g + 1], cst.t[0:127, 2 + g:3 + g], cst.t[0:127, 4 + g:5 + g], cst.t[0:127, 6 + g:7 + g], 64, [cst.b])
                                        kb.op(DVE, lambda g=g: V.scalar_tensor_tensor(out=kca.t[0:127, g, 0:64], in0=pf(bO)[0:127, 0:64],
                                                                                      scalar=cst.t[0:127, 6 + g:7 + g], in1=gk.t[0:127, 0, :],
                                                                                      op0=ALU.mult, op1=ALU.mult),
                                              reads=[PS[bO].b, cst.b, gk.b], writes=[kca.b])
                                    else:
                                        kb.op(ACT, lambda g=g: A.copy(out=Vc.t[0:127, g, 0:64], in_=pf(bO)[0:127, 0:64]),
                                              reads=[PS[bO].b], writes=[Vc.b])
                            transposes([kca.t[:, 0, :], kca.t[:, 1, :]], 6, [kca.b])
                            kb.op(ACT, lambda: A.copy(out=KcT.t[:], in_=pb(6)[:, 0:256].rearrange("p (a b) -> p a b", b=128)),
                                  reads=[PS[6].b], writes=[KcT.b])
                            if "kc" in dbg:
                                kcd = sb(s2b, "kcd", [128, 2, 64], F32)
                                kb.op(DVE, lambda: V.tensor_copy(out=kcd.t[:], in_=kca.t[:, :, 0:64]), reads=[kca.b], writes=[kcd.b])
                                kb.dma(SP, dbg["kc"].rearrange("p (a b) -> p a b", b=64), kcd.t[:], reads=[kcd.b], is_out=True)
                            if "vc" in dbg:
                                vcd = sb(s2b, "vcd", [128, 2, 64], F32)
                                kb.op(DVE, lambda: V.tensor_copy(out=vcd.t[:], in_=Vc.t[:, :, 0:64]), reads=[Vc.b], writes=[vcd.b])
                                kb.dma(SP, dbg["vc"].rearrange("p (a b) -> p a b", b=64), vcd.t[:], reads=[vcd.b], is_out=True)
                            kb.barrier()
                            chk(4)

                    phase_1c()
                    kb.barrier()
                    chk(5)

                phase_1d()
                kb.barrier()
                chk(6)
                phase_1e()
                kb.barrier()
                chk(7)

            phase_2()
            kb.finish()
    except _Stop:
        pass
    return nc


_NAMES = ["x", "norm1_g", "w_in", "nsa_q_norm", "nsa_k_norm", "cmp_pe", "cmp_w1", "cmp_w2", "ret_gn_g",
          "w_branch", "w_out", "norm2_g", "ffn_w_gate", "ffn_w_up", "ffn_w_down"]


def kernel(**inputs):
    n = 8
    arrs = {k: np.ascontiguousarray(np.asarray(inputs[k], dtype=np.float32)) for k in _NAMES}
    nc = build_nc()
    in_maps = []
    for i in range(n):
        m = {k: arrs[k] for k in _NAMES if k != "x"}
        m["x"] = np.ascontiguousarray(arrs["x"][i])
        in_maps.append(m)
    res = run_bass_kernel_spmd(nc, in_maps, core_ids=list(range(n)))
    return np.stack([np.asarray(r["out"], dtype=np.float32) for r in res.results], axis=0)
```

```python
import numpy as np
from contextlib import ExitStack
import concourse.bass as bass
import concourse.mybir as mybir
from concourse.bass_utils import run_bass_kernel_spmd

F32 = mybir.dt.float32
BF16 = mybir.dt.bfloat16
AF = mybir.ActivationFunctionType
ALU = mybir.AluOpType
AX = mybir.AxisListType

S = 2048
D = 1024
NT = 16
KC = 8
N_IN = 5400
DFF = 2816
NFB = 22
EPS = 1e-6
SEM_LIMIT = 24000
import os as _os
CUT = int(_os.environ.get('P1B_CUT', '99'))
CUTC = int(_os.environ.get('P1C_CUT', '99'))
SUBC = int(_os.environ.get('P1C_SUB', '99'))
WIDTH = int(_os.environ.get('P1C_WIDTH', '2'))
NEG = -30000.0

C_Q = 0
C_KV = 512
C_G = 1280
C_R = 1304
C_M = 3352


class Buf:
    __slots__ = ("name", "w", "r", "excl")

    def __init__(self, name):
        self.name = name
        self.w = None
        self.r = []
        self.excl = False


class SemW:
    __slots__ = ("h",)

    def __init__(self, h):
        self.h = h


class Slot:
    __slots__ = ("sem", "val")

    def __init__(self, sem):
        self.sem = sem
        self.val = 0


class Q:
    def __init__(self, name, eng):
        self.name = name
        self.eng = eng
        self.sem = None
        self.count = 0
        self.waited = {}
        self.ring = []
        self.ri = 0
        self.pending = False


class T:
    def __init__(self, t, name, nb=1):
        self.t = t
        self.bs = [Buf(f"{name}{i}") for i in range(nb)]

    @property
    def b(self):
        return self.bs[0]


class KB:
    def __init__(self, nc, es):
        self.nc = nc
        self.es = es
        self.nsem = 0
        self.pe = self.mkq("pe", nc.tensor)
        self.act = self.mkq("act", nc.scalar)
        self.dve = self.mkq("dve", nc.vector)
        self.pool = self.mkq("pool", nc.gpsimd)
        self.sp = self.mkq("sp", nc.sync)
        self.qs = [self.pe, self.act, self.dve, self.pool, self.sp]
        for q, n in ((self.sp, 16), (self.pool, 8), (self.act, 4)):
            q.ring = [Slot(self.new_sem(f"{q.name}_d{i}")) for i in range(n)]
        self.out_toks = []

    def new_sem(self, name):
        self.nsem += 1
        return SemW(self.es.enter_context(self.nc.semaphore(f"{name}_{self.nsem}")))

    def mkq(self, name, eng):
        q = Q(name, eng)
        q.sem = self.new_sem(name)
        return q

    def wait(self, q, tok):
        sw, val = tok[0], tok[1]
        if q.waited.get(sw, 0) >= val:
            return
        q.eng.wait_ge(sw.h, val)
        q.waited[sw] = val

    def _dep(self, q, tok, raw, force=False):
        if tok[2] is q and q is self.pe and not force:
            return
        self.wait(q, tok)

    def _deps(self, q, reads, writes, force=False):
        for b in reads:
            if b.w is not None:
                self._dep(q, b.w, True, force)
            if b.excl:
                for t in b.r:
                    if t[2] is not q:
                        self._dep(q, t, False, force)
        for b in writes:
            if b.w is not None:
                self._dep(q, b.w, False, force)
            for t in b.r:
                self._dep(q, t, False, force)

    def _record(self, tok, reads, writes):
        for b in reads:
            if tok[2] is not None:
                b.r = [t for t in b.r if t[2] is not tok[2]]
            b.r.append(tok)
        for b in writes:
            b.w = tok
            b.r = []

    def op(self, q, fn, reads=(), writes=(), inc=True):
        self._deps(q, reads, writes)
        ins = fn()
        if inc:
            if q.count >= SEM_LIMIT and not q.pending:
                q.sem = self.new_sem(q.name)
                q.count = 0
            ins.then_inc(q.sem.h, 1)
            q.count += 1
            q.pending = False
            tok = (q.sem, q.count, q)
        else:
            q.pending = True
            tok = (q.sem, q.count + 1, q)
        self._record(tok, reads, writes)
        return ins

    def dma(self, q, out, in_, reads=(), writes=(), is_out=False):
        self._deps(q, reads, writes, force=True)
        slot = q.ring[q.ri % len(q.ring)]
        q.ri += 1
        if slot.val > 0:
            self.wait(q, (slot.sem, slot.val))
        if slot.val >= SEM_LIMIT:
            slot.sem = self.new_sem(q.name + "_d")
            slot.val = 0
        ins = q.eng.dma_start(out=out, in_=in_)
        ins.then_inc(slot.sem.h, 16)
        slot.val += 16
        tok = (slot.sem, slot.val, None)
        self._record(tok, reads, writes)
        if is_out:
            self.out_toks.append(tok)
        return tok

    def barrier(self):
        toks = []
        for o in self.qs:
            if o.count > 0:
                toks.append((o.sem, o.count, o))
            for sl in o.ring:
                if sl.val > 0:
                    toks.append((sl.sem, sl.val, None))
        for q in self.qs:
            for t in toks:
                if t[2] is q:
                    continue
                self.wait(q, t)

    def finish(self):
        for t in self.out_toks:
            self.wait(self.sp, t)


class _Stop(Exception):
    pass


def build_nc(debug=None, stop=None):
    nc = bass.Bass("TRN2", target_bir_lowering=False)

    def din(name, shape):
        return nc.dram_tensor(name, list(shape), F32, kind="ExternalInput").ap()

    x = din("x", [S, D])
    norm1_g = din("norm1_g", [1, D])
    w_in = din("w_in", [1, D, N_IN])
    nsa_q_norm = din("nsa_q_norm", [1, 64])
    nsa_k_norm = din("nsa_k_norm", [1, 3, 64])
    cmp_pe = din("cmp_pe", [1, 2, 32, 64])
    cmp_w1 = din("cmp_w1", [1, 2, 32, 64, 128])
    cmp_w2 = din("cmp_w2", [1, 2, 128, 64])
    ret_gn_g = din("ret_gn_g", [1, 4, 128])
    w_branch = din("w_branch", [1, 2, 512, D])
    w_out = din("w_out", [1, D, D])
    norm2_g = din("norm2_g", [1, D])
    ffn_w_gate = din("ffn_w_gate", [1, D, DFF])
    ffn_w_up = din("ffn_w_up", [1, D, DFF])
    ffn_w_down = din("ffn_w_down", [1, DFF, D])
    out = nc.dram_tensor("out", [S, D], F32, kind="ExternalOutput").ap()
    xmid = nc.dram_tensor("xmid", [S, D], F32, kind="Internal").ap()
    dbg = {}
    if debug:
        for name, shape in debug.items():
            dbg[name] = nc.dram_tensor("dbg_" + name, list(shape), F32, kind="ExternalOutput").ap()

    w_in_v = w_in[0].rearrange("(k p) n -> p k n", p=128)

    try:
        with ExitStack() as es:
            kb = KB(nc, es)

            def chk(n):
                if stop is not None and n >= stop:
                    kb.barrier()
                    kb.finish()
                    raise _Stop()
            PE, ACT, DVE, POOL, SP = kb.pe, kb.act, kb.dve, kb.pool, kb.sp
            V, A, G, TE = nc.vector, nc.scalar, nc.gpsimd, nc.tensor

            def sb(scope, name, shape, dt, nb=1):
                return T(scope.enter_context(nc.sbuf_tensor(name, list(shape), dt)), name, nb)

            PS = [T(es.enter_context(nc.psum_tensor(f"ps{i}", [128, 512], F32)), f"ps{i}") for i in range(8)]
            for p_ in PS:
                p_.b.excl = True

            def pf(i):
                return PS[i].t[:]

            def pb(i):
                return PS[i].t[:].bitcast(BF16)

            ident_f = sb(es, "ident_f", [128, 128], F32)
            ident_b = sb(es, "ident_b", [128, 128], BF16)
            ones_f = sb(es, "ones_f", [128, 128], F32)
            hT = sb(es, "hT", [128, KC, S], BF16, NT)
            stat = sb(es, "stat", [128, NT, 4], F32, NT)
            hb = [sb(es, f"hb{i}", [128, D], BF16) for i in range(2)]
            junk = sb(es, "junk", [128, D], BF16)

            kb.op(POOL, lambda: G.memset(ones_f.t[:], 1.0), writes=[ones_f.b])
            kb.op(POOL, lambda: G.affine_select(out=ident_f.t[:], in_=ones_f.t[:, 0:128], pattern=[[1, 128]],
                                                compare_op=ALU.is_equal, fill=0.0, base=0, channel_multiplier=-1),
                  reads=[ones_f.b], writes=[ident_f.b])
            kb.op(DVE, lambda: V.tensor_copy(out=ident_b.t[:], in_=ident_f.t[:]), reads=[ident_f.b], writes=[ident_b.b])

            def rstd_from_ss(ss_ap, ms_ap, sd_ap, rs_ap, n, bufs):
                kb.op(DVE, lambda: V.tensor_scalar(out=ms_ap, in0=ss_ap, scalar1=1.0 / n, scalar2=EPS,
                                                   op0=ALU.mult, op1=ALU.add), reads=bufs, writes=bufs)
                kb.op(ACT, lambda: A.activation(out=sd_ap, in_=ms_ap, func=AF.Sqrt), reads=bufs, writes=bufs)
                kb.op(DVE, lambda: V.reciprocal(out=rs_ap, in_=sd_ap), reads=bufs, writes=bufs)

            def transposes(src_aps, bank, reads):
                pbv = pb(bank)
                n = len(src_aps)
                for i, ap in enumerate(src_aps):
                    kb.op(PE, lambda ap=ap, i=i: TE.transpose(out=pbv[:, i * 128:(i + 1) * 128], in_=ap, identity=ident_b.t[:]),
                          reads=list(reads) + [ident_b.b], writes=[PS[bank].b], inc=(i == n - 1))

            def norm_gen(src, c, gbc, sidx, bank):
                sbuf_ = [stat.bs[c]]
                st = stat.t
                kb.op(ACT, lambda: A.activation(out=junk.t[:], in_=src.t[:], func=AF.Square, accum_out=st[:, c, 0:1]),
                      reads=[src.b], writes=[junk.b] + sbuf_)
                yield
                kb.op(DVE, lambda: V.tensor_scalar(out=st[:, c, 1:2], in0=st[:, c, 0:1], scalar1=1.0 / D, scalar2=EPS,
                                                   op0=ALU.mult, op1=ALU.add), reads=sbuf_, writes=sbuf_)
                yield
                kb.op(ACT, lambda: A.activation(out=st[:, c, 2:3], in_=st[:, c, 1:2], func=AF.Sqrt), reads=sbuf_, writes=sbuf_)
                yield
                kb.op(DVE, lambda: V.reciprocal(out=st[:, c, 3:4], in_=st[:, c, 2:3]), reads=sbuf_, writes=sbuf_)
                yield
                h = hb[sidx % 2]
                kb.op(DVE, lambda: V.scalar_tensor_tensor(out=h.t[:], in0=src.t[:], scalar=st[:, c, 3:4], in1=gbc.t[:],
                                                          op0=ALU.mult, op1=ALU.mult),
                      reads=[src.b, gbc.b] + sbuf_, writes=[h.b])
                yield
                transposes([h.t[:, k * 128:(k + 1) * 128] for k in range(KC)], bank, [h.b])
                yield
                kb.op(ACT, lambda: A.copy(out=hT.t[:, :, c * 128:(c + 1) * 128],
                                          in_=pb(bank).rearrange("p (a b) -> p a b", b=128)),
                      reads=[PS[bank].b], writes=[hT.bs[c]])
                yield

            def run_interleaved(gen_fns, width=2):
                pending = list(gen_fns)
                active = []
                while pending or active:
                    while pending and len(active) < width:
                        active.append(pending.pop(0)())
                    for g_ in list(active):
                        try:
                            next(g_)
                        except StopIteration:
                            active.remove(g_)

            def load_w(dst, src_ap, q=None):
                kb.dma(q or POOL, dst.t[:], src_ap, writes=[dst.b])

            def dbg_store(name, src_ap, rows, reads):
                if name in dbg:
                    kb.dma(SP, dbg[name][rows], src_ap, reads=reads, is_out=True)

            def phase_1c():
                with ExitStack() as s3:
                    wq = sb(s3, "wq", [128, KC, 512], BF16)
                    load_w(wq, w_in_v[:, :, C_Q:C_Q + 512])
                    wg = sb(s3, "wg", [128, KC, 24], BF16)
                    load_w(wg, w_in_v[:, :, C_G:C_G + 24])
                    sqq = [sb(s3, f"sqq{i}", [128, 512], F32) for i in range(2)]
                    qst = sb(s3, "qst", [128, NT, 32], F32, NT)
                    tmpq = [sb(s3, f"tmpq{i}", [128, 8, 64], F32) for i in range(2)]
                    qaug = [sb(s3, f"qaug{i}", [128, 8, 128], BF16) for i in range(2)]
                    qT = [sb(s3, f"qT{i}", [128, 8, 128], BF16) for i in range(2)]
                    qT2 = qT
                    gate = [sb(s3, f"gate{i}", [128, 24], F32) for i in range(2)]
                    PcT = [[sb(s3, f"PcT{i}{g}", [128, 512], BF16) for g in range(2)] for i in range(2)]
                    scl = [[sb(s3, f"scl{i}", [128, 512], F32)] * 2 for i in range(2)]
                    NPT = 4
                    oTs = [sb(s3, f"oTs{i}", [128, 512], F32) for i in range(2)]
                    PT = [[sb(s3, f"PT{i}_{j}", [128, 512], BF16) for j in range(NPT)] for i in range(2)]
                    num = [sb(s3, f"num{i}", [128, 3, 8, 64], F32) for i in range(2)]
                    den = [sb(s3, f"den{i}", [128, 3, 8], F32) for i in range(2)]
                    rdc = [sb(s3, f"rdc{i}", [128, 8], F32) for i in range(2)]
                    impn = [sb(s3, f"impn{i}", [128, 8, 32], F32) for i in range(2)]
                    imp = [sb(s3, f"imp{i}", [128, 2, 32], F32) for i in range(2)]
                    top8 = [sb(s3, f"top8{i}", [128, 2, 8], F32) for i in range(2)]
                    rd = [sb(s3, f"rd{i}", [128, 3, 8], F32) for i in range(2)]
                    coef = [sb(s3, f"coef{i}", [128, 3, 8], F32) for i in range(2)]
                    oacc = [sb(s3, f"oacc{i}", [128, 8, 64], F32) for i in range(2)]
                    otmp = [sb(s3, f"otmp{i}", [128, 8, 64], F32) for i in range(2)]
                    ytok = [sb(s3, f"ytok{i}", [128, 512], BF16) for i in range(2)]
                    ydbg = sb(s3, "ydbg", [128, 512], F32) if "y_nsa" in dbg else None
                    for i in range(2):
                        kb.op(POOL, lambda i=i: G.memset(qaug[i].t[:], 0.0), writes=[qaug[i].b])
                    ptc = [0, 0]

                    def tile_gen(c):
                        i2 = c % 2
                        base = 4 * i2
                        bZ, bS, bS2, bO = base, base + 1, base + 2, base + 3
                        bX = bZ
                        sbk = [bS, bS2]
                        tok = slice(c * 128, (c + 1) * 128)
                        scn = [0]

                        def pxv():
                            return pf(bX).rearrange("p (h c) -> p h c", h=4)

                        def to_token_major(g, br, ncol):
                            nu, de = num[i2], den[i2]
                            ot = oTs[i2]
                            kb.op(DVE, lambda: V.tensor_scalar(out=ot.t[0:ncol, :], in0=pf(bO)[0:ncol, :], scalar1=1.0, scalar2=None, op0=ALU.mult), reads=[PS[bO].b], writes=[ot.b])
                            yield
                            for hh in range(4):
                                kb.op(PE, lambda hh=hh: TE.transpose(out=pxv()[:, hh, 0:ncol], in_=ot.t[0:ncol, hh * 128:(hh + 1) * 128],
                                                                     identity=ident_f.t[0:ncol, 0:ncol]),
                                      reads=[ot.b, ident_f.b], writes=[PS[bX].b], inc=(hh == 3))
                            yield
                            kb.op(ACT, lambda: A.copy(out=nu.t[:, br, 4 * g:4 * g + 4, :], in_=pxv()[:, :, 0:64]),
                                  reads=[PS[bX].b], writes=[nu.b])
                            kb.op(DVE, lambda: V.tensor_scalar(out=de.t[:, br, 4 * g:4 * g + 4], in0=pxv()[:, :, 64], scalar1=1e-30, scalar2=None, op0=ALU.max),
                                  reads=[PS[bX].b], writes=[de.b])
                            yield

                        def branch(g, br, KT, VV, kts, qsrc):
                            n = len(kts)
                            pts = [None] * n

                            def score(j):
                                kt = kts[j]
                                sb_ = sbk[scn[0] % 2]
                                scn[0] += 1
                                kb.op(PE, lambda: TE.matmul(pf(sb_), lhsT=KT.t[:, g, kt * 128:(kt + 1) * 128],
                                                            rhs=qsrc.t[:, 4 * g:4 * g + 4, :], start=True, stop=True),
                                      reads=[KT.bs[kt], qsrc.b], writes=[PS[sb_].b])
                                pt = PT[i2][ptc[i2] % NPT]
                                ptc[i2] += 1
                                kb.op(ACT, lambda: A.activation(out=pt.t[:], in_=pf(sb_), func=AF.Exp),
                                      reads=[PS[sb_].b], writes=[pt.b])
                                if kt == c:
                                    kb.op(DVE, lambda: V.tensor_tensor(out=pt.t[:], in0=pt.t[:], in1=dmask.t[:].rearrange("p a b -> p (a b)"), op=ALU.mult),
                                          reads=[pt.b, dmask.b], writes=[pt.b])
                                elif br == 2 and kt == c - 4:
                                    kb.op(DVE, lambda: V.tensor_tensor(out=pt.t[:], in0=pt.t[:], in1=tmask.t[:].rearrange("p a b -> p (a b)"), op=ALU.mult),
                                          reads=[pt.b, tmask.b], writes=[pt.b])
                                pts[j] = pt

                            score(0)
                            yield
                            for j, kt in enumerate(kts):
                                if j + 1 < n:
                                    score(j + 1)
                                pt = pts[j]
                                kb.op(PE, lambda kt=kt, pt=pt, j=j: TE.matmul(pf(bO)[0:65, :], lhsT=VV.t[:, kt, g, :], rhs=pt.t[:],
                                                                              start=(j == 0), stop=(j == n - 1)),
                                      reads=[pt.b, VV.bs[kt]], writes=[PS[bO].b], inc=(j == n - 1))
                                yield
                            yield from to_token_major(g, br, 65)

                        for k in range(KC):
                            kb.op(PE, lambda k=k: TE.matmul(pf(bZ), lhsT=hT.t[:, k, tok], rhs=wq.t[:, k, :], start=(k == 0), stop=(k == KC - 1)),
                                  reads=[hT.bs[c], wq.b], writes=[PS[bZ].b], inc=(k == KC - 1))
                        for k in range(KC):
                            kb.op(PE, lambda k=k: TE.matmul(pf(bS)[:, 0:24], lhsT=hT.t[:, k, tok], rhs=wg.t[:, k, :], start=(k == 0), stop=(k == KC - 1)),
                                  reads=[hT.bs[c], wg.b], writes=[PS[bS].b], inc=(k == KC - 1))
                        yield
                        sq = sqq[i2]
                        kb.op(ACT, lambda: A.activation(out=sq.t[:], in_=pf(bZ), func=AF.Square), reads=[PS[bZ].b], writes=[sq.b])
                        gt = gate[i2]
                        kb.op(ACT, lambda: A.activation(out=gt.t[:], in_=pf(bS)[:, 0:24], func=AF.Sigmoid), reads=[PS[bS].b], writes=[gt.b])
                        yield
                        qs = qst.t
                        qsb = [qst.bs[c]]
                        kb.op(DVE, lambda: V.tensor_reduce(out=qs[:, c, 0:8], in_=sq.t[:].rearrange("p (a b) -> p a b", b=64), axis=AX.X, op=ALU.add),
                              reads=[sq.b], writes=qsb)
                        yield
                        kb.op(DVE, lambda: V.tensor_scalar(out=qs[:, c, 8:16], in0=qs[:, c, 0:8], scalar1=1.0 / 64, scalar2=EPS, op0=ALU.mult, op1=ALU.add), reads=qsb, writes=qsb)
                        yield
                        kb.op(ACT, lambda: A.activation(out=qs[:, c, 16:24], in_=qs[:, c, 8:16], func=AF.Sqrt), reads=qsb, writes=qsb)
                        yield
                        kb.op(DVE, lambda: V.reciprocal(out=qs[:, c, 24:32], in_=qs[:, c, 16:24]), reads=qsb, writes=qsb)
                        yield
                        tq = tmpq[i2]
                        qa = qaug[i2]
                        kb.op(DVE, lambda: V.tensor_tensor(out=tq.t[:], in0=pf(bZ).rearrange("p (a b) -> p a b", b=64),
                                                           in1=qs[:, c, 24:32].unsqueeze(2).broadcast_to([128, 8, 64]), op=ALU.mult),
                              reads=[PS[bZ].b] + qsb, writes=[tq.b])
                        yield
                        kb.op(DVE, lambda: V.tensor_tensor(out=qa.t[:, :, 0:64], in0=tq.t[:], in1=gq.t[:].unsqueeze(1).broadcast_to([128, 8, 64]), op=ALU.mult),
                              reads=[tq.b, gq.b], writes=[qa.b])
                        kb.op(POOL, lambda: G.tensor_copy(out=qa.t[:, :, 96:100], in_=QAL.t[:, c, :, :]), reads=[QAL.b], writes=[qa.b])
                        yield
                        transposes([qa.t[:, h, :] for h in range(8)], bZ, [qa.b])
                        yield
                        q1 = qT[i2]
                        kb.op(ACT, lambda: A.copy(out=q1.t[:], in_=pb(bZ).rearrange("p (a b) -> p a b", b=128)), reads=[PS[bZ].b], writes=[q1.b])
                        yield
                        nu, de = num[i2], den[i2]
                        rdc_, impn_, imp_, top8_ = rdc[i2], impn[i2], imp[i2], top8[i2]
                        for g in range(2):
                            pc = PcT[i2][g]
                            sc_ = scl[i2][g]
                            kb.op(PE, lambda g=g: TE.matmul(pf(bS)[0:127, :], lhsT=KcT.t[:, g, 0:127], rhs=q1.t[:, 4 * g:4 * g + 4, :], start=True, stop=True),
                                  reads=[KcT.b, q1.b], writes=[PS[bS].b])
                            yield
                            kb.op(DVE, lambda sc_=sc_: V.scalar_tensor_tensor(out=sc_.t[0:127, :].rearrange("p (a b) -> p a b", b=128),
                                                                              in0=pf(bS)[0:127, :].rearrange("p (a b) -> p a b", b=128), scalar=60.0,
                                                                              in1=cmask.t[0:127, tok].unsqueeze(1).broadcast_to([127, 4, 128]),
                                                                              op0=ALU.min, op1=ALU.add),
                                  reads=[PS[bS].b, cmask.b], writes=[sc_.b])
                            yield
                            kb.op(ACT, lambda pc=pc, sc_=sc_: A.activation(out=pc.t[0:127, :], in_=sc_.t[0:127, :], func=AF.Exp),
                                  reads=[sc_.b], writes=[pc.b])
                            yield
                            kb.op(PE, lambda g=g, pc=pc: TE.matmul(pf(bO)[0:97, :], lhsT=Vc.t[0:127, g, :], rhs=pc.t[0:127, :], start=True, stop=True),
                                  reads=[pc.b, Vc.b], writes=[PS[bO].b])
                            yield
                            yield from to_token_major(g, 0, 97)
                            kb.op(DVE, lambda g=g: V.reciprocal(out=rdc_.t[:, 4 * g:4 * g + 4], in_=de.t[:, 0, 4 * g:4 * g + 4]), reads=[de.b], writes=[rdc_.b])
                            yield
                            kb.op(DVE, lambda g=g: V.tensor_tensor(out=impn_.t[:, 4 * g:4 * g + 4, :], in0=pxv()[:, :, 65:97],
                                                                   in1=rdc_.t[:, 4 * g:4 * g + 4].unsqueeze(2).broadcast_to([128, 4, 32]), op=ALU.mult),
                                  reads=[PS[bX].b, rdc_.b], writes=[impn_.b])
                            yield
                        kb.op(DVE, lambda: V.tensor_reduce(out=imp_.t[:], in_=impn_.t[:].rearrange("p (g r) j -> p g j r", g=2), axis=AX.X, op=ALU.add),
                              reads=[impn_.b], writes=[imp_.b])
                        yield
                        kb.op(DVE, lambda: V.tensor_tensor(out=imp_.t[:], in0=imp_.t[:], in1=addc.t[:, c:c + 1, :].broadcast_to([128, 2, 32]), op=ALU.add),
                              reads=[imp_.b, addc.b], writes=[imp_.b])
                        yield
                        for g in range(2):
                            kb.op(DVE, lambda g=g: V.max(out=top8_.t[:, g, :], in_=imp_.t[:, g, :]), reads=[imp_.b], writes=[top8_.b])
                        yield
                        for g in range(2):
                            kb.op(DVE, lambda g=g: V.tensor_scalar(out=qa.t[:, 4 * g:4 * g + 4, 64:96],
                                                                   in0=imp_.t[:, g:g + 1, :].broadcast_to([128, 4, 32]),
                                                                   scalar1=top8_.t[:, g, 7:8], scalar2=NEG, op0=ALU.is_lt, op1=ALU.mult),
                                  reads=[imp_.b, top8_.b], writes=[qa.b])
                        yield
                        transposes([qa.t[:, h, :] for h in range(8)], bZ, [qa.b])
                        yield
                        q2 = qT2[i2]
                        kb.op(ACT, lambda: A.copy(out=q2.t[:], in_=pb(bZ).rearrange("p (a b) -> p a b", b=128)), reads=[PS[bZ].b], writes=[q2.b])
                        yield
                        for g in range(2):
                            yield from branch(g, 1, KT_slc, V_slc, list(range(0, c + 1)), q2)
                            yield from branch(g, 2, KT_win, V_win, list(range(max(0, c - 4), c + 1)), q2)
                        rd_, coef_, oacc_, otmp_ = rd[i2], coef[i2], oacc[i2], otmp[i2]
                        kb.op(DVE, lambda: V.reciprocal(out=rd_.t[:], in_=de.t[:]), reads=[de.b], writes=[rd_.b])
                        yield
                        kb.op(DVE, lambda: V.tensor_tensor(out=coef_.t[:], in0=gt.t[:].rearrange("p (h b) -> p b h", b=3), in1=rd_.t[:], op=ALU.mult),
                              reads=[gt.b, rd_.b], writes=[coef_.b])
                        yield
                        kb.op(DVE, lambda: V.tensor_tensor(out=oacc_.t[:], in0=nu.t[:, 0], in1=coef_.t[:, 0, :].unsqueeze(2).broadcast_to([128, 8, 64]), op=ALU.mult),
                              reads=[nu.b, coef_.b], writes=[oacc_.b])
                        kb.op(POOL, lambda: G.tensor_tensor(out=otmp_.t[:], in0=nu.t[:, 1], in1=coef_.t[:, 1, :].unsqueeze(2).broadcast_to([128, 8, 64]), op=ALU.mult),
                              reads=[nu.b, coef_.b], writes=[otmp_.b])
                        yield
                        kb.op(DVE, lambda: V.tensor_tensor(out=oacc_.t[:], in0=oacc_.t[:], in1=otmp_.t[:], op=ALU.add), reads=[oacc_.b, otmp_.b], writes=[oacc_.b])
                        yield
                        kb.op(POOL, lambda: G.tensor_tensor(out=otmp_.t[:], in0=nu.t[:, 2], in1=coef_.t[:, 2, :].unsqueeze(2).broadcast_to([128, 8, 64]), op=ALU.mult),
                              reads=[nu.b, coef_.b], writes=[otmp_.b])
                        yield
                        yt = ytok[i2]
                        kb.op(DVE, lambda: V.tensor_tensor(out=yt.t[:], in0=oacc_.t[:].rearrange("p a b -> p (a b)"), in1=otmp_.t[:].rearrange("p a b -> p (a b)"), op=ALU.add),
                              reads=[oacc_.b, otmp_.b], writes=[yt.b])
                        if ydbg is not None:
                            kb.op(POOL, lambda: G.tensor_tensor(out=ydbg.t[:], in0=oacc_.t[:].rearrange("p a b -> p (a b)"), in1=otmp_.t[:].rearrange("p a b -> p (a b)"), op=ALU.add),
                                  reads=[oacc_.b, otmp_.b], writes=[ydbg.b])
                            dbg_store("y_nsa", ydbg.t[:], tok, [ydbg.b])
                        yield
                        transposes([yt.t[:, k * 128:(k + 1) * 128] for k in range(4)], bZ, [yt.b])
                        yield
                        kb.op(ACT, lambda: A.copy(out=ynsaT.t[:, :, tok], in_=pb(bZ)[:, 0:512].rearrange("p (a b) -> p a b", b=128)),
                              reads=[PS[bZ].b], writes=[ynsaT.bs[c]])
                        yield

                    run_interleaved([(lambda c=c: tile_gen(c)) for c in range(NT)], width=WIDTH)

            def phase_1d():
                with ExitStack() as s4:
                    wr = sb(s4, "wr", [128, KC, 2048], BF16, 4)
                    for j in range(4):
                        kb.dma(POOL, wr.t[:, :, j * 512:(j + 1) * 512], w_in_v[:, :, C_R + j * 512:C_R + (j + 1) * 512], writes=[wr.bs[j]])
                    idT = sb(s4, "idT", [128, 4, 128], F32)
                    qdec = sb(s4, "qdec", [128, 4, 128], F32)
                    kdec = sb(s4, "kdec", [128, 4, 128], F32)
                    gn = sb(s4, "gn", [128, 512], F32)
                    eij = sb(s4, "eij", [128, 128], F32)
                    rowq = sb(s4, "rowq", [128, 128], F32)
                    rowk = sb(s4, "rowk", [128, 128], F32)
                    kb.dma(SP, gn.t[:], ret_gn_g.rearrange("o a b -> o (a b)").broadcast_to([128, 512]), writes=[gn.b])
                    kb.op(POOL, lambda: G.iota(eij.t[:], pattern=[[1, 128]], base=0, channel_multiplier=-1, allow_small_or_imprecise_dtypes=True), writes=[eij.b])
                    kb.op(POOL, lambda: G.iota(rowq.t[:], pattern=[[1, 128]], base=1, channel_multiplier=0, allow_small_or_imprecise_dtypes=True), writes=[rowq.b])
                    kb.op(POOL, lambda: G.iota(rowk.t[:], pattern=[[-1, 128]], base=127, channel_multiplier=0, allow_small_or_imprecise_dtypes=True), writes=[rowk.b])
                    lgs = [float(np.log(1.0 - 2.0 ** (-5.0 - h))) for h in range(4)]
                    cds = [float(np.exp(128.0 * np.float32(lg))) for lg in lgs]
                    for h in range(4):
                        kb.op(ACT, lambda h=h: A.activation(out=idT.t[:, h, :], in_=eij.t[:], func=AF.Exp, scale=lgs[h]), reads=[eij.b], writes=[idT.b])
                        kb.op(POOL, lambda h=h: G.affine_select(out=idT.t[:, h, :], in_=idT.t[:, h, :], pattern=[[1, 128]], compare_op=ALU.is_ge, fill=0.0,
                                                                base=0, channel_multiplier=-1), reads=[idT.b], writes=[idT.b])
                        kb.op(ACT, lambda h=h: A.activation(out=qdec.t[:, h, :], in_=rowq.t[:], func=AF.Exp, scale=lgs[h]), reads=[rowq.b], writes=[qdec.b])
                        kb.op(ACT, lambda h=h: A.activation(out=kdec.t[:, h, :], in_=rowk.t[:], func=AF.Exp, scale=lgs[h]), reads=[rowk.b], writes=[kdec.b])
                    qTr = sb(s4, "qTr", [128, 4, 512], BF16)
                    qdT = sb(s4, "qdT", [128, 4, 512], BF16)
                    kTr = sb(s4, "kTr", [128, 4, 512], BF16)
                    kdT = sb(s4, "kdT", [128, 4, 512], BF16)
                    v_sb = [sb(s4, f"v_sb{i}", [128, 4, 128], BF16) for i in range(2)]
                    sgl = [sb(s4, f"sgl{i}", [128, 512], F32) for i in range(2)]
                    kd = [sb(s4, f"kd{i}", [128, 4, 128], BF16) for i in range(2)]
                    attb = [sb(s4, f"attb{i}", [128, 4, 128], BF16) for i in range(2)]
                    state_f = sb(s4, "state_f", [128, 4, 128], F32)
                    state_b = sb(s4, "state_b", [128, 4, 128], BF16)
                    bst = sb(s4, "bst", [128, 4, 6], F32)
                    mv = sb(s4, "mv", [128, 4, 2], F32)
                    rs4 = sb(s4, "rs4", [128, 12], F32)
                    on = sb(s4, "on", [128, 512], F32)
                    yr = [sb(s4, f"yr{i}", [128, 512], BF16) for i in range(2)]
                    yrdbg = sb(s4, "yrdbg", [128, 512], F32) if "y_ret" in dbg else None
                    kb.op(POOL, lambda: G.memset(state_f.t[:], 0.0), writes=[state_f.b])
                    kb.op(POOL, lambda: G.memset(state_b.t[:], 0.0), writes=[state_b.b])
                    KS = float(128.0 ** -0.5)
                    pcnt = [0]
                    for tg in range(4):
                        tks = slice(tg * 512, (tg + 1) * 512)
                        hbs = [hT.bs[4 * tg + i] for i in range(4)]
                        for qk in range(2):
                            for h in range(4):
                                bk = pcnt[0] % 2
                                pcnt[0] += 1
                                for k in range(KC):
                                    kb.op(PE, lambda k=k, qk=qk, h=h, bk=bk: TE.matmul(pf(bk), lhsT=wr.t[:, k, qk * 512 + h * 128:qk * 512 + (h + 1) * 128],
                                                                                      rhs=hT.t[:, k, tks], start=(k == 0), stop=(k == KC - 1)),
                                          reads=hbs + [wr.bs[qk]], writes=[PS[bk].b], inc=(k == KC - 1))
                                pv4 = pf(bk).rearrange("p (a b) -> p a b", b=128)
                                if qk == 0:
                                    kb.op(ACT, lambda h=h, bk=bk: A.copy(out=qTr.t[:, h, :], in_=pf(bk)), reads=[PS[bk].b], writes=[qTr.b])
                                    kb.op(DVE, lambda h=h, pv4=pv4: V.tensor_tensor(out=qdT.t[:, h, :].rearrange("p (a b) -> p a b", b=128), in0=pv4,
                                                                                    in1=qdec.t[:, h:h + 1, :].broadcast_to([128, 4, 128]), op=ALU.mult),
                                          reads=[PS[bk].b, qdec.b], writes=[qdT.b])
                                else:
                                    kb.op(ACT, lambda h=h, bk=bk: A.mul(out=kTr.t[:, h, :], in_=pf(bk), mul=KS), reads=[PS[bk].b], writes=[kTr.b])
                                    kb.op(DVE, lambda h=h, pv4=pv4: V.scalar_tensor_tensor(out=kdT.t[:, h, :].rearrange("p (a b) -> p a b", b=128), in0=pv4, scalar=KS,
                                                                                           in1=kdec.t[:, h:h + 1, :].broadcast_to([128, 4, 128]),
                                                                                           op0=ALU.mult, op1=ALU.mult),
                                          reads=[PS[bk].b, kdec.b], writes=[kdT.b])
                        for cl in range(4):
                            c = 4 * tg + cl
                            i2 = c % 2
                            tok = slice(c * 128, (c + 1) * 128)
                            cs = slice(cl * 128, (cl + 1) * 128)
                            for k in range(KC):
                                kb.op(PE, lambda k=k: TE.matmul(pf(2), lhsT=hT.t[:, k, tok], rhs=wr.t[:, k, 1024:1536], start=(k == 0), stop=(k == KC - 1)),
                                      reads=[hT.bs[c], wr.bs[2]], writes=[PS[2].b], inc=(k == KC - 1))
                            for k in range(KC):
                                kb.op(PE, lambda k=k: TE.matmul(pf(3), lhsT=hT.t[:, k, tok], rhs=wr.t[:, k, 1536:2048], start=(k == 0), stop=(k == KC - 1)),
                                      reads=[hT.bs[c], wr.bs[3]], writes=[PS[3].b], inc=(k == KC - 1))
                            vs, sg_, kd_, ab = v_sb[i2], sgl[i2], kd[i2], attb[i2]
                            kb.op(ACT, lambda: A.copy(out=vs.t[:].rearrange("p a b -> p (a b)"), in_=pf(2)), reads=[PS[2].b], writes=[vs.b])
                            kb.op(ACT, lambda: A.activation(out=sg_.t[:], in_=pf(3), func=AF.Silu), reads=[PS[3].b], writes=[sg_.b])
                            transposes([kdT.t[:, h, cs] for h in range(4)], 4, [kdT.b])
                            kb.op(ACT, lambda: A.copy(out=kd_.t[:].rearrange("p a b -> p (a b)"), in_=pb(4)[:, 0:512]), reads=[PS[4].b], writes=[kd_.b])
                            for h in range(4):
                                kb.op(PE, lambda h=h: TE.matmul(pf(5)[:, h * 128:(h + 1) * 128], lhsT=kTr.t[:, h, cs], rhs=qTr.t[:, h, cs], start=True, stop=True),
                                      reads=[kTr.b, qTr.b], writes=[PS[5].b], inc=(h == 3))
                            kb.op(DVE, lambda: V.tensor_tensor(out=ab.t[:], in0=pf(5).rearrange("p (a b) -> p a b", b=128), in1=idT.t[:], op=ALU.mult),
                                  reads=[PS[5].b, idT.b], writes=[ab.b])
                            for h in range(4):
                                kb.op(PE, lambda h=h: TE.matmul(pf(6)[:, h * 128:(h + 1) * 128], lhsT=ab.t[:, h, :], rhs=vs.t[:, h, :], start=True, stop=(c == 0)),
                                      reads=[ab.b, vs.b], writes=[PS[6].b], inc=(c == 0 and h == 3))
                                if c > 0:
                                    kb.op(PE, lambda h=h: TE.matmul(pf(6)[:, h * 128:(h + 1) * 128], lhsT=qdT.t[:, h, cs], rhs=state_b.t[:, h, :], start=False, stop=True),
                                          reads=[qdT.b, state_b.b], writes=[PS[6].b], inc=(h == 3))
                            if c < NT - 1:
                                for h in range(4):
                                    kb.op(PE, lambda h=h: TE.matmul(pf(7)[:, h * 128:(h + 1) * 128], lhsT=kd_.t[:, h, :], rhs=vs.t[:, h, :], start=True, stop=True),
                                          reads=[kd_.b, vs.b], writes=[PS[7].b], inc=(h == 3))
                                for h in range(4):
                                    kb.op(DVE, lambda h=h: V.scalar_tensor_tensor(out=state_f.t[:, h, :], in0=state_f.t[:, h, :], scalar=cds[h],
                                                                                  in1=pf(7)[:, h * 128:(h + 1) * 128], op0=ALU.mult, op1=ALU.add),
                                          reads=[state_f.b, PS[7].b], writes=[state_f.b])
                                kb.op(POOL, lambda: G.tensor_copy(out=state_b.t[:], in_=state_f.t[:]), reads=[state_f.b], writes=[state_b.b])
                            for h in range(4):
                                kb.op(DVE, lambda h=h: V.bn_stats(out=bst.t[:, h, :], in_=pf(6)[:, h * 128:(h + 1) * 128]), reads=[PS[6].b], writes=[bst.b])
                                kb.op(DVE, lambda h=h: V.bn_aggr(out=mv.t[:, h, :], in_=bst.t[:, h, :]), reads=[bst.b], writes=[mv.b])
                            kb.op(DVE, lambda: V.tensor_scalar(out=rs4.t[:, 0:4], in0=mv.t[:, :, 1], scalar1=EPS, scalar2=None, op0=ALU.add), reads=[mv.b], writes=[rs4.b])
                            kb.op(ACT, lambda: A.activation(out=rs4.t[:, 4:8], in_=rs4.t[:, 0:4], func=AF.Sqrt), reads=[rs4.b], writes=[rs4.b])
                            kb.op(DVE, lambda: V.reciprocal(out=rs4.t[:, 8:12], in_=rs4.t[:, 4:8]), reads=[rs4.b], writes=[rs4.b])
                            for h in range(4):
                                kb.op(DVE, lambda h=h: V.tensor_scalar(out=on.t[:, h * 128:(h + 1) * 128], in0=pf(6)[:, h * 128:(h + 1) * 128],
                                                                       scalar1=mv.t[:, h, 0:1], scalar2=rs4.t[:, 8 + h:9 + h], op0=ALU.subtract, op1=ALU.mult),
                                      reads=[PS[6].b, mv.b, rs4.b], writes=[on.b])
                            kb.op(POOL, lambda: G.tensor_tensor(out=on.t[:], in0=on.t[:], in1=gn.t[:], op=ALU.mult), reads=[on.b, gn.b], writes=[on.b])
                            y_ = yr[i2]
                            kb.op(DVE, lambda: V.tensor_tensor(out=y_.t[:], in0=on.t[:], in1=sg_.t[:], op=ALU.mult), reads=[on.b, sg_.b], writes=[y_.b])
                            if yrdbg is not None:
                                kb.op(DVE, lambda: V.tensor_tensor(out=yrdbg.t[:], in0=on.t[:], in1=sg_.t[:], op=ALU.mult), reads=[on.b, sg_.b], writes=[yrdbg.b])
                                dbg_store("y_ret", yrdbg.t[:], tok, [yrdbg.b])
                            transposes([y_.t[:, k * 128:(k + 1) * 128] for k in range(4)], 4, [y_.b])
                            kb.op(ACT, lambda: A.copy(out=yretT.t[:, :, tok], in_=pb(4)[:, 0:512].rearrange("p (a b) -> p a b", b=128)),
                                  reads=[PS[4].b], writes=[yretT.bs[c]])

            def phase_1e():
                with ExitStack() as s5:
                    xt = [sb(s5, f"xte{i}", [128, D], F32) for i in range(2)]
                    g2bc = sb(s5, "g2bc", [128, D], F32)
                    kb.dma(SP, g2bc.t[:], norm2_g[0:1, :].broadcast_to([128, D]), writes=[g2bc.b])
                    wm = sb(s5, "wm", [128, KC, 2048], BF16, 4)
                    for j in range(4):
                        kb.dma(POOL, wm.t[:, :, j * 512:(j + 1) * 512], w_in_v[:, :, C_M + j * 512:C_M + (j + 1) * 512], writes=[wm.bs[j]])
                    wbr = sb(s5, "wbr", [128, 8, D], BF16)
                    load_w(wbr, w_branch[0].rearrange("n (k p) d -> p (n k) d", p=128))
                    wo = sb(s5, "wo", [128, KC, D], BF16)
                    load_w(wo, w_out[0].rearrange("(k p) d -> p k d", p=128))
                    gates = [sb(s5, f"gates{i}", [128, D], F32, 2) for i in range(2)]
                    tmix = [sb(s5, f"tmix{i}", [128, D], F32) for i in range(2)]
                    tmix2 = [sb(s5, f"tmix2{i}", [128, D], F32) for i in range(2)]
                    mixed = [sb(s5, f"mixed{i}", [128, D], BF16) for i in range(2)]
                    mixT = [sb(s5, f"mixT{i}", [128, KC, 128], BF16) for i in range(2)]
                    x1t = [sb(s5, f"x1t{i}", [128, D], F32) for i in range(2)]

                    def p1e_gen(c):
                        i2 = c % 2
                        bs_ = [4 * i2 + i for i in range(4)]
                        tok = slice(c * 128, (c + 1) * 128)
                        xx = xt[i2]
                        kb.dma(SP, xx.t[:], x[tok, :], writes=[xx.b])
                        gt_, tm = gates[i2], (tmix[i2], tmix2[i2])
                        for n, yT in ((0, ynsaT), (1, yretT)):
                            for half in range(2):
                                j = 2 * n + half
                                for k in range(KC):
                                    kb.op(PE, lambda j=j, k=k, half=half: TE.matmul(pf(bs_[half]), lhsT=hT.t[:, k, tok], rhs=wm.t[:, k, j * 512:(j + 1) * 512],
                                                                                    start=(k == 0), stop=(k == KC - 1)),
                                          reads=[hT.bs[c], wm.bs[j]], writes=[PS[bs_[half]].b], inc=(k == KC - 1))
                                yield
                            for half in range(2):
                                bk = bs_[2 + half]
                                for k in range(4):
                                    kb.op(PE, lambda n=n, half=half, k=k, bk=bk, yT=yT: TE.matmul(pf(bk), lhsT=yT.t[:, k, tok], rhs=wbr.t[:, n * 4 + k, half * 512:(half + 1) * 512],
                                                                                                  start=(k == 0), stop=(k == 3)),
                                          reads=[yT.bs[c], wbr.b], writes=[PS[bk].b], inc=(k == 3))
                                yield
                            for half in range(2):
                                kb.op(ACT, lambda half=half: A.activation(out=gt_.t[:, half * 512:(half + 1) * 512], in_=pf(bs_[half]), func=AF.Sigmoid),
                                      reads=[PS[bs_[half]].b], writes=[gt_.bs[half]])
                                yield
                            for half in range(2):
                                hs = slice(half * 512, (half + 1) * 512)
                                kb.op(DVE, lambda half=half, hs=hs, n=n: V.tensor_tensor(out=tm[n].t[:, hs], in0=gt_.t[:, hs], in1=pf(bs_[2 + half]), op=ALU.mult),
                                      reads=[gt_.bs[half], PS[bs_[2 + half]].b], writes=[tm[n].b])
                                yield
                        mx = mixed[i2]
                        kb.op(POOL, lambda: G.tensor_tensor(out=mx.t[:], in0=tm[0].t[:], in1=tm[1].t[:], op=ALU.add), reads=[tm[0].b, tm[1].b], writes=[mx.b])
                        yield
                        transposes([mx.t[:, k * 128:(k + 1) * 128] for k in range(KC)], bs_[0], [mx.b])
                        yield
                        mt = mixT[i2]
                        kb.op(ACT, lambda: A.copy(out=mt.t[:], in_=pb(bs_[0]).rearrange("p (a b) -> p a b", b=128)), reads=[PS[bs_[0]].b], writes=[mt.b])
                        yield
                        for half in range(2):
                            for k in range(KC):
                                kb.op(PE, lambda half=half, k=k: TE.matmul(pf(bs_[2 + half]), lhsT=mt.t[:, k, :], rhs=wo.t[:, k, half * 512:(half + 1) * 512],
                                                                           start=(k == 0), stop=(k == KC - 1)),
                                      reads=[mt.b, wo.b], writes=[PS[bs_[2 + half]].b], inc=(k == KC - 1))
                            yield
                        x1 = x1t[i2]
                        for half in range(2):
                            hs = slice(half * 512, (half + 1) * 512)
                            kb.op(DVE, lambda half=half, hs=hs: V.tensor_tensor(out=x1.t[:, hs], in0=xx.t[:, hs], in1=pf(bs_[2 + half]), op=ALU.add),
                                  reads=[xx.b, PS[bs_[2 + half]].b], writes=[x1.b])
                            yield
                        kb.dma(SP, xmid[tok, :], x1.t[:], reads=[x1.b])
                        dbg_store("x1", x1.t[:], tok, [x1.b])
                        yield from norm_gen(x1, c, g2bc, i2, bs_[1])

                    run_interleaved([(lambda c=c: p1e_gen(c)) for c in range(NT)], width=2)

            def phase_2():
                with ExitStack() as s6:
                    xt = [sb(s6, f"xtf{i}", [128, D], F32) for i in range(2)]
                    wd = sb(s6, "wd", [128, NFB, D], BF16, 2)
                    wd_v = ffn_w_down[0].rearrange("(fb p) d -> p fb d", p=128)
                    kb.dma(POOL, wd.t[:, 0:11, :], wd_v[:, 0:11, :], writes=[wd.bs[0]])
                    kb.dma(POOL, wd.t[:, 11:22, :], wd_v[:, 11:22, :], writes=[wd.bs[1]])
                    wgs = [sb(s6, f"wgs{i}", [128, KC, 256], BF16) for i in range(2)]
                    wus = [sb(s6, f"wus{i}", [128, KC, 256], BF16) for i in range(2)]
                    act = sb(s6, "act", [128, NFB, 1024], BF16, NFB)
                    sgs = [sb(s6, f"sgs{i}", [128, 512], F32) for i in range(2)]
                    outt = [sb(s6, f"outt{i}", [128, D], F32) for i in range(2)]
                    wg_v = ffn_w_gate[0].rearrange("(k p) f -> p k f", p=128)
                    wu_v = ffn_w_up[0].rearrange("(k p) f -> p k f", p=128)
                    cn = [0, 0]
                    for hf in range(2):
                        for fg in range(11):
                            cols = slice(fg * 256, (fg + 1) * 256)
                            wg_, wu_ = wgs[fg % 2], wus[fg % 2]
                            kb.dma(POOL, wg_.t[:], wg_v[:, :, cols], writes=[wg_.b])
                            kb.dma(POOL, wu_.t[:], wu_v[:, :, cols], writes=[wu_.b])
                            for fl in range(2):
                                fb = fg * 2 + fl
                                for t2 in range(2):
                                    tokc = slice(hf * 1024 + t2 * 512, hf * 1024 + (t2 + 1) * 512)
                                    hbs = [hT.bs[hf * 8 + t2 * 4 + i] for i in range(4)]
                                    gb, ub = (0, 1) if cn[0] % 2 == 0 else (2, 3)
                                    cn[0] += 1
                                    for k in range(KC):
                                        kb.op(PE, lambda k=k, fl=fl, gb=gb, wg_=wg_, tokc=tokc: TE.matmul(pf(gb), lhsT=wg_.t[:, k, fl * 128:(fl + 1) * 128], rhs=hT.t[:, k, tokc],
                                                                                                          start=(k == 0), stop=(k == KC - 1)),
                                              reads=hbs + [wg_.b], writes=[PS[gb].b], inc=(k == KC - 1))
                                    for k in range(KC):
                                        kb.op(PE, lambda k=k, fl=fl, ub=ub, wu_=wu_, tokc=tokc: TE.matmul(pf(ub), lhsT=wu_.t[:, k, fl * 128:(fl + 1) * 128], rhs=hT.t[:, k, tokc],
                                                                                                          start=(k == 0), stop=(k == KC - 1)),
                                              reads=hbs + [wu_.b], writes=[PS[ub].b], inc=(k == KC - 1))
                                    sg_ = sgs[cn[0] % 2]
                                    kb.op(ACT, lambda sg_=sg_, gb=gb: A.activation(out=sg_.t[:], in_=pf(gb), func=AF.Silu), reads=[PS[gb].b], writes=[sg_.b])
                                    kb.op(DVE, lambda sg_=sg_, ub=ub, fb=fb, t2=t2: V.tensor_tensor(out=act.t[:, fb, t2 * 512:(t2 + 1) * 512], in0=sg_.t[:], in1=pf(ub), op=ALU.mult),
                                          reads=[sg_.b, PS[ub].b], writes=[act.bs[fb]])
                        for tl in range(8):
                            c = hf * 8 + tl
                            tok = slice(c * 128, (c + 1) * 128)
                            xx = xt[c % 2]
                            kb.dma(SP, xx.t[:], xmid[tok, :], writes=[xx.b])
                            ob = (4, 5) if cn[1] % 2 == 0 else (6, 7)
                            cn[1] += 1
                            for half in range(2):
                                for fb in range(NFB):
                                    kb.op(PE, lambda half=half, fb=fb, ob=ob, tl=tl: TE.matmul(pf(ob[half]), lhsT=act.t[:, fb, tl * 128:(tl + 1) * 128],
                                                                                               rhs=wd.t[:, fb, half * 512:(half + 1) * 512],
                                                                                               start=(fb == 0), stop=(fb == NFB - 1)),
                                          reads=[act.bs[fb], wd.bs[0 if fb < 11 else 1]], writes=[PS[ob[half]].b], inc=(fb == NFB - 1))
                            ot = outt[c % 2]
                            for half in range(2):
                                hs = slice(half * 512, (half + 1) * 512)
                                kb.op(DVE, lambda half=half, hs=hs, ob=ob, ot=ot, xx=xx: V.tensor_tensor(out=ot.t[:, hs], in0=xx.t[:, hs], in1=pf(ob[half]), op=ALU.add),
                                      reads=[xx.b, PS[ob[half]].b], writes=[ot.b])
                            kb.dma(SP, out[tok, :], ot.t[:], reads=[ot.b], is_out=True)

            with ExitStack() as sB:
                ynsaT = sb(sB, "ynsaT", [128, 4, S], BF16, NT)
                yretT = sb(sB, "yretT", [128, 4, S], BF16, NT)

                with ExitStack() as sA:
                    gq = sb(sA, "gq", [128, 64], F32)
                    gk = sb(sA, "gk", [128, 3, 64], F32)
                    QAL = sb(sA, "QAL", [128, NT, 8, 4], BF16)
                    KAL = sb(sA, "KAL", [128, NT, 4], BF16)
                    KCAL = sb(sA, "KCAL", [128, 4], BF16)
                    OH = sb(sA, "OH", [128, NT, 32], BF16)
                    dmask = sb(sA, "dmask", [128, 4, 128], BF16)
                    tmask = sb(sA, "tmask", [128, 4, 128], BF16)
                    cmask = sb(sA, "cmask", [128, S], BF16)
                    addc = sb(sA, "addc", [128, NT, 32], F32)
                    ov = sb(sA, "ov", [128, 32], BF16)
                    KT_slc = sb(sA, "KT_slc", [128, 2, S], BF16, NT)
                    KT_win = sb(sA, "KT_win", [128, 2, S], BF16, NT)
                    V_slc = sb(sA, "V_slc", [128, NT, 2, 65], BF16, NT)
                    V_win = sb(sA, "V_win", [128, NT, 2, 65], BF16, NT)
                    KcT = sb(sA, "KcT", [128, 2, 128], BF16)
                    Vc = sb(sA, "Vc", [128, 2, 97], BF16)

                    with ExitStack() as s0:
                        SL = sb(s0, "SL", [128, 8], F32)
                        th128 = sb(s0, "th128", [128, NT], F32)
                        pidx = sb(s0, "pidx", [128, 1], F32)
                        QALf = sb(s0, "QALf", [128, NT, 8, 4], F32)
                        KALf = sb(s0, "KALf", [128, NT, 4], F32)
                        KCALf = sb(s0, "KCALf", [128, 4], F32)
                        rel = sb(s0, "rel", [128, NT, 32], F32)
                        f0 = sb(s0, "f0", [128, NT, 32], F32)
                        f1 = sb(s0, "f1", [128, NT, 32], F32)
                        t1 = sb(s0, "t1", [128, NT, 32], F32)
                        hp = sb(s0, "hp", [128, 1], F32)
                        ovf = sb(s0, "ovf", [128, 32], F32)
                        ova = sb(s0, "ova", [128, 32], F32)
                        ones_b = sb(s0, "ones_b", [128, 512], BF16)
                        zeros_b = sb(s0, "zeros_b", [128, 512], BF16)

                        kb.dma(SP, gq.t[:], nsa_q_norm[0:1, :].broadcast_to([128, 64]), writes=[gq.b])
                        kb.dma(SP, gk.t[:].rearrange("p a b -> p (a b)"),
                               nsa_k_norm.rearrange("o a b -> o (a b)").broadcast_to([128, 192]), writes=[gk.b])
                        kb.op(DVE, lambda: V.tensor_scalar(out=gq.t[:], in0=gq.t[:], scalar1=0.125, scalar2=None, op0=ALU.mult),
                              reads=[gq.b], writes=[gq.b])
                        for h in range(8):
                            kb.op(POOL, lambda h=h: G.memset(SL.t[:, h:h + 1], 2.0 ** (-(h + 1))), writes=[SL.b])
                        kb.op(POOL, lambda: G.iota(th128.t[:], pattern=[[128, NT]], base=0, channel_multiplier=0,
                                                   allow_small_or_imprecise_dtypes=True), writes=[th128.b])
                        kb.op(POOL, lambda: G.iota(pidx.t[:], pattern=[[0, 1]], base=0, channel_multiplier=1,
                                                   allow_small_or_imprecise_dtypes=True), writes=[pidx.b])
                        SLb = SL.t[:].unsqueeze(1).broadcast_to([128, NT, 8])
                        THb = th128.t[:].unsqueeze(2).broadcast_to([128, NT, 8])
                        kb.op(DVE, lambda: V.scalar_tensor_tensor(out=QALf.t[:, :, :, 0], in0=THb, scalar=-1.0, in1=SLb,
                                                                  op0=ALU.mult, op1=ALU.mult),
                              reads=[SL.b, th128.b], writes=[QALf.b])
                        kb.op(DVE, lambda: V.tensor_scalar(out=QALf.t[:, :, :, 1], in0=SLb, scalar1=pidx.t[:, 0:1], scalar2=-1.0,
                                                           op0=ALU.mult, op1=ALU.mult),
                              reads=[SL.b, pidx.b], writes=[QALf.b])
                        kb.op(DVE, lambda: V.tensor_copy(out=QALf.t[:, :, :, 2], in_=SLb), reads=[SL.b], writes=[QALf.b])
                        kb.op(DVE, lambda: V.tensor_copy(out=QALf.t[:, :, :, 3], in_=SLb), reads=[SL.b], writes=[QALf.b])
                        kb.op(DVE, lambda: V.tensor_copy(out=QAL.t[:], in_=QALf.t[:]), reads=[QALf.b], writes=[QAL.b])
                        kb.op(POOL, lambda: G.memset(KALf.t[:, :, 0:2], 1.0), writes=[KALf.b])
                        kb.op(DVE, lambda: V.tensor_copy(out=KALf.t[:, :, 2], in_=th128.t[:]), reads=[th128.b], writes=[KALf.b])
                        kb.op(DVE, lambda: V.tensor_copy(out=KALf.t[:, :, 3], in_=pidx.t[:, 0:1].broadcast_to([128, NT])),
                              reads=[pidx.b], writes=[KALf.b])
                        kb.op(DVE, lambda: V.tensor_copy(out=KAL.t[:], in_=KALf.t[:]), reads=[KALf.b], writes=[KAL.b])
                        kb.op(POOL, lambda: G.memset(KCALf.t[:, 0:2], 1.0), writes=[KCALf.b])
                        kb.op(POOL, lambda: G.memset(KCALf.t[:, 3:4], 31.0), reads=[], writes=[KCALf.b])
                        kb.op(DVE, lambda: V.tensor_scalar(out=KCALf.t[:, 2:3], in0=pidx.t[:, 0:1], scalar1=16.0, scalar2=None,
                                                           op0=ALU.mult), reads=[pidx.b], writes=[KCALf.b])
                        kb.op(DVE, lambda: V.tensor_copy(out=KCAL.t[:], in_=KCALf.t[:]), reads=[KCALf.b], writes=[KCAL.b])
                        kb.op(POOL, lambda: G.memset(OH.t[:], 0.0), writes=[OH.b])
                        for kt in range(NT):
                            kb.op(POOL, lambda kt=kt: G.memset(OH.t[0:64, kt, 2 * kt:2 * kt + 1], 1.0), writes=[OH.b])
                            kb.op(POOL, lambda kt=kt: G.memset(OH.t[64:128, kt, 2 * kt + 1:2 * kt + 2], 1.0), writes=[OH.b])
                        kb.op(POOL, lambda: G.memset(ones_b.t[:], 1.0), writes=[ones_b.b])
                        kb.op(POOL, lambda: G.memset(zeros_b.t[:], 0.0), writes=[zeros_b.b])
                        ob4 = ones_b.t[:].rearrange("p (a b) -> p a b", b=128)
                        kb.op(POOL, lambda: G.affine_select(out=dmask.t[:], in_=ob4, pattern=[[0, 4], [1, 128]],
                                                            compare_op=ALU.is_ge, fill=0.0, base=0, channel_multiplier=-1),
                              reads=[ones_b.b], writes=[dmask.b])
                        kb.op(POOL, lambda: G.affine_select(out=tmask.t[:], in_=ob4, pattern=[[0, 4], [-1, 128]],
                                                            compare_op=ALU.is_gt, fill=0.0, base=0, channel_multiplier=1),
                              reads=[ones_b.b], writes=[tmask.b])
                        for i in range(4):
                            kb.op(POOL, lambda i=i: G.affine_select(out=cmask.t[:, i * 512:(i + 1) * 512], in_=zeros_b.t[:],
                                                                    pattern=[[1, 512]], compare_op=ALU.is_ge, fill=NEG,
                                                                    base=-31 + 512 * i, channel_multiplier=-16),
                                  reads=[zeros_b.b], writes=[cmask.b])
                        kb.op(POOL, lambda: G.iota(rel.t[:], pattern=[[-2, NT], [1, 32]], base=0, channel_multiplier=0,
                                                   allow_small_or_imprecise_dtypes=True), writes=[rel.b])
                        kb.op(DVE, lambda: V.tensor_scalar(out=hp.t[:], in0=pidx.t[:], scalar1=64.0, scalar2=None, op0=ALU.is_ge),
                              reads=[pidx.b], writes=[hp.b])
                        kb.op(DVE, lambda: V.tensor_scalar(out=rel.t[:], in0=rel.t[:], scalar1=hp.t[:, 0:1], scalar2=None,
                                                           op0=ALU.subtract), reads=[rel.b, hp.b], writes=[rel.b])
                        kb.op(DVE, lambda: V.tensor_scalar(out=t1.t[:], in0=rel.t[:], scalar1=0.0, scalar2=-1e9,
                                                           op0=ALU.is_gt, op1=ALU.mult), reads=[rel.b], writes=[t1.b])
                        kb.op(DVE, lambda: V.tensor_scalar(out=f0.t[:], in0=rel.t[:], scalar1=0.0, scalar2=None, op0=ALU.is_equal),
                              reads=[rel.b], writes=[f0.b])
                        kb.op(DVE, lambda: V.tensor_scalar(out=f1.t[:], in0=rel.t[:], scalar1=-1.0, scalar2=None, op0=ALU.is_equal),
                              reads=[rel.b], writes=[f1.b])
                        kb.op(DVE, lambda: V.tensor_tensor(out=f0.t[:], in0=f0.t[:], in1=f1.t[:], op=ALU.max),
                              reads=[f0.b, f1.b], writes=[f0.b])
                        kb.op(DVE, lambda: V.memset(f0.t[:, :, 0:1], 1.0), reads=[], writes=[f0.b])
                        kb.op(DVE, lambda: V.scalar_tensor_tensor(out=addc.t[:], in0=f0.t[:], scalar=1e4, in1=t1.t[:],
                                                                  op0=ALU.mult, op1=ALU.add), reads=[f0.b, t1.b], writes=[addc.b])
                        kb.op(POOL, lambda: G.iota(ovf.t[:], pattern=[[-64, 32]], base=0, channel_multiplier=16,
                                                   allow_small_or_imprecise_dtypes=True), writes=[ovf.b])
                        kb.op(DVE, lambda: V.tensor_scalar(out=ova.t[:], in0=ovf.t[:], scalar1=63.0, scalar2=None, op0=ALU.is_le),
                              reads=[ovf.b], writes=[ova.b])
                        kb.op(DVE, lambda: V.tensor_scalar(out=ovf.t[:], in0=ovf.t[:], scalar1=-31.0, scalar2=None, op0=ALU.is_ge),
                              reads=[ovf.b], writes=[ovf.b])
                        kb.op(DVE, lambda: V.tensor_tensor(out=ov.t[:], in0=ova.t[:], in1=ovf.t[:], op=ALU.mult),
                              reads=[ova.b, ovf.b], writes=[ov.b])
                        kb.op(POOL, lambda: G.memset(V_slc.t[:, :, :, 64:65], 1.0), writes=V_slc.bs)
                        kb.op(POOL, lambda: G.memset(V_win.t[:, :, :, 64:65], 1.0), writes=V_win.bs)
                        kb.barrier()
                        chk(1)

                    with ExitStack() as s1:
                        xt = [sb(s1, f"xta{i}", [128, D], F32) for i in range(2)]
                        g1bc = sb(s1, "g1bc", [128, D], F32)
                        kb.dma(SP, g1bc.t[:], norm1_g[0:1, :].broadcast_to([128, D]), writes=[g1bc.b])
                        def p1a_gen(c):
                            xx = xt[c % 2]
                            kb.dma(SP, xx.t[:], x[c * 128:(c + 1) * 128, :], writes=[xx.b])
                            yield
                            yield from norm_gen(xx, c, g1bc, c % 2, 6 + (c % 2))

                        run_interleaved([(lambda c=c: p1a_gen(c)) for c in range(NT)], width=2)
                        kb.barrier()
                        chk(2)

                    with ExitStack() as s2:
                        cmpT = sb(s2, "cmpT", [128, 2, S], BF16, NT)
                        with ExitStack() as s2a:
                            wkv = sb(s2a, "wkv", [128, KC, 768], BF16)
                            load_w(wkv, w_in_v[:, :, C_KV:C_KV + 768])
                            cmp_tok = [sb(s2a, f"cmp_tok{i}", [128, 256], BF16) for i in range(2)]
                            sqk = [sb(s2a, f"sqk{i}", [128, 256], F32) for i in range(2)]
                            kst = sb(s2a, "kst", [128, NT, 16], F32, NT)
                            tmpk = [sb(s2a, f"tmpk{i}", [128, 4, 64], F32) for i in range(2)]
                            ka_slc = [sb(s2a, f"ka_slc{i}", [128, 2, 128], BF16) for i in range(2)]
                            ka_win = [sb(s2a, f"ka_win{i}", [128, 2, 128], BF16) for i in range(2)]
                            for i in range(2):
                                kb.op(POOL, lambda i=i: G.memset(ka_slc[i].t[:], 0.0), writes=[ka_slc[i].b])
                                kb.op(POOL, lambda i=i: G.memset(ka_win[i].t[:], 0.0), writes=[ka_win[i].b])
                            for c in range(NT):
                                i2 = c % 2
                                bA, bB = (0, 1) if i2 == 0 else (2, 3)
                                tok = slice(c * 128, (c + 1) * 128)
                                if CUT >= 1:
                                    for k in range(KC):
                                        kb.op(PE, lambda k=k: TE.matmul(pf(bA), lhsT=hT.t[:, k, tok], rhs=wkv.t[:, k, 0:512],
                                                                        start=(k == 0), stop=(k == KC - 1)),
                                              reads=[hT.bs[c], wkv.b], writes=[PS[bA].b], inc=(k == KC - 1))
                                    for k in range(KC):
                                        kb.op(PE, lambda k=k: TE.matmul(pf(bB)[:, 0:256], lhsT=hT.t[:, k, tok], rhs=wkv.t[:, k, 512:768],
                                                                        start=(k == 0), stop=(k == KC - 1)),
                                              reads=[hT.bs[c], wkv.b], writes=[PS[bB].b], inc=(k == KC - 1))
                                if CUT >= 2:
                                    ct = cmp_tok[i2]
                                    kb.op(ACT, lambda: A.copy(out=ct.t[:], in_=pf(bA)[:, 0:256]), reads=[PS[bA].b], writes=[ct.b])
                                    sq = sqk[i2]
                                    kb.op(ACT, lambda: A.activation(out=sq.t[:, 0:128], in_=pf(bA)[:, 256:384], func=AF.Square),
                                          reads=[PS[bA].b], writes=[sq.b])
                                    kb.op(ACT, lambda: A.activation(out=sq.t[:, 128:256], in_=pf(bB)[:, 0:128], func=AF.Square),
                                          reads=[PS[bB].b], writes=[sq.b])
                                if CUT >= 3:
                                    ks = kst.t
                                    ksb = [kst.bs[c]]
                                    kb.op(DVE, lambda: V.tensor_reduce(out=ks[:, c, 0:4], in_=sq.t[:].rearrange("p (a b) -> p a b", b=64),
                                                                       axis=AX.X, op=ALU.add), reads=[sq.b], writes=ksb)
                                    rstd_from_ss(ks[:, c, 0:4], ks[:, c, 4:8], ks[:, c, 8:12], ks[:, c, 12:16], 64, ksb)
                                    tk = tmpk[i2]
                                    kb.op(DVE, lambda: V.tensor_tensor(out=tk.t[:, 0:2, :], in0=pf(bA)[:, 256:384].rearrange("p (a b) -> p a b", b=64),
                                                                       in1=ks[:, c, 12:14].unsqueeze(2).broadcast_to([128, 2, 64]), op=ALU.mult),
                                          reads=[PS[bA].b] + ksb, writes=[tk.b])
                                    kb.op(DVE, lambda: V.tensor_tensor(out=tk.t[:, 2:4, :], in0=pf(bB)[:, 0:128].rearrange("p (a b) -> p a b", b=64),
                                                                       in1=ks[:, c, 14:16].unsqueeze(2).broadcast_to([128, 2, 64]), op=ALU.mult),
                                          reads=[PS[bB].b] + ksb, writes=[tk.b])
                                    ksl, kwn = ka_slc[i2], ka_win[i2]
                                    kb.op(DVE, lambda: V.tensor_tensor(out=ksl.t[:, :, 0:64], in0=tk.t[:, 0:2, :],
                                                                       in1=gk.t[:, 1:2, :].broadcast_to([128, 2, 64]), op=ALU.mult),
                                          reads=[tk.b, gk.b], writes=[ksl.b])
                                    kb.op(DVE, lambda: V.tensor_tensor(out=kwn.t[:, :, 0:64], in0=tk.t[:, 2:4, :],
                                                                       in1=gk.t[:, 2:3, :].broadcast_to([128, 2, 64]), op=ALU.mult),
                                          reads=[tk.b, gk.b], writes=[kwn.b])
                                if CUT >= 4:
                                    kb.op(POOL, lambda: G.tensor_copy(out=ksl.t[:, :, 64:96], in_=OH.t[:, c:c + 1, :].broadcast_to([128, 2, 32])),
                                          reads=[OH.b], writes=[ksl.b])
                                    kb.op(POOL, lambda: G.tensor_copy(out=ksl.t[:, :, 96:100], in_=KAL.t[:, c:c + 1, :].broadcast_to([128, 2, 4])),
                                          reads=[KAL.b], writes=[ksl.b])
                                    kb.op(POOL, lambda: G.tensor_copy(out=kwn.t[:, :, 96:100], in_=KAL.t[:, c:c + 1, :].broadcast_to([128, 2, 4])),
                                          reads=[KAL.b], writes=[kwn.b])
                                if CUT >= 5:
                                    kb.op(ACT, lambda: A.copy(out=V_slc.t[:, c, :, 0:64], in_=pf(bA)[:, 384:512].rearrange("p (a b) -> p a b", b=64)),
                                          reads=[PS[bA].b], writes=[V_slc.bs[c]])
                                    kb.op(ACT, lambda: A.copy(out=V_win.t[:, c, :, 0:64], in_=pf(bB)[:, 128:256].rearrange("p (a b) -> p a b", b=64)),
                                          reads=[PS[bB].b], writes=[V_win.bs[c]])
                                if CUT >= 6:
                                    tb = 4 + i2
                                    transposes([ksl.t[:, 0, :], ksl.t[:, 1, :], kwn.t[:, 0, :], kwn.t[:, 1, :], ct.t[:, 0:128], ct.t[:, 128:256]],
                                               tb, [ksl.b, kwn.b, ct.b])
                                    pv3 = pb(tb).rearrange("p (a b) -> p a b", b=128)
                                    kb.op(ACT, lambda: A.copy(out=KT_slc.t[:, :, tok], in_=pv3[:, 0:2, :]), reads=[PS[tb].b], writes=[KT_slc.bs[c]])
                                    kb.op(ACT, lambda: A.copy(out=KT_win.t[:, :, tok], in_=pv3[:, 2:4, :]), reads=[PS[tb].b], writes=[KT_win.bs[c]])
                                    kb.op(ACT, lambda: A.copy(out=cmpT.t[:, :, tok], in_=pv3[:, 4:6, :]), reads=[PS[tb].b], writes=[cmpT.bs[c]])
                            if "KT_slc" in dbg:
                                kdb = sb(s2a, "kdb", [128, 2, S], F32)
                                kb.op(DVE, lambda: V.tensor_copy(out=kdb.t[:], in_=KT_slc.t[:]), reads=KT_slc.bs, writes=[kdb.b])
                                kb.dma(SP, dbg["KT_slc"].rearrange("p (a b) -> p a b", b=S), kdb.t[:], reads=[kdb.b], is_out=True)
                            kb.barrier()
                            chk(3)

                        with ExitStack() as s2b:
                            w1sb = sb(s2b, "w1sb", [128, 2, 32, 128], BF16)
                            w2sb = sb(s2b, "w2sb", [128, 2, 64], BF16)
                            pe_sb = sb(s2b, "pe_sb", [32, 2, 64], F32)
                            peT = sb(s2b, "peT", [64, 2, 32], BF16)
                            bias_c = sb(s2b, "bias_c", [128, 2], F32)
                            xh = sb(s2b, "xh", [128, 128], F32)
                            x2 = sb(s2b, "x2", [128, 128], F32)
                            sg = sb(s2b, "sgc", [128, 128], F32)
                            HTb = sb(s2b, "HTb", [128, 128], BF16)
                            kca = sb(s2b, "kca", [128, 2, 128], BF16)
                            cst = sb(s2b, "cst", [128, 8], F32)
                            tmpc = sb(s2b, "tmpc", [128, 64], F32)
                            for kv in range(2):
                                src = cmp_w1[0, kv].rearrange("l d f -> d l f")
                                kb.dma(POOL, w1sb.t[0:64, kv], src, writes=[w1sb.b])
                                kb.dma(POOL, w1sb.t[64:128, kv], src, writes=[w1sb.b])
                            kb.dma(POOL, w2sb.t[:], cmp_w2[0].rearrange("k f d -> f k d"), writes=[w2sb.b])
                            kb.dma(SP, pe_sb.t[:], cmp_pe[0].rearrange("k l d -> l k d"), writes=[pe_sb.b])
                            kb.op(POOL, lambda: G.memset(kca.t[:], 0.0), writes=[kca.b])
                            kb.op(POOL, lambda: G.memset(Vc.t[:], 0.0), writes=[Vc.b])
                            kb.op(POOL, lambda: G.tensor_copy(out=kca.t[:, :, 96:100], in_=KCAL.t[:].unsqueeze(1).broadcast_to([128, 2, 4])),
                                  reads=[KCAL.b], writes=[kca.b])
                            kb.op(POOL, lambda: G.memset(Vc.t[:, :, 64:65], 1.0), writes=[Vc.b])
                            kb.op(POOL, lambda: G.tensor_copy(out=Vc.t[:, :, 65:97], in_=ov.t[:].unsqueeze(1).broadcast_to([128, 2, 32])),
                                  reads=[ov.b], writes=[Vc.b])
                            for kv in range(2):
                                kb.op(PE, lambda kv=kv: TE.transpose(out=pf(0)[0:64, kv * 32:(kv + 1) * 32], in_=pe_sb.t[0:32, kv, :],
                                                                     identity=ident_f.t[0:32, 0:32]),
                                      reads=[pe_sb.b, ident_f.b], writes=[PS[0].b])
                            kb.op(DVE, lambda: V.tensor_copy(out=peT.t[:], in_=pf(0)[0:64, 0:64].rearrange("p (a b) -> p a b", b=32)),
                                  reads=[PS[0].b], writes=[peT.b])
                            for kv in range(2):
                                for l in range(32):
                                    kb.op(PE, lambda kv=kv, l=l: TE.matmul(pf(1)[:, kv:kv + 1], lhsT=w1sb.t[0:64, kv, l, :], rhs=peT.t[0:64, kv, l:l + 1],
                                                                           start=(l == 0), stop=(l == 31)),
                                          reads=[w1sb.b, peT.b], writes=[PS[1].b], inc=(l == 31))
                            kb.op(DVE, lambda: V.tensor_copy(out=bias_c.t[:], in_=pf(1)[:, 0:2]), reads=[PS[1].b], writes=[bias_c.b])
                            it = 0
                            for kv in range(2):
                                for g in range(2):
                                    bH = 2 + (it % 2)
                                    bO = 4 + (it % 2)
                                    it += 1
                                    for l in range(32):
                                        kb.op(PE, lambda kv=kv, g=g, l=l: TE.matmul(
                                            pf(bH)[:, 0:127], lhsT=w1sb.t[g * 64:(g + 1) * 64, kv, l, :],
                                            rhs=cmpT.t[g * 64:(g + 1) * 64, kv, l:l + 16 * 126 + 1:16],
                                            start=(l == 0), stop=(l == 31)),
                                            reads=[w1sb.b] + cmpT.bs, writes=[PS[bH].b], inc=(l == 31))
                                    kb.op(ACT, lambda kv=kv: A.activation(out=xh.t[:, 0:127], in_=pf(bH)[:, 0:127], func=AF.Identity,
                                                                          bias=bias_c.t[:, kv:kv + 1], scale=1.0),
                                          reads=[PS[bH].b, bias_c.b], writes=[xh.b])
                                    kb.op(DVE, lambda: V.tensor_tensor(out=x2.t[:, 0:127], in0=xh.t[:, 0:127], in1=xh.t[:, 0:127], op=ALU.mult),
                                          reads=[xh.b], writes=[x2.b])
                                    kb.op(DVE, lambda: V.tensor_scalar(out=x2.t[:, 0:127], in0=x2.t[:, 0:127], scalar1=0.044715, scalar2=1.0,
                                                                       op0=ALU.mult, op1=ALU.add), reads=[x2.b], writes=[x2.b])
                                    kb.op(DVE, lambda: V.tensor_tensor(out=x2.t[:, 0:127], in0=x2.t[:, 0:127], in1=xh.t[:, 0:127], op=ALU.mult),
                                          reads=[x2.b, xh.b], writes=[x2.b])
                                    kb.op(ACT, lambda: A.activation(out=sg.t[:, 0:127], in_=x2.t[:, 0:127], func=AF.Sigmoid, scale=1.5957691216057308),
                                          reads=[x2.b], writes=[sg.b])
                                    kb.op(DVE, lambda: V.tensor_tensor(out=HTb.t[:, 0:127], in0=xh.t[:, 0:127], in1=sg.t[:, 0:127], op=ALU.mult),
                                          reads=[xh.b, sg.b], writes=[HTb.b])
                                    kb.op(PE, lambda kv=kv: TE.matmul(pf(bO)[0:127, 0:64], lhsT=HTb.t[:, 0:127], rhs=w2sb.t[:, kv, :], start=True, stop=True),
                                          reads=[HTb.b, w2sb.b], writes=[PS[bO].b])
                                    if kv == 0:
                                        kb.op(ACT, lambda g=g: A.activation(out=tmpc.t[0:127, :], in_=pf(bO)[0:127, 0:64], func=AF.Square,
                                                                            accum_out=cst.t[0:127, g:g + 1]),
                                              reads=[PS[bO].b], writes=[tmpc.b, cst.b])
                                        rstd_from_ss(cst.t[0:127, g:g + 1], cst.t[0:127, 2 + g:3 + g], cst.t[0:127, 4 + g:5 + g], cst.t[0:127, 6 + g:7 + g], 64, [cst.b])
                                        kb.op(DVE, lambda g=g: V.scalar_tensor_tensor(out=kca.t[0:127, g, 0:64], in0=pf(bO)[0:127, 0:64],
                                                                                      scalar=cst.t[0:127, 6 + g:7 + g], in1=gk.t[0:127, 0, :],
                                                                                      op0=ALU.mult, op1=ALU.mult),
                                              reads=[PS[bO].b, cst.b, gk.b], writes=[kca.b])
                                    else:
                                        kb.op(ACT, lambda g=g: A.copy(out=Vc.t[0:127, g, 0:64], in_=pf(bO)[0:127, 0:64]),
                                              reads=[PS[bO].b], writes=[Vc.b])
                            transposes([kca.t[:, 0, :], kca.t[:, 1, :]], 6, [kca.b])
                            kb.op(ACT, lambda: A.copy(out=KcT.t[:], in_=pb(6)[:, 0:256].rearrange("p (a b) -> p a b", b=128)),
                                  reads=[PS[6].b], writes=[KcT.b])
                            if "kc" in dbg:
                                kcd = sb(s2b, "kcd", [128, 2, 64], F32)
                                kb.op(DVE, lambda: V.tensor_copy(out=kcd.t[:], in_=kca.t[:, :, 0:64]), reads=[kca.b], writes=[kcd.b])
                                kb.dma(SP, dbg["kc"].rearrange("p (a b) -> p a b", b=64), kcd.t[:], reads=[kcd.b], is_out=True)
                            if "vc" in dbg:
                                vcd = sb(s2b, "vcd", [128, 2, 64], F32)
                                kb.op(DVE, lambda: V.tensor_copy(out=vcd.t[:], in_=Vc.t[:, :, 0:64]), reads=[Vc.b], writes=[vcd.b])
                                kb.dma(SP, dbg["vc"].rearrange("p (a b) -> p a b", b=64), vcd.t[:], reads=[vcd.b], is_out=True)
                            kb.barrier()
                            chk(4)

                    phase_1c()
                    kb.barrier()
                    chk(5)

                phase_1d()
                kb.barrier()
                chk(6)
                phase_1e()
                kb.barrier()
                chk(7)

            phase_2()
            kb.finish()
    except _Stop:
        pass
    return nc


_NAMES = ["x", "norm1_g", "w_in", "nsa_q_norm", "nsa_k_norm", "cmp_pe", "cmp_w1", "cmp_w2", "ret_gn_g",
          "w_branch", "w_out", "norm2_g", "ffn_w_gate", "ffn_w_up", "ffn_w_down"]


def kernel(**inputs):
    n = 8
    arrs = {k: np.ascontiguousarray(np.asarray(inputs[k], dtype=np.float32)) for k in _NAMES}
    nc = build_nc()
    in_maps = []
    for i in range(n):
        m = {k: arrs[k] for k in _NAMES if k != "x"}
        m["x"] = np.ascontiguousarray(arrs["x"][i])
        in_maps.append(m)
    res = run_bass_kernel_spmd(nc, in_maps, core_ids=list(range(n)))
    return np.stack([np.asarray(r["out"], dtype=np.float32) for r in res.results], axis=0)
```

```python
import numpy as np
from contextlib import ExitStack
import concourse.bass as bass
import concourse.mybir as mybir
from concourse.bass_utils import run_bass_kernel_spmd

F32 = mybir.dt.float32
BF16 = mybir.dt.bfloat16
AF = mybir.ActivationFunctionType
ALU = mybir.AluOpType
AX = mybir.AxisListType

S = 2048
D = 1024
NT = 16
KC = 8
N_IN = 5400
DFF = 2816
NFB = 22
EPS = 1e-6
SEM_LIMIT = 24000
import os as _os
CUT = int(_os.environ.get('P1B_CUT', '99'))
CUTC = int(_os.environ.get('P1C_CUT', '99'))
SUBC = int(_os.environ.get('P1C_SUB', '99'))
WIDTH = int(_os.environ.get('P1C_WIDTH', '2'))
NEG = -30000.0

C_Q = 0
C_KV = 512
C_G = 1280
C_R = 1304
C_M = 3352


class Buf:
    __slots__ = ("name", "w", "r", "excl")

    def __init__(self, name):
        self.name = name
        self.w = None
        self.r = []
        self.excl = False


class SemW:
    __slots__ = ("h",)

    def __init__(self, h):
        self.h = h


class Slot:
    __slots__ = ("sem", "val")

    def __init__(self, sem):
        self.sem = sem
        self.val = 0


class Q:
    def __init__(self, name, eng):
        self.name = name
        self.eng = eng
        self.sem = None
        self.count = 0
        self.waited = {}
        self.ring = []
        self.ri = 0
        self.pending = False


class T:
    def __init__(self, t, name, nb=1):
        self.t = t
        self.bs = [Buf(f"{name}{i}") for i in range(nb)]

    @property
    def b(self):
        return self.bs[0]


class KB:
    def __init__(self, nc, es):
        self.nc = nc
        self.es = es
        self.nsem = 0
        self.pe = self.mkq("pe", nc.tensor)
        self.act = self.mkq("act", nc.scalar)
        self.dve = self.mkq("dve", nc.vector)
        self.pool = self.mkq("pool", nc.gpsimd)
        self.sp = self.mkq("sp", nc.sync)
        self.qs = [self.pe, self.act, self.dve, self.pool, self.sp]
        for q, n in ((self.sp, 16), (self.pool, 8), (self.act, 4)):
            q.ring = [Slot(self.new_sem(f"{q.name}_d{i}")) for i in range(n)]
        self.out_toks = []

    def new_sem(self, name):
        self.nsem += 1
        return SemW(self.es.enter_context(self.nc.semaphore(f"{name}_{self.nsem}")))

    def mkq(self, name, eng):
        q = Q(name, eng)
        q.sem = self.new_sem(name)
        return q

    def wait(self, q, tok):
        sw, val = tok[0], tok[1]
        if q.waited.get(sw, 0) >= val:
            return
        q.eng.wait_ge(sw.h, val)
        q.waited[sw] = val

    def _dep(self, q, tok, raw, force=False):
        if tok[2] is q and q is self.pe and not force:
            return
        self.wait(q, tok)

    def _deps(self, q, reads, writes, force=False):
        for b in reads:
            if b.w is not None:
                self._dep(q, b.w, True, force)
            if b.excl:
                for t in b.r:
                    if t[2] is not q:
                        self._dep(q, t, False, force)
        for b in writes:
            if b.w is not None:
                self._dep(q, b.w, False, force)
            for t in b.r:
                self._dep(q, t, False, force)

    def _record(self, tok, reads, writes):
        for b in reads:
            if tok[2] is not None:
                b.r = [t for t in b.r if t[2] is not tok[2]]
            b.r.append(tok)
        for b in writes:
            b.w = tok
            b.r = []

    def op(self, q, fn, reads=(), writes=(), inc=True):
        self._deps(q, reads, writes)
        ins = fn()
        if inc:
            if q.count >= SEM_LIMIT and not q.pending:
                q.sem = self.new_sem(q.name)
                q.count = 0
            ins.then_inc(q.sem.h, 1)
            q.count += 1
            q.pending = False
            tok = (q.sem, q.count, q)
        else:
            q.pending = True
            tok = (q.sem, q.count + 1, q)
        self._record(tok, reads, writes)
        return ins

    def dma(self, q, out, in_, reads=(), writes=(), is_out=False):
        self._deps(q, reads, writes, force=True)
        slot = q.ring[q.ri % len(q.ring)]
        q.ri += 1
        if slot.val > 0:
            self.wait(q, (slot.sem, slot.val))
        if slot.val >= SEM_LIMIT:
            slot.sem = self.new_sem(q.name + "_d")
            slot.val = 0
        ins = q.eng.dma_start(out=out, in_=in_)
        ins.then_inc(slot.sem.h, 16)
        slot.val += 16
        tok = (slot.sem, slot.val, None)
        self._record(tok, reads, writes)
        if is_out:
            self.out_toks.append(tok)
        return tok

    def barrier(self):
        toks = []
        for o in self.qs:
            if o.count > 0:
                toks.append((o.sem, o.count, o))
            for sl in o.ring:
                if sl.val > 0:
                    toks.append((sl.sem, sl.val, None))
        for q in self.qs:
            for t in toks:
                if t[2] is q:
                    continue
                self.wait(q, t)

    def finish(self):
        for t in self.out_toks:
            self.wait(self.sp, t)


class _Stop(Exception):
    pass


def build_nc(debug=None, stop=None):
    nc = bass.Bass("TRN2", target_bir_lowering=False)

    def din(name, shape):
        return nc.dram_tensor(name, list(shape), F32, kind="ExternalInput").ap()

    x = din("x", [S, D])
    norm1_g = din("norm1_g", [1, D])
    w_in = din("w_in", [1, D, N_IN])
    nsa_q_norm = din("nsa_q_norm", [1, 64])
    nsa_k_norm = din("nsa_k_norm", [1, 3, 64])
    cmp_pe = din("cmp_pe", [1, 2, 32, 64])
    cmp_w1 = din("cmp_w1", [1, 2, 32, 64, 128])
    cmp_w2 = din("cmp_w2", [1, 2, 128, 64])
    ret_gn_g = din("ret_gn_g", [1, 4, 128])
    w_branch = din("w_branch", [1, 2, 512, D])
    w_out = din("w_out", [1, D, D])
    norm2_g = din("norm2_g", [1, D])
    ffn_w_gate = din("ffn_w_gate", [1, D, DFF])
    ffn_w_up = din("ffn_w_up", [1, D, DFF])
    ffn_w_down = din("ffn_w_down", [1, DFF, D])
    out = nc.dram_tensor("out", [S, D], F32, kind="ExternalOutput").ap()
    xmid = nc.dram_tensor("xmid", [S, D], F32, kind="Internal").ap()
    dbg = {}
    if debug:
        for name, shape in debug.items():
            dbg[name] = nc.dram_tensor("dbg_" + name, list(shape), F32, kind="ExternalOutput").ap()

    w_in_v = w_in[0].rearrange("(k p) n -> p k n", p=128)

    try:
        with ExitStack() as es:
            kb = KB(nc, es)

            def chk(n):
                if stop is not None and n >= stop:
                    kb.barrier()
                    kb.finish()
                    raise _Stop()
            PE, ACT, DVE, POOL, SP = kb.pe, kb.act, kb.dve, kb.pool, kb.sp
            V, A, G, TE = nc.vector, nc.scalar, nc.gpsimd, nc.tensor

            def sb(scope, name, shape, dt, nb=1):
                return T(scope.enter_context(nc.sbuf_tensor(name, list(shape), dt)), name, nb)

            PS = [T(es.enter_context(nc.psum_tensor(f"ps{i}", [128, 512], F32)), f"ps{i}") for i in range(8)]
            for p_ in PS:
                p_.b.excl = True

            def pf(i):
                return PS[i].t[:]

            def pb(i):
                return PS[i].t[:].bitcast(BF16)

            ident_f = sb(es, "ident_f", [128, 128], F32)
            ident_b = sb(es, "ident_b", [128, 128], BF16)
            ones_f = sb(es, "ones_f", [128, 128], F32)
            nhalf = sb(es, "nhalf", [128, 16], F32)
            hT = sb(es, "hT", [128, KC, S], BF16, NT)
            stat = sb(es, "stat", [128, NT, 4], F32, NT)
            hb = [sb(es, f"hb{i}", [128, D], BF16) for i in range(2)]
            junk = sb(es, "junk", [128, D], BF16)

            kb.op(POOL, lambda: G.memset(ones_f.t[:], 1.0), writes=[ones_f.b])
            kb.op(POOL, lambda: G.memset(nhalf.t[:], -0.5), writes=[nhalf.b])
            kb.op(POOL, lambda: G.affine_select(out=ident_f.t[:], in_=ones_f.t[:, 0:128], pattern=[[1, 128]],
                                                compare_op=ALU.is_equal, fill=0.0, base=0, channel_multiplier=-1),
                  reads=[ones_f.b], writes=[ident_f.b])
            kb.op(DVE, lambda: V.tensor_copy(out=ident_b.t[:], in_=ident_f.t[:]), reads=[ident_f.b], writes=[ident_b.b])

            def rstd_from_ss(ss_ap, ms_ap, sd_ap, rs_ap, n, bufs):
                k = ms_ap.shape[-1]
                P_ = ms_ap.shape[0]
                kb.op(DVE, lambda: V.tensor_scalar(out=ms_ap, in0=ss_ap, scalar1=1.0 / n, scalar2=EPS,
                                                   op0=ALU.mult, op1=ALU.add), reads=bufs, writes=bufs)
                kb.op(POOL, lambda: G.tensor_tensor(out=rs_ap, in0=ms_ap, in1=nhalf.t[0:P_, 0:k], op=ALU.pow),
                      reads=list(bufs) + [nhalf.b], writes=bufs)

            def transposes(src_aps, bank, reads):
                pbv = pb(bank)
                n = len(src_aps)
                for i, ap in enumerate(src_aps):
                    kb.op(PE, lambda ap=ap, i=i: TE.transpose(out=pbv[:, i * 128:(i + 1) * 128], in_=ap, identity=ident_b.t[:]),
                          reads=list(reads) + [ident_b.b], writes=[PS[bank].b], inc=(i == n - 1))

            def norm_gen(src, c, gbc, sidx, bank):
                sbuf_ = [stat.bs[c]]
                st = stat.t
                kb.op(ACT, lambda: A.activation(out=junk.t[:], in_=src.t[:], func=AF.Square, accum_out=st[:, c, 0:1]),
                      reads=[src.b], writes=[junk.b] + sbuf_)
                yield
                kb.op(DVE, lambda: V.tensor_scalar(out=st[:, c, 1:2], in0=st[:, c, 0:1], scalar1=1.0 / D, scalar2=EPS,
                                                   op0=ALU.mult, op1=ALU.add), reads=sbuf_, writes=sbuf_)
                yield
                kb.op(POOL, lambda: G.tensor_tensor(out=st[:, c, 3:4], in0=st[:, c, 1:2], in1=nhalf.t[:, 0:1], op=ALU.pow),
                      reads=sbuf_ + [nhalf.b], writes=sbuf_)
                yield
                h = hb[sidx % 2]
                kb.op(DVE, lambda: V.scalar_tensor_tensor(out=h.t[:], in0=src.t[:], scalar=st[:, c, 3:4], in1=gbc.t[:],
                                                          op0=ALU.mult, op1=ALU.mult),
                      reads=[src.b, gbc.b] + sbuf_, writes=[h.b])
                yield
                transposes([h.t[:, k * 128:(k + 1) * 128] for k in range(KC)], bank, [h.b])
                yield
                kb.op(ACT, lambda: A.copy(out=hT.t[:, :, c * 128:(c + 1) * 128],
                                          in_=pb(bank).rearrange("p (a b) -> p a b", b=128)),
                      reads=[PS[bank].b], writes=[hT.bs[c]])
                yield

            def run_interleaved(gen_fns, width=2):
                pending = list(gen_fns)
                active = []
                while pending or active:
                    while pending and len(active) < width:
                        active.append(pending.pop(0)())
                    for g_ in list(active):
                        try:
                            next(g_)
                        except StopIteration:
                            active.remove(g_)

            def load_w(dst, src_ap, q=None):
                kb.dma(q or POOL, dst.t[:], src_ap, writes=[dst.b])

            def dbg_store(name, src_ap, rows, reads):
                if name in dbg:
                    kb.dma(SP, dbg[name][rows], src_ap, reads=reads, is_out=True)

            def phase_1c():
                with ExitStack() as s3:
                    wq = sb(s3, "wq", [128, KC, 512], BF16)
                    load_w(wq, w_in_v[:, :, C_Q:C_Q + 512])
                    wg = sb(s3, "wg", [128, KC, 24], BF16)
                    load_w(wg, w_in_v[:, :, C_G:C_G + 24])
                    sqq = [sb(s3, f"sqq{i}", [128, 512], F32) for i in range(2)]
                    qst = sb(s3, "qst", [128, NT, 32], F32, NT)
                    tmpq = [sb(s3, f"tmpq{i}", [128, 8, 64], F32) for i in range(2)]
                    qaug = [sb(s3, f"qaug{i}", [128, 8, 128], BF16) for i in range(2)]
                    qT = [sb(s3, f"qT{i}", [128, 8, 128], BF16) for i in range(2)]
                    qT2 = qT
                    gate = [sb(s3, f"gate{i}", [128, 24], F32) for i in range(2)]
                    PcT = [[sb(s3, f"PcT{i}{g}", [128, 512], BF16) for g in range(2)] for i in range(2)]
                    scl = [[sb(s3, f"scl{i}", [128, 512], F32)] * 2 for i in range(2)]
                    NPT = 4
                    oTs = [sb(s3, f"oTs{i}", [128, 512], F32) for i in range(2)]
                    PT = [[sb(s3, f"PT{i}_{j}", [128, 512], BF16) for j in range(NPT)] for i in range(2)]
                    num = [sb(s3, f"num{i}", [128, 3, 8, 64], F32) for i in range(2)]
                    den = [sb(s3, f"den{i}", [128, 3, 8], F32) for i in range(2)]
                    rdc = [sb(s3, f"rdc{i}", [128, 8], F32) for i in range(2)]
                    impn = [sb(s3, f"impn{i}", [128, 8, 32], F32) for i in range(2)]
                    imp = [sb(s3, f"imp{i}", [128, 2, 32], F32) for i in range(2)]
                    top8 = [sb(s3, f"top8{i}", [128, 2, 8], F32) for i in range(2)]
                    rd = [sb(s3, f"rd{i}", [128, 3, 8], F32) for i in range(2)]
                    coef = [sb(s3, f"coef{i}", [128, 3, 8], F32) for i in range(2)]
                    oacc = [sb(s3, f"oacc{i}", [128, 8, 64], F32) for i in range(2)]
                    otmp = [sb(s3, f"otmp{i}", [128, 8, 64], F32) for i in range(2)]
                    ytok = [sb(s3, f"ytok{i}", [128, 512], BF16) for i in range(2)]
                    ydbg = sb(s3, "ydbg", [128, 512], F32) if "y_nsa" in dbg else None
                    for i in range(2):
                        kb.op(POOL, lambda i=i: G.memset(qaug[i].t[:], 0.0), writes=[qaug[i].b])
                    ptc = [0, 0]

                    def tile_gen(c):
                        i2 = c % 2
                        base = 4 * i2
                        bZ, bS, bS2, bO = base, base + 1, base + 2, base + 3
                        bX = bZ
                        sbk = [bS, bS2]
                        tok = slice(c * 128, (c + 1) * 128)
                        scn = [0]

                        def pxv():
                            return pf(bX).rearrange("p (h c) -> p h c", h=4)

                        def to_token_major(g, br, ncol):
                            nu, de = num[i2], den[i2]
                            ot = oTs[i2]
                            kb.op(DVE, lambda: V.tensor_scalar(out=ot.t[0:ncol, :], in0=pf(bO)[0:ncol, :], scalar1=1.0, scalar2=None, op0=ALU.mult), reads=[PS[bO].b], writes=[ot.b])
                            yield
                            for hh in range(4):
                                kb.op(PE, lambda hh=hh: TE.transpose(out=pxv()[:, hh, 0:ncol], in_=ot.t[0:ncol, hh * 128:(hh + 1) * 128],
                                                                     identity=ident_f.t[0:ncol, 0:ncol]),
                                      reads=[ot.b, ident_f.b], writes=[PS[bX].b], inc=(hh == 3))
                            yield
                            kb.op(ACT, lambda: A.copy(out=nu.t[:, br, 4 * g:4 * g + 4, :], in_=pxv()[:, :, 0:64]),
                                  reads=[PS[bX].b], writes=[nu.b])
                            kb.op(DVE, lambda: V.tensor_scalar(out=de.t[:, br, 4 * g:4 * g + 4], in0=pxv()[:, :, 64], scalar1=1e-30, scalar2=None, op0=ALU.max),
                                  reads=[PS[bX].b], writes=[de.b])
                            yield

                        def branch(g, br, KT, VV, kts, qsrc):
                            n = len(kts)
                            pts = [None] * n

                            def score(j):
                                kt = kts[j]
                                sb_ = sbk[scn[0] % 2]
                                scn[0] += 1
                                kb.op(PE, lambda: TE.matmul(pf(sb_), lhsT=KT.t[:, g, kt * 128:(kt + 1) * 128],
                                                            rhs=qsrc.t[:, 4 * g:4 * g + 4, :], start=True, stop=True),
                                      reads=[KT.bs[kt], qsrc.b], writes=[PS[sb_].b])
                                pt = PT[i2][ptc[i2] % NPT]
                                ptc[i2] += 1
                                kb.op(ACT, lambda: A.activation(out=pt.t[:], in_=pf(sb_), func=AF.Exp),
                                      reads=[PS[sb_].b], writes=[pt.b])
                                if kt == c:
                                    kb.op(DVE, lambda: V.tensor_tensor(out=pt.t[:], in0=pt.t[:], in1=dmask.t[:].rearrange("p a b -> p (a b)"), op=ALU.mult),
                                          reads=[pt.b, dmask.b], writes=[pt.b])
                                elif br == 2 and kt == c - 4:
                                    kb.op(DVE, lambda: V.tensor_tensor(out=pt.t[:], in0=pt.t[:], in1=tmask.t[:].rearrange("p a b -> p (a b)"), op=ALU.mult),
                                          reads=[pt.b, tmask.b], writes=[pt.b])
                                pts[j] = pt

                            score(0)
                            yield
                            for j, kt in enumerate(kts):
                                if j + 1 < n:
                                    score(j + 1)
                                pt = pts[j]
                                kb.op(PE, lambda kt=kt, pt=pt, j=j: TE.matmul(pf(bO)[0:65, :], lhsT=VV.t[:, kt, g, :], rhs=pt.t[:],
                                                                              start=(j == 0), stop=(j == n - 1)),
                                      reads=[pt.b, VV.bs[kt]], writes=[PS[bO].b], inc=(j == n - 1))
                                yield
                            yield from to_token_major(g, br, 65)

                        for k in range(KC):
                            kb.op(PE, lambda k=k: TE.matmul(pf(bZ), lhsT=hT.t[:, k, tok], rhs=wq.t[:, k, :], start=(k == 0), stop=(k == KC - 1)),
                                  reads=[hT.bs[c], wq.b], writes=[PS[bZ].b], inc=(k == KC - 1))
                        for k in range(KC):
                            kb.op(PE, lambda k=k: TE.matmul(pf(bS)[:, 0:24], lhsT=hT.t[:, k, tok], rhs=wg.t[:, k, :], start=(k == 0), stop=(k == KC - 1)),
                                  reads=[hT.bs[c], wg.b], writes=[PS[bS].b], inc=(k == KC - 1))
                        yield
                        sq = sqq[i2]
                        kb.op(ACT, lambda: A.activation(out=sq.t[:], in_=pf(bZ), func=AF.Square), reads=[PS[bZ].b], writes=[sq.b])
                        gt = gate[i2]
                        kb.op(ACT, lambda: A.activation(out=gt.t[:], in_=pf(bS)[:, 0:24], func=AF.Tanh, scale=0.5), reads=[PS[bS].b], writes=[gt.b])
                        yield
                        kb.op(DVE, lambda: V.tensor_scalar(out=gt.t[:], in0=gt.t[:], scalar1=0.5, scalar2=0.5, op0=ALU.mult, op1=ALU.add),
                              reads=[gt.b], writes=[gt.b])
                        qs = qst.t
                        qsb = [qst.bs[c]]
                        kb.op(DVE, lambda: V.tensor_reduce(out=qs[:, c, 0:8], in_=sq.t[:].rearrange("p (a b) -> p a b", b=64), axis=AX.X, op=ALU.add),
                              reads=[sq.b], writes=qsb)
                        yield
                        kb.op(DVE, lambda: V.tensor_scalar(out=qs[:, c, 8:16], in0=qs[:, c, 0:8], scalar1=1.0 / 64, scalar2=EPS, op0=ALU.mult, op1=ALU.add), reads=qsb, writes=qsb)
                        yield
                        kb.op(POOL, lambda: G.tensor_tensor(out=qs[:, c, 24:32], in0=qs[:, c, 8:16], in1=nhalf.t[:, 0:8], op=ALU.pow),
                              reads=qsb + [nhalf.b], writes=qsb)
                        yield
                        tq = tmpq[i2]
                        qa = qaug[i2]
                        kb.op(DVE, lambda: V.tensor_tensor(out=tq.t[:], in0=pf(bZ).rearrange("p (a b) -> p a b", b=64),
                                                           in1=qs[:, c, 24:32].unsqueeze(2).broadcast_to([128, 8, 64]), op=ALU.mult),
                              reads=[PS[bZ].b] + qsb, writes=[tq.b])
                        yield
                        kb.op(DVE, lambda: V.tensor_tensor(out=qa.t[:, :, 0:64], in0=tq.t[:], in1=gq.t[:].unsqueeze(1).broadcast_to([128, 8, 64]), op=ALU.mult),
                              reads=[tq.b, gq.b], writes=[qa.b])
                        kb.op(POOL, lambda: G.tensor_copy(out=qa.t[:, :, 96:100], in_=QAL.t[:, c, :, :]), reads=[QAL.b], writes=[qa.b])
                        yield
                        transposes([qa.t[:, h, :] for h in range(8)], bZ, [qa.b])
                        yield
                        q1 = qT[i2]
                        kb.op(ACT, lambda: A.copy(out=q1.t[:], in_=pb(bZ).rearrange("p (a b) -> p a b", b=128)), reads=[PS[bZ].b], writes=[q1.b])
                        yield
                        nu, de = num[i2], den[i2]
                        rdc_, impn_, imp_, top8_ = rdc[i2], impn[i2], imp[i2], top8[i2]
                        for g in range(2):
                            pc = PcT[i2][g]
                            sc_ = scl[i2][g]
                            kb.op(PE, lambda g=g: TE.matmul(pf(bS)[0:127, :], lhsT=KcT.t[:, g, 0:127], rhs=q1.t[:, 4 * g:4 * g + 4, :], start=True, stop=True),
                                  reads=[KcT.b, q1.b], writes=[PS[bS].b])
                            yield
                            kb.op(DVE, lambda sc_=sc_: V.scalar_tensor_tensor(out=sc_.t[0:127, :].rearrange("p (a b) -> p a b", b=128),
                                                                              in0=pf(bS)[0:127, :].rearrange("p (a b) -> p a b", b=128), scalar=60.0,
                                                                              in1=cmask.t[0:127, tok].unsqueeze(1).broadcast_to([127, 4, 128]),
                                                                              op0=ALU.min, op1=ALU.add),
                                  reads=[PS[bS].b, cmask.b], writes=[sc_.b])
                            yield
                            kb.op(ACT, lambda pc=pc, sc_=sc_: A.activation(out=pc.t[0:127, :], in_=sc_.t[0:127, :], func=AF.Exp),
                                  reads=[sc_.b], writes=[pc.b])
                            yield
                            kb.op(PE, lambda g=g, pc=pc: TE.matmul(pf(bO)[0:97, :], lhsT=Vc.t[0:127, g, :], rhs=pc.t[0:127, :], start=True, stop=True),
                                  reads=[pc.b, Vc.b], writes=[PS[bO].b])
                            yield
                            yield from to_token_major(g, 0, 97)
                            kb.op(DVE, lambda g=g: V.reciprocal(out=rdc_.t[:, 4 * g:4 * g + 4], in_=de.t[:, 0, 4 * g:4 * g + 4]), reads=[de.b], writes=[rdc_.b])
                            yield
                            kb.op(DVE, lambda g=g: V.tensor_tensor(out=impn_.t[:, 4 * g:4 * g + 4, :], in0=pxv()[:, :, 65:97],
                                                                   in1=rdc_.t[:, 4 * g:4 * g + 4].unsqueeze(2).broadcast_to([128, 4, 32]), op=ALU.mult),
                                  reads=[PS[bX].b, rdc_.b], writes=[impn_.b])
                            yield
                        kb.op(DVE, lambda: V.tensor_reduce(out=imp_.t[:], in_=impn_.t[:].rearrange("p (g r) j -> p g j r", g=2), axis=AX.X, op=ALU.add),
                              reads=[impn_.b], writes=[imp_.b])
                        yield
                        kb.op(DVE, lambda: V.tensor_tensor(out=imp_.t[:], in0=imp_.t[:], in1=addc.t[:, c:c + 1, :].broadcast_to([128, 2, 32]), op=ALU.add),
                              reads=[imp_.b, addc.b], writes=[imp_.b])
                        yield
                        for g in range(2):
                            kb.op(DVE, lambda g=g: V.max(out=top8_.t[:, g, :], in_=imp_.t[:, g, :]), reads=[imp_.b], writes=[top8_.b])
                        yield
                        for g in range(2):
                            kb.op(DVE, lambda g=g: V.tensor_scalar(out=qa.t[:, 4 * g:4 * g + 4, 64:96],
                                                                   in0=imp_.t[:, g:g + 1, :].broadcast_to([128, 4, 32]),
                                                                   scalar1=top8_.t[:, g, 7:8], scalar2=NEG, op0=ALU.is_lt, op1=ALU.mult),
                                  reads=[imp_.b, top8_.b], writes=[qa.b])
                        yield
                        transposes([qa.t[:, h, :] for h in range(8)], bZ, [qa.b])
                        yield
                        q2 = qT2[i2]
                        kb.op(ACT, lambda: A.copy(out=q2.t[:], in_=pb(bZ).rearrange("p (a b) -> p a b", b=128)), reads=[PS[bZ].b], writes=[q2.b])
                        yield
                        for g in range(2):
                            yield from branch(g, 1, KT_slc, V_slc, list(range(0, c + 1)), q2)
                            yield from branch(g, 2, KT_win, V_win, list(range(max(0, c - 4), c + 1)), q2)
                        rd_, coef_, oacc_, otmp_ = rd[i2], coef[i2], oacc[i2], otmp[i2]
                        kb.op(DVE, lambda: V.reciprocal(out=rd_.t[:], in_=de.t[:]), reads=[de.b], writes=[rd_.b])
                        yield
                        kb.op(DVE, lambda: V.tensor_tensor(out=coef_.t[:], in0=gt.t[:].rearrange("p (h b) -> p b h", b=3), in1=rd_.t[:], op=ALU.mult),
                              reads=[gt.b, rd_.b], writes=[coef_.b])
                        yield
                        kb.op(DVE, lambda: V.tensor_tensor(out=oacc_.t[:], in0=nu.t[:, 0], in1=coef_.t[:, 0, :].unsqueeze(2).broadcast_to([128, 8, 64]), op=ALU.mult),
                              reads=[nu.b, coef_.b], writes=[oacc_.b])
                        kb.op(POOL, lambda: G.tensor_tensor(out=otmp_.t[:], in0=nu.t[:, 1], in1=coef_.t[:, 1, :].unsqueeze(2).broadcast_to([128, 8, 64]), op=ALU.mult),
                              reads=[nu.b, coef_.b], writes=[otmp_.b])
                        yield
                        kb.op(DVE, lambda: V.tensor_tensor(out=oacc_.t[:], in0=oacc_.t[:], in1=otmp_.t[:], op=ALU.add), reads=[oacc_.b, otmp_.b], writes=[oacc_.b])
                        yield
                        kb.op(POOL, lambda: G.tensor_tensor(out=otmp_.t[:], in0=nu.t[:, 2], in1=coef_.t[:, 2, :].unsqueeze(2).broadcast_to([128, 8, 64]), op=ALU.mult),
                              reads=[nu.b, coef_.b], writes=[otmp_.b])
                        yield
                        yt = ytok[i2]
                        kb.op(DVE, lambda: V.tensor_tensor(out=yt.t[:], in0=oacc_.t[:].rearrange("p a b -> p (a b)"), in1=otmp_.t[:].rearrange("p a b -> p (a b)"), op=ALU.add),
                              reads=[oacc_.b, otmp_.b], writes=[yt.b])
                        if ydbg is not None:
                            kb.op(POOL, lambda: G.tensor_tensor(out=ydbg.t[:], in0=oacc_.t[:].rearrange("p a b -> p (a b)"), in1=otmp_.t[:].rearrange("p a b -> p (a b)"), op=ALU.add),
                                  reads=[oacc_.b, otmp_.b], writes=[ydbg.b])
                            dbg_store("y_nsa", ydbg.t[:], tok, [ydbg.b])
                        yield
                        transposes([yt.t[:, k * 128:(k + 1) * 128] for k in range(4)], bZ, [yt.b])
                        yield
                        kb.op(ACT, lambda: A.copy(out=ynsaT.t[:, :, tok], in_=pb(bZ)[:, 0:512].rearrange("p (a b) -> p a b", b=128)),
                              reads=[PS[bZ].b], writes=[ynsaT.bs[c]])
                        yield

                    run_interleaved([(lambda c=c: tile_gen(c)) for c in range(NT)], width=WIDTH)

            def phase_1d():
                with ExitStack() as s4:
                    wr = sb(s4, "wr", [128, KC, 2048], BF16, 4)
                    for j in range(4):
                        kb.dma(POOL, wr.t[:, :, j * 512:(j + 1) * 512], w_in_v[:, :, C_R + j * 512:C_R + (j + 1) * 512], writes=[wr.bs[j]])
                    idT = sb(s4, "idT", [128, 4, 128], F32)
                    qdec = sb(s4, "qdec", [128, 4, 128], F32)
                    kdec = sb(s4, "kdec", [128, 4, 128], F32)
                    gn = sb(s4, "gn", [128, 512], F32)
                    eij = sb(s4, "eij", [128, 128], F32)
                    rowq = sb(s4, "rowq", [128, 128], F32)
                    rowk = sb(s4, "rowk", [128, 128], F32)
                    kb.dma(SP, gn.t[:], ret_gn_g.rearrange("o a b -> o (a b)").broadcast_to([128, 512]), writes=[gn.b])
                    kb.op(POOL, lambda: G.iota(eij.t[:], pattern=[[1, 128]], base=0, channel_multiplier=-1, allow_small_or_imprecise_dtypes=True), writes=[eij.b])
                    kb.op(POOL, lambda: G.iota(rowq.t[:], pattern=[[1, 128]], base=1, channel_multiplier=0, allow_small_or_imprecise_dtypes=True), writes=[rowq.b])
                    kb.op(POOL, lambda: G.iota(rowk.t[:], pattern=[[-1, 128]], base=127, channel_multiplier=0, allow_small_or_imprecise_dtypes=True), writes=[rowk.b])
                    lgs = [float(np.log(1.0 - 2.0 ** (-5.0 - h))) for h in range(4)]
                    cds = [float(np.exp(128.0 * np.float32(lg))) for lg in lgs]
                    for h in range(4):
                        kb.op(ACT, lambda h=h: A.activation(out=idT.t[:, h, :], in_=eij.t[:], func=AF.Exp, scale=lgs[h]), reads=[eij.b], writes=[idT.b])
                        kb.op(POOL, lambda h=h: G.affine_select(out=idT.t[:, h, :], in_=idT.t[:, h, :], pattern=[[1, 128]], compare_op=ALU.is_ge, fill=0.0,
                                                                base=0, channel_multiplier=-1), reads=[idT.b], writes=[idT.b])
                        kb.op(ACT, lambda h=h: A.activation(out=qdec.t[:, h, :], in_=rowq.t[:], func=AF.Exp, scale=lgs[h]), reads=[rowq.b], writes=[qdec.b])
                        kb.op(ACT, lambda h=h: A.activation(out=kdec.t[:, h, :], in_=rowk.t[:], func=AF.Exp, scale=lgs[h]), reads=[rowk.b], writes=[kdec.b])
                    qTr = sb(s4, "qTr", [128, 4, 512], BF16)
                    qdT = sb(s4, "qdT", [128, 4, 512], BF16)
                    kTr = sb(s4, "kTr", [128, 4, 512], BF16)
                    kdT = sb(s4, "kdT", [128, 4, 512], BF16)
                    v_sb = [sb(s4, f"v_sb{i}", [128, 4, 128], BF16) for i in range(2)]
                    sgl = [sb(s4, f"sgl{i}", [128, 512], F32) for i in range(2)]
                    kd = [sb(s4, f"kd{i}", [128, 4, 128], BF16) for i in range(2)]
                    attb = [sb(s4, f"attb{i}", [128, 4, 128], BF16) for i in range(2)]
                    state_f = sb(s4, "state_f", [128, 4, 128], F32)
                    state_b = sb(s4, "state_b", [128, 4, 128], BF16)
                    yr = [sb(s4, f"yr{i}", [128, 512], BF16) for i in range(2)]
                    yrdbg = sb(s4, "yrdbg", [128, 512], F32) if "y_ret" in dbg else None
                    kb.op(POOL, lambda: G.memset(state_f.t[:], 0.0), writes=[state_f.b])
                    kb.op(POOL, lambda: G.memset(state_b.t[:], 0.0), writes=[state_b.b])
                    KS = float(128.0 ** -0.5)
                    pcnt = [0]
                    state_done = [False] * (NT + 1)
                    bst = [sb(s4, f"bst{i}", [128, 4, 6], F32) for i in range(2)]
                    mv = [sb(s4, f"mv{i}", [128, 4, 2], F32) for i in range(2)]
                    rs4 = [sb(s4, f"rs4{i}", [128, 12], F32) for i in range(2)]
                    on = [sb(s4, f"on{i}", [128, 512], F32) for i in range(2)]

                    def p1d_gen(c):
                        i2 = c % 2
                        cl = c % 4
                        bA, bB, bT = 2 + 3 * i2, 3 + 3 * i2, 4 + 3 * i2
                        tok = slice(c * 128, (c + 1) * 128)
                        cs = slice(cl * 128, (cl + 1) * 128)
                        bst_, mv_, rs4_, on_ = bst[i2], mv[i2], rs4[i2], on[i2]
                        for k in range(KC):
                            kb.op(PE, lambda k=k: TE.matmul(pf(bA), lhsT=hT.t[:, k, tok], rhs=wr.t[:, k, 1024:1536], start=(k == 0), stop=(k == KC - 1)),
                                  reads=[hT.bs[c], wr.bs[2]], writes=[PS[bA].b], inc=(k == KC - 1))
                        yield
                        for k in range(KC):
                            kb.op(PE, lambda k=k: TE.matmul(pf(bB), lhsT=hT.t[:, k, tok], rhs=wr.t[:, k, 1536:2048], start=(k == 0), stop=(k == KC - 1)),
                                  reads=[hT.bs[c], wr.bs[3]], writes=[PS[bB].b], inc=(k == KC - 1))
                        yield
                        vs, sg_, kd_, ab = v_sb[i2], sgl[i2], kd[i2], attb[i2]
                        kb.op(ACT, lambda: A.copy(out=vs.t[:].rearrange("p a b -> p (a b)"), in_=pf(bA)), reads=[PS[bA].b], writes=[vs.b])
                        yield
                        kb.op(ACT, lambda: A.activation(out=sg_.t[:], in_=pf(bB), func=AF.Silu), reads=[PS[bB].b], writes=[sg_.b])
                        yield
                        transposes([kdT.t[:, h, cs] for h in range(4)], bT, [kdT.b])
                        yield
                        kb.op(ACT, lambda: A.copy(out=kd_.t[:].rearrange("p a b -> p (a b)"), in_=pb(bT)[:, 0:512]), reads=[PS[bT].b], writes=[kd_.b])
                        yield
                        for h in range(4):
                            kb.op(PE, lambda h=h: TE.matmul(pf(bA)[:, h * 128:(h + 1) * 128], lhsT=kTr.t[:, h, cs], rhs=qTr.t[:, h, cs], start=True, stop=True),
                                  reads=[kTr.b, qTr.b], writes=[PS[bA].b], inc=(h == 3))
                        yield
                        kb.op(DVE, lambda: V.tensor_tensor(out=ab.t[:], in0=pf(bA).rearrange("p (a b) -> p a b", b=128), in1=idT.t[:], op=ALU.mult),
                              reads=[PS[bA].b, idT.b], writes=[ab.b])
                        yield
                        while c > 0 and not state_done[c - 1]:
                            yield
                        for h in range(4):
                            kb.op(PE, lambda h=h: TE.matmul(pf(bB)[:, h * 128:(h + 1) * 128], lhsT=ab.t[:, h, :], rhs=vs.t[:, h, :], start=True, stop=(c == 0)),
                                  reads=[ab.b, vs.b], writes=[PS[bB].b], inc=(c == 0 and h == 3))
                            if c > 0:
                                kb.op(PE, lambda h=h: TE.matmul(pf(bB)[:, h * 128:(h + 1) * 128], lhsT=qdT.t[:, h, cs], rhs=state_b.t[:, h, :], start=False, stop=True),
                                      reads=[qdT.b, state_b.b], writes=[PS[bB].b], inc=(h == 3))
                        yield
                        if c < NT - 1:
                            for h in range(4):
                                kb.op(PE, lambda h=h: TE.matmul(pf(bA)[:, h * 128:(h + 1) * 128], lhsT=kd_.t[:, h, :], rhs=vs.t[:, h, :], start=True, stop=True),
                                      reads=[kd_.b, vs.b], writes=[PS[bA].b], inc=(h == 3))
                            yield
                            for h in range(4):
                                kb.op(DVE, lambda h=h: V.scalar_tensor_tensor(out=state_f.t[:, h, :], in0=state_f.t[:, h, :], scalar=cds[h],
                                                                              in1=pf(bA)[:, h * 128:(h + 1) * 128], op0=ALU.mult, op1=ALU.add),
                                      reads=[state_f.b, PS[bA].b], writes=[state_f.b])
                            kb.op(POOL, lambda: G.tensor_copy(out=state_b.t[:], in_=state_f.t[:]), reads=[state_f.b], writes=[state_b.b])
                        state_done[c] = True
                        yield
                        for h in range(4):
                            kb.op(DVE, lambda h=h: V.bn_stats(out=bst_.t[:, h, :], in_=pf(bB)[:, h * 128:(h + 1) * 128]), reads=[PS[bB].b], writes=[bst_.b])
                        yield
                        for h in range(4):
                            kb.op(DVE, lambda h=h: V.bn_aggr(out=mv_.t[:, h, :], in_=bst_.t[:, h, :]), reads=[bst_.b], writes=[mv_.b])
                        yield
                        kb.op(DVE, lambda: V.tensor_scalar(out=rs4_.t[:, 0:4], in0=mv_.t[:, :, 1], scalar1=EPS, scalar2=None, op0=ALU.add), reads=[mv_.b], writes=[rs4_.b])
                        yield
                        kb.op(POOL, lambda: G.tensor_tensor(out=rs4_.t[:, 8:12], in0=rs4_.t[:, 0:4], in1=nhalf.t[:, 0:4], op=ALU.pow),
                              reads=[rs4_.b, nhalf.b], writes=[rs4_.b])
                        yield
                        for h in range(4):
                            kb.op(DVE, lambda h=h: V.tensor_scalar(out=on_.t[:, h * 128:(h + 1) * 128], in0=pf(bB)[:, h * 128:(h + 1) * 128],
                                                                   scalar1=mv_.t[:, h, 0:1], scalar2=rs4_.t[:, 8 + h:9 + h], op0=ALU.subtract, op1=ALU.mult),
                                  reads=[PS[bB].b, mv_.b, rs4_.b], writes=[on_.b])
                        yield
                        kb.op(POOL, lambda: G.tensor_tensor(out=on_.t[:], in0=on_.t[:], in1=gn.t[:], op=ALU.mult), reads=[on_.b, gn.b], writes=[on_.b])
                        yield
                        y_ = yr[i2]
                        kb.op(DVE, lambda: V.tensor_tensor(out=y_.t[:], in0=on_.t[:], in1=sg_.t[:], op=ALU.mult), reads=[on_.b, sg_.b], writes=[y_.b])
                        if yrdbg is not None:
                            kb.op(DVE, lambda: V.tensor_tensor(out=yrdbg.t[:], in0=on_.t[:], in1=sg_.t[:], op=ALU.mult), reads=[on_.b, sg_.b], writes=[yrdbg.b])
                            dbg_store("y_ret", yrdbg.t[:], tok, [yrdbg.b])
                        yield
                        transposes([y_.t[:, k * 128:(k + 1) * 128] for k in range(4)], bT, [y_.b])
                        yield
                        kb.op(ACT, lambda: A.copy(out=yretT.t[:, :, tok], in_=pb(bT)[:, 0:512].rearrange("p (a b) -> p a b", b=128)),
                              reads=[PS[bT].b], writes=[yretT.bs[c]])
                        yield

                    for tg in range(4):
                        tks = slice(tg * 512, (tg + 1) * 512)
                        hbs = [hT.bs[4 * tg + i] for i in range(4)]
                        for qk in range(2):
                            for h in range(4):
                                bk = pcnt[0] % 2
                                pcnt[0] += 1
                                for k in range(KC):
                                    kb.op(PE, lambda k=k, qk=qk, h=h, bk=bk: TE.matmul(pf(bk), lhsT=wr.t[:, k, qk * 512 + h * 128:qk * 512 + (h + 1) * 128],
                                                                                      rhs=hT.t[:, k, tks], start=(k == 0), stop=(k == KC - 1)),
                                          reads=hbs + [wr.bs[qk]], writes=[PS[bk].b], inc=(k == KC - 1))
                                pv4 = pf(bk).rearrange("p (a b) -> p a b", b=128)
                                if qk == 0:
                                    kb.op(ACT, lambda h=h, bk=bk: A.copy(out=qTr.t[:, h, :], in_=pf(bk)), reads=[PS[bk].b], writes=[qTr.b])
                                    kb.op(DVE, lambda h=h, pv4=pv4: V.tensor_tensor(out=qdT.t[:, h, :].rearrange("p (a b) -> p a b", b=128), in0=pv4,
                                                                                    in1=qdec.t[:, h:h + 1, :].broadcast_to([128, 4, 128]), op=ALU.mult),
                                          reads=[PS[bk].b, qdec.b], writes=[qdT.b])
                                else:
                                    kb.op(ACT, lambda h=h, bk=bk: A.mul(out=kTr.t[:, h, :], in_=pf(bk), mul=KS), reads=[PS[bk].b], writes=[kTr.b])
                                    kb.op(DVE, lambda h=h, pv4=pv4: V.scalar_tensor_tensor(out=kdT.t[:, h, :].rearrange("p (a b) -> p a b", b=128), in0=pv4, scalar=KS,
                                                                                           in1=kdec.t[:, h:h + 1, :].broadcast_to([128, 4, 128]),
                                                                                           op0=ALU.mult, op1=ALU.mult),
                                          reads=[PS[bk].b, kdec.b], writes=[kdT.b])
                        run_interleaved([(lambda c=c: p1d_gen(c)) for c in range(4 * tg, 4 * tg + 4)], width=2)

            def phase_1e():
                with ExitStack() as s5:
                    xt = [sb(s5, f"xte{i}", [128, D], F32) for i in range(2)]
                    g2bc = sb(s5, "g2bc", [128, D], F32)
                    kb.dma(SP, g2bc.t[:], norm2_g[0:1, :].broadcast_to([128, D]), writes=[g2bc.b])
                    wm = sb(s5, "wm", [128, KC, 2048], BF16, 4)
                    for j in range(4):
                        kb.dma(POOL, wm.t[:, :, j * 512:(j + 1) * 512], w_in_v[:, :, C_M + j * 512:C_M + (j + 1) * 512], writes=[wm.bs[j]])
                    wbr = sb(s5, "wbr", [128, 8, D], BF16)
                    load_w(wbr, w_branch[0].rearrange("n (k p) d -> p (n k) d", p=128))
                    wo = sb(s5, "wo", [128, KC, D], BF16)
                    load_w(wo, w_out[0].rearrange("(k p) d -> p k d", p=128))
                    gates = [sb(s5, f"gates{i}", [128, D], F32, 2) for i in range(2)]
                    tmix = [sb(s5, f"tmix{i}", [128, D], F32) for i in range(2)]
                    tmix2 = [sb(s5, f"tmix2{i}", [128, D], F32) for i in range(2)]
                    mixed = [sb(s5, f"mixed{i}", [128, D], BF16) for i in range(2)]
                    mixT = [sb(s5, f"mixT{i}", [128, KC, 128], BF16) for i in range(2)]
                    x1t = [sb(s5, f"x1t{i}", [128, D], F32) for i in range(2)]

                    def p1e_gen(c):
                        i2 = c % 2
                        bs_ = [4 * i2 + i for i in range(4)]
                        tok = slice(c * 128, (c + 1) * 128)
                        xx = xt[i2]
                        kb.dma(SP, xx.t[:], x[tok, :], writes=[xx.b])
                        gt_, tm = gates[i2], (tmix[i2], tmix2[i2])
                        for n, yT in ((0, ynsaT), (1, yretT)):
                            for half in range(2):
                                j = 2 * n + half
                                for k in range(KC):
                                    kb.op(PE, lambda j=j, k=k, half=half: TE.matmul(pf(bs_[half]), lhsT=hT.t[:, k, tok], rhs=wm.t[:, k, j * 512:(j + 1) * 512],
                                                                                    start=(k == 0), stop=(k == KC - 1)),
                                          reads=[hT.bs[c], wm.bs[j]], writes=[PS[bs_[half]].b], inc=(k == KC - 1))
                                yield
                            for half in range(2):
                                bk = bs_[2 + half]
                                for k in range(4):
                                    kb.op(PE, lambda n=n, half=half, k=k, bk=bk, yT=yT: TE.matmul(pf(bk), lhsT=yT.t[:, k, tok], rhs=wbr.t[:, n * 4 + k, half * 512:(half + 1) * 512],
                                                                                                  start=(k == 0), stop=(k == 3)),
                                          reads=[yT.bs[c], wbr.b], writes=[PS[bk].b], inc=(k == 3))
                                yield
                            for half in range(2):
                                kb.op(ACT, lambda half=half: A.activation(out=gt_.t[:, half * 512:(half + 1) * 512], in_=pf(bs_[half]), func=AF.Sigmoid),
                                      reads=[PS[bs_[half]].b], writes=[gt_.bs[half]])
                                yield
                            for half in range(2):
                                hs = slice(half * 512, (half + 1) * 512)
                                kb.op(DVE, lambda half=half, hs=hs, n=n: V.tensor_tensor(out=tm[n].t[:, hs], in0=gt_.t[:, hs], in1=pf(bs_[2 + half]), op=ALU.mult),
                                      reads=[gt_.bs[half], PS[bs_[2 + half]].b], writes=[tm[n].b])
                                yield
                        mx = mixed[i2]
                        kb.op(POOL, lambda: G.tensor_tensor(out=mx.t[:], in0=tm[0].t[:], in1=tm[1].t[:], op=ALU.add), reads=[tm[0].b, tm[1].b], writes=[mx.b])
                        yield
                        transposes([mx.t[:, k * 128:(k + 1) * 128] for k in range(KC)], bs_[0], [mx.b])
                        yield
                        mt = mixT[i2]
                        kb.op(ACT, lambda: A.copy(out=mt.t[:], in_=pb(bs_[0]).rearrange("p (a b) -> p a b", b=128)), reads=[PS[bs_[0]].b], writes=[mt.b])
                        yield
                        for half in range(2):
                            for k in range(KC):
                                kb.op(PE, lambda half=half, k=k: TE.matmul(pf(bs_[2 + half]), lhsT=mt.t[:, k, :], rhs=wo.t[:, k, half * 512:(half + 1) * 512],
                                                                           start=(k == 0), stop=(k == KC - 1)),
                                      reads=[mt.b, wo.b], writes=[PS[bs_[2 + half]].b], inc=(k == KC - 1))
                            yield
                        x1 = x1t[i2]
                        for half in range(2):
                            hs = slice(half * 512, (half + 1) * 512)
                            kb.op(DVE, lambda half=half, hs=hs: V.tensor_tensor(out=x1.t[:, hs], in0=xx.t[:, hs], in1=pf(bs_[2 + half]), op=ALU.add),
                                  reads=[xx.b, PS[bs_[2 + half]].b], writes=[x1.b])
                            yield
                        kb.dma(SP, xmid[tok, :], x1.t[:], reads=[x1.b])
                        dbg_store("x1", x1.t[:], tok, [x1.b])
                        yield from norm_gen(x1, c, g2bc, i2, bs_[1])

                    run_interleaved([(lambda c=c: p1e_gen(c)) for c in range(NT)], width=2)

            def phase_2():
                with ExitStack() as s6:
                    xt = [sb(s6, f"xtf{i}", [128, D], F32) for i in range(2)]
                    wd = sb(s6, "wd", [128, NFB, D], BF16, 2)
                    wd_v = ffn_w_down[0].rearrange("(fb p) d -> p fb d", p=128)
                    kb.dma(POOL, wd.t[:, 0:11, :], wd_v[:, 0:11, :], writes=[wd.bs[0]])
                    kb.dma(POOL, wd.t[:, 11:22, :], wd_v[:, 11:22, :], writes=[wd.bs[1]])
                    wgs = [sb(s6, f"wgs{i}", [128, KC, 256], BF16) for i in range(2)]
                    wus = [sb(s6, f"wus{i}", [128, KC, 256], BF16) for i in range(2)]
                    act = sb(s6, "act", [128, NFB, 1024], BF16, NFB)
                    sgs = [sb(s6, f"sgs{i}", [128, 512], F32) for i in range(2)]
                    outt = [sb(s6, f"outt{i}", [128, D], F32) for i in range(2)]
                    wg_v = ffn_w_gate[0].rearrange("(k p) f -> p k f", p=128)
                    wu_v = ffn_w_up[0].rearrange("(k p) f -> p k f", p=128)
                    cn = [0, 0]
                    for hf in range(2):
                        for fg in range(11):
                            cols = slice(fg * 256, (fg + 1) * 256)
                            wg_, wu_ = wgs[fg % 2], wus[fg % 2]
                            kb.dma(POOL, wg_.t[:], wg_v[:, :, cols], writes=[wg_.b])
                            kb.dma(POOL, wu_.t[:], wu_v[:, :, cols], writes=[wu_.b])
                            for fl in range(2):
                                fb = fg * 2 + fl
                                for t2 in range(2):
                                    tokc = slice(hf * 1024 + t2 * 512, hf * 1024 + (t2 + 1) * 512)
                                    hbs = [hT.bs[hf * 8 + t2 * 4 + i] for i in range(4)]
                                    gb, ub = (0, 1) if cn[0] % 2 == 0 else (2, 3)
                                    cn[0] += 1
                                    for k in range(KC):
                                        kb.op(PE, lambda k=k, fl=fl, gb=gb, wg_=wg_, tokc=tokc: TE.matmul(pf(gb), lhsT=wg_.t[:, k, fl * 128:(fl + 1) * 128], rhs=hT.t[:, k, tokc],
                                                                                                          start=(k == 0), stop=(k == KC - 1)),
                                              reads=hbs + [wg_.b], writes=[PS[gb].b], inc=(k == KC - 1))
                                    for k in range(KC):
                                        kb.op(PE, lambda k=k, fl=fl, ub=ub, wu_=wu_, tokc=tokc: TE.matmul(pf(ub), lhsT=wu_.t[:, k, fl * 128:(fl + 1) * 128], rhs=hT.t[:, k, tokc],
                                                                                                          start=(k == 0), stop=(k == KC - 1)),
                                              reads=hbs + [wu_.b], writes=[PS[ub].b], inc=(k == KC - 1))
                                    sg_ = sgs[cn[0] % 2]
                                    kb.op(ACT, lambda sg_=sg_, gb=gb: A.activation(out=sg_.t[:], in_=pf(gb), func=AF.Silu), reads=[PS[gb].b], writes=[sg_.b])
                                    kb.op(DVE, lambda sg_=sg_, ub=ub, fb=fb, t2=t2: V.tensor_tensor(out=act.t[:, fb, t2 * 512:(t2 + 1) * 512], in0=sg_.t[:], in1=pf(ub), op=ALU.mult),
                                          reads=[sg_.b, PS[ub].b], writes=[act.bs[fb]])
                        for tl in range(8):
                            c = hf * 8 + tl
                            tok = slice(c * 128, (c + 1) * 128)
                            xx = xt[c % 2]
                            kb.dma(SP, xx.t[:], xmid[tok, :], writes=[xx.b])
                            ob = (4, 5) if cn[1] % 2 == 0 else (6, 7)
                            cn[1] += 1
                            for half in range(2):
                                for fb in range(NFB):
                                    kb.op(PE, lambda half=half, fb=fb, ob=ob, tl=tl: TE.matmul(pf(ob[half]), lhsT=act.t[:, fb, tl * 128:(tl + 1) * 128],
                                                                                               rhs=wd.t[:, fb, half * 512:(half + 1) * 512],
                                                                                               start=(fb == 0), stop=(fb == NFB - 1)),
                                          reads=[act.bs[fb], wd.bs[0 if fb < 11 else 1]], writes=[PS[ob[half]].b], inc=(fb == NFB - 1))
                            ot = outt[c % 2]
                            for half in range(2):
                                hs = slice(half * 512, (half + 1) * 512)
                                kb.op(DVE, lambda half=half, hs=hs, ob=ob, ot=ot, xx=xx: V.tensor_tensor(out=ot.t[:, hs], in0=xx.t[:, hs], in1=pf(ob[half]), op=ALU.add),
                                      reads=[xx.b, PS[ob[half]].b], writes=[ot.b])
                            kb.dma(SP, out[tok, :], ot.t[:], reads=[ot.b], is_out=True)

            with ExitStack() as sB:
                ynsaT = sb(sB, "ynsaT", [128, 4, S], BF16, NT)
                yretT = sb(sB, "yretT", [128, 4, S], BF16, NT)

                with ExitStack() as sA:
                    gq = sb(sA, "gq", [128, 64], F32)
                    gk = sb(sA, "gk", [128, 3, 64], F32)
                    QAL = sb(sA, "QAL", [128, NT, 8, 4], BF16)
                    KAL = sb(sA, "KAL", [128, NT, 4], BF16)
                    KCAL = sb(sA, "KCAL", [128, 4], BF16)
                    OH = sb(sA, "OH", [128, NT, 32], BF16)
                    dmask = sb(sA, "dmask", [128, 4, 128], BF16)
                    tmask = sb(sA, "tmask", [128, 4, 128], BF16)
                    cmask = sb(sA, "cmask", [128, S], BF16)
                    addc = sb(sA, "addc", [128, NT, 32], F32)
                    ov = sb(sA, "ov", [128, 32], BF16)
                    KT_slc = sb(sA, "KT_slc", [128, 2, S], BF16, NT)
                    KT_win = sb(sA, "KT_win", [128, 2, S], BF16, NT)
                    V_slc = sb(sA, "V_slc", [128, NT, 2, 65], BF16, NT)
                    V_win = sb(sA, "V_win", [128, NT, 2, 65], BF16, NT)
                    KcT = sb(sA, "KcT", [128, 2, 128], BF16)
                    Vc = sb(sA, "Vc", [128, 2, 97], BF16)

                    with ExitStack() as s0:
                        SL = sb(s0, "SL", [128, 8], F32)
                        th128 = sb(s0, "th128", [128, NT], F32)
                        pidx = sb(s0, "pidx", [128, 1], F32)
                        QALf = sb(s0, "QALf", [128, NT, 8, 4], F32)
                        KALf = sb(s0, "KALf", [128, NT, 4], F32)
                        KCALf = sb(s0, "KCALf", [128, 4], F32)
                        rel = sb(s0, "rel", [128, NT, 32], F32)
                        f0 = sb(s0, "f0", [128, NT, 32], F32)
                        f1 = sb(s0, "f1", [128, NT, 32], F32)
                        t1 = sb(s0, "t1", [128, NT, 32], F32)
                        hp = sb(s0, "hp", [128, 1], F32)
                        ovf = sb(s0, "ovf", [128, 32], F32)
                        ova = sb(s0, "ova", [128, 32], F32)
                        ones_b = sb(s0, "ones_b", [128, 512], BF16)
                        zeros_b = sb(s0, "zeros_b", [128, 512], BF16)

                        kb.dma(SP, gq.t[:], nsa_q_norm[0:1, :].broadcast_to([128, 64]), writes=[gq.b])
                        kb.dma(SP, gk.t[:].rearrange("p a b -> p (a b)"),
                               nsa_k_norm.rearrange("o a b -> o (a b)").broadcast_to([128, 192]), writes=[gk.b])
                        kb.op(DVE, lambda: V.tensor_scalar(out=gq.t[:], in0=gq.t[:], scalar1=0.125, scalar2=None, op0=ALU.mult),
                              reads=[gq.b], writes=[gq.b])
                        for h in range(8):
                            kb.op(POOL, lambda h=h: G.memset(SL.t[:, h:h + 1], 2.0 ** (-(h + 1))), writes=[SL.b])
                        kb.op(POOL, lambda: G.iota(th128.t[:], pattern=[[128, NT]], base=0, channel_multiplier=0,
                                                   allow_small_or_imprecise_dtypes=True), writes=[th128.b])
                        kb.op(POOL, lambda: G.iota(pidx.t[:], pattern=[[0, 1]], base=0, channel_multiplier=1,
                                                   allow_small_or_imprecise_dtypes=True), writes=[pidx.b])
                        SLb = SL.t[:].unsqueeze(1).broadcast_to([128, NT, 8])
                        THb = th128.t[:].unsqueeze(2).broadcast_to([128, NT, 8])
                        kb.op(DVE, lambda: V.scalar_tensor_tensor(out=QALf.t[:, :, :, 0], in0=THb, scalar=-1.0, in1=SLb,
                                                                  op0=ALU.mult, op1=ALU.mult),
                              reads=[SL.b, th128.b], writes=[QALf.b])
                        kb.op(DVE, lambda: V.tensor_scalar(out=QALf.t[:, :, :, 1], in0=SLb, scalar1=pidx.t[:, 0:1], scalar2=-1.0,
                                                           op0=ALU.mult, op1=ALU.mult),
                              reads=[SL.b, pidx.b], writes=[QALf.b])
                        kb.op(DVE, lambda: V.tensor_copy(out=QALf.t[:, :, :, 2], in_=SLb), reads=[SL.b], writes=[QALf.b])
                        kb.op(DVE, lambda: V.tensor_copy(out=QALf.t[:, :, :, 3], in_=SLb), reads=[SL.b], writes=[QALf.b])
                        kb.op(DVE, lambda: V.tensor_copy(out=QAL.t[:], in_=QALf.t[:]), reads=[QALf.b], writes=[QAL.b])
                        kb.op(POOL, lambda: G.memset(KALf.t[:, :, 0:2], 1.0), writes=[KALf.b])
                        kb.op(DVE, lambda: V.tensor_copy(out=KALf.t[:, :, 2], in_=th128.t[:]), reads=[th128.b], writes=[KALf.b])
                        kb.op(DVE, lambda: V.tensor_copy(out=KALf.t[:, :, 3], in_=pidx.t[:, 0:1].broadcast_to([128, NT])),
                              reads=[pidx.b], writes=[KALf.b])
                        kb.op(DVE, lambda: V.tensor_copy(out=KAL.t[:], in_=KALf.t[:]), reads=[KALf.b], writes=[KAL.b])
                        kb.op(POOL, lambda: G.memset(KCALf.t[:, 0:2], 1.0), writes=[KCALf.b])
                        kb.op(POOL, lambda: G.memset(KCALf.t[:, 3:4], 31.0), reads=[], writes=[KCALf.b])
                        kb.op(DVE, lambda: V.tensor_scalar(out=KCALf.t[:, 2:3], in0=pidx.t[:, 0:1], scalar1=16.0, scalar2=None,
                                                           op0=ALU.mult), reads=[pidx.b], writes=[KCALf.b])
                        kb.op(DVE, lambda: V.tensor_copy(out=KCAL.t[:], in_=KCALf.t[:]), reads=[KCALf.b], writes=[KCAL.b])
                        kb.op(POOL, lambda: G.memset(OH.t[:], 0.0), writes=[OH.b])
                        for kt in range(NT):
                            kb.op(POOL, lambda kt=kt: G.memset(OH.t[0:64, kt, 2 * kt:2 * kt + 1], 1.0), writes=[OH.b])
                            kb.op(POOL, lambda kt=kt: G.memset(OH.t[64:128, kt, 2 * kt + 1:2 * kt + 2], 1.0), writes=[OH.b])
                        kb.op(POOL, lambda: G.memset(ones_b.t[:], 1.0), writes=[ones_b.b])
                        kb.op(POOL, lambda: G.memset(zeros_b.t[:], 0.0), writes=[zeros_b.b])
                        ob4 = ones_b.t[:].rearrange("p (a b) -> p a b", b=128)
                        kb.op(POOL, lambda: G.affine_select(out=dmask.t[:], in_=ob4, pattern=[[0, 4], [1, 128]],
                                                            compare_op=ALU.is_ge, fill=0.0, base=0, channel_multiplier=-1),
                              reads=[ones_b.b], writes=[dmask.b])
                        kb.op(POOL, lambda: G.affine_select(out=tmask.t[:], in_=ob4, pattern=[[0, 4], [-1, 128]],
                                                            compare_op=ALU.is_gt, fill=0.0, base=0, channel_multiplier=1),
                              reads=[ones_b.b], writes=[tmask.b])
                        for i in range(4):
                            kb.op(POOL, lambda i=i: G.affine_select(out=cmask.t[:, i * 512:(i + 1) * 512], in_=zeros_b.t[:],
                                                                    pattern=[[1, 512]], compare_op=ALU.is_ge, fill=NEG,
                                                                    base=-31 + 512 * i, channel_multiplier=-16),
                                  reads=[zeros_b.b], writes=[cmask.b])
                        kb.op(POOL, lambda: G.iota(rel.t[:], pattern=[[-2, NT], [1, 32]], base=0, channel_multiplier=0,
                                                   allow_small_or_imprecise_dtypes=True), writes=[rel.b])
                        kb.op(DVE, lambda: V.tensor_scalar(out=hp.t[:], in0=pidx.t[:], scalar1=64.0, scalar2=None, op0=ALU.is_ge),
                              reads=[pidx.b], writes=[hp.b])
                        kb.op(DVE, lambda: V.tensor_scalar(out=rel.t[:], in0=rel.t[:], scalar1=hp.t[:, 0:1], scalar2=None,
                                                           op0=ALU.subtract), reads=[rel.b, hp.b], writes=[rel.b])
                        kb.op(DVE, lambda: V.tensor_scalar(out=t1.t[:], in0=rel.t[:], scalar1=0.0, scalar2=-1e9,
                                                           op0=ALU.is_gt, op1=ALU.mult), reads=[rel.b], writes=[t1.b])
                        kb.op(DVE, lambda: V.tensor_scalar(out=f0.t[:], in0=rel.t[:], scalar1=0.0, scalar2=None, op0=ALU.is_equal),
                              reads=[rel.b], writes=[f0.b])
                        kb.op(DVE, lambda: V.tensor_scalar(out=f1.t[:], in0=rel.t[:], scalar1=-1.0, scalar2=None, op0=ALU.is_equal),
                              reads=[rel.b], writes=[f1.b])
                        kb.op(DVE, lambda: V.tensor_tensor(out=f0.t[:], in0=f0.t[:], in1=f1.t[:], op=ALU.max),
                              reads=[f0.b, f1.b], writes=[f0.b])
                        kb.op(DVE, lambda: V.memset(f0.t[:, :, 0:1], 1.0), reads=[], writes=[f0.b])
                        kb.op(DVE, lambda: V.scalar_tensor_tensor(out=addc.t[:], in0=f0.t[:], scalar=1e4, in1=t1.t[:],
                                                                  op0=ALU.mult, op1=ALU.add), reads=[f0.b, t1.b], writes=[addc.b])
                        kb.op(POOL, lambda: G.iota(ovf.t[:], pattern=[[-64, 32]], base=0, channel_multiplier=16,
                                                   allow_small_or_imprecise_dtypes=True), writes=[ovf.b])
                        kb.op(DVE, lambda: V.tensor_scalar(out=ova.t[:], in0=ovf.t[:], scalar1=63.0, scalar2=None, op0=ALU.is_le),
                              reads=[ovf.b], writes=[ova.b])
                        kb.op(DVE, lambda: V.tensor_scalar(out=ovf.t[:], in0=ovf.t[:], scalar1=-31.0, scalar2=None, op0=ALU.is_ge),
                              reads=[ovf.b], writes=[ovf.b])
                        kb.op(DVE, lambda: V.tensor_tensor(out=ov.t[:], in0=ova.t[:], in1=ovf.t[:], op=ALU.mult),
                              reads=[ova.b, ovf.b], writes=[ov.b])
                        kb.op(POOL, lambda: G.memset(V_slc.t[:, :, :, 64:65], 1.0), writes=V_slc.bs)
                        kb.op(POOL, lambda: G.memset(V_win.t[:, :, :, 64:65], 1.0), writes=V_win.bs)
                        kb.barrier()
                        chk(1)

                    with ExitStack() as s1:
                        xt = [sb(s1, f"xta{i}", [128, D], F32) for i in range(2)]
                        g1bc = sb(s1, "g1bc", [128, D], F32)
                        kb.dma(SP, g1bc.t[:], norm1_g[0:1, :].broadcast_to([128, D]), writes=[g1bc.b])
                        def p1a_gen(c):
                            xx = xt[c % 2]
                            kb.dma(SP, xx.t[:], x[c * 128:(c + 1) * 128, :], writes=[xx.b])
                            yield
                            yield from norm_gen(xx, c, g1bc, c % 2, 6 + (c % 2))

                        run_interleaved([(lambda c=c: p1a_gen(c)) for c in range(NT)], width=2)
                        kb.barrier()
                        chk(2)

                    with ExitStack() as s2:
                        cmpT = sb(s2, "cmpT", [128, 2, S], BF16, NT)
                        with ExitStack() as s2a:
                            wkv = sb(s2a, "wkv", [128, KC, 768], BF16)
                            load_w(wkv, w_in_v[:, :, C_KV:C_KV + 768])
                            cmp_tok = [sb(s2a, f"cmp_tok{i}", [128, 256], BF16) for i in range(2)]
                            sqk = [sb(s2a, f"sqk{i}", [128, 256], F32) for i in range(2)]
                            kst = sb(s2a, "kst", [128, NT, 16], F32, NT)
                            tmpk = [sb(s2a, f"tmpk{i}", [128, 4, 64], F32) for i in range(2)]
                            ka_slc = [sb(s2a, f"ka_slc{i}", [128, 2, 128], BF16) for i in range(2)]
                            ka_win = [sb(s2a, f"ka_win{i}", [128, 2, 128], BF16) for i in range(2)]
                            for i in range(2):
                                kb.op(POOL, lambda i=i: G.memset(ka_slc[i].t[:], 0.0), writes=[ka_slc[i].b])
                                kb.op(POOL, lambda i=i: G.memset(ka_win[i].t[:], 0.0), writes=[ka_win[i].b])
                            def p1b_gen(c):
                                i2 = c % 2
                                bA, bB = (0, 1) if i2 == 0 else (2, 3)
                                tok = slice(c * 128, (c + 1) * 128)
                                if CUT >= 1:
                                    yield
                                    for k in range(KC):
                                        kb.op(PE, lambda k=k: TE.matmul(pf(bA), lhsT=hT.t[:, k, tok], rhs=wkv.t[:, k, 0:512],
                                                                        start=(k == 0), stop=(k == KC - 1)),
                                              reads=[hT.bs[c], wkv.b], writes=[PS[bA].b], inc=(k == KC - 1))
                                    for k in range(KC):
                                        kb.op(PE, lambda k=k: TE.matmul(pf(bB)[:, 0:256], lhsT=hT.t[:, k, tok], rhs=wkv.t[:, k, 512:768],
                                                                        start=(k == 0), stop=(k == KC - 1)),
                                              reads=[hT.bs[c], wkv.b], writes=[PS[bB].b], inc=(k == KC - 1))
                                if CUT >= 2:
                                    yield
                                    ct = cmp_tok[i2]
                                    kb.op(ACT, lambda: A.copy(out=ct.t[:], in_=pf(bA)[:, 0:256]), reads=[PS[bA].b], writes=[ct.b])
                                    sq = sqk[i2]
                                    kb.op(ACT, lambda: A.activation(out=sq.t[:, 0:128], in_=pf(bA)[:, 256:384], func=AF.Square),
                                          reads=[PS[bA].b], writes=[sq.b])
                                    kb.op(ACT, lambda: A.activation(out=sq.t[:, 128:256], in_=pf(bB)[:, 0:128], func=AF.Square),
                                          reads=[PS[bB].b], writes=[sq.b])
                                if CUT >= 3:
                                    yield
                                    ks = kst.t
                                    ksb = [kst.bs[c]]
                                    kb.op(DVE, lambda: V.tensor_reduce(out=ks[:, c, 0:4], in_=sq.t[:].rearrange("p (a b) -> p a b", b=64),
                                                                       axis=AX.X, op=ALU.add), reads=[sq.b], writes=ksb)
                                    rstd_from_ss(ks[:, c, 0:4], ks[:, c, 4:8], ks[:, c, 8:12], ks[:, c, 12:16], 64, ksb)
                                    tk = tmpk[i2]
                                    kb.op(DVE, lambda: V.tensor_tensor(out=tk.t[:, 0:2, :], in0=pf(bA)[:, 256:384].rearrange("p (a b) -> p a b", b=64),
                                                                       in1=ks[:, c, 12:14].unsqueeze(2).broadcast_to([128, 2, 64]), op=ALU.mult),
                                          reads=[PS[bA].b] + ksb, writes=[tk.b])
                                    kb.op(DVE, lambda: V.tensor_tensor(out=tk.t[:, 2:4, :], in0=pf(bB)[:, 0:128].rearrange("p (a b) -> p a b", b=64),
                                                                       in1=ks[:, c, 14:16].unsqueeze(2).broadcast_to([128, 2, 64]), op=ALU.mult),
                                          reads=[PS[bB].b] + ksb, writes=[tk.b])
                                    ksl, kwn = ka_slc[i2], ka_win[i2]
                                    kb.op(DVE, lambda: V.tensor_tensor(out=ksl.t[:, :, 0:64], in0=tk.t[:, 0:2, :],
                                                                       in1=gk.t[:, 1:2, :].broadcast_to([128, 2, 64]), op=ALU.mult),
                                          reads=[tk.b, gk.b], writes=[ksl.b])
                                    kb.op(DVE, lambda: V.tensor_tensor(out=kwn.t[:, :, 0:64], in0=tk.t[:, 2:4, :],
                                                                       in1=gk.t[:, 2:3, :].broadcast_to([128, 2, 64]), op=ALU.mult),
                                          reads=[tk.b, gk.b], writes=[kwn.b])
                                if CUT >= 4:
                                    yield
                                    kb.op(POOL, lambda: G.tensor_copy(out=ksl.t[:, :, 64:96], in_=OH.t[:, c:c + 1, :].broadcast_to([128, 2, 32])),
                                          reads=[OH.b], writes=[ksl.b])
                                    kb.op(POOL, lambda: G.tensor_copy(out=ksl.t[:, :, 96:100], in_=KAL.t[:, c:c + 1, :].broadcast_to([128, 2, 4])),
                                          reads=[KAL.b], writes=[ksl.b])
                                    kb.op(POOL, lambda: G.tensor_copy(out=kwn.t[:, :, 96:100], in_=KAL.t[:, c:c + 1, :].broadcast_to([128, 2, 4])),
                                          reads=[KAL.b], writes=[kwn.b])
                                if CUT >= 5:
                                    yield
                                    kb.op(ACT, lambda: A.copy(out=V_slc.t[:, c, :, 0:64], in_=pf(bA)[:, 384:512].rearrange("p (a b) -> p a b", b=64)),
                                          reads=[PS[bA].b], writes=[V_slc.bs[c]])
                                    kb.op(ACT, lambda: A.copy(out=V_win.t[:, c, :, 0:64], in_=pf(bB)[:, 128:256].rearrange("p (a b) -> p a b", b=64)),
                                          reads=[PS[bB].b], writes=[V_win.bs[c]])
                                if CUT >= 6:
                                    yield
                                    tb = 4 + i2
                                    transposes([ksl.t[:, 0, :], ksl.t[:, 1, :], kwn.t[:, 0, :], kwn.t[:, 1, :], ct.t[:, 0:128], ct.t[:, 128:256]],
                                               tb, [ksl.b, kwn.b, ct.b])
                                    pv3 = pb(tb).rearrange("p (a b) -> p a b", b=128)
                                    kb.op(ACT, lambda: A.copy(out=KT_slc.t[:, :, tok], in_=pv3[:, 0:2, :]), reads=[PS[tb].b], writes=[KT_slc.bs[c]])
                                    kb.op(ACT, lambda: A.copy(out=KT_win.t[:, :, tok], in_=pv3[:, 2:4, :]), reads=[PS[tb].b], writes=[KT_win.bs[c]])
                                    kb.op(ACT, lambda: A.copy(out=cmpT.t[:, :, tok], in_=pv3[:, 4:6, :]), reads=[PS[tb].b], writes=[cmpT.bs[c]])
                            run_interleaved([(lambda c=c: p1b_gen(c)) for c in range(NT)], width=2)
                            if "KT_slc" in dbg:
                                kdb = sb(s2a, "kdb", [128, 2, S], F32)
                                kb.op(DVE, lambda: V.tensor_copy(out=kdb.t[:], in_=KT_slc.t[:]), reads=KT_slc.bs, writes=[kdb.b])
                                kb.dma(SP, dbg["KT_slc"].rearrange("p (a b) -> p a b", b=S), kdb.t[:], reads=[kdb.b], is_out=True)
                            kb.barrier()
                            chk(3)

                        with ExitStack() as s2b:
                            w1sb = sb(s2b, "w1sb", [128, 2, 32, 128], BF16)
                            w2sb = sb(s2b, "w2sb", [128, 2, 64], BF16)
                            pe_sb = sb(s2b, "pe_sb", [32, 2, 64], F32)
                            peT = sb(s2b, "peT", [64, 2, 32], BF16)
                            bias_c = sb(s2b, "bias_c", [128, 2], F32)
                            xh = sb(s2b, "xh", [128, 128], F32)
                            x2 = sb(s2b, "x2", [128, 128], F32)
                            sg = sb(s2b, "sgc", [128, 128], F32)
                            HTb = sb(s2b, "HTb", [128, 128], BF16)
                            kca = sb(s2b, "kca", [128, 2, 128], BF16)
                            cst = sb(s2b, "cst", [128, 8], F32)
                            tmpc = sb(s2b, "tmpc", [128, 64], F32)
                            for kv in range(2):
                                src = cmp_w1[0, kv].rearrange("l d f -> d l f")
                                kb.dma(POOL, w1sb.t[0:64, kv], src, writes=[w1sb.b])
                                kb.dma(POOL, w1sb.t[64:128, kv], src, writes=[w1sb.b])
                            kb.dma(POOL, w2sb.t[:], cmp_w2[0].rearrange("k f d -> f k d"), writes=[w2sb.b])
                            kb.dma(SP, pe_sb.t[:], cmp_pe[0].rearrange("k l d -> l k d"), writes=[pe_sb.b])
                            kb.op(POOL, lambda: G.memset(kca.t[:], 0.0), writes=[kca.b])
                            kb.op(POOL, lambda: G.memset(Vc.t[:], 0.0), writes=[Vc.b])
                            kb.op(POOL, lambda: G.tensor_copy(out=kca.t[:, :, 96:100], in_=KCAL.t[:].unsqueeze(1).broadcast_to([128, 2, 4])),
                                  reads=[KCAL.b], writes=[kca.b])
                            kb.op(POOL, lambda: G.memset(Vc.t[:, :, 64:65], 1.0), writes=[Vc.b])
                            kb.op(POOL, lambda: G.tensor_copy(out=Vc.t[:, :, 65:97], in_=ov.t[:].unsqueeze(1).broadcast_to([128, 2, 32])),
                                  reads=[ov.b], writes=[Vc.b])
                            for kv in range(2):
                                kb.op(PE, lambda kv=kv: TE.transpose(out=pf(0)[0:64, kv * 32:(kv + 1) * 32], in_=pe_sb.t[0:32, kv, :],
                                                                     identity=ident_f.t[0:32, 0:32]),
                                      reads=[pe_sb.b, ident_f.b], writes=[PS[0].b])
                            kb.op(DVE, lambda: V.tensor_copy(out=peT.t[:], in_=pf(0)[0:64, 0:64].rearrange("p (a b) -> p a b", b=32)),
                                  reads=[PS[0].b], writes=[peT.b])
                            for kv in range(2):
                                for l in range(32):
                                    kb.op(PE, lambda kv=kv, l=l: TE.matmul(pf(1)[:, kv:kv + 1], lhsT=w1sb.t[0:64, kv, l, :], rhs=peT.t[0:64, kv, l:l + 1],
                                                                           start=(l == 0), stop=(l == 31)),
                                          reads=[w1sb.b, peT.b], writes=[PS[1].b], inc=(l == 31))
                            kb.op(DVE, lambda: V.tensor_copy(out=bias_c.t[:], in_=pf(1)[:, 0:2]), reads=[PS[1].b], writes=[bias_c.b])
                            it = 0
                            for kv in range(2):
                                for g in range(2):
                                    bH = 2 + (it % 2)
                                    bO = 4 + (it % 2)
                                    it += 1
                                    for l in range(32):
                                        kb.op(PE, lambda kv=kv, g=g, l=l: TE.matmul(
                                            pf(bH)[:, 0:127], lhsT=w1sb.t[g * 64:(g + 1) * 64, kv, l, :],
                                            rhs=cmpT.t[g * 64:(g + 1) * 64, kv, l:l + 16 * 126 + 1:16],
                                            start=(l == 0), stop=(l == 31)),
                                            reads=[w1sb.b] + cmpT.bs, writes=[PS[bH].b], inc=(l == 31))
                                    kb.op(ACT, lambda kv=kv: A.activation(out=xh.t[:, 0:127], in_=pf(bH)[:, 0:127], func=AF.Identity,
                                                                          bias=bias_c.t[:, kv:kv + 1], scale=1.0),
                                          reads=[PS[bH].b, bias_c.b], writes=[xh.b])
                                    kb.op(DVE, lambda: V.tensor_tensor(out=x2.t[:, 0:127], in0=xh.t[:, 0:127], in1=xh.t[:, 0:127], op=ALU.mult),
                                          reads=[xh.b], writes=[x2.b])
                                    kb.op(DVE, lambda: V.tensor_scalar(out=x2.t[:, 0:127], in0=x2.t[:, 0:127], scalar1=0.044715, scalar2=1.0,
                                                                       op0=ALU.mult, op1=ALU.add), reads=[x2.b], writes=[x2.b])
                                    kb.op(DVE, lambda: V.tensor_tensor(out=x2.t[:, 0:127], in0=x2.t[:, 0:127], in1=xh.t[:, 0:127], op=ALU.mult),
                                          reads=[x2.b, xh.b], writes=[x2.b])
                                    kb.op(ACT, lambda: A.activation(out=sg.t[:, 0:127], in_=x2.t[:, 0:127], func=AF.Sigmoid, scale=1.5957691216057308),
                                          reads=[x2.b], writes=[sg.b])
                                    kb.op(DVE, lambda: V.tensor_tensor(out=HTb.t[:, 0:127], in0=xh.t[:, 0:127], in1=sg.t[:, 0:127], op=ALU.mult),
                                          reads=[xh.b, sg.b], writes=[HTb.b])
                                    kb.op(PE, lambda kv=kv: TE.matmul(pf(bO)[0:127, 0:64], lhsT=HTb.t[:, 0:127], rhs=w2sb.t[:, kv, :], start=True, stop=True),
                                          reads=[HTb.b, w2sb.b], writes=[PS[bO].b])
                                    if kv == 0:
                                        kb.op(ACT, lambda g=g: A.activation(out=tmpc.t[0:127, :], in_=pf(bO)[0:127, 0:64], func=AF.Square,
                                                                            accum_out=cst.t[0:127, g:g + 1]),
                                              reads=[PS[bO].b], writes=[tmpc.b, cst.b])
                                        rstd_from_ss(cst.t[0:127, g:g + 1], cst.t[0:127, 2 + g:3 + g], cst.t[0:127, 4 + g:5 + g], cst.t[0:127, 6 + g:7 + g], 64, [cst.b])
                                        kb.op(DVE, lambda g=g: V.scalar_tensor_tensor(out=kca.t[0:127, g, 0:64], in0=pf(bO)[0:127, 0:64],
                                                                                      scalar=cst.t[0:127, 6 + g:7 + g], in1=gk.t[0:127, 0, :],
                                                                                      op0=ALU.mult, op1=ALU.mult),
                                              reads=[PS[bO].b, cst.b, gk.b], writes=[kca.b])
                                    else:
                                        kb.op(ACT, lambda g=g: A.copy(out=Vc.t[0:127, g, 0:64], in_=pf(bO)[0:127, 0:64]),
                                              reads=[PS[bO].b], writes=[Vc.b])
                            transposes([kca.t[:, 0, :], kca.t[:, 1, :]], 6, [kca.b])
                            kb.op(ACT, lambda: A.copy(out=KcT.t[:], in_=pb(6)[:, 0:256].rearrange("p (a b) -> p a b", b=128)),
                                  reads=[PS[6].b], writes=[KcT.b])
                            if "kc" in dbg:
                                kcd = sb(s2b, "kcd", [128, 2, 64], F32)
                                kb.op(DVE, lambda: V.tensor_copy(out=kcd.t[:], in_=kca.t[:, :, 0:64]), reads=[kca.b], writes=[kcd.b])
                                kb.dma(SP, dbg["kc"].rearrange("p (a b) -> p a b", b=64), kcd.t[:], reads=[kcd.b], is_out=True)
                            if "vc" in dbg:
                                vcd = sb(s2b, "vcd", [128, 2, 64], F32)
                                kb.op(DVE, lambda: V.tensor_copy(out=vcd.t[:], in_=Vc.t[:, :, 0:64]), reads=[Vc.b], writes=[vcd.b])
                                kb.dma(SP, dbg["vc"].rearrange("p (a b) -> p a b", b=64), vcd.t[:], reads=[vcd.b], is_out=True)
                            kb.barrier()
                            chk(4)

                    phase_1c()
                    kb.barrier()
                    chk(5)

                phase_1d()
                kb.barrier()
                chk(6)
                phase_1e()
                kb.barrier()
                chk(7)

            phase_2()
            kb.finish()
    except _Stop:
        pass
    return nc


_NAMES = ["x", "norm1_g", "w_in", "nsa_q_norm", "nsa_k_norm", "cmp_pe", "cmp_w1", "cmp_w2", "ret_gn_g",
          "w_branch", "w_out", "norm2_g", "ffn_w_gate", "ffn_w_up", "ffn_w_down"]


def kernel(**inputs):
    n = 8
    arrs = {k: np.ascontiguousarray(np.asarray(inputs[k], dtype=np.float32)) for k in _NAMES}
    nc = build_nc()
    in_maps = []
    for i in range(n):
        m = {k: arrs[k] for k in _NAMES if k != "x"}
        m["x"] = np.ascontiguousarray(arrs["x"][i])
        in_maps.append(m)
    res = run_bass_kernel_spmd(nc, in_maps, core_ids=list(range(n)))
    return np.stack([np.asarray(r["out"], dtype=np.float32) for r in res.results], axis=0)
```

```python
import numpy as np
from contextlib import ExitStack
import concourse.bass as bass
import concourse.mybir as mybir
from concourse.bass_utils import run_bass_kernel_spmd

F32 = mybir.dt.float32
BF16 = mybir.dt.bfloat16
AF = mybir.ActivationFunctionType
ALU = mybir.AluOpType
AX = mybir.AxisListType

S = 2048
D = 1024
NT = 16
KC = 8
N_IN = 5400
DFF = 2816
NFB = 22
EPS = 1e-6
SEM_LIMIT = 24000
import os as _os
CUT = int(_os.environ.get('P1B_CUT', '99'))
CUTC = int(_os.environ.get('P1C_CUT', '99'))
SUBC = int(_os.environ.get('P1C_SUB', '99'))
WIDTH = int(_os.environ.get('P1C_WIDTH', '2'))
NEG = -30000.0

C_Q = 0
C_KV = 512
C_G = 1280
C_R = 1304
C_M = 3352


class Buf:
    __slots__ = ("name", "w", "r", "excl")

    def __init__(self, name):
        self.name = name
        self.w = None
        self.r = []
        self.excl = False


class SemW:
    __slots__ = ("h",)

    def __init__(self, h):
        self.h = h


class Slot:
    __slots__ = ("sem", "val")

    def __init__(self, sem):
        self.sem = sem
        self.val = 0


class Q:
    def __init__(self, name, eng):
        self.name = name
        self.eng = eng
        self.sem = None
        self.count = 0
        self.waited = {}
        self.ring = []
        self.ri = 0
        self.pending = False


class T:
    def __init__(self, t, name, nb=1):
        self.t = t
        self.bs = [Buf(f"{name}{i}") for i in range(nb)]

    @property
    def b(self):
        return self.bs[0]


class KB:
    def __init__(self, nc, es):
        self.nc = nc
        self.es = es
        self.nsem = 0
        self.pe = self.mkq("pe", nc.tensor)
        self.act = self.mkq("act", nc.scalar)
        self.dve = self.mkq("dve", nc.vector)
        self.pool = self.mkq("pool", nc.gpsimd)
        self.sp = self.mkq("sp", nc.sync)
        self.qs = [self.pe, self.act, self.dve, self.pool, self.sp]
        for q, n in ((self.sp, 16), (self.pool, 8), (self.act, 4)):
            q.ring = [Slot(self.new_sem(f"{q.name}_d{i}")) for i in range(n)]
        self.out_toks = []

    def new_sem(self, name):
        self.nsem += 1
        return SemW(self.es.enter_context(self.nc.semaphore(f"{name}_{self.nsem}")))

    def mkq(self, name, eng):
        q = Q(name, eng)
        q.sem = self.new_sem(name)
        return q

    def wait(self, q, tok):
        sw, val = tok[0], tok[1]
        if q.waited.get(sw, 0) >= val:
            return
        q.eng.wait_ge(sw.h, val)
        q.waited[sw] = val

    def _dep(self, q, tok, raw, force=False):
        if tok[2] is q and q is self.pe and not force:
            return
        self.wait(q, tok)

    def _deps(self, q, reads, writes, force=False):
        for b in reads:
            if b.w is not None:
                self._dep(q, b.w, True, force)
            if b.excl:
                for t in b.r:
                    if t[2] is not q:
                        self._dep(q, t, False, force)
        for b in writes:
            if b.w is not None:
                self._dep(q, b.w, False, force)
            for t in b.r:
                self._dep(q, t, False, force)

    def _record(self, tok, reads, writes):
        for b in reads:
            if tok[2] is not None:
                b.r = [t for t in b.r if t[2] is not tok[2]]
            b.r.append(tok)
        for b in writes:
            b.w = tok
            b.r = []

    def op(self, q, fn, reads=(), writes=(), inc=True):
        self._deps(q, reads, writes)
        ins = fn()
        if inc:
            if q.count >= SEM_LIMIT and not q.pending:
                q.sem = self.new_sem(q.name)
                q.count = 0
            ins.then_inc(q.sem.h, 1)
            q.count += 1
            q.pending = False
            tok = (q.sem, q.count, q)
        else:
            q.pending = True
            tok = (q.sem, q.count + 1, q)
        self._record(tok, reads, writes)
        return ins

    def dma(self, q, out, in_, reads=(), writes=(), is_out=False):
        self._deps(q, reads, writes, force=True)
        slot = q.ring[q.ri % len(q.ring)]
        q.ri += 1
        if slot.val > 0:
            self.wait(q, (slot.sem, slot.val))
        if slot.val >= SEM_LIMIT:
            slot.sem = self.new_sem(q.name + "_d")
            slot.val = 0
        ins = q.eng.dma_start(out=out, in_=in_)
        ins.then_inc(slot.sem.h, 16)
        slot.val += 16
        tok = (slot.sem, slot.val, None)
        self._record(tok, reads, writes)
        if is_out:
            self.out_toks.append(tok)
        return tok

    def barrier(self):
        toks = []
        for o in self.qs:
            if o.count > 0:
                toks.append((o.sem, o.count, o))
            for sl in o.ring:
                if sl.val > 0:
                    toks.append((sl.sem, sl.val, None))
        for q in self.qs:
            for t in toks:
                if t[2] is q:
                    continue
                self.wait(q, t)

    def finish(self):
        for t in self.out_toks:
            self.wait(self.sp, t)


class _Stop(Exception):
    pass


def build_nc(debug=None, stop=None):
    nc = bass.Bass("TRN2", target_bir_lowering=False)

    def din(name, shape):
        return nc.dram_tensor(name, list(shape), F32, kind="ExternalInput").ap()

    x = din("x", [S, D])
    norm1_g = din("norm1_g", [1, D])
    w_in = din("w_in", [1, D, N_IN])
    nsa_q_norm = din("nsa_q_norm", [1, 64])
    nsa_k_norm = din("nsa_k_norm", [1, 3, 64])
    cmp_pe = din("cmp_pe", [1, 2, 32, 64])
    cmp_w1 = din("cmp_w1", [1, 2, 32, 64, 128])
    cmp_w2 = din("cmp_w2", [1, 2, 128, 64])
    ret_gn_g = din("ret_gn_g", [1, 4, 128])
    w_branch = din("w_branch", [1, 2, 512, D])
    w_out = din("w_out", [1, D, D])
    norm2_g = din("norm2_g", [1, D])
    ffn_w_gate = din("ffn_w_gate", [1, D, DFF])
    ffn_w_up = din("ffn_w_up", [1, D, DFF])
    ffn_w_down = din("ffn_w_down", [1, DFF, D])
    out = nc.dram_tensor("out", [S, D], F32, kind="ExternalOutput").ap()
    xmid = nc.dram_tensor("xmid", [S, D], F32, kind="Internal").ap()
    dbg = {}
    if debug:
        for name, shape in debug.items():
            dbg[name] = nc.dram_tensor("dbg_" + name, list(shape), F32, kind="ExternalOutput").ap()

    w_in_v = w_in[0].rearrange("(k p) n -> p k n", p=128)

    try:
        with ExitStack() as es:
            kb = KB(nc, es)

            def chk(n):
                if stop is not None and n >= stop:
                    kb.barrier()
                    kb.finish()
                    raise _Stop()
            PE, ACT, DVE, POOL, SP = kb.pe, kb.act, kb.dve, kb.pool, kb.sp
            V, A, G, TE = nc.vector, nc.scalar, nc.gpsimd, nc.tensor

            def sb(scope, name, shape, dt, nb=1):
                return T(scope.enter_context(nc.sbuf_tensor(name, list(shape), dt)), name, nb)

            PS2 = [es.enter_context(nc.psum_tensor(f"psp{j}", [128, 1024], F32)) for j in range(4)]
            PS = [T(None, f"ps{i}") for i in range(8)]
            for p_ in PS:
                p_.b.excl = True

            def pf(i):
                return PS2[i // 2][:, (i % 2) * 512:(i % 2 + 1) * 512]

            def pb(i):
                return PS2[i // 2][:].bitcast(BF16)[:, (i % 2) * 1024:(i % 2 + 1) * 1024]

            def pf2(j):
                return PS2[j][:]

            ident_f = sb(es, "ident_f", [128, 128], F32)
            ident_b = sb(es, "ident_b", [128, 128], BF16)
            ones_f = sb(es, "ones_f", [128, 128], F32)
            nhalf = sb(es, "nhalf", [128, 16], F32)
            hT = sb(es, "hT", [128, KC, S], BF16, NT)
            stat = sb(es, "stat", [128, NT, 4], F32, NT)
            hb = [sb(es, f"hb{i}", [128, D], BF16) for i in range(2)]
            junk = sb(es, "junk", [128, D], BF16)

            kb.op(POOL, lambda: G.memset(ones_f.t[:], 1.0), writes=[ones_f.b])
            kb.op(POOL, lambda: G.memset(nhalf.t[:], -0.5), writes=[nhalf.b])
            kb.op(POOL, lambda: G.affine_select(out=ident_f.t[:], in_=ones_f.t[:, 0:128], pattern=[[1, 128]],
                                                compare_op=ALU.is_equal, fill=0.0, base=0, channel_multiplier=-1),
                  reads=[ones_f.b], writes=[ident_f.b])
            kb.op(DVE, lambda: V.tensor_copy(out=ident_b.t[:], in_=ident_f.t[:]), reads=[ident_f.b], writes=[ident_b.b])

            def rstd_from_ss(ss_ap, ms_ap, sd_ap, rs_ap, n, bufs):
                k = ms_ap.shape[-1]
                P_ = ms_ap.shape[0]
                kb.op(DVE, lambda: V.tensor_scalar(out=ms_ap, in0=ss_ap, scalar1=1.0 / n, scalar2=EPS,
                                                   op0=ALU.mult, op1=ALU.add), reads=bufs, writes=bufs)
                kb.op(POOL, lambda: G.tensor_tensor(out=rs_ap, in0=ms_ap, in1=nhalf.t[0:P_, 0:k], op=ALU.pow),
                      reads=list(bufs) + [nhalf.b], writes=bufs)

            def transposes(src_aps, bank, reads):
                pbv = pb(bank)
                n = len(src_aps)
                for i, ap in enumerate(src_aps):
                    kb.op(PE, lambda ap=ap, i=i: TE.transpose(out=pbv[:, i * 128:(i + 1) * 128], in_=ap, identity=ident_b.t[:]),
                          reads=list(reads) + [ident_b.b], writes=[PS[bank].b], inc=(i == n - 1))

            def norm_gen(src, c, gbc, sidx, bank):
                sbuf_ = [stat.bs[c]]
                st = stat.t
                kb.op(ACT, lambda: A.activation(out=junk.t[:], in_=src.t[:], func=AF.Square, accum_out=st[:, c, 0:1]),
                      reads=[src.b], writes=[junk.b] + sbuf_)
                yield
                kb.op(DVE, lambda: V.tensor_scalar(out=st[:, c, 1:2], in0=st[:, c, 0:1], scalar1=1.0 / D, scalar2=EPS,
                                                   op0=ALU.mult, op1=ALU.add), reads=sbuf_, writes=sbuf_)
                yield
                kb.op(POOL, lambda: G.tensor_tensor(out=st[:, c, 3:4], in0=st[:, c, 1:2], in1=nhalf.t[:, 0:1], op=ALU.pow),
                      reads=sbuf_ + [nhalf.b], writes=sbuf_)
                yield
                h = hb[sidx % 2]
                kb.op(DVE, lambda: V.scalar_tensor_tensor(out=h.t[:], in0=src.t[:], scalar=st[:, c, 3:4], in1=gbc.t[:],
                                                          op0=ALU.mult, op1=ALU.mult),
                      reads=[src.b, gbc.b] + sbuf_, writes=[h.b])
                yield
                transposes([h.t[:, k * 128:(k + 1) * 128] for k in range(KC)], bank, [h.b])
                yield
                kb.op(ACT, lambda: A.copy(out=hT.t[:, :, c * 128:(c + 1) * 128],
                                          in_=pb(bank).rearrange("p (a b) -> p a b", b=128)),
                      reads=[PS[bank].b], writes=[hT.bs[c]])
                yield

            def run_interleaved(gen_fns, width=2):
                pending = list(gen_fns)
                active = []
                while pending or active:
                    while pending and len(active) < width:
                        active.append(pending.pop(0)())
                    for g_ in list(active):
                        try:
                            next(g_)
                        except StopIteration:
                            active.remove(g_)

            def load_w(dst, src_ap, q=None):
                kb.dma(q or POOL, dst.t[:], src_ap, writes=[dst.b])

            def dbg_store(name, src_ap, rows, reads):
                if name in dbg:
                    kb.dma(SP, dbg[name][rows], src_ap, reads=reads, is_out=True)

            def phase_1c():
                with ExitStack() as s3:
                    wq = sb(s3, "wq", [128, KC, 512], BF16)
                    load_w(wq, w_in_v[:, :, C_Q:C_Q + 512])
                    wg = sb(s3, "wg", [128, KC, 24], BF16)
                    load_w(wg, w_in_v[:, :, C_G:C_G + 24])
                    sqq = [sb(s3, f"sqq{i}", [128, 512], F32) for i in range(2)]
                    qst = sb(s3, "qst", [128, NT, 32], F32, NT)
                    tmpq = [sb(s3, f"tmpq{i}", [128, 8, 64], F32) for i in range(2)]
                    qaug = [sb(s3, f"qaug{i}", [128, 8, 128], BF16) for i in range(2)]
                    qT = [sb(s3, f"qT{i}", [128, 8, 128], BF16) for i in range(2)]
                    qT2 = [sb(s3, f"qT2{i}", [128, 8, 128], BF16) for i in range(2)]
                    gate = [sb(s3, f"gate{i}", [128, 24], F32) for i in range(2)]
                    scl = [sb(s3, f"scl{i}", [128, 1024], F32) for i in range(2)]
                    NPT = 3
                    PT = [[sb(s3, f"PT{i}_{j}", [128, 1024], BF16) for j in range(NPT)] for i in range(2)]
                    oTs = [sb(s3, f"oTs{i}", [128, 1024], F32) for i in range(2)]
                    num = [sb(s3, f"num{i}", [128, 3, 8, 64], F32) for i in range(2)]
                    den = [sb(s3, f"den{i}", [128, 3, 8], F32) for i in range(2)]
                    rdc = [sb(s3, f"rdc{i}", [128, 8], F32) for i in range(2)]
                    impn = [sb(s3, f"impn{i}", [128, 8, 32], F32) for i in range(2)]
                    imp = [sb(s3, f"imp{i}", [128, 2, 32], F32) for i in range(2)]
                    top8 = [sb(s3, f"top8{i}", [128, 2, 8], F32) for i in range(2)]
                    rd = [sb(s3, f"rd{i}", [128, 3, 8], F32) for i in range(2)]
                    coef = [sb(s3, f"coef{i}", [128, 3, 8], F32) for i in range(2)]
                    oacc = [sb(s3, f"oacc{i}", [128, 8, 64], F32) for i in range(2)]
                    otmp = [sb(s3, f"otmp{i}", [128, 8, 64], F32) for i in range(2)]
                    ytok = [sb(s3, f"ytok{i}", [128, 512], BF16) for i in range(2)]
                    ydbg = sb(s3, "ydbg", [128, 512], F32) if "y_nsa" in dbg else None
                    for i in range(2):
                        kb.op(POOL, lambda i=i: G.memset(qaug[i].t[:], 0.0), writes=[qaug[i].b])
                    ptc = [0, 0]

                    def tile_gen(c):
                        i2 = c % 2
                        base = 4 * i2
                        bZ, bG = base, base + 1
                        bO0, bO1 = base + 2, base + 3
                        Sb = [PS[bZ].b, PS[bG].b]
                        Ob = [PS[bO0].b, PS[bO1].b]
                        tok = slice(c * 128, (c + 1) * 128)

                        def S2():
                            return pf2(base // 2)

                        def O2():
                            return pf2(base // 2 + 1)

                        def X8():
                            return O2().rearrange("p (h c) -> p h c", h=8)

                        def to_token_major(br, ncol):
                            nu, de = num[i2], den[i2]
                            ot = oTs[i2]
                            kb.op(DVE, lambda: V.tensor_scalar(out=ot.t[0:ncol, :], in0=O2()[0:ncol, :], scalar1=1.0, scalar2=None, op0=ALU.mult),
                                  reads=Ob, writes=[ot.b])
                            yield
                            for hh in range(8):
                                kb.op(PE, lambda hh=hh: TE.transpose(out=X8()[:, hh, 0:ncol], in_=ot.t[0:ncol, hh * 128:(hh + 1) * 128],
                                                                     identity=ident_f.t[0:ncol, 0:ncol]),
                                      reads=[ot.b, ident_f.b], writes=[Ob[hh // 4]], inc=(hh % 4 == 3))
                            yield
                            kb.op(ACT, lambda: A.copy(out=nu.t[:, br, :, :], in_=X8()[:, :, 0:64]), reads=Ob, writes=[nu.b])
                            yield
                            kb.op(DVE, lambda: V.tensor_scalar(out=de.t[:, br, :], in0=X8()[:, :, 64], scalar1=1e-30, scalar2=None, op0=ALU.max),
                                  reads=Ob, writes=[de.b])
                            yield

                        def branch(br, KT, VV, kts, qsrc):
                            n = len(kts)
                            for j, kt in enumerate(kts):
                                for g in range(2):
                                    kb.op(PE, lambda kt=kt, g=g: TE.matmul(S2()[:, g * 512:(g + 1) * 512], lhsT=KT.t[:, g, kt * 128:(kt + 1) * 128],
                                                                           rhs=qsrc.t[:, 4 * g:4 * g + 4, :], start=True, stop=True),
                                          reads=[KT.bs[kt], qsrc.b], writes=[Sb[g]])
                                yield
                                pt = PT[i2][ptc[i2] % NPT]
                                ptc[i2] += 1
                                kb.op(ACT, lambda pt=pt: A.activation(out=pt.t[:], in_=S2(), func=AF.Exp), reads=Sb, writes=[pt.b])
                                yield
                                if kt == c:
                                    kb.op(DVE, lambda pt=pt: V.tensor_tensor(out=pt.t[:], in0=pt.t[:], in1=dmask8.t[:].rearrange("p a b -> p (a b)"), op=ALU.mult),
                                          reads=[pt.b, dmask8.b], writes=[pt.b])
                                    yield
                                elif br == 2 and kt == c - 4:
                                    kb.op(DVE, lambda pt=pt: V.tensor_tensor(out=pt.t[:], in0=pt.t[:], in1=tmask8.t[:].rearrange("p a b -> p (a b)"), op=ALU.mult),
                                          reads=[pt.b, tmask8.b], writes=[pt.b])
                                    yield
                                for g in range(2):
                                    kb.op(PE, lambda kt=kt, pt=pt, j=j, g=g: TE.matmul(O2()[0:65, g * 512:(g + 1) * 512], lhsT=VV.t[:, kt, g, :],
                                                                                       rhs=pt.t[:, g * 512:(g + 1) * 512], start=(j == 0), stop=(j == n - 1)),
                                          reads=[pt.b, VV.bs[kt]], writes=[Ob[g]], inc=(j == n - 1))
                                yield
                            yield from to_token_major(br, 65)

                        for k in range(KC):
                            kb.op(PE, lambda k=k: TE.matmul(pf(bZ), lhsT=hT.t[:, k, tok], rhs=wq.t[:, k, :], start=(k == 0), stop=(k == KC - 1)),
                                  reads=[hT.bs[c], wq.b], writes=[PS[bZ].b], inc=(k == KC - 1))
                        for k in range(KC):
                            kb.op(PE, lambda k=k: TE.matmul(pf(bG)[:, 0:24], lhsT=hT.t[:, k, tok], rhs=wg.t[:, k, :], start=(k == 0), stop=(k == KC - 1)),
                                  reads=[hT.bs[c], wg.b], writes=[PS[bG].b], inc=(k == KC - 1))
                        yield
                        sq = sqq[i2]
                        kb.op(ACT, lambda: A.activation(out=sq.t[:], in_=pf(bZ), func=AF.Square), reads=[PS[bZ].b], writes=[sq.b])
                        gt = gate[i2]
                        kb.op(ACT, lambda: A.activation(out=gt.t[:], in_=pf(bG)[:, 0:24], func=AF.Tanh, scale=0.5), reads=[PS[bG].b], writes=[gt.b])
                        yield
                        kb.op(DVE, lambda: V.tensor_scalar(out=gt.t[:], in0=gt.t[:], scalar1=0.5, scalar2=0.5, op0=ALU.mult, op1=ALU.add),
                              reads=[gt.b], writes=[gt.b])
                        qs = qst.t
                        qsb = [qst.bs[c]]
                        kb.op(DVE, lambda: V.tensor_reduce(out=qs[:, c, 0:8], in_=sq.t[:].rearrange("p (a b) -> p a b", b=64), axis=AX.X, op=ALU.add),
                              reads=[sq.b], writes=qsb)
                        yield
                        kb.op(DVE, lambda: V.tensor_scalar(out=qs[:, c, 8:16], in0=qs[:, c, 0:8], scalar1=1.0 / 64, scalar2=EPS, op0=ALU.mult, op1=ALU.add), reads=qsb, writes=qsb)
                        yield
                        kb.op(POOL, lambda: G.tensor_tensor(out=qs[:, c, 24:32], in0=qs[:, c, 8:16], in1=nhalf.t[:, 0:8], op=ALU.pow),
                              reads=qsb + [nhalf.b], writes=qsb)
                        yield
                        tq = tmpq[i2]
                        qa = qaug[i2]
                        kb.op(DVE, lambda: V.tensor_tensor(out=tq.t[:], in0=pf(bZ).rearrange("p (a b) -> p a b", b=64),
                                                           in1=qs[:, c, 24:32].unsqueeze(2).broadcast_to([128, 8, 64]), op=ALU.mult),
                              reads=[PS[bZ].b] + qsb, writes=[tq.b])
                        yield
                        kb.op(DVE, lambda: V.tensor_tensor(out=qa.t[:, :, 0:64], in0=tq.t[:], in1=gq.t[:].unsqueeze(1).broadcast_to([128, 8, 64]), op=ALU.mult),
                              reads=[tq.b, gq.b], writes=[qa.b])
                        kb.op(POOL, lambda: G.tensor_copy(out=qa.t[:, :, 96:100], in_=QAL.t[:, c, :, :]), reads=[QAL.b], writes=[qa.b])
                        yield
                        transposes([qa.t[:, h, :] for h in range(8)], bZ, [qa.b])
                        yield
                        q1 = qT[i2]
                        kb.op(ACT, lambda: A.copy(out=q1.t[:], in_=pb(bZ).rearrange("p (a b) -> p a b", b=128)), reads=[PS[bZ].b], writes=[q1.b])
                        yield
                        nu, de = num[i2], den[i2]
                        rdc_, impn_, imp_, top8_ = rdc[i2], impn[i2], imp[i2], top8[i2]
                        sc_ = scl[i2]
                        pc = PT[i2][ptc[i2] % NPT]
                        ptc[i2] += 1
                        for g in range(2):
                            kb.op(PE, lambda g=g: TE.matmul(S2()[0:127, g * 512:(g + 1) * 512], lhsT=KcT.t[:, g, 0:127], rhs=q1.t[:, 4 * g:4 * g + 4, :], start=True, stop=True),
                                  reads=[KcT.b, q1.b], writes=[Sb[g]])
                        yield
                        kb.op(DVE, lambda: V.scalar_tensor_tensor(out=sc_.t[0:127, :].rearrange("p (a b) -> p a b", b=128),
                                                                  in0=S2()[0:127, :].rearrange("p (a b) -> p a b", b=128), scalar=60.0,
                                                                  in1=cmask.t[0:127, tok].unsqueeze(1).broadcast_to([127, 8, 128]),
                                                                  op0=ALU.min, op1=ALU.add),
                              reads=Sb + [cmask.b], writes=[sc_.b])
                        yield
                        kb.op(ACT, lambda: A.activation(out=pc.t[0:127, :], in_=sc_.t[0:127, :], func=AF.Exp), reads=[sc_.b], writes=[pc.b])
                        yield
                        for g in range(2):
                            kb.op(PE, lambda g=g: TE.matmul(O2()[0:97, g * 512:(g + 1) * 512], lhsT=Vc.t[0:127, g, :], rhs=pc.t[0:127, g * 512:(g + 1) * 512], start=True, stop=True),
                                  reads=[pc.b, Vc.b], writes=[Ob[g]])
                        yield
                        yield from to_token_major(0, 97)
                        kb.op(DVE, lambda: V.reciprocal(out=rdc_.t[:], in_=de.t[:, 0, :]), reads=[de.b], writes=[rdc_.b])
                        yield
                        kb.op(DVE, lambda: V.tensor_tensor(out=impn_.t[:], in0=X8()[:, :, 65:97],
                                                           in1=rdc_.t[:].unsqueeze(2).broadcast_to([128, 8, 32]), op=ALU.mult),
                              reads=Ob + [rdc_.b], writes=[impn_.b])
                        yield
                        q2 = qT2[i2]

                        def imp_chain():
                            kb.op(DVE, lambda: V.tensor_reduce(out=imp_.t[:], in_=impn_.t[:].rearrange("p (g r) j -> p g j r", g=2), axis=AX.X, op=ALU.add),
                                  reads=[impn_.b], writes=[imp_.b])
                            yield
                            kb.op(DVE, lambda: V.tensor_tensor(out=imp_.t[:], in0=imp_.t[:], in1=addc.t[:, c:c + 1, :].broadcast_to([128, 2, 32]), op=ALU.add),
                                  reads=[imp_.b, addc.b], writes=[imp_.b])
                            yield
                            for g in range(2):
                                kb.op(DVE, lambda g=g: V.max(out=top8_.t[:, g, :], in_=imp_.t[:, g, :]), reads=[imp_.b], writes=[top8_.b])
                            yield
                            for g in range(2):
                                kb.op(DVE, lambda g=g: V.tensor_scalar(out=qa.t[:, 4 * g:4 * g + 4, 64:96],
                                                                       in0=imp_.t[:, g:g + 1, :].broadcast_to([128, 4, 32]),
                                                                       scalar1=top8_.t[:, g, 7:8], scalar2=NEG, op0=ALU.is_lt, op1=ALU.mult),
                                      reads=[imp_.b, top8_.b], writes=[qa.b])
                            yield

                        gw = branch(2, KT_win, V_win, list(range(max(0, c - 4), c + 1)), q1)
                        gi = imp_chain()
                        live = [gw, gi]
                        while live:
                            for g_ in list(live):
                                try:
                                    next(g_)
                                except StopIteration:
                                    live.remove(g_)
                            yield
                        transposes([qa.t[:, h, :] for h in range(8)], bZ, [qa.b])
                        kb.op(ACT, lambda: A.copy(out=q2.t[:], in_=pb(bZ).rearrange("p (a b) -> p a b", b=128)), reads=[PS[bZ].b], writes=[q2.b])
                        yield
                        yield from branch(1, KT_slc, V_slc, list(range(0, c + 1)), q2)
                        rd_, coef_, oacc_, otmp_ = rd[i2], coef[i2], oacc[i2], otmp[i2]
                        kb.op(DVE, lambda: V.reciprocal(out=rd_.t[:], in_=de.t[:]), reads=[de.b], writes=[rd_.b])
                        yield
                        kb.op(DVE, lambda: V.tensor_tensor(out=coef_.t[:], in0=gt.t[:].rearrange("p (h b) -> p b h", b=3), in1=rd_.t[:], op=ALU.mult),
                              reads=[gt.b, rd_.b], writes=[coef_.b])
                        yield
                        kb.op(DVE, lambda: V.tensor_tensor(out=oacc_.t[:], in0=nu.t[:, 0], in1=coef_.t[:, 0, :].unsqueeze(2).broadcast_to([128, 8, 64]), op=ALU.mult),
                              reads=[nu.b, coef_.b], writes=[oacc_.b])
                        kb.op(POOL, lambda: G.tensor_tensor(out=otmp_.t[:], in0=nu.t[:, 1], in1=coef_.t[:, 1, :].unsqueeze(2).broadcast_to([128, 8, 64]), op=ALU.mult),
                              reads=[nu.b, coef_.b], writes=[otmp_.b])
                        yield
                        kb.op(DVE, lambda: V.tensor_tensor(out=oacc_.t[:], in0=oacc_.t[:], in1=otmp_.t[:], op=ALU.add), reads=[oacc_.b, otmp_.b], writes=[oacc_.b])
                        yield
                        kb.op(POOL, lambda: G.tensor_tensor(out=otmp_.t[:], in0=nu.t[:, 2], in1=coef_.t[:, 2, :].unsqueeze(2).broadcast_to([128, 8, 64]), op=ALU.mult),
                              reads=[nu.b, coef_.b], writes=[otmp_.b])
                        yield
                        yt = ytok[i2]
                        kb.op(DVE, lambda: V.tensor_tensor(out=yt.t[:], in0=oacc_.t[:].rearrange("p a b -> p (a b)"), in1=otmp_.t[:].rearrange("p a b -> p (a b)"), op=ALU.add),
                              reads=[oacc_.b, otmp_.b], writes=[yt.b])
                        if ydbg is not None:
                            kb.op(POOL, lambda: G.tensor_tensor(out=ydbg.t[:], in0=oacc_.t[:].rearrange("p a b -> p (a b)"), in1=otmp_.t[:].rearrange("p a b -> p (a b)"), op=ALU.add),
                                  reads=[oacc_.b, otmp_.b], writes=[ydbg.b])
                            dbg_store("y_nsa", ydbg.t[:], tok, [ydbg.b])
                        yield
                        transposes([yt.t[:, k * 128:(k + 1) * 128] for k in range(4)], bZ, [yt.b])
                        yield
                        kb.op(ACT, lambda: A.copy(out=ynsaT.t[:, :, tok], in_=pb(bZ)[:, 0:512].rearrange("p (a b) -> p a b", b=128)),
                              reads=[PS[bZ].b], writes=[ynsaT.bs[c]])
                        yield

                    run_interleaved([(lambda c=c: tile_gen(c)) for c in range(NT)], width=WIDTH)

            def phase_1d():
                with ExitStack() as s4:
                    wr = sb(s4, "wr", [128, KC, 2048], BF16, 4)
                    for j in range(4):
                        kb.dma(POOL, wr.t[:, :, j * 512:(j + 1) * 512], w_in_v[:, :, C_R + j * 512:C_R + (j + 1) * 512], writes=[wr.bs[j]])
                    idT = sb(s4, "idT", [128, 4, 128], F32)
                    qdec = sb(s4, "qdec", [128, 4, 128], F32)
                    kdec = sb(s4, "kdec", [128, 4, 128], F32)
                    gn = sb(s4, "gn", [128, 512], F32)
                    eij = sb(s4, "eij", [128, 128], F32)
                    rowq = sb(s4, "rowq", [128, 128], F32)
                    rowk = sb(s4, "rowk", [128, 128], F32)
                    kb.dma(SP, gn.t[:], ret_gn_g.rearrange("o a b -> o (a b)").broadcast_to([128, 512]), writes=[gn.b])
                    kb.op(POOL, lambda: G.iota(eij.t[:], pattern=[[1, 128]], base=0, channel_multiplier=-1, allow_small_or_imprecise_dtypes=True), writes=[eij.b])
                    kb.op(POOL, lambda: G.iota(rowq.t[:], pattern=[[1, 128]], base=1, channel_multiplier=0, allow_small_or_imprecise_dtypes=True), writes=[rowq.b])
                    kb.op(POOL, lambda: G.iota(rowk.t[:], pattern=[[-1, 128]], base=127, channel_multiplier=0, allow_small_or_imprecise_dtypes=True), writes=[rowk.b])
                    lgs = [float(np.log(1.0 - 2.0 ** (-5.0 - h))) for h in range(4)]
                    cds = [float(np.exp(128.0 * np.float32(lg))) for lg in lgs]
                    for h in range(4):
                        kb.op(ACT, lambda h=h: A.activation(out=idT.t[:, h, :], in_=eij.t[:], func=AF.Exp, scale=lgs[h]), reads=[eij.b], writes=[idT.b])
                        kb.op(POOL, lambda h=h: G.affine_select(out=idT.t[:, h, :], in_=idT.t[:, h, :], pattern=[[1, 128]], compare_op=ALU.is_ge, fill=0.0,
                                                                base=0, channel_multiplier=-1), reads=[idT.b], writes=[idT.b])
                        kb.op(ACT, lambda h=h: A.activation(out=qdec.t[:, h, :], in_=rowq.t[:], func=AF.Exp, scale=lgs[h]), reads=[rowq.b], writes=[qdec.b])
                        kb.op(ACT, lambda h=h: A.activation(out=kdec.t[:, h, :], in_=rowk.t[:], func=AF.Exp, scale=lgs[h]), reads=[rowk.b], writes=[kdec.b])
                    qTr = sb(s4, "qTr", [128, 4, 512], BF16)
                    qdT = sb(s4, "qdT", [128, 4, 512], BF16)
                    kTr = sb(s4, "kTr", [128, 4, 512], BF16)
                    kdT = sb(s4, "kdT", [128, 4, 512], BF16)
                    v_sb = [sb(s4, f"v_sb{i}", [128, 4, 128], BF16) for i in range(2)]
                    sgl = [sb(s4, f"sgl{i}", [128, 512], F32) for i in range(2)]
                    kd = [sb(s4, f"kd{i}", [128, 4, 128], BF16) for i in range(2)]
                    attb = [sb(s4, f"attb{i}", [128, 4, 128], BF16) for i in range(2)]
                    state_f = sb(s4, "state_f", [128, 4, 128], F32)
                    state_b = sb(s4, "state_b", [128, 4, 128], BF16)
                    yr = [sb(s4, f"yr{i}", [128, 512], BF16) for i in range(2)]
                    yrdbg = sb(s4, "yrdbg", [128, 512], F32) if "y_ret" in dbg else None
                    kb.op(POOL, lambda: G.memset(state_f.t[:], 0.0), writes=[state_f.b])
                    kb.op(POOL, lambda: G.memset(state_b.t[:], 0.0), writes=[state_b.b])
                    KS = float(128.0 ** -0.5)
                    pcnt = [0]
                    state_done = [False] * (NT + 1)
                    bst = [sb(s4, f"bst{i}", [128, 4, 6], F32) for i in range(2)]
                    mv = [sb(s4, f"mv{i}", [128, 4, 2], F32) for i in range(2)]
                    rs4 = [sb(s4, f"rs4{i}", [128, 12], F32) for i in range(2)]
                    on = [sb(s4, f"on{i}", [128, 512], F32) for i in range(2)]

                    def p1d_gen(c):
                        i2 = c % 2
                        cl = c % 4
                        bA, bB, bT = 2 + 3 * i2, 3 + 3 * i2, 4 + 3 * i2
                        tok = slice(c * 128, (c + 1) * 128)
                        cs = slice(cl * 128, (cl + 1) * 128)
                        bst_, mv_, rs4_, on_ = bst[i2], mv[i2], rs4[i2], on[i2]
                        for k in range(KC):
                            kb.op(PE, lambda k=k: TE.matmul(pf(bA), lhsT=hT.t[:, k, tok], rhs=wr.t[:, k, 1024:1536], start=(k == 0), stop=(k == KC - 1)),
                                  reads=[hT.bs[c], wr.bs[2]], writes=[PS[bA].b], inc=(k == KC - 1))
                        yield
                        for k in range(KC):
                            kb.op(PE, lambda k=k: TE.matmul(pf(bB), lhsT=hT.t[:, k, tok], rhs=wr.t[:, k, 1536:2048], start=(k == 0), stop=(k == KC - 1)),
                                  reads=[hT.bs[c], wr.bs[3]], writes=[PS[bB].b], inc=(k == KC - 1))
                        yield
                        vs, sg_, kd_, ab = v_sb[i2], sgl[i2], kd[i2], attb[i2]
                        kb.op(ACT, lambda: A.copy(out=vs.t[:].rearrange("p a b -> p (a b)"), in_=pf(bA)), reads=[PS[bA].b], writes=[vs.b])
                        yield
                        kb.op(ACT, lambda: A.activation(out=sg_.t[:], in_=pf(bB), func=AF.Silu), reads=[PS[bB].b], writes=[sg_.b])
                        yield
                        transposes([kdT.t[:, h, cs] for h in range(4)], bT, [kdT.b])
                        yield
                        kb.op(ACT, lambda: A.copy(out=kd_.t[:].rearrange("p a b -> p (a b)"), in_=pb(bT)[:, 0:512]), reads=[PS[bT].b], writes=[kd_.b])
                        yield
                        for h in range(4):
                            kb.op(PE, lambda h=h: TE.matmul(pf(bA)[:, h * 128:(h + 1) * 128], lhsT=kTr.t[:, h, cs], rhs=qTr.t[:, h, cs], start=True, stop=True),
                                  reads=[kTr.b, qTr.b], writes=[PS[bA].b], inc=(h == 3))
                        yield
                        kb.op(DVE, lambda: V.tensor_tensor(out=ab.t[:], in0=pf(bA).rearrange("p (a b) -> p a b", b=128), in1=idT.t[:], op=ALU.mult),
                              reads=[PS[bA].b, idT.b], writes=[ab.b])
                        yield
                        while c > 0 and not state_done[c - 1]:
                            yield
                        for h in range(4):
                            kb.op(PE, lambda h=h: TE.matmul(pf(bB)[:, h * 128:(h + 1) * 128], lhsT=ab.t[:, h, :], rhs=vs.t[:, h, :], start=True, stop=(c == 0)),
                                  reads=[ab.b, vs.b], writes=[PS[bB].b], inc=(c == 0 and h == 3))
                            if c > 0:
                                kb.op(PE, lambda h=h: TE.matmul(pf(bB)[:, h * 128:(h + 1) * 128], lhsT=qdT.t[:, h, cs], rhs=state_b.t[:, h, :], start=False, stop=True),
                                      reads=[qdT.b, state_b.b], writes=[PS[bB].b], inc=(h == 3))
                        yield
                        if c < NT - 1:
                            for h in range(4):
                                kb.op(PE, lambda h=h: TE.matmul(pf(bA)[:, h * 128:(h + 1) * 128], lhsT=kd_.t[:, h, :], rhs=vs.t[:, h, :], start=True, stop=True),
                                      reads=[kd_.b, vs.b], writes=[PS[bA].b], inc=(h == 3))
                            yield
                            for h in range(4):
                                kb.op(DVE, lambda h=h: V.scalar_tensor_tensor(out=state_f.t[:, h, :], in0=state_f.t[:, h, :], scalar=cds[h],
                                                                              in1=pf(bA)[:, h * 128:(h + 1) * 128], op0=ALU.mult, op1=ALU.add),
                                      reads=[state_f.b, PS[bA].b], writes=[state_f.b])
                            kb.op(POOL, lambda: G.tensor_copy(out=state_b.t[:], in_=state_f.t[:]), reads=[state_f.b], writes=[state_b.b])
                        state_done[c] = True
                        yield
                        for h in range(4):
                            kb.op(DVE, lambda h=h: V.bn_stats(out=bst_.t[:, h, :], in_=pf(bB)[:, h * 128:(h + 1) * 128]), reads=[PS[bB].b], writes=[bst_.b])
                        yield
                        for h in range(4):
                            kb.op(DVE, lambda h=h: V.bn_aggr(out=mv_.t[:, h, :], in_=bst_.t[:, h, :]), reads=[bst_.b], writes=[mv_.b])
                        yield
                        kb.op(DVE, lambda: V.tensor_scalar(out=rs4_.t[:, 0:4], in0=mv_.t[:, :, 1], scalar1=EPS, scalar2=None, op0=ALU.add), reads=[mv_.b], writes=[rs4_.b])
                        yield
                        kb.op(POOL, lambda: G.tensor_tensor(out=rs4_.t[:, 8:12], in0=rs4_.t[:, 0:4], in1=nhalf.t[:, 0:4], op=ALU.pow),
                              reads=[rs4_.b, nhalf.b], writes=[rs4_.b])
                        yield
                        for h in range(4):
                            kb.op(DVE, lambda h=h: V.tensor_scalar(out=on_.t[:, h * 128:(h + 1) * 128], in0=pf(bB)[:, h * 128:(h + 1) * 128],
                                                                   scalar1=mv_.t[:, h, 0:1], scalar2=rs4_.t[:, 8 + h:9 + h], op0=ALU.subtract, op1=ALU.mult),
                                  reads=[PS[bB].b, mv_.b, rs4_.b], writes=[on_.b])
                        yield
                        kb.op(POOL, lambda: G.tensor_tensor(out=on_.t[:], in0=on_.t[:], in1=gn.t[:], op=ALU.mult), reads=[on_.b, gn.b], writes=[on_.b])
                        yield
                        y_ = yr[i2]
                        kb.op(DVE, lambda: V.tensor_tensor(out=y_.t[:], in0=on_.t[:], in1=sg_.t[:], op=ALU.mult), reads=[on_.b, sg_.b], writes=[y_.b])
                        if yrdbg is not None:
                            kb.op(DVE, lambda: V.tensor_tensor(out=yrdbg.t[:], in0=on_.t[:], in1=sg_.t[:], op=ALU.mult), reads=[on_.b, sg_.b], writes=[yrdbg.b])
                            dbg_store("y_ret", yrdbg.t[:], tok, [yrdbg.b])
                        yield
                        transposes([y_.t[:, k * 128:(k + 1) * 128] for k in range(4)], bT, [y_.b])
                        yield
                        kb.op(ACT, lambda: A.copy(out=yretT.t[:, :, tok], in_=pb(bT)[:, 0:512].rearrange("p (a b) -> p a b", b=128)),
                              reads=[PS[bT].b], writes=[yretT.bs[c]])
                        yield

                    for tg in range(4):
                        tks = slice(tg * 512, (tg + 1) * 512)
                        hbs = [hT.bs[4 * tg + i] for i in range(4)]
                        for qk in range(2):
                            for h in range(4):
                                bk = pcnt[0] % 2
                                pcnt[0] += 1
                                for k in range(KC):
                                    kb.op(PE, lambda k=k, qk=qk, h=h, bk=bk: TE.matmul(pf(bk), lhsT=wr.t[:, k, qk * 512 + h * 128:qk * 512 + (h + 1) * 128],
                                                                                      rhs=hT.t[:, k, tks], start=(k == 0), stop=(k == KC - 1)),
                                          reads=hbs + [wr.bs[qk]], writes=[PS[bk].b], inc=(k == KC - 1))
                                pv4 = pf(bk).rearrange("p (a b) -> p a b", b=128)
                                if qk == 0:
                                    kb.op(ACT, lambda h=h, bk=bk: A.copy(out=qTr.t[:, h, :], in_=pf(bk)), reads=[PS[bk].b], writes=[qTr.b])
                                    kb.op(DVE, lambda h=h, pv4=pv4: V.tensor_tensor(out=qdT.t[:, h, :].rearrange("p (a b) -> p a b", b=128), in0=pv4,
                                                                                    in1=qdec.t[:, h:h + 1, :].broadcast_to([128, 4, 128]), op=ALU.mult),
                                          reads=[PS[bk].b, qdec.b], writes=[qdT.b])
                                else:
                                    kb.op(ACT, lambda h=h, bk=bk: A.mul(out=kTr.t[:, h, :], in_=pf(bk), mul=KS), reads=[PS[bk].b], writes=[kTr.b])
                                    kb.op(DVE, lambda h=h, pv4=pv4: V.scalar_tensor_tensor(out=kdT.t[:, h, :].rearrange("p (a b) -> p a b", b=128), in0=pv4, scalar=KS,
                                                                                           in1=kdec.t[:, h:h + 1, :].broadcast_to([128, 4, 128]),
                                                                                           op0=ALU.mult, op1=ALU.mult),
                                          reads=[PS[bk].b, kdec.b], writes=[kdT.b])
                        run_interleaved([(lambda c=c: p1d_gen(c)) for c in range(4 * tg, 4 * tg + 4)], width=2)

            def phase_1e():
                with ExitStack() as s5:
                    xt = [sb(s5, f"xte{i}", [128, D], F32) for i in range(2)]
                    g2bc = sb(s5, "g2bc", [128, D], F32)
                    kb.dma(SP, g2bc.t[:], norm2_g[0:1, :].broadcast_to([128, D]), writes=[g2bc.b])
                    wm = sb(s5, "wm", [128, KC, 2048], BF16, 4)
                    for j in range(4):
                        kb.dma(POOL, wm.t[:, :, j * 512:(j + 1) * 512], w_in_v[:, :, C_M + j * 512:C_M + (j + 1) * 512], writes=[wm.bs[j]])
                    wbr = sb(s5, "wbr", [128, 8, D], BF16)
                    load_w(wbr, w_branch[0].rearrange("n (k p) d -> p (n k) d", p=128))
                    wo = sb(s5, "wo", [128, KC, D], BF16)
                    load_w(wo, w_out[0].rearrange("(k p) d -> p k d", p=128))
                    gates = [sb(s5, f"gates{i}", [128, D], F32, 2) for i in range(2)]
                    tmix = [sb(s5, f"tmix{i}", [128, D], F32) for i in range(2)]
                    tmix2 = [sb(s5, f"tmix2{i}", [128, D], F32) for i in range(2)]
                    mixed = [sb(s5, f"mixed{i}", [128, D], BF16) for i in range(2)]
                    mixT = [sb(s5, f"mixT{i}", [128, KC, 128], BF16) for i in range(2)]
                    x1t = [sb(s5, f"x1t{i}", [128, D], F32) for i in range(2)]

                    def p1e_gen(c):
                        i2 = c % 2
                        bs_ = [4 * i2 + i for i in range(4)]
                        tok = slice(c * 128, (c + 1) * 128)
                        xx = xt[i2]
                        kb.dma(SP, xx.t[:], x[tok, :], writes=[xx.b])
                        gt_, tm = gates[i2], (tmix[i2], tmix2[i2])
                        for n, yT in ((0, ynsaT), (1, yretT)):
                            for half in range(2):
                                j = 2 * n + half
                                for k in range(KC):
                                    kb.op(PE, lambda j=j, k=k, half=half: TE.matmul(pf(bs_[half]), lhsT=hT.t[:, k, tok], rhs=wm.t[:, k, j * 512:(j + 1) * 512],
                                                                                    start=(k == 0), stop=(k == KC - 1)),
                                          reads=[hT.bs[c], wm.bs[j]], writes=[PS[bs_[half]].b], inc=(k == KC - 1))
                                yield
                            for half in range(2):
                                bk = bs_[2 + half]
                                for k in range(4):
                                    kb.op(PE, lambda n=n, half=half, k=k, bk=bk, yT=yT: TE.matmul(pf(bk), lhsT=yT.t[:, k, tok], rhs=wbr.t[:, n * 4 + k, half * 512:(half + 1) * 512],
                                                                                                  start=(k == 0), stop=(k == 3)),
                                          reads=[yT.bs[c], wbr.b], writes=[PS[bk].b], inc=(k == 3))
                                yield
                            for half in range(2):
                                kb.op(ACT, lambda half=half: A.activation(out=gt_.t[:, half * 512:(half + 1) * 512], in_=pf(bs_[half]), func=AF.Sigmoid),
                                      reads=[PS[bs_[half]].b], writes=[gt_.bs[half]])
                                yield
                            for half in range(2):
                                hs = slice(half * 512, (half + 1) * 512)
                                kb.op(DVE, lambda half=half, hs=hs, n=n: V.tensor_tensor(out=tm[n].t[:, hs], in0=gt_.t[:, hs], in1=pf(bs_[2 + half]), op=ALU.mult),
                                      reads=[gt_.bs[half], PS[bs_[2 + half]].b], writes=[tm[n].b])
                                yield
                        mx = mixed[i2]
                        kb.op(POOL, lambda: G.tensor_tensor(out=mx.t[:], in0=tm[0].t[:], in1=tm[1].t[:], op=ALU.add), reads=[tm[0].b, tm[1].b], writes=[mx.b])
                        yield
                        transposes([mx.t[:, k * 128:(k + 1) * 128] for k in range(KC)], bs_[0], [mx.b])
                        yield
                        mt = mixT[i2]
                        kb.op(ACT, lambda: A.copy(out=mt.t[:], in_=pb(bs_[0]).rearrange("p (a b) -> p a b", b=128)), reads=[PS[bs_[0]].b], writes=[mt.b])
                        yield
                        for half in range(2):
                            for k in range(KC):
                                kb.op(PE, lambda half=half, k=k: TE.matmul(pf(bs_[2 + half]), lhsT=mt.t[:, k, :], rhs=wo.t[:, k, half * 512:(half + 1) * 512],
                                                                           start=(k == 0), stop=(k == KC - 1)),
                                      reads=[mt.b, wo.b], writes=[PS[bs_[2 + half]].b], inc=(k == KC - 1))
                            yield
                        x1 = x1t[i2]
                        for half in range(2):
                            hs = slice(half * 512, (half + 1) * 512)
                            kb.op(DVE, lambda half=half, hs=hs: V.tensor_tensor(out=x1.t[:, hs], in0=xx.t[:, hs], in1=pf(bs_[2 + half]), op=ALU.add),
                                  reads=[xx.b, PS[bs_[2 + half]].b], writes=[x1.b])
                            yield
                        kb.dma(SP, xmid[tok, :], x1.t[:], reads=[x1.b])
                        dbg_store("x1", x1.t[:], tok, [x1.b])
                        yield from norm_gen(x1, c, g2bc, i2, bs_[1])

                    run_interleaved([(lambda c=c: p1e_gen(c)) for c in range(NT)], width=2)

            def phase_2():
                with ExitStack() as s6:
                    xt = [sb(s6, f"xtf{i}", [128, D], F32) for i in range(2)]
                    wd = sb(s6, "wd", [128, NFB, D], BF16, 2)
                    wd_v = ffn_w_down[0].rearrange("(fb p) d -> p fb d", p=128)
                    kb.dma(POOL, wd.t[:, 0:11, :], wd_v[:, 0:11, :], writes=[wd.bs[0]])
                    kb.dma(POOL, wd.t[:, 11:22, :], wd_v[:, 11:22, :], writes=[wd.bs[1]])
                    wgs = [sb(s6, f"wgs{i}", [128, KC, 256], BF16) for i in range(2)]
                    wus = [sb(s6, f"wus{i}", [128, KC, 256], BF16) for i in range(2)]
                    act = sb(s6, "act", [128, NFB, 1024], BF16, NFB)
                    sgs = [sb(s6, f"sgs{i}", [128, 512], F32) for i in range(2)]
                    outt = [sb(s6, f"outt{i}", [128, D], F32) for i in range(2)]
                    wg_v = ffn_w_gate[0].rearrange("(k p) f -> p k f", p=128)
                    wu_v = ffn_w_up[0].rearrange("(k p) f -> p k f", p=128)
                    cn = [0, 0]
                    for hf in range(2):
                        for fg in range(11):
                            cols = slice(fg * 256, (fg + 1) * 256)
                            wg_, wu_ = wgs[fg % 2], wus[fg % 2]
                            kb.dma(POOL, wg_.t[:], wg_v[:, :, cols], writes=[wg_.b])
                            kb.dma(POOL, wu_.t[:], wu_v[:, :, cols], writes=[wu_.b])
                            for fl in range(2):
                                fb = fg * 2 + fl
                                for t2 in range(2):
                                    tokc = slice(hf * 1024 + t2 * 512, hf * 1024 + (t2 + 1) * 512)
                                    hbs = [hT.bs[hf * 8 + t2 * 4 + i] for i in range(4)]
                                    gb, ub = (0, 1) if cn[0] % 2 == 0 else (2, 3)
                                    cn[0] += 1
                                    for k in range(KC):
                                        kb.op(PE, lambda k=k, fl=fl, gb=gb, wg_=wg_, tokc=tokc: TE.matmul(pf(gb), lhsT=wg_.t[:, k, fl * 128:(fl + 1) * 128], rhs=hT.t[:, k, tokc],
                                                                                                          start=(k == 0), stop=(k == KC - 1)),
                                              reads=hbs + [wg_.b], writes=[PS[gb].b], inc=(k == KC - 1))
                                    for k in range(KC):
                                        kb.op(PE, lambda k=k, fl=fl, ub=ub, wu_=wu_, tokc=tokc: TE.matmul(pf(ub), lhsT=wu_.t[:, k, fl * 128:(fl + 1) * 128], rhs=hT.t[:, k, tokc],
                                                                                                          start=(k == 0), stop=(k == KC - 1)),
                                              reads=hbs + [wu_.b], writes=[PS[ub].b], inc=(k == KC - 1))
                                    sg_ = sgs[cn[0] % 2]
                                    kb.op(ACT, lambda sg_=sg_, gb=gb: A.activation(out=sg_.t[:], in_=pf(gb), func=AF.Silu), reads=[PS[gb].b], writes=[sg_.b])
                                    kb.op(DVE, lambda sg_=sg_, ub=ub, fb=fb, t2=t2: V.tensor_tensor(out=act.t[:, fb, t2 * 512:(t2 + 1) * 512], in0=sg_.t[:], in1=pf(ub), op=ALU.mult),
                                          reads=[sg_.b, PS[ub].b], writes=[act.bs[fb]])
                        for tl in range(8):
                            c = hf * 8 + tl
                            tok = slice(c * 128, (c + 1) * 128)
                            xx = xt[c % 2]
                            kb.dma(SP, xx.t[:], xmid[tok, :], writes=[xx.b])
                            ob = (4, 5) if cn[1] % 2 == 0 else (6, 7)
                            cn[1] += 1
                            for half in range(2):
                                for fb in range(NFB):
                                    kb.op(PE, lambda half=half, fb=fb, ob=ob, tl=tl: TE.matmul(pf(ob[half]), lhsT=act.t[:, fb, tl * 128:(tl + 1) * 128],
                                                                                               rhs=wd.t[:, fb, half * 512:(half + 1) * 512],
                                                                                               start=(fb == 0), stop=(fb == NFB - 1)),
                                          reads=[act.bs[fb], wd.bs[0 if fb < 11 else 1]], writes=[PS[ob[half]].b], inc=(fb == NFB - 1))
                            ot = outt[c % 2]
                            for half in range(2):
                                hs = slice(half * 512, (half + 1) * 512)
                                kb.op(DVE, lambda half=half, hs=hs, ob=ob, ot=ot, xx=xx: V.tensor_tensor(out=ot.t[:, hs], in0=xx.t[:, hs], in1=pf(ob[half]), op=ALU.add),
                                      reads=[xx.b, PS[ob[half]].b], writes=[ot.b])
                            kb.dma(SP, out[tok, :], ot.t[:], reads=[ot.b], is_out=True)

            with ExitStack() as sB:
                ynsaT = sb(sB, "ynsaT", [128, 4, S], BF16, NT)
                yretT = sb(sB, "yretT", [128, 4, S], BF16, NT)

                with ExitStack() as sA:
                    gq = sb(sA, "gq", [128, 64], F32)
                    gk = sb(sA, "gk", [128, 3, 64], F32)
                    QAL = sb(sA, "QAL", [128, NT, 8, 4], BF16)
                    KAL = sb(sA, "KAL", [128, NT, 4], BF16)
                    KCAL = sb(sA, "KCAL", [128, 4], BF16)
                    OH = sb(sA, "OH", [128, NT, 32], BF16)
                    dmask8 = sb(sA, "dmask8", [128, 8, 128], BF16)
                    tmask8 = sb(sA, "tmask8", [128, 8, 128], BF16)
                    cmask = sb(sA, "cmask", [128, S], BF16)
                    addc = sb(sA, "addc", [128, NT, 32], F32)
                    ov = sb(sA, "ov", [128, 32], BF16)
                    KT_slc = sb(sA, "KT_slc", [128, 2, S], BF16, NT)
                    KT_win = sb(sA, "KT_win", [128, 2, S], BF16, NT)
                    V_slc = sb(sA, "V_slc", [128, NT, 2, 65], BF16, NT)
                    V_win = sb(sA, "V_win", [128, NT, 2, 65], BF16, NT)
                    KcT = sb(sA, "KcT", [128, 2, 128], BF16)
                    Vc = sb(sA, "Vc", [128, 2, 97], BF16)

                    with ExitStack() as s0:
                        SL = sb(s0, "SL", [128, 8], F32)
                        th128 = sb(s0, "th128", [128, NT], F32)
                        pidx = sb(s0, "pidx", [128, 1], F32)
                        QALf = sb(s0, "QALf", [128, NT, 8, 4], F32)
                        KALf = sb(s0, "KALf", [128, NT, 4], F32)
                        KCALf = sb(s0, "KCALf", [128, 4], F32)
                        rel = sb(s0, "rel", [128, NT, 32], F32)
                        f0 = sb(s0, "f0", [128, NT, 32], F32)
                        f1 = sb(s0, "f1", [128, NT, 32], F32)
                        t1 = sb(s0, "t1", [128, NT, 32], F32)
                        hp = sb(s0, "hp", [128, 1], F32)
                        ovf = sb(s0, "ovf", [128, 32], F32)
                        ova = sb(s0, "ova", [128, 32], F32)
                        ones_b = sb(s0, "ones_b", [128, 512], BF16)
                        ones_b2 = sb(s0, "ones_b2", [128, 1024], BF16)
                        zeros_b = sb(s0, "zeros_b", [128, 512], BF16)

                        kb.dma(SP, gq.t[:], nsa_q_norm[0:1, :].broadcast_to([128, 64]), writes=[gq.b])
                        kb.dma(SP, gk.t[:].rearrange("p a b -> p (a b)"),
                               nsa_k_norm.rearrange("o a b -> o (a b)").broadcast_to([128, 192]), writes=[gk.b])
                        kb.op(DVE, lambda: V.tensor_scalar(out=gq.t[:], in0=gq.t[:], scalar1=0.125, scalar2=None, op0=ALU.mult),
                              reads=[gq.b], writes=[gq.b])
                        for h in range(8):
                            kb.op(POOL, lambda h=h: G.memset(SL.t[:, h:h + 1], 2.0 ** (-(h + 1))), writes=[SL.b])
                        kb.op(POOL, lambda: G.iota(th128.t[:], pattern=[[128, NT]], base=0, channel_multiplier=0,
                                                   allow_small_or_imprecise_dtypes=True), writes=[th128.b])
                        kb.op(POOL, lambda: G.iota(pidx.t[:], pattern=[[0, 1]], base=0, channel_multiplier=1,
                                                   allow_small_or_imprecise_dtypes=True), writes=[pidx.b])
                        SLb = SL.t[:].unsqueeze(1).broadcast_to([128, NT, 8])
                        THb = th128.t[:].unsqueeze(2).broadcast_to([128, NT, 8])
                        kb.op(DVE, lambda: V.scalar_tensor_tensor(out=QALf.t[:, :, :, 0], in0=THb, scalar=-1.0, in1=SLb,
                                                                  op0=ALU.mult, op1=ALU.mult),
                              reads=[SL.b, th128.b], writes=[QALf.b])
                        kb.op(DVE, lambda: V.tensor_scalar(out=QALf.t[:, :, :, 1], in0=SLb, scalar1=pidx.t[:, 0:1], scalar2=-1.0,
                                                           op0=ALU.mult, op1=ALU.mult),
                              reads=[SL.b, pidx.b], writes=[QALf.b])
                        kb.op(DVE, lambda: V.tensor_copy(out=QALf.t[:, :, :, 2], in_=SLb), reads=[SL.b], writes=[QALf.b])
                        kb.op(DVE, lambda: V.tensor_copy(out=QALf.t[:, :, :, 3], in_=SLb), reads=[SL.b], writes=[QALf.b])
                        kb.op(DVE, lambda: V.tensor_copy(out=QAL.t[:], in_=QALf.t[:]), reads=[QALf.b], writes=[QAL.b])
                        kb.op(POOL, lambda: G.memset(KALf.t[:, :, 0:2], 1.0), writes=[KALf.b])
                        kb.op(DVE, lambda: V.tensor_copy(out=KALf.t[:, :, 2], in_=th128.t[:]), reads=[th128.b], writes=[KALf.b])
                        kb.op(DVE, lambda: V.tensor_copy(out=KALf.t[:, :, 3], in_=pidx.t[:, 0:1].broadcast_to([128, NT])),
                              reads=[pidx.b], writes=[KALf.b])
                        kb.op(DVE, lambda: V.tensor_copy(out=KAL.t[:], in_=KALf.t[:]), reads=[KALf.b], writes=[KAL.b])
                        kb.op(POOL, lambda: G.memset(KCALf.t[:, 0:2], 1.0), writes=[KCALf.b])
                        kb.op(POOL, lambda: G.memset(KCALf.t[:, 3:4], 31.0), reads=[], writes=[KCALf.b])
                        kb.op(DVE, lambda: V.tensor_scalar(out=KCALf.t[:, 2:3], in0=pidx.t[:, 0:1], scalar1=16.0, scalar2=None,
                                                           op0=ALU.mult), reads=[pidx.b], writes=[KCALf.b])
                        kb.op(DVE, lambda: V.tensor_copy(out=KCAL.t[:], in_=KCALf.t[:]), reads=[KCALf.b], writes=[KCAL.b])
                        kb.op(POOL, lambda: G.memset(OH.t[:], 0.0), writes=[OH.b])
                        for kt in range(NT):
                            kb.op(POOL, lambda kt=kt: G.memset(OH.t[0:64, kt, 2 * kt:2 * kt + 1], 1.0), writes=[OH.b])
                            kb.op(POOL, lambda kt=kt: G.memset(OH.t[64:128, kt, 2 * kt + 1:2 * kt + 2], 1.0), writes=[OH.b])
                        kb.op(POOL, lambda: G.memset(ones_b.t[:], 1.0), writes=[ones_b.b])
                        kb.op(POOL, lambda: G.memset(zeros_b.t[:], 0.0), writes=[zeros_b.b])
                        ob8 = ones_b2.t[:].rearrange("p (a b) -> p a b", b=128)
                        kb.op(POOL, lambda: G.memset(ones_b2.t[:], 1.0), writes=[ones_b2.b])
                        kb.op(POOL, lambda: G.affine_select(out=dmask8.t[:], in_=ob8, pattern=[[0, 8], [1, 128]],
                                                            compare_op=ALU.is_ge, fill=0.0, base=0, channel_multiplier=-1),
                              reads=[ones_b2.b], writes=[dmask8.b])
                        kb.op(POOL, lambda: G.affine_select(out=tmask8.t[:], in_=ob8, pattern=[[0, 8], [-1, 128]],
                                                            compare_op=ALU.is_gt, fill=0.0, base=0, channel_multiplier=1),
                              reads=[ones_b2.b], writes=[tmask8.b])
                        for i in range(4):
                            kb.op(POOL, lambda i=i: G.affine_select(out=cmask.t[:, i * 512:(i + 1) * 512], in_=zeros_b.t[:],
                                                                    pattern=[[1, 512]], compare_op=ALU.is_ge, fill=NEG,
                                                                    base=-31 + 512 * i, channel_multiplier=-16),
                                  reads=[zeros_b.b], writes=[cmask.b])
                        kb.op(POOL, lambda: G.iota(rel.t[:], pattern=[[-2, NT], [1, 32]], base=0, channel_multiplier=0,
                                                   allow_small_or_imprecise_dtypes=True), writes=[rel.b])
                        kb.op(DVE, lambda: V.tensor_scalar(out=hp.t[:], in0=pidx.t[:], scalar1=64.0, scalar2=None, op0=ALU.is_ge),
                              reads=[pidx.b], writes=[hp.b])
                        kb.op(DVE, lambda: V.tensor_scalar(out=rel.t[:], in0=rel.t[:], scalar1=hp.t[:, 0:1], scalar2=None,
                                                           op0=ALU.subtract), reads=[rel.b, hp.b], writes=[rel.b])
                        kb.op(DVE, lambda: V.tensor_scalar(out=t1.t[:], in0=rel.t[:], scalar1=0.0, scalar2=-1e9,
                                                           op0=ALU.is_gt, op1=ALU.mult), reads=[rel.b], writes=[t1.b])
                        kb.op(DVE, lambda: V.tensor_scalar(out=f0.t[:], in0=rel.t[:], scalar1=0.0, scalar2=None, op0=ALU.is_equal),
                              reads=[rel.b], writes=[f0.b])
                        kb.op(DVE, lambda: V.tensor_scalar(out=f1.t[:], in0=rel.t[:], scalar1=-1.0, scalar2=None, op0=ALU.is_equal),
                              reads=[rel.b], writes=[f1.b])
                        kb.op(DVE, lambda: V.tensor_tensor(out=f0.t[:], in0=f0.t[:], in1=f1.t[:], op=ALU.max),
                              reads=[f0.b, f1.b], writes=[f0.b])
                        kb.op(DVE, lambda: V.memset(f0.t[:, :, 0:1], 1.0), reads=[], writes=[f0.b])
                        kb.op(DVE, lambda: V.scalar_tensor_tensor(out=addc.t[:], in0=f0.t[:], scalar=1e4, in1=t1.t[:],
                                                                  op0=ALU.mult, op1=ALU.add), reads=[f0.b, t1.b], writes=[addc.b])
                        kb.op(POOL, lambda: G.iota(ovf.t[:], pattern=[[-64, 32]], base=0, channel_multiplier=16,
                                                   allow_small_or_imprecise_dtypes=True), writes=[ovf.b])
                        kb.op(DVE, lambda: V.tensor_scalar(out=ova.t[:], in0=ovf.t[:], scalar1=63.0, scalar2=None, op0=ALU.is_le),
                              reads=[ovf.b], writes=[ova.b])
                        kb.op(DVE, lambda: V.tensor_scalar(out=ovf.t[:], in0=ovf.t[:], scalar1=-31.0, scalar2=None, op0=ALU.is_ge),
                              reads=[ovf.b], writes=[ovf.b])
                        kb.op(DVE, lambda: V.tensor_tensor(out=ov.t[:], in0=ova.t[:], in1=ovf.t[:], op=ALU.mult),
                              reads=[ova.b, ovf.b], writes=[ov.b])
                        kb.op(POOL, lambda: G.memset(V_slc.t[:, :, :, 64:65], 1.0), writes=V_slc.bs)
                        kb.op(POOL, lambda: G.memset(V_win.t[:, :, :, 64:65], 1.0), writes=V_win.bs)
                        kb.barrier()
                        chk(1)

                    with ExitStack() as s2:
                        cmpT = sb(s2, "cmpT", [128, 2, S], BF16, NT)
                        with ExitStack() as s2a:
                            xt = [sb(s2a, f"xta{i}", [128, D], F32) for i in range(2)]
                            g1bc = sb(s2a, "g1bc", [128, D], F32)
                            kb.dma(SP, g1bc.t[:], norm1_g[0:1, :].broadcast_to([128, D]), writes=[g1bc.b])

                            def p1a_gen(c):
                                xx = xt[c % 2]
                                kb.dma(SP, xx.t[:], x[c * 128:(c + 1) * 128, :], writes=[xx.b])
                                yield
                                yield from norm_gen(xx, c, g1bc, c % 2, 6 + (c % 2))

                            wkv = sb(s2a, "wkv", [128, KC, 768], BF16)
                            load_w(wkv, w_in_v[:, :, C_KV:C_KV + 768])
                            cmp_tok = [sb(s2a, f"cmp_tok{i}", [128, 256], BF16) for i in range(2)]
                            sqk = [sb(s2a, f"sqk{i}", [128, 256], F32) for i in range(2)]
                            kst = sb(s2a, "kst", [128, NT, 16], F32, NT)
                            tmpk = [sb(s2a, f"tmpk{i}", [128, 4, 64], F32) for i in range(2)]
                            ka_slc = [sb(s2a, f"ka_slc{i}", [128, 2, 128], BF16) for i in range(2)]
                            ka_win = [sb(s2a, f"ka_win{i}", [128, 2, 128], BF16) for i in range(2)]
                            for i in range(2):
                                kb.op(POOL, lambda i=i: G.memset(ka_slc[i].t[:], 0.0), writes=[ka_slc[i].b])
                                kb.op(POOL, lambda i=i: G.memset(ka_win[i].t[:], 0.0), writes=[ka_win[i].b])
                            def p1b_gen(c):
                                i2 = c % 2
                                bA, bB = (0, 1) if i2 == 0 else (2, 3)
                                tok = slice(c * 128, (c + 1) * 128)
                                if CUT >= 1:
                                    yield
                                    for k in range(KC):
                                        kb.op(PE, lambda k=k: TE.matmul(pf(bA), lhsT=hT.t[:, k, tok], rhs=wkv.t[:, k, 0:512],
                                                                        start=(k == 0), stop=(k == KC - 1)),
                                              reads=[hT.bs[c], wkv.b], writes=[PS[bA].b], inc=(k == KC - 1))
                                    for k in range(KC):
                                        kb.op(PE, lambda k=k: TE.matmul(pf(bB)[:, 0:256], lhsT=hT.t[:, k, tok], rhs=wkv.t[:, k, 512:768],
                                                                        start=(k == 0), stop=(k == KC - 1)),
                                              reads=[hT.bs[c], wkv.b], writes=[PS[bB].b], inc=(k == KC - 1))
                                if CUT >= 2:
                                    yield
                                    ct = cmp_tok[i2]
                                    kb.op(ACT, lambda: A.copy(out=ct.t[:], in_=pf(bA)[:, 0:256]), reads=[PS[bA].b], writes=[ct.b])
                                    sq = sqk[i2]
                                    kb.op(ACT, lambda: A.activation(out=sq.t[:, 0:128], in_=pf(bA)[:, 256:384], func=AF.Square),
                                          reads=[PS[bA].b], writes=[sq.b])
                                    kb.op(ACT, lambda: A.activation(out=sq.t[:, 128:256], in_=pf(bB)[:, 0:128], func=AF.Square),
                                          reads=[PS[bB].b], writes=[sq.b])
                                if CUT >= 3:
                                    yield
                                    ks = kst.t
                                    ksb = [kst.bs[c]]
                                    kb.op(DVE, lambda: V.tensor_reduce(out=ks[:, c, 0:4], in_=sq.t[:].rearrange("p (a b) -> p a b", b=64),
                                                                       axis=AX.X, op=ALU.add), reads=[sq.b], writes=ksb)
                                    rstd_from_ss(ks[:, c, 0:4], ks[:, c, 4:8], ks[:, c, 8:12], ks[:, c, 12:16], 64, ksb)
                                    tk = tmpk[i2]
                                    kb.op(DVE, lambda: V.tensor_tensor(out=tk.t[:, 0:2, :], in0=pf(bA)[:, 256:384].rearrange("p (a b) -> p a b", b=64),
                                                                       in1=ks[:, c, 12:14].unsqueeze(2).broadcast_to([128, 2, 64]), op=ALU.mult),
                                          reads=[PS[bA].b] + ksb, writes=[tk.b])
                                    kb.op(DVE, lambda: V.tensor_tensor(out=tk.t[:, 2:4, :], in0=pf(bB)[:, 0:128].rearrange("p (a b) -> p a b", b=64),
                                                                       in1=ks[:, c, 14:16].unsqueeze(2).broadcast_to([128, 2, 64]), op=ALU.mult),
                                          reads=[PS[bB].b] + ksb, writes=[tk.b])
                                    ksl, kwn = ka_slc[i2], ka_win[i2]
                                    kb.op(DVE, lambda: V.tensor_tensor(out=ksl.t[:, :, 0:64], in0=tk.t[:, 0:2, :],
                                                                       in1=gk.t[:, 1:2, :].broadcast_to([128, 2, 64]), op=ALU.mult),
                                          reads=[tk.b, gk.b], writes=[ksl.b])
                                    kb.op(DVE, lambda: V.tensor_tensor(out=kwn.t[:, :, 0:64], in0=tk.t[:, 2:4, :],
                                                                       in1=gk.t[:, 2:3, :].broadcast_to([128, 2, 64]), op=ALU.mult),
                                          reads=[tk.b, gk.b], writes=[kwn.b])
                                if CUT >= 4:
                                    yield
                                    kb.op(POOL, lambda: G.tensor_copy(out=ksl.t[:, :, 64:96], in_=OH.t[:, c:c + 1, :].broadcast_to([128, 2, 32])),
                                          reads=[OH.b], writes=[ksl.b])
                                    kb.op(POOL, lambda: G.tensor_copy(out=ksl.t[:, :, 96:100], in_=KAL.t[:, c:c + 1, :].broadcast_to([128, 2, 4])),
                                          reads=[KAL.b], writes=[ksl.b])
                                    kb.op(POOL, lambda: G.tensor_copy(out=kwn.t[:, :, 96:100], in_=KAL.t[:, c:c + 1, :].broadcast_to([128, 2, 4])),
                                          reads=[KAL.b], writes=[kwn.b])
                                if CUT >= 5:
                                    yield
                                    kb.op(ACT, lambda: A.copy(out=V_slc.t[:, c, :, 0:64], in_=pf(bA)[:, 384:512].rearrange("p (a b) -> p a b", b=64)),
                                          reads=[PS[bA].b], writes=[V_slc.bs[c]])
                                    kb.op(ACT, lambda: A.copy(out=V_win.t[:, c, :, 0:64], in_=pf(bB)[:, 128:256].rearrange("p (a b) -> p a b", b=64)),
                                          reads=[PS[bB].b], writes=[V_win.bs[c]])
                                if CUT >= 6:
                                    yield
                                    tb = 4 + i2
                                    transposes([ksl.t[:, 0, :], ksl.t[:, 1, :], kwn.t[:, 0, :], kwn.t[:, 1, :], ct.t[:, 0:128], ct.t[:, 128:256]],
                                               tb, [ksl.b, kwn.b, ct.b])
                                    pv3 = pb(tb).rearrange("p (a b) -> p a b", b=128)
                                    kb.op(ACT, lambda: A.copy(out=KT_slc.t[:, :, tok], in_=pv3[:, 0:2, :]), reads=[PS[tb].b], writes=[KT_slc.bs[c]])
                                    kb.op(ACT, lambda: A.copy(out=KT_win.t[:, :, tok], in_=pv3[:, 2:4, :]), reads=[PS[tb].b], writes=[KT_win.bs[c]])
                                    kb.op(ACT, lambda: A.copy(out=cmpT.t[:, :, tok], in_=pv3[:, 4:6, :]), reads=[PS[tb].b], writes=[cmpT.bs[c]])
                            def p1ab_gen(c):
                                yield from p1a_gen(c)
                                yield from p1b_gen(c)

                            run_interleaved([(lambda c=c: p1ab_gen(c)) for c in range(NT)], width=2)
                            if "KT_slc" in dbg:
                                kdb = sb(s2a, "kdb", [128, 2, S], F32)
                                kb.op(DVE, lambda: V.tensor_copy(out=kdb.t[:], in_=KT_slc.t[:]), reads=KT_slc.bs, writes=[kdb.b])
                                kb.dma(SP, dbg["KT_slc"].rearrange("p (a b) -> p a b", b=S), kdb.t[:], reads=[kdb.b], is_out=True)
                            kb.barrier()
                            chk(3)

                        with ExitStack() as s2b:
                            w1sb = sb(s2b, "w1sb", [128, 2, 32, 128], BF16)
                            w2sb = sb(s2b, "w2sb", [128, 2, 64], BF16)
                            pe_sb = sb(s2b, "pe_sb", [32, 2, 64], F32)
                            peT = sb(s2b, "peT", [64, 2, 32], BF16)
                            bias_c = sb(s2b, "bias_c", [128, 2], F32)
                            xhs = [sb(s2b, f"xh{i}", [128, 128], F32) for i in range(4)]
                            x2s = [sb(s2b, f"x2{i}", [128, 128], F32) for i in range(4)]
                            sgs_ = [sb(s2b, f"sgc{i}", [128, 128], F32) for i in range(4)]
                            HTbs = [sb(s2b, f"HTb{i}", [128, 128], BF16) for i in range(4)]
                            kca = sb(s2b, "kca", [128, 2, 128], BF16)
                            cst = sb(s2b, "cst", [128, 8], F32)
                            tmpcs = [sb(s2b, f"tmpc{i}", [128, 64], F32) for i in range(4)]
                            for kv in range(2):
                                src = cmp_w1[0, kv].rearrange("l d f -> d l f")
                                kb.dma(POOL, w1sb.t[0:64, kv], src, writes=[w1sb.b])
                                kb.dma(POOL, w1sb.t[64:128, kv], src, writes=[w1sb.b])
                            kb.dma(POOL, w2sb.t[:], cmp_w2[0].rearrange("k f d -> f k d"), writes=[w2sb.b])
                            kb.dma(SP, pe_sb.t[:], cmp_pe[0].rearrange("k l d -> l k d"), writes=[pe_sb.b])
                            kb.op(POOL, lambda: G.memset(kca.t[:], 0.0), writes=[kca.b])
                            kb.op(POOL, lambda: G.memset(Vc.t[:], 0.0), writes=[Vc.b])
                            kb.op(POOL, lambda: G.tensor_copy(out=kca.t[:, :, 96:100], in_=KCAL.t[:].unsqueeze(1).broadcast_to([128, 2, 4])),
                                  reads=[KCAL.b], writes=[kca.b])
                            kb.op(POOL, lambda: G.memset(Vc.t[:, :, 64:65], 1.0), writes=[Vc.b])
                            kb.op(POOL, lambda: G.tensor_copy(out=Vc.t[:, :, 65:97], in_=ov.t[:].unsqueeze(1).broadcast_to([128, 2, 32])),
                                  reads=[ov.b], writes=[Vc.b])
                            for kv in range(2):
                                kb.op(PE, lambda kv=kv: TE.transpose(out=pf(0)[0:64, kv * 32:(kv + 1) * 32], in_=pe_sb.t[0:32, kv, :],
                                                                     identity=ident_f.t[0:32, 0:32]),
                                      reads=[pe_sb.b, ident_f.b], writes=[PS[0].b])
                            kb.op(DVE, lambda: V.tensor_copy(out=peT.t[:], in_=pf(0)[0:64, 0:64].rearrange("p (a b) -> p a b", b=32)),
                                  reads=[PS[0].b], writes=[peT.b])
                            for kv in range(2):
                                for l in range(32):
                                    kb.op(PE, lambda kv=kv, l=l: TE.matmul(pf(1)[:, kv:kv + 1], lhsT=w1sb.t[0:64, kv, l, :], rhs=peT.t[0:64, kv, l:l + 1],
                                                                           start=(l == 0), stop=(l == 31)),
                                          reads=[w1sb.b, peT.b], writes=[PS[1].b], inc=(l == 31))
                            kb.op(DVE, lambda: V.tensor_copy(out=bias_c.t[:], in_=pf(1)[:, 0:2]), reads=[PS[1].b], writes=[bias_c.b])
                            def cmp_gen(kv, g, idx):
                                bH, bO = 2 * idx, 2 * idx + 1
                                xh, x2, sg, HTb, tmpc = xhs[idx], x2s[idx], sgs_[idx], HTbs[idx], tmpcs[idx]
                                for l in range(32):
                                    kb.op(PE, lambda kv=kv, g=g, l=l: TE.matmul(
                                        pf(bH)[:, 0:127], lhsT=w1sb.t[g * 64:(g + 1) * 64, kv, l, :],
                                        rhs=cmpT.t[g * 64:(g + 1) * 64, kv, l:l + 16 * 126 + 1:16],
                                        start=(l == 0), stop=(l == 31)),
                                        reads=[w1sb.b] + cmpT.bs, writes=[PS[bH].b], inc=(l == 31))
                                yield
                                kb.op(ACT, lambda kv=kv: A.activation(out=xh.t[:, 0:127], in_=pf(bH)[:, 0:127], func=AF.Identity,
                                                                      bias=bias_c.t[:, kv:kv + 1], scale=1.0),
                                      reads=[PS[bH].b, bias_c.b], writes=[xh.b])
                                yield
                                kb.op(DVE, lambda: V.tensor_tensor(out=x2.t[:, 0:127], in0=xh.t[:, 0:127], in1=xh.t[:, 0:127], op=ALU.mult),
                                      reads=[xh.b], writes=[x2.b])
                                yield
                                kb.op(DVE, lambda: V.tensor_scalar(out=x2.t[:, 0:127], in0=x2.t[:, 0:127], scalar1=0.044715, scalar2=1.0,
                                                                   op0=ALU.mult, op1=ALU.add), reads=[x2.b], writes=[x2.b])
                                yield
                                kb.op(DVE, lambda: V.tensor_tensor(out=x2.t[:, 0:127], in0=x2.t[:, 0:127], in1=xh.t[:, 0:127], op=ALU.mult),
                                      reads=[x2.b, xh.b], writes=[x2.b])
                                yield
                                kb.op(ACT, lambda: A.activation(out=sg.t[:, 0:127], in_=x2.t[:, 0:127], func=AF.Sigmoid, scale=1.5957691216057308),
                                      reads=[x2.b], writes=[sg.b])
                                yield
                                kb.op(DVE, lambda: V.tensor_tensor(out=HTb.t[:, 0:127], in0=xh.t[:, 0:127], in1=sg.t[:, 0:127], op=ALU.mult),
                                      reads=[xh.b, sg.b], writes=[HTb.b])
                                yield
                                kb.op(PE, lambda kv=kv: TE.matmul(pf(bO)[0:127, 0:64], lhsT=HTb.t[:, 0:127], rhs=w2sb.t[:, kv, :], start=True, stop=True),
                                      reads=[HTb.b, w2sb.b], writes=[PS[bO].b])
                                yield
                                if kv == 0:
                                    kb.op(ACT, lambda g=g: A.activation(out=tmpc.t[0:127, :], in_=pf(bO)[0:127, 0:64], func=AF.Square,
                                                                        accum_out=cst.t[0:127, g:g + 1]),
                                          reads=[PS[bO].b], writes=[tmpc.b, cst.b])
                                    rstd_from_ss(cst.t[0:127, g:g + 1], cst.t[0:127, 2 + g:3 + g], cst.t[0:127, 4 + g:5 + g], cst.t[0:127, 6 + g:7 + g], 64, [cst.b])
                                    kb.op(DVE, lambda g=g: V.scalar_tensor_tensor(out=kca.t[0:127, g, 0:64], in0=pf(bO)[0:127, 0:64],
                                                                                  scalar=cst.t[0:127, 6 + g:7 + g], in1=gk.t[0:127, 0, :],
                                                                                  op0=ALU.mult, op1=ALU.mult),
                                          reads=[PS[bO].b, cst.b, gk.b], writes=[kca.b])
                                else:
                                    kb.op(ACT, lambda g=g: A.copy(out=Vc.t[0:127, g, 0:64], in_=pf(bO)[0:127, 0:64]),
                                          reads=[PS[bO].b], writes=[Vc.b])
                                yield

                            run_interleaved([(lambda kv=kv, g=g: cmp_gen(kv, g, 2 * kv + g)) for kv in range(2) for g in range(2)], width=4)
                            transposes([kca.t[:, 0, :], kca.t[:, 1, :]], 6, [kca.b])
                            kb.op(ACT, lambda: A.copy(out=KcT.t[:], in_=pb(6)[:, 0:256].rearrange("p (a b) -> p a b", b=128)),
                                  reads=[PS[6].b], writes=[KcT.b])
                            if "kc" in dbg:
                                kcd = sb(s2b, "kcd", [128, 2, 64], F32)
                                kb.op(DVE, lambda: V.tensor_copy(out=kcd.t[:], in_=kca.t[:, :, 0:64]), reads=[kca.b], writes=[kcd.b])
                                kb.dma(SP, dbg["kc"].rearrange("p (a b) -> p a b", b=64), kcd.t[:], reads=[kcd.b], is_out=True)
                            if "vc" in dbg:
                                vcd = sb(s2b, "vcd", [128, 2, 64], F32)
                                kb.op(DVE, lambda: V.tensor_copy(out=vcd.t[:], in_=Vc.t[:, :, 0:64]), reads=[Vc.b], writes=[vcd.b])
                                kb.dma(SP, dbg["vc"].rearrange("p (a b) -> p a b", b=64), vcd.t[:], reads=[vcd.b], is_out=True)
                            kb.barrier()
                            chk(4)

                    phase_1c()
                    kb.barrier()
                    chk(5)

                phase_1d()
                kb.barrier()
                chk(6)
                phase_1e()
                kb.barrier()
                chk(7)

            phase_2()
            kb.finish()
    except _Stop:
        pass
    return nc


_NAMES = ["x", "norm1_g", "w_in", "nsa_q_norm", "nsa_k_norm", "cmp_pe", "cmp_w1", "cmp_w2", "ret_gn_g",
          "w_branch", "w_out", "norm2_g", "ffn_w_gate", "ffn_w_up", "ffn_w_down"]


def kernel(**inputs):
    n = 8
    arrs = {k: np.ascontiguousarray(np.asarray(inputs[k], dtype=np.float32)) for k in _NAMES}
    nc = build_nc()
    in_maps = []
    for i in range(n):
        m = {k: arrs[k] for k in _NAMES if k != "x"}
        m["x"] = np.ascontiguousarray(arrs["x"][i])
        in_maps.append(m)
    res = run_bass_kernel_spmd(nc, in_maps, core_ids=list(range(n)))
    return np.stack([np.asarray(r["out"], dtype=np.float32) for r in res.results], axis=0)
```

```python
import numpy as np
from contextlib import ExitStack
import concourse.bass as bass
import concourse.mybir as mybir
from concourse.bass_utils import run_bass_kernel_spmd

F32 = mybir.dt.float32
BF16 = mybir.dt.bfloat16
AF = mybir.ActivationFunctionType
ALU = mybir.AluOpType
AX = mybir.AxisListType

S = 2048
D = 1024
NT = 16
KC = 8
N_IN = 5400
DFF = 2816
NFB = 22
EPS = 1e-6
SEM_LIMIT = 24000
import os as _os
CUT = int(_os.environ.get('P1B_CUT', '99'))
CUTC = int(_os.environ.get('P1C_CUT', '99'))
SUBC = int(_os.environ.get('P1C_SUB', '99'))
WIDTH = int(_os.environ.get('P1C_WIDTH', '2'))
NEG = -30000.0

C_Q = 0
C_KV = 512
C_G = 1280
C_R = 1304
C_M = 3352


class Buf:
    __slots__ = ("name", "w", "r", "excl")

    def __init__(self, name):
        self.name = name
        self.w = None
        self.r = []
        self.excl = False


class SemW:
    __slots__ = ("h",)

    def __init__(self, h):
        self.h = h


class Slot:
    __slots__ = ("sem", "val")

    def __init__(self, sem):
        self.sem = sem
        self.val = 0


class Q:
    def __init__(self, name, eng):
        self.name = name
        self.eng = eng
        self.sem = None
        self.count = 0
        self.waited = {}
        self.ring = []
        self.ri = 0
        self.pending = False


class T:
    def __init__(self, t, name, nb=1):
        self.t = t
        self.bs = [Buf(f"{name}{i}") for i in range(nb)]

    @property
    def b(self):
        return self.bs[0]


class KB:
    def __init__(self, nc, es):
        self.nc = nc
        self.es = es
        self.nsem = 0
        self.pe = self.mkq("pe", nc.tensor)
        self.act = self.mkq("act", nc.scalar)
        self.dve = self.mkq("dve", nc.vector)
        self.pool = self.mkq("pool", nc.gpsimd)
        self.sp = self.mkq("sp", nc.sync)
        self.qs = [self.pe, self.act, self.dve, self.pool, self.sp]
        for q, n in ((self.sp, 16), (self.pool, 8), (self.act, 4)):
            q.ring = [Slot(self.new_sem(f"{q.name}_d{i}")) for i in range(n)]
        self.out_toks = []

    def new_sem(self, name):
        self.nsem += 1
        return SemW(self.es.enter_context(self.nc.semaphore(f"{name}_{self.nsem}")))

    def mkq(self, name, eng):
        q = Q(name, eng)
        q.sem = self.new_sem(name)
        return q

    def wait(self, q, tok):
        sw, val = tok[0], tok[1]
        if q.waited.get(sw, 0) >= val:
            return
        q.eng.wait_ge(sw.h, val)
        q.waited[sw] = val

    def _dep(self, q, tok, raw, force=False):
        if tok[2] is q and q is self.pe and not force:
            return
        self.wait(q, tok)

    def _deps(self, q, reads, writes, force=False):
        for b in reads:
            if b.w is not None:
                self._dep(q, b.w, True, force)
            if b.excl:
                for t in b.r:
                    if t[2] is not q:
                        self._dep(q, t, False, force)
        for b in writes:
            if b.w is not None:
                self._dep(q, b.w, False, force)
            for t in b.r:
                self._dep(q, t, False, force)

    def _record(self, tok, reads, writes):
        for b in reads:
            if tok[2] is not None:
                b.r = [t for t in b.r if t[2] is not tok[2]]
            b.r.append(tok)
        for b in writes:
            b.w = tok
            b.r = []

    def op(self, q, fn, reads=(), writes=(), inc=True):
        self._deps(q, reads, writes)
        ins = fn()
        if inc:
            if q.count >= SEM_LIMIT and not q.pending:
                q.sem = self.new_sem(q.name)
                q.count = 0
            ins.then_inc(q.sem.h, 1)
            q.count += 1
            q.pending = False
            tok = (q.sem, q.count, q)
        else:
            q.pending = True
            tok = (q.sem, q.count + 1, q)
        self._record(tok, reads, writes)
        return ins

    def dma(self, q, out, in_, reads=(), writes=(), is_out=False):
        self._deps(q, reads, writes, force=True)
        slot = q.ring[q.ri % len(q.ring)]
        q.ri += 1
        if slot.val > 0:
            self.wait(q, (slot.sem, slot.val))
        if slot.val >= SEM_LIMIT:
            slot.sem = self.new_sem(q.name + "_d")
            slot.val = 0
        ins = q.eng.dma_start(out=out, in_=in_)
        ins.then_inc(slot.sem.h, 16)
        slot.val += 16
        tok = (slot.sem, slot.val, None)
        self._record(tok, reads, writes)
        if is_out:
            self.out_toks.append(tok)
        return tok

    def barrier(self):
        toks = []
        for o in self.qs:
            if o.count > 0:
                toks.append((o.sem, o.count, o))
            for sl in o.ring:
                if sl.val > 0:
                    toks.append((sl.sem, sl.val, None))
        for q in self.qs:
            for t in toks:
                if t[2] is q:
                    continue
                self.wait(q, t)

    def finish(self):
        for t in self.out_toks:
            self.wait(self.sp, t)


class _Stop(Exception):
    pass


def build_nc(debug=None, stop=None):
    nc = bass.Bass("TRN2", target_bir_lowering=False)

    def din(name, shape):
        return nc.dram_tensor(name, list(shape), F32, kind="ExternalInput").ap()

    x = din("x", [S, D])
    norm1_g = din("norm1_g", [1, D])
    w_in = din("w_in", [1, D, N_IN])
    nsa_q_norm = din("nsa_q_norm", [1, 64])
    nsa_k_norm = din("nsa_k_norm", [1, 3, 64])
    cmp_pe = din("cmp_pe", [1, 2, 32, 64])
    cmp_w1 = din("cmp_w1", [1, 2, 32, 64, 128])
    cmp_w2 = din("cmp_w2", [1, 2, 128, 64])
    ret_gn_g = din("ret_gn_g", [1, 4, 128])
    w_branch = din("w_branch", [1, 2, 512, D])
    w_out = din("w_out", [1, D, D])
    norm2_g = din("norm2_g", [1, D])
    ffn_w_gate = din("ffn_w_gate", [1, D, DFF])
    ffn_w_up = din("ffn_w_up", [1, D, DFF])
    ffn_w_down = din("ffn_w_down", [1, DFF, D])
    out = nc.dram_tensor("out", [S, D], F32, kind="ExternalOutput").ap()
    xmid = nc.dram_tensor("xmid", [S, D], F32, kind="Internal").ap()
    dbg = {}
    if debug:
        for name, shape in debug.items():
            dbg[name] = nc.dram_tensor("dbg_" + name, list(shape), F32, kind="ExternalOutput").ap()

    w_in_v = w_in[0].rearrange("(k p) n -> p k n", p=128)

    try:
        with ExitStack() as es:
            kb = KB(nc, es)

            def chk(n):
                if stop is not None and n >= stop:
                    kb.barrier()
                    kb.finish()
                    raise _Stop()
            PE, ACT, DVE, POOL, SP = kb.pe, kb.act, kb.dve, kb.pool, kb.sp
            V, A, G, TE = nc.vector, nc.scalar, nc.gpsimd, nc.tensor

            def sb(scope, name, shape, dt, nb=1):
                return T(scope.enter_context(nc.sbuf_tensor(name, list(shape), dt)), name, nb)

            PS2 = [es.enter_context(nc.psum_tensor(f"psp{j}", [128, 1024], F32)) for j in range(4)]
            PS = [T(None, f"ps{i}") for i in range(8)]
            for p_ in PS:
                p_.b.excl = True

            def pf(i):
                return PS2[i // 2][:, (i % 2) * 512:(i % 2 + 1) * 512]

            def pb(i):
                return PS2[i // 2][:].bitcast(BF16)[:, (i % 2) * 1024:(i % 2 + 1) * 1024]

            def pf2(j):
                return PS2[j][:]

            ident_f = sb(es, "ident_f", [128, 128], F32)
            ident_b = sb(es, "ident_b", [128, 128], BF16)
            ones_f = sb(es, "ones_f", [128, 128], F32)
            nhalf = sb(es, "nhalf", [128, 16], F32)
            hT = sb(es, "hT", [128, KC, S], BF16, NT)
            stat = sb(es, "stat", [128, NT, 4], F32, NT)
            hb = [sb(es, f"hb{i}", [128, D], BF16) for i in range(2)]
            junk = sb(es, "junk", [128, D], BF16)

            kb.op(POOL, lambda: G.memset(ones_f.t[:], 1.0), writes=[ones_f.b])
            kb.op(POOL, lambda: G.memset(nhalf.t[:], -0.5), writes=[nhalf.b])
            kb.op(POOL, lambda: G.affine_select(out=ident_f.t[:], in_=ones_f.t[:, 0:128], pattern=[[1, 128]],
                                                compare_op=ALU.is_equal, fill=0.0, base=0, channel_multiplier=-1),
                  reads=[ones_f.b], writes=[ident_f.b])
            kb.op(DVE, lambda: V.tensor_copy(out=ident_b.t[:], in_=ident_f.t[:]), reads=[ident_f.b], writes=[ident_b.b])

            def rstd_from_ss(ss_ap, ms_ap, sd_ap, rs_ap, n, bufs):
                k = ms_ap.shape[-1]
                P_ = ms_ap.shape[0]
                kb.op(DVE, lambda: V.tensor_scalar(out=ms_ap, in0=ss_ap, scalar1=1.0 / n, scalar2=EPS,
                                                   op0=ALU.mult, op1=ALU.add), reads=bufs, writes=bufs)
                kb.op(POOL, lambda: G.tensor_tensor(out=rs_ap, in0=ms_ap, in1=nhalf.t[0:P_, 0:k], op=ALU.pow),
                      reads=list(bufs) + [nhalf.b], writes=bufs)

            def transposes(src_aps, bank, reads):
                pbv = pb(bank)
                n = len(src_aps)
                for i, ap in enumerate(src_aps):
                    kb.op(PE, lambda ap=ap, i=i: TE.transpose(out=pbv[:, i * 128:(i + 1) * 128], in_=ap, identity=ident_b.t[:]),
                          reads=list(reads) + [ident_b.b], writes=[PS[bank].b], inc=(i == n - 1))

            def norm_gen(src, c, gbc, sidx, bank):
                sbuf_ = [stat.bs[c]]
                st = stat.t
                kb.op(ACT, lambda: A.activation(out=junk.t[:], in_=src.t[:], func=AF.Square, accum_out=st[:, c, 0:1]),
                      reads=[src.b], writes=[junk.b] + sbuf_)
                yield
                kb.op(DVE, lambda: V.tensor_scalar(out=st[:, c, 1:2], in0=st[:, c, 0:1], scalar1=1.0 / D, scalar2=EPS,
                                                   op0=ALU.mult, op1=ALU.add), reads=sbuf_, writes=sbuf_)
                yield
                kb.op(POOL, lambda: G.tensor_tensor(out=st[:, c, 3:4], in0=st[:, c, 1:2], in1=nhalf.t[:, 0:1], op=ALU.pow),
                      reads=sbuf_ + [nhalf.b], writes=sbuf_)
                yield
                h = hb[sidx % 2]
                kb.op(DVE, lambda: V.scalar_tensor_tensor(out=h.t[:], in0=src.t[:], scalar=st[:, c, 3:4], in1=gbc.t[:],
                                                          op0=ALU.mult, op1=ALU.mult),
                      reads=[src.b, gbc.b] + sbuf_, writes=[h.b])
                yield
                transposes([h.t[:, k * 128:(k + 1) * 128] for k in range(KC)], bank, [h.b])
                yield
                kb.op(ACT, lambda: A.copy(out=hT.t[:, :, c * 128:(c + 1) * 128],
                                          in_=pb(bank).rearrange("p (a b) -> p a b", b=128)),
                      reads=[PS[bank].b], writes=[hT.bs[c]])
                yield

            def run_interleaved(gen_fns, width=2):
                pending = list(gen_fns)
                active = []
                while pending or active:
                    while pending and len(active) < width:
                        active.append(pending.pop(0)())
                    for g_ in list(active):
                        try:
                            next(g_)
                        except StopIteration:
                            active.remove(g_)

            def load_w(dst, src_ap, q=None):
                kb.dma(q or POOL, dst.t[:], src_ap, writes=[dst.b])

            def dbg_store(name, src_ap, rows, reads):
                if name in dbg:
                    kb.dma(SP, dbg[name][rows], src_ap, reads=reads, is_out=True)

            def phase_1c():
                with ExitStack() as s3:
                    wq = sb(s3, "wq", [128, KC, 512], BF16)
                    load_w(wq, w_in_v[:, :, C_Q:C_Q + 512])
                    wg = sb(s3, "wg", [128, KC, 24], BF16)
                    load_w(wg, w_in_v[:, :, C_G:C_G + 24])
                    sqq = [sb(s3, f"sqq{i}", [128, 512], F32) for i in range(2)]
                    qst = sb(s3, "qst", [128, NT, 32], F32, NT)
                    tmpq = [sb(s3, f"tmpq{i}", [128, 8, 64], F32) for i in range(2)]
                    qaug = [sb(s3, f"qaug{i}", [128, 8, 128], BF16) for i in range(2)]
                    qT = [sb(s3, f"qT{i}", [128, 8, 128], BF16) for i in range(2)]
                    qT2 = [sb(s3, f"qT2{i}", [128, 8, 128], BF16) for i in range(2)]
                    gate = [sb(s3, f"gate{i}", [128, 24], F32) for i in range(2)]
                    scl = [sb(s3, f"scl{i}", [128, 1024], F32) for i in range(2)]
                    NPT = 3
                    PT = [[sb(s3, f"PT{i}_{j}", [128, 1024], BF16) for j in range(NPT)] for i in range(2)]
                    oTs = [sb(s3, f"oTs{i}", [128, 1024], F32) for i in range(2)]
                    num = [sb(s3, f"num{i}", [128, 3, 8, 64], F32) for i in range(2)]
                    den = [sb(s3, f"den{i}", [128, 3, 8], F32) for i in range(2)]
                    rdc = [sb(s3, f"rdc{i}", [128, 8], F32) for i in range(2)]
                    impn = [sb(s3, f"impn{i}", [128, 8, 32], F32) for i in range(2)]
                    imp = [sb(s3, f"imp{i}", [128, 2, 32], F32) for i in range(2)]
                    top8 = [sb(s3, f"top8{i}", [128, 2, 8], F32) for i in range(2)]
                    rd = [sb(s3, f"rd{i}", [128, 3, 8], F32) for i in range(2)]
                    coef = [sb(s3, f"coef{i}", [128, 3, 8], F32) for i in range(2)]
                    oacc = [sb(s3, f"oacc{i}", [128, 8, 64], F32) for i in range(2)]
                    otmp = [sb(s3, f"otmp{i}", [128, 8, 64], F32) for i in range(2)]
                    ytok = [sb(s3, f"ytok{i}", [128, 512], BF16) for i in range(2)]
                    ydbg = sb(s3, "ydbg", [128, 512], F32) if "y_nsa" in dbg else None
                    for i in range(2):
                        kb.op(POOL, lambda i=i: G.memset(qaug[i].t[:], 0.0), writes=[qaug[i].b])
                    ptc = [0, 0]

                    def tile_gen(c):
                        i2 = c % 2
                        base = 4 * i2
                        bZ, bG = base, base + 1
                        bO0, bO1 = base + 2, base + 3
                        Sb = [PS[bZ].b, PS[bG].b]
                        Ob = [PS[bO0].b, PS[bO1].b]
                        tok = slice(c * 128, (c + 1) * 128)

                        def S2():
                            return pf2(base // 2)

                        def O2():
                            return pf2(base // 2 + 1)

                        def X8():
                            return O2().rearrange("p (h c) -> p h c", h=8)

                        def to_token_major(br, ncol):
                            nu, de = num[i2], den[i2]
                            ot = oTs[i2]
                            kb.op(DVE, lambda: V.tensor_scalar(out=ot.t[0:ncol, :], in0=O2()[0:ncol, :], scalar1=1.0, scalar2=None, op0=ALU.mult),
                                  reads=Ob, writes=[ot.b])
                            yield
                            for hh in range(8):
                                kb.op(PE, lambda hh=hh: TE.transpose(out=X8()[:, hh, 0:ncol], in_=ot.t[0:ncol, hh * 128:(hh + 1) * 128],
                                                                     identity=ident_f.t[0:ncol, 0:ncol]),
                                      reads=[ot.b, ident_f.b], writes=[Ob[hh // 4]], inc=(hh % 4 == 3))
                            yield
                            kb.op(ACT, lambda: A.copy(out=nu.t[:, br, :, :], in_=X8()[:, :, 0:64]), reads=Ob, writes=[nu.b])
                            yield
                            kb.op(DVE, lambda: V.tensor_scalar(out=de.t[:, br, :], in0=X8()[:, :, 64], scalar1=1e-30, scalar2=None, op0=ALU.max),
                                  reads=Ob, writes=[de.b])
                            yield

                        def branch(br, KT, VV, kts, qsrc):
                            n = len(kts)
                            for j, kt in enumerate(kts):
                                for g in range(2):
                                    kb.op(PE, lambda kt=kt, g=g: TE.matmul(S2()[:, g * 512:(g + 1) * 512], lhsT=KT.t[:, g, kt * 128:(kt + 1) * 128],
                                                                           rhs=qsrc.t[:, 4 * g:4 * g + 4, :], start=True, stop=True),
                                          reads=[KT.bs[kt], qsrc.b], writes=[Sb[g]])
                                yield
                                pt = PT[i2][ptc[i2] % NPT]
                                ptc[i2] += 1
                                kb.op(ACT, lambda pt=pt: A.activation(out=pt.t[:], in_=S2(), func=AF.Exp), reads=Sb, writes=[pt.b])
                                yield
                                if kt == c:
                                    kb.op(DVE, lambda pt=pt: V.tensor_tensor(out=pt.t[:], in0=pt.t[:], in1=dmask8.t[:].rearrange("p a b -> p (a b)"), op=ALU.mult),
                                          reads=[pt.b, dmask8.b], writes=[pt.b])
                                    yield
                                elif br == 2 and kt == c - 4:
                                    kb.op(DVE, lambda pt=pt: V.tensor_tensor(out=pt.t[:], in0=pt.t[:], in1=tmask8.t[:].rearrange("p a b -> p (a b)"), op=ALU.mult),
                                          reads=[pt.b, tmask8.b], writes=[pt.b])
                                    yield
                                for g in range(2):
                                    kb.op(PE, lambda kt=kt, pt=pt, j=j, g=g: TE.matmul(O2()[0:65, g * 512:(g + 1) * 512], lhsT=VV.t[:, kt, g, :],
                                                                                       rhs=pt.t[:, g * 512:(g + 1) * 512], start=(j == 0), stop=(j == n - 1)),
                                          reads=[pt.b, VV.bs[kt]], writes=[Ob[g]], inc=(j == n - 1))
                                yield
                            yield from to_token_major(br, 65)

                        for k in range(KC):
                            kb.op(PE, lambda k=k: TE.matmul(pf(bZ), lhsT=hT.t[:, k, tok], rhs=wq.t[:, k, :], start=(k == 0), stop=(k == KC - 1)),
                                  reads=[hT.bs[c], wq.b], writes=[PS[bZ].b], inc=(k == KC - 1))
                        for k in range(KC):
                            kb.op(PE, lambda k=k: TE.matmul(pf(bG)[:, 0:24], lhsT=hT.t[:, k, tok], rhs=wg.t[:, k, :], start=(k == 0), stop=(k == KC - 1)),
                                  reads=[hT.bs[c], wg.b], writes=[PS[bG].b], inc=(k == KC - 1))
                        yield
                        sq = sqq[i2]
                        kb.op(ACT, lambda: A.activation(out=sq.t[:], in_=pf(bZ), func=AF.Square), reads=[PS[bZ].b], writes=[sq.b])
                        gt = gate[i2]
                        kb.op(ACT, lambda: A.activation(out=gt.t[:], in_=pf(bG)[:, 0:24], func=AF.Tanh, scale=0.5), reads=[PS[bG].b], writes=[gt.b])
                        yield
                        kb.op(DVE, lambda: V.tensor_scalar(out=gt.t[:], in0=gt.t[:], scalar1=0.5, scalar2=0.5, op0=ALU.mult, op1=ALU.add),
                              reads=[gt.b], writes=[gt.b])
                        qs = qst.t
                        qsb = [qst.bs[c]]
                        kb.op(DVE, lambda: V.tensor_reduce(out=qs[:, c, 0:8], in_=sq.t[:].rearrange("p (a b) -> p a b", b=64), axis=AX.X, op=ALU.add),
                              reads=[sq.b], writes=qsb)
                        yield
                        kb.op(DVE, lambda: V.tensor_scalar(out=qs[:, c, 8:16], in0=qs[:, c, 0:8], scalar1=1.0 / 64, scalar2=EPS, op0=ALU.mult, op1=ALU.add), reads=qsb, writes=qsb)
                        yield
                        kb.op(POOL, lambda: G.tensor_tensor(out=qs[:, c, 24:32], in0=qs[:, c, 8:16], in1=nhalf.t[:, 0:8], op=ALU.pow),
                              reads=qsb + [nhalf.b], writes=qsb)
                        yield
                        tq = tmpq[i2]
                        qa = qaug[i2]
                        kb.op(DVE, lambda: V.tensor_tensor(out=tq.t[:], in0=pf(bZ).rearrange("p (a b) -> p a b", b=64),
                                                           in1=qs[:, c, 24:32].unsqueeze(2).broadcast_to([128, 8, 64]), op=ALU.mult),
                              reads=[PS[bZ].b] + qsb, writes=[tq.b])
                        yield
                        kb.op(DVE, lambda: V.tensor_tensor(out=qa.t[:, :, 0:64], in0=tq.t[:], in1=gq.t[:].unsqueeze(1).broadcast_to([128, 8, 64]), op=ALU.mult),
                              reads=[tq.b, gq.b], writes=[qa.b])
                        kb.op(POOL, lambda: G.tensor_copy(out=qa.t[:, :, 96:100], in_=QAL.t[:, c, :, :]), reads=[QAL.b], writes=[qa.b])
                        yield
                        transposes([qa.t[:, h, :] for h in range(8)], bZ, [qa.b])
                        yield
                        q1 = qT[i2]
                        kb.op(ACT, lambda: A.copy(out=q1.t[:], in_=pb(bZ).rearrange("p (a b) -> p a b", b=128)), reads=[PS[bZ].b], writes=[q1.b])
                        yield
                        nu, de = num[i2], den[i2]
                        rdc_, impn_, imp_, top8_ = rdc[i2], impn[i2], imp[i2], top8[i2]
                        sc_ = scl[i2]
                        pc = PT[i2][ptc[i2] % NPT]
                        ptc[i2] += 1
                        for g in range(2):
                            kb.op(PE, lambda g=g: TE.matmul(S2()[0:127, g * 512:(g + 1) * 512], lhsT=KcT.t[:, g, 0:127], rhs=q1.t[:, 4 * g:4 * g + 4, :], start=True, stop=True),
                                  reads=[KcT.b, q1.b], writes=[Sb[g]])
                        yield
                        kb.op(DVE, lambda: V.scalar_tensor_tensor(out=sc_.t[0:127, :].rearrange("p (a b) -> p a b", b=128),
                                                                  in0=S2()[0:127, :].rearrange("p (a b) -> p a b", b=128), scalar=60.0,
                                                                  in1=cmask.t[0:127, tok].unsqueeze(1).broadcast_to([127, 8, 128]),
                                                                  op0=ALU.min, op1=ALU.add),
                              reads=Sb + [cmask.b], writes=[sc_.b])
                        yield
                        kb.op(ACT, lambda: A.activation(out=pc.t[0:127, :], in_=sc_.t[0:127, :], func=AF.Exp), reads=[sc_.b], writes=[pc.b])
                        yield
                        for g in range(2):
                            kb.op(PE, lambda g=g: TE.matmul(O2()[0:97, g * 512:(g + 1) * 512], lhsT=Vc.t[0:127, g, :], rhs=pc.t[0:127, g * 512:(g + 1) * 512], start=True, stop=True),
                                  reads=[pc.b, Vc.b], writes=[Ob[g]])
                        yield
                        yield from to_token_major(0, 97)
                        kb.op(DVE, lambda: V.reciprocal(out=rdc_.t[:], in_=de.t[:, 0, :]), reads=[de.b], writes=[rdc_.b])
                        yield
                        kb.op(DVE, lambda: V.tensor_tensor(out=impn_.t[:], in0=X8()[:, :, 65:97],
                                                           in1=rdc_.t[:].unsqueeze(2).broadcast_to([128, 8, 32]), op=ALU.mult),
                              reads=Ob + [rdc_.b], writes=[impn_.b])
                        yield
                        q2 = qT2[i2]

                        def imp_chain():
                            kb.op(DVE, lambda: V.tensor_reduce(out=imp_.t[:], in_=impn_.t[:].rearrange("p (g r) j -> p g j r", g=2), axis=AX.X, op=ALU.add),
                                  reads=[impn_.b], writes=[imp_.b])
                            yield
                            kb.op(DVE, lambda: V.tensor_tensor(out=imp_.t[:], in0=imp_.t[:], in1=addc.t[:, c:c + 1, :].broadcast_to([128, 2, 32]), op=ALU.add),
                                  reads=[imp_.b, addc.b], writes=[imp_.b])
                            yield
                            for g in range(2):
                                kb.op(DVE, lambda g=g: V.max(out=top8_.t[:, g, :], in_=imp_.t[:, g, :]), reads=[imp_.b], writes=[top8_.b])
                            yield
                            for g in range(2):
                                kb.op(DVE, lambda g=g: V.tensor_scalar(out=qa.t[:, 4 * g:4 * g + 4, 64:96],
                                                                       in0=imp_.t[:, g:g + 1, :].broadcast_to([128, 4, 32]),
                                                                       scalar1=top8_.t[:, g, 7:8], scalar2=NEG, op0=ALU.is_lt, op1=ALU.mult),
                                      reads=[imp_.b, top8_.b], writes=[qa.b])
                            yield

                        gw = branch(2, KT_win, V_win, list(range(max(0, c - 4), c + 1)), q1)
                        gi = imp_chain()
                        live = [gw, gi]
                        while live:
                            for g_ in list(live):
                                try:
                                    next(g_)
                                except StopIteration:
                                    live.remove(g_)
                            yield
                        transposes([qa.t[:, h, :] for h in range(8)], bZ, [qa.b])
                        kb.op(ACT, lambda: A.copy(out=q2.t[:], in_=pb(bZ).rearrange("p (a b) -> p a b", b=128)), reads=[PS[bZ].b], writes=[q2.b])
                        yield
                        yield from branch(1, KT_slc, V_slc, list(range(0, c + 1)), q2)
                        rd_, coef_, oacc_, otmp_ = rd[i2], coef[i2], oacc[i2], otmp[i2]
                        kb.op(DVE, lambda: V.reciprocal(out=rd_.t[:], in_=de.t[:]), reads=[de.b], writes=[rd_.b])
                        yield
                        kb.op(DVE, lambda: V.tensor_tensor(out=coef_.t[:], in0=gt.t[:].rearrange("p (h b) -> p b h", b=3), in1=rd_.t[:], op=ALU.mult),
                              reads=[gt.b, rd_.b], writes=[coef_.b])
                        yield
                        kb.op(DVE, lambda: V.tensor_tensor(out=oacc_.t[:], in0=nu.t[:, 0], in1=coef_.t[:, 0, :].unsqueeze(2).broadcast_to([128, 8, 64]), op=ALU.mult),
                              reads=[nu.b, coef_.b], writes=[oacc_.b])
                        kb.op(POOL, lambda: G.tensor_tensor(out=otmp_.t[:], in0=nu.t[:, 1], in1=coef_.t[:, 1, :].unsqueeze(2).broadcast_to([128, 8, 64]), op=ALU.mult),
                              reads=[nu.b, coef_.b], writes=[otmp_.b])
                        yield
                        kb.op(DVE, lambda: V.tensor_tensor(out=oacc_.t[:], in0=oacc_.t[:], in1=otmp_.t[:], op=ALU.add), reads=[oacc_.b, otmp_.b], writes=[oacc_.b])
                        yield
                        kb.op(POOL, lambda: G.tensor_tensor(out=otmp_.t[:], in0=nu.t[:, 2], in1=coef_.t[:, 2, :].unsqueeze(2).broadcast_to([128, 8, 64]), op=ALU.mult),
                              reads=[nu.b, coef_.b], writes=[otmp_.b])
                        yield
                        yt = ytok[i2]
                        kb.op(DVE, lambda: V.tensor_tensor(out=yt.t[:], in0=oacc_.t[:].rearrange("p a b -> p (a b)"), in1=otmp_.t[:].rearrange("p a b -> p (a b)"), op=ALU.add),
                              reads=[oacc_.b, otmp_.b], writes=[yt.b])
                        if ydbg is not None:
                            kb.op(POOL, lambda: G.tensor_tensor(out=ydbg.t[:], in0=oacc_.t[:].rearrange("p a b -> p (a b)"), in1=otmp_.t[:].rearrange("p a b -> p (a b)"), op=ALU.add),
                                  reads=[oacc_.b, otmp_.b], writes=[ydbg.b])
                            dbg_store("y_nsa", ydbg.t[:], tok, [ydbg.b])
                        yield
                        transposes([yt.t[:, k * 128:(k + 1) * 128] for k in range(4)], bZ, [yt.b])
                        yield
                        kb.op(ACT, lambda: A.copy(out=ynsaT.t[:, :, tok], in_=pb(bZ)[:, 0:512].rearrange("p (a b) -> p a b", b=128)),
                              reads=[PS[bZ].b], writes=[ynsaT.bs[c]])
                        yield

                    run_interleaved([(lambda c=c: tile_gen(c)) for c in range(NT)], width=WIDTH)

            def phase_1d():
                with ExitStack() as s4:
                    wr = sb(s4, "wr", [128, KC, 2048], BF16, 4)
                    for j in range(4):
                        kb.dma(POOL, wr.t[:, :, j * 512:(j + 1) * 512], w_in_v[:, :, C_R + j * 512:C_R + (j + 1) * 512], writes=[wr.bs[j]])
                    for j in range(4):
                        kb.dma(POOL, wm.t[:, :, j * 512:(j + 1) * 512], w_in_v[:, :, C_M + j * 512:C_M + (j + 1) * 512], writes=[wm.bs[j]])
                    load_w(wbr, w_branch[0].rearrange("n (k p) d -> p (n k) d", p=128))
                    idT = sb(s4, "idT", [128, 4, 128], F32)
                    qdec = sb(s4, "qdec", [128, 4, 128], F32)
                    kdec = sb(s4, "kdec", [128, 4, 128], F32)
                    gn = sb(s4, "gn", [128, 512], F32)
                    eij = sb(s4, "eij", [128, 128], F32)
                    rowq = sb(s4, "rowq", [128, 128], F32)
                    rowk = sb(s4, "rowk", [128, 128], F32)
                    kb.dma(SP, gn.t[:], ret_gn_g.rearrange("o a b -> o (a b)").broadcast_to([128, 512]), writes=[gn.b])
                    kb.op(POOL, lambda: G.iota(eij.t[:], pattern=[[1, 128]], base=0, channel_multiplier=-1, allow_small_or_imprecise_dtypes=True), writes=[eij.b])
                    kb.op(POOL, lambda: G.iota(rowq.t[:], pattern=[[1, 128]], base=1, channel_multiplier=0, allow_small_or_imprecise_dtypes=True), writes=[rowq.b])
                    kb.op(POOL, lambda: G.iota(rowk.t[:], pattern=[[-1, 128]], base=127, channel_multiplier=0, allow_small_or_imprecise_dtypes=True), writes=[rowk.b])
                    lgs = [float(np.log(1.0 - 2.0 ** (-5.0 - h))) for h in range(4)]
                    cds = [float(np.exp(128.0 * np.float32(lg))) for lg in lgs]
                    for h in range(4):
                        kb.op(ACT, lambda h=h: A.activation(out=idT.t[:, h, :], in_=eij.t[:], func=AF.Exp, scale=lgs[h]), reads=[eij.b], writes=[idT.b])
                        kb.op(POOL, lambda h=h: G.affine_select(out=idT.t[:, h, :], in_=idT.t[:, h, :], pattern=[[1, 128]], compare_op=ALU.is_ge, fill=0.0,
                                                                base=0, channel_multiplier=-1), reads=[idT.b], writes=[idT.b])
                        kb.op(ACT, lambda h=h: A.activation(out=qdec.t[:, h, :], in_=rowq.t[:], func=AF.Exp, scale=lgs[h]), reads=[rowq.b], writes=[qdec.b])
                        kb.op(ACT, lambda h=h: A.activation(out=kdec.t[:, h, :], in_=rowk.t[:], func=AF.Exp, scale=lgs[h]), reads=[rowk.b], writes=[kdec.b])
                    qTr = sb(s4, "qTr", [128, 4, 512], BF16)
                    qdT = sb(s4, "qdT", [128, 4, 512], BF16)
                    kTr = sb(s4, "kTr", [128, 4, 512], BF16)
                    kdT = sb(s4, "kdT", [128, 4, 512], BF16)
                    v_sb = [sb(s4, f"v_sb{i}", [128, 4, 128], BF16) for i in range(2)]
                    sgl = [sb(s4, f"sgl{i}", [128, 512], F32) for i in range(2)]
                    kd = [sb(s4, f"kd{i}", [128, 4, 128], BF16) for i in range(2)]
                    attb = [sb(s4, f"attb{i}", [128, 4, 128], BF16) for i in range(2)]
                    state_f = sb(s4, "state_f", [128, 4, 128], F32)
                    state_b = sb(s4, "state_b", [128, 4, 128], BF16)
                    yr = [sb(s4, f"yr{i}", [128, 512], BF16) for i in range(2)]
                    yrdbg = sb(s4, "yrdbg", [128, 512], F32) if "y_ret" in dbg else None
                    kb.op(POOL, lambda: G.memset(state_f.t[:], 0.0), writes=[state_f.b])
                    kb.op(POOL, lambda: G.memset(state_b.t[:], 0.0), writes=[state_b.b])
                    KS = float(128.0 ** -0.5)
                    pcnt = [0]
                    state_done = [False] * (NT + 1)
                    bst = [sb(s4, f"bst{i}", [128, 4, 6], F32) for i in range(2)]
                    mv = [sb(s4, f"mv{i}", [128, 4, 2], F32) for i in range(2)]
                    rs4 = [sb(s4, f"rs4{i}", [128, 12], F32) for i in range(2)]
                    on = [sb(s4, f"on{i}", [128, 512], F32) for i in range(2)]

                    def p1d_gen(c):
                        i2 = c % 2
                        cl = c % 4
                        bA, bB, bT = 2 + 3 * i2, 3 + 3 * i2, 4 + 3 * i2
                        tok = slice(c * 128, (c + 1) * 128)
                        cs = slice(cl * 128, (cl + 1) * 128)
                        bst_, mv_, rs4_, on_ = bst[i2], mv[i2], rs4[i2], on[i2]
                        for k in range(KC):
                            kb.op(PE, lambda k=k: TE.matmul(pf(bA), lhsT=hT.t[:, k, tok], rhs=wr.t[:, k, 1024:1536], start=(k == 0), stop=(k == KC - 1)),
                                  reads=[hT.bs[c], wr.bs[2]], writes=[PS[bA].b], inc=(k == KC - 1))
                        yield
                        for k in range(KC):
                            kb.op(PE, lambda k=k: TE.matmul(pf(bB), lhsT=hT.t[:, k, tok], rhs=wr.t[:, k, 1536:2048], start=(k == 0), stop=(k == KC - 1)),
                                  reads=[hT.bs[c], wr.bs[3]], writes=[PS[bB].b], inc=(k == KC - 1))
                        yield
                        vs, sg_, kd_, ab = v_sb[i2], sgl[i2], kd[i2], attb[i2]
                        kb.op(ACT, lambda: A.copy(out=vs.t[:].rearrange("p a b -> p (a b)"), in_=pf(bA)), reads=[PS[bA].b], writes=[vs.b])
                        yield
                        kb.op(ACT, lambda: A.activation(out=sg_.t[:], in_=pf(bB), func=AF.Silu), reads=[PS[bB].b], writes=[sg_.b])
                        yield
                        transposes([kdT.t[:, h, cs] for h in range(4)], bT, [kdT.b])
                        yield
                        kb.op(ACT, lambda: A.copy(out=kd_.t[:].rearrange("p a b -> p (a b)"), in_=pb(bT)[:, 0:512]), reads=[PS[bT].b], writes=[kd_.b])
                        yield
                        for h in range(4):
                            kb.op(PE, lambda h=h: TE.matmul(pf(bA)[:, h * 128:(h + 1) * 128], lhsT=kTr.t[:, h, cs], rhs=qTr.t[:, h, cs], start=True, stop=True),
                                  reads=[kTr.b, qTr.b], writes=[PS[bA].b], inc=(h == 3))
                        yield
                        kb.op(DVE, lambda: V.tensor_tensor(out=ab.t[:], in0=pf(bA).rearrange("p (a b) -> p a b", b=128), in1=idT.t[:], op=ALU.mult),
                              reads=[PS[bA].b, idT.b], writes=[ab.b])
                        yield
                        while c > 0 and not state_done[c - 1]:
                            yield
                        for h in range(4):
                            kb.op(PE, lambda h=h: TE.matmul(pf(bB)[:, h * 128:(h + 1) * 128], lhsT=ab.t[:, h, :], rhs=vs.t[:, h, :], start=True, stop=(c == 0)),
                                  reads=[ab.b, vs.b], writes=[PS[bB].b], inc=(c == 0 and h == 3))
                            if c > 0:
                                kb.op(PE, lambda h=h: TE.matmul(pf(bB)[:, h * 128:(h + 1) * 128], lhsT=qdT.t[:, h, cs], rhs=state_b.t[:, h, :], start=False, stop=True),
                                      reads=[qdT.b, state_b.b], writes=[PS[bB].b], inc=(h == 3))
                        yield
                        if c < NT - 1:
                            for h in range(4):
                                kb.op(PE, lambda h=h: TE.matmul(pf(bA)[:, h * 128:(h + 1) * 128], lhsT=kd_.t[:, h, :], rhs=vs.t[:, h, :], start=True, stop=True),
                                      reads=[kd_.b, vs.b], writes=[PS[bA].b], inc=(h == 3))
                            yield
                            for h in range(4):
                                kb.op(DVE, lambda h=h: V.scalar_tensor_tensor(out=state_f.t[:, h, :], in0=state_f.t[:, h, :], scalar=cds[h],
                                                                              in1=pf(bA)[:, h * 128:(h + 1) * 128], op0=ALU.mult, op1=ALU.add),
                                      reads=[state_f.b, PS[bA].b], writes=[state_f.b])
                            kb.op(POOL, lambda: G.tensor_copy(out=state_b.t[:], in_=state_f.t[:]), reads=[state_f.b], writes=[state_b.b])
                        state_done[c] = True
                        yield
                        for h in range(4):
                            kb.op(DVE, lambda h=h: V.bn_stats(out=bst_.t[:, h, :], in_=pf(bB)[:, h * 128:(h + 1) * 128]), reads=[PS[bB].b], writes=[bst_.b])
                        yield
                        for h in range(4):
                            kb.op(DVE, lambda h=h: V.bn_aggr(out=mv_.t[:, h, :], in_=bst_.t[:, h, :]), reads=[bst_.b], writes=[mv_.b])
                        yield
                        kb.op(DVE, lambda: V.tensor_scalar(out=rs4_.t[:, 0:4], in0=mv_.t[:, :, 1], scalar1=EPS, scalar2=None, op0=ALU.add), reads=[mv_.b], writes=[rs4_.b])
                        yield
                        kb.op(POOL, lambda: G.tensor_tensor(out=rs4_.t[:, 8:12], in0=rs4_.t[:, 0:4], in1=nhalf.t[:, 0:4], op=ALU.pow),
                              reads=[rs4_.b, nhalf.b], writes=[rs4_.b])
                        yield
                        for h in range(4):
                            kb.op(DVE, lambda h=h: V.tensor_scalar(out=on_.t[:, h * 128:(h + 1) * 128], in0=pf(bB)[:, h * 128:(h + 1) * 128],
                                                                   scalar1=mv_.t[:, h, 0:1], scalar2=rs4_.t[:, 8 + h:9 + h], op0=ALU.subtract, op1=ALU.mult),
                                  reads=[PS[bB].b, mv_.b, rs4_.b], writes=[on_.b])
                        yield
                        kb.op(POOL, lambda: G.tensor_tensor(out=on_.t[:], in0=on_.t[:], in1=gn.t[:], op=ALU.mult), reads=[on_.b, gn.b], writes=[on_.b])
                        yield
                        y_ = yr[i2]
                        kb.op(DVE, lambda: V.tensor_tensor(out=y_.t[:], in0=on_.t[:], in1=sg_.t[:], op=ALU.mult), reads=[on_.b, sg_.b], writes=[y_.b])
                        if yrdbg is not None:
                            kb.op(DVE, lambda: V.tensor_tensor(out=yrdbg.t[:], in0=on_.t[:], in1=sg_.t[:], op=ALU.mult), reads=[on_.b, sg_.b], writes=[yrdbg.b])
                            dbg_store("y_ret", yrdbg.t[:], tok, [yrdbg.b])
                        yield
                        transposes([y_.t[:, k * 128:(k + 1) * 128] for k in range(4)], bT, [y_.b])
                        yield
                        kb.op(ACT, lambda: A.copy(out=yretT.t[:, :, tok], in_=pb(bT)[:, 0:512].rearrange("p (a b) -> p a b", b=128)),
                              reads=[PS[bT].b], writes=[yretT.bs[c]])
                        yield

                    for tg in range(4):
                        tks = slice(tg * 512, (tg + 1) * 512)
                        hbs = [hT.bs[4 * tg + i] for i in range(4)]
                        for qk in range(2):
                            for h in range(4):
                                bk = pcnt[0] % 2
                                pcnt[0] += 1
                                for k in range(KC):
                                    kb.op(PE, lambda k=k, qk=qk, h=h, bk=bk: TE.matmul(pf(bk), lhsT=wr.t[:, k, qk * 512 + h * 128:qk * 512 + (h + 1) * 128],
                                                                                      rhs=hT.t[:, k, tks], start=(k == 0), stop=(k == KC - 1)),
                                          reads=hbs + [wr.bs[qk]], writes=[PS[bk].b], inc=(k == KC - 1))
                                pv4 = pf(bk).rearrange("p (a b) -> p a b", b=128)
                                if qk == 0:
                                    kb.op(ACT, lambda h=h, bk=bk: A.copy(out=qTr.t[:, h, :], in_=pf(bk)), reads=[PS[bk].b], writes=[qTr.b])
                                    kb.op(DVE, lambda h=h, pv4=pv4: V.tensor_tensor(out=qdT.t[:, h, :].rearrange("p (a b) -> p a b", b=128), in0=pv4,
                                                                                    in1=qdec.t[:, h:h + 1, :].broadcast_to([128, 4, 128]), op=ALU.mult),
                                          reads=[PS[bk].b, qdec.b], writes=[qdT.b])
                                else:
                                    kb.op(ACT, lambda h=h, bk=bk: A.mul(out=kTr.t[:, h, :], in_=pf(bk), mul=KS), reads=[PS[bk].b], writes=[kTr.b])
                                    kb.op(DVE, lambda h=h, pv4=pv4: V.scalar_tensor_tensor(out=kdT.t[:, h, :].rearrange("p (a b) -> p a b", b=128), in0=pv4, scalar=KS,
                                                                                           in1=kdec.t[:, h:h + 1, :].broadcast_to([128, 4, 128]),
                                                                                           op0=ALU.mult, op1=ALU.mult),
                                          reads=[PS[bk].b, kdec.b], writes=[kdT.b])
                        run_interleaved([(lambda c=c: p1d_gen(c)) for c in range(4 * tg, 4 * tg + 4)], width=2)

            def phase_1e():
                with ExitStack() as s5:
                    xt = [sb(s5, f"xte{i}", [128, D], F32) for i in range(2)]
                    g2bc = sb(s5, "g2bc", [128, D], F32)
                    kb.dma(SP, g2bc.t[:], norm2_g[0:1, :].broadcast_to([128, D]), writes=[g2bc.b])
                    wo = sb(s5, "wo", [128, KC, D], BF16)
                    load_w(wo, w_out[0].rearrange("(k p) d -> p k d", p=128))
                    gates = [sb(s5, f"gates{i}", [128, D], F32, 2) for i in range(2)]
                    tmix = [sb(s5, f"tmix{i}", [128, D], F32) for i in range(2)]
                    tmix2 = [sb(s5, f"tmix2{i}", [128, D], F32) for i in range(2)]
                    mixed = [sb(s5, f"mixed{i}", [128, D], BF16) for i in range(2)]
                    mixT = [sb(s5, f"mixT{i}", [128, KC, 128], BF16) for i in range(2)]
                    x1t = [sb(s5, f"x1t{i}", [128, D], F32) for i in range(2)]

                    def p1e_gen(c):
                        i2 = c % 2
                        bs_ = [4 * i2 + i for i in range(4)]
                        tok = slice(c * 128, (c + 1) * 128)
                        xx = xt[i2]
                        kb.dma(SP, xx.t[:], x[tok, :], writes=[xx.b])
                        gt_, tm = gates[i2], (tmix[i2], tmix2[i2])
                        for n, yT in ((0, ynsaT), (1, yretT)):
                            for half in range(2):
                                j = 2 * n + half
                                for k in range(KC):
                                    kb.op(PE, lambda j=j, k=k, half=half: TE.matmul(pf(bs_[half]), lhsT=hT.t[:, k, tok], rhs=wm.t[:, k, j * 512:(j + 1) * 512],
                                                                                    start=(k == 0), stop=(k == KC - 1)),
                                          reads=[hT.bs[c], wm.bs[j]], writes=[PS[bs_[half]].b], inc=(k == KC - 1))
                                yield
                            for half in range(2):
                                bk = bs_[2 + half]
                                for k in range(4):
                                    kb.op(PE, lambda n=n, half=half, k=k, bk=bk, yT=yT: TE.matmul(pf(bk), lhsT=yT.t[:, k, tok], rhs=wbr.t[:, n * 4 + k, half * 512:(half + 1) * 512],
                                                                                                  start=(k == 0), stop=(k == 3)),
                                          reads=[yT.bs[c], wbr.b], writes=[PS[bk].b], inc=(k == 3))
                                yield
                            for half in range(2):
                                kb.op(ACT, lambda half=half: A.activation(out=gt_.t[:, half * 512:(half + 1) * 512], in_=pf(bs_[half]), func=AF.Sigmoid),
                                      reads=[PS[bs_[half]].b], writes=[gt_.bs[half]])
                                yield
                            for half in range(2):
                                hs = slice(half * 512, (half + 1) * 512)
                                kb.op(DVE, lambda half=half, hs=hs, n=n: V.tensor_tensor(out=tm[n].t[:, hs], in0=gt_.t[:, hs], in1=pf(bs_[2 + half]), op=ALU.mult),
                                      reads=[gt_.bs[half], PS[bs_[2 + half]].b], writes=[tm[n].b])
                                yield
                        mx = mixed[i2]
                        kb.op(POOL, lambda: G.tensor_tensor(out=mx.t[:], in0=tm[0].t[:], in1=tm[1].t[:], op=ALU.add), reads=[tm[0].b, tm[1].b], writes=[mx.b])
                        yield
                        transposes([mx.t[:, k * 128:(k + 1) * 128] for k in range(KC)], bs_[0], [mx.b])
                        yield
                        mt = mixT[i2]
                        kb.op(ACT, lambda: A.copy(out=mt.t[:], in_=pb(bs_[0]).rearrange("p (a b) -> p a b", b=128)), reads=[PS[bs_[0]].b], writes=[mt.b])
                        yield
                        for half in range(2):
                            for k in range(KC):
                                kb.op(PE, lambda half=half, k=k: TE.matmul(pf(bs_[2 + half]), lhsT=mt.t[:, k, :], rhs=wo.t[:, k, half * 512:(half + 1) * 512],
                                                                           start=(k == 0), stop=(k == KC - 1)),
                                      reads=[mt.b, wo.b], writes=[PS[bs_[2 + half]].b], inc=(k == KC - 1))
                            yield
                        x1 = x1t[i2]
                        for half in range(2):
                            hs = slice(half * 512, (half + 1) * 512)
                            kb.op(DVE, lambda half=half, hs=hs: V.tensor_tensor(out=x1.t[:, hs], in0=xx.t[:, hs], in1=pf(bs_[2 + half]), op=ALU.add),
                                  reads=[xx.b, PS[bs_[2 + half]].b], writes=[x1.b])
                            yield
                        kb.dma(SP, xmid[tok, :], x1.t[:], reads=[x1.b])
                        dbg_store("x1", x1.t[:], tok, [x1.b])
                        yield from norm_gen(x1, c, g2bc, i2, bs_[1])

                    run_interleaved([(lambda c=c: p1e_gen(c)) for c in range(NT)], width=2)

            def phase_2():
                with ExitStack() as s6:
                    xt = [sb(s6, f"xtf{i}", [128, D], F32) for i in range(2)]
                    wd = sb(s6, "wd", [128, NFB, D], BF16, 2)
                    wd_v = ffn_w_down[0].rearrange("(fb p) d -> p fb d", p=128)
                    kb.dma(POOL, wd.t[:, 0:11, :], wd_v[:, 0:11, :], writes=[wd.bs[0]])
                    kb.dma(POOL, wd.t[:, 11:22, :], wd_v[:, 11:22, :], writes=[wd.bs[1]])
                    wgs = [sb(s6, f"wgs{i}", [128, KC, 256], BF16) for i in range(2)]
                    wus = [sb(s6, f"wus{i}", [128, KC, 256], BF16) for i in range(2)]
                    act = sb(s6, "act", [128, NFB, 1024], BF16, NFB)
                    sgs = [sb(s6, f"sgs{i}", [128, 512], F32) for i in range(2)]
                    outt = [sb(s6, f"outt{i}", [128, D], F32) for i in range(2)]
                    wg_v = ffn_w_gate[0].rearrange("(k p) f -> p k f", p=128)
                    wu_v = ffn_w_up[0].rearrange("(k p) f -> p k f", p=128)
                    cn = [0, 0]
                    for hf in range(2):
                        for fg in range(11):
                            cols = slice(fg * 256, (fg + 1) * 256)
                            wg_, wu_ = wgs[fg % 2], wus[fg % 2]
                            kb.dma(POOL, wg_.t[:], wg_v[:, :, cols], writes=[wg_.b])
                            kb.dma(POOL, wu_.t[:], wu_v[:, :, cols], writes=[wu_.b])
                            for fl in range(2):
                                fb = fg * 2 + fl
                                for t2 in range(2):
                                    tokc = slice(hf * 1024 + t2 * 512, hf * 1024 + (t2 + 1) * 512)
                                    hbs = [hT.bs[hf * 8 + t2 * 4 + i] for i in range(4)]
                                    gb, ub = (0, 1) if cn[0] % 2 == 0 else (2, 3)
                                    cn[0] += 1
                                    for k in range(KC):
                                        kb.op(PE, lambda k=k, fl=fl, gb=gb, wg_=wg_, tokc=tokc: TE.matmul(pf(gb), lhsT=wg_.t[:, k, fl * 128:(fl + 1) * 128], rhs=hT.t[:, k, tokc],
                                                                                                          start=(k == 0), stop=(k == KC - 1)),
                                              reads=hbs + [wg_.b], writes=[PS[gb].b], inc=(k == KC - 1))
                                    for k in range(KC):
                                        kb.op(PE, lambda k=k, fl=fl, ub=ub, wu_=wu_, tokc=tokc: TE.matmul(pf(ub), lhsT=wu_.t[:, k, fl * 128:(fl + 1) * 128], rhs=hT.t[:, k, tokc],
                                                                                                          start=(k == 0), stop=(k == KC - 1)),
                                              reads=hbs + [wu_.b], writes=[PS[ub].b], inc=(k == KC - 1))
                                    sg_ = sgs[cn[0] % 2]
                                    kb.op(ACT, lambda sg_=sg_, gb=gb: A.activation(out=sg_.t[:], in_=pf(gb), func=AF.Silu), reads=[PS[gb].b], writes=[sg_.b])
                                    kb.op(DVE, lambda sg_=sg_, ub=ub, fb=fb, t2=t2: V.tensor_tensor(out=act.t[:, fb, t2 * 512:(t2 + 1) * 512], in0=sg_.t[:], in1=pf(ub), op=ALU.mult),
                                          reads=[sg_.b, PS[ub].b], writes=[act.bs[fb]])
                        for tl in range(8):
                            c = hf * 8 + tl
                            tok = slice(c * 128, (c + 1) * 128)
                            xx = xt[c % 2]
                            kb.dma(SP, xx.t[:], xmid[tok, :], writes=[xx.b])
                            ob = (4, 5) if cn[1] % 2 == 0 else (6, 7)
                            cn[1] += 1
                            for half in range(2):
                                for fb in range(NFB):
                                    kb.op(PE, lambda half=half, fb=fb, ob=ob, tl=tl: TE.matmul(pf(ob[half]), lhsT=act.t[:, fb, tl * 128:(tl + 1) * 128],
                                                                                               rhs=wd.t[:, fb, half * 512:(half + 1) * 512],
                                                                                               start=(fb == 0), stop=(fb == NFB - 1)),
                                          reads=[act.bs[fb], wd.bs[0 if fb < 11 else 1]], writes=[PS[ob[half]].b], inc=(fb == NFB - 1))
                            ot = outt[c % 2]
                            for half in range(2):
                                hs = slice(half * 512, (half + 1) * 512)
                                kb.op(DVE, lambda half=half, hs=hs, ob=ob, ot=ot, xx=xx: V.tensor_tensor(out=ot.t[:, hs], in0=xx.t[:, hs], in1=pf(ob[half]), op=ALU.add),
                                      reads=[xx.b, PS[ob[half]].b], writes=[ot.b])
                            kb.dma(SP, out[tok, :], ot.t[:], reads=[ot.b], is_out=True)

            with ExitStack() as sB:
                ynsaT = sb(sB, "ynsaT", [128, 4, S], BF16, NT)
                yretT = sb(sB, "yretT", [128, 4, S], BF16, NT)

                with ExitStack() as sA:
                    gq = sb(sA, "gq", [128, 64], F32)
                    gk = sb(sA, "gk", [128, 3, 64], F32)
                    QAL = sb(sA, "QAL", [128, NT, 8, 4], BF16)
                    KAL = sb(sA, "KAL", [128, NT, 4], BF16)
                    KCAL = sb(sA, "KCAL", [128, 4], BF16)
                    OH = sb(sA, "OH", [128, NT, 32], BF16)
                    dmask8 = sb(sA, "dmask8", [128, 8, 128], BF16)
                    tmask8 = sb(sA, "tmask8", [128, 8, 128], BF16)
                    cmask = sb(sA, "cmask", [128, S], BF16)
                    addc = sb(sA, "addc", [128, NT, 32], F32)
                    ov = sb(sA, "ov", [128, 32], BF16)
                    KT_slc = sb(sA, "KT_slc", [128, 2, S], BF16, NT)
                    KT_win = sb(sA, "KT_win", [128, 2, S], BF16, NT)
                    V_slc = sb(sA, "V_slc", [128, NT, 2, 65], BF16, NT)
                    V_win = sb(sA, "V_win", [128, NT, 2, 65], BF16, NT)
                    KcT = sb(sA, "KcT", [128, 2, 128], BF16)
                    Vc = sb(sA, "Vc", [128, 2, 97], BF16)

                    with ExitStack() as s0:
                        SL = sb(s0, "SL", [128, 8], F32)
                        th128 = sb(s0, "th128", [128, NT], F32)
                        pidx = sb(s0, "pidx", [128, 1], F32)
                        QALf = sb(s0, "QALf", [128, NT, 8, 4], F32)
                        KALf = sb(s0, "KALf", [128, NT, 4], F32)
                        KCALf = sb(s0, "KCALf", [128, 4], F32)
                        rel = sb(s0, "rel", [128, NT, 32], F32)
                        f0 = sb(s0, "f0", [128, NT, 32], F32)
                        f1 = sb(s0, "f1", [128, NT, 32], F32)
                        t1 = sb(s0, "t1", [128, NT, 32], F32)
                        hp = sb(s0, "hp", [128, 1], F32)
                        ovf = sb(s0, "ovf", [128, 32], F32)
                        ova = sb(s0, "ova", [128, 32], F32)
                        ones_b = sb(s0, "ones_b", [128, 512], BF16)
                        ones_b2 = sb(s0, "ones_b2", [128, 1024], BF16)
                        zeros_b = sb(s0, "zeros_b", [128, 512], BF16)

                        kb.dma(SP, gq.t[:], nsa_q_norm[0:1, :].broadcast_to([128, 64]), writes=[gq.b])
                        kb.dma(SP, gk.t[:].rearrange("p a b -> p (a b)"),
                               nsa_k_norm.rearrange("o a b -> o (a b)").broadcast_to([128, 192]), writes=[gk.b])
                        kb.op(DVE, lambda: V.tensor_scalar(out=gq.t[:], in0=gq.t[:], scalar1=0.125, scalar2=None, op0=ALU.mult),
                              reads=[gq.b], writes=[gq.b])
                        for h in range(8):
                            kb.op(POOL, lambda h=h: G.memset(SL.t[:, h:h + 1], 2.0 ** (-(h + 1))), writes=[SL.b])
                        kb.op(POOL, lambda: G.iota(th128.t[:], pattern=[[128, NT]], base=0, channel_multiplier=0,
                                                   allow_small_or_imprecise_dtypes=True), writes=[th128.b])
                        kb.op(POOL, lambda: G.iota(pidx.t[:], pattern=[[0, 1]], base=0, channel_multiplier=1,
                                                   allow_small_or_imprecise_dtypes=True), writes=[pidx.b])
                        SLb = SL.t[:].unsqueeze(1).broadcast_to([128, NT, 8])
                        THb = th128.t[:].unsqueeze(2).broadcast_to([128, NT, 8])
                        kb.op(DVE, lambda: V.scalar_tensor_tensor(out=QALf.t[:, :, :, 0], in0=THb, scalar=-1.0, in1=SLb,
                                                                  op0=ALU.mult, op1=ALU.mult),
                              reads=[SL.b, th128.b], writes=[QALf.b])
                        kb.op(DVE, lambda: V.tensor_scalar(out=QALf.t[:, :, :, 1], in0=SLb, scalar1=pidx.t[:, 0:1], scalar2=-1.0,
                                                           op0=ALU.mult, op1=ALU.mult),
                              reads=[SL.b, pidx.b], writes=[QALf.b])
                        kb.op(DVE, lambda: V.tensor_copy(out=QALf.t[:, :, :, 2], in_=SLb), reads=[SL.b], writes=[QALf.b])
                        kb.op(DVE, lambda: V.tensor_copy(out=QALf.t[:, :, :, 3], in_=SLb), reads=[SL.b], writes=[QALf.b])
                        kb.op(DVE, lambda: V.tensor_copy(out=QAL.t[:], in_=QALf.t[:]), reads=[QALf.b], writes=[QAL.b])
                        kb.op(POOL, lambda: G.memset(KALf.t[:, :, 0:2], 1.0), writes=[KALf.b])
                        kb.op(DVE, lambda: V.tensor_copy(out=KALf.t[:, :, 2], in_=th128.t[:]), reads=[th128.b], writes=[KALf.b])
                        kb.op(DVE, lambda: V.tensor_copy(out=KALf.t[:, :, 3], in_=pidx.t[:, 0:1].broadcast_to([128, NT])),
                              reads=[pidx.b], writes=[KALf.b])
                        kb.op(DVE, lambda: V.tensor_copy(out=KAL.t[:], in_=KALf.t[:]), reads=[KALf.b], writes=[KAL.b])
                        kb.op(POOL, lambda: G.memset(KCALf.t[:, 0:2], 1.0), writes=[KCALf.b])
                        kb.op(POOL, lambda: G.memset(KCALf.t[:, 3:4], 31.0), reads=[], writes=[KCALf.b])
                        kb.op(DVE, lambda: V.tensor_scalar(out=KCALf.t[:, 2:3], in0=pidx.t[:, 0:1], scalar1=16.0, scalar2=None,
                                                           op0=ALU.mult), reads=[pidx.b], writes=[KCALf.b])
                        kb.op(DVE, lambda: V.tensor_copy(out=KCAL.t[:], in_=KCALf.t[:]), reads=[KCALf.b], writes=[KCAL.b])
                        kb.op(POOL, lambda: G.memset(OH.t[:], 0.0), writes=[OH.b])
                        for kt in range(NT):
                            kb.op(POOL, lambda kt=kt: G.memset(OH.t[0:64, kt, 2 * kt:2 * kt + 1], 1.0), writes=[OH.b])
                            kb.op(POOL, lambda kt=kt: G.memset(OH.t[64:128, kt, 2 * kt + 1:2 * kt + 2], 1.0), writes=[OH.b])
                        kb.op(POOL, lambda: G.memset(ones_b.t[:], 1.0), writes=[ones_b.b])
                        kb.op(POOL, lambda: G.memset(zeros_b.t[:], 0.0), writes=[zeros_b.b])
                        ob8 = ones_b2.t[:].rearrange("p (a b) -> p a b", b=128)
                        kb.op(POOL, lambda: G.memset(ones_b2.t[:], 1.0), writes=[ones_b2.b])
                        kb.op(POOL, lambda: G.affine_select(out=dmask8.t[:], in_=ob8, pattern=[[0, 8], [1, 128]],
                                                            compare_op=ALU.is_ge, fill=0.0, base=0, channel_multiplier=-1),
                              reads=[ones_b2.b], writes=[dmask8.b])
                        kb.op(POOL, lambda: G.affine_select(out=tmask8.t[:], in_=ob8, pattern=[[0, 8], [-1, 128]],
                                                            compare_op=ALU.is_gt, fill=0.0, base=0, channel_multiplier=1),
                              reads=[ones_b2.b], writes=[tmask8.b])
                        for i in range(4):
                            kb.op(POOL, lambda i=i: G.affine_select(out=cmask.t[:, i * 512:(i + 1) * 512], in_=zeros_b.t[:],
                                                                    pattern=[[1, 512]], compare_op=ALU.is_ge, fill=NEG,
                                                                    base=-31 + 512 * i, channel_multiplier=-16),
                                  reads=[zeros_b.b], writes=[cmask.b])
                        kb.op(POOL, lambda: G.iota(rel.t[:], pattern=[[-2, NT], [1, 32]], base=0, channel_multiplier=0,
                                                   allow_small_or_imprecise_dtypes=True), writes=[rel.b])
                        kb.op(DVE, lambda: V.tensor_scalar(out=hp.t[:], in0=pidx.t[:], scalar1=64.0, scalar2=None, op0=ALU.is_ge),
                              reads=[pidx.b], writes=[hp.b])
                        kb.op(DVE, lambda: V.tensor_scalar(out=rel.t[:], in0=rel.t[:], scalar1=hp.t[:, 0:1], scalar2=None,
                                                           op0=ALU.subtract), reads=[rel.b, hp.b], writes=[rel.b])
                        kb.op(DVE, lambda: V.tensor_scalar(out=t1.t[:], in0=rel.t[:], scalar1=0.0, scalar2=-1e9,
                                                           op0=ALU.is_gt, op1=ALU.mult), reads=[rel.b], writes=[t1.b])
                        kb.op(DVE, lambda: V.tensor_scalar(out=f0.t[:], in0=rel.t[:], scalar1=0.0, scalar2=None, op0=ALU.is_equal),
                              reads=[rel.b], writes=[f0.b])
                        kb.op(DVE, lambda: V.tensor_scalar(out=f1.t[:], in0=rel.t[:], scalar1=-1.0, scalar2=None, op0=ALU.is_equal),
                              reads=[rel.b], writes=[f1.b])
                        kb.op(DVE, lambda: V.tensor_tensor(out=f0.t[:], in0=f0.t[:], in1=f1.t[:], op=ALU.max),
                              reads=[f0.b, f1.b], writes=[f0.b])
                        kb.op(DVE, lambda: V.memset(f0.t[:, :, 0:1], 1.0), reads=[], writes=[f0.b])
                        kb.op(DVE, lambda: V.scalar_tensor_tensor(out=addc.t[:], in0=f0.t[:], scalar=1e4, in1=t1.t[:],
                                                                  op0=ALU.mult, op1=ALU.add), reads=[f0.b, t1.b], writes=[addc.b])
                        kb.op(POOL, lambda: G.iota(ovf.t[:], pattern=[[-64, 32]], base=0, channel_multiplier=16,
                                                   allow_small_or_imprecise_dtypes=True), writes=[ovf.b])
                        kb.op(DVE, lambda: V.tensor_scalar(out=ova.t[:], in0=ovf.t[:], scalar1=63.0, scalar2=None, op0=ALU.is_le),
                              reads=[ovf.b], writes=[ova.b])
                        kb.op(DVE, lambda: V.tensor_scalar(out=ovf.t[:], in0=ovf.t[:], scalar1=-31.0, scalar2=None, op0=ALU.is_ge),
                              reads=[ovf.b], writes=[ovf.b])
                        kb.op(DVE, lambda: V.tensor_tensor(out=ov.t[:], in0=ova.t[:], in1=ovf.t[:], op=ALU.mult),
                              reads=[ova.b, ovf.b], writes=[ov.b])
                        kb.op(POOL, lambda: G.memset(V_slc.t[:, :, :, 64:65], 1.0), writes=V_slc.bs)
                        kb.op(POOL, lambda: G.memset(V_win.t[:, :, :, 64:65], 1.0), writes=V_win.bs)
                        kb.barrier()
                        chk(1)

                    with ExitStack() as s2:
                        cmpT = sb(s2, "cmpT", [128, 2, S], BF16, NT)
                        with ExitStack() as s2a:
                            xt = [sb(s2a, f"xta{i}", [128, D], F32) for i in range(2)]
                            g1bc = sb(s2a, "g1bc", [128, D], F32)
                            kb.dma(SP, g1bc.t[:], norm1_g[0:1, :].broadcast_to([128, D]), writes=[g1bc.b])

                            def p1a_gen(c):
                                xx = xt[c % 2]
                                kb.dma(SP, xx.t[:], x[c * 128:(c + 1) * 128, :], writes=[xx.b])
                                yield
                                yield from norm_gen(xx, c, g1bc, c % 2, 6 + (c % 2))

                            wkv = sb(s2a, "wkv", [128, KC, 768], BF16)
                            load_w(wkv, w_in_v[:, :, C_KV:C_KV + 768])
                            cmp_tok = [sb(s2a, f"cmp_tok{i}", [128, 256], BF16) for i in range(2)]
                            sqk = [sb(s2a, f"sqk{i}", [128, 256], F32) for i in range(2)]
                            kst = sb(s2a, "kst", [128, NT, 16], F32, NT)
                            tmpk = [sb(s2a, f"tmpk{i}", [128, 4, 64], F32) for i in range(2)]
                            ka_slc = [sb(s2a, f"ka_slc{i}", [128, 2, 128], BF16) for i in range(2)]
                            ka_win = [sb(s2a, f"ka_win{i}", [128, 2, 128], BF16) for i in range(2)]
                            for i in range(2):
                                kb.op(POOL, lambda i=i: G.memset(ka_slc[i].t[:], 0.0), writes=[ka_slc[i].b])
                                kb.op(POOL, lambda i=i: G.memset(ka_win[i].t[:], 0.0), writes=[ka_win[i].b])
                            def p1b_gen(c):
                                i2 = c % 2
                                bA, bB = (0, 1) if i2 == 0 else (2, 3)
                                tok = slice(c * 128, (c + 1) * 128)
                                if CUT >= 1:
                                    yield
                                    for k in range(KC):
                                        kb.op(PE, lambda k=k: TE.matmul(pf(bA), lhsT=hT.t[:, k, tok], rhs=wkv.t[:, k, 0:512],
                                                                        start=(k == 0), stop=(k == KC - 1)),
                                              reads=[hT.bs[c], wkv.b], writes=[PS[bA].b], inc=(k == KC - 1))
                                    for k in range(KC):
                                        kb.op(PE, lambda k=k: TE.matmul(pf(bB)[:, 0:256], lhsT=hT.t[:, k, tok], rhs=wkv.t[:, k, 512:768],
                                                                        start=(k == 0), stop=(k == KC - 1)),
                                              reads=[hT.bs[c], wkv.b], writes=[PS[bB].b], inc=(k == KC - 1))
                                if CUT >= 2:
                                    yield
                                    ct = cmp_tok[i2]
                                    kb.op(ACT, lambda: A.copy(out=ct.t[:], in_=pf(bA)[:, 0:256]), reads=[PS[bA].b], writes=[ct.b])
                                    sq = sqk[i2]
                                    kb.op(ACT, lambda: A.activation(out=sq.t[:, 0:128], in_=pf(bA)[:, 256:384], func=AF.Square),
                                          reads=[PS[bA].b], writes=[sq.b])
                                    kb.op(ACT, lambda: A.activation(out=sq.t[:, 128:256], in_=pf(bB)[:, 0:128], func=AF.Square),
                                          reads=[PS[bB].b], writes=[sq.b])
                                if CUT >= 3:
                                    yield
                                    ks = kst.t
                                    ksb = [kst.bs[c]]
                                    kb.op(DVE, lambda: V.tensor_reduce(out=ks[:, c, 0:4], in_=sq.t[:].rearrange("p (a b) -> p a b", b=64),
                                                                       axis=AX.X, op=ALU.add), reads=[sq.b], writes=ksb)
                                    rstd_from_ss(ks[:, c, 0:4], ks[:, c, 4:8], ks[:, c, 8:12], ks[:, c, 12:16], 64, ksb)
                                    tk = tmpk[i2]
                                    kb.op(DVE, lambda: V.tensor_tensor(out=tk.t[:, 0:2, :], in0=pf(bA)[:, 256:384].rearrange("p (a b) -> p a b", b=64),
                                                                       in1=ks[:, c, 12:14].unsqueeze(2).broadcast_to([128, 2, 64]), op=ALU.mult),
                                          reads=[PS[bA].b] + ksb, writes=[tk.b])
                                    kb.op(DVE, lambda: V.tensor_tensor(out=tk.t[:, 2:4, :], in0=pf(bB)[:, 0:128].rearrange("p (a b) -> p a b", b=64),
                                                                       in1=ks[:, c, 14:16].unsqueeze(2).broadcast_to([128, 2, 64]), op=ALU.mult),
                                          reads=[PS[bB].b] + ksb, writes=[tk.b])
                                    ksl, kwn = ka_slc[i2], ka_win[i2]
                                    kb.op(DVE, lambda: V.tensor_tensor(out=ksl.t[:, :, 0:64], in0=tk.t[:, 0:2, :],
                                                                       in1=gk.t[:, 1:2, :].broadcast_to([128, 2, 64]), op=ALU.mult),
                                          reads=[tk.b, gk.b], writes=[ksl.b])
                                    kb.op(DVE, lambda: V.tensor_tensor(out=kwn.t[:, :, 0:64], in0=tk.t[:, 2:4, :],
                                                                       in1=gk.t[:, 2:3, :].broadcast_to([128, 2, 64]), op=ALU.mult),
                                          reads=[tk.b, gk.b], writes=[kwn.b])
                                if CUT >= 4:
                                    yield
                                    kb.op(POOL, lambda: G.tensor_copy(out=ksl.t[:, :, 64:96], in_=OH.t[:, c:c + 1, :].broadcast_to([128, 2, 32])),
                                          reads=[OH.b], writes=[ksl.b])
                                    kb.op(POOL, lambda: G.tensor_copy(out=ksl.t[:, :, 96:100], in_=KAL.t[:, c:c + 1, :].broadcast_to([128, 2, 4])),
                                          reads=[KAL.b], writes=[ksl.b])
                                    kb.op(POOL, lambda: G.tensor_copy(out=kwn.t[:, :, 96:100], in_=KAL.t[:, c:c + 1, :].broadcast_to([128, 2, 4])),
                                          reads=[KAL.b], writes=[kwn.b])
                                if CUT >= 5:
                                    yield
                                    kb.op(ACT, lambda: A.copy(out=V_slc.t[:, c, :, 0:64], in_=pf(bA)[:, 384:512].rearrange("p (a b) -> p a b", b=64)),
                                          reads=[PS[bA].b], writes=[V_slc.bs[c]])
                                    kb.op(ACT, lambda: A.copy(out=V_win.t[:, c, :, 0:64], in_=pf(bB)[:, 128:256].rearrange("p (a b) -> p a b", b=64)),
                                          reads=[PS[bB].b], writes=[V_win.bs[c]])
                                if CUT >= 6:
                                    yield
                                    tb = 4 + i2
                                    transposes([ksl.t[:, 0, :], ksl.t[:, 1, :], kwn.t[:, 0, :], kwn.t[:, 1, :], ct.t[:, 0:128], ct.t[:, 128:256]],
                                               tb, [ksl.b, kwn.b, ct.b])
                                    pv3 = pb(tb).rearrange("p (a b) -> p a b", b=128)
                                    kb.op(ACT, lambda: A.copy(out=KT_slc.t[:, :, tok], in_=pv3[:, 0:2, :]), reads=[PS[tb].b], writes=[KT_slc.bs[c]])
                                    kb.op(ACT, lambda: A.copy(out=KT_win.t[:, :, tok], in_=pv3[:, 2:4, :]), reads=[PS[tb].b], writes=[KT_win.bs[c]])
                                    kb.op(ACT, lambda: A.copy(out=cmpT.t[:, :, tok], in_=pv3[:, 4:6, :]), reads=[PS[tb].b], writes=[cmpT.bs[c]])
                            def p1ab_gen(c):
                                yield from p1a_gen(c)
                                yield from p1b_gen(c)

                            run_interleaved([(lambda c=c: p1ab_gen(c)) for c in range(NT)], width=2)
                            if "KT_slc" in dbg:
                                kdb = sb(s2a, "kdb", [128, 2, S], F32)
                                kb.op(DVE, lambda: V.tensor_copy(out=kdb.t[:], in_=KT_slc.t[:]), reads=KT_slc.bs, writes=[kdb.b])
                                kb.dma(SP, dbg["KT_slc"].rearrange("p (a b) -> p a b", b=S), kdb.t[:], reads=[kdb.b], is_out=True)
                            kb.barrier()
                            chk(3)

                        with ExitStack() as s2b:
                            w1sb = sb(s2b, "w1sb", [128, 2, 32, 128], BF16)
                            w2sb = sb(s2b, "w2sb", [128, 2, 64], BF16)
                            pe_sb = sb(s2b, "pe_sb", [32, 2, 64], F32)
                            peT = sb(s2b, "peT", [64, 2, 32], BF16)
                            bias_c = sb(s2b, "bias_c", [128, 2], F32)
                            xhs = [sb(s2b, f"xh{i}", [128, 128], F32) for i in range(4)]
                            x2s = [sb(s2b, f"x2{i}", [128, 128], F32) for i in range(4)]
                            sgs_ = [sb(s2b, f"sgc{i}", [128, 128], F32) for i in range(4)]
                            HTbs = [sb(s2b, f"HTb{i}", [128, 128], BF16) for i in range(4)]
                            kca = sb(s2b, "kca", [128, 2, 128], BF16)
                            cst = sb(s2b, "cst", [128, 8], F32)
                            tmpcs = [sb(s2b, f"tmpc{i}", [128, 64], F32) for i in range(4)]
                            for kv in range(2):
                                src = cmp_w1[0, kv].rearrange("l d f -> d l f")
                                kb.dma(POOL, w1sb.t[0:64, kv], src, writes=[w1sb.b])
                                kb.dma(POOL, w1sb.t[64:128, kv], src, writes=[w1sb.b])
                            kb.dma(POOL, w2sb.t[:], cmp_w2[0].rearrange("k f d -> f k d"), writes=[w2sb.b])
                            kb.dma(SP, pe_sb.t[:], cmp_pe[0].rearrange("k l d -> l k d"), writes=[pe_sb.b])
                            kb.op(POOL, lambda: G.memset(kca.t[:], 0.0), writes=[kca.b])
                            kb.op(POOL, lambda: G.memset(Vc.t[:], 0.0), writes=[Vc.b])
                            kb.op(POOL, lambda: G.tensor_copy(out=kca.t[:, :, 96:100], in_=KCAL.t[:].unsqueeze(1).broadcast_to([128, 2, 4])),
                                  reads=[KCAL.b], writes=[kca.b])
                            kb.op(POOL, lambda: G.memset(Vc.t[:, :, 64:65], 1.0), writes=[Vc.b])
                            kb.op(POOL, lambda: G.tensor_copy(out=Vc.t[:, :, 65:97], in_=ov.t[:].unsqueeze(1).broadcast_to([128, 2, 32])),
                                  reads=[ov.b], writes=[Vc.b])
                            for kv in range(2):
                                kb.op(PE, lambda kv=kv: TE.transpose(out=pf(0)[0:64, kv * 32:(kv + 1) * 32], in_=pe_sb.t[0:32, kv, :],
                                                                     identity=ident_f.t[0:32, 0:32]),
                                      reads=[pe_sb.b, ident_f.b], writes=[PS[0].b])
                            kb.op(DVE, lambda: V.tensor_copy(out=peT.t[:], in_=pf(0)[0:64, 0:64].rearrange("p (a b) -> p a b", b=32)),
                                  reads=[PS[0].b], writes=[peT.b])
                            for kv in range(2):
                                for l in range(32):
                                    kb.op(PE, lambda kv=kv, l=l: TE.matmul(pf(1)[:, kv:kv + 1], lhsT=w1sb.t[0:64, kv, l, :], rhs=peT.t[0:64, kv, l:l + 1],
                                                                           start=(l == 0), stop=(l == 31)),
                                          reads=[w1sb.b, peT.b], writes=[PS[1].b], inc=(l == 31))
                            kb.op(DVE, lambda: V.tensor_copy(out=bias_c.t[:], in_=pf(1)[:, 0:2]), reads=[PS[1].b], writes=[bias_c.b])
                            def cmp_gen(kv, g, idx):
                                bH, bO = 2 * idx, 2 * idx + 1
                                xh, x2, sg, HTb, tmpc = xhs[idx], x2s[idx], sgs_[idx], HTbs[idx], tmpcs[idx]
                                for l in range(32):
                                    kb.op(PE, lambda kv=kv, g=g, l=l: TE.matmul(
                                        pf(bH)[:, 0:127], lhsT=w1sb.t[g * 64:(g + 1) * 64, kv, l, :],
                                        rhs=cmpT.t[g * 64:(g + 1) * 64, kv, l:l + 16 * 126 + 1:16],
                                        start=(l == 0), stop=(l == 31)),
                                        reads=[w1sb.b] + cmpT.bs, writes=[PS[bH].b], inc=(l == 31))
                                yield
                                kb.op(ACT, lambda kv=kv: A.activation(out=xh.t[:, 0:127], in_=pf(bH)[:, 0:127], func=AF.Identity,
                                                                      bias=bias_c.t[:, kv:kv + 1], scale=1.0),
                                      reads=[PS[bH].b, bias_c.b], writes=[xh.b])
                                yield
                                kb.op(DVE, lambda: V.tensor_tensor(out=x2.t[:, 0:127], in0=xh.t[:, 0:127], in1=xh.t[:, 0:127], op=ALU.mult),
                                      reads=[xh.b], writes=[x2.b])
                                yield
                                kb.op(DVE, lambda: V.tensor_scalar(out=x2.t[:, 0:127], in0=x2.t[:, 0:127], scalar1=0.044715, scalar2=1.0,
                                                                   op0=ALU.mult, op1=ALU.add), reads=[x2.b], writes=[x2.b])
                                yield
                                kb.op(DVE, lambda: V.tensor_tensor(out=x2.t[:, 0:127], in0=x2.t[:, 0:127], in1=xh.t[:, 0:127], op=ALU.mult),
                                      reads=[x2.b, xh.b], writes=[x2.b])
                                yield
                                kb.op(ACT, lambda: A.activation(out=sg.t[:, 0:127], in_=x2.t[:, 0:127], func=AF.Sigmoid, scale=1.5957691216057308),
                                      reads=[x2.b], writes=[sg.b])
                                yield
                                kb.op(DVE, lambda: V.tensor_tensor(out=HTb.t[:, 0:127], in0=xh.t[:, 0:127], in1=sg.t[:, 0:127], op=ALU.mult),
                                      reads=[xh.b, sg.b], writes=[HTb.b])
                                yield
                                kb.op(PE, lambda kv=kv: TE.matmul(pf(bO)[0:127, 0:64], lhsT=HTb.t[:, 0:127], rhs=w2sb.t[:, kv, :], start=True, stop=True),
                                      reads=[HTb.b, w2sb.b], writes=[PS[bO].b])
                                yield
                                if kv == 0:
                                    kb.op(ACT, lambda g=g: A.activation(out=tmpc.t[0:127, :], in_=pf(bO)[0:127, 0:64], func=AF.Square,
                                                                        accum_out=cst.t[0:127, g:g + 1]),
                                          reads=[PS[bO].b], writes=[tmpc.b, cst.b])
                                    rstd_from_ss(cst.t[0:127, g:g + 1], cst.t[0:127, 2 + g:3 + g], cst.t[0:127, 4 + g:5 + g], cst.t[0:127, 6 + g:7 + g], 64, [cst.b])
                                    kb.op(DVE, lambda g=g: V.scalar_tensor_tensor(out=kca.t[0:127, g, 0:64], in0=pf(bO)[0:127, 0:64],
                                                                                  scalar=cst.t[0:127, 6 + g:7 + g], in1=gk.t[0:127, 0, :],
                                                                                  op0=ALU.mult, op1=ALU.mult),
                                          reads=[PS[bO].b, cst.b, gk.b], writes=[kca.b])
                                else:
                                    kb.op(ACT, lambda g=g: A.copy(out=Vc.t[0:127, g, 0:64], in_=pf(bO)[0:127, 0:64]),
                                          reads=[PS[bO].b], writes=[Vc.b])
                                yield

                            run_interleaved([(lambda kv=kv, g=g: cmp_gen(kv, g, 2 * kv + g)) for kv in range(2) for g in range(2)], width=4)
                            transposes([kca.t[:, 0, :], kca.t[:, 1, :]], 6, [kca.b])
                            kb.op(ACT, lambda: A.copy(out=KcT.t[:], in_=pb(6)[:, 0:256].rearrange("p (a b) -> p a b", b=128)),
                                  reads=[PS[6].b], writes=[KcT.b])
                            if "kc" in dbg:
                                kcd = sb(s2b, "kcd", [128, 2, 64], F32)
                                kb.op(DVE, lambda: V.tensor_copy(out=kcd.t[:], in_=kca.t[:, :, 0:64]), reads=[kca.b], writes=[kcd.b])
                                kb.dma(SP, dbg["kc"].rearrange("p (a b) -> p a b", b=64), kcd.t[:], reads=[kcd.b], is_out=True)
                            if "vc" in dbg:
                                vcd = sb(s2b, "vcd", [128, 2, 64], F32)
                                kb.op(DVE, lambda: V.tensor_copy(out=vcd.t[:], in_=Vc.t[:, :, 0:64]), reads=[Vc.b], writes=[vcd.b])
                                kb.dma(SP, dbg["vc"].rearrange("p (a b) -> p a b", b=64), vcd.t[:], reads=[vcd.b], is_out=True)
                            kb.barrier()
                            chk(4)

                    phase_1c()
                    kb.barrier()
                    chk(5)

                wm = sb(sB, "wm", [128, KC, 2048], BF16, 4)
                wbr = sb(sB, "wbr", [128, 8, D], BF16)
                phase_1d()
                kb.barrier()
                chk(6)
                phase_1e()
                kb.barrier()
                chk(7)

            phase_2()
            kb.finish()
    except _Stop:
        pass
    return nc


_NAMES = ["x", "norm1_g", "w_in", "nsa_q_norm", "nsa_k_norm", "cmp_pe", "cmp_w1", "cmp_w2", "ret_gn_g",
          "w_branch", "w_out", "norm2_g", "ffn_w_gate", "ffn_w_up", "ffn_w_down"]


def kernel(**inputs):
    n = 8
    arrs = {k: np.ascontiguousarray(np.asarray(inputs[k], dtype=np.float32)) for k in _NAMES}
    nc = build_nc()
    in_maps = []
    for i in range(n):
        m = {k: arrs[k] for k in _NAMES if k != "x"}
        m["x"] = np.ascontiguousarray(arrs["x"][i])
        in_maps.append(m)
    res = run_bass_kernel_spmd(nc, in_maps, core_ids=list(range(n)))
    return np.stack([np.asarray(r["out"], dtype=np.float32) for r in res.results], axis=0)
```

```python
import numpy as np
from contextlib import ExitStack
import concourse.bass as bass
import concourse.mybir as mybir
from concourse.bass_utils import run_bass_kernel_spmd

F32 = mybir.dt.float32
BF16 = mybir.dt.bfloat16
AF = mybir.ActivationFunctionType
ALU = mybir.AluOpType
AX = mybir.AxisListType

S = 2048
D = 1024
NT = 16
KC = 8
N_IN = 5400
DFF = 2816
NFB = 22
EPS = 1e-6
SEM_LIMIT = 24000
import os as _os
CUT = int(_os.environ.get('P1B_CUT', '99'))
CUTC = int(_os.environ.get('P1C_CUT', '99'))
SUBC = int(_os.environ.get('P1C_SUB', '99'))
WIDTH = int(_os.environ.get('P1C_WIDTH', '2'))
NEG = -30000.0

C_Q = 0
C_KV = 512
C_G = 1280
C_R = 1304
C_M = 3352


class Buf:
    __slots__ = ("name", "w", "r", "excl")

    def __init__(self, name):
        self.name = name
        self.w = None
        self.r = []
        self.excl = False


class SemW:
    __slots__ = ("h",)

    def __init__(self, h):
        self.h = h


class Slot:
    __slots__ = ("sem", "val")

    def __init__(self, sem):
        self.sem = sem
        self.val = 0


class Q:
    def __init__(self, name, eng):
        self.name = name
        self.eng = eng
        self.sem = None
        self.count = 0
        self.waited = {}
        self.ring = []
        self.ri = 0
        self.pending = False


class T:
    def __init__(self, t, name, nb=1):
        self.t = t
        self.bs = [Buf(f"{name}{i}") for i in range(nb)]

    @property
    def b(self):
        return self.bs[0]


class KB:
    def __init__(self, nc, es):
        self.nc = nc
        self.es = es
        self.nsem = 0
        self.pe = self.mkq("pe", nc.tensor)
        self.act = self.mkq("act", nc.scalar)
        self.dve = self.mkq("dve", nc.vector)
        self.pool = self.mkq("pool", nc.gpsimd)
        self.sp = self.mkq("sp", nc.sync)
        self.qs = [self.pe, self.act, self.dve, self.pool, self.sp]
        for q, n in ((self.sp, 16), (self.pool, 8), (self.act, 4)):
            q.ring = [Slot(self.new_sem(f"{q.name}_d{i}")) for i in range(n)]
        self.out_toks = []

    def new_sem(self, name):
        self.nsem += 1
        return SemW(self.es.enter_context(self.nc.semaphore(f"{name}_{self.nsem}")))

    def mkq(self, name, eng):
        q = Q(name, eng)
        q.sem = self.new_sem(name)
        return q

    def wait(self, q, tok):
        sw, val = tok[0], tok[1]
        if q.waited.get(sw, 0) >= val:
            return
        q.eng.wait_ge(sw.h, val)
        q.waited[sw] = val

    def _dep(self, q, tok, raw, force=False):
        if tok[2] is q and q is self.pe and not force:
            return
        self.wait(q, tok)

    def _deps(self, q, reads, writes, force=False):
        for b in reads:
            if b.w is not None:
                self._dep(q, b.w, True, force)
            if b.excl:
                for t in b.r:
                    if t[2] is not q:
                        self._dep(q, t, False, force)
        for b in writes:
            if b.w is not None:
                self._dep(q, b.w, False, force)
            for t in b.r:
                self._dep(q, t, False, force)

    def _record(self, tok, reads, writes):
        for b in reads:
            if tok[2] is not None:
                b.r = [t for t in b.r if t[2] is not tok[2]]
            b.r.append(tok)
        for b in writes:
            b.w = tok
            b.r = []

    def op(self, q, fn, reads=(), writes=(), inc=True):
        self._deps(q, reads, writes)
        ins = fn()
        if inc:
            if q.count >= SEM_LIMIT and not q.pending:
                q.sem = self.new_sem(q.name)
                q.count = 0
            ins.then_inc(q.sem.h, 1)
            q.count += 1
            q.pending = False
            tok = (q.sem, q.count, q)
        else:
            q.pending = True
            tok = (q.sem, q.count + 1, q)
        self._record(tok, reads, writes)
        return ins

    def dma(self, q, out, in_, reads=(), writes=(), is_out=False):
        self._deps(q, reads, writes, force=True)
        slot = q.ring[q.ri % len(q.ring)]
        q.ri += 1
        if slot.val > 0:
            self.wait(q, (slot.sem, slot.val))
        if slot.val >= SEM_LIMIT:
            slot.sem = self.new_sem(q.name + "_d")
            slot.val = 0
        ins = q.eng.dma_start(out=out, in_=in_)
        ins.then_inc(slot.sem.h, 16)
        slot.val += 16
        tok = (slot.sem, slot.val, None)
        self._record(tok, reads, writes)
        if is_out:
            self.out_toks.append(tok)
        return tok

    def barrier(self):
        toks = []
        for o in self.qs:
            if o.count > 0:
                toks.append((o.sem, o.count, o))
            for sl in o.ring:
                if sl.val > 0:
                    toks.append((sl.sem, sl.val, None))
        for q in self.qs:
            for t in toks:
                if t[2] is q:
                    continue
                self.wait(q, t)

    def finish(self):
        for t in self.out_toks:
            self.wait(self.sp, t)


class _Stop(Exception):
    pass


def build_nc(debug=None, stop=None):
    nc = bass.Bass("TRN2", target_bir_lowering=False)

    def din(name, shape):
        return nc.dram_tensor(name, list(shape), F32, kind="ExternalInput").ap()

    x = din("x", [S, D])
    norm1_g = din("norm1_g", [1, D])
    w_in = din("w_in", [1, D, N_IN])
    nsa_q_norm = din("nsa_q_norm", [1, 64])
    nsa_k_norm = din("nsa_k_norm", [1, 3, 64])
    cmp_pe = din("cmp_pe", [1, 2, 32, 64])
    cmp_w1 = din("cmp_w1", [1, 2, 32, 64, 128])
    cmp_w2 = din("cmp_w2", [1, 2, 128, 64])
    ret_gn_g = din("ret_gn_g", [1, 4, 128])
    w_branch = din("w_branch", [1, 2, 512, D])
    w_out = din("w_out", [1, D, D])
    norm2_g = din("norm2_g", [1, D])
    ffn_w_gate = din("ffn_w_gate", [1, D, DFF])
    ffn_w_up = din("ffn_w_up", [1, D, DFF])
    ffn_w_down = din("ffn_w_down", [1, DFF, D])
    out = nc.dram_tensor("out", [S, D], F32, kind="ExternalOutput").ap()
    xmid = nc.dram_tensor("xmid", [S, D], F32, kind="Internal").ap()
    dbg = {}
    if debug:
        for name, shape in debug.items():
            dbg[name] = nc.dram_tensor("dbg_" + name, list(shape), F32, kind="ExternalOutput").ap()

    w_in_v = w_in[0].rearrange("(k p) n -> p k n", p=128)

    try:
        with ExitStack() as es:
            kb = KB(nc, es)

            def chk(n):
                if stop is not None and n >= stop:
                    kb.barrier()
                    kb.finish()
                    raise _Stop()
            PE, ACT, DVE, POOL, SP = kb.pe, kb.act, kb.dve, kb.pool, kb.sp
            V, A, G, TE = nc.vector, nc.scalar, nc.gpsimd, nc.tensor

            def sb(scope, name, shape, dt, nb=1):
                return T(scope.enter_context(nc.sbuf_tensor(name, list(shape), dt)), name, nb)

            PS2 = [es.enter_context(nc.psum_tensor(f"psp{j}", [128, 1024], F32)) for j in range(4)]
            PS = [T(None, f"ps{i}") for i in range(8)]
            for p_ in PS:
                p_.b.excl = True

            def pf(i):
                return PS2[i // 2][:, (i % 2) * 512:(i % 2 + 1) * 512]

            def pb(i):
                return PS2[i // 2][:].bitcast(BF16)[:, (i % 2) * 1024:(i % 2 + 1) * 1024]

            def pf2(j):
                return PS2[j][:]

            ident_f = sb(es, "ident_f", [128, 128], F32)
            ident_b = sb(es, "ident_b", [128, 128], BF16)
            ones_f = sb(es, "ones_f", [128, 128], F32)
            nhalf = sb(es, "nhalf", [128, 16], F32)
            hT = sb(es, "hT", [128, KC, S], BF16, NT)
            stat = sb(es, "stat", [128, NT, 4], F32, NT)
            hb = [sb(es, f"hb{i}", [128, D], BF16) for i in range(2)]
            junk = sb(es, "junk", [128, D], BF16)

            kb.op(POOL, lambda: G.memset(ones_f.t[:], 1.0), writes=[ones_f.b])
            kb.op(POOL, lambda: G.memset(nhalf.t[:], -0.5), writes=[nhalf.b])
            kb.op(POOL, lambda: G.affine_select(out=ident_f.t[:], in_=ones_f.t[:, 0:128], pattern=[[1, 128]],
                                                compare_op=ALU.is_equal, fill=0.0, base=0, channel_multiplier=-1),
                  reads=[ones_f.b], writes=[ident_f.b])
            kb.op(DVE, lambda: V.tensor_copy(out=ident_b.t[:], in_=ident_f.t[:]), reads=[ident_f.b], writes=[ident_b.b])

            def rstd_from_ss(ss_ap, ms_ap, sd_ap, rs_ap, n, bufs):
                k = ms_ap.shape[-1]
                P_ = ms_ap.shape[0]
                kb.op(DVE, lambda: V.tensor_scalar(out=ms_ap, in0=ss_ap, scalar1=1.0 / n, scalar2=EPS,
                                                   op0=ALU.mult, op1=ALU.add), reads=bufs, writes=bufs)
                kb.op(POOL, lambda: G.tensor_tensor(out=rs_ap, in0=ms_ap, in1=nhalf.t[0:P_, 0:k], op=ALU.pow),
                      reads=list(bufs) + [nhalf.b], writes=bufs)

            def transposes(src_aps, bank, reads):
                pbv = pb(bank)
                n = len(src_aps)
                for i, ap in enumerate(src_aps):
                    kb.op(PE, lambda ap=ap, i=i: TE.transpose(out=pbv[:, i * 128:(i + 1) * 128], in_=ap, identity=ident_b.t[:]),
                          reads=list(reads) + [ident_b.b], writes=[PS[bank].b], inc=(i == n - 1))

            def norm_gen(src, c, gbc, sidx, bank):
                sbuf_ = [stat.bs[c]]
                st = stat.t
                kb.op(ACT, lambda: A.activation(out=junk.t[:], in_=src.t[:], func=AF.Square, accum_out=st[:, c, 0:1]),
                      reads=[src.b], writes=[junk.b] + sbuf_)
                yield
                kb.op(DVE, lambda: V.tensor_scalar(out=st[:, c, 1:2], in0=st[:, c, 0:1], scalar1=1.0 / D, scalar2=EPS,
                                                   op0=ALU.mult, op1=ALU.add), reads=sbuf_, writes=sbuf_)
                yield
                kb.op(POOL, lambda: G.tensor_tensor(out=st[:, c, 3:4], in0=st[:, c, 1:2], in1=nhalf.t[:, 0:1], op=ALU.pow),
                      reads=sbuf_ + [nhalf.b], writes=sbuf_)
                yield
                h = hb[sidx % 2]
                kb.op(DVE, lambda: V.scalar_tensor_tensor(out=h.t[:], in0=src.t[:], scalar=st[:, c, 3:4], in1=gbc.t[:],
                                                          op0=ALU.mult, op1=ALU.mult),
                      reads=[src.b, gbc.b] + sbuf_, writes=[h.b])
                yield
                transposes([h.t[:, k * 128:(k + 1) * 128] for k in range(KC)], bank, [h.b])
                yield
                kb.op(ACT, lambda: A.copy(out=hT.t[:, :, c * 128:(c + 1) * 128],
                                          in_=pb(bank).rearrange("p (a b) -> p a b", b=128)),
                      reads=[PS[bank].b], writes=[hT.bs[c]])
                yield

            def run_interleaved(gen_fns, width=2):
                pending = list(gen_fns)
                active = []
                while pending or active:
                    while pending and len(active) < width:
                        active.append(pending.pop(0)())
                    for g_ in list(active):
                        try:
                            next(g_)
                        except StopIteration:
                            active.remove(g_)

            def load_w(dst, src_ap, q=None):
                kb.dma(q or POOL, dst.t[:], src_ap, writes=[dst.b])

            def dbg_store(name, src_ap, rows, reads):
                if name in dbg:
                    kb.dma(SP, dbg[name][rows], src_ap, reads=reads, is_out=True)

            def phase_1c():
                with ExitStack() as s3:
                    wq = sb(s3, "wq", [128, KC, 512], BF16)
                    load_w(wq, w_in_v[:, :, C_Q:C_Q + 512])
                    wg = sb(s3, "wg", [128, KC, 24], BF16)
                    load_w(wg, w_in_v[:, :, C_G:C_G + 24])
                    sqq = [sb(s3, f"sqq{i}", [128, 512], F32) for i in range(2)]
                    qst = sb(s3, "qst", [128, NT, 32], F32, NT)
                    tmpq = [sb(s3, f"tmpq{i}", [128, 8, 64], F32) for i in range(2)]
                    qaug = [sb(s3, f"qaug{i}", [128, 8, 128], BF16) for i in range(2)]
                    qT = [sb(s3, f"qT{i}", [128, 8, 128], BF16) for i in range(2)]
                    qT2 = [sb(s3, f"qT2{i}", [128, 8, 128], BF16) for i in range(2)]
                    gate = [sb(s3, f"gate{i}", [128, 24], F32) for i in range(2)]
                    scl = [sb(s3, f"scl{i}", [128, 1024], F32) for i in range(2)]
                    NPT = 3
                    PT = [[sb(s3, f"PT{i}_{j}", [128, 1024], BF16) for j in range(NPT)] for i in range(2)]
                    oTs = [sb(s3, f"oTs{i}", [128, 1024], F32) for i in range(2)]
                    num = [sb(s3, f"num{i}", [128, 3, 8, 64], F32) for i in range(2)]
                    den = [sb(s3, f"den{i}", [128, 3, 8], F32) for i in range(2)]
                    rdc = [sb(s3, f"rdc{i}", [128, 8], F32) for i in range(2)]
                    impn = [sb(s3, f"impn{i}", [128, 8, 32], F32) for i in range(2)]
                    imp = [sb(s3, f"imp{i}", [128, 2, 32], F32) for i in range(2)]
                    top8 = [sb(s3, f"top8{i}", [128, 2, 8], F32) for i in range(2)]
                    rd = [sb(s3, f"rd{i}", [128, 3, 8], F32) for i in range(2)]
                    coef = [sb(s3, f"coef{i}", [128, 3, 8], F32) for i in range(2)]
                    oacc = [sb(s3, f"oacc{i}", [128, 8, 64], F32) for i in range(2)]
                    otmp = [sb(s3, f"otmp{i}", [128, 8, 64], F32) for i in range(2)]
                    ytok = [sb(s3, f"ytok{i}", [128, 512], BF16) for i in range(2)]
                    ydbg = sb(s3, "ydbg", [128, 512], F32) if "y_nsa" in dbg else None
                    for i in range(2):
                        kb.op(POOL, lambda i=i: G.memset(qaug[i].t[:], 0.0), writes=[qaug[i].b])
                    ptc = [0, 0]

                    def tile_gen(c):
                        i2 = c % 2
                        base = 4 * i2
                        bZ, bG = base, base + 1
                        bO0, bO1 = base + 2, base + 3
                        Sb = [PS[bZ].b, PS[bG].b]
                        Ob = [PS[bO0].b, PS[bO1].b]
                        tok = slice(c * 128, (c + 1) * 128)

                        def S2():
                            return pf2(base // 2)

                        def O2():
                            return pf2(base // 2 + 1)

                        def X8():
                            return O2().rearrange("p (h c) -> p h c", h=8)

                        def to_token_major(br, ncol):
                            nu, de = num[i2], den[i2]
                            ot = oTs[i2]
                            kb.op(DVE, lambda: V.tensor_scalar(out=ot.t[0:ncol, :], in0=O2()[0:ncol, :], scalar1=1.0, scalar2=None, op0=ALU.mult),
                                  reads=Ob, writes=[ot.b])
                            yield
                            for hh in range(8):
                                kb.op(PE, lambda hh=hh: TE.transpose(out=X8()[:, hh, 0:ncol], in_=ot.t[0:ncol, hh * 128:(hh + 1) * 128],
                                                                     identity=ident_f.t[0:ncol, 0:ncol]),
                                      reads=[ot.b, ident_f.b], writes=[Ob[hh // 4]], inc=(hh % 4 == 3))
                            yield
                            kb.op(ACT, lambda: A.copy(out=nu.t[:, br, :, :], in_=X8()[:, :, 0:64]), reads=Ob, writes=[nu.b])
                            yield
                            kb.op(DVE, lambda: V.tensor_scalar(out=de.t[:, br, :], in0=X8()[:, :, 64], scalar1=1e-30, scalar2=None, op0=ALU.max),
                                  reads=Ob, writes=[de.b])
                            yield

                        def branch(br, KT, VV, kts, qsrc):
                            n = len(kts)
                            for j, kt in enumerate(kts):
                                for g in range(2):
                                    kb.op(PE, lambda kt=kt, g=g: TE.matmul(S2()[:, g * 512:(g + 1) * 512], lhsT=KT.t[:, g, kt * 128:(kt + 1) * 128],
                                                                           rhs=qsrc.t[:, 4 * g:4 * g + 4, :], start=True, stop=True),
                                          reads=[KT.bs[kt], qsrc.b], writes=[Sb[g]])
                                yield
                                pt = PT[i2][ptc[i2] % NPT]
                                ptc[i2] += 1
                                kb.op(ACT, lambda pt=pt: A.activation(out=pt.t[:], in_=S2(), func=AF.Exp), reads=Sb, writes=[pt.b])
                                yield
                                if kt == c:
                                    kb.op(DVE, lambda pt=pt: V.tensor_tensor(out=pt.t[:], in0=pt.t[:], in1=dmask8.t[:].rearrange("p a b -> p (a b)"), op=ALU.mult),
                                          reads=[pt.b, dmask8.b], writes=[pt.b])
                                    yield
                                elif br == 2 and kt == c - 4:
                                    kb.op(DVE, lambda pt=pt: V.tensor_tensor(out=pt.t[:], in0=pt.t[:], in1=tmask8.t[:].rearrange("p a b -> p (a b)"), op=ALU.mult),
                                          reads=[pt.b, tmask8.b], writes=[pt.b])
                                    yield
                                for g in range(2):
                                    kb.op(PE, lambda kt=kt, pt=pt, j=j, g=g: TE.matmul(O2()[0:65, g * 512:(g + 1) * 512], lhsT=VV.t[:, kt, g, :],
                                                                                       rhs=pt.t[:, g * 512:(g + 1) * 512], start=(j == 0), stop=(j == n - 1)),
                                          reads=[pt.b, VV.bs[kt]], writes=[Ob[g]], inc=(j == n - 1))
                                yield
                            yield from to_token_major(br, 65)

                        for k in range(KC):
                            kb.op(PE, lambda k=k: TE.matmul(pf(bZ), lhsT=hT.t[:, k, tok], rhs=wq.t[:, k, :], start=(k == 0), stop=(k == KC - 1)),
                                  reads=[hT.bs[c], wq.b], writes=[PS[bZ].b], inc=(k == KC - 1))
                        for k in range(KC):
                            kb.op(PE, lambda k=k: TE.matmul(pf(bG)[:, 0:24], lhsT=hT.t[:, k, tok], rhs=wg.t[:, k, :], start=(k == 0), stop=(k == KC - 1)),
                                  reads=[hT.bs[c], wg.b], writes=[PS[bG].b], inc=(k == KC - 1))
                        yield
                        sq = sqq[i2]
                        kb.op(ACT, lambda: A.activation(out=sq.t[:], in_=pf(bZ), func=AF.Square), reads=[PS[bZ].b], writes=[sq.b])
                        gt = gate[i2]
                        kb.op(ACT, lambda: A.activation(out=gt.t[:], in_=pf(bG)[:, 0:24], func=AF.Tanh, scale=0.5), reads=[PS[bG].b], writes=[gt.b])
                        yield
                        kb.op(DVE, lambda: V.tensor_scalar(out=gt.t[:], in0=gt.t[:], scalar1=0.5, scalar2=0.5, op0=ALU.mult, op1=ALU.add),
                              reads=[gt.b], writes=[gt.b])
                        qs = qst.t
                        qsb = [qst.bs[c]]
                        kb.op(DVE, lambda: V.tensor_reduce(out=qs[:, c, 0:8], in_=sq.t[:].rearrange("p (a b) -> p a b", b=64), axis=AX.X, op=ALU.add),
                              reads=[sq.b], writes=qsb)
                        yield
                        kb.op(DVE, lambda: V.tensor_scalar(out=qs[:, c, 8:16], in0=qs[:, c, 0:8], scalar1=1.0 / 64, scalar2=EPS, op0=ALU.mult, op1=ALU.add), reads=qsb, writes=qsb)
                        yield
                        kb.op(POOL, lambda: G.tensor_tensor(out=qs[:, c, 24:32], in0=qs[:, c, 8:16], in1=nhalf.t[:, 0:8], op=ALU.pow),
                              reads=qsb + [nhalf.b], writes=qsb)
                        yield
                        tq = tmpq[i2]
                        qa = qaug[i2]
                        kb.op(DVE, lambda: V.tensor_tensor(out=tq.t[:], in0=pf(bZ).rearrange("p (a b) -> p a b", b=64),
                                                           in1=qs[:, c, 24:32].unsqueeze(2).broadcast_to([128, 8, 64]), op=ALU.mult),
                              reads=[PS[bZ].b] + qsb, writes=[tq.b])
                        yield
                        kb.op(DVE, lambda: V.tensor_tensor(out=qa.t[:, :, 0:64], in0=tq.t[:], in1=gq.t[:].unsqueeze(1).broadcast_to([128, 8, 64]), op=ALU.mult),
                              reads=[tq.b, gq.b], writes=[qa.b])
                        kb.op(POOL, lambda: G.tensor_copy(out=qa.t[:, :, 96:100], in_=QAL.t[:, c, :, :]), reads=[QAL.b], writes=[qa.b])
                        yield
                        transposes([qa.t[:, h, :] for h in range(8)], bZ, [qa.b])
                        yield
                        q1 = qT[i2]
                        kb.op(ACT, lambda: A.copy(out=q1.t[:], in_=pb(bZ).rearrange("p (a b) -> p a b", b=128)), reads=[PS[bZ].b], writes=[q1.b])
                        yield
                        nu, de = num[i2], den[i2]
                        rdc_, impn_, imp_, top8_ = rdc[i2], impn[i2], imp[i2], top8[i2]
                        sc_ = scl[i2]
                        pc = PT[i2][ptc[i2] % NPT]
                        ptc[i2] += 1
                        for g in range(2):
                            kb.op(PE, lambda g=g: TE.matmul(S2()[0:127, g * 512:(g + 1) * 512], lhsT=KcT.t[:, g, 0:127], rhs=q1.t[:, 4 * g:4 * g + 4, :], start=True, stop=True),
                                  reads=[KcT.b, q1.b], writes=[Sb[g]])
                        yield
                        kb.op(DVE, lambda: V.scalar_tensor_tensor(out=sc_.t[0:127, :].rearrange("p (a b) -> p a b", b=128),
                                                                  in0=S2()[0:127, :].rearrange("p (a b) -> p a b", b=128), scalar=60.0,
                                                                  in1=cmask.t[0:127, tok].unsqueeze(1).broadcast_to([127, 8, 128]),
                                                                  op0=ALU.min, op1=ALU.add),
                              reads=Sb + [cmask.b], writes=[sc_.b])
                        yield
                        kb.op(ACT, lambda: A.activation(out=pc.t[0:127, :], in_=sc_.t[0:127, :], func=AF.Exp), reads=[sc_.b], writes=[pc.b])
                        yield
                        for g in range(2):
                            kb.op(PE, lambda g=g: TE.matmul(O2()[0:97, g * 512:(g + 1) * 512], lhsT=Vc.t[0:127, g, :], rhs=pc.t[0:127, g * 512:(g + 1) * 512], start=True, stop=True),
                                  reads=[pc.b, Vc.b], writes=[Ob[g]])
                        yield
                        yield from to_token_major(0, 97)
                        kb.op(DVE, lambda: V.reciprocal(out=rdc_.t[:], in_=de.t[:, 0, :]), reads=[de.b], writes=[rdc_.b])
                        yield
                        kb.op(DVE, lambda: V.tensor_tensor(out=impn_.t[:], in0=X8()[:, :, 65:97],
                                                           in1=rdc_.t[:].unsqueeze(2).broadcast_to([128, 8, 32]), op=ALU.mult),
                              reads=Ob + [rdc_.b], writes=[impn_.b])
                        yield
                        q2 = qT2[i2]

                        def imp_chain():
                            kb.op(DVE, lambda: V.tensor_reduce(out=imp_.t[:], in_=impn_.t[:].rearrange("p (g r) j -> p g j r", g=2), axis=AX.X, op=ALU.add),
                                  reads=[impn_.b], writes=[imp_.b])
                            yield
                            kb.op(DVE, lambda: V.tensor_tensor(out=imp_.t[:], in0=imp_.t[:], in1=addc.t[:, c:c + 1, :].broadcast_to([128, 2, 32]), op=ALU.add),
                                  reads=[imp_.b, addc.b], writes=[imp_.b])
                            yield
                            for g in range(2):
                                kb.op(DVE, lambda g=g: V.max(out=top8_.t[:, g, :], in_=imp_.t[:, g, :]), reads=[imp_.b], writes=[top8_.b])
                            yield
                            for g in range(2):
                                kb.op(DVE, lambda g=g: V.tensor_scalar(out=qa.t[:, 4 * g:4 * g + 4, 64:96],
                                                                       in0=imp_.t[:, g:g + 1, :].broadcast_to([128, 4, 32]),
                                                                       scalar1=top8_.t[:, g, 7:8], scalar2=NEG, op0=ALU.is_lt, op1=ALU.mult),
                                      reads=[imp_.b, top8_.b], writes=[qa.b])
                            yield

                        gw = branch(2, KT_win, V_win, list(range(max(0, c - 4), c + 1)), q1)
                        gi = imp_chain()
                        live = [gw, gi]
                        while live:
                            for g_ in list(live):
                                try:
                                    next(g_)
                                except StopIteration:
                                    live.remove(g_)
                            yield
                        transposes([qa.t[:, h, :] for h in range(8)], bZ, [qa.b])
                        kb.op(ACT, lambda: A.copy(out=q2.t[:], in_=pb(bZ).rearrange("p (a b) -> p a b", b=128)), reads=[PS[bZ].b], writes=[q2.b])
                        yield
                        yield from branch(1, KT_slc, V_slc, list(range(0, c + 1)), q2)
                        rd_, coef_, oacc_, otmp_ = rd[i2], coef[i2], oacc[i2], otmp[i2]
                        kb.op(DVE, lambda: V.reciprocal(out=rd_.t[:], in_=de.t[:]), reads=[de.b], writes=[rd_.b])
                        yield
                        kb.op(DVE, lambda: V.tensor_tensor(out=coef_.t[:], in0=gt.t[:].rearrange("p (h b) -> p b h", b=3), in1=rd_.t[:], op=ALU.mult),
                              reads=[gt.b, rd_.b], writes=[coef_.b])
                        yield
                        kb.op(DVE, lambda: V.tensor_tensor(out=oacc_.t[:], in0=nu.t[:, 0], in1=coef_.t[:, 0, :].unsqueeze(2).broadcast_to([128, 8, 64]), op=ALU.mult),
                              reads=[nu.b, coef_.b], writes=[oacc_.b])
                        kb.op(POOL, lambda: G.tensor_tensor(out=otmp_.t[:], in0=nu.t[:, 1], in1=coef_.t[:, 1, :].unsqueeze(2).broadcast_to([128, 8, 64]), op=ALU.mult),
                              reads=[nu.b, coef_.b], writes=[otmp_.b])
                        yield
                        kb.op(DVE, lambda: V.tensor_tensor(out=oacc_.t[:], in0=oacc_.t[:], in1=otmp_.t[:], op=ALU.add), reads=[oacc_.b, otmp_.b], writes=[oacc_.b])
                        yield
                        kb.op(POOL, lambda: G.tensor_tensor(out=otmp_.t[:], in0=nu.t[:, 2], in1=coef_.t[:, 2, :].unsqueeze(2).broadcast_to([128, 8, 64]), op=ALU.mult),
                              reads=[nu.b, coef_.b], writes=[otmp_.b])
                        yield
                        yt = ytok[i2]
                        kb.op(DVE, lambda: V.tensor_tensor(out=yt.t[:], in0=oacc_.t[:].rearrange("p a b -> p (a b)"), in1=otmp_.t[:].rearrange("p a b -> p (a b)"), op=ALU.add),
                              reads=[oacc_.b, otmp_.b], writes=[yt.b])
                        if ydbg is not None:
                            kb.op(POOL, lambda: G.tensor_tensor(out=ydbg.t[:], in0=oacc_.t[:].rearrange("p a b -> p (a b)"), in1=otmp_.t[:].rearrange("p a b -> p (a b)"), op=ALU.add),
                                  reads=[oacc_.b, otmp_.b], writes=[ydbg.b])
                            dbg_store("y_nsa", ydbg.t[:], tok, [ydbg.b])
                        yield
                        transposes([yt.t[:, k * 128:(k + 1) * 128] for k in range(4)], bZ, [yt.b])
                        yield
                        kb.op(ACT, lambda: A.copy(out=ynsaT.t[:, :, tok], in_=pb(bZ)[:, 0:512].rearrange("p (a b) -> p a b", b=128)),
                              reads=[PS[bZ].b], writes=[ynsaT.bs[c]])
                        yield

                    run_interleaved([(lambda c=c: tile_gen(c)) for c in range(NT)], width=WIDTH)

            def phase_1d():
                with ExitStack() as s4:
                    wr = sb(s4, "wr", [128, KC, 2048], BF16, 4)
                    for j in range(4):
                        kb.dma(POOL, wr.t[:, :, j * 512:(j + 1) * 512], w_in_v[:, :, C_R + j * 512:C_R + (j + 1) * 512], writes=[wr.bs[j]])
                    for j in range(4):
                        kb.dma(POOL, wm.t[:, :, j * 512:(j + 1) * 512], w_in_v[:, :, C_M + j * 512:C_M + (j + 1) * 512], writes=[wm.bs[j]])
                    load_w(wbr, w_branch[0].rearrange("n (k p) d -> p (n k) d", p=128))
                    idT = sb(s4, "idT", [128, 4, 128], F32)
                    qdec = sb(s4, "qdec", [128, 4, 128], F32)
                    kdec = sb(s4, "kdec", [128, 4, 128], F32)
                    gn = sb(s4, "gn", [128, 512], F32)
                    eij = sb(s4, "eij", [128, 128], F32)
                    rowq = sb(s4, "rowq", [128, 128], F32)
                    rowk = sb(s4, "rowk", [128, 128], F32)
                    kb.dma(SP, gn.t[:], ret_gn_g.rearrange("o a b -> o (a b)").broadcast_to([128, 512]), writes=[gn.b])
                    kb.op(POOL, lambda: G.iota(eij.t[:], pattern=[[1, 128]], base=0, channel_multiplier=-1, allow_small_or_imprecise_dtypes=True), writes=[eij.b])
                    kb.op(POOL, lambda: G.iota(rowq.t[:], pattern=[[1, 128]], base=1, channel_multiplier=0, allow_small_or_imprecise_dtypes=True), writes=[rowq.b])
                    kb.op(POOL, lambda: G.iota(rowk.t[:], pattern=[[-1, 128]], base=127, channel_multiplier=0, allow_small_or_imprecise_dtypes=True), writes=[rowk.b])
                    lgs = [float(np.log(1.0 - 2.0 ** (-5.0 - h))) for h in range(4)]
                    cds = [float(np.exp(128.0 * np.float32(lg))) for lg in lgs]
                    for h in range(4):
                        kb.op(ACT, lambda h=h: A.activation(out=idT.t[:, h, :], in_=eij.t[:], func=AF.Exp, scale=lgs[h]), reads=[eij.b], writes=[idT.b])
                        kb.op(POOL, lambda h=h: G.affine_select(out=idT.t[:, h, :], in_=idT.t[:, h, :], pattern=[[1, 128]], compare_op=ALU.is_ge, fill=0.0,
                                                                base=0, channel_multiplier=-1), reads=[idT.b], writes=[idT.b])
                        kb.op(ACT, lambda h=h: A.activation(out=qdec.t[:, h, :], in_=rowq.t[:], func=AF.Exp, scale=lgs[h]), reads=[rowq.b], writes=[qdec.b])
                        kb.op(ACT, lambda h=h: A.activation(out=kdec.t[:, h, :], in_=rowk.t[:], func=AF.Exp, scale=lgs[h]), reads=[rowk.b], writes=[kdec.b])
                    qTr = sb(s4, "qTr", [128, 4, 512], BF16)
                    qdT = sb(s4, "qdT", [128, 4, 512], BF16)
                    kTr = sb(s4, "kTr", [128, 4, 512], BF16)
                    kdT = sb(s4, "kdT", [128, 4, 512], BF16)
                    v_sb = [sb(s4, f"v_sb{i}", [128, 4, 128], BF16) for i in range(2)]
                    sgl = [sb(s4, f"sgl{i}", [128, 512], F32) for i in range(2)]
                    kd = [sb(s4, f"kd{i}", [128, 4, 128], BF16) for i in range(2)]
                    attb = [sb(s4, f"attb{i}", [128, 4, 128], BF16) for i in range(2)]
                    state_f = sb(s4, "state_f", [128, 4, 128], F32)
                    state_b = sb(s4, "state_b", [128, 4, 128], BF16)
                    yr = [sb(s4, f"yr{i}", [128, 512], BF16) for i in range(2)]
                    yrdbg = sb(s4, "yrdbg", [128, 512], F32) if "y_ret" in dbg else None
                    kb.op(POOL, lambda: G.memset(state_f.t[:], 0.0), writes=[state_f.b])
                    kb.op(POOL, lambda: G.memset(state_b.t[:], 0.0), writes=[state_b.b])
                    KS = float(128.0 ** -0.5)
                    pcnt = [0]
                    state_done = [False] * (NT + 1)
                    bst = [sb(s4, f"bst{i}", [128, 4, 6], F32) for i in range(2)]
                    mv = [sb(s4, f"mv{i}", [128, 4, 2], F32) for i in range(2)]
                    rs4 = [sb(s4, f"rs4{i}", [128, 12], F32) for i in range(2)]
                    on = [sb(s4, f"on{i}", [128, 512], F32) for i in range(2)]

                    def p1d_gen(c):
                        i2 = c % 2
                        cl = c % 4
                        bA, bB, bT = 2 + 3 * i2, 3 + 3 * i2, 4 + 3 * i2
                        tok = slice(c * 128, (c + 1) * 128)
                        cs = slice(cl * 128, (cl + 1) * 128)
                        bst_, mv_, rs4_, on_ = bst[i2], mv[i2], rs4[i2], on[i2]
                        for k in range(KC):
                            kb.op(PE, lambda k=k: TE.matmul(pf(bA), lhsT=hT.t[:, k, tok], rhs=wr.t[:, k, 1024:1536], start=(k == 0), stop=(k == KC - 1)),
                                  reads=[hT.bs[c], wr.bs[2]], writes=[PS[bA].b], inc=(k == KC - 1))
                        yield
                        for k in range(KC):
                            kb.op(PE, lambda k=k: TE.matmul(pf(bB), lhsT=hT.t[:, k, tok], rhs=wr.t[:, k, 1536:2048], start=(k == 0), stop=(k == KC - 1)),
                                  reads=[hT.bs[c], wr.bs[3]], writes=[PS[bB].b], inc=(k == KC - 1))
                        yield
                        vs, sg_, kd_, ab = v_sb[i2], sgl[i2], kd[i2], attb[i2]
                        kb.op(ACT, lambda: A.copy(out=vs.t[:].rearrange("p a b -> p (a b)"), in_=pf(bA)), reads=[PS[bA].b], writes=[vs.b])
                        yield
                        kb.op(ACT, lambda: A.activation(out=sg_.t[:], in_=pf(bB), func=AF.Silu), reads=[PS[bB].b], writes=[sg_.b])
                        yield
                        transposes([kdT.t[:, h, cs] for h in range(4)], bT, [kdT.b])
                        yield
                        kb.op(ACT, lambda: A.copy(out=kd_.t[:].rearrange("p a b -> p (a b)"), in_=pb(bT)[:, 0:512]), reads=[PS[bT].b], writes=[kd_.b])
                        yield
                        for h in range(4):
                            kb.op(PE, lambda h=h: TE.matmul(pf(bA)[:, h * 128:(h + 1) * 128], lhsT=kTr.t[:, h, cs], rhs=qTr.t[:, h, cs], start=True, stop=True),
                                  reads=[kTr.b, qTr.b], writes=[PS[bA].b], inc=(h == 3))
                        yield
                        kb.op(DVE, lambda: V.tensor_tensor(out=ab.t[:], in0=pf(bA).rearrange("p (a b) -> p a b", b=128), in1=idT.t[:], op=ALU.mult),
                              reads=[PS[bA].b, idT.b], writes=[ab.b])
                        yield
                        while c > 0 and not state_done[c - 1]:
                            yield
                        for h in range(4):
                            kb.op(PE, lambda h=h: TE.matmul(pf(bB)[:, h * 128:(h + 1) * 128], lhsT=ab.t[:, h, :], rhs=vs.t[:, h, :], start=True, stop=(c == 0)),
                                  reads=[ab.b, vs.b], writes=[PS[bB].b], inc=(c == 0 and h == 3))
                            if c > 0:
                                kb.op(PE, lambda h=h: TE.matmul(pf(bB)[:, h * 128:(h + 1) * 128], lhsT=qdT.t[:, h, cs], rhs=state_b.t[:, h, :], start=False, stop=True),
                                      reads=[qdT.b, state_b.b], writes=[PS[bB].b], inc=(h == 3))
                        yield
                        if c < NT - 1:
                            for h in range(4):
                                kb.op(PE, lambda h=h: TE.matmul(pf(bA)[:, h * 128:(h + 1) * 128], lhsT=kd_.t[:, h, :], rhs=vs.t[:, h, :], start=True, stop=True),
                                      reads=[kd_.b, vs.b], writes=[PS[bA].b], inc=(h == 3))
                            yield
                            for h in range(4):
                                kb.op(DVE, lambda h=h: V.scalar_tensor_tensor(out=state_f.t[:, h, :], in0=state_f.t[:, h, :], scalar=cds[h],
                                                                              in1=pf(bA)[:, h * 128:(h + 1) * 128], op0=ALU.mult, op1=ALU.add),
                                      reads=[state_f.b, PS[bA].b], writes=[state_f.b])
                            kb.op(POOL, lambda: G.tensor_copy(out=state_b.t[:], in_=state_f.t[:]), reads=[state_f.b], writes=[state_b.b])
                        state_done[c] = True
                        yield
                        for h in range(4):
                            kb.op(DVE, lambda h=h: V.bn_stats(out=bst_.t[:, h, :], in_=pf(bB)[:, h * 128:(h + 1) * 128]), reads=[PS[bB].b], writes=[bst_.b])
                        yield
                        for h in range(4):
                            kb.op(DVE, lambda h=h: V.bn_aggr(out=mv_.t[:, h, :], in_=bst_.t[:, h, :]), reads=[bst_.b], writes=[mv_.b])
                        yield
                        kb.op(DVE, lambda: V.tensor_scalar(out=rs4_.t[:, 0:4], in0=mv_.t[:, :, 1], scalar1=EPS, scalar2=None, op0=ALU.add), reads=[mv_.b], writes=[rs4_.b])
                        yield
                        kb.op(POOL, lambda: G.tensor_tensor(out=rs4_.t[:, 8:12], in0=rs4_.t[:, 0:4], in1=nhalf.t[:, 0:4], op=ALU.pow),
                              reads=[rs4_.b, nhalf.b], writes=[rs4_.b])
                        yield
                        for h in range(4):
                            kb.op(DVE, lambda h=h: V.tensor_scalar(out=on_.t[:, h * 128:(h + 1) * 128], in0=pf(bB)[:, h * 128:(h + 1) * 128],
                                                                   scalar1=mv_.t[:, h, 0:1], scalar2=rs4_.t[:, 8 + h:9 + h], op0=ALU.subtract, op1=ALU.mult),
                                  reads=[PS[bB].b, mv_.b, rs4_.b], writes=[on_.b])
                        yield
                        kb.op(POOL, lambda: G.tensor_tensor(out=on_.t[:], in0=on_.t[:], in1=gn.t[:], op=ALU.mult), reads=[on_.b, gn.b], writes=[on_.b])
                        yield
                        y_ = yr[i2]
                        kb.op(DVE, lambda: V.tensor_tensor(out=y_.t[:], in0=on_.t[:], in1=sg_.t[:], op=ALU.mult), reads=[on_.b, sg_.b], writes=[y_.b])
                        if yrdbg is not None:
                            kb.op(DVE, lambda: V.tensor_tensor(out=yrdbg.t[:], in0=on_.t[:], in1=sg_.t[:], op=ALU.mult), reads=[on_.b, sg_.b], writes=[yrdbg.b])
                            dbg_store("y_ret", yrdbg.t[:], tok, [yrdbg.b])
                        yield
                        transposes([y_.t[:, k * 128:(k + 1) * 128] for k in range(4)], bT, [y_.b])
                        yield
                        kb.op(ACT, lambda: A.copy(out=yretT.t[:, :, tok], in_=pb(bT)[:, 0:512].rearrange("p (a b) -> p a b", b=128)),
                              reads=[PS[bT].b], writes=[yretT.bs[c]])
                        yield

                    for tg in range(4):
                        tks = slice(tg * 512, (tg + 1) * 512)
                        hbs = [hT.bs[4 * tg + i] for i in range(4)]
                        for qk in range(2):
                            for h in range(4):
                                bk = pcnt[0] % 2
                                pcnt[0] += 1
                                for k in range(KC):
                                    kb.op(PE, lambda k=k, qk=qk, h=h, bk=bk: TE.matmul(pf(bk), lhsT=wr.t[:, k, qk * 512 + h * 128:qk * 512 + (h + 1) * 128],
                                                                                      rhs=hT.t[:, k, tks], start=(k == 0), stop=(k == KC - 1)),
                                          reads=hbs + [wr.bs[qk]], writes=[PS[bk].b], inc=(k == KC - 1))
                                pv4 = pf(bk).rearrange("p (a b) -> p a b", b=128)
                                if qk == 0:
                                    kb.op(ACT, lambda h=h, bk=bk: A.copy(out=qTr.t[:, h, :], in_=pf(bk)), reads=[PS[bk].b], writes=[qTr.b])
                                    kb.op(DVE, lambda h=h, pv4=pv4: V.tensor_tensor(out=qdT.t[:, h, :].rearrange("p (a b) -> p a b", b=128), in0=pv4,
                                                                                    in1=qdec.t[:, h:h + 1, :].broadcast_to([128, 4, 128]), op=ALU.mult),
                                          reads=[PS[bk].b, qdec.b], writes=[qdT.b])
                                else:
                                    kb.op(ACT, lambda h=h, bk=bk: A.mul(out=kTr.t[:, h, :], in_=pf(bk), mul=KS), reads=[PS[bk].b], writes=[kTr.b])
                                    kb.op(DVE, lambda h=h, pv4=pv4: V.scalar_tensor_tensor(out=kdT.t[:, h, :].rearrange("p (a b) -> p a b", b=128), in0=pv4, scalar=KS,
                                                                                           in1=kdec.t[:, h:h + 1, :].broadcast_to([128, 4, 128]),
                                                                                           op0=ALU.mult, op1=ALU.mult),
                                          reads=[PS[bk].b, kdec.b], writes=[kdT.b])
                        run_interleaved([(lambda c=c: p1d_gen(c)) for c in range(4 * tg, 4 * tg + 4)], width=2)

            def phase_1e():
                with ExitStack() as s5:
                    xt = [sb(s5, f"xte{i}", [128, D], F32) for i in range(2)]
                    g2bc = sb(s5, "g2bc", [128, D], F32)
                    kb.dma(SP, g2bc.t[:], norm2_g[0:1, :].broadcast_to([128, D]), writes=[g2bc.b])
                    wo = sb(s5, "wo", [128, KC, D], BF16)
                    load_w(wo, w_out[0].rearrange("(k p) d -> p k d", p=128))
                    gates = [sb(s5, f"gates{i}", [128, D], F32, 2) for i in range(2)]
                    tmix = [sb(s5, f"tmix{i}", [128, D], F32) for i in range(2)]
                    tmix2 = [sb(s5, f"tmix2{i}", [128, D], F32) for i in range(2)]
                    mixed = [sb(s5, f"mixed{i}", [128, D], BF16) for i in range(2)]
                    mixT = [sb(s5, f"mixT{i}", [128, KC, 128], BF16) for i in range(2)]
                    x1t = [sb(s5, f"x1t{i}", [128, D], F32) for i in range(2)]

                    def p1e_gen(c):
                        i2 = c % 2
                        bs_ = [4 * i2 + i for i in range(4)]
                        tok = slice(c * 128, (c + 1) * 128)
                        xx = xt[i2]
                        kb.dma(SP, xx.t[:], x[tok, :], writes=[xx.b])
                        gt_, tm = gates[i2], (tmix[i2], tmix2[i2])
                        for n, yT in ((0, ynsaT), (1, yretT)):
                            for half in range(2):
                                j = 2 * n + half
                                for k in range(KC):
                                    kb.op(PE, lambda j=j, k=k, half=half: TE.matmul(pf(bs_[half]), lhsT=hT.t[:, k, tok], rhs=wm.t[:, k, j * 512:(j + 1) * 512],
                                                                                    start=(k == 0), stop=(k == KC - 1)),
                                          reads=[hT.bs[c], wm.bs[j]], writes=[PS[bs_[half]].b], inc=(k == KC - 1))
                                yield
                            for half in range(2):
                                bk = bs_[2 + half]
                                for k in range(4):
                                    kb.op(PE, lambda n=n, half=half, k=k, bk=bk, yT=yT: TE.matmul(pf(bk), lhsT=yT.t[:, k, tok], rhs=wbr.t[:, n * 4 + k, half * 512:(half + 1) * 512],
                                                                                                  start=(k == 0), stop=(k == 3)),
                                          reads=[yT.bs[c], wbr.b], writes=[PS[bk].b], inc=(k == 3))
                                yield
                            for half in range(2):
                                kb.op(ACT, lambda half=half: A.activation(out=gt_.t[:, half * 512:(half + 1) * 512], in_=pf(bs_[half]), func=AF.Sigmoid),
                                      reads=[PS[bs_[half]].b], writes=[gt_.bs[half]])
                                yield
                            for half in range(2):
                                hs = slice(half * 512, (half + 1) * 512)
                                kb.op(DVE, lambda half=half, hs=hs, n=n: V.tensor_tensor(out=tm[n].t[:, hs], in0=gt_.t[:, hs], in1=pf(bs_[2 + half]), op=ALU.mult),
                                      reads=[gt_.bs[half], PS[bs_[2 + half]].b], writes=[tm[n].b])
                                yield
                        mx = mixed[i2]
                        kb.op(POOL, lambda: G.tensor_tensor(out=mx.t[:], in0=tm[0].t[:], in1=tm[1].t[:], op=ALU.add), reads=[tm[0].b, tm[1].b], writes=[mx.b])
                        yield
                        transposes([mx.t[:, k * 128:(k + 1) * 128] for k in range(KC)], bs_[0], [mx.b])
                        yield
                        mt = mixT[i2]
                        kb.op(ACT, lambda: A.copy(out=mt.t[:], in_=pb(bs_[0]).rearrange("p (a b) -> p a b", b=128)), reads=[PS[bs_[0]].b], writes=[mt.b])
                        yield
                        for half in range(2):
                            for k in range(KC):
                                kb.op(PE, lambda half=half, k=k: TE.matmul(pf(bs_[2 + half]), lhsT=mt.t[:, k, :], rhs=wo.t[:, k, half * 512:(half + 1) * 512],
                                                                           start=(k == 0), stop=(k == KC - 1)),
                                      reads=[mt.b, wo.b], writes=[PS[bs_[2 + half]].b], inc=(k == KC - 1))
                            yield
                        x1 = x1t[i2]
                        for half in range(2):
                            hs = slice(half * 512, (half + 1) * 512)
                            kb.op(DVE, lambda half=half, hs=hs: V.tensor_tensor(out=x1.t[:, hs], in0=xx.t[:, hs], in1=pf(bs_[2 + half]), op=ALU.add),
                                  reads=[xx.b, PS[bs_[2 + half]].b], writes=[x1.b])
                            yield
                        kb.dma(SP, xmid[tok, :], x1.t[:], reads=[x1.b])
                        dbg_store("x1", x1.t[:], tok, [x1.b])
                        yield from norm_gen(x1, c, g2bc, i2, bs_[1])

                    run_interleaved([(lambda c=c: p1e_gen(c)) for c in range(NT)], width=2)

            def phase_2():
                with ExitStack() as s6:
                    xt = [sb(s6, f"xtf{i}", [128, D], F32) for i in range(2)]
                    wd = sb(s6, "wd", [128, NFB, D], BF16, 2)
                    wd_v = ffn_w_down[0].rearrange("(fb p) d -> p fb d", p=128)
                    wgs = [sb(s6, f"wgs{i}", [128, KC, 256], BF16) for i in range(2)]
                    wus = [sb(s6, f"wus{i}", [128, KC, 256], BF16) for i in range(2)]
                    act = sb(s6, "act", [128, NFB, 1024], BF16, NFB)
                    sgs = [sb(s6, f"sgs{i}", [128, 512], F32) for i in range(2)]
                    outt = [sb(s6, f"outt{i}", [128, D], F32) for i in range(2)]
                    wg_v = ffn_w_gate[0].rearrange("(k p) f -> p k f", p=128)
                    wu_v = ffn_w_up[0].rearrange("(k p) f -> p k f", p=128)
                    cn = [0, 0]
                    for hf in range(2):
                        for fg in range(11):
                            cols = slice(fg * 256, (fg + 1) * 256)
                            wg_, wu_ = wgs[fg % 2], wus[fg % 2]
                            kb.dma(POOL, wg_.t[:], wg_v[:, :, cols], writes=[wg_.b])
                            kb.dma(POOL, wu_.t[:], wu_v[:, :, cols], writes=[wu_.b])
                            if hf == 0 and fg == 1:
                                kb.dma(POOL, wd.t[:, 0:11, :], wd_v[:, 0:11, :], writes=[wd.bs[0]])
                                kb.dma(POOL, wd.t[:, 11:22, :], wd_v[:, 11:22, :], writes=[wd.bs[1]])
                            for fl in range(2):
                                fb = fg * 2 + fl
                                for t2 in range(2):
                                    tokc = slice(hf * 1024 + t2 * 512, hf * 1024 + (t2 + 1) * 512)
                                    hbs = [hT.bs[hf * 8 + t2 * 4 + i] for i in range(4)]
                                    gb, ub = (0, 1) if cn[0] % 2 == 0 else (2, 3)
                                    cn[0] += 1
                                    for k in range(KC):
                                        kb.op(PE, lambda k=k, fl=fl, gb=gb, wg_=wg_, tokc=tokc: TE.matmul(pf(gb), lhsT=wg_.t[:, k, fl * 128:(fl + 1) * 128], rhs=hT.t[:, k, tokc],
                                                                                                          start=(k == 0), stop=(k == KC - 1)),
                                              reads=hbs + [wg_.b], writes=[PS[gb].b], inc=(k == KC - 1))
                                    for k in range(KC):
                                        kb.op(PE, lambda k=k, fl=fl, ub=ub, wu_=wu_, tokc=tokc: TE.matmul(pf(ub), lhsT=wu_.t[:, k, fl * 128:(fl + 1) * 128], rhs=hT.t[:, k, tokc],
                                                                                                          start=(k == 0), stop=(k == KC - 1)),
                                              reads=hbs + [wu_.b], writes=[PS[ub].b], inc=(k == KC - 1))
                                    sg_ = sgs[cn[0] % 2]
                                    kb.op(ACT, lambda sg_=sg_, gb=gb: A.activation(out=sg_.t[:], in_=pf(gb), func=AF.Silu), reads=[PS[gb].b], writes=[sg_.b])
                                    kb.op(DVE, lambda sg_=sg_, ub=ub, fb=fb, t2=t2: V.tensor_tensor(out=act.t[:, fb, t2 * 512:(t2 + 1) * 512], in0=sg_.t[:], in1=pf(ub), op=ALU.mult),
                                          reads=[sg_.b, PS[ub].b], writes=[act.bs[fb]])
                        for tl in range(8):
                            c = hf * 8 + tl
                            tok = slice(c * 128, (c + 1) * 128)
                            xx = xt[c % 2]
                            kb.dma(SP, xx.t[:], xmid[tok, :], writes=[xx.b])
                            ob = (4, 5) if cn[1] % 2 == 0 else (6, 7)
                            cn[1] += 1
                            for half in range(2):
                                for fb in range(NFB):
                                    kb.op(PE, lambda half=half, fb=fb, ob=ob, tl=tl: TE.matmul(pf(ob[half]), lhsT=act.t[:, fb, tl * 128:(tl + 1) * 128],
                                                                                               rhs=wd.t[:, fb, half * 512:(half + 1) * 512],
                                                                                               start=(fb == 0), stop=(fb == NFB - 1)),
                                          reads=[act.bs[fb], wd.bs[0 if fb < 11 else 1]], writes=[PS[ob[half]].b], inc=(fb == NFB - 1))
                            ot = outt[c % 2]
                            for half in range(2):
                                hs = slice(half * 512, (half + 1) * 512)
                                kb.op(DVE, lambda half=half, hs=hs, ob=ob, ot=ot, xx=xx: V.tensor_tensor(out=ot.t[:, hs], in0=xx.t[:, hs], in1=pf(ob[half]), op=ALU.add),
                                      reads=[xx.b, PS[ob[half]].b], writes=[ot.b])
                            kb.dma(SP, out[tok, :], ot.t[:], reads=[ot.b], is_out=True)

            with ExitStack() as sB:
                ynsaT = sb(sB, "ynsaT", [128, 4, S], BF16, NT)
                yretT = sb(sB, "yretT", [128, 4, S], BF16, NT)

                with ExitStack() as sA:
                    gq = sb(sA, "gq", [128, 64], F32)
                    gk = sb(sA, "gk", [128, 3, 64], F32)
                    QAL = sb(sA, "QAL", [128, NT, 8, 4], BF16)
                    KAL = sb(sA, "KAL", [128, NT, 4], BF16)
                    KCAL = sb(sA, "KCAL", [128, 4], BF16)
                    OH = sb(sA, "OH", [128, NT, 32], BF16)
                    dmask8 = sb(sA, "dmask8", [128, 8, 128], BF16)
                    tmask8 = sb(sA, "tmask8", [128, 8, 128], BF16)
                    cmask = sb(sA, "cmask", [128, S], BF16)
                    addc = sb(sA, "addc", [128, NT, 32], F32)
                    ov = sb(sA, "ov", [128, 32], BF16)
                    KT_slc = sb(sA, "KT_slc", [128, 2, S], BF16, NT)
                    KT_win = sb(sA, "KT_win", [128, 2, S], BF16, NT)
                    V_slc = sb(sA, "V_slc", [128, NT, 2, 65], BF16, NT)
                    V_win = sb(sA, "V_win", [128, NT, 2, 65], BF16, NT)
                    KcT = sb(sA, "KcT", [128, 2, 128], BF16)
                    Vc = sb(sA, "Vc", [128, 2, 97], BF16)

                    with ExitStack() as s0:
                        SL = sb(s0, "SL", [128, 8], F32)
                        th128 = sb(s0, "th128", [128, NT], F32)
                        pidx = sb(s0, "pidx", [128, 1], F32)
                        QALf = sb(s0, "QALf", [128, NT, 8, 4], F32)
                        KALf = sb(s0, "KALf", [128, NT, 4], F32)
                        KCALf = sb(s0, "KCALf", [128, 4], F32)
                        rel = sb(s0, "rel", [128, NT, 32], F32)
                        f0 = sb(s0, "f0", [128, NT, 32], F32)
                        f1 = sb(s0, "f1", [128, NT, 32], F32)
                        t1 = sb(s0, "t1", [128, NT, 32], F32)
                        hp = sb(s0, "hp", [128, 1], F32)
                        ovf = sb(s0, "ovf", [128, 32], F32)
                        ova = sb(s0, "ova", [128, 32], F32)
                        ones_b = sb(s0, "ones_b", [128, 512], BF16)
                        ones_b2 = sb(s0, "ones_b2", [128, 1024], BF16)
                        zeros_b = sb(s0, "zeros_b", [128, 512], BF16)

                        kb.dma(SP, gq.t[:], nsa_q_norm[0:1, :].broadcast_to([128, 64]), writes=[gq.b])
                        kb.dma(SP, gk.t[:].rearrange("p a b -> p (a b)"),
                               nsa_k_norm.rearrange("o a b -> o (a b)").broadcast_to([128, 192]), writes=[gk.b])
                        kb.op(DVE, lambda: V.tensor_scalar(out=gq.t[:], in0=gq.t[:], scalar1=0.125, scalar2=None, op0=ALU.mult),
                              reads=[gq.b], writes=[gq.b])
                        for h in range(8):
                            kb.op(POOL, lambda h=h: G.memset(SL.t[:, h:h + 1], 2.0 ** (-(h + 1))), writes=[SL.b])
                        kb.op(POOL, lambda: G.iota(th128.t[:], pattern=[[128, NT]], base=0, channel_multiplier=0,
                                                   allow_small_or_imprecise_dtypes=True), writes=[th128.b])
                        kb.op(POOL, lambda: G.iota(pidx.t[:], pattern=[[0, 1]], base=0, channel_multiplier=1,
                                                   allow_small_or_imprecise_dtypes=True), writes=[pidx.b])
                        SLb = SL.t[:].unsqueeze(1).broadcast_to([128, NT, 8])
                        THb = th128.t[:].unsqueeze(2).broadcast_to([128, NT, 8])
                        kb.op(DVE, lambda: V.scalar_tensor_tensor(out=QALf.t[:, :, :, 0], in0=THb, scalar=-1.0, in1=SLb,
                                                                  op0=ALU.mult, op1=ALU.mult),
                              reads=[SL.b, th128.b], writes=[QALf.b])
                        kb.op(DVE, lambda: V.tensor_scalar(out=QALf.t[:, :, :, 1], in0=SLb, scalar1=pidx.t[:, 0:1], scalar2=-1.0,
                                                           op0=ALU.mult, op1=ALU.mult),
                              reads=[SL.b, pidx.b], writes=[QALf.b])
                        kb.op(DVE, lambda: V.tensor_copy(out=QALf.t[:, :, :, 2], in_=SLb), reads=[SL.b], writes=[QALf.b])
                        kb.op(DVE, lambda: V.tensor_copy(out=QALf.t[:, :, :, 3], in_=SLb), reads=[SL.b], writes=[QALf.b])
                        kb.op(DVE, lambda: V.tensor_copy(out=QAL.t[:], in_=QALf.t[:]), reads=[QALf.b], writes=[QAL.b])
                        kb.op(POOL, lambda: G.memset(KALf.t[:, :, 0:2], 1.0), writes=[KALf.b])
                        kb.op(DVE, lambda: V.tensor_copy(out=KALf.t[:, :, 2], in_=th128.t[:]), reads=[th128.b], writes=[KALf.b])
                        kb.op(DVE, lambda: V.tensor_copy(out=KALf.t[:, :, 3], in_=pidx.t[:, 0:1].broadcast_to([128, NT])),
                              reads=[pidx.b], writes=[KALf.b])
                        kb.op(DVE, lambda: V.tensor_copy(out=KAL.t[:], in_=KALf.t[:]), reads=[KALf.b], writes=[KAL.b])
                        kb.op(POOL, lambda: G.memset(KCALf.t[:, 0:2], 1.0), writes=[KCALf.b])
                        kb.op(POOL, lambda: G.memset(KCALf.t[:, 3:4], 31.0), reads=[], writes=[KCALf.b])
                        kb.op(DVE, lambda: V.tensor_scalar(out=KCALf.t[:, 2:3], in0=pidx.t[:, 0:1], scalar1=16.0, scalar2=None,
                                                           op0=ALU.mult), reads=[pidx.b], writes=[KCALf.b])
                        kb.op(DVE, lambda: V.tensor_copy(out=KCAL.t[:], in_=KCALf.t[:]), reads=[KCALf.b], writes=[KCAL.b])
                        kb.op(POOL, lambda: G.memset(OH.t[:], 0.0), writes=[OH.b])
                        for kt in range(NT):
                            kb.op(POOL, lambda kt=kt: G.memset(OH.t[0:64, kt, 2 * kt:2 * kt + 1], 1.0), writes=[OH.b])
                            kb.op(POOL, lambda kt=kt: G.memset(OH.t[64:128, kt, 2 * kt + 1:2 * kt + 2], 1.0), writes=[OH.b])
                        kb.op(POOL, lambda: G.memset(ones_b.t[:], 1.0), writes=[ones_b.b])
                        kb.op(POOL, lambda: G.memset(zeros_b.t[:], 0.0), writes=[zeros_b.b])
                        ob8 = ones_b2.t[:].rearrange("p (a b) -> p a b", b=128)
                        kb.op(POOL, lambda: G.memset(ones_b2.t[:], 1.0), writes=[ones_b2.b])
                        kb.op(POOL, lambda: G.affine_select(out=dmask8.t[:], in_=ob8, pattern=[[0, 8], [1, 128]],
                                                            compare_op=ALU.is_ge, fill=0.0, base=0, channel_multiplier=-1),
                              reads=[ones_b2.b], writes=[dmask8.b])
                        kb.op(POOL, lambda: G.affine_select(out=tmask8.t[:], in_=ob8, pattern=[[0, 8], [-1, 128]],
                                                            compare_op=ALU.is_gt, fill=0.0, base=0, channel_multiplier=1),
                              reads=[ones_b2.b], writes=[tmask8.b])
                        for i in range(4):
                            kb.op(POOL, lambda i=i: G.affine_select(out=cmask.t[:, i * 512:(i + 1) * 512], in_=zeros_b.t[:],
                                                                    pattern=[[1, 512]], compare_op=ALU.is_ge, fill=NEG,
                                                                    base=-31 + 512 * i, channel_multiplier=-16),
                                  reads=[zeros_b.b], writes=[cmask.b])
                        kb.op(POOL, lambda: G.iota(rel.t[:], pattern=[[-2, NT], [1, 32]], base=0, channel_multiplier=0,
                                                   allow_small_or_imprecise_dtypes=True), writes=[rel.b])
                        kb.op(DVE, lambda: V.tensor_scalar(out=hp.t[:], in0=pidx.t[:], scalar1=64.0, scalar2=None, op0=ALU.is_ge),
                              reads=[pidx.b], writes=[hp.b])
                        kb.op(DVE, lambda: V.tensor_scalar(out=rel.t[:], in0=rel.t[:], scalar1=hp.t[:, 0:1], scalar2=None,
                                                           op0=ALU.subtract), reads=[rel.b, hp.b], writes=[rel.b])
                        kb.op(DVE, lambda: V.tensor_scalar(out=t1.t[:], in0=rel.t[:], scalar1=0.0, scalar2=-1e9,
                                                           op0=ALU.is_gt, op1=ALU.mult), reads=[rel.b], writes=[t1.b])
                        kb.op(DVE, lambda: V.tensor_scalar(out=f0.t[:], in0=rel.t[:], scalar1=0.0, scalar2=None, op0=ALU.is_equal),
                              reads=[rel.b], writes=[f0.b])
                        kb.op(DVE, lambda: V.tensor_scalar(out=f1.t[:], in0=rel.t[:], scalar1=-1.0, scalar2=None, op0=ALU.is_equal),
                              reads=[rel.b], writes=[f1.b])
                        kb.op(DVE, lambda: V.tensor_tensor(out=f0.t[:], in0=f0.t[:], in1=f1.t[:], op=ALU.max),
                              reads=[f0.b, f1.b], writes=[f0.b])
                        kb.op(DVE, lambda: V.memset(f0.t[:, :, 0:1], 1.0), reads=[], writes=[f0.b])
                        kb.op(DVE, lambda: V.scalar_tensor_tensor(out=addc.t[:], in0=f0.t[:], scalar=1e4, in1=t1.t[:],
                                                                  op0=ALU.mult, op1=ALU.add), reads=[f0.b, t1.b], writes=[addc.b])
                        kb.op(POOL, lambda: G.iota(ovf.t[:], pattern=[[-64, 32]], base=0, channel_multiplier=16,
                                                   allow_small_or_imprecise_dtypes=True), writes=[ovf.b])
                        kb.op(DVE, lambda: V.tensor_scalar(out=ova.t[:], in0=ovf.t[:], scalar1=63.0, scalar2=None, op0=ALU.is_le),
                              reads=[ovf.b], writes=[ova.b])
                        kb.op(DVE, lambda: V.tensor_scalar(out=ovf.t[:], in0=ovf.t[:], scalar1=-31.0, scalar2=None, op0=ALU.is_ge),
                              reads=[ovf.b], writes=[ovf.b])
                        kb.op(DVE, lambda: V.tensor_tensor(out=ov.t[:], in0=ova.t[:], in1=ovf.t[:], op=ALU.mult),
                              reads=[ova.b, ovf.b], writes=[ov.b])
                        kb.op(POOL, lambda: G.memset(V_slc.t[:, :, :, 64:65], 1.0), writes=V_slc.bs)
                        kb.op(POOL, lambda: G.memset(V_win.t[:, :, :, 64:65], 1.0), writes=V_win.bs)
                        kb.barrier()
                        chk(1)

                    with ExitStack() as s2:
                        cmpT = sb(s2, "cmpT", [128, 2, S], BF16, NT)
                        w1sb = sb(s2, "w1sb", [128, 2, 32, 128], BF16)
                        w2sb = sb(s2, "w2sb", [128, 2, 64], BF16)
                        pe_sb = sb(s2, "pe_sb", [32, 2, 64], F32)
                        with ExitStack() as s2a:
                            xt = [sb(s2a, f"xta{i}", [128, D], F32) for i in range(2)]
                            g1bc = sb(s2a, "g1bc", [128, D], F32)
                            kb.dma(SP, g1bc.t[:], norm1_g[0:1, :].broadcast_to([128, D]), writes=[g1bc.b])

                            def p1a_gen(c):
                                xx = xt[c % 2]
                                kb.dma(SP, xx.t[:], x[c * 128:(c + 1) * 128, :], writes=[xx.b])
                                yield
                                yield from norm_gen(xx, c, g1bc, c % 2, 6 + (c % 2))

                            wkv = sb(s2a, "wkv", [128, KC, 768], BF16)
                            load_w(wkv, w_in_v[:, :, C_KV:C_KV + 768])
                            for kv in range(2):
                                src = cmp_w1[0, kv].rearrange("l d f -> d l f")
                                kb.dma(POOL, w1sb.t[0:64, kv], src, writes=[w1sb.b])
                                kb.dma(POOL, w1sb.t[64:128, kv], src, writes=[w1sb.b])
                            kb.dma(POOL, w2sb.t[:], cmp_w2[0].rearrange("k f d -> f k d"), writes=[w2sb.b])
                            kb.dma(SP, pe_sb.t[:], cmp_pe[0].rearrange("k l d -> l k d"), writes=[pe_sb.b])
                            cmp_tok = [sb(s2a, f"cmp_tok{i}", [128, 256], BF16) for i in range(2)]
                            sqk = [sb(s2a, f"sqk{i}", [128, 256], F32) for i in range(2)]
                            kst = sb(s2a, "kst", [128, NT, 16], F32, NT)
                            tmpk = [sb(s2a, f"tmpk{i}", [128, 4, 64], F32) for i in range(2)]
                            ka_slc = [sb(s2a, f"ka_slc{i}", [128, 2, 128], BF16) for i in range(2)]
                            ka_win = [sb(s2a, f"ka_win{i}", [128, 2, 128], BF16) for i in range(2)]
                            for i in range(2):
                                kb.op(POOL, lambda i=i: G.memset(ka_slc[i].t[:], 0.0), writes=[ka_slc[i].b])
                                kb.op(POOL, lambda i=i: G.memset(ka_win[i].t[:], 0.0), writes=[ka_win[i].b])
                            def p1b_gen(c):
                                i2 = c % 2
                                bA, bB = (0, 1) if i2 == 0 else (2, 3)
                                tok = slice(c * 128, (c + 1) * 128)
                                if CUT >= 1:
                                    yield
                                    for k in range(KC):
                                        kb.op(PE, lambda k=k: TE.matmul(pf(bA), lhsT=hT.t[:, k, tok], rhs=wkv.t[:, k, 0:512],
                                                                        start=(k == 0), stop=(k == KC - 1)),
                                              reads=[hT.bs[c], wkv.b], writes=[PS[bA].b], inc=(k == KC - 1))
                                    for k in range(KC):
                                        kb.op(PE, lambda k=k: TE.matmul(pf(bB)[:, 0:256], lhsT=hT.t[:, k, tok], rhs=wkv.t[:, k, 512:768],
                                                                        start=(k == 0), stop=(k == KC - 1)),
                                              reads=[hT.bs[c], wkv.b], writes=[PS[bB].b], inc=(k == KC - 1))
                                if CUT >= 2:
                                    yield
                                    ct = cmp_tok[i2]
                                    kb.op(ACT, lambda: A.copy(out=ct.t[:], in_=pf(bA)[:, 0:256]), reads=[PS[bA].b], writes=[ct.b])
                                    sq = sqk[i2]
                                    kb.op(ACT, lambda: A.activation(out=sq.t[:, 0:128], in_=pf(bA)[:, 256:384], func=AF.Square),
                                          reads=[PS[bA].b], writes=[sq.b])
                                    kb.op(ACT, lambda: A.activation(out=sq.t[:, 128:256], in_=pf(bB)[:, 0:128], func=AF.Square),
                                          reads=[PS[bB].b], writes=[sq.b])
                                if CUT >= 3:
                                    yield
                                    ks = kst.t
                                    ksb = [kst.bs[c]]
                                    kb.op(DVE, lambda: V.tensor_reduce(out=ks[:, c, 0:4], in_=sq.t[:].rearrange("p (a b) -> p a b", b=64),
                                                                       axis=AX.X, op=ALU.add), reads=[sq.b], writes=ksb)
                                    rstd_from_ss(ks[:, c, 0:4], ks[:, c, 4:8], ks[:, c, 8:12], ks[:, c, 12:16], 64, ksb)
                                    tk = tmpk[i2]
                                    kb.op(DVE, lambda: V.tensor_tensor(out=tk.t[:, 0:2, :], in0=pf(bA)[:, 256:384].rearrange("p (a b) -> p a b", b=64),
                                                                       in1=ks[:, c, 12:14].unsqueeze(2).broadcast_to([128, 2, 64]), op=ALU.mult),
                                          reads=[PS[bA].b] + ksb, writes=[tk.b])
                                    kb.op(DVE, lambda: V.tensor_tensor(out=tk.t[:, 2:4, :], in0=pf(bB)[:, 0:128].rearrange("p (a b) -> p a b", b=64),
                                                                       in1=ks[:, c, 14:16].unsqueeze(2).broadcast_to([128, 2, 64]), op=ALU.mult),
                                          reads=[PS[bB].b] + ksb, writes=[tk.b])
                                    ksl, kwn = ka_slc[i2], ka_win[i2]
                                    kb.op(DVE, lambda: V.tensor_tensor(out=ksl.t[:, :, 0:64], in0=tk.t[:, 0:2, :],
                                                                       in1=gk.t[:, 1:2, :].broadcast_to([128, 2, 64]), op=ALU.mult),
                                          reads=[tk.b, gk.b], writes=[ksl.b])
                                    kb.op(DVE, lambda: V.tensor_tensor(out=kwn.t[:, :, 0:64], in0=tk.t[:, 2:4, :],
                                                                       in1=gk.t[:, 2:3, :].broadcast_to([128, 2, 64]), op=ALU.mult),
                                          reads=[tk.b, gk.b], writes=[kwn.b])
                                if CUT >= 4:
                                    yield
                                    kb.op(POOL, lambda: G.tensor_copy(out=ksl.t[:, :, 64:96], in_=OH.t[:, c:c + 1, :].broadcast_to([128, 2, 32])),
                                          reads=[OH.b], writes=[ksl.b])
                                    kb.op(POOL, lambda: G.tensor_copy(out=ksl.t[:, :, 96:100], in_=KAL.t[:, c:c + 1, :].broadcast_to([128, 2, 4])),
                                          reads=[KAL.b], writes=[ksl.b])
                                    kb.op(POOL, lambda: G.tensor_copy(out=kwn.t[:, :, 96:100], in_=KAL.t[:, c:c + 1, :].broadcast_to([128, 2, 4])),
                                          reads=[KAL.b], writes=[kwn.b])
                                if CUT >= 5:
                                    yield
                                    kb.op(ACT, lambda: A.copy(out=V_slc.t[:, c, :, 0:64], in_=pf(bA)[:, 384:512].rearrange("p (a b) -> p a b", b=64)),
                                          reads=[PS[bA].b], writes=[V_slc.bs[c]])
                                    kb.op(ACT, lambda: A.copy(out=V_win.t[:, c, :, 0:64], in_=pf(bB)[:, 128:256].rearrange("p (a b) -> p a b", b=64)),
                                          reads=[PS[bB].b], writes=[V_win.bs[c]])
                                if CUT >= 6:
                                    yield
                                    tb = 4 + i2
                                    transposes([ksl.t[:, 0, :], ksl.t[:, 1, :], kwn.t[:, 0, :], kwn.t[:, 1, :], ct.t[:, 0:128], ct.t[:, 128:256]],
                                               tb, [ksl.b, kwn.b, ct.b])
                                    pv3 = pb(tb).rearrange("p (a b) -> p a b", b=128)
                                    kb.op(ACT, lambda: A.copy(out=KT_slc.t[:, :, tok], in_=pv3[:, 0:2, :]), reads=[PS[tb].b], writes=[KT_slc.bs[c]])
                                    kb.op(ACT, lambda: A.copy(out=KT_win.t[:, :, tok], in_=pv3[:, 2:4, :]), reads=[PS[tb].b], writes=[KT_win.bs[c]])
                                    kb.op(ACT, lambda: A.copy(out=cmpT.t[:, :, tok], in_=pv3[:, 4:6, :]), reads=[PS[tb].b], writes=[cmpT.bs[c]])
                            def p1ab_gen(c):
                                yield from p1a_gen(c)
                                yield from p1b_gen(c)

                            run_interleaved([(lambda c=c: p1ab_gen(c)) for c in range(NT)], width=2)
                            if "KT_slc" in dbg:
                                kdb = sb(s2a, "kdb", [128, 2, S], F32)
                                kb.op(DVE, lambda: V.tensor_copy(out=kdb.t[:], in_=KT_slc.t[:]), reads=KT_slc.bs, writes=[kdb.b])
                                kb.dma(SP, dbg["KT_slc"].rearrange("p (a b) -> p a b", b=S), kdb.t[:], reads=[kdb.b], is_out=True)
                            kb.barrier()
                            chk(3)

                        with ExitStack() as s2b:
                            peT = sb(s2b, "peT", [64, 2, 32], BF16)
                            bias_c = sb(s2b, "bias_c", [128, 2], F32)
                            xhs = [sb(s2b, f"xh{i}", [128, 128], F32) for i in range(4)]
                            x2s = [sb(s2b, f"x2{i}", [128, 128], F32) for i in range(4)]
                            sgs_ = [sb(s2b, f"sgc{i}", [128, 128], F32) for i in range(4)]
                            HTbs = [sb(s2b, f"HTb{i}", [128, 128], BF16) for i in range(4)]
                            kca = sb(s2b, "kca", [128, 2, 128], BF16)
                            cst = sb(s2b, "cst", [128, 8], F32)
                            tmpcs = [sb(s2b, f"tmpc{i}", [128, 64], F32) for i in range(4)]
                            kb.op(POOL, lambda: G.memset(kca.t[:], 0.0), writes=[kca.b])
                            kb.op(POOL, lambda: G.memset(Vc.t[:], 0.0), writes=[Vc.b])
                            kb.op(POOL, lambda: G.tensor_copy(out=kca.t[:, :, 96:100], in_=KCAL.t[:].unsqueeze(1).broadcast_to([128, 2, 4])),
                                  reads=[KCAL.b], writes=[kca.b])
                            kb.op(POOL, lambda: G.memset(Vc.t[:, :, 64:65], 1.0), writes=[Vc.b])
                            kb.op(POOL, lambda: G.tensor_copy(out=Vc.t[:, :, 65:97], in_=ov.t[:].unsqueeze(1).broadcast_to([128, 2, 32])),
                                  reads=[ov.b], writes=[Vc.b])
                            for kv in range(2):
                                kb.op(PE, lambda kv=kv: TE.transpose(out=pf(0)[0:64, kv * 32:(kv + 1) * 32], in_=pe_sb.t[0:32, kv, :],
                                                                     identity=ident_f.t[0:32, 0:32]),
                                      reads=[pe_sb.b, ident_f.b], writes=[PS[0].b])
                            kb.op(DVE, lambda: V.tensor_copy(out=peT.t[:], in_=pf(0)[0:64, 0:64].rearrange("p (a b) -> p a b", b=32)),
                                  reads=[PS[0].b], writes=[peT.b])
                            for kv in range(2):
                                for l in range(32):
                                    kb.op(PE, lambda kv=kv, l=l: TE.matmul(pf(1)[:, kv:kv + 1], lhsT=w1sb.t[0:64, kv, l, :], rhs=peT.t[0:64, kv, l:l + 1],
                                                                           start=(l == 0), stop=(l == 31)),
                                          reads=[w1sb.b, peT.b], writes=[PS[1].b], inc=(l == 31))
                            kb.op(DVE, lambda: V.tensor_copy(out=bias_c.t[:], in_=pf(1)[:, 0:2]), reads=[PS[1].b], writes=[bias_c.b])
                            def cmp_gen(kv, g, idx):
                                bH, bO = 2 * idx, 2 * idx + 1
                                xh, x2, sg, HTb, tmpc = xhs[idx], x2s[idx], sgs_[idx], HTbs[idx], tmpcs[idx]
                                for l in range(32):
                                    kb.op(PE, lambda kv=kv, g=g, l=l: TE.matmul(
                                        pf(bH)[:, 0:127], lhsT=w1sb.t[g * 64:(g + 1) * 64, kv, l, :],
                                        rhs=cmpT.t[g * 64:(g + 1) * 64, kv, l:l + 16 * 126 + 1:16],
                                        start=(l == 0), stop=(l == 31)),
                                        reads=[w1sb.b] + cmpT.bs, writes=[PS[bH].b], inc=(l == 31))
                                yield
                                kb.op(ACT, lambda kv=kv: A.activation(out=xh.t[:, 0:127], in_=pf(bH)[:, 0:127], func=AF.Identity,
                                                                      bias=bias_c.t[:, kv:kv + 1], scale=1.0),
                                      reads=[PS[bH].b, bias_c.b], writes=[xh.b])
                                yield
                                kb.op(DVE, lambda: V.tensor_tensor(out=x2.t[:, 0:127], in0=xh.t[:, 0:127], in1=xh.t[:, 0:127], op=ALU.mult),
                                      reads=[xh.b], writes=[x2.b])
                                yield
                                kb.op(DVE, lambda: V.tensor_scalar(out=x2.t[:, 0:127], in0=x2.t[:, 0:127], scalar1=0.044715, scalar2=1.0,
                                                                   op0=ALU.mult, op1=ALU.add), reads=[x2.b], writes=[x2.b])
                                yield
                                kb.op(DVE, lambda: V.tensor_tensor(out=x2.t[:, 0:127], in0=x2.t[:, 0:127], in1=xh.t[:, 0:127], op=ALU.mult),
                                      reads=[x2.b, xh.b], writes=[x2.b])
                                yield
                                kb.op(ACT, lambda: A.activation(out=sg.t[:, 0:127], in_=x2.t[:, 0:127], func=AF.Sigmoid, scale=1.5957691216057308),
                                      reads=[x2.b], writes=[sg.b])
                                yield
                                kb.op(DVE, lambda: V.tensor_tensor(out=HTb.t[:, 0:127], in0=xh.t[:, 0:127], in1=sg.t[:, 0:127], op=ALU.mult),
                                      reads=[xh.b, sg.b], writes=[HTb.b])
                                yield
                                kb.op(PE, lambda kv=kv: TE.matmul(pf(bO)[0:127, 0:64], lhsT=HTb.t[:, 0:127], rhs=w2sb.t[:, kv, :], start=True, stop=True),
                                      reads=[HTb.b, w2sb.b], writes=[PS[bO].b])
                                yield
                                if kv == 0:
                                    kb.op(ACT, lambda g=g: A.activation(out=tmpc.t[0:127, :], in_=pf(bO)[0:127, 0:64], func=AF.Square,
                                                                        accum_out=cst.t[0:127, g:g + 1]),
                                          reads=[PS[bO].b], writes=[tmpc.b, cst.b])
                                    rstd_from_ss(cst.t[0:127, g:g + 1], cst.t[0:127, 2 + g:3 + g], cst.t[0:127, 4 + g:5 + g], cst.t[0:127, 6 + g:7 + g], 64, [cst.b])
                                    kb.op(DVE, lambda g=g: V.scalar_tensor_tensor(out=kca.t[0:127, g, 0:64], in0=pf(bO)[0:127, 0:64],
                                                                                  scalar=cst.t[0:127, 6 + g:7 + g], in1=gk.t[0:127, 0, :],
                                                                                  op0=ALU.mult, op1=ALU.mult),
                                          reads=[PS[bO].b, cst.b, gk.b], writes=[kca.b])
                                else:
                                    kb.op(ACT, lambda g=g: A.copy(out=Vc.t[0:127, g, 0:64], in_=pf(bO)[0:127, 0:64]),
                                          reads=[PS[bO].b], writes=[Vc.b])
                                yield

                            run_interleaved([(lambda kv=kv, g=g: cmp_gen(kv, g, 2 * kv + g)) for kv in range(2) for g in range(2)], width=4)
                            transposes([kca.t[:, 0, :], kca.t[:, 1, :]], 6, [kca.b])
                            kb.op(ACT, lambda: A.copy(out=KcT.t[:], in_=pb(6)[:, 0:256].rearrange("p (a b) -> p a b", b=128)),
                                  reads=[PS[6].b], writes=[KcT.b])
                            if "kc" in dbg:
                                kcd = sb(s2b, "kcd", [128, 2, 64], F32)
                                kb.op(DVE, lambda: V.tensor_copy(out=kcd.t[:], in_=kca.t[:, :, 0:64]), reads=[kca.b], writes=[kcd.b])
                                kb.dma(SP, dbg["kc"].rearrange("p (a b) -> p a b", b=64), kcd.t[:], reads=[kcd.b], is_out=True)
                            if "vc" in dbg:
                                vcd = sb(s2b, "vcd", [128, 2, 64], F32)
                                kb.op(DVE, lambda: V.tensor_copy(out=vcd.t[:], in_=Vc.t[:, :, 0:64]), reads=[Vc.b], writes=[vcd.b])
                                kb.dma(SP, dbg["vc"].rearrange("p (a b) -> p a b", b=64), vcd.t[:], reads=[vcd.b], is_out=True)
                            kb.barrier()
                            chk(4)

                    phase_1c()
                    kb.barrier()
                    chk(5)

                wm = sb(sB, "wm", [128, KC, 2048], BF16, 4)
                wbr = sb(sB, "wbr", [128, 8, D], BF16)
                phase_1d()
                kb.barrier()
                chk(6)
                phase_1e()
                kb.barrier()
                chk(7)

            phase_2()
            kb.finish()
    except _Stop:
        pass
    return nc


_NAMES = ["x", "norm1_g", "w_in", "nsa_q_norm", "nsa_k_norm", "cmp_pe", "cmp_w1", "cmp_w2", "ret_gn_g",
          "w_branch", "w_out", "norm2_g", "ffn_w_gate", "ffn_w_up", "ffn_w_down"]


def kernel(**inputs):
    n = 8
    arrs = {k: np.ascontiguousarray(np.asarray(inputs[k], dtype=np.float32)) for k in _NAMES}
    nc = build_nc()
    in_maps = []
    for i in range(n):
        m = {k: arrs[k] for k in _NAMES if k != "x"}
        m["x"] = np.ascontiguousarray(arrs["x"][i])
        in_maps.append(m)
    res = run_bass_kernel_spmd(nc, in_maps, core_ids=list(range(n)))
    return np.stack([np.asarray(r["out"], dtype=np.float32) for r in res.results], axis=0)
```

```python
import numpy as np
from contextlib import ExitStack
import concourse.bass as bass
import concourse.mybir as mybir
from concourse.bass_utils import run_bass_kernel_spmd

F32 = mybir.dt.float32
BF16 = mybir.dt.bfloat16
AF = mybir.ActivationFunctionType
ALU = mybir.AluOpType
AX = mybir.AxisListType

S = 2048
D = 1024
NT = 16
KC = 8
N_IN = 5400
DFF = 2816
NFB = 22
EPS = 1e-6
SEM_LIMIT = 24000
import os as _os
CUT = int(_os.environ.get('P1B_CUT', '99'))
CUTC = int(_os.environ.get('P1C_CUT', '99'))
SUBC = int(_os.environ.get('P1C_SUB', '99'))
WIDTH = int(_os.environ.get('P1C_WIDTH', '2'))
NEG = -30000.0

C_Q = 0
C_KV = 512
C_G = 1280
C_R = 1304
C_M = 3352


class Buf:
    __slots__ = ("name", "w", "r", "excl")

    def __init__(self, name):
        self.name = name
        self.w = None
        self.r = []
        self.excl = False


class SemW:
    __slots__ = ("h",)

    def __init__(self, h):
        self.h = h


class Slot:
    __slots__ = ("sem", "val")

    def __init__(self, sem):
        self.sem = sem
        self.val = 0


class Q:
    def __init__(self, name, eng):
        self.name = name
        self.eng = eng
        self.sem = None
        self.count = 0
        self.waited = {}
        self.ring = []
        self.ri = 0
        self.pending = False


class T:
    def __init__(self, t, name, nb=1):
        self.t = t
        self.bs = [Buf(f"{name}{i}") for i in range(nb)]

    @property
    def b(self):
        return self.bs[0]


class KB:
    def __init__(self, nc, es):
        self.nc = nc
        self.es = es
        self.nsem = 0
        self.pe = self.mkq("pe", nc.tensor)
        self.act = self.mkq("act", nc.scalar)
        self.dve = self.mkq("dve", nc.vector)
        self.pool = self.mkq("pool", nc.gpsimd)
        self.sp = self.mkq("sp", nc.sync)
        self.qs = [self.pe, self.act, self.dve, self.pool, self.sp]
        for q, n in ((self.sp, 16), (self.pool, 8), (self.act, 4)):
            q.ring = [Slot(self.new_sem(f"{q.name}_d{i}")) for i in range(n)]
        self.out_toks = []

    def new_sem(self, name):
        self.nsem += 1
        return SemW(self.es.enter_context(self.nc.semaphore(f"{name}_{self.nsem}")))

    def mkq(self, name, eng):
        q = Q(name, eng)
        q.sem = self.new_sem(name)
        return q

    def wait(self, q, tok):
        sw, val = tok[0], tok[1]
        if q.waited.get(sw, 0) >= val:
            return
        q.eng.wait_ge(sw.h, val)
        q.waited[sw] = val

    def _dep(self, q, tok, raw, force=False):
        if tok[2] is q and q is self.pe and not force:
            return
        self.wait(q, tok)

    def _deps(self, q, reads, writes, force=False):
        for b in reads:
            if b.w is not None:
                self._dep(q, b.w, True, force)
            if b.excl:
                for t in b.r:
                    if t[2] is not q:
                        self._dep(q, t, False, force)
        for b in writes:
            if b.w is not None:
                self._dep(q, b.w, False, force)
            for t in b.r:
                self._dep(q, t, False, force)

    def _record(self, tok, reads, writes):
        for b in reads:
            if tok[2] is not None:
                b.r = [t for t in b.r if t[2] is not tok[2]]
            b.r.append(tok)
        for b in writes:
            b.w = tok
            b.r = []

    def op(self, q, fn, reads=(), writes=(), inc=True):
        self._deps(q, reads, writes)
        ins = fn()
        if inc:
            if q.count >= SEM_LIMIT and not q.pending:
                q.sem = self.new_sem(q.name)
                q.count = 0
            ins.then_inc(q.sem.h, 1)
            q.count += 1
            q.pending = False
            tok = (q.sem, q.count, q)
        else:
            q.pending = True
            tok = (q.sem, q.count + 1, q)
        self._record(tok, reads, writes)
        return ins

    def dma(self, q, out, in_, reads=(), writes=(), is_out=False):
        self._deps(q, reads, writes, force=True)
        slot = q.ring[q.ri % len(q.ring)]
        q.ri += 1
        if slot.val > 0:
            self.wait(q, (slot.sem, slot.val))
        if slot.val >= SEM_LIMIT:
            slot.sem = self.new_sem(q.name + "_d")
            slot.val = 0
        ins = q.eng.dma_start(out=out, in_=in_)
        ins.then_inc(slot.sem.h, 16)
        slot.val += 16
        tok = (slot.sem, slot.val, None)
        self._record(tok, reads, writes)
        if is_out:
            self.out_toks.append(tok)
        return tok

    def barrier(self):
        toks = []
        for o in self.qs:
            if o.count > 0:
                toks.append((o.sem, o.count, o))
            for sl in o.ring:
                if sl.val > 0:
                    toks.append((sl.sem, sl.val, None))
        for q in self.qs:
            for t in toks:
                if t[2] is q:
                    continue
                self.wait(q, t)

    def finish(self):
        for t in self.out_toks:
            self.wait(self.sp, t)


class _Stop(Exception):
    pass


def build_nc(debug=None, stop=None):
    nc = bass.Bass("TRN2", target_bir_lowering=False)

    def din(name, shape):
        return nc.dram_tensor(name, list(shape), F32, kind="ExternalInput").ap()

    x = din("x", [S, D])
    norm1_g = din("norm1_g", [1, D])
    w_in = din("w_in", [1, D, N_IN])
    nsa_q_norm = din("nsa_q_norm", [1, 64])
    nsa_k_norm = din("nsa_k_norm", [1, 3, 64])
    cmp_pe = din("cmp_pe", [1, 2, 32, 64])
    cmp_w1 = din("cmp_w1", [1, 2, 32, 64, 128])
    cmp_w2 = din("cmp_w2", [1, 2, 128, 64])
    ret_gn_g = din("ret_gn_g", [1, 4, 128])
    w_branch = din("w_branch", [1, 2, 512, D])
    w_out = din("w_out", [1, D, D])
    norm2_g = din("norm2_g", [1, D])
    ffn_w_gate = din("ffn_w_gate", [1, D, DFF])
    ffn_w_up = din("ffn_w_up", [1, D, DFF])
    ffn_w_down = din("ffn_w_down", [1, DFF, D])
    out = nc.dram_tensor("out", [S, D], F32, kind="ExternalOutput").ap()
    xmid = nc.dram_tensor("xmid", [S, D], F32, kind="Internal").ap()
    dbg = {}
    if debug:
        for name, shape in debug.items():
            dbg[name] = nc.dram_tensor("dbg_" + name, list(shape), F32, kind="ExternalOutput").ap()

    w_in_v = w_in[0].rearrange("(k p) n -> p k n", p=128)

    try:
        with ExitStack() as es:
            kb = KB(nc, es)

            def chk(n):
                if stop is not None and n >= stop:
                    kb.barrier()
                    kb.finish()
                    raise _Stop()
            PE, ACT, DVE, POOL, SP = kb.pe, kb.act, kb.dve, kb.pool, kb.sp
            V, A, G, TE = nc.vector, nc.scalar, nc.gpsimd, nc.tensor

            def sb(scope, name, shape, dt, nb=1):
                return T(scope.enter_context(nc.sbuf_tensor(name, list(shape), dt)), name, nb)

            PS2 = [es.enter_context(nc.psum_tensor(f"psp{j}", [128, 1024], F32)) for j in range(4)]
            PS = [T(None, f"ps{i}") for i in range(8)]
            for p_ in PS:
                p_.b.excl = True

            def pf(i):
                return PS2[i // 2][:, (i % 2) * 512:(i % 2 + 1) * 512]

            def pb(i):
                return PS2[i // 2][:].bitcast(BF16)[:, (i % 2) * 1024:(i % 2 + 1) * 1024]

            def pf2(j):
                return PS2[j][:]

            ident_f = sb(es, "ident_f", [128, 128], F32)
            ident_b = sb(es, "ident_b", [128, 128], BF16)
            ones_f = sb(es, "ones_f", [128, 128], F32)
            nhalf = sb(es, "nhalf", [128, 16], F32)
            hT = sb(es, "hT", [128, KC, S], BF16, NT)
            stat = sb(es, "stat", [128, NT, 4], F32, NT)
            hb = [sb(es, f"hb{i}", [128, D], BF16) for i in range(2)]
            junk = sb(es, "junk", [128, D], BF16)

            kb.op(POOL, lambda: G.memset(ones_f.t[:], 1.0), writes=[ones_f.b])
            kb.op(POOL, lambda: G.memset(nhalf.t[:], -0.5), writes=[nhalf.b])
            kb.op(POOL, lambda: G.affine_select(out=ident_f.t[:], in_=ones_f.t[:, 0:128], pattern=[[1, 128]],
                                                compare_op=ALU.is_equal, fill=0.0, base=0, channel_multiplier=-1),
                  reads=[ones_f.b], writes=[ident_f.b])
            kb.op(DVE, lambda: V.tensor_copy(out=ident_b.t[:], in_=ident_f.t[:]), reads=[ident_f.b], writes=[ident_b.b])

            def rstd_from_ss(ss_ap, ms_ap, sd_ap, rs_ap, n, bufs):
                k = ms_ap.shape[-1]
                P_ = ms_ap.shape[0]
                kb.op(DVE, lambda: V.tensor_scalar(out=ms_ap, in0=ss_ap, scalar1=1.0 / n, scalar2=EPS,
                                                   op0=ALU.mult, op1=ALU.add), reads=bufs, writes=bufs)
                kb.op(POOL, lambda: G.tensor_tensor(out=rs_ap, in0=ms_ap, in1=nhalf.t[0:P_, 0:k], op=ALU.pow),
                      reads=list(bufs) + [nhalf.b], writes=bufs)

            def transposes(src_aps, bank, reads):
                pbv = pb(bank)
                n = len(src_aps)
                for i, ap in enumerate(src_aps):
                    kb.op(PE, lambda ap=ap, i=i: TE.transpose(out=pbv[:, i * 128:(i + 1) * 128], in_=ap, identity=ident_b.t[:]),
                          reads=list(reads) + [ident_b.b], writes=[PS[bank].b], inc=(i == n - 1))

            def norm_gen(src, c, gbc, sidx, bank):
                sbuf_ = [stat.bs[c]]
                st = stat.t
                kb.op(ACT, lambda: A.activation(out=junk.t[:], in_=src.t[:], func=AF.Square, accum_out=st[:, c, 0:1]),
                      reads=src.bs, writes=[junk.b] + sbuf_)
                yield
                kb.op(DVE, lambda: V.tensor_scalar(out=st[:, c, 1:2], in0=st[:, c, 0:1], scalar1=1.0 / D, scalar2=EPS,
                                                   op0=ALU.mult, op1=ALU.add), reads=sbuf_, writes=sbuf_)
                yield
                kb.op(POOL, lambda: G.tensor_tensor(out=st[:, c, 3:4], in0=st[:, c, 1:2], in1=nhalf.t[:, 0:1], op=ALU.pow),
                      reads=sbuf_ + [nhalf.b], writes=sbuf_)
                yield
                h = hb[sidx % 2]
                kb.op(DVE, lambda: V.scalar_tensor_tensor(out=h.t[:], in0=src.t[:], scalar=st[:, c, 3:4], in1=gbc.t[:],
                                                          op0=ALU.mult, op1=ALU.mult),
                      reads=src.bs + [gbc.b] + sbuf_, writes=[h.b])
                yield
                transposes([h.t[:, k * 128:(k + 1) * 128] for k in range(KC)], bank, [h.b])
                yield
                kb.op(ACT, lambda: A.copy(out=hT.t[:, :, c * 128:(c + 1) * 128],
                                          in_=pb(bank).rearrange("p (a b) -> p a b", b=128)),
                      reads=[PS[bank].b], writes=[hT.bs[c]])
                yield

            def run_interleaved(gen_fns, width=2):
                pending = list(gen_fns)
                active = []
                while pending or active:
                    while pending and len(active) < width:
                        active.append(pending.pop(0)())
                    for g_ in list(active):
                        try:
                            next(g_)
                        except StopIteration:
                            active.remove(g_)

            def load_w(dst, src_ap, q=None):
                kb.dma(q or POOL, dst.t[:], src_ap, writes=[dst.b])

            def dbg_store(name, src_ap, rows, reads):
                if name in dbg:
                    kb.dma(SP, dbg[name][rows], src_ap, reads=reads, is_out=True)

            def phase_1c():
                with ExitStack() as s3:
                    wq = sb(s3, "wq", [128, KC, 512], BF16)
                    load_w(wq, w_in_v[:, :, C_Q:C_Q + 512])
                    wg = sb(s3, "wg", [128, KC, 24], BF16)
                    load_w(wg, w_in_v[:, :, C_G:C_G + 24])
                    sqq = [sb(s3, f"sqq{i}", [128, 512], F32) for i in range(2)]
                    qst = sb(s3, "qst", [128, NT, 32], F32, NT)
                    tmpq = [sb(s3, f"tmpq{i}", [128, 8, 64], F32) for i in range(2)]
                    qaug = [sb(s3, f"qaug{i}", [128, 8, 128], BF16, 4) for i in range(2)]
                    qT = [sb(s3, f"qT{i}", [128, 8, 128], BF16) for i in range(2)]
                    qT2 = [sb(s3, f"qT2{i}", [128, 8, 128], BF16) for i in range(2)]
                    gate = [sb(s3, f"gate{i}", [128, 24], F32) for i in range(2)]
                    scl = [sb(s3, f"scl{i}", [128, 1024], F32) for i in range(2)]
                    NPT = 3
                    PT = [[sb(s3, f"PT{i}_{j}", [128, 1024], BF16) for j in range(NPT)] for i in range(2)]
                    oTs = [sb(s3, f"oTs{i}", [128, 1024], F32) for i in range(2)]
                    num = [sb(s3, f"num{i}", [128, 3, 8, 64], F32) for i in range(2)]
                    den = [sb(s3, f"den{i}", [128, 3, 8], F32) for i in range(2)]
                    rdc = [sb(s3, f"rdc{i}", [128, 8], F32) for i in range(2)]
                    impn = [sb(s3, f"impn{i}", [128, 8, 32], F32) for i in range(2)]
                    imp = [sb(s3, f"imp{i}", [128, 2, 32], F32) for i in range(2)]
                    top8 = [sb(s3, f"top8{i}", [128, 2, 8], F32, 2) for i in range(2)]
                    rd = [sb(s3, f"rd{i}", [128, 3, 8], F32) for i in range(2)]
                    coef = [sb(s3, f"coef{i}", [128, 3, 8], F32) for i in range(2)]
                    oacc = [sb(s3, f"oacc{i}", [128, 8, 64], F32) for i in range(2)]
                    otmp = [sb(s3, f"otmp{i}", [128, 8, 64], F32) for i in range(2)]
                    ytok = [sb(s3, f"ytok{i}", [128, 512], BF16) for i in range(2)]
                    ydbg = sb(s3, "ydbg", [128, 512], F32) if "y_nsa" in dbg else None
                    for i in range(2):
                        kb.op(POOL, lambda i=i: G.memset(qaug[i].t[:], 0.0), writes=qaug[i].bs)
                    ptc = [0, 0]

                    def tile_gen(c):
                        i2 = c % 2
                        base = 4 * i2
                        bZ, bG = base, base + 1
                        bO0, bO1 = base + 2, base + 3
                        Sb = [PS[bZ].b, PS[bG].b]
                        Ob = [PS[bO0].b, PS[bO1].b]
                        tok = slice(c * 128, (c + 1) * 128)

                        def S2():
                            return pf2(base // 2)

                        def O2():
                            return pf2(base // 2 + 1)

                        def X8():
                            return O2().rearrange("p (h c) -> p h c", h=8)

                        def to_token_major(br, ncol):
                            nu, de = num[i2], den[i2]
                            ot = oTs[i2]
                            kb.op(DVE, lambda: V.tensor_scalar(out=ot.t[0:ncol, :], in0=O2()[0:ncol, :], scalar1=1.0, scalar2=None, op0=ALU.mult),
                                  reads=Ob, writes=[ot.b])
                            yield
                            for hh in range(8):
                                kb.op(PE, lambda hh=hh: TE.transpose(out=X8()[:, hh, 0:ncol], in_=ot.t[0:ncol, hh * 128:(hh + 1) * 128],
                                                                     identity=ident_f.t[0:ncol, 0:ncol]),
                                      reads=[ot.b, ident_f.b], writes=[Ob[hh // 4]], inc=(hh % 4 == 3))
                            yield
                            kb.op(ACT, lambda: A.copy(out=nu.t[:, br, :, :], in_=X8()[:, :, 0:64]), reads=Ob, writes=[nu.b])
                            yield
                            kb.op(DVE, lambda: V.tensor_scalar(out=de.t[:, br, :], in0=X8()[:, :, 64], scalar1=1e-30, scalar2=None, op0=ALU.max),
                                  reads=Ob, writes=[de.b])
                            yield

                        def branch(br, KT, VV, kts, qsrc):
                            n = len(kts)
                            for j, kt in enumerate(kts):
                                for g in range(2):
                                    kb.op(PE, lambda kt=kt, g=g: TE.matmul(S2()[:, g * 512:(g + 1) * 512], lhsT=KT.t[:, g, kt * 128:(kt + 1) * 128],
                                                                           rhs=qsrc.t[:, 4 * g:4 * g + 4, :], start=True, stop=True),
                                          reads=[KT.bs[kt], qsrc.b], writes=[Sb[g]])
                                yield
                                pt = PT[i2][ptc[i2] % NPT]
                                ptc[i2] += 1
                                kb.op(ACT, lambda pt=pt: A.activation(out=pt.t[:], in_=S2(), func=AF.Exp), reads=Sb, writes=[pt.b])
                                yield
                                if kt == c:
                                    kb.op(DVE, lambda pt=pt: V.tensor_tensor(out=pt.t[:], in0=pt.t[:], in1=dmask8.t[:].rearrange("p a b -> p (a b)"), op=ALU.mult),
                                          reads=[pt.b, dmask8.b], writes=[pt.b])
                                    yield
                                elif br == 2 and kt == c - 4:
                                    kb.op(DVE, lambda pt=pt: V.tensor_tensor(out=pt.t[:], in0=pt.t[:], in1=tmask8.t[:].rearrange("p a b -> p (a b)"), op=ALU.mult),
                                          reads=[pt.b, tmask8.b], writes=[pt.b])
                                    yield
                                for g in range(2):
                                    kb.op(PE, lambda kt=kt, pt=pt, j=j, g=g: TE.matmul(O2()[0:65, g * 512:(g + 1) * 512], lhsT=VV.t[:, kt, g, :],
                                                                                       rhs=pt.t[:, g * 512:(g + 1) * 512], start=(j == 0), stop=(j == n - 1)),
                                          reads=[pt.b, VV.bs[kt]], writes=[Ob[g]], inc=(j == n - 1))
                                yield
                            yield from to_token_major(br, 65)

                        for k in range(KC):
                            kb.op(PE, lambda k=k: TE.matmul(pf(bZ), lhsT=hT.t[:, k, tok], rhs=wq.t[:, k, :], start=(k == 0), stop=(k == KC - 1)),
                                  reads=[hT.bs[c], wq.b], writes=[PS[bZ].b], inc=(k == KC - 1))
                        for k in range(KC):
                            kb.op(PE, lambda k=k: TE.matmul(pf(bG)[:, 0:24], lhsT=hT.t[:, k, tok], rhs=wg.t[:, k, :], start=(k == 0), stop=(k == KC - 1)),
                                  reads=[hT.bs[c], wg.b], writes=[PS[bG].b], inc=(k == KC - 1))
                        yield
                        sq = sqq[i2]
                        kb.op(ACT, lambda: A.activation(out=sq.t[:], in_=pf(bZ), func=AF.Square), reads=[PS[bZ].b], writes=[sq.b])
                        gt = gate[i2]
                        kb.op(ACT, lambda: A.activation(out=gt.t[:], in_=pf(bG)[:, 0:24], func=AF.Tanh, scale=0.5), reads=[PS[bG].b], writes=[gt.b])
                        yield
                        kb.op(DVE, lambda: V.tensor_scalar(out=gt.t[:], in0=gt.t[:], scalar1=0.5, scalar2=0.5, op0=ALU.mult, op1=ALU.add),
                              reads=[gt.b], writes=[gt.b])
                        qs = qst.t
                        qsb = [qst.bs[c]]
                        kb.op(DVE, lambda: V.tensor_reduce(out=qs[:, c, 0:8], in_=sq.t[:].rearrange("p (a b) -> p a b", b=64), axis=AX.X, op=ALU.add),
                              reads=[sq.b], writes=qsb)
                        yield
                        kb.op(DVE, lambda: V.tensor_scalar(out=qs[:, c, 8:16], in0=qs[:, c, 0:8], scalar1=1.0 / 64, scalar2=EPS, op0=ALU.mult, op1=ALU.add), reads=qsb, writes=qsb)
                        yield
                        kb.op(POOL, lambda: G.tensor_tensor(out=qs[:, c, 24:32], in0=qs[:, c, 8:16], in1=nhalf.t[:, 0:8], op=ALU.pow),
                              reads=qsb + [nhalf.b], writes=qsb)
                        yield
                        tq = tmpq[i2]
                        qa = qaug[i2]
                        kb.op(DVE, lambda: V.tensor_tensor(out=tq.t[:], in0=pf(bZ).rearrange("p (a b) -> p a b", b=64),
                                                           in1=qs[:, c, 24:32].unsqueeze(2).broadcast_to([128, 8, 64]), op=ALU.mult),
                              reads=[PS[bZ].b] + qsb, writes=[tq.b])
                        yield
                        kb.op(DVE, lambda: V.tensor_tensor(out=qa.t[:, :, 0:64], in0=tq.t[:], in1=gq.t[:].unsqueeze(1).broadcast_to([128, 8, 64]), op=ALU.mult),
                              reads=[tq.b, gq.b], writes=[qa.bs[0]])
                        kb.op(POOL, lambda: G.tensor_copy(out=qa.t[:, :, 96:100], in_=QAL.t[:, c, :, :]), reads=[QAL.b], writes=[qa.bs[1]])
                        yield
                        transposes([qa.t[:, h, :] for h in range(8)], bZ, qa.bs)
                        yield
                        q1 = qT[i2]
                        kb.op(ACT, lambda: A.copy(out=q1.t[:], in_=pb(bZ).rearrange("p (a b) -> p a b", b=128)), reads=[PS[bZ].b], writes=[q1.b])
                        yield
                        nu, de = num[i2], den[i2]
                        rdc_, impn_, imp_, top8_ = rdc[i2], impn[i2], imp[i2], top8[i2]
                        sc_ = scl[i2]
                        pc = PT[i2][ptc[i2] % NPT]
                        ptc[i2] += 1
                        for g in range(2):
                            kb.op(PE, lambda g=g: TE.matmul(S2()[0:127, g * 512:(g + 1) * 512], lhsT=KcT.t[:, g, 0:127], rhs=q1.t[:, 4 * g:4 * g + 4, :], start=True, stop=True),
                                  reads=[KcT.b, q1.b], writes=[Sb[g]])
                        yield
                        kb.op(DVE, lambda: V.scalar_tensor_tensor(out=sc_.t[0:127, :].rearrange("p (a b) -> p a b", b=128),
                                                                  in0=S2()[0:127, :].rearrange("p (a b) -> p a b", b=128), scalar=60.0,
                                                                  in1=cmask.t[0:127, tok].unsqueeze(1).broadcast_to([127, 8, 128]),
                                                                  op0=ALU.min, op1=ALU.add),
                              reads=Sb + [cmask.b], writes=[sc_.b])
                        yield
                        kb.op(ACT, lambda: A.activation(out=pc.t[0:127, :], in_=sc_.t[0:127, :], func=AF.Exp), reads=[sc_.b], writes=[pc.b])
                        yield
                        for g in range(2):
                            kb.op(PE, lambda g=g: TE.matmul(O2()[0:97, g * 512:(g + 1) * 512], lhsT=Vc.t[0:127, g, :], rhs=pc.t[0:127, g * 512:(g + 1) * 512], start=True, stop=True),
                                  reads=[pc.b, Vc.b], writes=[Ob[g]])
                        yield
                        yield from to_token_major(0, 97)
                        kb.op(DVE, lambda: V.reciprocal(out=rdc_.t[:], in_=de.t[:, 0, :]), reads=[de.b], writes=[rdc_.b])
                        yield
                        kb.op(DVE, lambda: V.tensor_tensor(out=impn_.t[:], in0=X8()[:, :, 65:97],
                                                           in1=rdc_.t[:].unsqueeze(2).broadcast_to([128, 8, 32]), op=ALU.mult),
                              reads=Ob + [rdc_.b], writes=[impn_.b])
                        yield
                        q2 = qT2[i2]

                        def imp_chain():
                            kb.op(DVE, lambda: V.tensor_reduce(out=imp_.t[:], in_=impn_.t[:].rearrange("p (g r) j -> p g j r", g=2), axis=AX.X, op=ALU.add),
                                  reads=[impn_.b], writes=[imp_.b])
                            yield
                            kb.op(DVE, lambda: V.tensor_tensor(out=imp_.t[:], in0=imp_.t[:], in1=addc.t[:, c:c + 1, :].broadcast_to([128, 2, 32]), op=ALU.add),
                                  reads=[imp_.b, addc.b], writes=[imp_.b])
                            yield
                            for g in range(2):
                                kb.op(DVE, lambda g=g: V.max(out=top8_.t[:, g, :], in_=imp_.t[:, g, :]), reads=[imp_.b], writes=[top8_.bs[g]])
                            yield
                            for g in range(2):
                                kb.op(DVE, lambda g=g: V.tensor_scalar(out=qa.t[:, 4 * g:4 * g + 4, 64:96],
                                                                       in0=imp_.t[:, g:g + 1, :].broadcast_to([128, 4, 32]),
                                                                       scalar1=top8_.t[:, g, 7:8], scalar2=NEG, op0=ALU.is_lt, op1=ALU.mult),
                                      reads=[imp_.b, top8_.bs[g]], writes=[qa.bs[2 + g]])
                            yield

                        gw = branch(2, KT_win, V_win, list(range(max(0, c - 4), c + 1)), q1)
                        gi = imp_chain()
                        live = [gw, gi]
                        while live:
                            for g_ in list(live):
                                try:
                                    next(g_)
                                except StopIteration:
                                    live.remove(g_)
                            yield
                        transposes([qa.t[:, h, :] for h in range(8)], bZ, qa.bs)
                        kb.op(ACT, lambda: A.copy(out=q2.t[:], in_=pb(bZ).rearrange("p (a b) -> p a b", b=128)), reads=[PS[bZ].b], writes=[q2.b])
                        yield
                        yield from branch(1, KT_slc, V_slc, list(range(0, c + 1)), q2)
                        rd_, coef_, oacc_, otmp_ = rd[i2], coef[i2], oacc[i2], otmp[i2]
                        kb.op(DVE, lambda: V.reciprocal(out=rd_.t[:], in_=de.t[:]), reads=[de.b], writes=[rd_.b])
                        yield
                        kb.op(DVE, lambda: V.tensor_tensor(out=coef_.t[:], in0=gt.t[:].rearrange("p (h b) -> p b h", b=3), in1=rd_.t[:], op=ALU.mult),
                              reads=[gt.b, rd_.b], writes=[coef_.b])
                        yield
                        kb.op(DVE, lambda: V.tensor_tensor(out=oacc_.t[:], in0=nu.t[:, 0], in1=coef_.t[:, 0, :].unsqueeze(2).broadcast_to([128, 8, 64]), op=ALU.mult),
                              reads=[nu.b, coef_.b], writes=[oacc_.b])
                        kb.op(POOL, lambda: G.tensor_tensor(out=otmp_.t[:], in0=nu.t[:, 1], in1=coef_.t[:, 1, :].unsqueeze(2).broadcast_to([128, 8, 64]), op=ALU.mult),
                              reads=[nu.b, coef_.b], writes=[otmp_.b])
                        yield
                        kb.op(DVE, lambda: V.tensor_tensor(out=oacc_.t[:], in0=oacc_.t[:], in1=otmp_.t[:], op=ALU.add), reads=[oacc_.b, otmp_.b], writes=[oacc_.b])
                        yield
                        kb.op(POOL, lambda: G.tensor_tensor(out=otmp_.t[:], in0=nu.t[:, 2], in1=coef_.t[:, 2, :].unsqueeze(2).broadcast_to([128, 8, 64]), op=ALU.mult),
                              reads=[nu.b, coef_.b], writes=[otmp_.b])
                        yield
                        yt = ytok[i2]
                        kb.op(DVE, lambda: V.tensor_tensor(out=yt.t[:], in0=oacc_.t[:].rearrange("p a b -> p (a b)"), in1=otmp_.t[:].rearrange("p a b -> p (a b)"), op=ALU.add),
                              reads=[oacc_.b, otmp_.b], writes=[yt.b])
                        if ydbg is not None:
                            kb.op(POOL, lambda: G.tensor_tensor(out=ydbg.t[:], in0=oacc_.t[:].rearrange("p a b -> p (a b)"), in1=otmp_.t[:].rearrange("p a b -> p (a b)"), op=ALU.add),
                                  reads=[oacc_.b, otmp_.b], writes=[ydbg.b])
                            dbg_store("y_nsa", ydbg.t[:], tok, [ydbg.b])
                        yield
                        transposes([yt.t[:, k * 128:(k + 1) * 128] for k in range(4)], bZ, [yt.b])
                        yield
                        kb.op(ACT, lambda: A.copy(out=ynsaT.t[:, :, tok], in_=pb(bZ)[:, 0:512].rearrange("p (a b) -> p a b", b=128)),
                              reads=[PS[bZ].b], writes=[ynsaT.bs[c]])
                        yield

                    run_interleaved([(lambda c=c: tile_gen(c)) for c in range(NT)], width=WIDTH)

            def phase_1d():
                with ExitStack() as s4:
                    wr = sb(s4, "wr", [128, KC, 2048], BF16, 4)
                    for j in range(4):
                        kb.dma(POOL, wr.t[:, :, j * 512:(j + 1) * 512], w_in_v[:, :, C_R + j * 512:C_R + (j + 1) * 512], writes=[wr.bs[j]])
                    for j in range(4):
                        kb.dma(POOL, wm.t[:, :, j * 512:(j + 1) * 512], w_in_v[:, :, C_M + j * 512:C_M + (j + 1) * 512], writes=[wm.bs[j]])
                    load_w(wbr, w_branch[0].rearrange("n (k p) d -> p (n k) d", p=128))
                    idT = sb(s4, "idT", [128, 4, 128], F32)
                    qdec = sb(s4, "qdec", [128, 4, 128], F32)
                    kdec = sb(s4, "kdec", [128, 4, 128], F32)
                    gn = sb(s4, "gn", [128, 512], F32)
                    eij = sb(s4, "eij", [128, 128], F32)
                    rowq = sb(s4, "rowq", [128, 128], F32)
                    rowk = sb(s4, "rowk", [128, 128], F32)
                    kb.dma(SP, gn.t[:], ret_gn_g.rearrange("o a b -> o (a b)").broadcast_to([128, 512]), writes=[gn.b])
                    kb.op(POOL, lambda: G.iota(eij.t[:], pattern=[[1, 128]], base=0, channel_multiplier=-1, allow_small_or_imprecise_dtypes=True), writes=[eij.b])
                    kb.op(POOL, lambda: G.iota(rowq.t[:], pattern=[[1, 128]], base=1, channel_multiplier=0, allow_small_or_imprecise_dtypes=True), writes=[rowq.b])
                    kb.op(POOL, lambda: G.iota(rowk.t[:], pattern=[[-1, 128]], base=127, channel_multiplier=0, allow_small_or_imprecise_dtypes=True), writes=[rowk.b])
                    lgs = [float(np.log(1.0 - 2.0 ** (-5.0 - h))) for h in range(4)]
                    cds = [float(np.exp(128.0 * np.float32(lg))) for lg in lgs]
                    for h in range(4):
                        kb.op(ACT, lambda h=h: A.activation(out=idT.t[:, h, :], in_=eij.t[:], func=AF.Exp, scale=lgs[h]), reads=[eij.b], writes=[idT.b])
                        kb.op(POOL, lambda h=h: G.affine_select(out=idT.t[:, h, :], in_=idT.t[:, h, :], pattern=[[1, 128]], compare_op=ALU.is_ge, fill=0.0,
                                                                base=0, channel_multiplier=-1), reads=[idT.b], writes=[idT.b])
                        kb.op(ACT, lambda h=h: A.activation(out=qdec.t[:, h, :], in_=rowq.t[:], func=AF.Exp, scale=lgs[h]), reads=[rowq.b], writes=[qdec.b])
                        kb.op(ACT, lambda h=h: A.activation(out=kdec.t[:, h, :], in_=rowk.t[:], func=AF.Exp, scale=lgs[h]), reads=[rowk.b], writes=[kdec.b])
                    qTr = sb(s4, "qTr", [128, 4, 512], BF16)
                    qdT = sb(s4, "qdT", [128, 4, 512], BF16)
                    kTr = sb(s4, "kTr", [128, 4, 512], BF16)
                    kdT = sb(s4, "kdT", [128, 4, 512], BF16)
                    v_sb = [sb(s4, f"v_sb{i}", [128, 4, 128], BF16) for i in range(2)]
                    sgl = [sb(s4, f"sgl{i}", [128, 512], F32) for i in range(2)]
                    kd = [sb(s4, f"kd{i}", [128, 4, 128], BF16) for i in range(2)]
                    attb = [sb(s4, f"attb{i}", [128, 4, 128], BF16) for i in range(2)]
                    state_f = sb(s4, "state_f", [128, 4, 128], F32, 4)
                    state_b = sb(s4, "state_b", [128, 4, 128], BF16)
                    yr = [sb(s4, f"yr{i}", [128, 512], BF16) for i in range(2)]
                    yrdbg = sb(s4, "yrdbg", [128, 512], F32) if "y_ret" in dbg else None
                    kb.op(POOL, lambda: G.memset(state_f.t[:], 0.0), writes=state_f.bs)
                    kb.op(POOL, lambda: G.memset(state_b.t[:], 0.0), writes=[state_b.b])
                    KS = float(128.0 ** -0.5)
                    pcnt = [0]
                    state_done = [False] * (NT + 1)
                    bst = [sb(s4, f"bst{i}", [128, 4, 6], F32, 4) for i in range(2)]
                    mv = [sb(s4, f"mv{i}", [128, 4, 2], F32, 4) for i in range(2)]
                    rs4 = [sb(s4, f"rs4{i}", [128, 12], F32) for i in range(2)]
                    on = [sb(s4, f"on{i}", [128, 512], F32, 4) for i in range(2)]

                    def p1d_gen(c):
                        i2 = c % 2
                        cl = c % 4
                        bA, bB, bT = 2 + 3 * i2, 3 + 3 * i2, 4 + 3 * i2
                        tok = slice(c * 128, (c + 1) * 128)
                        cs = slice(cl * 128, (cl + 1) * 128)
                        bst_, mv_, rs4_, on_ = bst[i2], mv[i2], rs4[i2], on[i2]
                        for k in range(KC):
                            kb.op(PE, lambda k=k: TE.matmul(pf(bA), lhsT=hT.t[:, k, tok], rhs=wr.t[:, k, 1024:1536], start=(k == 0), stop=(k == KC - 1)),
                                  reads=[hT.bs[c], wr.bs[2]], writes=[PS[bA].b], inc=(k == KC - 1))
                        yield
                        for k in range(KC):
                            kb.op(PE, lambda k=k: TE.matmul(pf(bB), lhsT=hT.t[:, k, tok], rhs=wr.t[:, k, 1536:2048], start=(k == 0), stop=(k == KC - 1)),
                                  reads=[hT.bs[c], wr.bs[3]], writes=[PS[bB].b], inc=(k == KC - 1))
                        yield
                        vs, sg_, kd_, ab = v_sb[i2], sgl[i2], kd[i2], attb[i2]
                        kb.op(ACT, lambda: A.copy(out=vs.t[:].rearrange("p a b -> p (a b)"), in_=pf(bA)), reads=[PS[bA].b], writes=[vs.b])
                        yield
                        kb.op(ACT, lambda: A.activation(out=sg_.t[:], in_=pf(bB), func=AF.Silu), reads=[PS[bB].b], writes=[sg_.b])
                        yield
                        transposes([kdT.t[:, h, cs] for h in range(4)], bT, [kdT.b])
                        yield
                        kb.op(ACT, lambda: A.copy(out=kd_.t[:].rearrange("p a b -> p (a b)"), in_=pb(bT)[:, 0:512]), reads=[PS[bT].b], writes=[kd_.b])
                        yield
                        for h in range(4):
                            kb.op(PE, lambda h=h: TE.matmul(pf(bA)[:, h * 128:(h + 1) * 128], lhsT=kTr.t[:, h, cs], rhs=qTr.t[:, h, cs], start=True, stop=True),
                                  reads=[kTr.b, qTr.b], writes=[PS[bA].b], inc=(h == 3))
                        yield
                        kb.op(DVE, lambda: V.tensor_tensor(out=ab.t[:], in0=pf(bA).rearrange("p (a b) -> p a b", b=128), in1=idT.t[:], op=ALU.mult),
                              reads=[PS[bA].b, idT.b], writes=[ab.b])
                        yield
                        while c > 0 and not state_done[c - 1]:
                            yield
                        for h in range(4):
                            kb.op(PE, lambda h=h: TE.matmul(pf(bB)[:, h * 128:(h + 1) * 128], lhsT=ab.t[:, h, :], rhs=vs.t[:, h, :], start=True, stop=(c == 0)),
                                  reads=[ab.b, vs.b], writes=[PS[bB].b], inc=(c == 0 and h == 3))
                            if c > 0:
                                kb.op(PE, lambda h=h: TE.matmul(pf(bB)[:, h * 128:(h + 1) * 128], lhsT=qdT.t[:, h, cs], rhs=state_b.t[:, h, :], start=False, stop=True),
                                      reads=[qdT.b, state_b.b], writes=[PS[bB].b], inc=(h == 3))
                        yield
                        if c < NT - 1:
                            for h in range(4):
                                kb.op(PE, lambda h=h: TE.matmul(pf(bA)[:, h * 128:(h + 1) * 128], lhsT=kd_.t[:, h, :], rhs=vs.t[:, h, :], start=True, stop=True),
                                      reads=[kd_.b, vs.b], writes=[PS[bA].b], inc=(h == 3))
                            yield
                            for h in range(4):
                                kb.op(DVE, lambda h=h: V.scalar_tensor_tensor(out=state_f.t[:, h, :], in0=state_f.t[:, h, :], scalar=cds[h],
                                                                              in1=pf(bA)[:, h * 128:(h + 1) * 128], op0=ALU.mult, op1=ALU.add),
                                      reads=[state_f.bs[h], PS[bA].b], writes=[state_f.bs[h]])
                            kb.op(POOL, lambda: G.tensor_copy(out=state_b.t[:], in_=state_f.t[:]), reads=state_f.bs, writes=[state_b.b])
                        state_done[c] = True
                        yield
                        for h in range(4):
                            kb.op(DVE, lambda h=h: V.bn_stats(out=bst_.t[:, h, :], in_=pf(bB)[:, h * 128:(h + 1) * 128]), reads=[PS[bB].b], writes=[bst_.bs[h]])
                        yield
                        for h in range(4):
                            kb.op(DVE, lambda h=h: V.bn_aggr(out=mv_.t[:, h, :], in_=bst_.t[:, h, :]), reads=[bst_.bs[h]], writes=[mv_.bs[h]])
                        yield
                        kb.op(DVE, lambda: V.tensor_scalar(out=rs4_.t[:, 0:4], in0=mv_.t[:, :, 1], scalar1=EPS, scalar2=None, op0=ALU.add), reads=mv_.bs, writes=[rs4_.b])
                        yield
                        kb.op(POOL, lambda: G.tensor_tensor(out=rs4_.t[:, 8:12], in0=rs4_.t[:, 0:4], in1=nhalf.t[:, 0:4], op=ALU.pow),
                              reads=[rs4_.b, nhalf.b], writes=[rs4_.b])
                        yield
                        for h in range(4):
                            kb.op(DVE, lambda h=h: V.tensor_scalar(out=on_.t[:, h * 128:(h + 1) * 128], in0=pf(bB)[:, h * 128:(h + 1) * 128],
                                                                   scalar1=mv_.t[:, h, 0:1], scalar2=rs4_.t[:, 8 + h:9 + h], op0=ALU.subtract, op1=ALU.mult),
                                  reads=[PS[bB].b, mv_.bs[h], rs4_.b], writes=[on_.bs[h]])
                        yield
                        kb.op(POOL, lambda: G.tensor_tensor(out=on_.t[:], in0=on_.t[:], in1=gn.t[:], op=ALU.mult), reads=on_.bs + [gn.b], writes=on_.bs)
                        yield
                        y_ = yr[i2]
                        kb.op(DVE, lambda: V.tensor_tensor(out=y_.t[:], in0=on_.t[:], in1=sg_.t[:], op=ALU.mult), reads=on_.bs + [sg_.b], writes=[y_.b])
                        if yrdbg is not None:
                            kb.op(DVE, lambda: V.tensor_tensor(out=yrdbg.t[:], in0=on_.t[:], in1=sg_.t[:], op=ALU.mult), reads=on_.bs + [sg_.b], writes=[yrdbg.b])
                            dbg_store("y_ret", yrdbg.t[:], tok, [yrdbg.b])
                        yield
                        transposes([y_.t[:, k * 128:(k + 1) * 128] for k in range(4)], bT, [y_.b])
                        yield
                        kb.op(ACT, lambda: A.copy(out=yretT.t[:, :, tok], in_=pb(bT)[:, 0:512].rearrange("p (a b) -> p a b", b=128)),
                              reads=[PS[bT].b], writes=[yretT.bs[c]])
                        yield

                    for tg in range(4):
                        tks = slice(tg * 512, (tg + 1) * 512)
                        hbs = [hT.bs[4 * tg + i] for i in range(4)]
                        for qk in range(2):
                            for h in range(4):
                                bk = pcnt[0] % 2
                                pcnt[0] += 1
                                for k in range(KC):
                                    kb.op(PE, lambda k=k, qk=qk, h=h, bk=bk: TE.matmul(pf(bk), lhsT=wr.t[:, k, qk * 512 + h * 128:qk * 512 + (h + 1) * 128],
                                                                                      rhs=hT.t[:, k, tks], start=(k == 0), stop=(k == KC - 1)),
                                          reads=hbs + [wr.bs[qk]], writes=[PS[bk].b], inc=(k == KC - 1))
                                pv4 = pf(bk).rearrange("p (a b) -> p a b", b=128)
                                if qk == 0:
                                    kb.op(ACT, lambda h=h, bk=bk: A.copy(out=qTr.t[:, h, :], in_=pf(bk)), reads=[PS[bk].b], writes=[qTr.b])
                                    kb.op(DVE, lambda h=h, pv4=pv4: V.tensor_tensor(out=qdT.t[:, h, :].rearrange("p (a b) -> p a b", b=128), in0=pv4,
                                                                                    in1=qdec.t[:, h:h + 1, :].broadcast_to([128, 4, 128]), op=ALU.mult),
                                          reads=[PS[bk].b, qdec.b], writes=[qdT.b])
                                else:
                                    kb.op(ACT, lambda h=h, bk=bk: A.mul(out=kTr.t[:, h, :], in_=pf(bk), mul=KS), reads=[PS[bk].b], writes=[kTr.b])
                                    kb.op(DVE, lambda h=h, pv4=pv4: V.scalar_tensor_tensor(out=kdT.t[:, h, :].rearrange("p (a b) -> p a b", b=128), in0=pv4, scalar=KS,
                                                                                           in1=kdec.t[:, h:h + 1, :].broadcast_to([128, 4, 128]),
                                                                                           op0=ALU.mult, op1=ALU.mult),
                                          reads=[PS[bk].b, kdec.b], writes=[kdT.b])
                        run_interleaved([(lambda c=c: p1d_gen(c)) for c in range(4 * tg, 4 * tg + 4)], width=2)

            def phase_1e():
                with ExitStack() as s5:
                    xt = [sb(s5, f"xte{i}", [128, D], F32) for i in range(2)]
                    g2bc = sb(s5, "g2bc", [128, D], F32)
                    kb.dma(SP, g2bc.t[:], norm2_g[0:1, :].broadcast_to([128, D]), writes=[g2bc.b])
                    wo = sb(s5, "wo", [128, KC, D], BF16)
                    load_w(wo, w_out[0].rearrange("(k p) d -> p k d", p=128))
                    gates = [sb(s5, f"gates{i}", [128, D], F32, 2) for i in range(2)]
                    tmix = [sb(s5, f"tmix{i}", [128, D], F32, 2) for i in range(2)]
                    tmix2 = [sb(s5, f"tmix2{i}", [128, D], F32, 2) for i in range(2)]
                    mixed = [sb(s5, f"mixed{i}", [128, D], BF16) for i in range(2)]
                    mixT = [sb(s5, f"mixT{i}", [128, KC, 128], BF16) for i in range(2)]
                    x1t = [sb(s5, f"x1t{i}", [128, D], F32, 2) for i in range(2)]

                    def p1e_gen(c):
                        i2 = c % 2
                        bs_ = [4 * i2 + i for i in range(4)]
                        tok = slice(c * 128, (c + 1) * 128)
                        xx = xt[i2]
                        kb.dma(SP, xx.t[:], x[tok, :], writes=[xx.b])
                        gt_, tm = gates[i2], (tmix[i2], tmix2[i2])
                        for n, yT in ((0, ynsaT), (1, yretT)):
                            for half in range(2):
                                j = 2 * n + half
                                for k in range(KC):
                                    kb.op(PE, lambda j=j, k=k, half=half: TE.matmul(pf(bs_[half]), lhsT=hT.t[:, k, tok], rhs=wm.t[:, k, j * 512:(j + 1) * 512],
                                                                                    start=(k == 0), stop=(k == KC - 1)),
                                          reads=[hT.bs[c], wm.bs[j]], writes=[PS[bs_[half]].b], inc=(k == KC - 1))
                                yield
                            for half in range(2):
                                bk = bs_[2 + half]
                                for k in range(4):
                                    kb.op(PE, lambda n=n, half=half, k=k, bk=bk, yT=yT: TE.matmul(pf(bk), lhsT=yT.t[:, k, tok], rhs=wbr.t[:, n * 4 + k, half * 512:(half + 1) * 512],
                                                                                                  start=(k == 0), stop=(k == 3)),
                                          reads=[yT.bs[c], wbr.b], writes=[PS[bk].b], inc=(k == 3))
                                yield
                            for half in range(2):
                                kb.op(ACT, lambda half=half: A.activation(out=gt_.t[:, half * 512:(half + 1) * 512], in_=pf(bs_[half]), func=AF.Sigmoid),
                                      reads=[PS[bs_[half]].b], writes=[gt_.bs[half]])
                                yield
                            for half in range(2):
                                hs = slice(half * 512, (half + 1) * 512)
                                kb.op(DVE, lambda half=half, hs=hs, n=n: V.tensor_tensor(out=tm[n].t[:, hs], in0=gt_.t[:, hs], in1=pf(bs_[2 + half]), op=ALU.mult),
                                      reads=[gt_.bs[half], PS[bs_[2 + half]].b], writes=[tm[n].bs[half]])
                                yield
                        mx = mixed[i2]
                        kb.op(POOL, lambda: G.tensor_tensor(out=mx.t[:], in0=tm[0].t[:], in1=tm[1].t[:], op=ALU.add), reads=tm[0].bs + tm[1].bs, writes=[mx.b])
                        yield
                        transposes([mx.t[:, k * 128:(k + 1) * 128] for k in range(KC)], bs_[0], [mx.b])
                        yield
                        mt = mixT[i2]
                        kb.op(ACT, lambda: A.copy(out=mt.t[:], in_=pb(bs_[0]).rearrange("p (a b) -> p a b", b=128)), reads=[PS[bs_[0]].b], writes=[mt.b])
                        yield
                        for half in range(2):
                            for k in range(KC):
                                kb.op(PE, lambda half=half, k=k: TE.matmul(pf(bs_[2 + half]), lhsT=mt.t[:, k, :], rhs=wo.t[:, k, half * 512:(half + 1) * 512],
                                                                           start=(k == 0), stop=(k == KC - 1)),
                                      reads=[mt.b, wo.b], writes=[PS[bs_[2 + half]].b], inc=(k == KC - 1))
                            yield
                        x1 = x1t[i2]
                        for half in range(2):
                            hs = slice(half * 512, (half + 1) * 512)
                            kb.op(DVE, lambda half=half, hs=hs: V.tensor_tensor(out=x1.t[:, hs], in0=xx.t[:, hs], in1=pf(bs_[2 + half]), op=ALU.add),
                                  reads=[xx.b, PS[bs_[2 + half]].b], writes=[x1.bs[half]])
                            yield
                        kb.dma(SP, xmid[tok, :], x1.t[:], reads=x1.bs)
                        dbg_store("x1", x1.t[:], tok, x1.bs)
                        yield from norm_gen(x1, c, g2bc, i2, bs_[1])

                    run_interleaved([(lambda c=c: p1e_gen(c)) for c in range(NT)], width=2)

            def phase_2():
                with ExitStack() as s6:
                    xt = [sb(s6, f"xtf{i}", [128, D], F32) for i in range(2)]
                    wd = sb(s6, "wd", [128, NFB, D], BF16, 2)
                    wd_v = ffn_w_down[0].rearrange("(fb p) d -> p fb d", p=128)
                    wgs = [sb(s6, f"wgs{i}", [128, KC, 256], BF16) for i in range(2)]
                    wus = [sb(s6, f"wus{i}", [128, KC, 256], BF16) for i in range(2)]
                    act = sb(s6, "act", [128, NFB, 1024], BF16, NFB)
                    sgs = [sb(s6, f"sgs{i}", [128, 512], F32) for i in range(2)]
                    outt = [sb(s6, f"outt{i}", [128, D], F32, 2) for i in range(2)]
                    wg_v = ffn_w_gate[0].rearrange("(k p) f -> p k f", p=128)
                    wu_v = ffn_w_up[0].rearrange("(k p) f -> p k f", p=128)
                    cn = [0, 0]
                    for hf in range(2):
                        for fg in range(11):
                            cols = slice(fg * 256, (fg + 1) * 256)
                            wg_, wu_ = wgs[fg % 2], wus[fg % 2]
                            kb.dma(POOL, wg_.t[:], wg_v[:, :, cols], writes=[wg_.b])
                            kb.dma(POOL, wu_.t[:], wu_v[:, :, cols], writes=[wu_.b])
                            if hf == 0 and fg == 1:
                                kb.dma(POOL, wd.t[:, 0:11, :], wd_v[:, 0:11, :], writes=[wd.bs[0]])
                                kb.dma(POOL, wd.t[:, 11:22, :], wd_v[:, 11:22, :], writes=[wd.bs[1]])
                            for fl in range(2):
                                fb = fg * 2 + fl
                                for t2 in range(2):
                                    tokc = slice(hf * 1024 + t2 * 512, hf * 1024 + (t2 + 1) * 512)
                                    hbs = [hT.bs[hf * 8 + t2 * 4 + i] for i in range(4)]
                                    gb, ub = (0, 1) if cn[0] % 2 == 0 else (2, 3)
                                    cn[0] += 1
                                    for k in range(KC):
                                        kb.op(PE, lambda k=k, fl=fl, gb=gb, wg_=wg_, tokc=tokc: TE.matmul(pf(gb), lhsT=wg_.t[:, k, fl * 128:(fl + 1) * 128], rhs=hT.t[:, k, tokc],
                                                                                                          start=(k == 0), stop=(k == KC - 1)),
                                              reads=hbs + [wg_.b], writes=[PS[gb].b], inc=(k == KC - 1))
                                    for k in range(KC):
                                        kb.op(PE, lambda k=k, fl=fl, ub=ub, wu_=wu_, tokc=tokc: TE.matmul(pf(ub), lhsT=wu_.t[:, k, fl * 128:(fl + 1) * 128], rhs=hT.t[:, k, tokc],
                                                                                                          start=(k == 0), stop=(k == KC - 1)),
                                              reads=hbs + [wu_.b], writes=[PS[ub].b], inc=(k == KC - 1))
                                    sg_ = sgs[cn[0] % 2]
                                    kb.op(ACT, lambda sg_=sg_, gb=gb: A.activation(out=sg_.t[:], in_=pf(gb), func=AF.Silu), reads=[PS[gb].b], writes=[sg_.b])
                                    kb.op(DVE, lambda sg_=sg_, ub=ub, fb=fb, t2=t2: V.tensor_tensor(out=act.t[:, fb, t2 * 512:(t2 + 1) * 512], in0=sg_.t[:], in1=pf(ub), op=ALU.mult),
                                          reads=[sg_.b, PS[ub].b], writes=[act.bs[fb]])
                        for tl in range(8):
                            c = hf * 8 + tl
                            tok = slice(c * 128, (c + 1) * 128)
                            xx = xt[c % 2]
                            kb.dma(SP, xx.t[:], xmid[tok, :], writes=[xx.b])
                            ob = (4, 5) if cn[1] % 2 == 0 else (6, 7)
                            cn[1] += 1
                            for half in range(2):
                                for fb in range(NFB):
                                    kb.op(PE, lambda half=half, fb=fb, ob=ob, tl=tl: TE.matmul(pf(ob[half]), lhsT=act.t[:, fb, tl * 128:(tl + 1) * 128],
                                                                                               rhs=wd.t[:, fb, half * 512:(half + 1) * 512],
                                                                                               start=(fb == 0), stop=(fb == NFB - 1)),
                                          reads=[act.bs[fb], wd.bs[0 if fb < 11 else 1]], writes=[PS[ob[half]].b], inc=(fb == NFB - 1))
                            ot = outt[c % 2]
                            for half in range(2):
                                hs = slice(half * 512, (half + 1) * 512)
                                kb.op(DVE, lambda half=half, hs=hs, ob=ob, ot=ot, xx=xx: V.tensor_tensor(out=ot.t[:, hs], in0=xx.t[:, hs], in1=pf(ob[half]), op=ALU.add),
                                      reads=[xx.b, PS[ob[half]].b], writes=[ot.bs[half]])
                            kb.dma(SP, out[tok, :], ot.t[:], reads=ot.bs, is_out=True)

            with ExitStack() as sB:
                ynsaT = sb(sB, "ynsaT", [128, 4, S], BF16, NT)
                yretT = sb(sB, "yretT", [128, 4, S], BF16, NT)

                with ExitStack() as sA:
                    gq = sb(sA, "gq", [128, 64], F32)
                    gk = sb(sA, "gk", [128, 3, 64], F32)
                    QAL = sb(sA, "QAL", [128, NT, 8, 4], BF16)
                    KAL = sb(sA, "KAL", [128, NT, 4], BF16)
                    KCAL = sb(sA, "KCAL", [128, 4], BF16)
                    OH = sb(sA, "OH", [128, NT, 32], BF16)
                    dmask8 = sb(sA, "dmask8", [128, 8, 128], BF16)
                    tmask8 = sb(sA, "tmask8", [128, 8, 128], BF16)
                    cmask = sb(sA, "cmask", [128, S], BF16)
                    addc = sb(sA, "addc", [128, NT, 32], F32)
                    ov = sb(sA, "ov", [128, 32], BF16)
                    KT_slc = sb(sA, "KT_slc", [128, 2, S], BF16, NT)
                    KT_win = sb(sA, "KT_win", [128, 2, S], BF16, NT)
                    V_slc = sb(sA, "V_slc", [128, NT, 2, 65], BF16, NT)
                    V_win = sb(sA, "V_win", [128, NT, 2, 65], BF16, NT)
                    KcT = sb(sA, "KcT", [128, 2, 128], BF16)
                    Vc = sb(sA, "Vc", [128, 2, 97], BF16)

                    with ExitStack() as s0:
                        SL = sb(s0, "SL", [128, 8], F32)
                        th128 = sb(s0, "th128", [128, NT], F32)
                        pidx = sb(s0, "pidx", [128, 1], F32)
                        QALf = sb(s0, "QALf", [128, NT, 8, 4], F32)
                        KALf = sb(s0, "KALf", [128, NT, 4], F32)
                        KCALf = sb(s0, "KCALf", [128, 4], F32)
                        rel = sb(s0, "rel", [128, NT, 32], F32)
                        f0 = sb(s0, "f0", [128, NT, 32], F32)
                        f1 = sb(s0, "f1", [128, NT, 32], F32)
                        t1 = sb(s0, "t1", [128, NT, 32], F32)
                        hp = sb(s0, "hp", [128, 1], F32)
                        ovf = sb(s0, "ovf", [128, 32], F32)
                        ova = sb(s0, "ova", [128, 32], F32)
                        ones_b = sb(s0, "ones_b", [128, 512], BF16)
                        ones_b2 = sb(s0, "ones_b2", [128, 1024], BF16)
                        zeros_b = sb(s0, "zeros_b", [128, 512], BF16)

                        kb.dma(SP, gq.t[:], nsa_q_norm[0:1, :].broadcast_to([128, 64]), writes=[gq.b])
                        kb.dma(SP, gk.t[:].rearrange("p a b -> p (a b)"),
                               nsa_k_norm.rearrange("o a b -> o (a b)").broadcast_to([128, 192]), writes=[gk.b])
                        kb.op(DVE, lambda: V.tensor_scalar(out=gq.t[:], in0=gq.t[:], scalar1=0.125, scalar2=None, op0=ALU.mult),
                              reads=[gq.b], writes=[gq.b])
                        for h in range(8):
                            kb.op(POOL, lambda h=h: G.memset(SL.t[:, h:h + 1], 2.0 ** (-(h + 1))), writes=[SL.b])
                        kb.op(POOL, lambda: G.iota(th128.t[:], pattern=[[128, NT]], base=0, channel_multiplier=0,
                                                   allow_small_or_imprecise_dtypes=True), writes=[th128.b])
                        kb.op(POOL, lambda: G.iota(pidx.t[:], pattern=[[0, 1]], base=0, channel_multiplier=1,
                                                   allow_small_or_imprecise_dtypes=True), writes=[pidx.b])
                        SLb = SL.t[:].unsqueeze(1).broadcast_to([128, NT, 8])
                        THb = th128.t[:].unsqueeze(2).broadcast_to([128, NT, 8])
                        kb.op(DVE, lambda: V.scalar_tensor_tensor(out=QALf.t[:, :, :, 0], in0=THb, scalar=-1.0, in1=SLb,
                                                                  op0=ALU.mult, op1=ALU.mult),
                              reads=[SL.b, th128.b], writes=[QALf.b])
                        kb.op(DVE, lambda: V.tensor_scalar(out=QALf.t[:, :, :, 1], in0=SLb, scalar1=pidx.t[:, 0:1], scalar2=-1.0,
                                                           op0=ALU.mult, op1=ALU.mult),
                              reads=[SL.b, pidx.b], writes=[QALf.b])
                        kb.op(DVE, lambda: V.tensor_copy(out=QALf.t[:, :, :, 2], in_=SLb), reads=[SL.b], writes=[QALf.b])
                        kb.op(DVE, lambda: V.tensor_copy(out=QALf.t[:, :, :, 3], in_=SLb), reads=[SL.b], writes=[QALf.b])
                        kb.op(DVE, lambda: V.tensor_copy(out=QAL.t[:], in_=QALf.t[:]), reads=[QALf.b], writes=[QAL.b])
                        kb.op(POOL, lambda: G.memset(KALf.t[:, :, 0:2], 1.0), writes=[KALf.b])
                        kb.op(DVE, lambda: V.tensor_copy(out=KALf.t[:, :, 2], in_=th128.t[:]), reads=[th128.b], writes=[KALf.b])
                        kb.op(DVE, lambda: V.tensor_copy(out=KALf.t[:, :, 3], in_=pidx.t[:, 0:1].broadcast_to([128, NT])),
                              reads=[pidx.b], writes=[KALf.b])
                        kb.op(DVE, lambda: V.tensor_copy(out=KAL.t[:], in_=KALf.t[:]), reads=[KALf.b], writes=[KAL.b])
                        kb.op(POOL, lambda: G.memset(KCALf.t[:, 0:2], 1.0), writes=[KCALf.b])
                        kb.op(POOL, lambda: G.memset(KCALf.t[:, 3:4], 31.0), reads=[], writes=[KCALf.b])
                        kb.op(DVE, lambda: V.tensor_scalar(out=KCALf.t[:, 2:3], in0=pidx.t[:, 0:1], scalar1=16.0, scalar2=None,
                                                           op0=ALU.mult), reads=[pidx.b], writes=[KCALf.b])
                        kb.op(DVE, lambda: V.tensor_copy(out=KCAL.t[:], in_=KCALf.t[:]), reads=[KCALf.b], writes=[KCAL.b])
                        kb.op(POOL, lambda: G.memset(OH.t[:], 0.0), writes=[OH.b])
                        for kt in range(NT):
                            kb.op(POOL, lambda kt=kt: G.memset(OH.t[0:64, kt, 2 * kt:2 * kt + 1], 1.0), writes=[OH.b])
                            kb.op(POOL, lambda kt=kt: G.memset(OH.t[64:128, kt, 2 * kt + 1:2 * kt + 2], 1.0), writes=[OH.b])
                        kb.op(POOL, lambda: G.memset(ones_b.t[:], 1.0), writes=[ones_b.b])
                        kb.op(POOL, lambda: G.memset(zeros_b.t[:], 0.0), writes=[zeros_b.b])
                        ob8 = ones_b2.t[:].rearrange("p (a b) -> p a b", b=128)
                        kb.op(POOL, lambda: G.memset(ones_b2.t[:], 1.0), writes=[ones_b2.b])
                        kb.op(POOL, lambda: G.affine_select(out=dmask8.t[:], in_=ob8, pattern=[[0, 8], [1, 128]],
                                                            compare_op=ALU.is_ge, fill=0.0, base=0, channel_multiplier=-1),
                              reads=[ones_b2.b], writes=[dmask8.b])
                        kb.op(POOL, lambda: G.affine_select(out=tmask8.t[:], in_=ob8, pattern=[[0, 8], [-1, 128]],
                                                            compare_op=ALU.is_gt, fill=0.0, base=0, channel_multiplier=1),
                              reads=[ones_b2.b], writes=[tmask8.b])
                        for i in range(4):
                            kb.op(POOL, lambda i=i: G.affine_select(out=cmask.t[:, i * 512:(i + 1) * 512], in_=zeros_b.t[:],
                                                                    pattern=[[1, 512]], compare_op=ALU.is_ge, fill=NEG,
                                                                    base=-31 + 512 * i, channel_multiplier=-16),
                                  reads=[zeros_b.b], writes=[cmask.b])
                        kb.op(POOL, lambda: G.iota(rel.t[:], pattern=[[-2, NT], [1, 32]], base=0, channel_multiplier=0,
                                                   allow_small_or_imprecise_dtypes=True), writes=[rel.b])
                        kb.op(DVE, lambda: V.tensor_scalar(out=hp.t[:], in0=pidx.t[:], scalar1=64.0, scalar2=None, op0=ALU.is_ge),
                              reads=[pidx.b], writes=[hp.b])
                        kb.op(DVE, lambda: V.tensor_scalar(out=rel.t[:], in0=rel.t[:], scalar1=hp.t[:, 0:1], scalar2=None,
                                                           op0=ALU.subtract), reads=[rel.b, hp.b], writes=[rel.b])
                        kb.op(DVE, lambda: V.tensor_scalar(out=t1.t[:], in0=rel.t[:], scalar1=0.0, scalar2=-1e9,
                                                           op0=ALU.is_gt, op1=ALU.mult), reads=[rel.b], writes=[t1.b])
                        kb.op(DVE, lambda: V.tensor_scalar(out=f0.t[:], in0=rel.t[:], scalar1=0.0, scalar2=None, op0=ALU.is_equal),
                              reads=[rel.b], writes=[f0.b])
                        kb.op(DVE, lambda: V.tensor_scalar(out=f1.t[:], in0=rel.t[:], scalar1=-1.0, scalar2=None, op0=ALU.is_equal),
                              reads=[rel.b], writes=[f1.b])
                        kb.op(DVE, lambda: V.tensor_tensor(out=f0.t[:], in0=f0.t[:], in1=f1.t[:], op=ALU.max),
                              reads=[f0.b, f1.b], writes=[f0.b])
                        kb.op(DVE, lambda: V.memset(f0.t[:, :, 0:1], 1.0), reads=[], writes=[f0.b])
                        kb.op(DVE, lambda: V.scalar_tensor_tensor(out=addc.t[:], in0=f0.t[:], scalar=1e4, in1=t1.t[:],
                                                                  op0=ALU.mult, op1=ALU.add), reads=[f0.b, t1.b], writes=[addc.b])
                        kb.op(POOL, lambda: G.iota(ovf.t[:], pattern=[[-64, 32]], base=0, channel_multiplier=16,
                                                   allow_small_or_imprecise_dtypes=True), writes=[ovf.b])
                        kb.op(DVE, lambda: V.tensor_scalar(out=ova.t[:], in0=ovf.t[:], scalar1=63.0, scalar2=None, op0=ALU.is_le),
                              reads=[ovf.b], writes=[ova.b])
                        kb.op(DVE, lambda: V.tensor_scalar(out=ovf.t[:], in0=ovf.t[:], scalar1=-31.0, scalar2=None, op0=ALU.is_ge),
                              reads=[ovf.b], writes=[ovf.b])
                        kb.op(DVE, lambda: V.tensor_tensor(out=ov.t[:], in0=ova.t[:], in1=ovf.t[:], op=ALU.mult),
                              reads=[ova.b, ovf.b], writes=[ov.b])
                        kb.op(POOL, lambda: G.memset(V_slc.t[:, :, :, 64:65], 1.0), writes=V_slc.bs)
                        kb.op(POOL, lambda: G.memset(V_win.t[:, :, :, 64:65], 1.0), writes=V_win.bs)
                        kb.barrier()
                        chk(1)

                    with ExitStack() as s2:
                        cmpT = sb(s2, "cmpT", [128, 2, S], BF16, NT)
                        w1sb = sb(s2, "w1sb", [128, 2, 32, 128], BF16)
                        w2sb = sb(s2, "w2sb", [128, 2, 64], BF16)
                        pe_sb = sb(s2, "pe_sb", [32, 2, 64], F32)
                        with ExitStack() as s2a:
                            xt = [sb(s2a, f"xta{i}", [128, D], F32) for i in range(2)]
                            g1bc = sb(s2a, "g1bc", [128, D], F32)
                            kb.dma(SP, g1bc.t[:], norm1_g[0:1, :].broadcast_to([128, D]), writes=[g1bc.b])

                            def p1a_gen(c):
                                xx = xt[c % 2]
                                kb.dma(SP, xx.t[:], x[c * 128:(c + 1) * 128, :], writes=[xx.b])
                                yield
                                yield from norm_gen(xx, c, g1bc, c % 2, 6 + (c % 2))

                            wkv = sb(s2a, "wkv", [128, KC, 768], BF16)
                            load_w(wkv, w_in_v[:, :, C_KV:C_KV + 768])
                            for kv in range(2):
                                src = cmp_w1[0, kv].rearrange("l d f -> d l f")
                                kb.dma(POOL, w1sb.t[0:64, kv], src, writes=[w1sb.b])
                                kb.dma(POOL, w1sb.t[64:128, kv], src, writes=[w1sb.b])
                            kb.dma(POOL, w2sb.t[:], cmp_w2[0].rearrange("k f d -> f k d"), writes=[w2sb.b])
                            kb.dma(SP, pe_sb.t[:], cmp_pe[0].rearrange("k l d -> l k d"), writes=[pe_sb.b])
                            cmp_tok = [sb(s2a, f"cmp_tok{i}", [128, 256], BF16) for i in range(2)]
                            sqk = [sb(s2a, f"sqk{i}", [128, 256], F32) for i in range(2)]
                            kst = sb(s2a, "kst", [128, NT, 16], F32, NT)
                            tmpk = [sb(s2a, f"tmpk{i}", [128, 4, 64], F32) for i in range(2)]
                            ka_slc = [sb(s2a, f"ka_slc{i}", [128, 2, 128], BF16, 2) for i in range(2)]
                            ka_win = [sb(s2a, f"ka_win{i}", [128, 2, 128], BF16, 2) for i in range(2)]
                            for i in range(2):
                                kb.op(POOL, lambda i=i: G.memset(ka_slc[i].t[:], 0.0), writes=ka_slc[i].bs)
                                kb.op(POOL, lambda i=i: G.memset(ka_win[i].t[:], 0.0), writes=ka_win[i].bs)
                            def p1b_gen(c):
                                i2 = c % 2
                                bA, bB = (0, 1) if i2 == 0 else (2, 3)
                                tok = slice(c * 128, (c + 1) * 128)
                                if CUT >= 1:
                                    yield
                                    for k in range(KC):
                                        kb.op(PE, lambda k=k: TE.matmul(pf(bA), lhsT=hT.t[:, k, tok], rhs=wkv.t[:, k, 0:512],
                                                                        start=(k == 0), stop=(k == KC - 1)),
                                              reads=[hT.bs[c], wkv.b], writes=[PS[bA].b], inc=(k == KC - 1))
                                    for k in range(KC):
                                        kb.op(PE, lambda k=k: TE.matmul(pf(bB)[:, 0:256], lhsT=hT.t[:, k, tok], rhs=wkv.t[:, k, 512:768],
                                                                        start=(k == 0), stop=(k == KC - 1)),
                                              reads=[hT.bs[c], wkv.b], writes=[PS[bB].b], inc=(k == KC - 1))
                                if CUT >= 2:
                                    yield
                                    ct = cmp_tok[i2]
                                    kb.op(ACT, lambda: A.copy(out=ct.t[:], in_=pf(bA)[:, 0:256]), reads=[PS[bA].b], writes=[ct.b])
                                    sq = sqk[i2]
                                    kb.op(ACT, lambda: A.activation(out=sq.t[:, 0:128], in_=pf(bA)[:, 256:384], func=AF.Square),
                                          reads=[PS[bA].b], writes=[sq.b])
                                    kb.op(ACT, lambda: A.activation(out=sq.t[:, 128:256], in_=pf(bB)[:, 0:128], func=AF.Square),
                                          reads=[PS[bB].b], writes=[sq.b])
                                if CUT >= 3:
                                    yield
                                    ks = kst.t
                                    ksb = [kst.bs[c]]
                                    kb.op(DVE, lambda: V.tensor_reduce(out=ks[:, c, 0:4], in_=sq.t[:].rearrange("p (a b) -> p a b", b=64),
                                                                       axis=AX.X, op=ALU.add), reads=[sq.b], writes=ksb)
                                    rstd_from_ss(ks[:, c, 0:4], ks[:, c, 4:8], ks[:, c, 8:12], ks[:, c, 12:16], 64, ksb)
                                    tk = tmpk[i2]
                                    kb.op(DVE, lambda: V.tensor_tensor(out=tk.t[:, 0:2, :], in0=pf(bA)[:, 256:384].rearrange("p (a b) -> p a b", b=64),
                                                                       in1=ks[:, c, 12:14].unsqueeze(2).broadcast_to([128, 2, 64]), op=ALU.mult),
                                          reads=[PS[bA].b] + ksb, writes=[tk.b])
                                    kb.op(DVE, lambda: V.tensor_tensor(out=tk.t[:, 2:4, :], in0=pf(bB)[:, 0:128].rearrange("p (a b) -> p a b", b=64),
                                                                       in1=ks[:, c, 14:16].unsqueeze(2).broadcast_to([128, 2, 64]), op=ALU.mult),
                                          reads=[PS[bB].b] + ksb, writes=[tk.b])
                                    ksl, kwn = ka_slc[i2], ka_win[i2]
                                    kb.op(DVE, lambda: V.tensor_tensor(out=ksl.t[:, :, 0:64], in0=tk.t[:, 0:2, :],
                                                                       in1=gk.t[:, 1:2, :].broadcast_to([128, 2, 64]), op=ALU.mult),
                                          reads=[tk.b, gk.b], writes=[ksl.bs[0]])
                                    kb.op(DVE, lambda: V.tensor_tensor(out=kwn.t[:, :, 0:64], in0=tk.t[:, 2:4, :],
                                                                       in1=gk.t[:, 2:3, :].broadcast_to([128, 2, 64]), op=ALU.mult),
                                          reads=[tk.b, gk.b], writes=[kwn.bs[0]])
                                if CUT >= 4:
                                    yield
                                    kb.op(POOL, lambda: G.tensor_copy(out=ksl.t[:, :, 64:96], in_=OH.t[:, c:c + 1, :].broadcast_to([128, 2, 32])),
                                          reads=[OH.b], writes=[ksl.bs[1]])
                                    kb.op(POOL, lambda: G.tensor_copy(out=ksl.t[:, :, 96:100], in_=KAL.t[:, c:c + 1, :].broadcast_to([128, 2, 4])),
                                          reads=[KAL.b], writes=[ksl.bs[1]])
                                    kb.op(POOL, lambda: G.tensor_copy(out=kwn.t[:, :, 96:100], in_=KAL.t[:, c:c + 1, :].broadcast_to([128, 2, 4])),
                                          reads=[KAL.b], writes=[kwn.bs[1]])
                                if CUT >= 5:
                                    yield
                                    kb.op(ACT, lambda: A.copy(out=V_slc.t[:, c, :, 0:64], in_=pf(bA)[:, 384:512].rearrange("p (a b) -> p a b", b=64)),
                                          reads=[PS[bA].b], writes=[V_slc.bs[c]])
                                    kb.op(ACT, lambda: A.copy(out=V_win.t[:, c, :, 0:64], in_=pf(bB)[:, 128:256].rearrange("p (a b) -> p a b", b=64)),
                                          reads=[PS[bB].b], writes=[V_win.bs[c]])
                                if CUT >= 6:
                                    yield
                                    tb = 4 + i2
                                    transposes([ksl.t[:, 0, :], ksl.t[:, 1, :], kwn.t[:, 0, :], kwn.t[:, 1, :], ct.t[:, 0:128], ct.t[:, 128:256]],
                                               tb, ksl.bs + kwn.bs + [ct.b])
                                    pv3 = pb(tb).rearrange("p (a b) -> p a b", b=128)
                                    kb.op(ACT, lambda: A.copy(out=KT_slc.t[:, :, tok], in_=pv3[:, 0:2, :]), reads=[PS[tb].b], writes=[KT_slc.bs[c]])
                                    kb.op(ACT, lambda: A.copy(out=KT_win.t[:, :, tok], in_=pv3[:, 2:4, :]), reads=[PS[tb].b], writes=[KT_win.bs[c]])
                                    kb.op(ACT, lambda: A.copy(out=cmpT.t[:, :, tok], in_=pv3[:, 4:6, :]), reads=[PS[tb].b], writes=[cmpT.bs[c]])
                            def p1ab_gen(c):
                                yield from p1a_gen(c)
                                yield from p1b_gen(c)

                            run_interleaved([(lambda c=c: p1ab_gen(c)) for c in range(NT)], width=2)
                            if "KT_slc" in dbg:
                                kdb = sb(s2a, "kdb", [128, 2, S], F32)
                                kb.op(DVE, lambda: V.tensor_copy(out=kdb.t[:], in_=KT_slc.t[:]), reads=KT_slc.bs, writes=[kdb.b])
                                kb.dma(SP, dbg["KT_slc"].rearrange("p (a b) -> p a b", b=S), kdb.t[:], reads=[kdb.b], is_out=True)
                            kb.barrier()
                            chk(3)

                        with ExitStack() as s2b:
                            peT = sb(s2b, "peT", [64, 2, 32], BF16)
                            bias_c = sb(s2b, "bias_c", [128, 2], F32)
                            xhs = [sb(s2b, f"xh{i}", [128, 128], F32) for i in range(4)]
                            x2s = [sb(s2b, f"x2{i}", [128, 128], F32) for i in range(4)]
                            sgs_ = [sb(s2b, f"sgc{i}", [128, 128], F32) for i in range(4)]
                            HTbs = [sb(s2b, f"HTb{i}", [128, 128], BF16) for i in range(4)]
                            kca = sb(s2b, "kca", [128, 2, 128], BF16)
                            cst = sb(s2b, "cst", [128, 8], F32)
                            tmpcs = [sb(s2b, f"tmpc{i}", [128, 64], F32) for i in range(4)]
                            kb.op(POOL, lambda: G.memset(kca.t[:], 0.0), writes=[kca.b])
                            kb.op(POOL, lambda: G.memset(Vc.t[:], 0.0), writes=[Vc.b])
                            kb.op(POOL, lambda: G.tensor_copy(out=kca.t[:, :, 96:100], in_=KCAL.t[:].unsqueeze(1).broadcast_to([128, 2, 4])),
                                  reads=[KCAL.b], writes=[kca.b])
                            kb.op(POOL, lambda: G.memset(Vc.t[:, :, 64:65], 1.0), writes=[Vc.b])
                            kb.op(POOL, lambda: G.tensor_copy(out=Vc.t[:, :, 65:97], in_=ov.t[:].unsqueeze(1).broadcast_to([128, 2, 32])),
                                  reads=[ov.b], writes=[Vc.b])
                            for kv in range(2):
                                kb.op(PE, lambda kv=kv: TE.transpose(out=pf(0)[0:64, kv * 32:(kv + 1) * 32], in_=pe_sb.t[0:32, kv, :],
                                                                     identity=ident_f.t[0:32, 0:32]),
                                      reads=[pe_sb.b, ident_f.b], writes=[PS[0].b])
                            kb.op(DVE, lambda: V.tensor_copy(out=peT.t[:], in_=pf(0)[0:64, 0:64].rearrange("p (a b) -> p a b", b=32)),
                                  reads=[PS[0].b], writes=[peT.b])
                            for kv in range(2):
                                for l in range(32):
                                    kb.op(PE, lambda kv=kv, l=l: TE.matmul(pf(1)[:, kv:kv + 1], lhsT=w1sb.t[0:64, kv, l, :], rhs=peT.t[0:64, kv, l:l + 1],
                                                                           start=(l == 0), stop=(l == 31)),
                                          reads=[w1sb.b, peT.b], writes=[PS[1].b], inc=(l == 31))
                            kb.op(DVE, lambda: V.tensor_copy(out=bias_c.t[:], in_=pf(1)[:, 0:2]), reads=[PS[1].b], writes=[bias_c.b])
                            def cmp_gen(kv, g, idx):
                                bH, bO = 2 * idx, 2 * idx + 1
                                xh, x2, sg, HTb, tmpc = xhs[idx], x2s[idx], sgs_[idx], HTbs[idx], tmpcs[idx]
                                for l in range(32):
                                    kb.op(PE, lambda kv=kv, g=g, l=l: TE.matmul(
                                        pf(bH)[:, 0:127], lhsT=w1sb.t[g * 64:(g + 1) * 64, kv, l, :],
                                        rhs=cmpT.t[g * 64:(g + 1) * 64, kv, l:l + 16 * 126 + 1:16],
                                        start=(l == 0), stop=(l == 31)),
                                        reads=[w1sb.b] + cmpT.bs, writes=[PS[bH].b], inc=(l == 31))
                                yield
                                kb.op(ACT, lambda kv=kv: A.activation(out=xh.t[:, 0:127], in_=pf(bH)[:, 0:127], func=AF.Identity,
                                                                      bias=bias_c.t[:, kv:kv + 1], scale=1.0),
                                      reads=[PS[bH].b, bias_c.b], writes=[xh.b])
                                yield
                                kb.op(DVE, lambda: V.tensor_tensor(out=x2.t[:, 0:127], in0=xh.t[:, 0:127], in1=xh.t[:, 0:127], op=ALU.mult),
                                      reads=[xh.b], writes=[x2.b])
                                yield
                                kb.op(DVE, lambda: V.tensor_scalar(out=x2.t[:, 0:127], in0=x2.t[:, 0:127], scalar1=0.044715, scalar2=1.0,
                                                                   op0=ALU.mult, op1=ALU.add), reads=[x2.b], writes=[x2.b])
                                yield
                                kb.op(DVE, lambda: V.tensor_tensor(out=x2.t[:, 0:127], in0=x2.t[:, 0:127], in1=xh.t[:, 0:127], op=ALU.mult),
                                      reads=[x2.b, xh.b], writes=[x2.b])
                                yield
                                kb.op(ACT, lambda: A.activation(out=sg.t[:, 0:127], in_=x2.t[:, 0:127], func=AF.Sigmoid, scale=1.5957691216057308),
                                      reads=[x2.b], writes=[sg.b])
                                yield
                                kb.op(DVE, lambda: V.tensor_tensor(out=HTb.t[:, 0:127], in0=xh.t[:, 0:127], in1=sg.t[:, 0:127], op=ALU.mult),
                                      reads=[xh.b, sg.b], writes=[HTb.b])
                                yield
                                kb.op(PE, lambda kv=kv: TE.matmul(pf(bO)[0:127, 0:64], lhsT=HTb.t[:, 0:127], rhs=w2sb.t[:, kv, :], start=True, stop=True),
                                      reads=[HTb.b, w2sb.b], writes=[PS[bO].b])
                                yield
                                if kv == 0:
                                    kb.op(ACT, lambda g=g: A.activation(out=tmpc.t[0:127, :], in_=pf(bO)[0:127, 0:64], func=AF.Square,
                                                                        accum_out=cst.t[0:127, g:g + 1]),
                                          reads=[PS[bO].b], writes=[tmpc.b, cst.b])
                                    rstd_from_ss(cst.t[0:127, g:g + 1], cst.t[0:127, 2 + g:3 + g], cst.t[0:127, 4 + g:5 + g], cst.t[0:127, 6 + g:7 + g], 64, [cst.b])
                                    kb.op(DVE, lambda g=g: V.scalar_tensor_tensor(out=kca.t[0:127, g, 0:64], in0=pf(bO)[0:127, 0:64],
                                                                                  scalar=cst.t[0:127, 6 + g:7 + g], in1=gk.t[0:127, 0, :],
                                                                                  op0=ALU.mult, op1=ALU.mult),
                                          reads=[PS[bO].b, cst.b, gk.b], writes=[kca.b])
                                else:
                                    kb.op(ACT, lambda g=g: A.copy(out=Vc.t[0:127, g, 0:64], in_=pf(bO)[0:127, 0:64]),
                                          reads=[PS[bO].b], writes=[Vc.b])
                                yield

                            run_interleaved([(lambda kv=kv, g=g: cmp_gen(kv, g, 2 * kv + g)) for kv in range(2) for g in range(2)], width=4)
                            transposes([kca.t[:, 0, :], kca.t[:, 1, :]], 6, [kca.b])
                            kb.op(ACT, lambda: A.copy(out=KcT.t[:], in_=pb(6)[:, 0:256].rearrange("p (a b) -> p a b", b=128)),
                                  reads=[PS[6].b], writes=[KcT.b])
                            if "kc" in dbg:
                                kcd = sb(s2b, "kcd", [128, 2, 64], F32)
                                kb.op(DVE, lambda: V.tensor_copy(out=kcd.t[:], in_=kca.t[:, :, 0:64]), reads=[kca.b], writes=[kcd.b])
                                kb.dma(SP, dbg["kc"].rearrange("p (a b) -> p a b", b=64), kcd.t[:], reads=[kcd.b], is_out=True)
                            if "vc" in dbg:
                                vcd = sb(s2b, "vcd", [128, 2, 64], F32)
                                kb.op(DVE, lambda: V.tensor_copy(out=vcd.t[:], in_=Vc.t[:, :, 0:64]), reads=[Vc.b], writes=[vcd.b])
                                kb.dma(SP, dbg["vc"].rearrange("p (a b) -> p a b", b=64), vcd.t[:], reads=[vcd.b], is_out=True)
                            kb.barrier()
                            chk(4)

                    phase_1c()
                    kb.barrier()
                    chk(5)

                wm = sb(sB, "wm", [128, KC, 2048], BF16, 4)
                wbr = sb(sB, "wbr", [128, 8, D], BF16)
                phase_1d()
                kb.barrier()
                chk(6)
                phase_1e()
                kb.barrier()
                chk(7)

            phase_2()
            kb.finish()
    except _Stop:
        pass
    return nc


_NAMES = ["x", "norm1_g", "w_in", "nsa_q_norm", "nsa_k_norm", "cmp_pe", "cmp_w1", "cmp_w2", "ret_gn_g",
          "w_branch", "w_out", "norm2_g", "ffn_w_gate", "ffn_w_up", "ffn_w_down"]


def kernel(**inputs):
    n = 8
    arrs = {k: np.ascontiguousarray(np.asarray(inputs[k], dtype=np.float32)) for k in _NAMES}
    nc = build_nc()
    in_maps = []
    for i in range(n):
        m = {k: arrs[k] for k in _NAMES if k != "x"}
        m["x"] = np.ascontiguousarray(arrs["x"][i])
        in_maps.append(m)
    res = run_bass_kernel_spmd(nc, in_maps, core_ids=list(range(n)))
    return np.stack([np.asarray(r["out"], dtype=np.float32) for r in res.results], axis=0)
```

```python
import numpy as np
from contextlib import ExitStack
import concourse.bass as bass
import concourse.mybir as mybir
from concourse.bass_utils import run_bass_kernel_spmd

F32 = mybir.dt.float32
BF16 = mybir.dt.bfloat16
AF = mybir.ActivationFunctionType
ALU = mybir.AluOpType
AX = mybir.AxisListType

S = 2048
D = 1024
NT = 16
KC = 8
N_IN = 5400
DFF = 2816
NFB = 22
EPS = 1e-6
SEM_LIMIT = 24000
import os as _os
CUT = int(_os.environ.get('P1B_CUT', '99'))
CUTC = int(_os.environ.get('P1C_CUT', '99'))
SUBC = int(_os.environ.get('P1C_SUB', '99'))
WIDTH = int(_os.environ.get('P1C_WIDTH', '2'))
NEG = -30000.0

C_Q = 0
C_KV = 512
C_G = 1280
C_R = 1304
C_M = 3352


class Buf:
    __slots__ = ("name", "w", "r", "excl")

    def __init__(self, name):
        self.name = name
        self.w = None
        self.r = []
        self.excl = False


class SemW:
    __slots__ = ("h",)

    def __init__(self, h):
        self.h = h


class Slot:
    __slots__ = ("sem", "val")

    def __init__(self, sem):
        self.sem = sem
        self.val = 0


class Q:
    def __init__(self, name, eng):
        self.name = name
        self.eng = eng
        self.sem = None
        self.count = 0
        self.waited = {}
        self.ring = []
        self.ri = 0
        self.pending = False


class T:
    def __init__(self, t, name, nb=1):
        self.t = t
        self.bs = [Buf(f"{name}{i}") for i in range(nb)]

    @property
    def b(self):
        return self.bs[0]


class KB:
    def __init__(self, nc, es):
        self.nc = nc
        self.es = es
        self.nsem = 0
        self.pe = self.mkq("pe", nc.tensor)
        self.act = self.mkq("act", nc.scalar)
        self.dve = self.mkq("dve", nc.vector)
        self.pool = self.mkq("pool", nc.gpsimd)
        self.sp = self.mkq("sp", nc.sync)
        self.qs = [self.pe, self.act, self.dve, self.pool, self.sp]
        for q, n in ((self.sp, 16), (self.pool, 8), (self.act, 4)):
            q.ring = [Slot(self.new_sem(f"{q.name}_d{i}")) for i in range(n)]
        self.out_toks = []

    def new_sem(self, name):
        self.nsem += 1
        return SemW(self.es.enter_context(self.nc.semaphore(f"{name}_{self.nsem}")))

    def mkq(self, name, eng):
        q = Q(name, eng)
        q.sem = self.new_sem(name)
        return q

    def wait(self, q, tok):
        sw, val = tok[0], tok[1]
        if q.waited.get(sw, 0) >= val:
            return
        q.eng.wait_ge(sw.h, val)
        q.waited[sw] = val

    def _dep(self, q, tok, raw, force=False):
        if tok[2] is q and q is self.pe and not force:
            return
        self.wait(q, tok)

    def _deps(self, q, reads, writes, force=False):
        for b in reads:
            if b.w is not None:
                self._dep(q, b.w, True, force)
            if b.excl:
                for t in b.r:
                    if t[2] is not q:
                        self._dep(q, t, False, force)
        for b in writes:
            if b.w is not None:
                self._dep(q, b.w, False, force)
            for t in b.r:
                self._dep(q, t, False, force)

    def _record(self, tok, reads, writes):
        for b in reads:
            if tok[2] is not None:
                b.r = [t for t in b.r if t[2] is not tok[2]]
            b.r.append(tok)
        for b in writes:
            b.w = tok
            b.r = []

    def op(self, q, fn, reads=(), writes=(), inc=True):
        self._deps(q, reads, writes)
        ins = fn()
        if inc:
            if q.count >= SEM_LIMIT and not q.pending:
                q.sem = self.new_sem(q.name)
                q.count = 0
            ins.then_inc(q.sem.h, 1)
            q.count += 1
            q.pending = False
            tok = (q.sem, q.count, q)
        else:
            q.pending = True
            tok = (q.sem, q.count + 1, q)
        self._record(tok, reads, writes)
        return ins

    def dma(self, q, out, in_, reads=(), writes=(), is_out=False):
        self._deps(q, reads, writes, force=True)
        slot = q.ring[q.ri % len(q.ring)]
        q.ri += 1
        if slot.val > 0:
            self.wait(q, (slot.sem, slot.val))
        if slot.val >= SEM_LIMIT:
            slot.sem = self.new_sem(q.name + "_d")
            slot.val = 0
        ins = q.eng.dma_start(out=out, in_=in_)
        ins.then_inc(slot.sem.h, 16)
        slot.val += 16
        tok = (slot.sem, slot.val, None)
        self._record(tok, reads, writes)
        if is_out:
            self.out_toks.append(tok)
        return tok

    def barrier(self):
        toks = []
        for o in self.qs:
            if o.count > 0:
                toks.append((o.sem, o.count, o))
            for sl in o.ring:
                if sl.val > 0:
                    toks.append((sl.sem, sl.val, None))
        for q in self.qs:
            for t in toks:
                if t[2] is q:
                    continue
                self.wait(q, t)

    def finish(self):
        for t in self.out_toks:
            self.wait(self.sp, t)


class _Stop(Exception):
    pass


def build_nc(debug=None, stop=None):
    nc = bass.Bass("TRN2", target_bir_lowering=False)

    def din(name, shape):
        return nc.dram_tensor(name, list(shape), F32, kind="ExternalInput").ap()

    x = din("x", [S, D])
    norm1_g = din("norm1_g", [1, D])
    w_in = din("w_in", [1, D, N_IN])
    nsa_q_norm = din("nsa_q_norm", [1, 64])
    nsa_k_norm = din("nsa_k_norm", [1, 3, 64])
    cmp_pe = din("cmp_pe", [1, 2, 32, 64])
    cmp_w1 = din("cmp_w1", [1, 2, 32, 64, 128])
    cmp_w2 = din("cmp_w2", [1, 2, 128, 64])
    ret_gn_g = din("ret_gn_g", [1, 4, 128])
    w_branch = din("w_branch", [1, 2, 512, D])
    w_out = din("w_out", [1, D, D])
    norm2_g = din("norm2_g", [1, D])
    ffn_w_gate = din("ffn_w_gate", [1, D, DFF])
    ffn_w_up = din("ffn_w_up", [1, D, DFF])
    ffn_w_down = din("ffn_w_down", [1, DFF, D])
    out = nc.dram_tensor("out", [S, D], F32, kind="ExternalOutput").ap()
    xmid = nc.dram_tensor("xmid", [S, D], F32, kind="Internal").ap()
    dbg = {}
    if debug:
        for name, shape in debug.items():
            dbg[name] = nc.dram_tensor("dbg_" + name, list(shape), F32, kind="ExternalOutput").ap()

    w_in_v = w_in[0].rearrange("(k p) n -> p k n", p=128)

    try:
        with ExitStack() as es:
            kb = KB(nc, es)

            def chk(n):
                if stop is not None and n >= stop:
                    kb.barrier()
                    kb.finish()
                    raise _Stop()
            PE, ACT, DVE, POOL, SP = kb.pe, kb.act, kb.dve, kb.pool, kb.sp
            V, A, G, TE = nc.vector, nc.scalar, nc.gpsimd, nc.tensor

            def sb(scope, name, shape, dt, nb=1):
                return T(scope.enter_context(nc.sbuf_tensor(name, list(shape), dt)), name, nb)

            PS2 = [es.enter_context(nc.psum_tensor(f"psp{j}", [128, 1024], F32)) for j in range(4)]
            PS = [T(None, f"ps{i}") for i in range(8)]
            for p_ in PS:
                p_.b.excl = True

            def pf(i):
                return PS2[i // 2][:, (i % 2) * 512:(i % 2 + 1) * 512]

            def pb(i):
                return PS2[i // 2][:].bitcast(BF16)[:, (i % 2) * 1024:(i % 2 + 1) * 1024]

            def pf2(j):
                return PS2[j][:]

            ident_f = sb(es, "ident_f", [128, 128], F32)
            ident_b = sb(es, "ident_b", [128, 128], BF16)
            ones_f = sb(es, "ones_f", [128, 128], F32)
            nhalf = sb(es, "nhalf", [128, 16], F32)
            hT = sb(es, "hT", [128, KC, S], BF16, NT)
            stat = sb(es, "stat", [128, NT, 4], F32, NT)
            hb = [sb(es, f"hb{i}", [128, D], BF16) for i in range(2)]
            junk = sb(es, "junk", [128, D], BF16)

            kb.op(POOL, lambda: G.memset(ones_f.t[:], 1.0), writes=[ones_f.b])
            kb.op(POOL, lambda: G.memset(nhalf.t[:], -0.5), writes=[nhalf.b])
            kb.op(POOL, lambda: G.affine_select(out=ident_f.t[:], in_=ones_f.t[:, 0:128], pattern=[[1, 128]],
                                                compare_op=ALU.is_equal, fill=0.0, base=0, channel_multiplier=-1),
                  reads=[ones_f.b], writes=[ident_f.b])
            kb.op(DVE, lambda: V.tensor_copy(out=ident_b.t[:], in_=ident_f.t[:]), reads=[ident_f.b], writes=[ident_b.b])

            def rstd_from_ss(ss_ap, ms_ap, sd_ap, rs_ap, n, bufs):
                k = ms_ap.shape[-1]
                P_ = ms_ap.shape[0]
                kb.op(DVE, lambda: V.tensor_scalar(out=ms_ap, in0=ss_ap, scalar1=1.0 / n, scalar2=EPS,
                                                   op0=ALU.mult, op1=ALU.add), reads=bufs, writes=bufs)
                kb.op(POOL, lambda: G.tensor_tensor(out=rs_ap, in0=ms_ap, in1=nhalf.t[0:P_, 0:k], op=ALU.pow),
                      reads=list(bufs) + [nhalf.b], writes=bufs)

            def transposes(src_aps, bank, reads):
                pbv = pb(bank)
                n = len(src_aps)
                for i, ap in enumerate(src_aps):
                    kb.op(PE, lambda ap=ap, i=i: TE.transpose(out=pbv[:, i * 128:(i + 1) * 128], in_=ap, identity=ident_b.t[:]),
                          reads=list(reads) + [ident_b.b], writes=[PS[bank].b], inc=(i == n - 1))

            def norm_gen(src, c, gbc, sidx, bank):
                sbuf_ = [stat.bs[c]]
                st = stat.t
                kb.op(ACT, lambda: A.activation(out=junk.t[:], in_=src.t[:], func=AF.Square, accum_out=st[:, c, 0:1]),
                      reads=src.bs, writes=[junk.b] + sbuf_)
                yield
                kb.op(DVE, lambda: V.tensor_scalar(out=st[:, c, 1:2], in0=st[:, c, 0:1], scalar1=1.0 / D, scalar2=EPS,
                                                   op0=ALU.mult, op1=ALU.add), reads=sbuf_, writes=sbuf_)
                yield
                kb.op(POOL, lambda: G.tensor_tensor(out=st[:, c, 3:4], in0=st[:, c, 1:2], in1=nhalf.t[:, 0:1], op=ALU.pow),
                      reads=sbuf_ + [nhalf.b], writes=sbuf_)
                yield
                h = hb[sidx % 2]
                kb.op(DVE, lambda: V.scalar_tensor_tensor(out=h.t[:], in0=src.t[:], scalar=st[:, c, 3:4], in1=gbc.t[:],
                                                          op0=ALU.mult, op1=ALU.mult),
                      reads=src.bs + [gbc.b] + sbuf_, writes=[h.b])
                yield
                transposes([h.t[:, k * 128:(k + 1) * 128] for k in range(KC)], bank, [h.b])
                yield
                kb.op(ACT, lambda: A.copy(out=hT.t[:, :, c * 128:(c + 1) * 128],
                                          in_=pb(bank).rearrange("p (a b) -> p a b", b=128)),
                      reads=[PS[bank].b], writes=[hT.bs[c]])
                yield

            def run_interleaved(gen_fns, width=2):
                pending = list(gen_fns)
                active = []
                while pending or active:
                    while pending and len(active) < width:
                        active.append(pending.pop(0)())
                    for g_ in list(active):
                        try:
                            next(g_)
                        except StopIteration:
                            active.remove(g_)

            def load_w(dst, src_ap, q=None):
                kb.dma(q or POOL, dst.t[:], src_ap, writes=[dst.b])

            def dbg_store(name, src_ap, rows, reads):
                if name in dbg:
                    kb.dma(SP, dbg[name][rows], src_ap, reads=reads, is_out=True)

            def phase_1c():
                with ExitStack() as s3:
                    wq = sb(s3, "wq", [128, KC, 512], BF16)
                    load_w(wq, w_in_v[:, :, C_Q:C_Q + 512])
                    wg = sb(s3, "wg", [128, KC, 24], BF16)
                    load_w(wg, w_in_v[:, :, C_G:C_G + 24])
                    sqq = [sb(s3, f"sqq{i}", [128, 512], F32) for i in range(2)]
                    qst = sb(s3, "qst", [128, NT, 32], F32, NT)
                    tmpq = [sb(s3, f"tmpq{i}", [128, 8, 64], F32) for i in range(2)]
                    qaug = [sb(s3, f"qaug{i}", [128, 8, 128], BF16, 4) for i in range(2)]
                    qT = [sb(s3, f"qT{i}", [128, 8, 128], BF16) for i in range(2)]
                    qT2 = [sb(s3, f"qT2{i}", [128, 8, 128], BF16) for i in range(2)]
                    gate = [sb(s3, f"gate{i}", [128, 24], F32) for i in range(2)]
                    scl = [sb(s3, f"scl{i}", [128, 1024], F32) for i in range(2)]
                    NPT = 3
                    PT = [[sb(s3, f"PT{i}_{j}", [128, 1024], BF16) for j in range(NPT)] for i in range(2)]
                    oTs = [sb(s3, f"oTs{i}", [128, 1024], F32) for i in range(2)]
                    num = [sb(s3, f"num{i}", [128, 3, 8, 64], F32) for i in range(2)]
                    den = [sb(s3, f"den{i}", [128, 3, 8], F32) for i in range(2)]
                    rdc = [sb(s3, f"rdc{i}", [128, 8], F32) for i in range(2)]
                    impn = [sb(s3, f"impn{i}", [128, 8, 32], F32) for i in range(2)]
                    imp = [sb(s3, f"imp{i}", [128, 2, 32], F32) for i in range(2)]
                    top8 = [sb(s3, f"top8{i}", [128, 2, 8], F32, 2) for i in range(2)]
                    rd = [sb(s3, f"rd{i}", [128, 3, 8], F32) for i in range(2)]
                    coef = [sb(s3, f"coef{i}", [128, 3, 8], F32) for i in range(2)]
                    oacc = [sb(s3, f"oacc{i}", [128, 8, 64], F32) for i in range(2)]
                    otmp = [sb(s3, f"otmp{i}", [128, 8, 64], F32) for i in range(2)]
                    ytok = [sb(s3, f"ytok{i}", [128, 512], BF16) for i in range(2)]
                    ydbg = sb(s3, "ydbg", [128, 512], F32) if "y_nsa" in dbg else None
                    for i in range(2):
                        kb.op(POOL, lambda i=i: G.memset(qaug[i].t[:], 0.0), writes=qaug[i].bs)
                    ptc = [0, 0]

                    def tile_gen(c):
                        i2 = c % 2
                        base = 4 * i2
                        bZ, bG = base, base + 1
                        bO0, bO1 = base + 2, base + 3
                        Sb = [PS[bZ].b, PS[bG].b]
                        Ob = [PS[bO0].b, PS[bO1].b]
                        tok = slice(c * 128, (c + 1) * 128)

                        def S2():
                            return pf2(base // 2)

                        def O2():
                            return pf2(base // 2 + 1)

                        def X8():
                            return O2().rearrange("p (h c) -> p h c", h=8)

                        def to_token_major(br, ncol):
                            nu, de = num[i2], den[i2]
                            ot = oTs[i2]
                            kb.op(DVE, lambda: V.tensor_scalar(out=ot.t[0:ncol, :], in0=O2()[0:ncol, :], scalar1=1.0, scalar2=None, op0=ALU.mult),
                                  reads=Ob, writes=[ot.b])
                            yield
                            for hh in range(8):
                                kb.op(PE, lambda hh=hh: TE.transpose(out=X8()[:, hh, 0:ncol], in_=ot.t[0:ncol, hh * 128:(hh + 1) * 128],
                                                                     identity=ident_f.t[0:ncol, 0:ncol]),
                                      reads=[ot.b, ident_f.b], writes=[Ob[hh // 4]], inc=(hh % 4 == 3))
                            yield
                            kb.op(ACT, lambda: A.copy(out=nu.t[:, br, :, :], in_=X8()[:, :, 0:64]), reads=Ob, writes=[nu.b])
                            yield
                            kb.op(DVE, lambda: V.tensor_scalar(out=de.t[:, br, :], in0=X8()[:, :, 64], scalar1=1e-30, scalar2=None, op0=ALU.max),
                                  reads=Ob, writes=[de.b])
                            yield

                        def branch(br, KT, VV, kts, qsrc):
                            n = len(kts)

                            def scores(kt):
                                for g in range(2):
                                    kb.op(PE, lambda g=g: TE.matmul(S2()[:, g * 512:(g + 1) * 512], lhsT=KT.t[:, g, kt * 128:(kt + 1) * 128],
                                                                    rhs=qsrc.t[:, 4 * g:4 * g + 4, :], start=True, stop=True),
                                          reads=[KT.bs[kt], qsrc.b], writes=[Sb[g]])

                            scores(kts[0])
                            yield
                            for j, kt in enumerate(kts):
                                pt = PT[i2][ptc[i2] % NPT]
                                ptc[i2] += 1
                                kb.op(ACT, lambda pt=pt: A.activation(out=pt.t[:], in_=S2(), func=AF.Exp), reads=Sb, writes=[pt.b])
                                yield
                                if j + 1 < n:
                                    scores(kts[j + 1])
                                    yield
                                if kt == c:
                                    kb.op(DVE, lambda pt=pt: V.tensor_tensor(out=pt.t[:], in0=pt.t[:], in1=dmask8.t[:].rearrange("p a b -> p (a b)"), op=ALU.mult),
                                          reads=[pt.b, dmask8.b], writes=[pt.b])
                                    yield
                                elif br == 2 and kt == c - 4:
                                    kb.op(DVE, lambda pt=pt: V.tensor_tensor(out=pt.t[:], in0=pt.t[:], in1=tmask8.t[:].rearrange("p a b -> p (a b)"), op=ALU.mult),
                                          reads=[pt.b, tmask8.b], writes=[pt.b])
                                    yield
                                for g in range(2):
                                    kb.op(PE, lambda kt=kt, pt=pt, j=j, g=g: TE.matmul(O2()[0:65, g * 512:(g + 1) * 512], lhsT=VV.t[:, kt, g, :],
                                                                                       rhs=pt.t[:, g * 512:(g + 1) * 512], start=(j == 0), stop=(j == n - 1)),
                                          reads=[pt.b, VV.bs[kt]], writes=[Ob[g]], inc=(j == n - 1))
                                yield
                            yield from to_token_major(br, 65)

                        for k in range(KC):
                            kb.op(PE, lambda k=k: TE.matmul(pf(bZ), lhsT=hT.t[:, k, tok], rhs=wq.t[:, k, :], start=(k == 0), stop=(k == KC - 1)),
                                  reads=[hT.bs[c], wq.b], writes=[PS[bZ].b], inc=(k == KC - 1))
                        for k in range(KC):
                            kb.op(PE, lambda k=k: TE.matmul(pf(bG)[:, 0:24], lhsT=hT.t[:, k, tok], rhs=wg.t[:, k, :], start=(k == 0), stop=(k == KC - 1)),
                                  reads=[hT.bs[c], wg.b], writes=[PS[bG].b], inc=(k == KC - 1))
                        yield
                        sq = sqq[i2]
                        kb.op(ACT, lambda: A.activation(out=sq.t[:], in_=pf(bZ), func=AF.Square), reads=[PS[bZ].b], writes=[sq.b])
                        gt = gate[i2]
                        kb.op(ACT, lambda: A.activation(out=gt.t[:], in_=pf(bG)[:, 0:24], func=AF.Tanh, scale=0.5), reads=[PS[bG].b], writes=[gt.b])
                        yield
                        kb.op(DVE, lambda: V.tensor_scalar(out=gt.t[:], in0=gt.t[:], scalar1=0.5, scalar2=0.5, op0=ALU.mult, op1=ALU.add),
                              reads=[gt.b], writes=[gt.b])
                        qs = qst.t
                        qsb = [qst.bs[c]]
                        kb.op(DVE, lambda: V.tensor_reduce(out=qs[:, c, 0:8], in_=sq.t[:].rearrange("p (a b) -> p a b", b=64), axis=AX.X, op=ALU.add),
                              reads=[sq.b], writes=qsb)
                        yield
                        kb.op(DVE, lambda: V.tensor_scalar(out=qs[:, c, 8:16], in0=qs[:, c, 0:8], scalar1=1.0 / 64, scalar2=EPS, op0=ALU.mult, op1=ALU.add), reads=qsb, writes=qsb)
                        yield
                        kb.op(POOL, lambda: G.tensor_tensor(out=qs[:, c, 24:32], in0=qs[:, c, 8:16], in1=nhalf.t[:, 0:8], op=ALU.pow),
                              reads=qsb + [nhalf.b], writes=qsb)
                        yield
                        tq = tmpq[i2]
                        qa = qaug[i2]
                        kb.op(DVE, lambda: V.tensor_tensor(out=tq.t[:], in0=pf(bZ).rearrange("p (a b) -> p a b", b=64),
                                                           in1=qs[:, c, 24:32].unsqueeze(2).broadcast_to([128, 8, 64]), op=ALU.mult),
                              reads=[PS[bZ].b] + qsb, writes=[tq.b])
                        yield
                        kb.op(DVE, lambda: V.tensor_tensor(out=qa.t[:, :, 0:64], in0=tq.t[:], in1=gq.t[:].unsqueeze(1).broadcast_to([128, 8, 64]), op=ALU.mult),
                              reads=[tq.b, gq.b], writes=[qa.bs[0]])
                        kb.op(POOL, lambda: G.tensor_copy(out=qa.t[:, :, 96:100], in_=QAL.t[:, c, :, :]), reads=[QAL.b], writes=[qa.bs[1]])
                        yield
                        transposes([qa.t[:, h, :] for h in range(8)], bZ, qa.bs)
                        yield
                        q1 = qT[i2]
                        kb.op(ACT, lambda: A.copy(out=q1.t[:], in_=pb(bZ).rearrange("p (a b) -> p a b", b=128)), reads=[PS[bZ].b], writes=[q1.b])
                        yield
                        nu, de = num[i2], den[i2]
                        rdc_, impn_, imp_, top8_ = rdc[i2], impn[i2], imp[i2], top8[i2]
                        sc_ = scl[i2]
                        pc = PT[i2][ptc[i2] % NPT]
                        ptc[i2] += 1
                        for g in range(2):
                            kb.op(PE, lambda g=g: TE.matmul(S2()[0:127, g * 512:(g + 1) * 512], lhsT=KcT.t[:, g, 0:127], rhs=q1.t[:, 4 * g:4 * g + 4, :], start=True, stop=True),
                                  reads=[KcT.b, q1.b], writes=[Sb[g]])
                        yield
                        kb.op(DVE, lambda: V.scalar_tensor_tensor(out=sc_.t[0:127, :].rearrange("p (a b) -> p a b", b=128),
                                                                  in0=S2()[0:127, :].rearrange("p (a b) -> p a b", b=128), scalar=60.0,
                                                                  in1=cmask.t[0:127, tok].unsqueeze(1).broadcast_to([127, 8, 128]),
                                                                  op0=ALU.min, op1=ALU.add),
                              reads=Sb + [cmask.b], writes=[sc_.b])
                        yield
                        kb.op(ACT, lambda: A.activation(out=pc.t[0:127, :], in_=sc_.t[0:127, :], func=AF.Exp), reads=[sc_.b], writes=[pc.b])
                        yield
                        for g in range(2):
                            kb.op(PE, lambda g=g: TE.matmul(O2()[0:97, g * 512:(g + 1) * 512], lhsT=Vc.t[0:127, g, :], rhs=pc.t[0:127, g * 512:(g + 1) * 512], start=True, stop=True),
                                  reads=[pc.b, Vc.b], writes=[Ob[g]])
                        yield
                        yield from to_token_major(0, 97)
                        kb.op(DVE, lambda: V.reciprocal(out=rdc_.t[:], in_=de.t[:, 0, :]), reads=[de.b], writes=[rdc_.b])
                        yield
                        kb.op(DVE, lambda: V.tensor_tensor(out=impn_.t[:], in0=X8()[:, :, 65:97],
                                                           in1=rdc_.t[:].unsqueeze(2).broadcast_to([128, 8, 32]), op=ALU.mult),
                              reads=Ob + [rdc_.b], writes=[impn_.b])
                        yield
                        q2 = qT2[i2]

                        def imp_chain():
                            kb.op(DVE, lambda: V.tensor_reduce(out=imp_.t[:], in_=impn_.t[:].rearrange("p (g r) j -> p g j r", g=2), axis=AX.X, op=ALU.add),
                                  reads=[impn_.b], writes=[imp_.b])
                            yield
                            kb.op(DVE, lambda: V.tensor_tensor(out=imp_.t[:], in0=imp_.t[:], in1=addc.t[:, c:c + 1, :].broadcast_to([128, 2, 32]), op=ALU.add),
                                  reads=[imp_.b, addc.b], writes=[imp_.b])
                            yield
                            for g in range(2):
                                kb.op(DVE, lambda g=g: V.max(out=top8_.t[:, g, :], in_=imp_.t[:, g, :]), reads=[imp_.b], writes=[top8_.bs[g]])
                            yield
                            for g in range(2):
                                kb.op(DVE, lambda g=g: V.tensor_scalar(out=qa.t[:, 4 * g:4 * g + 4, 64:96],
                                                                       in0=imp_.t[:, g:g + 1, :].broadcast_to([128, 4, 32]),
                                                                       scalar1=top8_.t[:, g, 7:8], scalar2=NEG, op0=ALU.is_lt, op1=ALU.mult),
                                      reads=[imp_.b, top8_.bs[g]], writes=[qa.bs[2 + g]])
                            yield

                        gw = branch(2, KT_win, V_win, list(range(max(0, c - 4), c + 1)), q1)
                        gi = imp_chain()
                        live = [gw, gi]
                        while live:
                            for g_ in list(live):
                                try:
                                    next(g_)
                                except StopIteration:
                                    live.remove(g_)
                            yield
                        transposes([qa.t[:, h, :] for h in range(8)], bZ, qa.bs)
                        kb.op(ACT, lambda: A.copy(out=q2.t[:], in_=pb(bZ).rearrange("p (a b) -> p a b", b=128)), reads=[PS[bZ].b], writes=[q2.b])
                        yield
                        yield from branch(1, KT_slc, V_slc, list(range(0, c + 1)), q2)
                        rd_, coef_, oacc_, otmp_ = rd[i2], coef[i2], oacc[i2], otmp[i2]
                        kb.op(DVE, lambda: V.reciprocal(out=rd_.t[:], in_=de.t[:]), reads=[de.b], writes=[rd_.b])
                        yield
                        kb.op(DVE, lambda: V.tensor_tensor(out=coef_.t[:], in0=gt.t[:].rearrange("p (h b) -> p b h", b=3), in1=rd_.t[:], op=ALU.mult),
                              reads=[gt.b, rd_.b], writes=[coef_.b])
                        yield
                        kb.op(DVE, lambda: V.tensor_tensor(out=oacc_.t[:], in0=nu.t[:, 0], in1=coef_.t[:, 0, :].unsqueeze(2).broadcast_to([128, 8, 64]), op=ALU.mult),
                              reads=[nu.b, coef_.b], writes=[oacc_.b])
                        kb.op(POOL, lambda: G.tensor_tensor(out=otmp_.t[:], in0=nu.t[:, 1], in1=coef_.t[:, 1, :].unsqueeze(2).broadcast_to([128, 8, 64]), op=ALU.mult),
                              reads=[nu.b, coef_.b], writes=[otmp_.b])
                        yield
                        kb.op(DVE, lambda: V.tensor_tensor(out=oacc_.t[:], in0=oacc_.t[:], in1=otmp_.t[:], op=ALU.add), reads=[oacc_.b, otmp_.b], writes=[oacc_.b])
                        yield
                        kb.op(POOL, lambda: G.tensor_tensor(out=otmp_.t[:], in0=nu.t[:, 2], in1=coef_.t[:, 2, :].unsqueeze(2).broadcast_to([128, 8, 64]), op=ALU.mult),
                              reads=[nu.b, coef_.b], writes=[otmp_.b])
                        yield
                        yt = ytok[i2]
                        kb.op(DVE, lambda: V.tensor_tensor(out=yt.t[:], in0=oacc_.t[:].rearrange("p a b -> p (a b)"), in1=otmp_.t[:].rearrange("p a b -> p (a b)"), op=ALU.add),
                              reads=[oacc_.b, otmp_.b], writes=[yt.b])
                        if ydbg is not None:
                            kb.op(POOL, lambda: G.tensor_tensor(out=ydbg.t[:], in0=oacc_.t[:].rearrange("p a b -> p (a b)"), in1=otmp_.t[:].rearrange("p a b -> p (a b)"), op=ALU.add),
                                  reads=[oacc_.b, otmp_.b], writes=[ydbg.b])
                            dbg_store("y_nsa", ydbg.t[:], tok, [ydbg.b])
                        yield
                        transposes([yt.t[:, k * 128:(k + 1) * 128] for k in range(4)], bZ, [yt.b])
                        yield
                        kb.op(ACT, lambda: A.copy(out=ynsaT.t[:, :, tok], in_=pb(bZ)[:, 0:512].rearrange("p (a b) -> p a b", b=128)),
                              reads=[PS[bZ].b], writes=[ynsaT.bs[c]])
                        yield

                    run_interleaved([(lambda c=c: tile_gen(c)) for c in range(NT)], width=WIDTH)

            def phase_1d():
                with ExitStack() as s4:
                    wr = sb(s4, "wr", [128, KC, 2048], BF16, 4)
                    for j in range(4):
                        kb.dma(POOL, wr.t[:, :, j * 512:(j + 1) * 512], w_in_v[:, :, C_R + j * 512:C_R + (j + 1) * 512], writes=[wr.bs[j]])
                    for j in range(4):
                        kb.dma(POOL, wm.t[:, :, j * 512:(j + 1) * 512], w_in_v[:, :, C_M + j * 512:C_M + (j + 1) * 512], writes=[wm.bs[j]])
                    load_w(wbr, w_branch[0].rearrange("n (k p) d -> p (n k) d", p=128))
                    idT = sb(s4, "idT", [128, 4, 128], F32)
                    qdec = sb(s4, "qdec", [128, 4, 128], F32)
                    kdec = sb(s4, "kdec", [128, 4, 128], F32)
                    gn = sb(s4, "gn", [128, 512], F32)
                    eij = sb(s4, "eij", [128, 128], F32)
                    rowq = sb(s4, "rowq", [128, 128], F32)
                    rowk = sb(s4, "rowk", [128, 128], F32)
                    kb.dma(SP, gn.t[:], ret_gn_g.rearrange("o a b -> o (a b)").broadcast_to([128, 512]), writes=[gn.b])
                    kb.op(POOL, lambda: G.iota(eij.t[:], pattern=[[1, 128]], base=0, channel_multiplier=-1, allow_small_or_imprecise_dtypes=True), writes=[eij.b])
                    kb.op(POOL, lambda: G.iota(rowq.t[:], pattern=[[1, 128]], base=1, channel_multiplier=0, allow_small_or_imprecise_dtypes=True), writes=[rowq.b])
                    kb.op(POOL, lambda: G.iota(rowk.t[:], pattern=[[-1, 128]], base=127, channel_multiplier=0, allow_small_or_imprecise_dtypes=True), writes=[rowk.b])
                    lgs = [float(np.log(1.0 - 2.0 ** (-5.0 - h))) for h in range(4)]
                    cds = [float(np.exp(128.0 * np.float32(lg))) for lg in lgs]
                    for h in range(4):
                        kb.op(ACT, lambda h=h: A.activation(out=idT.t[:, h, :], in_=eij.t[:], func=AF.Exp, scale=lgs[h]), reads=[eij.b], writes=[idT.b])
                        kb.op(POOL, lambda h=h: G.affine_select(out=idT.t[:, h, :], in_=idT.t[:, h, :], pattern=[[1, 128]], compare_op=ALU.is_ge, fill=0.0,
                                                                base=0, channel_multiplier=-1), reads=[idT.b], writes=[idT.b])
                        kb.op(ACT, lambda h=h: A.activation(out=qdec.t[:, h, :], in_=rowq.t[:], func=AF.Exp, scale=lgs[h]), reads=[rowq.b], writes=[qdec.b])
                        kb.op(ACT, lambda h=h: A.activation(out=kdec.t[:, h, :], in_=rowk.t[:], func=AF.Exp, scale=lgs[h]), reads=[rowk.b], writes=[kdec.b])
                    qTr = sb(s4, "qTr", [128, 4, 512], BF16)
                    qdT = sb(s4, "qdT", [128, 4, 512], BF16)
                    kTr = sb(s4, "kTr", [128, 4, 512], BF16)
                    kdT = sb(s4, "kdT", [128, 4, 512], BF16)
                    v_sb = [sb(s4, f"v_sb{i}", [128, 4, 128], BF16) for i in range(2)]
                    sgl = [sb(s4, f"sgl{i}", [128, 512], F32) for i in range(2)]
                    kd = [sb(s4, f"kd{i}", [128, 4, 128], BF16) for i in range(2)]
                    attb = [sb(s4, f"attb{i}", [128, 4, 128], BF16) for i in range(2)]
                    state_f = sb(s4, "state_f", [128, 4, 128], F32, 4)
                    state_b = sb(s4, "state_b", [128, 4, 128], BF16)
                    yr = [sb(s4, f"yr{i}", [128, 512], BF16) for i in range(2)]
                    yrdbg = sb(s4, "yrdbg", [128, 512], F32) if "y_ret" in dbg else None
                    kb.op(POOL, lambda: G.memset(state_f.t[:], 0.0), writes=state_f.bs)
                    kb.op(POOL, lambda: G.memset(state_b.t[:], 0.0), writes=[state_b.b])
                    KS = float(128.0 ** -0.5)
                    pcnt = [0]
                    state_done = [False] * (NT + 1)
                    stt_done = [False] * (NT + 1)
                    bst = [sb(s4, f"bst{i}", [128, 4, 6], F32, 4) for i in range(2)]
                    mv = [sb(s4, f"mv{i}", [128, 4, 2], F32, 4) for i in range(2)]
                    rs4 = [sb(s4, f"rs4{i}", [128, 12], F32) for i in range(2)]
                    on = [sb(s4, f"on{i}", [128, 512], F32, 4) for i in range(2)]

                    def p1d_gen(c):
                        i2 = c % 2
                        cl = c % 4
                        bA, bB, bT = 2 + 3 * i2, 3 + 3 * i2, 4 + 3 * i2
                        tok = slice(c * 128, (c + 1) * 128)
                        cs = slice(cl * 128, (cl + 1) * 128)
                        bst_, mv_, rs4_, on_ = bst[i2], mv[i2], rs4[i2], on[i2]
                        for k in range(KC):
                            kb.op(PE, lambda k=k: TE.matmul(pf(bA), lhsT=hT.t[:, k, tok], rhs=wr.t[:, k, 1024:1536], start=(k == 0), stop=(k == KC - 1)),
                                  reads=[hT.bs[c], wr.bs[2]], writes=[PS[bA].b], inc=(k == KC - 1))
                        yield
                        for k in range(KC):
                            kb.op(PE, lambda k=k: TE.matmul(pf(bB), lhsT=hT.t[:, k, tok], rhs=wr.t[:, k, 1536:2048], start=(k == 0), stop=(k == KC - 1)),
                                  reads=[hT.bs[c], wr.bs[3]], writes=[PS[bB].b], inc=(k == KC - 1))
                        yield
                        vs, sg_, kd_, ab = v_sb[i2], sgl[i2], kd[i2], attb[i2]
                        kb.op(ACT, lambda: A.copy(out=vs.t[:].rearrange("p a b -> p (a b)"), in_=pf(bA)), reads=[PS[bA].b], writes=[vs.b])
                        yield
                        kb.op(ACT, lambda: A.activation(out=sg_.t[:], in_=pf(bB), func=AF.Silu), reads=[PS[bB].b], writes=[sg_.b])
                        yield
                        transposes([kdT.t[:, h, cs] for h in range(4)], bT, [kdT.b])
                        yield
                        kb.op(ACT, lambda: A.copy(out=kd_.t[:].rearrange("p a b -> p (a b)"), in_=pb(bT)[:, 0:512]), reads=[PS[bT].b], writes=[kd_.b])
                        yield
                        for h in range(4):
                            kb.op(PE, lambda h=h: TE.matmul(pf(bA)[:, h * 128:(h + 1) * 128], lhsT=kTr.t[:, h, cs], rhs=qTr.t[:, h, cs], start=True, stop=True),
                                  reads=[kTr.b, qTr.b], writes=[PS[bA].b], inc=(h == 3))
                        yield
                        kb.op(DVE, lambda: V.tensor_tensor(out=ab.t[:], in0=pf(bA).rearrange("p (a b) -> p a b", b=128), in1=idT.t[:], op=ALU.mult),
                              reads=[PS[bA].b, idT.b], writes=[ab.b])
                        yield
                        if c < NT - 1:
                            for h in range(4):
                                kb.op(PE, lambda h=h: TE.matmul(pf(bT)[:, h * 128:(h + 1) * 128], lhsT=kd_.t[:, h, :], rhs=vs.t[:, h, :], start=True, stop=True),
                                      reads=[kd_.b, vs.b], writes=[PS[bT].b], inc=(h == 3))
                            yield
                            while c > 0 and not state_done[c - 1]:
                                yield
                            for h in range(4):
                                kb.op(DVE, lambda h=h: V.scalar_tensor_tensor(out=state_f.t[:, h, :], in0=state_f.t[:, h, :], scalar=cds[h],
                                                                              in1=pf(bT)[:, h * 128:(h + 1) * 128], op0=ALU.mult, op1=ALU.add),
                                      reads=[state_f.bs[h], PS[bT].b], writes=[state_f.bs[h]])
                        stt_done[c] = True
                        yield
                        while c > 0 and not state_done[c - 1]:
                            yield
                        for h in range(4):
                            kb.op(PE, lambda h=h: TE.matmul(pf(bB)[:, h * 128:(h + 1) * 128], lhsT=ab.t[:, h, :], rhs=vs.t[:, h, :], start=True, stop=(c == 0)),
                                  reads=[ab.b, vs.b], writes=[PS[bB].b], inc=(c == 0 and h == 3))
                            if c > 0:
                                kb.op(PE, lambda h=h: TE.matmul(pf(bB)[:, h * 128:(h + 1) * 128], lhsT=qdT.t[:, h, cs], rhs=state_b.t[:, h, :], start=False, stop=True),
                                      reads=[qdT.b, state_b.b], writes=[PS[bB].b], inc=(h == 3))
                        yield
                        if c < NT - 1:
                            kb.op(POOL, lambda: G.tensor_copy(out=state_b.t[:], in_=state_f.t[:]), reads=state_f.bs, writes=[state_b.b])
                        state_done[c] = True
                        yield
                        for h in range(4):
                            kb.op(DVE, lambda h=h: V.bn_stats(out=bst_.t[:, h, :], in_=pf(bB)[:, h * 128:(h + 1) * 128]), reads=[PS[bB].b], writes=[bst_.bs[h]])
                        yield
                        for h in range(4):
                            kb.op(DVE, lambda h=h: V.bn_aggr(out=mv_.t[:, h, :], in_=bst_.t[:, h, :]), reads=[bst_.bs[h]], writes=[mv_.bs[h]])
                        yield
                        kb.op(DVE, lambda: V.tensor_scalar(out=rs4_.t[:, 0:4], in0=mv_.t[:, :, 1], scalar1=EPS, scalar2=None, op0=ALU.add), reads=mv_.bs, writes=[rs4_.b])
                        yield
                        kb.op(POOL, lambda: G.tensor_tensor(out=rs4_.t[:, 8:12], in0=rs4_.t[:, 0:4], in1=nhalf.t[:, 0:4], op=ALU.pow),
                              reads=[rs4_.b, nhalf.b], writes=[rs4_.b])
                        yield
                        for h in range(4):
                            kb.op(DVE, lambda h=h: V.tensor_scalar(out=on_.t[:, h * 128:(h + 1) * 128], in0=pf(bB)[:, h * 128:(h + 1) * 128],
                                                                   scalar1=mv_.t[:, h, 0:1], scalar2=rs4_.t[:, 8 + h:9 + h], op0=ALU.subtract, op1=ALU.mult),
                                  reads=[PS[bB].b, mv_.bs[h], rs4_.b], writes=[on_.bs[h]])
                        yield
                        kb.op(POOL, lambda: G.tensor_tensor(out=on_.t[:], in0=on_.t[:], in1=gn.t[:], op=ALU.mult), reads=on_.bs + [gn.b], writes=on_.bs)
                        yield
                        y_ = yr[i2]
                        kb.op(DVE, lambda: V.tensor_tensor(out=y_.t[:], in0=on_.t[:], in1=sg_.t[:], op=ALU.mult), reads=on_.bs + [sg_.b], writes=[y_.b])
                        if yrdbg is not None:
                            kb.op(DVE, lambda: V.tensor_tensor(out=yrdbg.t[:], in0=on_.t[:], in1=sg_.t[:], op=ALU.mult), reads=on_.bs + [sg_.b], writes=[yrdbg.b])
                            dbg_store("y_ret", yrdbg.t[:], tok, [yrdbg.b])
                        yield
                        transposes([y_.t[:, k * 128:(k + 1) * 128] for k in range(4)], bT, [y_.b])
                        yield
                        kb.op(ACT, lambda: A.copy(out=yretT.t[:, :, tok], in_=pb(bT)[:, 0:512].rearrange("p (a b) -> p a b", b=128)),
                              reads=[PS[bT].b], writes=[yretT.bs[c]])
                        yield

                    for tg in range(4):
                        tks = slice(tg * 512, (tg + 1) * 512)
                        hbs = [hT.bs[4 * tg + i] for i in range(4)]
                        for qk in range(2):
                            for h in range(4):
                                bk = pcnt[0] % 2
                                pcnt[0] += 1
                                for k in range(KC):
                                    kb.op(PE, lambda k=k, qk=qk, h=h, bk=bk: TE.matmul(pf(bk), lhsT=wr.t[:, k, qk * 512 + h * 128:qk * 512 + (h + 1) * 128],
                                                                                      rhs=hT.t[:, k, tks], start=(k == 0), stop=(k == KC - 1)),
                                          reads=hbs + [wr.bs[qk]], writes=[PS[bk].b], inc=(k == KC - 1))
                                pv4 = pf(bk).rearrange("p (a b) -> p a b", b=128)
                                if qk == 0:
                                    kb.op(ACT, lambda h=h, bk=bk: A.copy(out=qTr.t[:, h, :], in_=pf(bk)), reads=[PS[bk].b], writes=[qTr.b])
                                    kb.op(DVE, lambda h=h, pv4=pv4: V.tensor_tensor(out=qdT.t[:, h, :].rearrange("p (a b) -> p a b", b=128), in0=pv4,
                                                                                    in1=qdec.t[:, h:h + 1, :].broadcast_to([128, 4, 128]), op=ALU.mult),
                                          reads=[PS[bk].b, qdec.b], writes=[qdT.b])
                                else:
                                    kb.op(ACT, lambda h=h, bk=bk: A.mul(out=kTr.t[:, h, :], in_=pf(bk), mul=KS), reads=[PS[bk].b], writes=[kTr.b])
                                    kb.op(DVE, lambda h=h, pv4=pv4: V.scalar_tensor_tensor(out=kdT.t[:, h, :].rearrange("p (a b) -> p a b", b=128), in0=pv4, scalar=KS,
                                                                                           in1=kdec.t[:, h:h + 1, :].broadcast_to([128, 4, 128]),
                                                                                           op0=ALU.mult, op1=ALU.mult),
                                          reads=[PS[bk].b, kdec.b], writes=[kdT.b])
                        run_interleaved([(lambda c=c: p1d_gen(c)) for c in range(4 * tg, 4 * tg + 4)], width=2)

            def phase_1e():
                with ExitStack() as s5:
                    xt = [sb(s5, f"xte{i}", [128, D], F32) for i in range(2)]
                    g2bc = sb(s5, "g2bc", [128, D], F32)
                    kb.dma(SP, g2bc.t[:], norm2_g[0:1, :].broadcast_to([128, D]), writes=[g2bc.b])
                    wo = sb(s5, "wo", [128, KC, D], BF16)
                    load_w(wo, w_out[0].rearrange("(k p) d -> p k d", p=128))
                    gates = [sb(s5, f"gates{i}", [128, D], F32, 2) for i in range(2)]
                    tmix = [sb(s5, f"tmix{i}", [128, D], F32, 2) for i in range(2)]
                    tmix2 = [sb(s5, f"tmix2{i}", [128, D], F32, 2) for i in range(2)]
                    mixed = [sb(s5, f"mixed{i}", [128, D], BF16) for i in range(2)]
                    mixT = [sb(s5, f"mixT{i}", [128, KC, 128], BF16) for i in range(2)]
                    x1t = [sb(s5, f"x1t{i}", [128, D], F32, 2) for i in range(2)]

                    def p1e_gen(c):
                        i2 = c % 2
                        bs_ = [4 * i2 + i for i in range(4)]
                        tok = slice(c * 128, (c + 1) * 128)
                        xx = xt[i2]
                        kb.dma(SP, xx.t[:], x[tok, :], writes=[xx.b])
                        gt_, tm = gates[i2], (tmix[i2], tmix2[i2])
                        for n, yT in ((0, ynsaT), (1, yretT)):
                            for half in range(2):
                                j = 2 * n + half
                                for k in range(KC):
                                    kb.op(PE, lambda j=j, k=k, half=half: TE.matmul(pf(bs_[half]), lhsT=hT.t[:, k, tok], rhs=wm.t[:, k, j * 512:(j + 1) * 512],
                                                                                    start=(k == 0), stop=(k == KC - 1)),
                                          reads=[hT.bs[c], wm.bs[j]], writes=[PS[bs_[half]].b], inc=(k == KC - 1))
                                yield
                            for half in range(2):
                                bk = bs_[2 + half]
                                for k in range(4):
                                    kb.op(PE, lambda n=n, half=half, k=k, bk=bk, yT=yT: TE.matmul(pf(bk), lhsT=yT.t[:, k, tok], rhs=wbr.t[:, n * 4 + k, half * 512:(half + 1) * 512],
                                                                                                  start=(k == 0), stop=(k == 3)),
                                          reads=[yT.bs[c], wbr.b], writes=[PS[bk].b], inc=(k == 3))
                                yield
                            for half in range(2):
                                kb.op(ACT, lambda half=half: A.activation(out=gt_.t[:, half * 512:(half + 1) * 512], in_=pf(bs_[half]), func=AF.Sigmoid),
                                      reads=[PS[bs_[half]].b], writes=[gt_.bs[half]])
                                yield
                            for half in range(2):
                                hs = slice(half * 512, (half + 1) * 512)
                                kb.op(DVE, lambda half=half, hs=hs, n=n: V.tensor_tensor(out=tm[n].t[:, hs], in0=gt_.t[:, hs], in1=pf(bs_[2 + half]), op=ALU.mult),
                                      reads=[gt_.bs[half], PS[bs_[2 + half]].b], writes=[tm[n].bs[half]])
                                yield
                        mx = mixed[i2]
                        kb.op(POOL, lambda: G.tensor_tensor(out=mx.t[:], in0=tm[0].t[:], in1=tm[1].t[:], op=ALU.add), reads=tm[0].bs + tm[1].bs, writes=[mx.b])
                        yield
                        transposes([mx.t[:, k * 128:(k + 1) * 128] for k in range(KC)], bs_[0], [mx.b])
                        yield
                        mt = mixT[i2]
                        kb.op(ACT, lambda: A.copy(out=mt.t[:], in_=pb(bs_[0]).rearrange("p (a b) -> p a b", b=128)), reads=[PS[bs_[0]].b], writes=[mt.b])
                        yield
                        for half in range(2):
                            for k in range(KC):
                                kb.op(PE, lambda half=half, k=k: TE.matmul(pf(bs_[2 + half]), lhsT=mt.t[:, k, :], rhs=wo.t[:, k, half * 512:(half + 1) * 512],
                                                                           start=(k == 0), stop=(k == KC - 1)),
                                      reads=[mt.b, wo.b], writes=[PS[bs_[2 + half]].b], inc=(k == KC - 1))
                            yield
                        x1 = x1t[i2]
                        for half in range(2):
                            hs = slice(half * 512, (half + 1) * 512)
                            kb.op(DVE, lambda half=half, hs=hs: V.tensor_tensor(out=x1.t[:, hs], in0=xx.t[:, hs], in1=pf(bs_[2 + half]), op=ALU.add),
                                  reads=[xx.b, PS[bs_[2 + half]].b], writes=[x1.bs[half]])
                            yield
                        kb.dma(SP, xmid[tok, :], x1.t[:], reads=x1.bs)
                        dbg_store("x1", x1.t[:], tok, x1.bs)
                        yield from norm_gen(x1, c, g2bc, i2, bs_[1])

                    run_interleaved([(lambda c=c: p1e_gen(c)) for c in range(NT)], width=2)

            def phase_2():
                with ExitStack() as s6:
                    xt = [sb(s6, f"xtf{i}", [128, D], F32) for i in range(2)]
                    wd = sb(s6, "wd", [128, NFB, D], BF16, 2)
                    wd_v = ffn_w_down[0].rearrange("(fb p) d -> p fb d", p=128)
                    wgs = [sb(s6, f"wgs{i}", [128, KC, 256], BF16) for i in range(2)]
                    wus = [sb(s6, f"wus{i}", [128, KC, 256], BF16) for i in range(2)]
                    act = sb(s6, "act", [128, NFB, 1024], BF16, NFB)
                    sgs = [sb(s6, f"sgs{i}", [128, 512], F32) for i in range(2)]
                    outt = [sb(s6, f"outt{i}", [128, D], F32, 2) for i in range(2)]
                    wg_v = ffn_w_gate[0].rearrange("(k p) f -> p k f", p=128)
                    wu_v = ffn_w_up[0].rearrange("(k p) f -> p k f", p=128)
                    cn = [0, 0]
                    for hf in range(2):
                        for fg in range(11):
                            cols = slice(fg * 256, (fg + 1) * 256)
                            wg_, wu_ = wgs[fg % 2], wus[fg % 2]
                            kb.dma(POOL, wg_.t[:], wg_v[:, :, cols], writes=[wg_.b])
                            kb.dma(POOL, wu_.t[:], wu_v[:, :, cols], writes=[wu_.b])
                            if hf == 0 and fg == 1:
                                kb.dma(POOL, wd.t[:, 0:11, :], wd_v[:, 0:11, :], writes=[wd.bs[0]])
                                kb.dma(POOL, wd.t[:, 11:22, :], wd_v[:, 11:22, :], writes=[wd.bs[1]])
                            for fl in range(2):
                                fb = fg * 2 + fl
                                for t2 in range(2):
                                    tokc = slice(hf * 1024 + t2 * 512, hf * 1024 + (t2 + 1) * 512)
                                    hbs = [hT.bs[hf * 8 + t2 * 4 + i] for i in range(4)]
                                    gb, ub = (0, 1) if cn[0] % 2 == 0 else (2, 3)
                                    cn[0] += 1
                                    for k in range(KC):
                                        kb.op(PE, lambda k=k, fl=fl, gb=gb, wg_=wg_, tokc=tokc: TE.matmul(pf(gb), lhsT=wg_.t[:, k, fl * 128:(fl + 1) * 128], rhs=hT.t[:, k, tokc],
                                                                                                          start=(k == 0), stop=(k == KC - 1)),
                                              reads=hbs + [wg_.b], writes=[PS[gb].b], inc=(k == KC - 1))
                                    for k in range(KC):
                                        kb.op(PE, lambda k=k, fl=fl, ub=ub, wu_=wu_, tokc=tokc: TE.matmul(pf(ub), lhsT=wu_.t[:, k, fl * 128:(fl + 1) * 128], rhs=hT.t[:, k, tokc],
                                                                                                          start=(k == 0), stop=(k == KC - 1)),
                                              reads=hbs + [wu_.b], writes=[PS[ub].b], inc=(k == KC - 1))
                                    sg_ = sgs[cn[0] % 2]
                                    kb.op(ACT, lambda sg_=sg_, gb=gb: A.activation(out=sg_.t[:], in_=pf(gb), func=AF.Silu), reads=[PS[gb].b], writes=[sg_.b])
                                    kb.op(DVE, lambda sg_=sg_, ub=ub, fb=fb, t2=t2: V.tensor_tensor(out=act.t[:, fb, t2 * 512:(t2 + 1) * 512], in0=sg_.t[:], in1=pf(ub), op=ALU.mult),
                                          reads=[sg_.b, PS[ub].b], writes=[act.bs[fb]])
                        for tl in range(8):
                            c = hf * 8 + tl
                            tok = slice(c * 128, (c + 1) * 128)
                            xx = xt[c % 2]
                            kb.dma(SP, xx.t[:], xmid[tok, :], writes=[xx.b])
                            ob = (4, 5) if cn[1] % 2 == 0 else (6, 7)
                            cn[1] += 1
                            for half in range(2):
                                for fb in range(NFB):
                                    kb.op(PE, lambda half=half, fb=fb, ob=ob, tl=tl: TE.matmul(pf(ob[half]), lhsT=act.t[:, fb, tl * 128:(tl + 1) * 128],
                                                                                               rhs=wd.t[:, fb, half * 512:(half + 1) * 512],
                                                                                               start=(fb == 0), stop=(fb == NFB - 1)),
                                          reads=[act.bs[fb], wd.bs[0 if fb < 11 else 1]], writes=[PS[ob[half]].b], inc=(fb == NFB - 1))
                            ot = outt[c % 2]
                            for half in range(2):
                                hs = slice(half * 512, (half + 1) * 512)
                                kb.op(DVE, lambda half=half, hs=hs, ob=ob, ot=ot, xx=xx: V.tensor_tensor(out=ot.t[:, hs], in0=xx.t[:, hs], in1=pf(ob[half]), op=ALU.add),
                                      reads=[xx.b, PS[ob[half]].b], writes=[ot.bs[half]])
                            kb.dma(SP, out[tok, :], ot.t[:], reads=ot.bs, is_out=True)

            with ExitStack() as sB:
                ynsaT = sb(sB, "ynsaT", [128, 4, S], BF16, NT)
                yretT = sb(sB, "yretT", [128, 4, S], BF16, NT)

                with ExitStack() as sA:
                    gq = sb(sA, "gq", [128, 64], F32)
                    gk = sb(sA, "gk", [128, 3, 64], F32)
                    QAL = sb(sA, "QAL", [128, NT, 8, 4], BF16)
                    KAL = sb(sA, "KAL", [128, NT, 4], BF16)
                    KCAL = sb(sA, "KCAL", [128, 4], BF16)
                    OH = sb(sA, "OH", [128, NT, 32], BF16)
                    dmask8 = sb(sA, "dmask8", [128, 8, 128], BF16)
                    tmask8 = sb(sA, "tmask8", [128, 8, 128], BF16)
                    cmask = sb(sA, "cmask", [128, S], BF16)
                    addc = sb(sA, "addc", [128, NT, 32], F32)
                    ov = sb(sA, "ov", [128, 32], BF16)
                    KT_slc = sb(sA, "KT_slc", [128, 2, S], BF16, NT)
                    KT_win = sb(sA, "KT_win", [128, 2, S], BF16, NT)
                    V_slc = sb(sA, "V_slc", [128, NT, 2, 65], BF16, NT)
                    V_win = sb(sA, "V_win", [128, NT, 2, 65], BF16, NT)
                    KcT = sb(sA, "KcT", [128, 2, 128], BF16)
                    Vc = sb(sA, "Vc", [128, 2, 97], BF16)

                    with ExitStack() as s0:
                        SL = sb(s0, "SL", [128, 8], F32)
                        th128 = sb(s0, "th128", [128, NT], F32)
                        pidx = sb(s0, "pidx", [128, 1], F32)
                        QALf = sb(s0, "QALf", [128, NT, 8, 4], F32)
                        KALf = sb(s0, "KALf", [128, NT, 4], F32)
                        KCALf = sb(s0, "KCALf", [128, 4], F32)
                        rel = sb(s0, "rel", [128, NT, 32], F32)
                        f0 = sb(s0, "f0", [128, NT, 32], F32)
                        f1 = sb(s0, "f1", [128, NT, 32], F32)
                        t1 = sb(s0, "t1", [128, NT, 32], F32)
                        hp = sb(s0, "hp", [128, 1], F32)
                        ovf = sb(s0, "ovf", [128, 32], F32)
                        ova = sb(s0, "ova", [128, 32], F32)
                        ones_b = sb(s0, "ones_b", [128, 512], BF16)
                        ones_b2 = sb(s0, "ones_b2", [128, 1024], BF16)
                        zeros_b = sb(s0, "zeros_b", [128, 512], BF16)

                        kb.dma(SP, gq.t[:], nsa_q_norm[0:1, :].broadcast_to([128, 64]), writes=[gq.b])
                        kb.dma(SP, gk.t[:].rearrange("p a b -> p (a b)"),
                               nsa_k_norm.rearrange("o a b -> o (a b)").broadcast_to([128, 192]), writes=[gk.b])
                        kb.op(DVE, lambda: V.tensor_scalar(out=gq.t[:], in0=gq.t[:], scalar1=0.125, scalar2=None, op0=ALU.mult),
                              reads=[gq.b], writes=[gq.b])
                        for h in range(8):
                            kb.op(POOL, lambda h=h: G.memset(SL.t[:, h:h + 1], 2.0 ** (-(h + 1))), writes=[SL.b])
                        kb.op(POOL, lambda: G.iota(th128.t[:], pattern=[[128, NT]], base=0, channel_multiplier=0,
                                                   allow_small_or_imprecise_dtypes=True), writes=[th128.b])
                        kb.op(POOL, lambda: G.iota(pidx.t[:], pattern=[[0, 1]], base=0, channel_multiplier=1,
                                                   allow_small_or_imprecise_dtypes=True), writes=[pidx.b])
                        SLb = SL.t[:].unsqueeze(1).broadcast_to([128, NT, 8])
                        THb = th128.t[:].unsqueeze(2).broadcast_to([128, NT, 8])
                        kb.op(DVE, lambda: V.scalar_tensor_tensor(out=QALf.t[:, :, :, 0], in0=THb, scalar=-1.0, in1=SLb,
                                                                  op0=ALU.mult, op1=ALU.mult),
                              reads=[SL.b, th128.b], writes=[QALf.b])
                        kb.op(DVE, lambda: V.tensor_scalar(out=QALf.t[:, :, :, 1], in0=SLb, scalar1=pidx.t[:, 0:1], scalar2=-1.0,
                                                           op0=ALU.mult, op1=ALU.mult),
                              reads=[SL.b, pidx.b], writes=[QALf.b])
                        kb.op(DVE, lambda: V.tensor_copy(out=QALf.t[:, :, :, 2], in_=SLb), reads=[SL.b], writes=[QALf.b])
                        kb.op(DVE, lambda: V.tensor_copy(out=QALf.t[:, :, :, 3], in_=SLb), reads=[SL.b], writes=[QALf.b])
                        kb.op(DVE, lambda: V.tensor_copy(out=QAL.t[:], in_=QALf.t[:]), reads=[QALf.b], writes=[QAL.b])
                        kb.op(POOL, lambda: G.memset(KALf.t[:, :, 0:2], 1.0), writes=[KALf.b])
                        kb.op(DVE, lambda: V.tensor_copy(out=KALf.t[:, :, 2], in_=th128.t[:]), reads=[th128.b], writes=[KALf.b])
                        kb.op(DVE, lambda: V.tensor_copy(out=KALf.t[:, :, 3], in_=pidx.t[:, 0:1].broadcast_to([128, NT])),
                              reads=[pidx.b], writes=[KALf.b])
                        kb.op(DVE, lambda: V.tensor_copy(out=KAL.t[:], in_=KALf.t[:]), reads=[KALf.b], writes=[KAL.b])
                        kb.op(POOL, lambda: G.memset(KCALf.t[:, 0:2], 1.0), writes=[KCALf.b])
                        kb.op(POOL, lambda: G.memset(KCALf.t[:, 3:4], 31.0), reads=[], writes=[KCALf.b])
                        kb.op(DVE, lambda: V.tensor_scalar(out=KCALf.t[:, 2:3], in0=pidx.t[:, 0:1], scalar1=16.0, scalar2=None,
                                                           op0=ALU.mult), reads=[pidx.b], writes=[KCALf.b])
                        kb.op(DVE, lambda: V.tensor_copy(out=KCAL.t[:], in_=KCALf.t[:]), reads=[KCALf.b], writes=[KCAL.b])
                        kb.op(POOL, lambda: G.memset(OH.t[:], 0.0), writes=[OH.b])
                        for kt in range(NT):
                            kb.op(POOL, lambda kt=kt: G.memset(OH.t[0:64, kt, 2 * kt:2 * kt + 1], 1.0), writes=[OH.b])
                            kb.op(POOL, lambda kt=kt: G.memset(OH.t[64:128, kt, 2 * kt + 1:2 * kt + 2], 1.0), writes=[OH.b])
                        kb.op(POOL, lambda: G.memset(ones_b.t[:], 1.0), writes=[ones_b.b])
                        kb.op(POOL, lambda: G.memset(zeros_b.t[:], 0.0), writes=[zeros_b.b])
                        ob8 = ones_b2.t[:].rearrange("p (a b) -> p a b", b=128)
                        kb.op(POOL, lambda: G.memset(ones_b2.t[:], 1.0), writes=[ones_b2.b])
                        kb.op(POOL, lambda: G.affine_select(out=dmask8.t[:], in_=ob8, pattern=[[0, 8], [1, 128]],
                                                            compare_op=ALU.is_ge, fill=0.0, base=0, channel_multiplier=-1),
                              reads=[ones_b2.b], writes=[dmask8.b])
                        kb.op(POOL, lambda: G.affine_select(out=tmask8.t[:], in_=ob8, pattern=[[0, 8], [-1, 128]],
                                                            compare_op=ALU.is_gt, fill=0.0, base=0, channel_multiplier=1),
                              reads=[ones_b2.b], writes=[tmask8.b])
                        for i in range(4):
                            kb.op(POOL, lambda i=i: G.affine_select(out=cmask.t[:, i * 512:(i + 1) * 512], in_=zeros_b.t[:],
                                                                    pattern=[[1, 512]], compare_op=ALU.is_ge, fill=NEG,
                                                                    base=-31 + 512 * i, channel_multiplier=-16),
                                  reads=[zeros_b.b], writes=[cmask.b])
                        kb.op(POOL, lambda: G.iota(rel.t[:], pattern=[[-2, NT], [1, 32]], base=0, channel_multiplier=0,
                                                   allow_small_or_imprecise_dtypes=True), writes=[rel.b])
                        kb.op(DVE, lambda: V.tensor_scalar(out=hp.t[:], in0=pidx.t[:], scalar1=64.0, scalar2=None, op0=ALU.is_ge),
                              reads=[pidx.b], writes=[hp.b])
                        kb.op(DVE, lambda: V.tensor_scalar(out=rel.t[:], in0=rel.t[:], scalar1=hp.t[:, 0:1], scalar2=None,
                                                           op0=ALU.subtract), reads=[rel.b, hp.b], writes=[rel.b])
                        kb.op(DVE, lambda: V.tensor_scalar(out=t1.t[:], in0=rel.t[:], scalar1=0.0, scalar2=-1e9,
                                                           op0=ALU.is_gt, op1=ALU.mult), reads=[rel.b], writes=[t1.b])
                        kb.op(DVE, lambda: V.tensor_scalar(out=f0.t[:], in0=rel.t[:], scalar1=0.0, scalar2=None, op0=ALU.is_equal),
                              reads=[rel.b], writes=[f0.b])
                        kb.op(DVE, lambda: V.tensor_scalar(out=f1.t[:], in0=rel.t[:], scalar1=-1.0, scalar2=None, op0=ALU.is_equal),
                              reads=[rel.b], writes=[f1.b])
                        kb.op(DVE, lambda: V.tensor_tensor(out=f0.t[:], in0=f0.t[:], in1=f1.t[:], op=ALU.max),
                              reads=[f0.b, f1.b], writes=[f0.b])
                        kb.op(DVE, lambda: V.memset(f0.t[:, :, 0:1], 1.0), reads=[], writes=[f0.b])
                        kb.op(DVE, lambda: V.scalar_tensor_tensor(out=addc.t[:], in0=f0.t[:], scalar=1e4, in1=t1.t[:],
                                                                  op0=ALU.mult, op1=ALU.add), reads=[f0.b, t1.b], writes=[addc.b])
                        kb.op(POOL, lambda: G.iota(ovf.t[:], pattern=[[-64, 32]], base=0, channel_multiplier=16,
                                                   allow_small_or_imprecise_dtypes=True), writes=[ovf.b])
                        kb.op(DVE, lambda: V.tensor_scalar(out=ova.t[:], in0=ovf.t[:], scalar1=63.0, scalar2=None, op0=ALU.is_le),
                              reads=[ovf.b], writes=[ova.b])
                        kb.op(DVE, lambda: V.tensor_scalar(out=ovf.t[:], in0=ovf.t[:], scalar1=-31.0, scalar2=None, op0=ALU.is_ge),
                              reads=[ovf.b], writes=[ovf.b])
                        kb.op(DVE, lambda: V.tensor_tensor(out=ov.t[:], in0=ova.t[:], in1=ovf.t[:], op=ALU.mult),
                              reads=[ova.b, ovf.b], writes=[ov.b])
                        kb.op(POOL, lambda: G.memset(V_slc.t[:, :, :, 64:65], 1.0), writes=V_slc.bs)
                        kb.op(POOL, lambda: G.memset(V_win.t[:, :, :, 64:65], 1.0), writes=V_win.bs)
                        kb.barrier()
                        chk(1)

                    with ExitStack() as s2:
                        cmpT = sb(s2, "cmpT", [128, 2, S], BF16, NT)
                        w1sb = sb(s2, "w1sb", [128, 2, 32, 128], BF16)
                        w2sb = sb(s2, "w2sb", [128, 2, 64], BF16)
                        pe_sb = sb(s2, "pe_sb", [32, 2, 64], F32)
                        with ExitStack() as s2a:
                            xt = [sb(s2a, f"xta{i}", [128, D], F32) for i in range(2)]
                            g1bc = sb(s2a, "g1bc", [128, D], F32)
                            kb.dma(SP, g1bc.t[:], norm1_g[0:1, :].broadcast_to([128, D]), writes=[g1bc.b])

                            def p1a_gen(c):
                                xx = xt[c % 2]
                                kb.dma(SP, xx.t[:], x[c * 128:(c + 1) * 128, :], writes=[xx.b])
                                yield
                                yield from norm_gen(xx, c, g1bc, c % 2, 6 + (c % 2))

                            wkv = sb(s2a, "wkv", [128, KC, 768], BF16)
                            load_w(wkv, w_in_v[:, :, C_KV:C_KV + 768])
                            for kv in range(2):
                                src = cmp_w1[0, kv].rearrange("l d f -> d l f")
                                kb.dma(POOL, w1sb.t[0:64, kv], src, writes=[w1sb.b])
                                kb.dma(POOL, w1sb.t[64:128, kv], src, writes=[w1sb.b])
                            kb.dma(POOL, w2sb.t[:], cmp_w2[0].rearrange("k f d -> f k d"), writes=[w2sb.b])
                            kb.dma(SP, pe_sb.t[:], cmp_pe[0].rearrange("k l d -> l k d"), writes=[pe_sb.b])
                            cmp_tok = [sb(s2a, f"cmp_tok{i}", [128, 256], BF16) for i in range(2)]
                            sqk = [sb(s2a, f"sqk{i}", [128, 256], F32) for i in range(2)]
                            kst = sb(s2a, "kst", [128, NT, 16], F32, NT)
                            tmpk = [sb(s2a, f"tmpk{i}", [128, 4, 64], F32) for i in range(2)]
                            ka_slc = [sb(s2a, f"ka_slc{i}", [128, 2, 128], BF16, 2) for i in range(2)]
                            ka_win = [sb(s2a, f"ka_win{i}", [128, 2, 128], BF16, 2) for i in range(2)]
                            for i in range(2):
                                kb.op(POOL, lambda i=i: G.memset(ka_slc[i].t[:], 0.0), writes=ka_slc[i].bs)
                                kb.op(POOL, lambda i=i: G.memset(ka_win[i].t[:], 0.0), writes=ka_win[i].bs)
                            def p1b_gen(c):
                                i2 = c % 2
                                bA, bB = (0, 1) if i2 == 0 else (2, 3)
                                tok = slice(c * 128, (c + 1) * 128)
                                if CUT >= 1:
                                    yield
                                    for k in range(KC):
                                        kb.op(PE, lambda k=k: TE.matmul(pf(bA), lhsT=hT.t[:, k, tok], rhs=wkv.t[:, k, 0:512],
                                                                        start=(k == 0), stop=(k == KC - 1)),
                                              reads=[hT.bs[c], wkv.b], writes=[PS[bA].b], inc=(k == KC - 1))
                                    for k in range(KC):
                                        kb.op(PE, lambda k=k: TE.matmul(pf(bB)[:, 0:256], lhsT=hT.t[:, k, tok], rhs=wkv.t[:, k, 512:768],
                                                                        start=(k == 0), stop=(k == KC - 1)),
                                              reads=[hT.bs[c], wkv.b], writes=[PS[bB].b], inc=(k == KC - 1))
                                if CUT >= 2:
                                    yield
                                    ct = cmp_tok[i2]
                                    kb.op(ACT, lambda: A.copy(out=ct.t[:], in_=pf(bA)[:, 0:256]), reads=[PS[bA].b], writes=[ct.b])
                                    sq = sqk[i2]
                                    kb.op(ACT, lambda: A.activation(out=sq.t[:, 0:128], in_=pf(bA)[:, 256:384], func=AF.Square),
                                          reads=[PS[bA].b], writes=[sq.b])
                                    kb.op(ACT, lambda: A.activation(out=sq.t[:, 128:256], in_=pf(bB)[:, 0:128], func=AF.Square),
                                          reads=[PS[bB].b], writes=[sq.b])
                                if CUT >= 3:
                                    yield
                                    ks = kst.t
                                    ksb = [kst.bs[c]]
                                    kb.op(DVE, lambda: V.tensor_reduce(out=ks[:, c, 0:4], in_=sq.t[:].rearrange("p (a b) -> p a b", b=64),
                                                                       axis=AX.X, op=ALU.add), reads=[sq.b], writes=ksb)
                                    rstd_from_ss(ks[:, c, 0:4], ks[:, c, 4:8], ks[:, c, 8:12], ks[:, c, 12:16], 64, ksb)
                                    tk = tmpk[i2]
                                    kb.op(DVE, lambda: V.tensor_tensor(out=tk.t[:, 0:2, :], in0=pf(bA)[:, 256:384].rearrange("p (a b) -> p a b", b=64),
                                                                       in1=ks[:, c, 12:14].unsqueeze(2).broadcast_to([128, 2, 64]), op=ALU.mult),
                                          reads=[PS[bA].b] + ksb, writes=[tk.b])
                                    kb.op(DVE, lambda: V.tensor_tensor(out=tk.t[:, 2:4, :], in0=pf(bB)[:, 0:128].rearrange("p (a b) -> p a b", b=64),
                                                                       in1=ks[:, c, 14:16].unsqueeze(2).broadcast_to([128, 2, 64]), op=ALU.mult),
                                          reads=[PS[bB].b] + ksb, writes=[tk.b])
                                    ksl, kwn = ka_slc[i2], ka_win[i2]
                                    kb.op(DVE, lambda: V.tensor_tensor(out=ksl.t[:, :, 0:64], in0=tk.t[:, 0:2, :],
                                                                       in1=gk.t[:, 1:2, :].broadcast_to([128, 2, 64]), op=ALU.mult),
                                          reads=[tk.b, gk.b], writes=[ksl.bs[0]])
                                    kb.op(DVE, lambda: V.tensor_tensor(out=kwn.t[:, :, 0:64], in0=tk.t[:, 2:4, :],
                                                                       in1=gk.t[:, 2:3, :].broadcast_to([128, 2, 64]), op=ALU.mult),
                                          reads=[tk.b, gk.b], writes=[kwn.bs[0]])
                                if CUT >= 4:
                                    yield
                                    kb.op(POOL, lambda: G.tensor_copy(out=ksl.t[:, :, 64:96], in_=OH.t[:, c:c + 1, :].broadcast_to([128, 2, 32])),
                                          reads=[OH.b], writes=[ksl.bs[1]])
                                    kb.op(POOL, lambda: G.tensor_copy(out=ksl.t[:, :, 96:100], in_=KAL.t[:, c:c + 1, :].broadcast_to([128, 2, 4])),
                                          reads=[KAL.b], writes=[ksl.bs[1]])
                                    kb.op(POOL, lambda: G.tensor_copy(out=kwn.t[:, :, 96:100], in_=KAL.t[:, c:c + 1, :].broadcast_to([128, 2, 4])),
                                          reads=[KAL.b], writes=[kwn.bs[1]])
                                if CUT >= 5:
                                    yield
                                    kb.op(ACT, lambda: A.copy(out=V_slc.t[:, c, :, 0:64], in_=pf(bA)[:, 384:512].rearrange("p (a b) -> p a b", b=64)),
                                          reads=[PS[bA].b], writes=[V_slc.bs[c]])
                                    kb.op(ACT, lambda: A.copy(out=V_win.t[:, c, :, 0:64], in_=pf(bB)[:, 128:256].rearrange("p (a b) -> p a b", b=64)),
                                          reads=[PS[bB].b], writes=[V_win.bs[c]])
                                if CUT >= 6:
                                    yield
                                    tb = 4 + i2
                                    transposes([ksl.t[:, 0, :], ksl.t[:, 1, :], kwn.t[:, 0, :], kwn.t[:, 1, :], ct.t[:, 0:128], ct.t[:, 128:256]],
                                               tb, ksl.bs + kwn.bs + [ct.b])
                                    pv3 = pb(tb).rearrange("p (a b) -> p a b", b=128)
                                    kb.op(ACT, lambda: A.copy(out=KT_slc.t[:, :, tok], in_=pv3[:, 0:2, :]), reads=[PS[tb].b], writes=[KT_slc.bs[c]])
                                    kb.op(ACT, lambda: A.copy(out=KT_win.t[:, :, tok], in_=pv3[:, 2:4, :]), reads=[PS[tb].b], writes=[KT_win.bs[c]])
                                    kb.op(ACT, lambda: A.copy(out=cmpT.t[:, :, tok], in_=pv3[:, 4:6, :]), reads=[PS[tb].b], writes=[cmpT.bs[c]])
                            def p1ab_gen(c):
                                yield from p1a_gen(c)
                                yield from p1b_gen(c)

                            run_interleaved([(lambda c=c: p1ab_gen(c)) for c in range(NT)], width=2)
                            if "KT_slc" in dbg:
                                kdb = sb(s2a, "kdb", [128, 2, S], F32)
                                kb.op(DVE, lambda: V.tensor_copy(out=kdb.t[:], in_=KT_slc.t[:]), reads=KT_slc.bs, writes=[kdb.b])
                                kb.dma(SP, dbg["KT_slc"].rearrange("p (a b) -> p a b", b=S), kdb.t[:], reads=[kdb.b], is_out=True)
                            kb.barrier()
                            chk(3)

                        with ExitStack() as s2b:
                            peT = sb(s2b, "peT", [64, 2, 32], BF16)
                            bias_c = sb(s2b, "bias_c", [128, 2], F32)
                            xhs = [sb(s2b, f"xh{i}", [128, 128], F32) for i in range(4)]
                            x2s = [sb(s2b, f"x2{i}", [128, 128], F32) for i in range(4)]
                            sgs_ = [sb(s2b, f"sgc{i}", [128, 128], F32) for i in range(4)]
                            HTbs = [sb(s2b, f"HTb{i}", [128, 128], BF16) for i in range(4)]
                            kca = sb(s2b, "kca", [128, 2, 128], BF16)
                            cst = sb(s2b, "cst", [128, 8], F32)
                            tmpcs = [sb(s2b, f"tmpc{i}", [128, 64], F32) for i in range(4)]
                            kb.op(POOL, lambda: G.memset(kca.t[:], 0.0), writes=[kca.b])
                            kb.op(POOL, lambda: G.memset(Vc.t[:], 0.0), writes=[Vc.b])
                            kb.op(POOL, lambda: G.tensor_copy(out=kca.t[:, :, 96:100], in_=KCAL.t[:].unsqueeze(1).broadcast_to([128, 2, 4])),
                                  reads=[KCAL.b], writes=[kca.b])
                            kb.op(POOL, lambda: G.memset(Vc.t[:, :, 64:65], 1.0), writes=[Vc.b])
                            kb.op(POOL, lambda: G.tensor_copy(out=Vc.t[:, :, 65:97], in_=ov.t[:].unsqueeze(1).broadcast_to([128, 2, 32])),
                                  reads=[ov.b], writes=[Vc.b])
                            for kv in range(2):
                                kb.op(PE, lambda kv=kv: TE.transpose(out=pf(0)[0:64, kv * 32:(kv + 1) * 32], in_=pe_sb.t[0:32, kv, :],
                                                                     identity=ident_f.t[0:32, 0:32]),
                                      reads=[pe_sb.b, ident_f.b], writes=[PS[0].b])
                            kb.op(DVE, lambda: V.tensor_copy(out=peT.t[:], in_=pf(0)[0:64, 0:64].rearrange("p (a b) -> p a b", b=32)),
                                  reads=[PS[0].b], writes=[peT.b])
                            for kv in range(2):
                                for l in range(32):
                                    kb.op(PE, lambda kv=kv, l=l: TE.matmul(pf(1)[:, kv:kv + 1], lhsT=w1sb.t[0:64, kv, l, :], rhs=peT.t[0:64, kv, l:l + 1],
                                                                           start=(l == 0), stop=(l == 31)),
                                          reads=[w1sb.b, peT.b], writes=[PS[1].b], inc=(l == 31))
                            kb.op(DVE, lambda: V.tensor_copy(out=bias_c.t[:], in_=pf(1)[:, 0:2]), reads=[PS[1].b], writes=[bias_c.b])
                            def cmp_gen(kv, g, idx):
                                bH, bO = 2 * idx, 2 * idx + 1
                                xh, x2, sg, HTb, tmpc = xhs[idx], x2s[idx], sgs_[idx], HTbs[idx], tmpcs[idx]
                                for l in range(32):
                                    kb.op(PE, lambda kv=kv, g=g, l=l: TE.matmul(
                                        pf(bH)[:, 0:127], lhsT=w1sb.t[g * 64:(g + 1) * 64, kv, l, :],
                                        rhs=cmpT.t[g * 64:(g + 1) * 64, kv, l:l + 16 * 126 + 1:16],
                                        start=(l == 0), stop=(l == 31)),
                                        reads=[w1sb.b] + cmpT.bs, writes=[PS[bH].b], inc=(l == 31))
                                yield
                                kb.op(ACT, lambda kv=kv: A.activation(out=xh.t[:, 0:127], in_=pf(bH)[:, 0:127], func=AF.Identity,
                                                                      bias=bias_c.t[:, kv:kv + 1], scale=1.0),
                                      reads=[PS[bH].b, bias_c.b], writes=[xh.b])
                                yield
                                kb.op(DVE, lambda: V.tensor_tensor(out=x2.t[:, 0:127], in0=xh.t[:, 0:127], in1=xh.t[:, 0:127], op=ALU.mult),
                                      reads=[xh.b], writes=[x2.b])
                                yield
                                kb.op(DVE, lambda: V.tensor_scalar(out=x2.t[:, 0:127], in0=x2.t[:, 0:127], scalar1=0.044715, scalar2=1.0,
                                                                   op0=ALU.mult, op1=ALU.add), reads=[x2.b], writes=[x2.b])
                                yield
                                kb.op(DVE, lambda: V.tensor_tensor(out=x2.t[:, 0:127], in0=x2.t[:, 0:127], in1=xh.t[:, 0:127], op=ALU.mult),
                                      reads=[x2.b, xh.b], writes=[x2.b])
                                yield
                                kb.op(ACT, lambda: A.activation(out=sg.t[:, 0:127], in_=x2.t[:, 0:127], func=AF.Sigmoid, scale=1.5957691216057308),
                                      reads=[x2.b], writes=[sg.b])
                                yield
                                kb.op(DVE, lambda: V.tensor_tensor(out=HTb.t[:, 0:127], in0=xh.t[:, 0:127], in1=sg.t[:, 0:127], op=ALU.mult),
                                      reads=[xh.b, sg.b], writes=[HTb.b])
                                yield
                                kb.op(PE, lambda kv=kv: TE.matmul(pf(bO)[0:127, 0:64], lhsT=HTb.t[:, 0:127], rhs=w2sb.t[:, kv, :], start=True, stop=True),
                                      reads=[HTb.b, w2sb.b], writes=[PS[bO].b])
                                yield
                                if kv == 0:
                                    kb.op(ACT, lambda g=g: A.activation(out=tmpc.t[0:127, :], in_=pf(bO)[0:127, 0:64], func=AF.Square,
                                                                        accum_out=cst.t[0:127, g:g + 1]),
                                          reads=[PS[bO].b], writes=[tmpc.b, cst.b])
                                    rstd_from_ss(cst.t[0:127, g:g + 1], cst.t[0:127, 2 + g:3 + g], cst.t[0:127, 4 + g:5 + g], cst.t[0:127, 6 + g:7 + g], 64, [cst.b])
                                    kb.op(DVE, lambda g=g: V.scalar_tensor_tensor(out=kca.t[0:127, g, 0:64], in0=pf(bO)[0:127, 0:64],
                                                                                  scalar=cst.t[0:127, 6 + g:7 + g], in1=gk.t[0:127, 0, :],
                                                                                  op0=ALU.mult, op1=ALU.mult),
                                          reads=[PS[bO].b, cst.b, gk.b], writes=[kca.b])
                                else:
                                    kb.op(ACT, lambda g=g: A.copy(out=Vc.t[0:127, g, 0:64], in_=pf(bO)[0:127, 0:64]),
                                          reads=[PS[bO].b], writes=[Vc.b])
                                yield

                            run_interleaved([(lambda kv=kv, g=g: cmp_gen(kv, g, 2 * kv + g)) for kv in range(2) for g in range(2)], width=4)
                            transposes([kca.t[:, 0, :], kca.t[:, 1, :]], 6, [kca.b])
                            kb.op(ACT, lambda: A.copy(out=KcT.t[:], in_=pb(6)[:, 0:256].rearrange("p (a b) -> p a b", b=128)),
                                  reads=[PS[6].b], writes=[KcT.b])
                            if "kc" in dbg:
                                kcd = sb(s2b, "kcd", [128, 2, 64], F32)
                                kb.op(DVE, lambda: V.tensor_copy(out=kcd.t[:], in_=kca.t[:, :, 0:64]), reads=[kca.b], writes=[kcd.b])
                                kb.dma(SP, dbg["kc"].rearrange("p (a b) -> p a b", b=64), kcd.t[:], reads=[kcd.b], is_out=True)
                            if "vc" in dbg:
                                vcd = sb(s2b, "vcd", [128, 2, 64], F32)
                                kb.op(DVE, lambda: V.tensor_copy(out=vcd.t[:], in_=Vc.t[:, :, 0:64]), reads=[Vc.b], writes=[vcd.b])
                                kb.dma(SP, dbg["vc"].rearrange("p (a b) -> p a b", b=64), vcd.t[:], reads=[vcd.b], is_out=True)
                            kb.barrier()
                            chk(4)

                    phase_1c()
                    kb.barrier()
                    chk(5)

                wm = sb(sB, "wm", [128, KC, 2048], BF16, 4)
                wbr = sb(sB, "wbr", [128, 8, D], BF16)
                phase_1d()
                kb.barrier()
                chk(6)
                phase_1e()
                kb.barrier()
                chk(7)

            phase_2()
            kb.finish()
    except _Stop:
        pass
    return nc


_NAMES = ["x", "norm1_g", "w_in", "nsa_q_norm", "nsa_k_norm", "cmp_pe", "cmp_w1", "cmp_w2", "ret_gn_g",
          "w_branch", "w_out", "norm2_g", "ffn_w_gate", "ffn_w_up", "ffn_w_down"]


def kernel(**inputs):
    n = 8
    arrs = {k: np.ascontiguousarray(np.asarray(inputs[k], dtype=np.float32)) for k in _NAMES}
    nc = build_nc()
    in_maps = []
    for i in range(n):
        m = {k: arrs[k] for k in _NAMES if k != "x"}
        m["x"] = np.ascontiguousarray(arrs["x"][i])
        in_maps.append(m)
    res = run_bass_kernel_spmd(nc, in_maps, core_ids=list(range(n)))
    return np.stack([np.asarray(r["out"], dtype=np.float32) for r in res.results], axis=0)
```

```python
import numpy as np
from contextlib import ExitStack
import concourse.bass as bass
import concourse.mybir as mybir
from concourse.bass_utils import run_bass_kernel_spmd

F32 = mybir.dt.float32
BF16 = mybir.dt.bfloat16
AF = mybir.ActivationFunctionType
ALU = mybir.AluOpType
AX = mybir.AxisListType

S = 2048
D = 1024
NT = 16
KC = 8
N_IN = 5400
DFF = 2816
NFB = 22
EPS = 1e-6
SEM_LIMIT = 24000
import os as _os
CUT = int(_os.environ.get('P1B_CUT', '99'))
CUTC = int(_os.environ.get('P1C_CUT', '99'))
SUBC = int(_os.environ.get('P1C_SUB', '99'))
WIDTH = int(_os.environ.get('P1C_WIDTH', '2'))
XY1 = int(_os.environ.get('XY1', '0'))
XY2 = int(_os.environ.get('XY2', '2'))
XY3 = int(_os.environ.get('XY3', '0'))
PREWIN = int(_os.environ.get('PREWIN', '1'))
NEG = -30000.0

C_Q = 0
C_KV = 512
C_G = 1280
C_R = 1304
C_M = 3352


class Buf:
    __slots__ = ("name", "w", "r", "excl")

    def __init__(self, name):
        self.name = name
        self.w = None
        self.r = []
        self.excl = False


class SemW:
    __slots__ = ("h",)

    def __init__(self, h):
        self.h = h


class Slot:
    __slots__ = ("sem", "val")

    def __init__(self, sem):
        self.sem = sem
        self.val = 0


class Q:
    def __init__(self, name, eng):
        self.name = name
        self.eng = eng
        self.sem = None
        self.count = 0
        self.waited = {}
        self.ring = []
        self.ri = 0
        self.pending = False


class T:
    def __init__(self, t, name, nb=1):
        self.t = t
        self.bs = [Buf(f"{name}{i}") for i in range(nb)]

    @property
    def b(self):
        return self.bs[0]


class KB:
    def __init__(self, nc, es):
        self.nc = nc
        self.es = es
        self.nsem = 0
        self.pe = self.mkq("pe", nc.tensor)
        self.act = self.mkq("act", nc.scalar)
        self.dve = self.mkq("dve", nc.vector)
        self.pool = self.mkq("pool", nc.gpsimd)
        self.sp = self.mkq("sp", nc.sync)
        self.qs = [self.pe, self.act, self.dve, self.pool, self.sp]
        for q, n in ((self.sp, 16), (self.pool, 8), (self.act, 4)):
            q.ring = [Slot(self.new_sem(f"{q.name}_d{i}")) for i in range(n)]
        self.out_toks = []

    def new_sem(self, name):
        self.nsem += 1
        return SemW(self.es.enter_context(self.nc.semaphore(f"{name}_{self.nsem}")))

    def mkq(self, name, eng):
        q = Q(name, eng)
        q.sem = self.new_sem(name)
        return q

    def wait(self, q, tok):
        sw, val = tok[0], tok[1]
        if q.waited.get(sw, 0) >= val:
            return
        q.eng.wait_ge(sw.h, val)
        q.waited[sw] = val

    def _dep(self, q, tok, raw, force=False):
        if tok[2] is q and q is self.pe and not force:
            return
        self.wait(q, tok)

    def _deps(self, q, reads, writes, force=False):
        for b in reads:
            if b.w is not None:
                self._dep(q, b.w, True, force)
            if b.excl:
                for t in b.r:
                    if t[2] is not q:
                        self._dep(q, t, False, force)
        for b in writes:
            if b.w is not None:
                self._dep(q, b.w, False, force)
            for t in b.r:
                self._dep(q, t, False, force)

    def _record(self, tok, reads, writes):
        for b in reads:
            if tok[2] is not None:
                b.r = [t for t in b.r if t[2] is not tok[2]]
            b.r.append(tok)
        for b in writes:
            b.w = tok
            b.r = []

    def op(self, q, fn, reads=(), writes=(), inc=True):
        self._deps(q, reads, writes)
        ins = fn()
        if inc:
            if q.count >= SEM_LIMIT and not q.pending:
                q.sem = self.new_sem(q.name)
                q.count = 0
            ins.then_inc(q.sem.h, 1)
            q.count += 1
            q.pending = False
            tok = (q.sem, q.count, q)
        else:
            q.pending = True
            tok = (q.sem, q.count + 1, q)
        self._record(tok, reads, writes)
        return ins

    def dma(self, q, out, in_, reads=(), writes=(), is_out=False):
        self._deps(q, reads, writes, force=True)
        slot = q.ring[q.ri % len(q.ring)]
        q.ri += 1
        if slot.val > 0:
            self.wait(q, (slot.sem, slot.val))
        if slot.val >= SEM_LIMIT:
            slot.sem = self.new_sem(q.name + "_d")
            slot.val = 0
        ins = q.eng.dma_start(out=out, in_=in_)
        ins.then_inc(slot.sem.h, 16)
        slot.val += 16
        tok = (slot.sem, slot.val, None)
        self._record(tok, reads, writes)
        if is_out:
            self.out_toks.append(tok)
        return tok

    def barrier(self):
        toks = []
        for o in self.qs:
            if o.count > 0:
                toks.append((o.sem, o.count, o))
            for sl in o.ring:
                if sl.val > 0:
                    toks.append((sl.sem, sl.val, None))
        for q in self.qs:
            for t in toks:
                if t[2] is q:
                    continue
                self.wait(q, t)

    def finish(self):
        for t in self.out_toks:
            self.wait(self.sp, t)


class _Stop(Exception):
    pass


def build_nc(debug=None, stop=None):
    nc = bass.Bass("TRN2", target_bir_lowering=False)

    def din(name, shape):
        return nc.dram_tensor(name, list(shape), F32, kind="ExternalInput").ap()

    x = din("x", [S, D])
    norm1_g = din("norm1_g", [1, D])
    w_in = din("w_in", [1, D, N_IN])
    nsa_q_norm = din("nsa_q_norm", [1, 64])
    nsa_k_norm = din("nsa_k_norm", [1, 3, 64])
    cmp_pe = din("cmp_pe", [1, 2, 32, 64])
    cmp_w1 = din("cmp_w1", [1, 2, 32, 64, 128])
    cmp_w2 = din("cmp_w2", [1, 2, 128, 64])
    ret_gn_g = din("ret_gn_g", [1, 4, 128])
    w_branch = din("w_branch", [1, 2, 512, D])
    w_out = din("w_out", [1, D, D])
    norm2_g = din("norm2_g", [1, D])
    ffn_w_gate = din("ffn_w_gate", [1, D, DFF])
    ffn_w_up = din("ffn_w_up", [1, D, DFF])
    ffn_w_down = din("ffn_w_down", [1, DFF, D])
    out = nc.dram_tensor("out", [S, D], F32, kind="ExternalOutput").ap()
    xmid = nc.dram_tensor("xmid", [S, D], F32, kind="Internal").ap()
    dbg = {}
    if debug:
        for name, shape in debug.items():
            dbg[name] = nc.dram_tensor("dbg_" + name, list(shape), F32, kind="ExternalOutput").ap()

    w_in_v = w_in[0].rearrange("(k p) n -> p k n", p=128)

    try:
        with ExitStack() as es:
            kb = KB(nc, es)

            def chk(n):
                if stop is not None and n >= stop:
                    kb.barrier()
                    kb.finish()
                    raise _Stop()
            PE, ACT, DVE, POOL, SP = kb.pe, kb.act, kb.dve, kb.pool, kb.sp
            V, A, G, TE = nc.vector, nc.scalar, nc.gpsimd, nc.tensor

            def sb(scope, name, shape, dt, nb=1):
                return T(scope.enter_context(nc.sbuf_tensor(name, list(shape), dt)), name, nb)

            PS2 = [es.enter_context(nc.psum_tensor(f"psp{j}", [128, 1024], F32)) for j in range(4)]
            PS = [T(None, f"ps{i}") for i in range(8)]
            for p_ in PS:
                p_.b.excl = True

            def pf(i):
                return PS2[i // 2][:, (i % 2) * 512:(i % 2 + 1) * 512]

            def pb(i):
                return PS2[i // 2][:].bitcast(BF16)[:, (i % 2) * 1024:(i % 2 + 1) * 1024]

            def pf2(j):
                return PS2[j][:]

            ident_f = sb(es, "ident_f", [128, 128], F32)
            ident_b = sb(es, "ident_b", [128, 128], BF16)
            ones_f = sb(es, "ones_f", [128, 128], F32)
            nhalf = sb(es, "nhalf", [128, 16], F32)
            hT = sb(es, "hT", [128, KC, S], BF16, NT)
            stat = sb(es, "stat", [128, NT, 4], F32, NT)
            hb = [sb(es, f"hb{i}", [128, D], BF16) for i in range(2)]
            junk = sb(es, "junk", [128, D], BF16)

            kb.op(POOL, lambda: G.memset(ones_f.t[:], 1.0), writes=[ones_f.b])
            kb.op(POOL, lambda: G.memset(nhalf.t[:], -0.5), writes=[nhalf.b])
            kb.op(POOL, lambda: G.affine_select(out=ident_f.t[:], in_=ones_f.t[:, 0:128], pattern=[[1, 128]],
                                                compare_op=ALU.is_equal, fill=0.0, base=0, channel_multiplier=-1),
                  reads=[ones_f.b], writes=[ident_f.b])
            kb.op(DVE, lambda: V.tensor_copy(out=ident_b.t[:], in_=ident_f.t[:]), reads=[ident_f.b], writes=[ident_b.b])

            def rstd_from_ss(ss_ap, ms_ap, sd_ap, rs_ap, n, bufs):
                k = ms_ap.shape[-1]
                P_ = ms_ap.shape[0]
                kb.op(DVE, lambda: V.tensor_scalar(out=ms_ap, in0=ss_ap, scalar1=1.0 / n, scalar2=EPS,
                                                   op0=ALU.mult, op1=ALU.add), reads=bufs, writes=bufs)
                kb.op(POOL, lambda: G.tensor_tensor(out=rs_ap, in0=ms_ap, in1=nhalf.t[0:P_, 0:k], op=ALU.pow),
                      reads=list(bufs) + [nhalf.b], writes=bufs)

            def transposes(src_aps, bank, reads):
                pbv = pb(bank)
                n = len(src_aps)
                for i, ap in enumerate(src_aps):
                    kb.op(PE, lambda ap=ap, i=i: TE.transpose(out=pbv[:, i * 128:(i + 1) * 128], in_=ap, identity=ident_b.t[:]),
                          reads=list(reads) + [ident_b.b], writes=[PS[bank].b], inc=(i == n - 1))

            def norm_gen(src, c, gbc, sidx, bank):
                sbuf_ = [stat.bs[c]]
                st = stat.t
                kb.op(ACT, lambda: A.activation(out=junk.t[:], in_=src.t[:], func=AF.Square, accum_out=st[:, c, 0:1]),
                      reads=src.bs, writes=[junk.b] + sbuf_)
                yield
                kb.op(DVE, lambda: V.tensor_scalar(out=st[:, c, 1:2], in0=st[:, c, 0:1], scalar1=1.0 / D, scalar2=EPS,
                                                   op0=ALU.mult, op1=ALU.add), reads=sbuf_, writes=sbuf_)
                yield
                kb.op(POOL, lambda: G.tensor_tensor(out=st[:, c, 3:4], in0=st[:, c, 1:2], in1=nhalf.t[:, 0:1], op=ALU.pow),
                      reads=sbuf_ + [nhalf.b], writes=sbuf_)
                yield
                h = hb[sidx % 2]
                kb.op(DVE, lambda: V.scalar_tensor_tensor(out=h.t[:], in0=src.t[:], scalar=st[:, c, 3:4], in1=gbc.t[:],
                                                          op0=ALU.mult, op1=ALU.mult),
                      reads=src.bs + [gbc.b] + sbuf_, writes=[h.b])
                yield
                for _ in range(XY3):
                    yield
                transposes([h.t[:, k * 128:(k + 1) * 128] for k in range(KC)], bank, [h.b])
                yield
                kb.op(ACT, lambda: A.copy(out=hT.t[:, :, c * 128:(c + 1) * 128],
                                          in_=pb(bank).rearrange("p (a b) -> p a b", b=128)),
                      reads=[PS[bank].b], writes=[hT.bs[c]])
                yield

            def run_interleaved(gen_fns, width=2):
                pending = list(gen_fns)
                active = []
                while pending or active:
                    while pending and len(active) < width:
                        active.append(pending.pop(0)())
                    for g_ in list(active):
                        try:
                            next(g_)
                        except StopIteration:
                            active.remove(g_)

            def load_w(dst, src_ap, q=None):
                kb.dma(q or POOL, dst.t[:], src_ap, writes=[dst.b])

            def dbg_store(name, src_ap, rows, reads):
                if name in dbg:
                    kb.dma(SP, dbg[name][rows], src_ap, reads=reads, is_out=True)

            def phase_1c():
                with ExitStack() as s3:
                    wq = sb(s3, "wq", [128, KC, 512], BF16)
                    load_w(wq, w_in_v[:, :, C_Q:C_Q + 512])
                    wg = sb(s3, "wg", [128, KC, 24], BF16)
                    load_w(wg, w_in_v[:, :, C_G:C_G + 24])
                    sqq = [sb(s3, f"sqq{i}", [128, 512], F32) for i in range(2)]
                    qst = sb(s3, "qst", [128, NT, 32], F32, NT)
                    tmpq = [sb(s3, f"tmpq{i}", [128, 8, 64], F32) for i in range(2)]
                    qaug = [sb(s3, f"qaug{i}", [128, 8, 128], BF16, 4) for i in range(2)]
                    qT = [sb(s3, f"qT{i}", [128, 8, 128], BF16) for i in range(2)]
                    qT2 = [sb(s3, f"qT2{i}", [128, 8, 128], BF16) for i in range(2)]
                    gate = [sb(s3, f"gate{i}", [128, 24], F32) for i in range(2)]
                    scl = [sb(s3, f"scl{i}", [128, 1024], F32) for i in range(2)]
                    NPT = 3
                    PT = [[sb(s3, f"PT{i}_{j}", [128, 1024], BF16, 2) for j in range(NPT)] for i in range(2)]
                    oTs = [sb(s3, f"oTs{i}", [128, 1024], F32) for i in range(2)]
                    num = [sb(s3, f"num{i}", [128, 3, 8, 64], F32) for i in range(2)]
                    den = [sb(s3, f"den{i}", [128, 3, 8], F32) for i in range(2)]
                    rdc = [sb(s3, f"rdc{i}", [128, 8], F32) for i in range(2)]
                    impn = [sb(s3, f"impn{i}", [128, 8, 32], F32) for i in range(2)]
                    imp = [sb(s3, f"imp{i}", [128, 2, 32], F32) for i in range(2)]
                    top8 = [sb(s3, f"top8{i}", [128, 2, 8], F32, 2) for i in range(2)]
                    rd = [sb(s3, f"rd{i}", [128, 3, 8], F32) for i in range(2)]
                    coef = [sb(s3, f"coef{i}", [128, 3, 8], F32) for i in range(2)]
                    oacc = [sb(s3, f"oacc{i}", [128, 8, 64], F32) for i in range(2)]
                    otmp = [sb(s3, f"otmp{i}", [128, 8, 64], F32) for i in range(2)]
                    ytok = [sb(s3, f"ytok{i}", [128, 512], BF16) for i in range(2)]
                    ydbg = sb(s3, "ydbg", [128, 512], F32) if "y_nsa" in dbg else None
                    for i in range(2):
                        kb.op(POOL, lambda i=i: G.memset(qaug[i].t[:], 0.0), writes=qaug[i].bs)
                    ptc = [0, 0]

                    def tile_gen(c):
                        i2 = c % 2
                        base = 4 * i2
                        bZ, bG = base, base + 1
                        bO0, bO1 = base + 2, base + 3
                        Sb = [PS[bZ].b, PS[bG].b]
                        Ob = [PS[bO0].b, PS[bO1].b]
                        tok = slice(c * 128, (c + 1) * 128)

                        def S2():
                            return pf2(base // 2)

                        def O2():
                            return pf2(base // 2 + 1)

                        def X8():
                            return O2().rearrange("p (h c) -> p h c", h=8)

                        def to_token_major(br, ncol):
                            nu, de = num[i2], den[i2]
                            ot = oTs[i2]
                            kb.op(DVE, lambda: V.tensor_scalar(out=ot.t[0:ncol, :], in0=O2()[0:ncol, :], scalar1=1.0, scalar2=None, op0=ALU.mult),
                                  reads=Ob, writes=[ot.b])
                            yield
                            for _ in range(XY2):
                                yield
                            for hh in range(8):
                                kb.op(PE, lambda hh=hh: TE.transpose(out=X8()[:, hh, 0:ncol], in_=ot.t[0:ncol, hh * 128:(hh + 1) * 128],
                                                                     identity=ident_f.t[0:ncol, 0:ncol]),
                                      reads=[ot.b, ident_f.b], writes=[Ob[hh // 4]], inc=(hh % 4 == 3))
                            yield
                            kb.op(ACT, lambda: A.copy(out=nu.t[:, br, :, :], in_=X8()[:, :, 0:64]), reads=Ob, writes=[nu.b])
                            yield
                            kb.op(DVE, lambda: V.tensor_scalar(out=de.t[:, br, :], in0=X8()[:, :, 64], scalar1=1e-30, scalar2=None, op0=ALU.max),
                                  reads=Ob, writes=[de.b])
                            yield

                        def branch(br, KT, VV, kts, qsrc, pre=False):
                            n = len(kts)

                            def scores(kt):
                                for g in range(2):
                                    kb.op(PE, lambda g=g: TE.matmul(S2()[:, g * 512:(g + 1) * 512], lhsT=KT.t[:, g, kt * 128:(kt + 1) * 128],
                                                                    rhs=qsrc.t[:, 4 * g:4 * g + 4, :], start=True, stop=True),
                                          reads=[KT.bs[kt], qsrc.b], writes=[Sb[g]])

                            if not pre:
                                scores(kts[0])
                                yield
                            for j, kt in enumerate(kts):
                                pt = PT[i2][ptc[i2] % NPT]
                                ptc[i2] += 1
                                for g in range(2):
                                    kb.op(ACT, lambda pt=pt, g=g: A.activation(out=pt.t[:, g * 512:(g + 1) * 512], in_=S2()[:, g * 512:(g + 1) * 512], func=AF.Exp),
                                          reads=[Sb[g]], writes=[pt.bs[g]])
                                yield
                                if j + 1 < n:
                                    for _ in range(XY1):
                                        yield
                                    scores(kts[j + 1])
                                    yield
                                if kt == c:
                                    kb.op(DVE, lambda pt=pt: V.tensor_tensor(out=pt.t[:], in0=pt.t[:], in1=dmask8.t[:].rearrange("p a b -> p (a b)"), op=ALU.mult),
                                          reads=pt.bs + [dmask8.b], writes=pt.bs)
                                    yield
                                elif br == 2 and kt == c - 4:
                                    kb.op(DVE, lambda pt=pt: V.tensor_tensor(out=pt.t[:], in0=pt.t[:], in1=tmask8.t[:].rearrange("p a b -> p (a b)"), op=ALU.mult),
                                          reads=pt.bs + [tmask8.b], writes=pt.bs)
                                    yield
                                for g in range(2):
                                    kb.op(PE, lambda kt=kt, pt=pt, j=j, g=g: TE.matmul(O2()[0:65, g * 512:(g + 1) * 512], lhsT=VV.t[:, kt, g, :],
                                                                                       rhs=pt.t[:, g * 512:(g + 1) * 512], start=(j == 0), stop=(j == n - 1)),
                                          reads=[pt.bs[g], VV.bs[kt]], writes=[Ob[g]], inc=(j == n - 1))
                                yield
                            yield from to_token_major(br, 65)

                        for k in range(KC):
                            kb.op(PE, lambda k=k: TE.matmul(pf(bZ), lhsT=hT.t[:, k, tok], rhs=wq.t[:, k, :], start=(k == 0), stop=(k == KC - 1)),
                                  reads=[hT.bs[c], wq.b], writes=[PS[bZ].b], inc=(k == KC - 1))
                        for k in range(KC):
                            kb.op(PE, lambda k=k: TE.matmul(pf(bG)[:, 0:24], lhsT=hT.t[:, k, tok], rhs=wg.t[:, k, :], start=(k == 0), stop=(k == KC - 1)),
                                  reads=[hT.bs[c], wg.b], writes=[PS[bG].b], inc=(k == KC - 1))
                        yield
                        sq = sqq[i2]
                        kb.op(ACT, lambda: A.activation(out=sq.t[:], in_=pf(bZ), func=AF.Square), reads=[PS[bZ].b], writes=[sq.b])
                        gt = gate[i2]
                        kb.op(ACT, lambda: A.activation(out=gt.t[:], in_=pf(bG)[:, 0:24], func=AF.Tanh, scale=0.5), reads=[PS[bG].b], writes=[gt.b])
                        yield
                        kb.op(DVE, lambda: V.tensor_scalar(out=gt.t[:], in0=gt.t[:], scalar1=0.5, scalar2=0.5, op0=ALU.mult, op1=ALU.add),
                              reads=[gt.b], writes=[gt.b])
                        qs = qst.t
                        qsb = [qst.bs[c]]
                        kb.op(DVE, lambda: V.tensor_reduce(out=qs[:, c, 0:8], in_=sq.t[:].rearrange("p (a b) -> p a b", b=64), axis=AX.X, op=ALU.add),
                              reads=[sq.b], writes=qsb)
                        yield
                        kb.op(DVE, lambda: V.tensor_scalar(out=qs[:, c, 8:16], in0=qs[:, c, 0:8], scalar1=1.0 / 64, scalar2=EPS, op0=ALU.mult, op1=ALU.add), reads=qsb, writes=qsb)
                        yield
                        kb.op(POOL, lambda: G.tensor_tensor(out=qs[:, c, 24:32], in0=qs[:, c, 8:16], in1=nhalf.t[:, 0:8], op=ALU.pow),
                              reads=qsb + [nhalf.b], writes=qsb)
                        yield
                        tq = tmpq[i2]
                        qa = qaug[i2]
                        kb.op(DVE, lambda: V.tensor_tensor(out=tq.t[:], in0=pf(bZ).rearrange("p (a b) -> p a b", b=64),
                                                           in1=qs[:, c, 24:32].unsqueeze(2).broadcast_to([128, 8, 64]), op=ALU.mult),
                              reads=[PS[bZ].b] + qsb, writes=[tq.b])
                        yield
                        kb.op(DVE, lambda: V.tensor_tensor(out=qa.t[:, :, 0:64], in0=tq.t[:], in1=gq.t[:].unsqueeze(1).broadcast_to([128, 8, 64]), op=ALU.mult),
                              reads=[tq.b, gq.b], writes=[qa.bs[0]])
                        kb.op(POOL, lambda: G.tensor_copy(out=qa.t[:, :, 96:100], in_=QAL.t[:, c, :, :]), reads=[QAL.b], writes=[qa.bs[1]])
                        yield
                        transposes([qa.t[:, h, :] for h in range(8)], bZ, qa.bs)
                        yield
                        q1 = qT[i2]
                        kb.op(ACT, lambda: A.copy(out=q1.t[:], in_=pb(bZ).rearrange("p (a b) -> p a b", b=128)), reads=[PS[bZ].b], writes=[q1.b])
                        yield
                        nu, de = num[i2], den[i2]
                        rdc_, impn_, imp_, top8_ = rdc[i2], impn[i2], imp[i2], top8[i2]
                        sc_ = scl[i2]
                        pc = PT[i2][ptc[i2] % NPT]
                        ptc[i2] += 1
                        for g in range(2):
                            kb.op(PE, lambda g=g: TE.matmul(S2()[0:127, g * 512:(g + 1) * 512], lhsT=KcT.t[:, g, 0:127], rhs=q1.t[:, 4 * g:4 * g + 4, :], start=True, stop=True),
                                  reads=[KcT.b, q1.b], writes=[Sb[g]])
                        yield
                        kb.op(DVE, lambda: V.scalar_tensor_tensor(out=sc_.t[0:127, :].rearrange("p (a b) -> p a b", b=128),
                                                                  in0=S2()[0:127, :].rearrange("p (a b) -> p a b", b=128), scalar=60.0,
                                                                  in1=cmask.t[0:127, tok].unsqueeze(1).broadcast_to([127, 8, 128]),
                                                                  op0=ALU.min, op1=ALU.add),
                              reads=Sb + [cmask.b], writes=[sc_.b])
                        yield
                        if PREWIN:
                            kt0 = max(0, c - 4)
                            for g in range(2):
                                kb.op(PE, lambda g=g: TE.matmul(S2()[:, g * 512:(g + 1) * 512], lhsT=KT_win.t[:, g, kt0 * 128:(kt0 + 1) * 128],
                                                                rhs=q1.t[:, 4 * g:4 * g + 4, :], start=True, stop=True),
                                      reads=[KT_win.bs[kt0], q1.b], writes=[Sb[g]])
                            yield
                        kb.op(ACT, lambda: A.activation(out=pc.t[0:127, :], in_=sc_.t[0:127, :], func=AF.Exp), reads=[sc_.b], writes=pc.bs)
                        yield
                        for g in range(2):
                            kb.op(PE, lambda g=g: TE.matmul(O2()[0:97, g * 512:(g + 1) * 512], lhsT=Vc.t[0:127, g, :], rhs=pc.t[0:127, g * 512:(g + 1) * 512], start=True, stop=True),
                                  reads=[pc.bs[g], Vc.b], writes=[Ob[g]])
                        yield
                        yield from to_token_major(0, 97)
                        kb.op(DVE, lambda: V.reciprocal(out=rdc_.t[:], in_=de.t[:, 0, :]), reads=[de.b], writes=[rdc_.b])
                        yield
                        kb.op(DVE, lambda: V.tensor_tensor(out=impn_.t[:], in0=X8()[:, :, 65:97],
                                                           in1=rdc_.t[:].unsqueeze(2).broadcast_to([128, 8, 32]), op=ALU.mult),
                              reads=Ob + [rdc_.b], writes=[impn_.b])
                        yield
                        q2 = qT2[i2]

                        def imp_chain():
                            kb.op(DVE, lambda: V.tensor_reduce(out=imp_.t[:], in_=impn_.t[:].rearrange("p (g r) j -> p g j r", g=2), axis=AX.X, op=ALU.add),
                                  reads=[impn_.b], writes=[imp_.b])
                            yield
                            kb.op(DVE, lambda: V.tensor_tensor(out=imp_.t[:], in0=imp_.t[:], in1=addc.t[:, c:c + 1, :].broadcast_to([128, 2, 32]), op=ALU.add),
                                  reads=[imp_.b, addc.b], writes=[imp_.b])
                            yield
                            for g in range(2):
                                kb.op(DVE, lambda g=g: V.max(out=top8_.t[:, g, :], in_=imp_.t[:, g, :]), reads=[imp_.b], writes=[top8_.bs[g]])
                            yield
                            for g in range(2):
                                kb.op(DVE, lambda g=g: V.tensor_scalar(out=qa.t[:, 4 * g:4 * g + 4, 64:96],
                                                                       in0=imp_.t[:, g:g + 1, :].broadcast_to([128, 4, 32]),
                                                                       scalar1=top8_.t[:, g, 7:8], scalar2=NEG, op0=ALU.is_lt, op1=ALU.mult),
                                      reads=[imp_.b, top8_.bs[g]], writes=[qa.bs[2 + g]])
                            yield

                        gw = branch(2, KT_win, V_win, list(range(max(0, c - 4), c + 1)), q1, pre=bool(PREWIN))
                        gi = imp_chain()
                        live = [gw, gi]
                        while live:
                            for g_ in list(live):
                                try:
                                    next(g_)
                                except StopIteration:
                                    live.remove(g_)
                            yield
                        for _ in range(XY3):
                            yield
                        transposes([qa.t[:, h, :] for h in range(8)], bZ, qa.bs)
                        kb.op(ACT, lambda: A.copy(out=q2.t[:], in_=pb(bZ).rearrange("p (a b) -> p a b", b=128)), reads=[PS[bZ].b], writes=[q2.b])
                        yield
                        yield from branch(1, KT_slc, V_slc, list(range(0, c + 1)), q2)
                        rd_, coef_, oacc_, otmp_ = rd[i2], coef[i2], oacc[i2], otmp[i2]
                        kb.op(DVE, lambda: V.reciprocal(out=rd_.t[:], in_=de.t[:]), reads=[de.b], writes=[rd_.b])
                        yield
                        kb.op(DVE, lambda: V.tensor_tensor(out=coef_.t[:], in0=gt.t[:].rearrange("p (h b) -> p b h", b=3), in1=rd_.t[:], op=ALU.mult),
                              reads=[gt.b, rd_.b], writes=[coef_.b])
                        yield
                        kb.op(DVE, lambda: V.tensor_tensor(out=oacc_.t[:], in0=nu.t[:, 0], in1=coef_.t[:, 0, :].unsqueeze(2).broadcast_to([128, 8, 64]), op=ALU.mult),
                              reads=[nu.b, coef_.b], writes=[oacc_.b])
                        kb.op(POOL, lambda: G.tensor_tensor(out=otmp_.t[:], in0=nu.t[:, 1], in1=coef_.t[:, 1, :].unsqueeze(2).broadcast_to([128, 8, 64]), op=ALU.mult),
                              reads=[nu.b, coef_.b], writes=[otmp_.b])
                        yield
                        kb.op(DVE, lambda: V.tensor_tensor(out=oacc_.t[:], in0=oacc_.t[:], in1=otmp_.t[:], op=ALU.add), reads=[oacc_.b, otmp_.b], writes=[oacc_.b])
                        yield
                        kb.op(POOL, lambda: G.tensor_tensor(out=otmp_.t[:], in0=nu.t[:, 2], in1=coef_.t[:, 2, :].unsqueeze(2).broadcast_to([128, 8, 64]), op=ALU.mult),
                              reads=[nu.b, coef_.b], writes=[otmp_.b])
                        yield
                        yt = ytok[i2]
                        kb.op(DVE, lambda: V.tensor_tensor(out=yt.t[:], in0=oacc_.t[:].rearrange("p a b -> p (a b)"), in1=otmp_.t[:].rearrange("p a b -> p (a b)"), op=ALU.add),
                              reads=[oacc_.b, otmp_.b], writes=[yt.b])
                        if ydbg is not None:
                            kb.op(POOL, lambda: G.tensor_tensor(out=ydbg.t[:], in0=oacc_.t[:].rearrange("p a b -> p (a b)"), in1=otmp_.t[:].rearrange("p a b -> p (a b)"), op=ALU.add),
                                  reads=[oacc_.b, otmp_.b], writes=[ydbg.b])
                            dbg_store("y_nsa", ydbg.t[:], tok, [ydbg.b])
                        yield
                        for _ in range(XY3):
                            yield
                        transposes([yt.t[:, k * 128:(k + 1) * 128] for k in range(4)], bZ, [yt.b])
                        yield
                        kb.op(ACT, lambda: A.copy(out=ynsaT.t[:, :, tok], in_=pb(bZ)[:, 0:512].rearrange("p (a b) -> p a b", b=128)),
                              reads=[PS[bZ].b], writes=[ynsaT.bs[c]])
                        yield

                    run_interleaved([(lambda c=c: tile_gen(c)) for c in range(NT)], width=WIDTH)

            def phase_1d():
                with ExitStack() as s4:
                    wr = sb(s4, "wr", [128, KC, 2048], BF16, 4)
                    for j in range(4):
                        kb.dma(POOL, wr.t[:, :, j * 512:(j + 1) * 512], w_in_v[:, :, C_R + j * 512:C_R + (j + 1) * 512], writes=[wr.bs[j]])
                    for j in range(4):
                        kb.dma(POOL, wm.t[:, :, j * 512:(j + 1) * 512], w_in_v[:, :, C_M + j * 512:C_M + (j + 1) * 512], writes=[wm.bs[j]])
                    load_w(wbr, w_branch[0].rearrange("n (k p) d -> p (n k) d", p=128))
                    idT = sb(s4, "idT", [128, 4, 128], F32)
                    qdec = sb(s4, "qdec", [128, 4, 128], F32)
                    kdec = sb(s4, "kdec", [128, 4, 128], F32)
                    gn = sb(s4, "gn", [128, 512], F32)
                    eij = sb(s4, "eij", [128, 128], F32)
                    rowq = sb(s4, "rowq", [128, 128], F32)
                    rowk = sb(s4, "rowk", [128, 128], F32)
                    kb.dma(SP, gn.t[:], ret_gn_g.rearrange("o a b -> o (a b)").broadcast_to([128, 512]), writes=[gn.b])
                    kb.op(POOL, lambda: G.iota(eij.t[:], pattern=[[1, 128]], base=0, channel_multiplier=-1, allow_small_or_imprecise_dtypes=True), writes=[eij.b])
                    kb.op(POOL, lambda: G.iota(rowq.t[:], pattern=[[1, 128]], base=1, channel_multiplier=0, allow_small_or_imprecise_dtypes=True), writes=[rowq.b])
                    kb.op(POOL, lambda: G.iota(rowk.t[:], pattern=[[-1, 128]], base=127, channel_multiplier=0, allow_small_or_imprecise_dtypes=True), writes=[rowk.b])
                    lgs = [float(np.log(1.0 - 2.0 ** (-5.0 - h))) for h in range(4)]
                    cds = [float(np.exp(128.0 * np.float32(lg))) for lg in lgs]
                    for h in range(4):
                        kb.op(ACT, lambda h=h: A.activation(out=idT.t[:, h, :], in_=eij.t[:], func=AF.Exp, scale=lgs[h]), reads=[eij.b], writes=[idT.b])
                        kb.op(POOL, lambda h=h: G.affine_select(out=idT.t[:, h, :], in_=idT.t[:, h, :], pattern=[[1, 128]], compare_op=ALU.is_ge, fill=0.0,
                                                                base=0, channel_multiplier=-1), reads=[idT.b], writes=[idT.b])
                        kb.op(ACT, lambda h=h: A.activation(out=qdec.t[:, h, :], in_=rowq.t[:], func=AF.Exp, scale=lgs[h]), reads=[rowq.b], writes=[qdec.b])
                        kb.op(ACT, lambda h=h: A.activation(out=kdec.t[:, h, :], in_=rowk.t[:], func=AF.Exp, scale=lgs[h]), reads=[rowk.b], writes=[kdec.b])
                    qTr = sb(s4, "qTr", [128, 4, 512], BF16)
                    qdT = sb(s4, "qdT", [128, 4, 512], BF16)
                    kTr = sb(s4, "kTr", [128, 4, 512], BF16)
                    kdT = sb(s4, "kdT", [128, 4, 512], BF16)
                    v_sb = [sb(s4, f"v_sb{i}", [128, 4, 128], BF16) for i in range(2)]
                    sgl = [sb(s4, f"sgl{i}", [128, 512], F32) for i in range(2)]
                    kd = [sb(s4, f"kd{i}", [128, 4, 128], BF16) for i in range(2)]
                    attb = [sb(s4, f"attb{i}", [128, 4, 128], BF16) for i in range(2)]
                    state_f = sb(s4, "state_f", [128, 4, 128], F32, 4)
                    state_b = sb(s4, "state_b", [128, 4, 128], BF16)
                    yr = [sb(s4, f"yr{i}", [128, 512], BF16) for i in range(2)]
                    yrdbg = sb(s4, "yrdbg", [128, 512], F32) if "y_ret" in dbg else None
                    kb.op(POOL, lambda: G.memset(state_f.t[:], 0.0), writes=state_f.bs)
                    kb.op(POOL, lambda: G.memset(state_b.t[:], 0.0), writes=[state_b.b])
                    KS = float(128.0 ** -0.5)
                    pcnt = [0]
                    state_done = [False] * (NT + 1)
                    stt_done = [False] * (NT + 1)
                    bst = [sb(s4, f"bst{i}", [128, 4, 6], F32, 4) for i in range(2)]
                    mv = [sb(s4, f"mv{i}", [128, 4, 2], F32, 4) for i in range(2)]
                    rs4 = [sb(s4, f"rs4{i}", [128, 12], F32) for i in range(2)]
                    on = [sb(s4, f"on{i}", [128, 512], F32, 4) for i in range(2)]

                    def p1d_gen(c):
                        i2 = c % 2
                        cl = c % 4
                        bA, bB, bT = 2 + 3 * i2, 3 + 3 * i2, 4 + 3 * i2
                        tok = slice(c * 128, (c + 1) * 128)
                        cs = slice(cl * 128, (cl + 1) * 128)
                        bst_, mv_, rs4_, on_ = bst[i2], mv[i2], rs4[i2], on[i2]
                        for k in range(KC):
                            kb.op(PE, lambda k=k: TE.matmul(pf(bA), lhsT=hT.t[:, k, tok], rhs=wr.t[:, k, 1024:1536], start=(k == 0), stop=(k == KC - 1)),
                                  reads=[hT.bs[c], wr.bs[2]], writes=[PS[bA].b], inc=(k == KC - 1))
                        yield
                        for k in range(KC):
                            kb.op(PE, lambda k=k: TE.matmul(pf(bB), lhsT=hT.t[:, k, tok], rhs=wr.t[:, k, 1536:2048], start=(k == 0), stop=(k == KC - 1)),
                                  reads=[hT.bs[c], wr.bs[3]], writes=[PS[bB].b], inc=(k == KC - 1))
                        yield
                        vs, sg_, kd_, ab = v_sb[i2], sgl[i2], kd[i2], attb[i2]
                        kb.op(ACT, lambda: A.copy(out=vs.t[:].rearrange("p a b -> p (a b)"), in_=pf(bA)), reads=[PS[bA].b], writes=[vs.b])
                        yield
                        kb.op(ACT, lambda: A.activation(out=sg_.t[:], in_=pf(bB), func=AF.Silu), reads=[PS[bB].b], writes=[sg_.b])
                        yield
                        for _ in range(XY3):
                            yield
                        transposes([kdT.t[:, h, cs] for h in range(4)], bT, [kdT.b])
                        yield
                        kb.op(ACT, lambda: A.copy(out=kd_.t[:].rearrange("p a b -> p (a b)"), in_=pb(bT)[:, 0:512]), reads=[PS[bT].b], writes=[kd_.b])
                        yield
                        for h in range(4):
                            kb.op(PE, lambda h=h: TE.matmul(pf(bA)[:, h * 128:(h + 1) * 128], lhsT=kTr.t[:, h, cs], rhs=qTr.t[:, h, cs], start=True, stop=True),
                                  reads=[kTr.b, qTr.b], writes=[PS[bA].b], inc=(h == 3))
                        yield
                        kb.op(DVE, lambda: V.tensor_tensor(out=ab.t[:], in0=pf(bA).rearrange("p (a b) -> p a b", b=128), in1=idT.t[:], op=ALU.mult),
                              reads=[PS[bA].b, idT.b], writes=[ab.b])
                        yield
                        if c < NT - 1:
                            for h in range(4):
                                kb.op(PE, lambda h=h: TE.matmul(pf(bT)[:, h * 128:(h + 1) * 128], lhsT=kd_.t[:, h, :], rhs=vs.t[:, h, :], start=True, stop=True),
                                      reads=[kd_.b, vs.b], writes=[PS[bT].b], inc=(h == 3))
                            yield
                            while c > 0 and not state_done[c - 1]:
                                yield
                            for h in range(4):
                                kb.op(DVE, lambda h=h: V.scalar_tensor_tensor(out=state_f.t[:, h, :], in0=state_f.t[:, h, :], scalar=cds[h],
                                                                              in1=pf(bT)[:, h * 128:(h + 1) * 128], op0=ALU.mult, op1=ALU.add),
                                      reads=[state_f.bs[h], PS[bT].b], writes=[state_f.bs[h]])
                        stt_done[c] = True
                        yield
                        while c > 0 and not state_done[c - 1]:
                            yield
                        for h in range(4):
                            kb.op(PE, lambda h=h: TE.matmul(pf(bB)[:, h * 128:(h + 1) * 128], lhsT=ab.t[:, h, :], rhs=vs.t[:, h, :], start=True, stop=(c == 0)),
                                  reads=[ab.b, vs.b], writes=[PS[bB].b], inc=(c == 0 and h == 3))
                            if c > 0:
                                kb.op(PE, lambda h=h: TE.matmul(pf(bB)[:, h * 128:(h + 1) * 128], lhsT=qdT.t[:, h, cs], rhs=state_b.t[:, h, :], start=False, stop=True),
                                      reads=[qdT.b, state_b.b], writes=[PS[bB].b], inc=(h == 3))
                        yield
                        if c < NT - 1:
                            kb.op(POOL, lambda: G.tensor_copy(out=state_b.t[:], in_=state_f.t[:]), reads=state_f.bs, writes=[state_b.b])
                        state_done[c] = True
                        yield
                        for h in range(4):
                            kb.op(DVE, lambda h=h: V.bn_stats(out=bst_.t[:, h, :], in_=pf(bB)[:, h * 128:(h + 1) * 128]), reads=[PS[bB].b], writes=[bst_.bs[h]])
                        yield
                        for h in range(4):
                            kb.op(DVE, lambda h=h: V.bn_aggr(out=mv_.t[:, h, :], in_=bst_.t[:, h, :]), reads=[bst_.bs[h]], writes=[mv_.bs[h]])
                        yield
                        kb.op(DVE, lambda: V.tensor_scalar(out=rs4_.t[:, 0:4], in0=mv_.t[:, :, 1], scalar1=EPS, scalar2=None, op0=ALU.add), reads=mv_.bs, writes=[rs4_.b])
                        yield
                        kb.op(POOL, lambda: G.tensor_tensor(out=rs4_.t[:, 8:12], in0=rs4_.t[:, 0:4], in1=nhalf.t[:, 0:4], op=ALU.pow),
                              reads=[rs4_.b, nhalf.b], writes=[rs4_.b])
                        yield
                        for h in range(4):
                            kb.op(DVE, lambda h=h: V.tensor_scalar(out=on_.t[:, h * 128:(h + 1) * 128], in0=pf(bB)[:, h * 128:(h + 1) * 128],
                                                                   scalar1=mv_.t[:, h, 0:1], scalar2=rs4_.t[:, 8 + h:9 + h], op0=ALU.subtract, op1=ALU.mult),
                                  reads=[PS[bB].b, mv_.bs[h], rs4_.b], writes=[on_.bs[h]])
                        yield
                        kb.op(POOL, lambda: G.tensor_tensor(out=on_.t[:], in0=on_.t[:], in1=gn.t[:], op=ALU.mult), reads=on_.bs + [gn.b], writes=on_.bs)
                        yield
                        y_ = yr[i2]
                        kb.op(DVE, lambda: V.tensor_tensor(out=y_.t[:], in0=on_.t[:], in1=sg_.t[:], op=ALU.mult), reads=on_.bs + [sg_.b], writes=[y_.b])
                        if yrdbg is not None:
                            kb.op(DVE, lambda: V.tensor_tensor(out=yrdbg.t[:], in0=on_.t[:], in1=sg_.t[:], op=ALU.mult), reads=on_.bs + [sg_.b], writes=[yrdbg.b])
                            dbg_store("y_ret", yrdbg.t[:], tok, [yrdbg.b])
                        yield
                        for _ in range(XY3):
                            yield
                        transposes([y_.t[:, k * 128:(k + 1) * 128] for k in range(4)], bT, [y_.b])
                        yield
                        kb.op(ACT, lambda: A.copy(out=yretT.t[:, :, tok], in_=pb(bT)[:, 0:512].rearrange("p (a b) -> p a b", b=128)),
                              reads=[PS[bT].b], writes=[yretT.bs[c]])
                        yield

                    for tg in range(4):
                        tks = slice(tg * 512, (tg + 1) * 512)
                        hbs = [hT.bs[4 * tg + i] for i in range(4)]
                        for qk in range(2):
                            for h in range(4):
                                bk = pcnt[0] % 2
                                pcnt[0] += 1
                                for k in range(KC):
                                    kb.op(PE, lambda k=k, qk=qk, h=h, bk=bk: TE.matmul(pf(bk), lhsT=wr.t[:, k, qk * 512 + h * 128:qk * 512 + (h + 1) * 128],
                                                                                      rhs=hT.t[:, k, tks], start=(k == 0), stop=(k == KC - 1)),
                                          reads=hbs + [wr.bs[qk]], writes=[PS[bk].b], inc=(k == KC - 1))
                                pv4 = pf(bk).rearrange("p (a b) -> p a b", b=128)
                                if qk == 0:
                                    kb.op(ACT, lambda h=h, bk=bk: A.copy(out=qTr.t[:, h, :], in_=pf(bk)), reads=[PS[bk].b], writes=[qTr.b])
                                    kb.op(DVE, lambda h=h, pv4=pv4: V.tensor_tensor(out=qdT.t[:, h, :].rearrange("p (a b) -> p a b", b=128), in0=pv4,
                                                                                    in1=qdec.t[:, h:h + 1, :].broadcast_to([128, 4, 128]), op=ALU.mult),
                                          reads=[PS[bk].b, qdec.b], writes=[qdT.b])
                                else:
                                    kb.op(ACT, lambda h=h, bk=bk: A.mul(out=kTr.t[:, h, :], in_=pf(bk), mul=KS), reads=[PS[bk].b], writes=[kTr.b])
                                    kb.op(DVE, lambda h=h, pv4=pv4: V.scalar_tensor_tensor(out=kdT.t[:, h, :].rearrange("p (a b) -> p a b", b=128), in0=pv4, scalar=KS,
                                                                                           in1=kdec.t[:, h:h + 1, :].broadcast_to([128, 4, 128]),
                                                                                           op0=ALU.mult, op1=ALU.mult),
                                          reads=[PS[bk].b, kdec.b], writes=[kdT.b])
                        run_interleaved([(lambda c=c: p1d_gen(c)) for c in range(4 * tg, 4 * tg + 4)], width=2)

            def phase_1e():
                with ExitStack() as s5:
                    xt = [sb(s5, f"xte{i}", [128, D], F32) for i in range(2)]
                    g2bc = sb(s5, "g2bc", [128, D], F32)
                    kb.dma(SP, g2bc.t[:], norm2_g[0:1, :].broadcast_to([128, D]), writes=[g2bc.b])
                    wo = sb(s5, "wo", [128, KC, D], BF16)
                    load_w(wo, w_out[0].rearrange("(k p) d -> p k d", p=128))
                    gates = [sb(s5, f"gates{i}", [128, D], F32, 2) for i in range(2)]
                    tmix = [sb(s5, f"tmix{i}", [128, D], F32, 2) for i in range(2)]
                    tmix2 = [sb(s5, f"tmix2{i}", [128, D], F32, 2) for i in range(2)]
                    mixed = [sb(s5, f"mixed{i}", [128, D], BF16) for i in range(2)]
                    mixT = [sb(s5, f"mixT{i}", [128, KC, 128], BF16) for i in range(2)]
                    x1t = [sb(s5, f"x1t{i}", [128, D], F32, 2) for i in range(2)]

                    def p1e_gen(c):
                        i2 = c % 2
                        bs_ = [4 * i2 + i for i in range(4)]
                        tok = slice(c * 128, (c + 1) * 128)
                        xx = xt[i2]
                        kb.dma(SP, xx.t[:], x[tok, :], writes=[xx.b])
                        gt_, tm = gates[i2], (tmix[i2], tmix2[i2])
                        for n, yT in ((0, ynsaT), (1, yretT)):
                            for half in range(2):
                                j = 2 * n + half
                                for k in range(KC):
                                    kb.op(PE, lambda j=j, k=k, half=half: TE.matmul(pf(bs_[half]), lhsT=hT.t[:, k, tok], rhs=wm.t[:, k, j * 512:(j + 1) * 512],
                                                                                    start=(k == 0), stop=(k == KC - 1)),
                                          reads=[hT.bs[c], wm.bs[j]], writes=[PS[bs_[half]].b], inc=(k == KC - 1))
                                yield
                            for half in range(2):
                                bk = bs_[2 + half]
                                for k in range(4):
                                    kb.op(PE, lambda n=n, half=half, k=k, bk=bk, yT=yT: TE.matmul(pf(bk), lhsT=yT.t[:, k, tok], rhs=wbr.t[:, n * 4 + k, half * 512:(half + 1) * 512],
                                                                                                  start=(k == 0), stop=(k == 3)),
                                          reads=[yT.bs[c], wbr.b], writes=[PS[bk].b], inc=(k == 3))
                                yield
                            for half in range(2):
                                kb.op(ACT, lambda half=half: A.activation(out=gt_.t[:, half * 512:(half + 1) * 512], in_=pf(bs_[half]), func=AF.Sigmoid),
                                      reads=[PS[bs_[half]].b], writes=[gt_.bs[half]])
                                yield
                            for half in range(2):
                                hs = slice(half * 512, (half + 1) * 512)
                                kb.op(DVE, lambda half=half, hs=hs, n=n: V.tensor_tensor(out=tm[n].t[:, hs], in0=gt_.t[:, hs], in1=pf(bs_[2 + half]), op=ALU.mult),
                                      reads=[gt_.bs[half], PS[bs_[2 + half]].b], writes=[tm[n].bs[half]])
                                yield
                        mx = mixed[i2]
                        kb.op(POOL, lambda: G.tensor_tensor(out=mx.t[:], in0=tm[0].t[:], in1=tm[1].t[:], op=ALU.add), reads=tm[0].bs + tm[1].bs, writes=[mx.b])
                        yield
                        for _ in range(XY3):
                            yield
                        transposes([mx.t[:, k * 128:(k + 1) * 128] for k in range(KC)], bs_[0], [mx.b])
                        yield
                        mt = mixT[i2]
                        kb.op(ACT, lambda: A.copy(out=mt.t[:], in_=pb(bs_[0]).rearrange("p (a b) -> p a b", b=128)), reads=[PS[bs_[0]].b], writes=[mt.b])
                        yield
                        for half in range(2):
                            for k in range(KC):
                                kb.op(PE, lambda half=half, k=k: TE.matmul(pf(bs_[2 + half]), lhsT=mt.t[:, k, :], rhs=wo.t[:, k, half * 512:(half + 1) * 512],
                                                                           start=(k == 0), stop=(k == KC - 1)),
                                      reads=[mt.b, wo.b], writes=[PS[bs_[2 + half]].b], inc=(k == KC - 1))
                            yield
                        x1 = x1t[i2]
                        for half in range(2):
                            hs = slice(half * 512, (half + 1) * 512)
                            kb.op(DVE, lambda half=half, hs=hs: V.tensor_tensor(out=x1.t[:, hs], in0=xx.t[:, hs], in1=pf(bs_[2 + half]), op=ALU.add),
                                  reads=[xx.b, PS[bs_[2 + half]].b], writes=[x1.bs[half]])
                            yield
                        kb.dma(SP, xmid[tok, :], x1.t[:], reads=x1.bs)
                        dbg_store("x1", x1.t[:], tok, x1.bs)
                        yield from norm_gen(x1, c, g2bc, i2, bs_[1])

                    run_interleaved([(lambda c=c: p1e_gen(c)) for c in range(NT)], width=2)

            def phase_2():
                with ExitStack() as s6:
                    xt = [sb(s6, f"xtf{i}", [128, D], F32) for i in range(2)]
                    wd = sb(s6, "wd", [128, NFB, D], BF16, 2)
                    wd_v = ffn_w_down[0].rearrange("(fb p) d -> p fb d", p=128)
                    wgs = [sb(s6, f"wgs{i}", [128, KC, 256], BF16) for i in range(2)]
                    wus = [sb(s6, f"wus{i}", [128, KC, 256], BF16) for i in range(2)]
                    act = sb(s6, "act", [128, NFB, 1024], BF16, NFB)
                    sgs = [sb(s6, f"sgs{i}", [128, 512], F32) for i in range(2)]
                    outt = [sb(s6, f"outt{i}", [128, D], F32, 2) for i in range(2)]
                    wg_v = ffn_w_gate[0].rearrange("(k p) f -> p k f", p=128)
                    wu_v = ffn_w_up[0].rearrange("(k p) f -> p k f", p=128)
                    cn = [0, 0]
                    for hf in range(2):
                        for fg in range(11):
                            cols = slice(fg * 256, (fg + 1) * 256)
                            wg_, wu_ = wgs[fg % 2], wus[fg % 2]
                            kb.dma(POOL, wg_.t[:], wg_v[:, :, cols], writes=[wg_.b])
                            kb.dma(POOL, wu_.t[:], wu_v[:, :, cols], writes=[wu_.b])
                            if hf == 0 and fg == 1:
                                kb.dma(POOL, wd.t[:, 0:11, :], wd_v[:, 0:11, :], writes=[wd.bs[0]])
                                kb.dma(POOL, wd.t[:, 11:22, :], wd_v[:, 11:22, :], writes=[wd.bs[1]])
                            for fl in range(2):
                                fb = fg * 2 + fl
                                for t2 in range(2):
                                    tokc = slice(hf * 1024 + t2 * 512, hf * 1024 + (t2 + 1) * 512)
                                    hbs = [hT.bs[hf * 8 + t2 * 4 + i] for i in range(4)]
                                    gb, ub = (0, 1) if cn[0] % 2 == 0 else (2, 3)
                                    cn[0] += 1
                                    for k in range(KC):
                                        kb.op(PE, lambda k=k, fl=fl, gb=gb, wg_=wg_, tokc=tokc: TE.matmul(pf(gb), lhsT=wg_.t[:, k, fl * 128:(fl + 1) * 128], rhs=hT.t[:, k, tokc],
                                                                                                          start=(k == 0), stop=(k == KC - 1)),
                                              reads=hbs + [wg_.b], writes=[PS[gb].b], inc=(k == KC - 1))
                                    for k in range(KC):
                                        kb.op(PE, lambda k=k, fl=fl, ub=ub, wu_=wu_, tokc=tokc: TE.matmul(pf(ub), lhsT=wu_.t[:, k, fl * 128:(fl + 1) * 128], rhs=hT.t[:, k, tokc],
                                                                                                          start=(k == 0), stop=(k == KC - 1)),
                                              reads=hbs + [wu_.b], writes=[PS[ub].b], inc=(k == KC - 1))
                                    sg_ = sgs[cn[0] % 2]
                                    kb.op(ACT, lambda sg_=sg_, gb=gb: A.activation(out=sg_.t[:], in_=pf(gb), func=AF.Silu), reads=[PS[gb].b], writes=[sg_.b])
                                    kb.op(DVE, lambda sg_=sg_, ub=ub, fb=fb, t2=t2: V.tensor_tensor(out=act.t[:, fb, t2 * 512:(t2 + 1) * 512], in0=sg_.t[:], in1=pf(ub), op=ALU.mult),
                                          reads=[sg_.b, PS[ub].b], writes=[act.bs[fb]])
                        for tl in range(8):
                            c = hf * 8 + tl
                            tok = slice(c * 128, (c + 1) * 128)
                            xx = xt[c % 2]
                            kb.dma(SP, xx.t[:], xmid[tok, :], writes=[xx.b])
                            ob = (4, 5) if cn[1] % 2 == 0 else (6, 7)
                            cn[1] += 1
                            for half in range(2):
                                for fb in range(NFB):
                                    kb.op(PE, lambda half=half, fb=fb, ob=ob, tl=tl: TE.matmul(pf(ob[half]), lhsT=act.t[:, fb, tl * 128:(tl + 1) * 128],
                                                                                               rhs=wd.t[:, fb, half * 512:(half + 1) * 512],
                                                                                               start=(fb == 0), stop=(fb == NFB - 1)),
                                          reads=[act.bs[fb], wd.bs[0 if fb < 11 else 1]], writes=[PS[ob[half]].b], inc=(fb == NFB - 1))
                            ot = outt[c % 2]
                            for half in range(2):
                                hs = slice(half * 512, (half + 1) * 512)
                                kb.op(DVE, lambda half=half, hs=hs, ob=ob, ot=ot, xx=xx: V.tensor_tensor(out=ot.t[:, hs], in0=xx.t[:, hs], in1=pf(ob[half]), op=ALU.add),
                                      reads=[xx.b, PS[ob[half]].b], writes=[ot.bs[half]])
                            kb.dma(SP, out[tok, :], ot.t[:], reads=ot.bs, is_out=True)

            with ExitStack() as sB:
                ynsaT = sb(sB, "ynsaT", [128, 4, S], BF16, NT)
                yretT = sb(sB, "yretT", [128, 4, S], BF16, NT)

                with ExitStack() as sA:
                    gq = sb(sA, "gq", [128, 64], F32)
                    gk = sb(sA, "gk", [128, 3, 64], F32)
                    QAL = sb(sA, "QAL", [128, NT, 8, 4], BF16)
                    KAL = sb(sA, "KAL", [128, NT, 4], BF16)
                    KCAL = sb(sA, "KCAL", [128, 4], BF16)
                    OH = sb(sA, "OH", [128, NT, 32], BF16)
                    dmask8 = sb(sA, "dmask8", [128, 8, 128], BF16)
                    tmask8 = sb(sA, "tmask8", [128, 8, 128], BF16)
                    cmask = sb(sA, "cmask", [128, S], BF16)
                    addc = sb(sA, "addc", [128, NT, 32], F32)
                    ov = sb(sA, "ov", [128, 32], BF16)
                    KT_slc = sb(sA, "KT_slc", [128, 2, S], BF16, NT)
                    KT_win = sb(sA, "KT_win", [128, 2, S], BF16, NT)
                    V_slc = sb(sA, "V_slc", [128, NT, 2, 65], BF16, NT)
                    V_win = sb(sA, "V_win", [128, NT, 2, 65], BF16, NT)
                    KcT = sb(sA, "KcT", [128, 2, 128], BF16)
                    Vc = sb(sA, "Vc", [128, 2, 97], BF16)

                    with ExitStack() as s0:
                        SL = sb(s0, "SL", [128, 8], F32)
                        th128 = sb(s0, "th128", [128, NT], F32)
                        pidx = sb(s0, "pidx", [128, 1], F32)
                        QALf = sb(s0, "QALf", [128, NT, 8, 4], F32)
                        KALf = sb(s0, "KALf", [128, NT, 4], F32)
                        KCALf = sb(s0, "KCALf", [128, 4], F32)
                        rel = sb(s0, "rel", [128, NT, 32], F32)
                        f0 = sb(s0, "f0", [128, NT, 32], F32)
                        f1 = sb(s0, "f1", [128, NT, 32], F32)
                        t1 = sb(s0, "t1", [128, NT, 32], F32)
                        hp = sb(s0, "hp", [128, 1], F32)
                        ovf = sb(s0, "ovf", [128, 32], F32)
                        ova = sb(s0, "ova", [128, 32], F32)
                        ones_b = sb(s0, "ones_b", [128, 512], BF16)
                        ones_b2 = sb(s0, "ones_b2", [128, 1024], BF16)
                        zeros_b = sb(s0, "zeros_b", [128, 512], BF16)

                        kb.dma(SP, gq.t[:], nsa_q_norm[0:1, :].broadcast_to([128, 64]), writes=[gq.b])
                        kb.dma(SP, gk.t[:].rearrange("p a b -> p (a b)"),
                               nsa_k_norm.rearrange("o a b -> o (a b)").broadcast_to([128, 192]), writes=[gk.b])
                        kb.op(DVE, lambda: V.tensor_scalar(out=gq.t[:], in0=gq.t[:], scalar1=0.125, scalar2=None, op0=ALU.mult),
                              reads=[gq.b], writes=[gq.b])
                        for h in range(8):
                            kb.op(POOL, lambda h=h: G.memset(SL.t[:, h:h + 1], 2.0 ** (-(h + 1))), writes=[SL.b])
                        kb.op(POOL, lambda: G.iota(th128.t[:], pattern=[[128, NT]], base=0, channel_multiplier=0,
                                                   allow_small_or_imprecise_dtypes=True), writes=[th128.b])
                        kb.op(POOL, lambda: G.iota(pidx.t[:], pattern=[[0, 1]], base=0, channel_multiplier=1,
                                                   allow_small_or_imprecise_dtypes=True), writes=[pidx.b])
                        SLb = SL.t[:].unsqueeze(1).broadcast_to([128, NT, 8])
                        THb = th128.t[:].unsqueeze(2).broadcast_to([128, NT, 8])
                        kb.op(DVE, lambda: V.scalar_tensor_tensor(out=QALf.t[:, :, :, 0], in0=THb, scalar=-1.0, in1=SLb,
                                                                  op0=ALU.mult, op1=ALU.mult),
                              reads=[SL.b, th128.b], writes=[QALf.b])
                        kb.op(DVE, lambda: V.tensor_scalar(out=QALf.t[:, :, :, 1], in0=SLb, scalar1=pidx.t[:, 0:1], scalar2=-1.0,
                                                           op0=ALU.mult, op1=ALU.mult),
                              reads=[SL.b, pidx.b], writes=[QALf.b])
                        kb.op(DVE, lambda: V.tensor_copy(out=QALf.t[:, :, :, 2], in_=SLb), reads=[SL.b], writes=[QALf.b])
                        kb.op(DVE, lambda: V.tensor_copy(out=QALf.t[:, :, :, 3], in_=SLb), reads=[SL.b], writes=[QALf.b])
                        kb.op(DVE, lambda: V.tensor_copy(out=QAL.t[:], in_=QALf.t[:]), reads=[QALf.b], writes=[QAL.b])
                        kb.op(POOL, lambda: G.memset(KALf.t[:, :, 0:2], 1.0), writes=[KALf.b])
                        kb.op(DVE, lambda: V.tensor_copy(out=KALf.t[:, :, 2], in_=th128.t[:]), reads=[th128.b], writes=[KALf.b])
                        kb.op(DVE, lambda: V.tensor_copy(out=KALf.t[:, :, 3], in_=pidx.t[:, 0:1].broadcast_to([128, NT])),
                              reads=[pidx.b], writes=[KALf.b])
                        kb.op(DVE, lambda: V.tensor_copy(out=KAL.t[:], in_=KALf.t[:]), reads=[KALf.b], writes=[KAL.b])
                        kb.op(POOL, lambda: G.memset(KCALf.t[:, 0:2], 1.0), writes=[KCALf.b])
                        kb.op(POOL, lambda: G.memset(KCALf.t[:, 3:4], 31.0), reads=[], writes=[KCALf.b])
                        kb.op(DVE, lambda: V.tensor_scalar(out=KCALf.t[:, 2:3], in0=pidx.t[:, 0:1], scalar1=16.0, scalar2=None,
                                                           op0=ALU.mult), reads=[pidx.b], writes=[KCALf.b])
                        kb.op(DVE, lambda: V.tensor_copy(out=KCAL.t[:], in_=KCALf.t[:]), reads=[KCALf.b], writes=[KCAL.b])
                        kb.op(POOL, lambda: G.memset(OH.t[:], 0.0), writes=[OH.b])
                        for kt in range(NT):
                            kb.op(POOL, lambda kt=kt: G.memset(OH.t[0:64, kt, 2 * kt:2 * kt + 1], 1.0), writes=[OH.b])
                            kb.op(POOL, lambda kt=kt: G.memset(OH.t[64:128, kt, 2 * kt + 1:2 * kt + 2], 1.0), writes=[OH.b])
                        kb.op(POOL, lambda: G.memset(ones_b.t[:], 1.0), writes=[ones_b.b])
                        kb.op(POOL, lambda: G.memset(zeros_b.t[:], 0.0), writes=[zeros_b.b])
                        ob8 = ones_b2.t[:].rearrange("p (a b) -> p a b", b=128)
                        kb.op(POOL, lambda: G.memset(ones_b2.t[:], 1.0), writes=[ones_b2.b])
                        kb.op(POOL, lambda: G.affine_select(out=dmask8.t[:], in_=ob8, pattern=[[0, 8], [1, 128]],
                                                            compare_op=ALU.is_ge, fill=0.0, base=0, channel_multiplier=-1),
                              reads=[ones_b2.b], writes=[dmask8.b])
                        kb.op(POOL, lambda: G.affine_select(out=tmask8.t[:], in_=ob8, pattern=[[0, 8], [-1, 128]],
                                                            compare_op=ALU.is_gt, fill=0.0, base=0, channel_multiplier=1),
                              reads=[ones_b2.b], writes=[tmask8.b])
                        for i in range(4):
                            kb.op(POOL, lambda i=i: G.affine_select(out=cmask.t[:, i * 512:(i + 1) * 512], in_=zeros_b.t[:],
                                                                    pattern=[[1, 512]], compare_op=ALU.is_ge, fill=NEG,
                                                                    base=-31 + 512 * i, channel_multiplier=-16),
                                  reads=[zeros_b.b], writes=[cmask.b])
                        kb.op(POOL, lambda: G.iota(rel.t[:], pattern=[[-2, NT], [1, 32]], base=0, channel_multiplier=0,
                                                   allow_small_or_imprecise_dtypes=True), writes=[rel.b])
                        kb.op(DVE, lambda: V.tensor_scalar(out=hp.t[:], in0=pidx.t[:], scalar1=64.0, scalar2=None, op0=ALU.is_ge),
                              reads=[pidx.b], writes=[hp.b])
                        kb.op(DVE, lambda: V.tensor_scalar(out=rel.t[:], in0=rel.t[:], scalar1=hp.t[:, 0:1], scalar2=None,
                                                           op0=ALU.subtract), reads=[rel.b, hp.b], writes=[rel.b])
                        kb.op(DVE, lambda: V.tensor_scalar(out=t1.t[:], in0=rel.t[:], scalar1=0.0, scalar2=-1e9,
                                                           op0=ALU.is_gt, op1=ALU.mult), reads=[rel.b], writes=[t1.b])
                        kb.op(DVE, lambda: V.tensor_scalar(out=f0.t[:], in0=rel.t[:], scalar1=0.0, scalar2=None, op0=ALU.is_equal),
                              reads=[rel.b], writes=[f0.b])
                        kb.op(DVE, lambda: V.tensor_scalar(out=f1.t[:], in0=rel.t[:], scalar1=-1.0, scalar2=None, op0=ALU.is_equal),
                              reads=[rel.b], writes=[f1.b])
                        kb.op(DVE, lambda: V.tensor_tensor(out=f0.t[:], in0=f0.t[:], in1=f1.t[:], op=ALU.max),
                              reads=[f0.b, f1.b], writes=[f0.b])
                        kb.op(DVE, lambda: V.memset(f0.t[:, :, 0:1], 1.0), reads=[], writes=[f0.b])
                        kb.op(DVE, lambda: V.scalar_tensor_tensor(out=addc.t[:], in0=f0.t[:], scalar=1e4, in1=t1.t[:],
                                                                  op0=ALU.mult, op1=ALU.add), reads=[f0.b, t1.b], writes=[addc.b])
                        kb.op(POOL, lambda: G.iota(ovf.t[:], pattern=[[-64, 32]], base=0, channel_multiplier=16,
                                                   allow_small_or_imprecise_dtypes=True), writes=[ovf.b])
                        kb.op(DVE, lambda: V.tensor_scalar(out=ova.t[:], in0=ovf.t[:], scalar1=63.0, scalar2=None, op0=ALU.is_le),
                              reads=[ovf.b], writes=[ova.b])
                        kb.op(DVE, lambda: V.tensor_scalar(out=ovf.t[:], in0=ovf.t[:], scalar1=-31.0, scalar2=None, op0=ALU.is_ge),
                              reads=[ovf.b], writes=[ovf.b])
                        kb.op(DVE, lambda: V.tensor_tensor(out=ov.t[:], in0=ova.t[:], in1=ovf.t[:], op=ALU.mult),
                              reads=[ova.b, ovf.b], writes=[ov.b])
                        kb.op(POOL, lambda: G.memset(V_slc.t[:, :, :, 64:65], 1.0), writes=V_slc.bs)
                        kb.op(POOL, lambda: G.memset(V_win.t[:, :, :, 64:65], 1.0), writes=V_win.bs)
                        kb.barrier()
                        chk(1)

                    with ExitStack() as s2:
                        cmpT = sb(s2, "cmpT", [128, 2, S], BF16, NT)
                        w1sb = sb(s2, "w1sb", [128, 2, 32, 128], BF16)
                        w2sb = sb(s2, "w2sb", [128, 2, 64], BF16)
                        pe_sb = sb(s2, "pe_sb", [32, 2, 64], F32)
                        with ExitStack() as s2a:
                            xt = [sb(s2a, f"xta{i}", [128, D], F32) for i in range(2)]
                            g1bc = sb(s2a, "g1bc", [128, D], F32)
                            kb.dma(SP, g1bc.t[:], norm1_g[0:1, :].broadcast_to([128, D]), writes=[g1bc.b])

                            def p1a_gen(c):
                                xx = xt[c % 2]
                                kb.dma(SP, xx.t[:], x[c * 128:(c + 1) * 128, :], writes=[xx.b])
                                yield
                                yield from norm_gen(xx, c, g1bc, c % 2, 6 + (c % 2))

                            wkv = sb(s2a, "wkv", [128, KC, 768], BF16)
                            load_w(wkv, w_in_v[:, :, C_KV:C_KV + 768])
                            for kv in range(2):
                                src = cmp_w1[0, kv].rearrange("l d f -> d l f")
                                kb.dma(POOL, w1sb.t[0:64, kv], src, writes=[w1sb.b])
                                kb.dma(POOL, w1sb.t[64:128, kv], src, writes=[w1sb.b])
                            kb.dma(POOL, w2sb.t[:], cmp_w2[0].rearrange("k f d -> f k d"), writes=[w2sb.b])
                            kb.dma(SP, pe_sb.t[:], cmp_pe[0].rearrange("k l d -> l k d"), writes=[pe_sb.b])
                            cmp_tok = [sb(s2a, f"cmp_tok{i}", [128, 256], BF16) for i in range(2)]
                            sqk = [sb(s2a, f"sqk{i}", [128, 256], F32) for i in range(2)]
                            kst = sb(s2a, "kst", [128, NT, 16], F32, NT)
                            tmpk = [sb(s2a, f"tmpk{i}", [128, 4, 64], F32) for i in range(2)]
                            ka_slc = [sb(s2a, f"ka_slc{i}", [128, 2, 128], BF16, 2) for i in range(2)]
                            ka_win = [sb(s2a, f"ka_win{i}", [128, 2, 128], BF16, 2) for i in range(2)]
                            for i in range(2):
                                kb.op(POOL, lambda i=i: G.memset(ka_slc[i].t[:], 0.0), writes=ka_slc[i].bs)
                                kb.op(POOL, lambda i=i: G.memset(ka_win[i].t[:], 0.0), writes=ka_win[i].bs)
                            def p1b_gen(c):
                                i2 = c % 2
                                bA, bB = (0, 1) if i2 == 0 else (2, 3)
                                tok = slice(c * 128, (c + 1) * 128)
                                if CUT >= 1:
                                    yield
                                    for k in range(KC):
                                        kb.op(PE, lambda k=k: TE.matmul(pf(bA), lhsT=hT.t[:, k, tok], rhs=wkv.t[:, k, 0:512],
                                                                        start=(k == 0), stop=(k == KC - 1)),
                                              reads=[hT.bs[c], wkv.b], writes=[PS[bA].b], inc=(k == KC - 1))
                                    for k in range(KC):
                                        kb.op(PE, lambda k=k: TE.matmul(pf(bB)[:, 0:256], lhsT=hT.t[:, k, tok], rhs=wkv.t[:, k, 512:768],
                                                                        start=(k == 0), stop=(k == KC - 1)),
                                              reads=[hT.bs[c], wkv.b], writes=[PS[bB].b], inc=(k == KC - 1))
                                if CUT >= 2:
                                    yield
                                    ct = cmp_tok[i2]
                                    kb.op(ACT, lambda: A.copy(out=ct.t[:], in_=pf(bA)[:, 0:256]), reads=[PS[bA].b], writes=[ct.b])
                                    sq = sqk[i2]
                                    kb.op(ACT, lambda: A.activation(out=sq.t[:, 0:128], in_=pf(bA)[:, 256:384], func=AF.Square),
                                          reads=[PS[bA].b], writes=[sq.b])
                                    kb.op(ACT, lambda: A.activation(out=sq.t[:, 128:256], in_=pf(bB)[:, 0:128], func=AF.Square),
                                          reads=[PS[bB].b], writes=[sq.b])
                                if CUT >= 3:
                                    yield
                                    ks = kst.t
                                    ksb = [kst.bs[c]]
                                    kb.op(DVE, lambda: V.tensor_reduce(out=ks[:, c, 0:4], in_=sq.t[:].rearrange("p (a b) -> p a b", b=64),
                                                                       axis=AX.X, op=ALU.add), reads=[sq.b], writes=ksb)
                                    rstd_from_ss(ks[:, c, 0:4], ks[:, c, 4:8], ks[:, c, 8:12], ks[:, c, 12:16], 64, ksb)
                                    tk = tmpk[i2]
                                    kb.op(DVE, lambda: V.tensor_tensor(out=tk.t[:, 0:2, :], in0=pf(bA)[:, 256:384].rearrange("p (a b) -> p a b", b=64),
                                                                       in1=ks[:, c, 12:14].unsqueeze(2).broadcast_to([128, 2, 64]), op=ALU.mult),
                                          reads=[PS[bA].b] + ksb, writes=[tk.b])
                                    kb.op(DVE, lambda: V.tensor_tensor(out=tk.t[:, 2:4, :], in0=pf(bB)[:, 0:128].rearrange("p (a b) -> p a b", b=64),
                                                                       in1=ks[:, c, 14:16].unsqueeze(2).broadcast_to([128, 2, 64]), op=ALU.mult),
                                          reads=[PS[bB].b] + ksb, writes=[tk.b])
                                    ksl, kwn = ka_slc[i2], ka_win[i2]
                                    kb.op(DVE, lambda: V.tensor_tensor(out=ksl.t[:, :, 0:64], in0=tk.t[:, 0:2, :],
                                                                       in1=gk.t[:, 1:2, :].broadcast_to([128, 2, 64]), op=ALU.mult),
                                          reads=[tk.b, gk.b], writes=[ksl.bs[0]])
                                    kb.op(DVE, lambda: V.tensor_tensor(out=kwn.t[:, :, 0:64], in0=tk.t[:, 2:4, :],
                                                                       in1=gk.t[:, 2:3, :].broadcast_to([128, 2, 64]), op=ALU.mult),
                                          reads=[tk.b, gk.b], writes=[kwn.bs[0]])
                                if CUT >= 4:
                                    yield
                                    kb.op(POOL, lambda: G.tensor_copy(out=ksl.t[:, :, 64:96], in_=OH.t[:, c:c + 1, :].broadcast_to([128, 2, 32])),
                                          reads=[OH.b], writes=[ksl.bs[1]])
                                    kb.op(POOL, lambda: G.tensor_copy(out=ksl.t[:, :, 96:100], in_=KAL.t[:, c:c + 1, :].broadcast_to([128, 2, 4])),
                                          reads=[KAL.b], writes=[ksl.bs[1]])
                                    kb.op(POOL, lambda: G.tensor_copy(out=kwn.t[:, :, 96:100], in_=KAL.t[:, c:c + 1, :].broadcast_to([128, 2, 4])),
                                          reads=[KAL.b], writes=[kwn.bs[1]])
                                if CUT >= 5:
                                    yield
                                    kb.op(ACT, lambda: A.copy(out=V_slc.t[:, c, :, 0:64], in_=pf(bA)[:, 384:512].rearrange("p (a b) -> p a b", b=64)),
                                          reads=[PS[bA].b], writes=[V_slc.bs[c]])
                                    kb.op(ACT, lambda: A.copy(out=V_win.t[:, c, :, 0:64], in_=pf(bB)[:, 128:256].rearrange("p (a b) -> p a b", b=64)),
                                          reads=[PS[bB].b], writes=[V_win.bs[c]])
                                if CUT >= 6:
                                    yield
                                    tb = 4 + i2
                                    for _ in range(XY3):
                                        yield
                                    transposes([ksl.t[:, 0, :], ksl.t[:, 1, :], kwn.t[:, 0, :], kwn.t[:, 1, :], ct.t[:, 0:128], ct.t[:, 128:256]],
                                               tb, ksl.bs + kwn.bs + [ct.b])
                                    pv3 = pb(tb).rearrange("p (a b) -> p a b", b=128)
                                    kb.op(ACT, lambda: A.copy(out=KT_slc.t[:, :, tok], in_=pv3[:, 0:2, :]), reads=[PS[tb].b], writes=[KT_slc.bs[c]])
                                    kb.op(ACT, lambda: A.copy(out=KT_win.t[:, :, tok], in_=pv3[:, 2:4, :]), reads=[PS[tb].b], writes=[KT_win.bs[c]])
                                    kb.op(ACT, lambda: A.copy(out=cmpT.t[:, :, tok], in_=pv3[:, 4:6, :]), reads=[PS[tb].b], writes=[cmpT.bs[c]])
                            def p1ab_gen(c):
                                yield from p1a_gen(c)
                                yield from p1b_gen(c)

                            run_interleaved([(lambda c=c: p1ab_gen(c)) for c in range(NT)], width=2)
                            if "KT_slc" in dbg:
                                kdb = sb(s2a, "kdb", [128, 2, S], F32)
                                kb.op(DVE, lambda: V.tensor_copy(out=kdb.t[:], in_=KT_slc.t[:]), reads=KT_slc.bs, writes=[kdb.b])
                                kb.dma(SP, dbg["KT_slc"].rearrange("p (a b) -> p a b", b=S), kdb.t[:], reads=[kdb.b], is_out=True)
                            kb.barrier()
                            chk(3)

                        with ExitStack() as s2b:
                            peT = sb(s2b, "peT", [64, 2, 32], BF16)
                            bias_c = sb(s2b, "bias_c", [128, 2], F32)
                            xhs = [sb(s2b, f"xh{i}", [128, 128], F32) for i in range(4)]
                            x2s = [sb(s2b, f"x2{i}", [128, 128], F32) for i in range(4)]
                            sgs_ = [sb(s2b, f"sgc{i}", [128, 128], F32) for i in range(4)]
                            HTbs = [sb(s2b, f"HTb{i}", [128, 128], BF16) for i in range(4)]
                            kca = sb(s2b, "kca", [128, 2, 128], BF16)
                            cst = sb(s2b, "cst", [128, 8], F32)
                            tmpcs = [sb(s2b, f"tmpc{i}", [128, 64], F32) for i in range(4)]
                            kb.op(POOL, lambda: G.memset(kca.t[:], 0.0), writes=[kca.b])
                            kb.op(POOL, lambda: G.memset(Vc.t[:], 0.0), writes=[Vc.b])
                            kb.op(POOL, lambda: G.tensor_copy(out=kca.t[:, :, 96:100], in_=KCAL.t[:].unsqueeze(1).broadcast_to([128, 2, 4])),
                                  reads=[KCAL.b], writes=[kca.b])
                            kb.op(POOL, lambda: G.memset(Vc.t[:, :, 64:65], 1.0), writes=[Vc.b])
                            kb.op(POOL, lambda: G.tensor_copy(out=Vc.t[:, :, 65:97], in_=ov.t[:].unsqueeze(1).broadcast_to([128, 2, 32])),
                                  reads=[ov.b], writes=[Vc.b])
                            for kv in range(2):
                                kb.op(PE, lambda kv=kv: TE.transpose(out=pf(0)[0:64, kv * 32:(kv + 1) * 32], in_=pe_sb.t[0:32, kv, :],
                                                                     identity=ident_f.t[0:32, 0:32]),
                                      reads=[pe_sb.b, ident_f.b], writes=[PS[0].b])
                            kb.op(DVE, lambda: V.tensor_copy(out=peT.t[:], in_=pf(0)[0:64, 0:64].rearrange("p (a b) -> p a b", b=32)),
                                  reads=[PS[0].b], writes=[peT.b])
                            for kv in range(2):
                                for l in range(32):
                                    kb.op(PE, lambda kv=kv, l=l: TE.matmul(pf(1)[:, kv:kv + 1], lhsT=w1sb.t[0:64, kv, l, :], rhs=peT.t[0:64, kv, l:l + 1],
                                                                           start=(l == 0), stop=(l == 31)),
                                          reads=[w1sb.b, peT.b], writes=[PS[1].b], inc=(l == 31))
                            kb.op(DVE, lambda: V.tensor_copy(out=bias_c.t[:], in_=pf(1)[:, 0:2]), reads=[PS[1].b], writes=[bias_c.b])
                            def cmp_gen(kv, g, idx):
                                bH, bO = 2 * idx, 2 * idx + 1
                                xh, x2, sg, HTb, tmpc = xhs[idx], x2s[idx], sgs_[idx], HTbs[idx], tmpcs[idx]
                                for l in range(32):
                                    kb.op(PE, lambda kv=kv, g=g, l=l: TE.matmul(
                                        pf(bH)[:, 0:127], lhsT=w1sb.t[g * 64:(g + 1) * 64, kv, l, :],
                                        rhs=cmpT.t[g * 64:(g + 1) * 64, kv, l:l + 16 * 126 + 1:16],
                                        start=(l == 0), stop=(l == 31)),
                                        reads=[w1sb.b] + cmpT.bs, writes=[PS[bH].b], inc=(l == 31))
                                yield
                                kb.op(ACT, lambda kv=kv: A.activation(out=xh.t[:, 0:127], in_=pf(bH)[:, 0:127], func=AF.Identity,
                                                                      bias=bias_c.t[:, kv:kv + 1], scale=1.0),
                                      reads=[PS[bH].b, bias_c.b], writes=[xh.b])
                                yield
                                kb.op(DVE, lambda: V.tensor_tensor(out=x2.t[:, 0:127], in0=xh.t[:, 0:127], in1=xh.t[:, 0:127], op=ALU.mult),
                                      reads=[xh.b], writes=[x2.b])
                                yield
                                kb.op(DVE, lambda: V.tensor_scalar(out=x2.t[:, 0:127], in0=x2.t[:, 0:127], scalar1=0.044715, scalar2=1.0,
                                                                   op0=ALU.mult, op1=ALU.add), reads=[x2.b], writes=[x2.b])
                                yield
                                kb.op(DVE, lambda: V.tensor_tensor(out=x2.t[:, 0:127], in0=x2.t[:, 0:127], in1=xh.t[:, 0:127], op=ALU.mult),
                                      reads=[x2.b, xh.b], writes=[x2.b])
                                yield
                                kb.op(ACT, lambda: A.activation(out=sg.t[:, 0:127], in_=x2.t[:, 0:127], func=AF.Sigmoid, scale=1.5957691216057308),
                                      reads=[x2.b], writes=[sg.b])
                                yield
                                kb.op(DVE, lambda: V.tensor_tensor(out=HTb.t[:, 0:127], in0=xh.t[:, 0:127], in1=sg.t[:, 0:127], op=ALU.mult),
                                      reads=[xh.b, sg.b], writes=[HTb.b])
                                yield
                                kb.op(PE, lambda kv=kv: TE.matmul(pf(bO)[0:127, 0:64], lhsT=HTb.t[:, 0:127], rhs=w2sb.t[:, kv, :], start=True, stop=True),
                                      reads=[HTb.b, w2sb.b], writes=[PS[bO].b])
                                yield
                                if kv == 0:
                                    kb.op(ACT, lambda g=g: A.activation(out=tmpc.t[0:127, :], in_=pf(bO)[0:127, 0:64], func=AF.Square,
                                                                        accum_out=cst.t[0:127, g:g + 1]),
                                          reads=[PS[bO].b], writes=[tmpc.b, cst.b])
                                    rstd_from_ss(cst.t[0:127, g:g + 1], cst.t[0:127, 2 + g:3 + g], cst.t[0:127, 4 + g:5 + g], cst.t[0:127, 6 + g:7 + g], 64, [cst.b])
                                    kb.op(DVE, lambda g=g: V.scalar_tensor_tensor(out=kca.t[0:127, g, 0:64], in0=pf(bO)[0:127, 0:64],
                                                                                  scalar=cst.t[0:127, 6 + g:7 + g], in1=gk.t[0:127, 0, :],
                                                                                  op0=ALU.mult, op1=ALU.mult),
                                          reads=[PS[bO].b, cst.b, gk.b], writes=[kca.b])
                                else:
                                    kb.op(ACT, lambda g=g: A.copy(out=Vc.t[0:127, g, 0:64], in_=pf(bO)[0:127, 0:64]),
                                          reads=[PS[bO].b], writes=[Vc.b])
                                yield

                            run_interleaved([(lambda kv=kv, g=g: cmp_gen(kv, g, 2 * kv + g)) for kv in range(2) for g in range(2)], width=4)
                            transposes([kca.t[:, 0, :], kca.t[:, 1, :]], 6, [kca.b])
                            kb.op(ACT, lambda: A.copy(out=KcT.t[:], in_=pb(6)[:, 0:256].rearrange("p (a b) -> p a b", b=128)),
                                  reads=[PS[6].b], writes=[KcT.b])
                            if "kc" in dbg:
                                kcd = sb(s2b, "kcd", [128, 2, 64], F32)
                                kb.op(DVE, lambda: V.tensor_copy(out=kcd.t[:], in_=kca.t[:, :, 0:64]), reads=[kca.b], writes=[kcd.b])
                                kb.dma(SP, dbg["kc"].rearrange("p (a b) -> p a b", b=64), kcd.t[:], reads=[kcd.b], is_out=True)
                            if "vc" in dbg:
                                vcd = sb(s2b, "vcd", [128, 2, 64], F32)
                                kb.op(DVE, lambda: V.tensor_copy(out=vcd.t[:], in_=Vc.t[:, :, 0:64]), reads=[Vc.b], writes=[vcd.b])
                                kb.dma(SP, dbg["vc"].rearrange("p (a b) -> p a b", b=64), vcd.t[:], reads=[vcd.b], is_out=True)
                            kb.barrier()
                            chk(4)

                    phase_1c()
                    kb.barrier()
                    chk(5)

                wm = sb(sB, "wm", [128, KC, 2048], BF16, 4)
                wbr = sb(sB, "wbr", [128, 8, D], BF16)
                phase_1d()
                kb.barrier()
                chk(6)
                phase_1e()
                kb.barrier()
                chk(7)

            phase_2()
            kb.finish()
    except _Stop:
        pass
    return nc


_NAMES = ["x", "norm1_g", "w_in", "nsa_q_norm", "nsa_k_norm", "cmp_pe", "cmp_w1", "cmp_w2", "ret_gn_g",
          "w_branch", "w_out", "norm2_g", "ffn_w_gate", "ffn_w_up", "ffn_w_down"]


def kernel(**inputs):
    n = 8
    arrs = {k: np.ascontiguousarray(np.asarray(inputs[k], dtype=np.float32)) for k in _NAMES}
    nc = build_nc()
    in_maps = []
    for i in range(n):
        m = {k: arrs[k] for k in _NAMES if k != "x"}
        m["x"] = np.ascontiguousarray(arrs["x"][i])
        in_maps.append(m)
    res = run_bass_kernel_spmd(nc, in_maps, core_ids=list(range(n)))
    return np.stack([np.asarray(r["out"], dtype=np.float32) for r in res.results], axis=0)
```

```python
import numpy as np
from contextlib import ExitStack
import concourse.bass as bass
import concourse.mybir as mybir
from concourse.bass_utils import run_bass_kernel_spmd

F32 = mybir.dt.float32
BF16 = mybir.dt.bfloat16
AF = mybir.ActivationFunctionType
ALU = mybir.AluOpType
AX = mybir.AxisListType

S = 2048
D = 1024
NT = 16
KC = 8
N_IN = 5400
DFF = 2816
NFB = 22
EPS = 1e-6
SEM_LIMIT = 24000
import os as _os
CUT = int(_os.environ.get('P1B_CUT', '99'))
CUTC = int(_os.environ.get('P1C_CUT', '99'))
SUBC = int(_os.environ.get('P1C_SUB', '99'))
WIDTH = int(_os.environ.get('P1C_WIDTH', '2'))
XY1 = int(_os.environ.get('XY1', '0'))
XY2 = int(_os.environ.get('XY2', '2'))
XY3 = int(_os.environ.get('XY3', '0'))
PREWIN = int(_os.environ.get('PREWIN', '1'))
QDVE = int(_os.environ.get('QDVE', '1'))
NUDVE = int(_os.environ.get('NUDVE', '1'))
NEG = -30000.0

C_Q = 0
C_KV = 512
C_G = 1280
C_R = 1304
C_M = 3352


class Buf:
    __slots__ = ("name", "w", "r", "excl")

    def __init__(self, name):
        self.name = name
        self.w = None
        self.r = []
        self.excl = False


class SemW:
    __slots__ = ("h",)

    def __init__(self, h):
        self.h = h


class Slot:
    __slots__ = ("sem", "val")

    def __init__(self, sem):
        self.sem = sem
        self.val = 0


class Q:
    def __init__(self, name, eng):
        self.name = name
        self.eng = eng
        self.sem = None
        self.count = 0
        self.waited = {}
        self.ring = []
        self.ri = 0
        self.pending = False


class T:
    def __init__(self, t, name, nb=1):
        self.t = t
        self.bs = [Buf(f"{name}{i}") for i in range(nb)]

    @property
    def b(self):
        return self.bs[0]


class KB:
    def __init__(self, nc, es):
        self.nc = nc
        self.es = es
        self.nsem = 0
        self.pe = self.mkq("pe", nc.tensor)
        self.act = self.mkq("act", nc.scalar)
        self.dve = self.mkq("dve", nc.vector)
        self.pool = self.mkq("pool", nc.gpsimd)
        self.sp = self.mkq("sp", nc.sync)
        self.qs = [self.pe, self.act, self.dve, self.pool, self.sp]
        for q, n in ((self.sp, 16), (self.pool, 8), (self.act, 4)):
            q.ring = [Slot(self.new_sem(f"{q.name}_d{i}")) for i in range(n)]
        self.out_toks = []

    def new_sem(self, name):
        self.nsem += 1
        return SemW(self.es.enter_context(self.nc.semaphore(f"{name}_{self.nsem}")))

    def mkq(self, name, eng):
        q = Q(name, eng)
        q.sem = self.new_sem(name)
        return q

    def wait(self, q, tok):
        sw, val = tok[0], tok[1]
        if q.waited.get(sw, 0) >= val:
            return
        q.eng.wait_ge(sw.h, val)
        q.waited[sw] = val

    def _dep(self, q, tok, raw, force=False):
        if tok[2] is q and q is self.pe and not force:
            return
        self.wait(q, tok)

    def _deps(self, q, reads, writes, force=False):
        for b in reads:
            if b.w is not None:
                self._dep(q, b.w, True, force)
            if b.excl:
                for t in b.r:
                    if t[2] is not q:
                        self._dep(q, t, False, force)
        for b in writes:
            if b.w is not None:
                self._dep(q, b.w, False, force)
            for t in b.r:
                self._dep(q, t, False, force)

    def _record(self, tok, reads, writes):
        for b in reads:
            if tok[2] is not None:
                b.r = [t for t in b.r if t[2] is not tok[2]]
            b.r.append(tok)
        for b in writes:
            b.w = tok
            b.r = []

    def op(self, q, fn, reads=(), writes=(), inc=True):
        self._deps(q, reads, writes)
        ins = fn()
        if inc:
            if q.count >= SEM_LIMIT and not q.pending:
                q.sem = self.new_sem(q.name)
                q.count = 0
            ins.then_inc(q.sem.h, 1)
            q.count += 1
            q.pending = False
            tok = (q.sem, q.count, q)
        else:
            q.pending = True
            tok = (q.sem, q.count + 1, q)
        self._record(tok, reads, writes)
        return ins

    def dma(self, q, out, in_, reads=(), writes=(), is_out=False):
        self._deps(q, reads, writes, force=True)
        slot = q.ring[q.ri % len(q.ring)]
        q.ri += 1
        if slot.val > 0:
            self.wait(q, (slot.sem, slot.val))
        if slot.val >= SEM_LIMIT:
            slot.sem = self.new_sem(q.name + "_d")
            slot.val = 0
        ins = q.eng.dma_start(out=out, in_=in_)
        ins.then_inc(slot.sem.h, 16)
        slot.val += 16
        tok = (slot.sem, slot.val, None)
        self._record(tok, reads, writes)
        if is_out:
            self.out_toks.append(tok)
        return tok

    def barrier(self):
        toks = []
        for o in self.qs:
            if o.count > 0:
                toks.append((o.sem, o.count, o))
            for sl in o.ring:
                if sl.val > 0:
                    toks.append((sl.sem, sl.val, None))
        for q in self.qs:
            for t in toks:
                if t[2] is q:
                    continue
                self.wait(q, t)

    def finish(self):
        for t in self.out_toks:
            self.wait(self.sp, t)


class _Stop(Exception):
    pass


def build_nc(debug=None, stop=None):
    nc = bass.Bass("TRN2", target_bir_lowering=False)

    def din(name, shape):
        return nc.dram_tensor(name, list(shape), F32, kind="ExternalInput").ap()

    x = din("x", [S, D])
    norm1_g = din("norm1_g", [1, D])
    w_in = din("w_in", [1, D, N_IN])
    nsa_q_norm = din("nsa_q_norm", [1, 64])
    nsa_k_norm = din("nsa_k_norm", [1, 3, 64])
    cmp_pe = din("cmp_pe", [1, 2, 32, 64])
    cmp_w1 = din("cmp_w1", [1, 2, 32, 64, 128])
    cmp_w2 = din("cmp_w2", [1, 2, 128, 64])
    ret_gn_g = din("ret_gn_g", [1, 4, 128])
    w_branch = din("w_branch", [1, 2, 512, D])
    w_out = din("w_out", [1, D, D])
    norm2_g = din("norm2_g", [1, D])
    ffn_w_gate = din("ffn_w_gate", [1, D, DFF])
    ffn_w_up = din("ffn_w_up", [1, D, DFF])
    ffn_w_down = din("ffn_w_down", [1, DFF, D])
    out = nc.dram_tensor("out", [S, D], F32, kind="ExternalOutput").ap()
    xmid = nc.dram_tensor("xmid", [S, D], F32, kind="Internal").ap()
    dbg = {}
    if debug:
        for name, shape in debug.items():
            dbg[name] = nc.dram_tensor("dbg_" + name, list(shape), F32, kind="ExternalOutput").ap()

    w_in_v = w_in[0].rearrange("(k p) n -> p k n", p=128)

    try:
        with ExitStack() as es:
            kb = KB(nc, es)

            def chk(n):
                if stop is not None and n >= stop:
                    kb.barrier()
                    kb.finish()
                    raise _Stop()
            PE, ACT, DVE, POOL, SP = kb.pe, kb.act, kb.dve, kb.pool, kb.sp
            V, A, G, TE = nc.vector, nc.scalar, nc.gpsimd, nc.tensor

            def sb(scope, name, shape, dt, nb=1):
                return T(scope.enter_context(nc.sbuf_tensor(name, list(shape), dt)), name, nb)

            PS2 = [es.enter_context(nc.psum_tensor(f"psp{j}", [128, 1024], F32)) for j in range(4)]
            PS = [T(None, f"ps{i}") for i in range(8)]
            for p_ in PS:
                p_.b.excl = True

            def pf(i):
                return PS2[i // 2][:, (i % 2) * 512:(i % 2 + 1) * 512]

            def pb(i):
                return PS2[i // 2][:].bitcast(BF16)[:, (i % 2) * 1024:(i % 2 + 1) * 1024]

            def pf2(j):
                return PS2[j][:]

            ident_f = sb(es, "ident_f", [128, 128], F32)
            ident_b = sb(es, "ident_b", [128, 128], BF16)
            ones_f = sb(es, "ones_f", [128, 128], F32)
            nhalf = sb(es, "nhalf", [128, 16], F32)
            hT = sb(es, "hT", [128, KC, S], BF16, NT)
            stat = sb(es, "stat", [128, NT, 4], F32, NT)
            hb = [sb(es, f"hb{i}", [128, D], BF16) for i in range(2)]
            junk = sb(es, "junk", [128, D], BF16)

            kb.op(POOL, lambda: G.memset(ones_f.t[:], 1.0), writes=[ones_f.b])
            kb.op(POOL, lambda: G.memset(nhalf.t[:], -0.5), writes=[nhalf.b])
            kb.op(POOL, lambda: G.affine_select(out=ident_f.t[:], in_=ones_f.t[:, 0:128], pattern=[[1, 128]],
                                                compare_op=ALU.is_equal, fill=0.0, base=0, channel_multiplier=-1),
                  reads=[ones_f.b], writes=[ident_f.b])
            kb.op(DVE, lambda: V.tensor_copy(out=ident_b.t[:], in_=ident_f.t[:]), reads=[ident_f.b], writes=[ident_b.b])

            def rstd_from_ss(ss_ap, ms_ap, sd_ap, rs_ap, n, bufs):
                k = ms_ap.shape[-1]
                P_ = ms_ap.shape[0]
                kb.op(DVE, lambda: V.tensor_scalar(out=ms_ap, in0=ss_ap, scalar1=1.0 / n, scalar2=EPS,
                                                   op0=ALU.mult, op1=ALU.add), reads=bufs, writes=bufs)
                kb.op(POOL, lambda: G.tensor_tensor(out=rs_ap, in0=ms_ap, in1=nhalf.t[0:P_, 0:k], op=ALU.pow),
                      reads=list(bufs) + [nhalf.b], writes=bufs)

            def transposes(src_aps, bank, reads):
                pbv = pb(bank)
                n = len(src_aps)
                for i, ap in enumerate(src_aps):
                    kb.op(PE, lambda ap=ap, i=i: TE.transpose(out=pbv[:, i * 128:(i + 1) * 128], in_=ap, identity=ident_b.t[:]),
                          reads=list(reads) + [ident_b.b], writes=[PS[bank].b], inc=(i == n - 1))

            def norm_gen(src, c, gbc, sidx, bank):
                sbuf_ = [stat.bs[c]]
                st = stat.t
                kb.op(ACT, lambda: A.activation(out=junk.t[:], in_=src.t[:], func=AF.Square, accum_out=st[:, c, 0:1]),
                      reads=src.bs, writes=[junk.b] + sbuf_)
                yield
                kb.op(DVE, lambda: V.tensor_scalar(out=st[:, c, 1:2], in0=st[:, c, 0:1], scalar1=1.0 / D, scalar2=EPS,
                                                   op0=ALU.mult, op1=ALU.add), reads=sbuf_, writes=sbuf_)
                yield
                kb.op(POOL, lambda: G.tensor_tensor(out=st[:, c, 3:4], in0=st[:, c, 1:2], in1=nhalf.t[:, 0:1], op=ALU.pow),
                      reads=sbuf_ + [nhalf.b], writes=sbuf_)
                yield
                h = hb[sidx % 2]
                kb.op(DVE, lambda: V.scalar_tensor_tensor(out=h.t[:], in0=src.t[:], scalar=st[:, c, 3:4], in1=gbc.t[:],
                                                          op0=ALU.mult, op1=ALU.mult),
                      reads=src.bs + [gbc.b] + sbuf_, writes=[h.b])
                yield
                for _ in range(XY3):
                    yield
                transposes([h.t[:, k * 128:(k + 1) * 128] for k in range(KC)], bank, [h.b])
                yield
                kb.op(ACT, lambda: A.copy(out=hT.t[:, :, c * 128:(c + 1) * 128],
                                          in_=pb(bank).rearrange("p (a b) -> p a b", b=128)),
                      reads=[PS[bank].b], writes=[hT.bs[c]])
                yield

            def run_interleaved(gen_fns, width=2):
                pending = list(gen_fns)
                active = []
                while pending or active:
                    while pending and len(active) < width:
                        active.append(pending.pop(0)())
                    for g_ in list(active):
                        try:
                            next(g_)
                        except StopIteration:
                            active.remove(g_)

            def load_w(dst, src_ap, q=None):
                kb.dma(q or POOL, dst.t[:], src_ap, writes=[dst.b])

            def dbg_store(name, src_ap, rows, reads):
                if name in dbg:
                    kb.dma(SP, dbg[name][rows], src_ap, reads=reads, is_out=True)

            def phase_1c():
                with ExitStack() as s3:
                    wq = sb(s3, "wq", [128, KC, 512], BF16)
                    load_w(wq, w_in_v[:, :, C_Q:C_Q + 512])
                    wg = sb(s3, "wg", [128, KC, 24], BF16)
                    load_w(wg, w_in_v[:, :, C_G:C_G + 24])
                    sqq = [sb(s3, f"sqq{i}", [128, 512], F32) for i in range(2)]
                    qst = sb(s3, "qst", [128, NT, 32], F32, NT)
                    tmpq = [sb(s3, f"tmpq{i}", [128, 8, 64], F32) for i in range(2)]
                    qaug = [sb(s3, f"qaug{i}", [128, 8, 128], BF16, 4) for i in range(2)]
                    qT = [sb(s3, f"qT{i}", [128, 8, 128], BF16) for i in range(2)]
                    qT2 = [sb(s3, f"qT2{i}", [128, 8, 128], BF16) for i in range(2)]
                    gate = [sb(s3, f"gate{i}", [128, 24], F32) for i in range(2)]
                    scl = [sb(s3, f"scl{i}", [128, 1024], F32) for i in range(2)]
                    NPT = 3
                    PT = [[sb(s3, f"PT{i}_{j}", [128, 1024], BF16, 2) for j in range(NPT)] for i in range(2)]
                    oTs = [sb(s3, f"oTs{i}", [128, 1024], F32) for i in range(2)]
                    num = [sb(s3, f"num{i}", [128, 3, 8, 64], F32) for i in range(2)]
                    den = [sb(s3, f"den{i}", [128, 3, 8], F32) for i in range(2)]
                    rdc = [sb(s3, f"rdc{i}", [128, 8], F32) for i in range(2)]
                    impn = [sb(s3, f"impn{i}", [128, 8, 32], F32) for i in range(2)]
                    imp = [sb(s3, f"imp{i}", [128, 2, 32], F32) for i in range(2)]
                    top8 = [sb(s3, f"top8{i}", [128, 2, 8], F32, 2) for i in range(2)]
                    rd = [sb(s3, f"rd{i}", [128, 3, 8], F32) for i in range(2)]
                    coef = [sb(s3, f"coef{i}", [128, 3, 8], F32) for i in range(2)]
                    oacc = [sb(s3, f"oacc{i}", [128, 8, 64], F32) for i in range(2)]
                    otmp = [sb(s3, f"otmp{i}", [128, 8, 64], F32) for i in range(2)]
                    ytok = [sb(s3, f"ytok{i}", [128, 512], BF16) for i in range(2)]
                    ydbg = sb(s3, "ydbg", [128, 512], F32) if "y_nsa" in dbg else None
                    for i in range(2):
                        kb.op(POOL, lambda i=i: G.memset(qaug[i].t[:], 0.0), writes=qaug[i].bs)
                    ptc = [0, 0]

                    def tile_gen(c):
                        i2 = c % 2
                        base = 4 * i2
                        bZ, bG = base, base + 1
                        bO0, bO1 = base + 2, base + 3
                        Sb = [PS[bZ].b, PS[bG].b]
                        Ob = [PS[bO0].b, PS[bO1].b]
                        tok = slice(c * 128, (c + 1) * 128)

                        def S2():
                            return pf2(base // 2)

                        def O2():
                            return pf2(base // 2 + 1)

                        def X8():
                            return O2().rearrange("p (h c) -> p h c", h=8)

                        def to_token_major(br, ncol):
                            nu, de = num[i2], den[i2]
                            ot = oTs[i2]
                            kb.op(DVE, lambda: V.tensor_scalar(out=ot.t[0:ncol, :], in0=O2()[0:ncol, :], scalar1=1.0, scalar2=None, op0=ALU.mult),
                                  reads=Ob, writes=[ot.b])
                            yield
                            for _ in range(XY2):
                                yield
                            for hh in range(8):
                                kb.op(PE, lambda hh=hh: TE.transpose(out=X8()[:, hh, 0:ncol], in_=ot.t[0:ncol, hh * 128:(hh + 1) * 128],
                                                                     identity=ident_f.t[0:ncol, 0:ncol]),
                                      reads=[ot.b, ident_f.b], writes=[Ob[hh // 4]], inc=(hh % 4 == 3))
                            yield
                            if NUDVE:
                                kb.op(DVE, lambda: V.tensor_scalar(out=nu.t[:, br, :, :], in0=X8()[:, :, 0:64], scalar1=1.0, scalar2=None, op0=ALU.mult), reads=Ob, writes=[nu.b])
                            else:
                                kb.op(ACT, lambda: A.copy(out=nu.t[:, br, :, :], in_=X8()[:, :, 0:64]), reads=Ob, writes=[nu.b])
                            yield
                            kb.op(DVE, lambda: V.tensor_scalar(out=de.t[:, br, :], in0=X8()[:, :, 64], scalar1=1e-30, scalar2=None, op0=ALU.max),
                                  reads=Ob, writes=[de.b])
                            yield

                        def branch(br, KT, VV, kts, qsrc, pre=False):
                            n = len(kts)

                            def scores(kt):
                                for g in range(2):
                                    kb.op(PE, lambda g=g: TE.matmul(S2()[:, g * 512:(g + 1) * 512], lhsT=KT.t[:, g, kt * 128:(kt + 1) * 128],
                                                                    rhs=qsrc.t[:, 4 * g:4 * g + 4, :], start=True, stop=True),
                                          reads=[KT.bs[kt], qsrc.b], writes=[Sb[g]])

                            if not pre:
                                scores(kts[0])
                                yield
                            for j, kt in enumerate(kts):
                                pt = PT[i2][ptc[i2] % NPT]
                                ptc[i2] += 1
                                for g in range(2):
                                    kb.op(ACT, lambda pt=pt, g=g: A.activation(out=pt.t[:, g * 512:(g + 1) * 512], in_=S2()[:, g * 512:(g + 1) * 512], func=AF.Exp),
                                          reads=[Sb[g]], writes=[pt.bs[g]])
                                yield
                                if j + 1 < n:
                                    for _ in range(XY1):
                                        yield
                                    scores(kts[j + 1])
                                    yield
                                if kt == c:
                                    kb.op(DVE, lambda pt=pt: V.tensor_tensor(out=pt.t[:], in0=pt.t[:], in1=dmask8.t[:].rearrange("p a b -> p (a b)"), op=ALU.mult),
                                          reads=pt.bs + [dmask8.b], writes=pt.bs)
                                    yield
                                elif br == 2 and kt == c - 4:
                                    kb.op(DVE, lambda pt=pt: V.tensor_tensor(out=pt.t[:], in0=pt.t[:], in1=tmask8.t[:].rearrange("p a b -> p (a b)"), op=ALU.mult),
                                          reads=pt.bs + [tmask8.b], writes=pt.bs)
                                    yield
                                for g in range(2):
                                    kb.op(PE, lambda kt=kt, pt=pt, j=j, g=g: TE.matmul(O2()[0:65, g * 512:(g + 1) * 512], lhsT=VV.t[:, kt, g, :],
                                                                                       rhs=pt.t[:, g * 512:(g + 1) * 512], start=(j == 0), stop=(j == n - 1)),
                                          reads=[pt.bs[g], VV.bs[kt]], writes=[Ob[g]], inc=(j == n - 1))
                                yield
                            yield from to_token_major(br, 65)

                        for k in range(KC):
                            kb.op(PE, lambda k=k: TE.matmul(pf(bZ), lhsT=hT.t[:, k, tok], rhs=wq.t[:, k, :], start=(k == 0), stop=(k == KC - 1)),
                                  reads=[hT.bs[c], wq.b], writes=[PS[bZ].b], inc=(k == KC - 1))
                        for k in range(KC):
                            kb.op(PE, lambda k=k: TE.matmul(pf(bG)[:, 0:24], lhsT=hT.t[:, k, tok], rhs=wg.t[:, k, :], start=(k == 0), stop=(k == KC - 1)),
                                  reads=[hT.bs[c], wg.b], writes=[PS[bG].b], inc=(k == KC - 1))
                        yield
                        sq = sqq[i2]
                        kb.op(ACT, lambda: A.activation(out=sq.t[:], in_=pf(bZ), func=AF.Square), reads=[PS[bZ].b], writes=[sq.b])
                        gt = gate[i2]
                        kb.op(ACT, lambda: A.activation(out=gt.t[:], in_=pf(bG)[:, 0:24], func=AF.Tanh, scale=0.5), reads=[PS[bG].b], writes=[gt.b])
                        yield
                        kb.op(DVE, lambda: V.tensor_scalar(out=gt.t[:], in0=gt.t[:], scalar1=0.5, scalar2=0.5, op0=ALU.mult, op1=ALU.add),
                              reads=[gt.b], writes=[gt.b])
                        qs = qst.t
                        qsb = [qst.bs[c]]
                        kb.op(DVE, lambda: V.tensor_reduce(out=qs[:, c, 0:8], in_=sq.t[:].rearrange("p (a b) -> p a b", b=64), axis=AX.X, op=ALU.add),
                              reads=[sq.b], writes=qsb)
                        yield
                        kb.op(DVE, lambda: V.tensor_scalar(out=qs[:, c, 8:16], in0=qs[:, c, 0:8], scalar1=1.0 / 64, scalar2=EPS, op0=ALU.mult, op1=ALU.add), reads=qsb, writes=qsb)
                        yield
                        kb.op(POOL, lambda: G.tensor_tensor(out=qs[:, c, 24:32], in0=qs[:, c, 8:16], in1=nhalf.t[:, 0:8], op=ALU.pow),
                              reads=qsb + [nhalf.b], writes=qsb)
                        yield
                        tq = tmpq[i2]
                        qa = qaug[i2]
                        kb.op(DVE, lambda: V.tensor_tensor(out=tq.t[:], in0=pf(bZ).rearrange("p (a b) -> p a b", b=64),
                                                           in1=qs[:, c, 24:32].unsqueeze(2).broadcast_to([128, 8, 64]), op=ALU.mult),
                              reads=[PS[bZ].b] + qsb, writes=[tq.b])
                        yield
                        kb.op(DVE, lambda: V.tensor_tensor(out=qa.t[:, :, 0:64], in0=tq.t[:], in1=gq.t[:].unsqueeze(1).broadcast_to([128, 8, 64]), op=ALU.mult),
                              reads=[tq.b, gq.b], writes=[qa.bs[0]])
                        kb.op(POOL, lambda: G.tensor_copy(out=qa.t[:, :, 96:100], in_=QAL.t[:, c, :, :]), reads=[QAL.b], writes=[qa.bs[1]])
                        yield
                        transposes([qa.t[:, h, :] for h in range(8)], bZ, qa.bs)
                        yield
                        q1 = qT[i2]
                        if QDVE:
                            kb.op(DVE, lambda: V.tensor_copy(out=q1.t[:], in_=pb(bZ).rearrange("p (a b) -> p a b", b=128)), reads=[PS[bZ].b], writes=[q1.b])
                        else:
                            kb.op(ACT, lambda: A.copy(out=q1.t[:], in_=pb(bZ).rearrange("p (a b) -> p a b", b=128)), reads=[PS[bZ].b], writes=[q1.b])
                        yield
                        nu, de = num[i2], den[i2]
                        rdc_, impn_, imp_, top8_ = rdc[i2], impn[i2], imp[i2], top8[i2]
                        sc_ = scl[i2]
                        pc = PT[i2][ptc[i2] % NPT]
                        ptc[i2] += 1
                        for g in range(2):
                            kb.op(PE, lambda g=g: TE.matmul(S2()[0:127, g * 512:(g + 1) * 512], lhsT=KcT.t[:, g, 0:127], rhs=q1.t[:, 4 * g:4 * g + 4, :], start=True, stop=True),
                                  reads=[KcT.b, q1.b], writes=[Sb[g]])
                        yield
                        kb.op(DVE, lambda: V.scalar_tensor_tensor(out=sc_.t[0:127, :].rearrange("p (a b) -> p a b", b=128),
                                                                  in0=S2()[0:127, :].rearrange("p (a b) -> p a b", b=128), scalar=60.0,
                                                                  in1=cmask.t[0:127, tok].unsqueeze(1).broadcast_to([127, 8, 128]),
                                                                  op0=ALU.min, op1=ALU.add),
                              reads=Sb + [cmask.b], writes=[sc_.b])
                        yield
                        if PREWIN:
                            kt0 = max(0, c - 4)
                            for g in range(2):
                                kb.op(PE, lambda g=g: TE.matmul(S2()[:, g * 512:(g + 1) * 512], lhsT=KT_win.t[:, g, kt0 * 128:(kt0 + 1) * 128],
                                                                rhs=q1.t[:, 4 * g:4 * g + 4, :], start=True, stop=True),
                                      reads=[KT_win.bs[kt0], q1.b], writes=[Sb[g]])
                            yield
                        kb.op(ACT, lambda: A.activation(out=pc.t[0:127, :], in_=sc_.t[0:127, :], func=AF.Exp), reads=[sc_.b], writes=pc.bs)
                        yield
                        for g in range(2):
                            kb.op(PE, lambda g=g: TE.matmul(O2()[0:97, g * 512:(g + 1) * 512], lhsT=Vc.t[0:127, g, :], rhs=pc.t[0:127, g * 512:(g + 1) * 512], start=True, stop=True),
                                  reads=[pc.bs[g], Vc.b], writes=[Ob[g]])
                        yield
                        yield from to_token_major(0, 97)
                        kb.op(DVE, lambda: V.reciprocal(out=rdc_.t[:], in_=de.t[:, 0, :]), reads=[de.b], writes=[rdc_.b])
                        yield
                        kb.op(DVE, lambda: V.tensor_tensor(out=impn_.t[:], in0=X8()[:, :, 65:97],
                                                           in1=rdc_.t[:].unsqueeze(2).broadcast_to([128, 8, 32]), op=ALU.mult),
                              reads=Ob + [rdc_.b], writes=[impn_.b])
                        yield
                        q2 = qT2[i2]

                        def imp_chain():
                            kb.op(DVE, lambda: V.tensor_reduce(out=imp_.t[:], in_=impn_.t[:].rearrange("p (g r) j -> p g j r", g=2), axis=AX.X, op=ALU.add),
                                  reads=[impn_.b], writes=[imp_.b])
                            yield
                            kb.op(DVE, lambda: V.tensor_tensor(out=imp_.t[:], in0=imp_.t[:], in1=addc.t[:, c:c + 1, :].broadcast_to([128, 2, 32]), op=ALU.add),
                                  reads=[imp_.b, addc.b], writes=[imp_.b])
                            yield
                            for g in range(2):
                                kb.op(DVE, lambda g=g: V.max(out=top8_.t[:, g, :], in_=imp_.t[:, g, :]), reads=[imp_.b], writes=[top8_.bs[g]])
                            yield
                            for g in range(2):
                                kb.op(DVE, lambda g=g: V.tensor_scalar(out=qa.t[:, 4 * g:4 * g + 4, 64:96],
                                                                       in0=imp_.t[:, g:g + 1, :].broadcast_to([128, 4, 32]),
                                                                       scalar1=top8_.t[:, g, 7:8], scalar2=NEG, op0=ALU.is_lt, op1=ALU.mult),
                                      reads=[imp_.b, top8_.bs[g]], writes=[qa.bs[2 + g]])
                            yield

                        gw = branch(2, KT_win, V_win, list(range(max(0, c - 4), c + 1)), q1, pre=bool(PREWIN))
                        gi = imp_chain()
                        live = [gw, gi]
                        while live:
                            for g_ in list(live):
                                try:
                                    next(g_)
                                except StopIteration:
                                    live.remove(g_)
                            yield
                        for _ in range(XY3):
                            yield
                        transposes([qa.t[:, h, :] for h in range(8)], bZ, qa.bs)
                        if QDVE:
                            kb.op(DVE, lambda: V.tensor_copy(out=q2.t[:], in_=pb(bZ).rearrange("p (a b) -> p a b", b=128)), reads=[PS[bZ].b], writes=[q2.b])
                        else:
                            kb.op(ACT, lambda: A.copy(out=q2.t[:], in_=pb(bZ).rearrange("p (a b) -> p a b", b=128)), reads=[PS[bZ].b], writes=[q2.b])
                        yield
                        yield from branch(1, KT_slc, V_slc, list(range(0, c + 1)), q2)
                        rd_, coef_, oacc_, otmp_ = rd[i2], coef[i2], oacc[i2], otmp[i2]
                        kb.op(DVE, lambda: V.reciprocal(out=rd_.t[:], in_=de.t[:]), reads=[de.b], writes=[rd_.b])
                        yield
                        kb.op(DVE, lambda: V.tensor_tensor(out=coef_.t[:], in0=gt.t[:].rearrange("p (h b) -> p b h", b=3), in1=rd_.t[:], op=ALU.mult),
                              reads=[gt.b, rd_.b], writes=[coef_.b])
                        yield
                        kb.op(DVE, lambda: V.tensor_tensor(out=oacc_.t[:], in0=nu.t[:, 0], in1=coef_.t[:, 0, :].unsqueeze(2).broadcast_to([128, 8, 64]), op=ALU.mult),
                              reads=[nu.b, coef_.b], writes=[oacc_.b])
                        kb.op(POOL, lambda: G.tensor_tensor(out=otmp_.t[:], in0=nu.t[:, 1], in1=coef_.t[:, 1, :].unsqueeze(2).broadcast_to([128, 8, 64]), op=ALU.mult),
                              reads=[nu.b, coef_.b], writes=[otmp_.b])
                        yield
                        kb.op(DVE, lambda: V.tensor_tensor(out=oacc_.t[:], in0=oacc_.t[:], in1=otmp_.t[:], op=ALU.add), reads=[oacc_.b, otmp_.b], writes=[oacc_.b])
                        yield
                        kb.op(POOL, lambda: G.tensor_tensor(out=otmp_.t[:], in0=nu.t[:, 2], in1=coef_.t[:, 2, :].unsqueeze(2).broadcast_to([128, 8, 64]), op=ALU.mult),
                              reads=[nu.b, coef_.b], writes=[otmp_.b])
                        yield
                        yt = ytok[i2]
                        kb.op(DVE, lambda: V.tensor_tensor(out=yt.t[:], in0=oacc_.t[:].rearrange("p a b -> p (a b)"), in1=otmp_.t[:].rearrange("p a b -> p (a b)"), op=ALU.add),
                              reads=[oacc_.b, otmp_.b], writes=[yt.b])
                        if ydbg is not None:
                            kb.op(POOL, lambda: G.tensor_tensor(out=ydbg.t[:], in0=oacc_.t[:].rearrange("p a b -> p (a b)"), in1=otmp_.t[:].rearrange("p a b -> p (a b)"), op=ALU.add),
                                  reads=[oacc_.b, otmp_.b], writes=[ydbg.b])
                            dbg_store("y_nsa", ydbg.t[:], tok, [ydbg.b])
                        yield
                        for _ in range(XY3):
                            yield
                        transposes([yt.t[:, k * 128:(k + 1) * 128] for k in range(4)], bZ, [yt.b])
                        yield
                        kb.op(ACT, lambda: A.copy(out=ynsaT.t[:, :, tok], in_=pb(bZ)[:, 0:512].rearrange("p (a b) -> p a b", b=128)),
                              reads=[PS[bZ].b], writes=[ynsaT.bs[c]])
                        yield

                    run_interleaved([(lambda c=c: tile_gen(c)) for c in range(NT)], width=WIDTH)

            def phase_1d():
                with ExitStack() as s4:
                    wr = sb(s4, "wr", [128, KC, 2048], BF16, 4)
                    for j in range(4):
                        kb.dma(POOL, wr.t[:, :, j * 512:(j + 1) * 512], w_in_v[:, :, C_R + j * 512:C_R + (j + 1) * 512], writes=[wr.bs[j]])
                    for j in range(4):
                        kb.dma(POOL, wm.t[:, :, j * 512:(j + 1) * 512], w_in_v[:, :, C_M + j * 512:C_M + (j + 1) * 512], writes=[wm.bs[j]])
                    load_w(wbr, w_branch[0].rearrange("n (k p) d -> p (n k) d", p=128))
                    idT = sb(s4, "idT", [128, 4, 128], F32)
                    qdec = sb(s4, "qdec", [128, 4, 128], F32)
                    kdec = sb(s4, "kdec", [128, 4, 128], F32)
                    gn = sb(s4, "gn", [128, 512], F32)
                    eij = sb(s4, "eij", [128, 128], F32)
                    rowq = sb(s4, "rowq", [128, 128], F32)
                    rowk = sb(s4, "rowk", [128, 128], F32)
                    kb.dma(SP, gn.t[:], ret_gn_g.rearrange("o a b -> o (a b)").broadcast_to([128, 512]), writes=[gn.b])
                    kb.op(POOL, lambda: G.iota(eij.t[:], pattern=[[1, 128]], base=0, channel_multiplier=-1, allow_small_or_imprecise_dtypes=True), writes=[eij.b])
                    kb.op(POOL, lambda: G.iota(rowq.t[:], pattern=[[1, 128]], base=1, channel_multiplier=0, allow_small_or_imprecise_dtypes=True), writes=[rowq.b])
                    kb.op(POOL, lambda: G.iota(rowk.t[:], pattern=[[-1, 128]], base=127, channel_multiplier=0, allow_small_or_imprecise_dtypes=True), writes=[rowk.b])
                    lgs = [float(np.log(1.0 - 2.0 ** (-5.0 - h))) for h in range(4)]
                    cds = [float(np.exp(128.0 * np.float32(lg))) for lg in lgs]
                    for h in range(4):
                        kb.op(ACT, lambda h=h: A.activation(out=idT.t[:, h, :], in_=eij.t[:], func=AF.Exp, scale=lgs[h]), reads=[eij.b], writes=[idT.b])
                        kb.op(POOL, lambda h=h: G.affine_select(out=idT.t[:, h, :], in_=idT.t[:, h, :], pattern=[[1, 128]], compare_op=ALU.is_ge, fill=0.0,
                                                                base=0, channel_multiplier=-1), reads=[idT.b], writes=[idT.b])
                        kb.op(ACT, lambda h=h: A.activation(out=qdec.t[:, h, :], in_=rowq.t[:], func=AF.Exp, scale=lgs[h]), reads=[rowq.b], writes=[qdec.b])
                        kb.op(ACT, lambda h=h: A.activation(out=kdec.t[:, h, :], in_=rowk.t[:], func=AF.Exp, scale=lgs[h]), reads=[rowk.b], writes=[kdec.b])
                    qTr = sb(s4, "qTr", [128, 4, 512], BF16)
                    qdT = sb(s4, "qdT", [128, 4, 512], BF16)
                    kTr = sb(s4, "kTr", [128, 4, 512], BF16)
                    kdT = sb(s4, "kdT", [128, 4, 512], BF16)
                    v_sb = [sb(s4, f"v_sb{i}", [128, 4, 128], BF16) for i in range(2)]
                    sgl = [sb(s4, f"sgl{i}", [128, 512], F32) for i in range(2)]
                    kd = [sb(s4, f"kd{i}", [128, 4, 128], BF16) for i in range(2)]
                    attb = [sb(s4, f"attb{i}", [128, 4, 128], BF16) for i in range(2)]
                    state_f = sb(s4, "state_f", [128, 4, 128], F32, 4)
                    state_b = sb(s4, "state_b", [128, 4, 128], BF16)
                    yr = [sb(s4, f"yr{i}", [128, 512], BF16) for i in range(2)]
                    yrdbg = sb(s4, "yrdbg", [128, 512], F32) if "y_ret" in dbg else None
                    kb.op(POOL, lambda: G.memset(state_f.t[:], 0.0), writes=state_f.bs)
                    kb.op(POOL, lambda: G.memset(state_b.t[:], 0.0), writes=[state_b.b])
                    KS = float(128.0 ** -0.5)
                    pcnt = [0]
                    state_done = [False] * (NT + 1)
                    stt_done = [False] * (NT + 1)
                    bst = [sb(s4, f"bst{i}", [128, 4, 6], F32, 4) for i in range(2)]
                    mv = [sb(s4, f"mv{i}", [128, 4, 2], F32, 4) for i in range(2)]
                    rs4 = [sb(s4, f"rs4{i}", [128, 12], F32) for i in range(2)]
                    on = [sb(s4, f"on{i}", [128, 512], F32, 4) for i in range(2)]

                    def p1d_gen(c):
                        i2 = c % 2
                        cl = c % 4
                        bA, bB, bT = 2 + 3 * i2, 3 + 3 * i2, 4 + 3 * i2
                        tok = slice(c * 128, (c + 1) * 128)
                        cs = slice(cl * 128, (cl + 1) * 128)
                        bst_, mv_, rs4_, on_ = bst[i2], mv[i2], rs4[i2], on[i2]
                        for k in range(KC):
                            kb.op(PE, lambda k=k: TE.matmul(pf(bA), lhsT=hT.t[:, k, tok], rhs=wr.t[:, k, 1024:1536], start=(k == 0), stop=(k == KC - 1)),
                                  reads=[hT.bs[c], wr.bs[2]], writes=[PS[bA].b], inc=(k == KC - 1))
                        yield
                        for k in range(KC):
                            kb.op(PE, lambda k=k: TE.matmul(pf(bB), lhsT=hT.t[:, k, tok], rhs=wr.t[:, k, 1536:2048], start=(k == 0), stop=(k == KC - 1)),
                                  reads=[hT.bs[c], wr.bs[3]], writes=[PS[bB].b], inc=(k == KC - 1))
                        yield
                        vs, sg_, kd_, ab = v_sb[i2], sgl[i2], kd[i2], attb[i2]
                        kb.op(ACT, lambda: A.copy(out=vs.t[:].rearrange("p a b -> p (a b)"), in_=pf(bA)), reads=[PS[bA].b], writes=[vs.b])
                        yield
                        kb.op(ACT, lambda: A.activation(out=sg_.t[:], in_=pf(bB), func=AF.Silu), reads=[PS[bB].b], writes=[sg_.b])
                        yield
                        for _ in range(XY3):
                            yield
                        transposes([kdT.t[:, h, cs] for h in range(4)], bT, [kdT.b])
                        yield
                        kb.op(ACT, lambda: A.copy(out=kd_.t[:].rearrange("p a b -> p (a b)"), in_=pb(bT)[:, 0:512]), reads=[PS[bT].b], writes=[kd_.b])
                        yield
                        for h in range(4):
                            kb.op(PE, lambda h=h: TE.matmul(pf(bA)[:, h * 128:(h + 1) * 128], lhsT=kTr.t[:, h, cs], rhs=qTr.t[:, h, cs], start=True, stop=True),
                                  reads=[kTr.b, qTr.b], writes=[PS[bA].b], inc=(h == 3))
                        yield
                        kb.op(DVE, lambda: V.tensor_tensor(out=ab.t[:], in0=pf(bA).rearrange("p (a b) -> p a b", b=128), in1=idT.t[:], op=ALU.mult),
                              reads=[PS[bA].b, idT.b], writes=[ab.b])
                        yield
                        if c < NT - 1:
                            for h in range(4):
                                kb.op(PE, lambda h=h: TE.matmul(pf(bT)[:, h * 128:(h + 1) * 128], lhsT=kd_.t[:, h, :], rhs=vs.t[:, h, :], start=True, stop=True),
                                      reads=[kd_.b, vs.b], writes=[PS[bT].b], inc=(h == 3))
                            yield
                            while c > 0 and not state_done[c - 1]:
                                yield
                            for h in range(4):
                                kb.op(DVE, lambda h=h: V.scalar_tensor_tensor(out=state_f.t[:, h, :], in0=state_f.t[:, h, :], scalar=cds[h],
                                                                              in1=pf(bT)[:, h * 128:(h + 1) * 128], op0=ALU.mult, op1=ALU.add),
                                      reads=[state_f.bs[h], PS[bT].b], writes=[state_f.bs[h]])
                        stt_done[c] = True
                        yield
                        while c > 0 and not state_done[c - 1]:
                            yield
                        for h in range(4):
                            kb.op(PE, lambda h=h: TE.matmul(pf(bB)[:, h * 128:(h + 1) * 128], lhsT=ab.t[:, h, :], rhs=vs.t[:, h, :], start=True, stop=(c == 0)),
                                  reads=[ab.b, vs.b], writes=[PS[bB].b], inc=(c == 0 and h == 3))
                            if c > 0:
                                kb.op(PE, lambda h=h: TE.matmul(pf(bB)[:, h * 128:(h + 1) * 128], lhsT=qdT.t[:, h, cs], rhs=state_b.t[:, h, :], start=False, stop=True),
                                      reads=[qdT.b, state_b.b], writes=[PS[bB].b], inc=(h == 3))
                        yield
                        if c < NT - 1:
                            kb.op(POOL, lambda: G.tensor_copy(out=state_b.t[:], in_=state_f.t[:]), reads=state_f.bs, writes=[state_b.b])
                        state_done[c] = True
                        yield
                        for h in range(4):
                            kb.op(DVE, lambda h=h: V.bn_stats(out=bst_.t[:, h, :], in_=pf(bB)[:, h * 128:(h + 1) * 128]), reads=[PS[bB].b], writes=[bst_.bs[h]])
                        yield
                        for h in range(4):
                            kb.op(DVE, lambda h=h: V.bn_aggr(out=mv_.t[:, h, :], in_=bst_.t[:, h, :]), reads=[bst_.bs[h]], writes=[mv_.bs[h]])
                        yield
                        kb.op(DVE, lambda: V.tensor_scalar(out=rs4_.t[:, 0:4], in0=mv_.t[:, :, 1], scalar1=EPS, scalar2=None, op0=ALU.add), reads=mv_.bs, writes=[rs4_.b])
                        yield
                        kb.op(POOL, lambda: G.tensor_tensor(out=rs4_.t[:, 8:12], in0=rs4_.t[:, 0:4], in1=nhalf.t[:, 0:4], op=ALU.pow),
                              reads=[rs4_.b, nhalf.b], writes=[rs4_.b])
                        yield
                        for h in range(4):
                            kb.op(DVE, lambda h=h: V.tensor_scalar(out=on_.t[:, h * 128:(h + 1) * 128], in0=pf(bB)[:, h * 128:(h + 1) * 128],
                                                                   scalar1=mv_.t[:, h, 0:1], scalar2=rs4_.t[:, 8 + h:9 + h], op0=ALU.subtract, op1=ALU.mult),
                                  reads=[PS[bB].b, mv_.bs[h], rs4_.b], writes=[on_.bs[h]])
                        yield
                        kb.op(POOL, lambda: G.tensor_tensor(out=on_.t[:], in0=on_.t[:], in1=gn.t[:], op=ALU.mult), reads=on_.bs + [gn.b], writes=on_.bs)
                        yield
                        y_ = yr[i2]
                        kb.op(DVE, lambda: V.tensor_tensor(out=y_.t[:], in0=on_.t[:], in1=sg_.t[:], op=ALU.mult), reads=on_.bs + [sg_.b], writes=[y_.b])
                        if yrdbg is not None:
                            kb.op(DVE, lambda: V.tensor_tensor(out=yrdbg.t[:], in0=on_.t[:], in1=sg_.t[:], op=ALU.mult), reads=on_.bs + [sg_.b], writes=[yrdbg.b])
                            dbg_store("y_ret", yrdbg.t[:], tok, [yrdbg.b])
                        yield
                        for _ in range(XY3):
                            yield
                        transposes([y_.t[:, k * 128:(k + 1) * 128] for k in range(4)], bT, [y_.b])
                        yield
                        kb.op(ACT, lambda: A.copy(out=yretT.t[:, :, tok], in_=pb(bT)[:, 0:512].rearrange("p (a b) -> p a b", b=128)),
                              reads=[PS[bT].b], writes=[yretT.bs[c]])
                        yield

                    for tg in range(4):
                        tks = slice(tg * 512, (tg + 1) * 512)
                        hbs = [hT.bs[4 * tg + i] for i in range(4)]
                        for qk in range(2):
                            for h in range(4):
                                bk = pcnt[0] % 2
                                pcnt[0] += 1
                                for k in range(KC):
                                    kb.op(PE, lambda k=k, qk=qk, h=h, bk=bk: TE.matmul(pf(bk), lhsT=wr.t[:, k, qk * 512 + h * 128:qk * 512 + (h + 1) * 128],
                                                                                      rhs=hT.t[:, k, tks], start=(k == 0), stop=(k == KC - 1)),
                                          reads=hbs + [wr.bs[qk]], writes=[PS[bk].b], inc=(k == KC - 1))
                                pv4 = pf(bk).rearrange("p (a b) -> p a b", b=128)
                                if qk == 0:
                                    kb.op(ACT, lambda h=h, bk=bk: A.copy(out=qTr.t[:, h, :], in_=pf(bk)), reads=[PS[bk].b], writes=[qTr.b])
                                    kb.op(DVE, lambda h=h, pv4=pv4: V.tensor_tensor(out=qdT.t[:, h, :].rearrange("p (a b) -> p a b", b=128), in0=pv4,
                                                                                    in1=qdec.t[:, h:h + 1, :].broadcast_to([128, 4, 128]), op=ALU.mult),
                                          reads=[PS[bk].b, qdec.b], writes=[qdT.b])
                                else:
                                    kb.op(ACT, lambda h=h, bk=bk: A.mul(out=kTr.t[:, h, :], in_=pf(bk), mul=KS), reads=[PS[bk].b], writes=[kTr.b])
                                    kb.op(DVE, lambda h=h, pv4=pv4: V.scalar_tensor_tensor(out=kdT.t[:, h, :].rearrange("p (a b) -> p a b", b=128), in0=pv4, scalar=KS,
                                                                                           in1=kdec.t[:, h:h + 1, :].broadcast_to([128, 4, 128]),
                                                                                           op0=ALU.mult, op1=ALU.mult),
                                          reads=[PS[bk].b, kdec.b], writes=[kdT.b])
                        run_interleaved([(lambda c=c: p1d_gen(c)) for c in range(4 * tg, 4 * tg + 4)], width=2)

            def phase_1e():
                with ExitStack() as s5:
                    xt = [sb(s5, f"xte{i}", [128, D], F32) for i in range(2)]
                    g2bc = sb(s5, "g2bc", [128, D], F32)
                    kb.dma(SP, g2bc.t[:], norm2_g[0:1, :].broadcast_to([128, D]), writes=[g2bc.b])
                    wo = sb(s5, "wo", [128, KC, D], BF16)
                    load_w(wo, w_out[0].rearrange("(k p) d -> p k d", p=128))
                    gates = [sb(s5, f"gates{i}", [128, D], F32, 2) for i in range(2)]
                    tmix = [sb(s5, f"tmix{i}", [128, D], F32, 2) for i in range(2)]
                    tmix2 = [sb(s5, f"tmix2{i}", [128, D], F32, 2) for i in range(2)]
                    mixed = [sb(s5, f"mixed{i}", [128, D], BF16) for i in range(2)]
                    mixT = [sb(s5, f"mixT{i}", [128, KC, 128], BF16) for i in range(2)]
                    x1t = [sb(s5, f"x1t{i}", [128, D], F32, 2) for i in range(2)]

                    def p1e_gen(c):
                        i2 = c % 2
                        bs_ = [4 * i2 + i for i in range(4)]
                        tok = slice(c * 128, (c + 1) * 128)
                        xx = xt[i2]
                        kb.dma(SP, xx.t[:], x[tok, :], writes=[xx.b])
                        gt_, tm = gates[i2], (tmix[i2], tmix2[i2])
                        for n, yT in ((0, ynsaT), (1, yretT)):
                            for half in range(2):
                                j = 2 * n + half
                                for k in range(KC):
                                    kb.op(PE, lambda j=j, k=k, half=half: TE.matmul(pf(bs_[half]), lhsT=hT.t[:, k, tok], rhs=wm.t[:, k, j * 512:(j + 1) * 512],
                                                                                    start=(k == 0), stop=(k == KC - 1)),
                                          reads=[hT.bs[c], wm.bs[j]], writes=[PS[bs_[half]].b], inc=(k == KC - 1))
                                yield
                            for half in range(2):
                                bk = bs_[2 + half]
                                for k in range(4):
                                    kb.op(PE, lambda n=n, half=half, k=k, bk=bk, yT=yT: TE.matmul(pf(bk), lhsT=yT.t[:, k, tok], rhs=wbr.t[:, n * 4 + k, half * 512:(half + 1) * 512],
                                                                                                  start=(k == 0), stop=(k == 3)),
                                          reads=[yT.bs[c], wbr.b], writes=[PS[bk].b], inc=(k == 3))
                                yield
                            for half in range(2):
                                kb.op(ACT, lambda half=half: A.activation(out=gt_.t[:, half * 512:(half + 1) * 512], in_=pf(bs_[half]), func=AF.Sigmoid),
                                      reads=[PS[bs_[half]].b], writes=[gt_.bs[half]])
                                yield
                            for half in range(2):
                                hs = slice(half * 512, (half + 1) * 512)
                                kb.op(DVE, lambda half=half, hs=hs, n=n: V.tensor_tensor(out=tm[n].t[:, hs], in0=gt_.t[:, hs], in1=pf(bs_[2 + half]), op=ALU.mult),
                                      reads=[gt_.bs[half], PS[bs_[2 + half]].b], writes=[tm[n].bs[half]])
                                yield
                        mx = mixed[i2]
                        kb.op(POOL, lambda: G.tensor_tensor(out=mx.t[:], in0=tm[0].t[:], in1=tm[1].t[:], op=ALU.add), reads=tm[0].bs + tm[1].bs, writes=[mx.b])
                        yield
                        for _ in range(XY3):
                            yield
                        transposes([mx.t[:, k * 128:(k + 1) * 128] for k in range(KC)], bs_[0], [mx.b])
                        yield
                        mt = mixT[i2]
                        kb.op(ACT, lambda: A.copy(out=mt.t[:], in_=pb(bs_[0]).rearrange("p (a b) -> p a b", b=128)), reads=[PS[bs_[0]].b], writes=[mt.b])
                        yield
                        for half in range(2):
                            for k in range(KC):
                                kb.op(PE, lambda half=half, k=k: TE.matmul(pf(bs_[2 + half]), lhsT=mt.t[:, k, :], rhs=wo.t[:, k, half * 512:(half + 1) * 512],
                                                                           start=(k == 0), stop=(k == KC - 1)),
                                      reads=[mt.b, wo.b], writes=[PS[bs_[2 + half]].b], inc=(k == KC - 1))
                            yield
                        x1 = x1t[i2]
                        for half in range(2):
                            hs = slice(half * 512, (half + 1) * 512)
                            kb.op(DVE, lambda half=half, hs=hs: V.tensor_tensor(out=x1.t[:, hs], in0=xx.t[:, hs], in1=pf(bs_[2 + half]), op=ALU.add),
                                  reads=[xx.b, PS[bs_[2 + half]].b], writes=[x1.bs[half]])
                            yield
                        kb.dma(SP, xmid[tok, :], x1.t[:], reads=x1.bs)
                        dbg_store("x1", x1.t[:], tok, x1.bs)
                        yield from norm_gen(x1, c, g2bc, i2, bs_[1])

                    run_interleaved([(lambda c=c: p1e_gen(c)) for c in range(NT)], width=2)

            def phase_2():
                with ExitStack() as s6:
                    xt = [sb(s6, f"xtf{i}", [128, D], F32) for i in range(2)]
                    wd = sb(s6, "wd", [128, NFB, D], BF16, 2)
                    wd_v = ffn_w_down[0].rearrange("(fb p) d -> p fb d", p=128)
                    wgs = [sb(s6, f"wgs{i}", [128, KC, 256], BF16) for i in range(2)]
                    wus = [sb(s6, f"wus{i}", [128, KC, 256], BF16) for i in range(2)]
                    act = sb(s6, "act", [128, NFB, 1024], BF16, NFB)
                    sgs = [sb(s6, f"sgs{i}", [128, 512], F32) for i in range(2)]
                    outt = [sb(s6, f"outt{i}", [128, D], F32, 2) for i in range(2)]
                    wg_v = ffn_w_gate[0].rearrange("(k p) f -> p k f", p=128)
                    wu_v = ffn_w_up[0].rearrange("(k p) f -> p k f", p=128)
                    cn = [0, 0]
                    for hf in range(2):
                        for fg in range(11):
                            cols = slice(fg * 256, (fg + 1) * 256)
                            wg_, wu_ = wgs[fg % 2], wus[fg % 2]
                            kb.dma(POOL, wg_.t[:], wg_v[:, :, cols], writes=[wg_.b])
                            kb.dma(POOL, wu_.t[:], wu_v[:, :, cols], writes=[wu_.b])
                            if hf == 0 and fg == 1:
                                kb.dma(POOL, wd.t[:, 0:11, :], wd_v[:, 0:11, :], writes=[wd.bs[0]])
                                kb.dma(POOL, wd.t[:, 11:22, :], wd_v[:, 11:22, :], writes=[wd.bs[1]])
                            for fl in range(2):
                                fb = fg * 2 + fl
                                for t2 in range(2):
                                    tokc = slice(hf * 1024 + t2 * 512, hf * 1024 + (t2 + 1) * 512)
                                    hbs = [hT.bs[hf * 8 + t2 * 4 + i] for i in range(4)]
                                    gb, ub = (0, 1) if cn[0] % 2 == 0 else (2, 3)
                                    cn[0] += 1
                                    for k in range(KC):
                                        kb.op(PE, lambda k=k, fl=fl, gb=gb, wg_=wg_, tokc=tokc: TE.matmul(pf(gb), lhsT=wg_.t[:, k, fl * 128:(fl + 1) * 128], rhs=hT.t[:, k, tokc],
                                                                                                          start=(k == 0), stop=(k == KC - 1)),
                                              reads=hbs + [wg_.b], writes=[PS[gb].b], inc=(k == KC - 1))
                                    for k in range(KC):
                                        kb.op(PE, lambda k=k, fl=fl, ub=ub, wu_=wu_, tokc=tokc: TE.matmul(pf(ub), lhsT=wu_.t[:, k, fl * 128:(fl + 1) * 128], rhs=hT.t[:, k, tokc],
                                                                                                          start=(k == 0), stop=(k == KC - 1)),
                                              reads=hbs + [wu_.b], writes=[PS[ub].b], inc=(k == KC - 1))
                                    sg_ = sgs[cn[0] % 2]
                                    kb.op(ACT, lambda sg_=sg_, gb=gb: A.activation(out=sg_.t[:], in_=pf(gb), func=AF.Silu), reads=[PS[gb].b], writes=[sg_.b])
                                    kb.op(DVE, lambda sg_=sg_, ub=ub, fb=fb, t2=t2: V.tensor_tensor(out=act.t[:, fb, t2 * 512:(t2 + 1) * 512], in0=sg_.t[:], in1=pf(ub), op=ALU.mult),
                                          reads=[sg_.b, PS[ub].b], writes=[act.bs[fb]])
                        for tl in range(8):
                            c = hf * 8 + tl
                            tok = slice(c * 128, (c + 1) * 128)
                            xx = xt[c % 2]
                            kb.dma(SP, xx.t[:], xmid[tok, :], writes=[xx.b])
                            ob = (4, 5) if cn[1] % 2 == 0 else (6, 7)
                            cn[1] += 1
                            for half in range(2):
                                for fb in range(NFB):
                                    kb.op(PE, lambda half=half, fb=fb, ob=ob, tl=tl: TE.matmul(pf(ob[half]), lhsT=act.t[:, fb, tl * 128:(tl + 1) * 128],
                                                                                               rhs=wd.t[:, fb, half * 512:(half + 1) * 512],
                                                                                               start=(fb == 0), stop=(fb == NFB - 1)),
                                          reads=[act.bs[fb], wd.bs[0 if fb < 11 else 1]], writes=[PS[ob[half]].b], inc=(fb == NFB - 1))
                            ot = outt[c % 2]
                            for half in range(2):
                                hs = slice(half * 512, (half + 1) * 512)
                                kb.op(DVE, lambda half=half, hs=hs, ob=ob, ot=ot, xx=xx: V.tensor_tensor(out=ot.t[:, hs], in0=xx.t[:, hs], in1=pf(ob[half]), op=ALU.add),
                                      reads=[xx.b, PS[ob[half]].b], writes=[ot.bs[half]])
                            kb.dma(SP, out[tok, :], ot.t[:], reads=ot.bs, is_out=True)

            with ExitStack() as sB:
                ynsaT = sb(sB, "ynsaT", [128, 4, S], BF16, NT)
                yretT = sb(sB, "yretT", [128, 4, S], BF16, NT)

                with ExitStack() as sA:
                    gq = sb(sA, "gq", [128, 64], F32)
                    gk = sb(sA, "gk", [128, 3, 64], F32)
                    QAL = sb(sA, "QAL", [128, NT, 8, 4], BF16)
                    KAL = sb(sA, "KAL", [128, NT, 4], BF16)
                    KCAL = sb(sA, "KCAL", [128, 4], BF16)
                    OH = sb(sA, "OH", [128, NT, 32], BF16)
                    dmask8 = sb(sA, "dmask8", [128, 8, 128], BF16)
                    tmask8 = sb(sA, "tmask8", [128, 8, 128], BF16)
                    cmask = sb(sA, "cmask", [128, S], BF16)
                    addc = sb(sA, "addc", [128, NT, 32], F32)
                    ov = sb(sA, "ov", [128, 32], BF16)
                    KT_slc = sb(sA, "KT_slc", [128, 2, S], BF16, NT)
                    KT_win = sb(sA, "KT_win", [128, 2, S], BF16, NT)
                    V_slc = sb(sA, "V_slc", [128, NT, 2, 65], BF16, NT)
                    V_win = sb(sA, "V_win", [128, NT, 2, 65], BF16, NT)
                    KcT = sb(sA, "KcT", [128, 2, 128], BF16)
                    Vc = sb(sA, "Vc", [128, 2, 97], BF16)

                    with ExitStack() as s0:
                        SL = sb(s0, "SL", [128, 8], F32)
                        th128 = sb(s0, "th128", [128, NT], F32)
                        pidx = sb(s0, "pidx", [128, 1], F32)
                        QALf = sb(s0, "QALf", [128, NT, 8, 4], F32)
                        KALf = sb(s0, "KALf", [128, NT, 4], F32)
                        KCALf = sb(s0, "KCALf", [128, 4], F32)
                        rel = sb(s0, "rel", [128, NT, 32], F32)
                        f0 = sb(s0, "f0", [128, NT, 32], F32)
                        f1 = sb(s0, "f1", [128, NT, 32], F32)
                        t1 = sb(s0, "t1", [128, NT, 32], F32)
                        hp = sb(s0, "hp", [128, 1], F32)
                        ovf = sb(s0, "ovf", [128, 32], F32)
                        ova = sb(s0, "ova", [128, 32], F32)
                        ones_b = sb(s0, "ones_b", [128, 512], BF16)
                        ones_b2 = sb(s0, "ones_b2", [128, 1024], BF16)
                        zeros_b = sb(s0, "zeros_b", [128, 512], BF16)

                        kb.dma(SP, gq.t[:], nsa_q_norm[0:1, :].broadcast_to([128, 64]), writes=[gq.b])
                        kb.dma(SP, gk.t[:].rearrange("p a b -> p (a b)"),
                               nsa_k_norm.rearrange("o a b -> o (a b)").broadcast_to([128, 192]), writes=[gk.b])
                        kb.op(DVE, lambda: V.tensor_scalar(out=gq.t[:], in0=gq.t[:], scalar1=0.125, scalar2=None, op0=ALU.mult),
                              reads=[gq.b], writes=[gq.b])
                        for h in range(8):
                            kb.op(POOL, lambda h=h: G.memset(SL.t[:, h:h + 1], 2.0 ** (-(h + 1))), writes=[SL.b])
                        kb.op(POOL, lambda: G.iota(th128.t[:], pattern=[[128, NT]], base=0, channel_multiplier=0,
                                                   allow_small_or_imprecise_dtypes=True), writes=[th128.b])
                        kb.op(POOL, lambda: G.iota(pidx.t[:], pattern=[[0, 1]], base=0, channel_multiplier=1,
                                                   allow_small_or_imprecise_dtypes=True), writes=[pidx.b])
                        SLb = SL.t[:].unsqueeze(1).broadcast_to([128, NT, 8])
                        THb = th128.t[:].unsqueeze(2).broadcast_to([128, NT, 8])
                        kb.op(DVE, lambda: V.scalar_tensor_tensor(out=QALf.t[:, :, :, 0], in0=THb, scalar=-1.0, in1=SLb,
                                                                  op0=ALU.mult, op1=ALU.mult),
                              reads=[SL.b, th128.b], writes=[QALf.b])
                        kb.op(DVE, lambda: V.tensor_scalar(out=QALf.t[:, :, :, 1], in0=SLb, scalar1=pidx.t[:, 0:1], scalar2=-1.0,
                                                           op0=ALU.mult, op1=ALU.mult),
                              reads=[SL.b, pidx.b], writes=[QALf.b])
                        kb.op(DVE, lambda: V.tensor_copy(out=QALf.t[:, :, :, 2], in_=SLb), reads=[SL.b], writes=[QALf.b])
                        kb.op(DVE, lambda: V.tensor_copy(out=QALf.t[:, :, :, 3], in_=SLb), reads=[SL.b], writes=[QALf.b])
                        kb.op(DVE, lambda: V.tensor_copy(out=QAL.t[:], in_=QALf.t[:]), reads=[QALf.b], writes=[QAL.b])
                        kb.op(POOL, lambda: G.memset(KALf.t[:, :, 0:2], 1.0), writes=[KALf.b])
                        kb.op(DVE, lambda: V.tensor_copy(out=KALf.t[:, :, 2], in_=th128.t[:]), reads=[th128.b], writes=[KALf.b])
                        kb.op(DVE, lambda: V.tensor_copy(out=KALf.t[:, :, 3], in_=pidx.t[:, 0:1].broadcast_to([128, NT])),
                              reads=[pidx.b], writes=[KALf.b])
                        kb.op(DVE, lambda: V.tensor_copy(out=KAL.t[:], in_=KALf.t[:]), reads=[KALf.b], writes=[KAL.b])
                        kb.op(POOL, lambda: G.memset(KCALf.t[:, 0:2], 1.0), writes=[KCALf.b])
                        kb.op(POOL, lambda: G.memset(KCALf.t[:, 3:4], 31.0), reads=[], writes=[KCALf.b])
                        kb.op(DVE, lambda: V.tensor_scalar(out=KCALf.t[:, 2:3], in0=pidx.t[:, 0:1], scalar1=16.0, scalar2=None,
                                                           op0=ALU.mult), reads=[pidx.b], writes=[KCALf.b])
                        kb.op(DVE, lambda: V.tensor_copy(out=KCAL.t[:], in_=KCALf.t[:]), reads=[KCALf.b], writes=[KCAL.b])
                        kb.op(POOL, lambda: G.memset(OH.t[:], 0.0), writes=[OH.b])
                        for kt in range(NT):
                            kb.op(POOL, lambda kt=kt: G.memset(OH.t[0:64, kt, 2 * kt:2 * kt + 1], 1.0), writes=[OH.b])
                            kb.op(POOL, lambda kt=kt: G.memset(OH.t[64:128, kt, 2 * kt + 1:2 * kt + 2], 1.0), writes=[OH.b])
                        kb.op(POOL, lambda: G.memset(ones_b.t[:], 1.0), writes=[ones_b.b])
                        kb.op(POOL, lambda: G.memset(zeros_b.t[:], 0.0), writes=[zeros_b.b])
                        ob8 = ones_b2.t[:].rearrange("p (a b) -> p a b", b=128)
                        kb.op(POOL, lambda: G.memset(ones_b2.t[:], 1.0), writes=[ones_b2.b])
                        kb.op(POOL, lambda: G.affine_select(out=dmask8.t[:], in_=ob8, pattern=[[0, 8], [1, 128]],
                                                            compare_op=ALU.is_ge, fill=0.0, base=0, channel_multiplier=-1),
                              reads=[ones_b2.b], writes=[dmask8.b])
                        kb.op(POOL, lambda: G.affine_select(out=tmask8.t[:], in_=ob8, pattern=[[0, 8], [-1, 128]],
                                                            compare_op=ALU.is_gt, fill=0.0, base=0, channel_multiplier=1),
                              reads=[ones_b2.b], writes=[tmask8.b])
                        for i in range(4):
                            kb.op(POOL, lambda i=i: G.affine_select(out=cmask.t[:, i * 512:(i + 1) * 512], in_=zeros_b.t[:],
                                                                    pattern=[[1, 512]], compare_op=ALU.is_ge, fill=NEG,
                                                                    base=-31 + 512 * i, channel_multiplier=-16),
                                  reads=[zeros_b.b], writes=[cmask.b])
                        kb.op(POOL, lambda: G.iota(rel.t[:], pattern=[[-2, NT], [1, 32]], base=0, channel_multiplier=0,
                                                   allow_small_or_imprecise_dtypes=True), writes=[rel.b])
                        kb.op(DVE, lambda: V.tensor_scalar(out=hp.t[:], in0=pidx.t[:], scalar1=64.0, scalar2=None, op0=ALU.is_ge),
                              reads=[pidx.b], writes=[hp.b])
                        kb.op(DVE, lambda: V.tensor_scalar(out=rel.t[:], in0=rel.t[:], scalar1=hp.t[:, 0:1], scalar2=None,
                                                           op0=ALU.subtract), reads=[rel.b, hp.b], writes=[rel.b])
                        kb.op(DVE, lambda: V.tensor_scalar(out=t1.t[:], in0=rel.t[:], scalar1=0.0, scalar2=-1e9,
                                                           op0=ALU.is_gt, op1=ALU.mult), reads=[rel.b], writes=[t1.b])
                        kb.op(DVE, lambda: V.tensor_scalar(out=f0.t[:], in0=rel.t[:], scalar1=0.0, scalar2=None, op0=ALU.is_equal),
                              reads=[rel.b], writes=[f0.b])
                        kb.op(DVE, lambda: V.tensor_scalar(out=f1.t[:], in0=rel.t[:], scalar1=-1.0, scalar2=None, op0=ALU.is_equal),
                              reads=[rel.b], writes=[f1.b])
                        kb.op(DVE, lambda: V.tensor_tensor(out=f0.t[:], in0=f0.t[:], in1=f1.t[:], op=ALU.max),
                              reads=[f0.b, f1.b], writes=[f0.b])
                        kb.op(DVE, lambda: V.memset(f0.t[:, :, 0:1], 1.0), reads=[], writes=[f0.b])
                        kb.op(DVE, lambda: V.scalar_tensor_tensor(out=addc.t[:], in0=f0.t[:], scalar=1e4, in1=t1.t[:],
                                                                  op0=ALU.mult, op1=ALU.add), reads=[f0.b, t1.b], writes=[addc.b])
                        kb.op(POOL, lambda: G.iota(ovf.t[:], pattern=[[-64, 32]], base=0, channel_multiplier=16,
                                                   allow_small_or_imprecise_dtypes=True), writes=[ovf.b])
                        kb.op(DVE, lambda: V.tensor_scalar(out=ova.t[:], in0=ovf.t[:], scalar1=63.0, scalar2=None, op0=ALU.is_le),
                              reads=[ovf.b], writes=[ova.b])
                        kb.op(DVE, lambda: V.tensor_scalar(out=ovf.t[:], in0=ovf.t[:], scalar1=-31.0, scalar2=None, op0=ALU.is_ge),
                              reads=[ovf.b], writes=[ovf.b])
                        kb.op(DVE, lambda: V.tensor_tensor(out=ov.t[:], in0=ova.t[:], in1=ovf.t[:], op=ALU.mult),
                              reads=[ova.b, ovf.b], writes=[ov.b])
                        kb.op(POOL, lambda: G.memset(V_slc.t[:, :, :, 64:65], 1.0), writes=V_slc.bs)
                        kb.op(POOL, lambda: G.memset(V_win.t[:, :, :, 64:65], 1.0), writes=V_win.bs)
                        kb.barrier()
                        chk(1)

                    with ExitStack() as s2:
                        cmpT = sb(s2, "cmpT", [128, 2, S], BF16, NT)
                        w1sb = sb(s2, "w1sb", [128, 2, 32, 128], BF16)
                        w2sb = sb(s2, "w2sb", [128, 2, 64], BF16)
                        pe_sb = sb(s2, "pe_sb", [32, 2, 64], F32)
                        with ExitStack() as s2a:
                            xt = [sb(s2a, f"xta{i}", [128, D], F32) for i in range(2)]
                            g1bc = sb(s2a, "g1bc", [128, D], F32)
                            kb.dma(SP, g1bc.t[:], norm1_g[0:1, :].broadcast_to([128, D]), writes=[g1bc.b])

                            def p1a_gen(c):
                                xx = xt[c % 2]
                                kb.dma(SP, xx.t[:], x[c * 128:(c + 1) * 128, :], writes=[xx.b])
                                yield
                                yield from norm_gen(xx, c, g1bc, c % 2, 6 + (c % 2))

                            wkv = sb(s2a, "wkv", [128, KC, 768], BF16)
                            load_w(wkv, w_in_v[:, :, C_KV:C_KV + 768])
                            for kv in range(2):
                                src = cmp_w1[0, kv].rearrange("l d f -> d l f")
                                kb.dma(POOL, w1sb.t[0:64, kv], src, writes=[w1sb.b])
                                kb.dma(POOL, w1sb.t[64:128, kv], src, writes=[w1sb.b])
                            kb.dma(POOL, w2sb.t[:], cmp_w2[0].rearrange("k f d -> f k d"), writes=[w2sb.b])
                            kb.dma(SP, pe_sb.t[:], cmp_pe[0].rearrange("k l d -> l k d"), writes=[pe_sb.b])
                            cmp_tok = [sb(s2a, f"cmp_tok{i}", [128, 256], BF16) for i in range(2)]
                            sqk = [sb(s2a, f"sqk{i}", [128, 256], F32) for i in range(2)]
                            kst = sb(s2a, "kst", [128, NT, 16], F32, NT)
                            tmpk = [sb(s2a, f"tmpk{i}", [128, 4, 64], F32) for i in range(2)]
                            ka_slc = [sb(s2a, f"ka_slc{i}", [128, 2, 128], BF16, 2) for i in range(2)]
                            ka_win = [sb(s2a, f"ka_win{i}", [128, 2, 128], BF16, 2) for i in range(2)]
                            for i in range(2):
                                kb.op(POOL, lambda i=i: G.memset(ka_slc[i].t[:], 0.0), writes=ka_slc[i].bs)
                                kb.op(POOL, lambda i=i: G.memset(ka_win[i].t[:], 0.0), writes=ka_win[i].bs)
                            def p1b_gen(c):
                                i2 = c % 2
                                bA, bB = (0, 1) if i2 == 0 else (2, 3)
                                tok = slice(c * 128, (c + 1) * 128)
                                if CUT >= 1:
                                    yield
                                    for k in range(KC):
                                        kb.op(PE, lambda k=k: TE.matmul(pf(bA), lhsT=hT.t[:, k, tok], rhs=wkv.t[:, k, 0:512],
                                                                        start=(k == 0), stop=(k == KC - 1)),
                                              reads=[hT.bs[c], wkv.b], writes=[PS[bA].b], inc=(k == KC - 1))
                                    for k in range(KC):
                                        kb.op(PE, lambda k=k: TE.matmul(pf(bB)[:, 0:256], lhsT=hT.t[:, k, tok], rhs=wkv.t[:, k, 512:768],
                                                                        start=(k == 0), stop=(k == KC - 1)),
                                              reads=[hT.bs[c], wkv.b], writes=[PS[bB].b], inc=(k == KC - 1))
                                if CUT >= 2:
                                    yield
                                    ct = cmp_tok[i2]
                                    kb.op(ACT, lambda: A.copy(out=ct.t[:], in_=pf(bA)[:, 0:256]), reads=[PS[bA].b], writes=[ct.b])
                                    sq = sqk[i2]
                                    kb.op(ACT, lambda: A.activation(out=sq.t[:, 0:128], in_=pf(bA)[:, 256:384], func=AF.Square),
                                          reads=[PS[bA].b], writes=[sq.b])
                                    kb.op(ACT, lambda: A.activation(out=sq.t[:, 128:256], in_=pf(bB)[:, 0:128], func=AF.Square),
                                          reads=[PS[bB].b], writes=[sq.b])
                                if CUT >= 3:
                                    yield
                                    ks = kst.t
                                    ksb = [kst.bs[c]]
                                    kb.op(DVE, lambda: V.tensor_reduce(out=ks[:, c, 0:4], in_=sq.t[:].rearrange("p (a b) -> p a b", b=64),
                                                                       axis=AX.X, op=ALU.add), reads=[sq.b], writes=ksb)
                                    rstd_from_ss(ks[:, c, 0:4], ks[:, c, 4:8], ks[:, c, 8:12], ks[:, c, 12:16], 64, ksb)
                                    tk = tmpk[i2]
                                    kb.op(DVE, lambda: V.tensor_tensor(out=tk.t[:, 0:2, :], in0=pf(bA)[:, 256:384].rearrange("p (a b) -> p a b", b=64),
                                                                       in1=ks[:, c, 12:14].unsqueeze(2).broadcast_to([128, 2, 64]), op=ALU.mult),
                                          reads=[PS[bA].b] + ksb, writes=[tk.b])
                                    kb.op(DVE, lambda: V.tensor_tensor(out=tk.t[:, 2:4, :], in0=pf(bB)[:, 0:128].rearrange("p (a b) -> p a b", b=64),
                                                                       in1=ks[:, c, 14:16].unsqueeze(2).broadcast_to([128, 2, 64]), op=ALU.mult),
                                          reads=[PS[bB].b] + ksb, writes=[tk.b])
                                    ksl, kwn = ka_slc[i2], ka_win[i2]
                                    kb.op(DVE, lambda: V.tensor_tensor(out=ksl.t[:, :, 0:64], in0=tk.t[:, 0:2, :],
                                                                       in1=gk.t[:, 1:2, :].broadcast_to([128, 2, 64]), op=ALU.mult),
                                          reads=[tk.b, gk.b], writes=[ksl.bs[0]])
                                    kb.op(DVE, lambda: V.tensor_tensor(out=kwn.t[:, :, 0:64], in0=tk.t[:, 2:4, :],
                                                                       in1=gk.t[:, 2:3, :].broadcast_to([128, 2, 64]), op=ALU.mult),
                                          reads=[tk.b, gk.b], writes=[kwn.bs[0]])
                                if CUT >= 4:
                                    yield
                                    kb.op(POOL, lambda: G.tensor_copy(out=ksl.t[:, :, 64:96], in_=OH.t[:, c:c + 1, :].broadcast_to([128, 2, 32])),
                                          reads=[OH.b], writes=[ksl.bs[1]])
                                    kb.op(POOL, lambda: G.tensor_copy(out=ksl.t[:, :, 96:100], in_=KAL.t[:, c:c + 1, :].broadcast_to([128, 2, 4])),
                                          reads=[KAL.b], writes=[ksl.bs[1]])
                                    kb.op(POOL, lambda: G.tensor_copy(out=kwn.t[:, :, 96:100], in_=KAL.t[:, c:c + 1, :].broadcast_to([128, 2, 4])),
                                          reads=[KAL.b], writes=[kwn.bs[1]])
                                if CUT >= 5:
                                    yield
                                    kb.op(ACT, lambda: A.copy(out=V_slc.t[:, c, :, 0:64], in_=pf(bA)[:, 384:512].rearrange("p (a b) -> p a b", b=64)),
                                          reads=[PS[bA].b], writes=[V_slc.bs[c]])
                                    kb.op(ACT, lambda: A.copy(out=V_win.t[:, c, :, 0:64], in_=pf(bB)[:, 128:256].rearrange("p (a b) -> p a b", b=64)),
                                          reads=[PS[bB].b], writes=[V_win.bs[c]])
                                if CUT >= 6:
                                    yield
                                    tb = 4 + i2
                                    for _ in range(XY3):
                                        yield
                                    transposes([ksl.t[:, 0, :], ksl.t[:, 1, :], kwn.t[:, 0, :], kwn.t[:, 1, :], ct.t[:, 0:128], ct.t[:, 128:256]],
                                               tb, ksl.bs + kwn.bs + [ct.b])
                                    pv3 = pb(tb).rearrange("p (a b) -> p a b", b=128)
                                    kb.op(ACT, lambda: A.copy(out=KT_slc.t[:, :, tok], in_=pv3[:, 0:2, :]), reads=[PS[tb].b], writes=[KT_slc.bs[c]])
                                    kb.op(ACT, lambda: A.copy(out=KT_win.t[:, :, tok], in_=pv3[:, 2:4, :]), reads=[PS[tb].b], writes=[KT_win.bs[c]])
                                    kb.op(ACT, lambda: A.copy(out=cmpT.t[:, :, tok], in_=pv3[:, 4:6, :]), reads=[PS[tb].b], writes=[cmpT.bs[c]])
                            def p1ab_gen(c):
                                yield from p1a_gen(c)
                                yield from p1b_gen(c)

                            run_interleaved([(lambda c=c: p1ab_gen(c)) for c in range(NT)], width=2)
                            if "KT_slc" in dbg:
                                kdb = sb(s2a, "kdb", [128, 2, S], F32)
                                kb.op(DVE, lambda: V.tensor_copy(out=kdb.t[:], in_=KT_slc.t[:]), reads=KT_slc.bs, writes=[kdb.b])
                                kb.dma(SP, dbg["KT_slc"].rearrange("p (a b) -> p a b", b=S), kdb.t[:], reads=[kdb.b], is_out=True)
                            kb.barrier()
                            chk(3)

                        with ExitStack() as s2b:
                            peT = sb(s2b, "peT", [64, 2, 32], BF16)
                            bias_c = sb(s2b, "bias_c", [128, 2], F32)
                            xhs = [sb(s2b, f"xh{i}", [128, 128], F32) for i in range(4)]
                            x2s = [sb(s2b, f"x2{i}", [128, 128], F32) for i in range(4)]
                            sgs_ = [sb(s2b, f"sgc{i}", [128, 128], F32) for i in range(4)]
                            HTbs = [sb(s2b, f"HTb{i}", [128, 128], BF16) for i in range(4)]
                            kca = sb(s2b, "kca", [128, 2, 128], BF16)
                            cst = sb(s2b, "cst", [128, 8], F32)
                            tmpcs = [sb(s2b, f"tmpc{i}", [128, 64], F32) for i in range(4)]
                            kb.op(POOL, lambda: G.memset(kca.t[:], 0.0), writes=[kca.b])
                            kb.op(POOL, lambda: G.memset(Vc.t[:], 0.0), writes=[Vc.b])
                            kb.op(POOL, lambda: G.tensor_copy(out=kca.t[:, :, 96:100], in_=KCAL.t[:].unsqueeze(1).broadcast_to([128, 2, 4])),
                                  reads=[KCAL.b], writes=[kca.b])
                            kb.op(POOL, lambda: G.memset(Vc.t[:, :, 64:65], 1.0), writes=[Vc.b])
                            kb.op(POOL, lambda: G.tensor_copy(out=Vc.t[:, :, 65:97], in_=ov.t[:].unsqueeze(1).broadcast_to([128, 2, 32])),
                                  reads=[ov.b], writes=[Vc.b])
                            for kv in range(2):
                                kb.op(PE, lambda kv=kv: TE.transpose(out=pf(0)[0:64, kv * 32:(kv + 1) * 32], in_=pe_sb.t[0:32, kv, :],
                                                                     identity=ident_f.t[0:32, 0:32]),
                                      reads=[pe_sb.b, ident_f.b], writes=[PS[0].b])
                            kb.op(DVE, lambda: V.tensor_copy(out=peT.t[:], in_=pf(0)[0:64, 0:64].rearrange("p (a b) -> p a b", b=32)),
                                  reads=[PS[0].b], writes=[peT.b])
                            for kv in range(2):
                                for l in range(32):
                                    kb.op(PE, lambda kv=kv, l=l: TE.matmul(pf(1)[:, kv:kv + 1], lhsT=w1sb.t[0:64, kv, l, :], rhs=peT.t[0:64, kv, l:l + 1],
                                                                           start=(l == 0), stop=(l == 31)),
                                          reads=[w1sb.b, peT.b], writes=[PS[1].b], inc=(l == 31))
                            kb.op(DVE, lambda: V.tensor_copy(out=bias_c.t[:], in_=pf(1)[:, 0:2]), reads=[PS[1].b], writes=[bias_c.b])
                            def cmp_gen(kv, g, idx):
                                bH, bO = 2 * idx, 2 * idx + 1
                                xh, x2, sg, HTb, tmpc = xhs[idx], x2s[idx], sgs_[idx], HTbs[idx], tmpcs[idx]
                                for l in range(32):
                                    kb.op(PE, lambda kv=kv, g=g, l=l: TE.matmul(
                                        pf(bH)[:, 0:127], lhsT=w1sb.t[g * 64:(g + 1) * 64, kv, l, :],
                                        rhs=cmpT.t[g * 64:(g + 1) * 64, kv, l:l + 16 * 126 + 1:16],
                                        start=(l == 0), stop=(l == 31)),
                                        reads=[w1sb.b] + cmpT.bs, writes=[PS[bH].b], inc=(l == 31))
                                yield
                                kb.op(ACT, lambda kv=kv: A.activation(out=xh.t[:, 0:127], in_=pf(bH)[:, 0:127], func=AF.Identity,
                                                                      bias=bias_c.t[:, kv:kv + 1], scale=1.0),
                                      reads=[PS[bH].b, bias_c.b], writes=[xh.b])
                                yield
                                kb.op(DVE, lambda: V.tensor_tensor(out=x2.t[:, 0:127], in0=xh.t[:, 0:127], in1=xh.t[:, 0:127], op=ALU.mult),
                                      reads=[xh.b], writes=[x2.b])
                                yield
                                kb.op(DVE, lambda: V.tensor_scalar(out=x2.t[:, 0:127], in0=x2.t[:, 0:127], scalar1=0.044715, scalar2=1.0,
                                                                   op0=ALU.mult, op1=ALU.add), reads=[x2.b], writes=[x2.b])
                                yield
                                kb.op(DVE, lambda: V.tensor_tensor(out=x2.t[:, 0:127], in0=x2.t[:, 0:127], in1=xh.t[:, 0:127], op=ALU.mult),
                                      reads=[x2.b, xh.b], writes=[x2.b])
                                yield
                                kb.op(ACT, lambda: A.activation(out=sg.t[:, 0:127], in_=x2.t[:, 0:127], func=AF.Sigmoid, scale=1.5957691216057308),
                                      reads=[x2.b], writes=[sg.b])
                                yield
                                kb.op(DVE, lambda: V.tensor_tensor(out=HTb.t[:, 0:127], in0=xh.t[:, 0:127], in1=sg.t[:, 0:127], op=ALU.mult),
                                      reads=[xh.b, sg.b], writes=[HTb.b])
                                yield
                                kb.op(PE, lambda kv=kv: TE.matmul(pf(bO)[0:127, 0:64], lhsT=HTb.t[:, 0:127], rhs=w2sb.t[:, kv, :], start=True, stop=True),
                                      reads=[HTb.b, w2sb.b], writes=[PS[bO].b])
                                yield
                                if kv == 0:
                                    kb.op(ACT, lambda g=g: A.activation(out=tmpc.t[0:127, :], in_=pf(bO)[0:127, 0:64], func=AF.Square,
                                                                        accum_out=cst.t[0:127, g:g + 1]),
                                          reads=[PS[bO].b], writes=[tmpc.b, cst.b])
                                    rstd_from_ss(cst.t[0:127, g:g + 1], cst.t[0:127, 2 + g:3 + g], cst.t[0:127, 4 + g:5 + g], cst.t[0:127, 6 + g:7 + g], 64, [cst.b])
                                    kb.op(DVE, lambda g=g: V.scalar_tensor_tensor(out=kca.t[0:127, g, 0:64], in0=pf(bO)[0:127, 0:64],
                                                                                  scalar=cst.t[0:127, 6 + g:7 + g], in1=gk.t[0:127, 0, :],
                                                                                  op0=ALU.mult, op1=ALU.mult),
                                          reads=[PS[bO].b, cst.b, gk.b], writes=[kca.b])
                                else:
                                    kb.op(ACT, lambda g=g: A.copy(out=Vc.t[0:127, g, 0:64], in_=pf(bO)[0:127, 0:64]),
                                          reads=[PS[bO].b], writes=[Vc.b])
                                yield

                            run_interleaved([(lambda kv=kv, g=g: cmp_gen(kv, g, 2 * kv + g)) for kv in range(2) for g in range(2)], width=4)
                            transposes([kca.t[:, 0, :], kca.t[:, 1, :]], 6, [kca.b])
                            kb.op(ACT, lambda: A.copy(out=KcT.t[:], in_=pb(6)[:, 0:256].rearrange("p (a b) -> p a b", b=128)),
                                  reads=[PS[6].b], writes=[KcT.b])
                            if "kc" in dbg:
                                kcd = sb(s2b, "kcd", [128, 2, 64], F32)
                                kb.op(DVE, lambda: V.tensor_copy(out=kcd.t[:], in_=kca.t[:, :, 0:64]), reads=[kca.b], writes=[kcd.b])
                                kb.dma(SP, dbg["kc"].rearrange("p (a b) -> p a b", b=64), kcd.t[:], reads=[kcd.b], is_out=True)
                            if "vc" in dbg:
                                vcd = sb(s2b, "vcd", [128, 2, 64], F32)
                                kb.op(DVE, lambda: V.tensor_copy(out=vcd.t[:], in_=Vc.t[:, :, 0:64]), reads=[Vc.b], writes=[vcd.b])
                                kb.dma(SP, dbg["vc"].rearrange("p (a b) -> p a b", b=64), vcd.t[:], reads=[vcd.b], is_out=True)
                            kb.barrier()
                            chk(4)

                    phase_1c()
                    kb.barrier()
                    chk(5)

                wm = sb(sB, "wm", [128, KC, 2048], BF16, 4)
                wbr = sb(sB, "wbr", [128, 8, D], BF16)
                phase_1d()
                kb.barrier()
                chk(6)
                phase_1e()
                kb.barrier()
                chk(7)

            phase_2()
            kb.finish()
    except _Stop:
        pass
    return nc


_NAMES = ["x", "norm1_g", "w_in", "nsa_q_norm", "nsa_k_norm", "cmp_pe", "cmp_w1", "cmp_w2", "ret_gn_g",
          "w_branch", "w_out", "norm2_g", "ffn_w_gate", "ffn_w_up", "ffn_w_down"]


def kernel(**inputs):
    n = 8
    arrs = {k: np.ascontiguousarray(np.asarray(inputs[k], dtype=np.float32)) for k in _NAMES}
    nc = build_nc()
    in_maps = []
    for i in range(n):
        m = {k: arrs[k] for k in _NAMES if k != "x"}
        m["x"] = np.ascontiguousarray(arrs["x"][i])
        in_maps.append(m)
    res = run_bass_kernel_spmd(nc, in_maps, core_ids=list(range(n)))
    return np.stack([np.asarray(r["out"], dtype=np.float32) for r in res.results], axis=0)
```

```python
import numpy as np
from contextlib import ExitStack
import concourse.bass as bass
import concourse.mybir as mybir
from concourse.bass_utils import run_bass_kernel_spmd

F32 = mybir.dt.float32
BF16 = mybir.dt.bfloat16
AF = mybir.ActivationFunctionType
ALU = mybir.AluOpType
AX = mybir.AxisListType

S = 2048
D = 1024
NT = 16
KC = 8
N_IN = 5400
DFF = 2816
NFB = 22
EPS = 1e-6
SEM_LIMIT = 24000
import os as _os
CUT = int(_os.environ.get('P1B_CUT', '99'))
CUTC = int(_os.environ.get('P1C_CUT', '99'))
SUBC = int(_os.environ.get('P1C_SUB', '99'))
WIDTH = int(_os.environ.get('P1C_WIDTH', '2'))
XY1 = int(_os.environ.get('XY1', '0'))
XY2 = int(_os.environ.get('XY2', '2'))
XY3 = int(_os.environ.get('XY3', '0'))
PREWIN = int(_os.environ.get('PREWIN', '1'))
QDVE = int(_os.environ.get('QDVE', '1'))
NUDVE = int(_os.environ.get('NUDVE', '1'))
NEG = -30000.0

C_Q = 0
C_KV = 512
C_G = 1280
C_R = 1304
C_M = 3352


class Buf:
    __slots__ = ("name", "w", "r", "excl")

    def __init__(self, name):
        self.name = name
        self.w = None
        self.r = []
        self.excl = False


class SemW:
    __slots__ = ("h",)

    def __init__(self, h):
        self.h = h


class Slot:
    __slots__ = ("sem", "val")

    def __init__(self, sem):
        self.sem = sem
        self.val = 0


class Q:
    def __init__(self, name, eng):
        self.name = name
        self.eng = eng
        self.sem = None
        self.count = 0
        self.waited = {}
        self.ring = []
        self.ri = 0
        self.pending = False


class T:
    def __init__(self, t, name, nb=1):
        self.t = t
        self.bs = [Buf(f"{name}{i}") for i in range(nb)]

    @property
    def b(self):
        return self.bs[0]


class KB:
    def __init__(self, nc, es):
        self.nc = nc
        self.es = es
        self.nsem = 0
        self.pe = self.mkq("pe", nc.tensor)
        self.act = self.mkq("act", nc.scalar)
        self.dve = self.mkq("dve", nc.vector)
        self.pool = self.mkq("pool", nc.gpsimd)
        self.sp = self.mkq("sp", nc.sync)
        self.qs = [self.pe, self.act, self.dve, self.pool, self.sp]
        for q, n in ((self.sp, 16), (self.pool, 8), (self.act, 4)):
            q.ring = [Slot(self.new_sem(f"{q.name}_d{i}")) for i in range(n)]
        self.out_toks = []

    def new_sem(self, name):
        self.nsem += 1
        return SemW(self.es.enter_context(self.nc.semaphore(f"{name}_{self.nsem}")))

    def mkq(self, name, eng):
        q = Q(name, eng)
        q.sem = self.new_sem(name)
        return q

    def wait(self, q, tok):
        sw, val = tok[0], tok[1]
        if q.waited.get(sw, 0) >= val:
            return
        q.eng.wait_ge(sw.h, val)
        q.waited[sw] = val

    def _dep(self, q, tok, raw, force=False):
        if tok[2] is q and q is self.pe and not force:
            return
        self.wait(q, tok)

    def _deps(self, q, reads, writes, force=False):
        for b in reads:
            if b.w is not None:
                self._dep(q, b.w, True, force)
            if b.excl:
                for t in b.r:
                    if t[2] is not q:
                        self._dep(q, t, False, force)
        for b in writes:
            if b.w is not None:
                self._dep(q, b.w, False, force)
            for t in b.r:
                self._dep(q, t, False, force)

    def _record(self, tok, reads, writes):
        for b in reads:
            if tok[2] is not None:
                b.r = [t for t in b.r if t[2] is not tok[2]]
            b.r.append(tok)
        for b in writes:
            b.w = tok
            b.r = []

    def op(self, q, fn, reads=(), writes=(), inc=True):
        self._deps(q, reads, writes)
        ins = fn()
        if inc:
            if q.count >= SEM_LIMIT and not q.pending:
                q.sem = self.new_sem(q.name)
                q.count = 0
            ins.then_inc(q.sem.h, 1)
            q.count += 1
            q.pending = False
            tok = (q.sem, q.count, q)
        else:
            q.pending = True
            tok = (q.sem, q.count + 1, q)
        self._record(tok, reads, writes)
        return ins

    def dma(self, q, out, in_, reads=(), writes=(), is_out=False):
        self._deps(q, reads, writes, force=True)
        slot = q.ring[q.ri % len(q.ring)]
        q.ri += 1
        if slot.val > 0:
            self.wait(q, (slot.sem, slot.val))
        if slot.val >= SEM_LIMIT:
            slot.sem = self.new_sem(q.name + "_d")
            slot.val = 0
        ins = q.eng.dma_start(out=out, in_=in_)
        ins.then_inc(slot.sem.h, 16)
        slot.val += 16
        tok = (slot.sem, slot.val, None)
        self._record(tok, reads, writes)
        if is_out:
            self.out_toks.append(tok)
        return tok

    def barrier(self):
        toks = []
        for o in self.qs:
            if o.count > 0:
                toks.append((o.sem, o.count, o))
            for sl in o.ring:
                if sl.val > 0:
                    toks.append((sl.sem, sl.val, None))
        for q in self.qs:
            for t in toks:
                if t[2] is q:
                    continue
                self.wait(q, t)

    def finish(self):
        for t in self.out_toks:
            self.wait(self.sp, t)


class _Stop(Exception):
    pass


def build_nc(debug=None, stop=None):
    nc = bass.Bass("TRN2", target_bir_lowering=False)

    def din(name, shape):
        return nc.dram_tensor(name, list(shape), F32, kind="ExternalInput").ap()

    x = din("x", [S, D])
    norm1_g = din("norm1_g", [1, D])
    w_in = din("w_in", [1, D, N_IN])
    nsa_q_norm = din("nsa_q_norm", [1, 64])
    nsa_k_norm = din("nsa_k_norm", [1, 3, 64])
    cmp_pe = din("cmp_pe", [1, 2, 32, 64])
    cmp_w1 = din("cmp_w1", [1, 2, 32, 64, 128])
    cmp_w2 = din("cmp_w2", [1, 2, 128, 64])
    ret_gn_g = din("ret_gn_g", [1, 4, 128])
    w_branch = din("w_branch", [1, 2, 512, D])
    w_out = din("w_out", [1, D, D])
    norm2_g = din("norm2_g", [1, D])
    ffn_w_gate = din("ffn_w_gate", [1, D, DFF])
    ffn_w_up = din("ffn_w_up", [1, D, DFF])
    ffn_w_down = din("ffn_w_down", [1, DFF, D])
    out = nc.dram_tensor("out", [S, D], F32, kind="ExternalOutput").ap()
    xmid = nc.dram_tensor("xmid", [S, D], F32, kind="Internal").ap()
    dbg = {}
    if debug:
        for name, shape in debug.items():
            dbg[name] = nc.dram_tensor("dbg_" + name, list(shape), F32, kind="ExternalOutput").ap()

    w_in_v = w_in[0].rearrange("(k p) n -> p k n", p=128)

    try:
        with ExitStack() as es:
            kb = KB(nc, es)

            def chk(n):
                if stop is not None and n >= stop:
                    kb.barrier()
                    kb.finish()
                    raise _Stop()
            PE, ACT, DVE, POOL, SP = kb.pe, kb.act, kb.dve, kb.pool, kb.sp
            V, A, G, TE = nc.vector, nc.scalar, nc.gpsimd, nc.tensor

            def sb(scope, name, shape, dt, nb=1):
                return T(scope.enter_context(nc.sbuf_tensor(name, list(shape), dt)), name, nb)

            PS2 = [es.enter_context(nc.psum_tensor(f"psp{j}", [128, 1024], F32)) for j in range(4)]
            PS = [T(None, f"ps{i}") for i in range(8)]
            for p_ in PS:
                p_.b.excl = True

            def pf(i):
                return PS2[i // 2][:, (i % 2) * 512:(i % 2 + 1) * 512]

            def pb(i):
                return PS2[i // 2][:].bitcast(BF16)[:, (i % 2) * 1024:(i % 2 + 1) * 1024]

            def pf2(j):
                return PS2[j][:]

            ident_f = sb(es, "ident_f", [128, 128], F32)
            ident_b = sb(es, "ident_b", [128, 128], BF16)
            ones_f = sb(es, "ones_f", [128, 128], F32)
            nhalf = sb(es, "nhalf", [128, 16], F32)
            hT = sb(es, "hT", [128, KC, S], BF16, NT)
            stat = sb(es, "stat", [128, NT, 4], F32, NT)
            hb = [sb(es, f"hb{i}", [128, D], BF16) for i in range(2)]
            junk = sb(es, "junk", [128, D], BF16)

            kb.op(POOL, lambda: G.memset(ones_f.t[:], 1.0), writes=[ones_f.b])
            kb.op(POOL, lambda: G.memset(nhalf.t[:], -0.5), writes=[nhalf.b])
            kb.op(POOL, lambda: G.affine_select(out=ident_f.t[:], in_=ones_f.t[:, 0:128], pattern=[[1, 128]],
                                                compare_op=ALU.is_equal, fill=0.0, base=0, channel_multiplier=-1),
                  reads=[ones_f.b], writes=[ident_f.b])
            kb.op(DVE, lambda: V.tensor_copy(out=ident_b.t[:], in_=ident_f.t[:]), reads=[ident_f.b], writes=[ident_b.b])

            def rstd_from_ss(ss_ap, ms_ap, sd_ap, rs_ap, n, bufs):
                k = ms_ap.shape[-1]
                P_ = ms_ap.shape[0]
                kb.op(DVE, lambda: V.tensor_scalar(out=ms_ap, in0=ss_ap, scalar1=1.0 / n, scalar2=EPS,
                                                   op0=ALU.mult, op1=ALU.add), reads=bufs, writes=bufs)
                kb.op(POOL, lambda: G.tensor_tensor(out=rs_ap, in0=ms_ap, in1=nhalf.t[0:P_, 0:k], op=ALU.pow),
                      reads=list(bufs) + [nhalf.b], writes=bufs)

            def transposes(src_aps, bank, reads):
                pbv = pb(bank)
                n = len(src_aps)
                for i, ap in enumerate(src_aps):
                    kb.op(PE, lambda ap=ap, i=i: TE.transpose(out=pbv[:, i * 128:(i + 1) * 128], in_=ap, identity=ident_b.t[:]),
                          reads=list(reads) + [ident_b.b], writes=[PS[bank].b], inc=(i == n - 1))

            def norm_gen(src, c, gbc, sidx, bank):
                sbuf_ = [stat.bs[c]]
                st = stat.t
                kb.op(ACT, lambda: A.activation(out=junk.t[:], in_=src.t[:], func=AF.Square, accum_out=st[:, c, 0:1]),
                      reads=src.bs, writes=[junk.b] + sbuf_)
                yield
                kb.op(DVE, lambda: V.tensor_scalar(out=st[:, c, 1:2], in0=st[:, c, 0:1], scalar1=1.0 / D, scalar2=EPS,
                                                   op0=ALU.mult, op1=ALU.add), reads=sbuf_, writes=sbuf_)
                yield
                kb.op(POOL, lambda: G.tensor_tensor(out=st[:, c, 3:4], in0=st[:, c, 1:2], in1=nhalf.t[:, 0:1], op=ALU.pow),
                      reads=sbuf_ + [nhalf.b], writes=sbuf_)
                yield
                h = hb[sidx % 2]
                kb.op(DVE, lambda: V.scalar_tensor_tensor(out=h.t[:], in0=src.t[:], scalar=st[:, c, 3:4], in1=gbc.t[:],
                                                          op0=ALU.mult, op1=ALU.mult),
                      reads=src.bs + [gbc.b] + sbuf_, writes=[h.b])
                yield
                for _ in range(XY3):
                    yield
                transposes([h.t[:, k * 128:(k + 1) * 128] for k in range(KC)], bank, [h.b])
                yield
                kb.op(ACT, lambda: A.copy(out=hT.t[:, :, c * 128:(c + 1) * 128],
                                          in_=pb(bank).rearrange("p (a b) -> p a b", b=128)),
                      reads=[PS[bank].b], writes=[hT.bs[c]])
                yield

            def run_interleaved(gen_fns, width=2):
                pending = list(gen_fns)
                active = []
                while pending or active:
                    while pending and len(active) < width:
                        active.append(pending.pop(0)())
                    for g_ in list(active):
                        try:
                            next(g_)
                        except StopIteration:
                            active.remove(g_)

            def load_w(dst, src_ap, q=None):
                kb.dma(q or POOL, dst.t[:], src_ap, writes=[dst.b])

            def dbg_store(name, src_ap, rows, reads):
                if name in dbg:
                    kb.dma(SP, dbg[name][rows], src_ap, reads=reads, is_out=True)

            def phase_1c():
                with ExitStack() as s3:
                    wq = sb(s3, "wq", [128, KC, 512], BF16)
                    load_w(wq, w_in_v[:, :, C_Q:C_Q + 512])
                    wg = sb(s3, "wg", [128, KC, 24], BF16)
                    load_w(wg, w_in_v[:, :, C_G:C_G + 24])
                    sqq = [sb(s3, f"sqq{i}", [128, 512], F32) for i in range(2)]
                    qst = sb(s3, "qst", [128, NT, 32], F32, NT)
                    tmpq = [sb(s3, f"tmpq{i}", [128, 8, 64], F32) for i in range(2)]
                    qaug = [sb(s3, f"qaug{i}", [128, 8, 128], BF16, 4) for i in range(2)]
                    qT = [sb(s3, f"qT{i}", [128, 8, 128], BF16) for i in range(2)]
                    qT2 = [sb(s3, f"qT2{i}", [128, 8, 128], BF16) for i in range(2)]
                    gate = [sb(s3, f"gate{i}", [128, 24], F32) for i in range(2)]
                    scl = [sb(s3, f"scl{i}", [128, 1024], F32) for i in range(2)]
                    NPT = 3
                    PT = [[sb(s3, f"PT{i}_{j}", [128, 1024], BF16, 2) for j in range(NPT)] for i in range(2)]
                    oTs = [sb(s3, f"oTs{i}", [128, 1024], F32) for i in range(2)]
                    num = [sb(s3, f"num{i}", [128, 3, 8, 64], F32) for i in range(2)]
                    den = [sb(s3, f"den{i}", [128, 3, 8], F32) for i in range(2)]
                    rdc = [sb(s3, f"rdc{i}", [128, 8], F32) for i in range(2)]
                    impn = [sb(s3, f"impn{i}", [128, 8, 32], F32) for i in range(2)]
                    imp = [sb(s3, f"imp{i}", [128, 2, 32], F32) for i in range(2)]
                    top8 = [sb(s3, f"top8{i}", [128, 2, 8], F32, 2) for i in range(2)]
                    rd = [sb(s3, f"rd{i}", [128, 3, 8], F32) for i in range(2)]
                    coef = [sb(s3, f"coef{i}", [128, 3, 8], F32) for i in range(2)]
                    oacc = [sb(s3, f"oacc{i}", [128, 8, 64], F32) for i in range(2)]
                    otmp = [sb(s3, f"otmp{i}", [128, 8, 64], F32) for i in range(2)]
                    ytok = [sb(s3, f"ytok{i}", [128, 512], BF16) for i in range(2)]
                    ydbg = sb(s3, "ydbg", [128, 512], F32) if "y_nsa" in dbg else None
                    for i in range(2):
                        kb.op(POOL, lambda i=i: G.memset(qaug[i].t[:], 0.0), writes=qaug[i].bs)
                    ptc = [0, 0]

                    def tile_gen(c):
                        i2 = c % 2
                        base = 4 * i2
                        bZ, bG = base, base + 1
                        bO0, bO1 = base + 2, base + 3
                        Sb = [PS[bZ].b, PS[bG].b]
                        Ob = [PS[bO0].b, PS[bO1].b]
                        tok = slice(c * 128, (c + 1) * 128)

                        def S2():
                            return pf2(base // 2)

                        def O2():
                            return pf2(base // 2 + 1)

                        def X8():
                            return O2().rearrange("p (h c) -> p h c", h=8)

                        def to_token_major(br, ncol):
                            nu, de = num[i2], den[i2]
                            ot = oTs[i2]
                            kb.op(DVE, lambda: V.tensor_scalar(out=ot.t[0:ncol, :], in0=O2()[0:ncol, :], scalar1=1.0, scalar2=None, op0=ALU.mult),
                                  reads=Ob, writes=[ot.b])
                            yield
                            for _ in range(XY2):
                                yield
                            for hh in range(8):
                                kb.op(PE, lambda hh=hh: TE.transpose(out=X8()[:, hh, 0:ncol], in_=ot.t[0:ncol, hh * 128:(hh + 1) * 128],
                                                                     identity=ident_f.t[0:ncol, 0:ncol]),
                                      reads=[ot.b, ident_f.b], writes=[Ob[hh // 4]], inc=(hh % 4 == 3))
                            yield
                            if NUDVE:
                                kb.op(DVE, lambda: V.tensor_scalar(out=nu.t[:, br, :, :], in0=X8()[:, :, 0:64], scalar1=1.0, scalar2=None, op0=ALU.mult), reads=Ob, writes=[nu.b])
                            else:
                                kb.op(ACT, lambda: A.copy(out=nu.t[:, br, :, :], in_=X8()[:, :, 0:64]), reads=Ob, writes=[nu.b])
                            yield
                            kb.op(DVE, lambda: V.tensor_scalar(out=de.t[:, br, :], in0=X8()[:, :, 64], scalar1=1e-30, scalar2=None, op0=ALU.max),
                                  reads=Ob, writes=[de.b])
                            yield

                        def branch(br, KT, VV, kts, qsrc, pre=False):
                            n = len(kts)

                            def scores(kt):
                                for g in range(2):
                                    kb.op(PE, lambda g=g: TE.matmul(S2()[:, g * 512:(g + 1) * 512], lhsT=KT.t[:, g, kt * 128:(kt + 1) * 128],
                                                                    rhs=qsrc.t[:, 4 * g:4 * g + 4, :], start=True, stop=True),
                                          reads=[KT.bs[kt], qsrc.b], writes=[Sb[g]])

                            if not pre:
                                scores(kts[0])
                                yield
                            for j, kt in enumerate(kts):
                                pt = PT[i2][ptc[i2] % NPT]
                                ptc[i2] += 1
                                for g in range(2):
                                    kb.op(ACT, lambda pt=pt, g=g: A.activation(out=pt.t[:, g * 512:(g + 1) * 512], in_=S2()[:, g * 512:(g + 1) * 512], func=AF.Exp),
                                          reads=[Sb[g]], writes=[pt.bs[g]])
                                yield
                                if j + 1 < n:
                                    for _ in range(XY1):
                                        yield
                                    scores(kts[j + 1])
                                    yield
                                if kt == c:
                                    kb.op(DVE, lambda pt=pt: V.tensor_tensor(out=pt.t[:], in0=pt.t[:], in1=dmask8.t[:].rearrange("p a b -> p (a b)"), op=ALU.mult),
                                          reads=pt.bs + [dmask8.b], writes=pt.bs)
                                    yield
                                elif br == 2 and kt == c - 4:
                                    kb.op(DVE, lambda pt=pt: V.tensor_tensor(out=pt.t[:], in0=pt.t[:], in1=tmask8.t[:].rearrange("p a b -> p (a b)"), op=ALU.mult),
                                          reads=pt.bs + [tmask8.b], writes=pt.bs)
                                    yield
                                for g in range(2):
                                    kb.op(PE, lambda kt=kt, pt=pt, j=j, g=g: TE.matmul(O2()[0:65, g * 512:(g + 1) * 512], lhsT=VV.t[:, kt, g, :],
                                                                                       rhs=pt.t[:, g * 512:(g + 1) * 512], start=(j == 0), stop=(j == n - 1)),
                                          reads=[pt.bs[g], VV.bs[kt]], writes=[Ob[g]], inc=(j == n - 1))
                                yield
                            yield from to_token_major(br, 65)

                        for k in range(KC):
                            kb.op(PE, lambda k=k: TE.matmul(pf(bZ), lhsT=hT.t[:, k, tok], rhs=wq.t[:, k, :], start=(k == 0), stop=(k == KC - 1)),
                                  reads=[hT.bs[c], wq.b], writes=[PS[bZ].b], inc=(k == KC - 1))
                        for k in range(KC):
                            kb.op(PE, lambda k=k: TE.matmul(pf(bG)[:, 0:24], lhsT=hT.t[:, k, tok], rhs=wg.t[:, k, :], start=(k == 0), stop=(k == KC - 1)),
                                  reads=[hT.bs[c], wg.b], writes=[PS[bG].b], inc=(k == KC - 1))
                        yield
                        sq = sqq[i2]
                        kb.op(ACT, lambda: A.activation(out=sq.t[:], in_=pf(bZ), func=AF.Square), reads=[PS[bZ].b], writes=[sq.b])
                        gt = gate[i2]
                        kb.op(ACT, lambda: A.activation(out=gt.t[:], in_=pf(bG)[:, 0:24], func=AF.Tanh, scale=0.5), reads=[PS[bG].b], writes=[gt.b])
                        yield
                        kb.op(DVE, lambda: V.tensor_scalar(out=gt.t[:], in0=gt.t[:], scalar1=0.5, scalar2=0.5, op0=ALU.mult, op1=ALU.add),
                              reads=[gt.b], writes=[gt.b])
                        qs = qst.t
                        qsb = [qst.bs[c]]
                        kb.op(DVE, lambda: V.tensor_reduce(out=qs[:, c, 0:8], in_=sq.t[:].rearrange("p (a b) -> p a b", b=64), axis=AX.X, op=ALU.add),
                              reads=[sq.b], writes=qsb)
                        yield
                        kb.op(DVE, lambda: V.tensor_scalar(out=qs[:, c, 8:16], in0=qs[:, c, 0:8], scalar1=1.0 / 64, scalar2=EPS, op0=ALU.mult, op1=ALU.add), reads=qsb, writes=qsb)
                        yield
                        kb.op(POOL, lambda: G.tensor_tensor(out=qs[:, c, 24:32], in0=qs[:, c, 8:16], in1=nhalf.t[:, 0:8], op=ALU.pow),
                              reads=qsb + [nhalf.b], writes=qsb)
                        yield
                        tq = tmpq[i2]
                        qa = qaug[i2]
                        kb.op(DVE, lambda: V.tensor_tensor(out=tq.t[:], in0=pf(bZ).rearrange("p (a b) -> p a b", b=64),
                                                           in1=qs[:, c, 24:32].unsqueeze(2).broadcast_to([128, 8, 64]), op=ALU.mult),
                              reads=[PS[bZ].b] + qsb, writes=[tq.b])
                        yield
                        kb.op(DVE, lambda: V.tensor_tensor(out=qa.t[:, :, 0:64], in0=tq.t[:], in1=gq.t[:].unsqueeze(1).broadcast_to([128, 8, 64]), op=ALU.mult),
                              reads=[tq.b, gq.b], writes=[qa.bs[0]])
                        kb.op(POOL, lambda: G.tensor_copy(out=qa.t[:, :, 96:100], in_=QAL.t[:, c, :, :]), reads=[QAL.b], writes=[qa.bs[1]])
                        yield
                        transposes([qa.t[:, h, :] for h in range(8)], bZ, qa.bs)
                        yield
                        q1 = qT[i2]
                        if QDVE:
                            kb.op(DVE, lambda: V.tensor_copy(out=q1.t[:], in_=pb(bZ).rearrange("p (a b) -> p a b", b=128)), reads=[PS[bZ].b], writes=[q1.b])
                        else:
                            kb.op(ACT, lambda: A.copy(out=q1.t[:], in_=pb(bZ).rearrange("p (a b) -> p a b", b=128)), reads=[PS[bZ].b], writes=[q1.b])
                        yield
                        nu, de = num[i2], den[i2]
                        rdc_, impn_, imp_, top8_ = rdc[i2], impn[i2], imp[i2], top8[i2]
                        sc_ = scl[i2]
                        pc = PT[i2][ptc[i2] % NPT]
                        ptc[i2] += 1
                        for g in range(2):
                            kb.op(PE, lambda g=g: TE.matmul(S2()[0:127, g * 512:(g + 1) * 512], lhsT=KcT.t[:, g, 0:127], rhs=q1.t[:, 4 * g:4 * g + 4, :], start=True, stop=True),
                                  reads=[KcT.b, q1.b], writes=[Sb[g]])
                        yield
                        kb.op(DVE, lambda: V.scalar_tensor_tensor(out=sc_.t[0:127, :].rearrange("p (a b) -> p a b", b=128),
                                                                  in0=S2()[0:127, :].rearrange("p (a b) -> p a b", b=128), scalar=60.0,
                                                                  in1=cmask.t[0:127, tok].unsqueeze(1).broadcast_to([127, 8, 128]),
                                                                  op0=ALU.min, op1=ALU.add),
                              reads=Sb + [cmask.b], writes=[sc_.b])
                        yield
                        if PREWIN:
                            kt0 = max(0, c - 4)
                            for g in range(2):
                                kb.op(PE, lambda g=g: TE.matmul(S2()[:, g * 512:(g + 1) * 512], lhsT=KT_win.t[:, g, kt0 * 128:(kt0 + 1) * 128],
                                                                rhs=q1.t[:, 4 * g:4 * g + 4, :], start=True, stop=True),
                                      reads=[KT_win.bs[kt0], q1.b], writes=[Sb[g]])
                            yield
                        kb.op(ACT, lambda: A.activation(out=pc.t[0:127, :], in_=sc_.t[0:127, :], func=AF.Exp), reads=[sc_.b], writes=pc.bs)
                        yield
                        for g in range(2):
                            kb.op(PE, lambda g=g: TE.matmul(O2()[0:97, g * 512:(g + 1) * 512], lhsT=Vc.t[0:127, g, :], rhs=pc.t[0:127, g * 512:(g + 1) * 512], start=True, stop=True),
                                  reads=[pc.bs[g], Vc.b], writes=[Ob[g]])
                        yield
                        yield from to_token_major(0, 97)
                        kb.op(DVE, lambda: V.reciprocal(out=rdc_.t[:], in_=de.t[:, 0, :]), reads=[de.b], writes=[rdc_.b])
                        yield
                        kb.op(DVE, lambda: V.tensor_tensor(out=impn_.t[:], in0=X8()[:, :, 65:97],
                                                           in1=rdc_.t[:].unsqueeze(2).broadcast_to([128, 8, 32]), op=ALU.mult),
                              reads=Ob + [rdc_.b], writes=[impn_.b])
                        yield
                        q2 = qT2[i2]

                        def imp_chain():
                            kb.op(DVE, lambda: V.tensor_reduce(out=imp_.t[:], in_=impn_.t[:].rearrange("p (g r) j -> p g j r", g=2), axis=AX.X, op=ALU.add),
                                  reads=[impn_.b], writes=[imp_.b])
                            yield
                            kb.op(DVE, lambda: V.tensor_tensor(out=imp_.t[:], in0=imp_.t[:], in1=addc.t[:, c:c + 1, :].broadcast_to([128, 2, 32]), op=ALU.add),
                                  reads=[imp_.b, addc.b], writes=[imp_.b])
                            yield
                            for g in range(2):
                                kb.op(DVE, lambda g=g: V.max(out=top8_.t[:, g, :], in_=imp_.t[:, g, :]), reads=[imp_.b], writes=[top8_.bs[g]])
                            yield
                            for g in range(2):
                                kb.op(DVE, lambda g=g: V.tensor_scalar(out=qa.t[:, 4 * g:4 * g + 4, 64:96],
                                                                       in0=imp_.t[:, g:g + 1, :].broadcast_to([128, 4, 32]),
                                                                       scalar1=top8_.t[:, g, 7:8], scalar2=NEG, op0=ALU.is_lt, op1=ALU.mult),
                                      reads=[imp_.b, top8_.bs[g]], writes=[qa.bs[2 + g]])
                            yield

                        gw = branch(2, KT_win, V_win, list(range(max(0, c - 4), c + 1)), q1, pre=bool(PREWIN))
                        gi = imp_chain()
                        live = [gw, gi]
                        while live:
                            for g_ in list(live):
                                try:
                                    next(g_)
                                except StopIteration:
                                    live.remove(g_)
                            yield
                        for _ in range(XY3):
                            yield
                        transposes([qa.t[:, h, :] for h in range(8)], bZ, qa.bs)
                        if QDVE:
                            kb.op(DVE, lambda: V.tensor_copy(out=q2.t[:], in_=pb(bZ).rearrange("p (a b) -> p a b", b=128)), reads=[PS[bZ].b], writes=[q2.b])
                        else:
                            kb.op(ACT, lambda: A.copy(out=q2.t[:], in_=pb(bZ).rearrange("p (a b) -> p a b", b=128)), reads=[PS[bZ].b], writes=[q2.b])
                        yield
                        yield from branch(1, KT_slc, V_slc, list(range(0, c + 1)), q2)
                        rd_, coef_, oacc_, otmp_ = rd[i2], coef[i2], oacc[i2], otmp[i2]
                        kb.op(DVE, lambda: V.reciprocal(out=rd_.t[:], in_=de.t[:]), reads=[de.b], writes=[rd_.b])
                        yield
                        kb.op(DVE, lambda: V.tensor_tensor(out=coef_.t[:], in0=gt.t[:].rearrange("p (h b) -> p b h", b=3), in1=rd_.t[:], op=ALU.mult),
                              reads=[gt.b, rd_.b], writes=[coef_.b])
                        yield
                        kb.op(DVE, lambda: V.tensor_tensor(out=oacc_.t[:], in0=nu.t[:, 0], in1=coef_.t[:, 0, :].unsqueeze(2).broadcast_to([128, 8, 64]), op=ALU.mult),
                              reads=[nu.b, coef_.b], writes=[oacc_.b])
                        kb.op(POOL, lambda: G.tensor_tensor(out=otmp_.t[:], in0=nu.t[:, 1], in1=coef_.t[:, 1, :].unsqueeze(2).broadcast_to([128, 8, 64]), op=ALU.mult),
                              reads=[nu.b, coef_.b], writes=[otmp_.b])
                        yield
                        kb.op(DVE, lambda: V.tensor_tensor(out=oacc_.t[:], in0=oacc_.t[:], in1=otmp_.t[:], op=ALU.add), reads=[oacc_.b, otmp_.b], writes=[oacc_.b])
                        yield
                        kb.op(POOL, lambda: G.tensor_tensor(out=otmp_.t[:], in0=nu.t[:, 2], in1=coef_.t[:, 2, :].unsqueeze(2).broadcast_to([128, 8, 64]), op=ALU.mult),
                              reads=[nu.b, coef_.b], writes=[otmp_.b])
                        yield
                        yt = ytok[i2]
                        kb.op(DVE, lambda: V.tensor_tensor(out=yt.t[:], in0=oacc_.t[:].rearrange("p a b -> p (a b)"), in1=otmp_.t[:].rearrange("p a b -> p (a b)"), op=ALU.add),
                              reads=[oacc_.b, otmp_.b], writes=[yt.b])
                        if ydbg is not None:
                            kb.op(POOL, lambda: G.tensor_tensor(out=ydbg.t[:], in0=oacc_.t[:].rearrange("p a b -> p (a b)"), in1=otmp_.t[:].rearrange("p a b -> p (a b)"), op=ALU.add),
                                  reads=[oacc_.b, otmp_.b], writes=[ydbg.b])
                            dbg_store("y_nsa", ydbg.t[:], tok, [ydbg.b])
                        yield
                        for _ in range(XY3):
                            yield
                        transposes([yt.t[:, k * 128:(k + 1) * 128] for k in range(4)], bZ, [yt.b])
                        yield
                        kb.op(DVE, lambda: V.tensor_copy(out=ynsaT.t[:, :, tok], in_=pb(bZ)[:, 0:512].rearrange("p (a b) -> p a b", b=128)),
                              reads=[PS[bZ].b], writes=[ynsaT.bs[c]])
                        yield

                    run_interleaved([(lambda c=c: tile_gen(c)) for c in range(NT)], width=WIDTH)

            def phase_1d():
                with ExitStack() as s4:
                    wr = sb(s4, "wr", [128, KC, 2048], BF16, 4)
                    for j in range(4):
                        kb.dma(POOL, wr.t[:, :, j * 512:(j + 1) * 512], w_in_v[:, :, C_R + j * 512:C_R + (j + 1) * 512], writes=[wr.bs[j]])
                    for j in range(4):
                        kb.dma(POOL, wm.t[:, :, j * 512:(j + 1) * 512], w_in_v[:, :, C_M + j * 512:C_M + (j + 1) * 512], writes=[wm.bs[j]])
                    load_w(wbr, w_branch[0].rearrange("n (k p) d -> p (n k) d", p=128))
                    idT = sb(s4, "idT", [128, 4, 128], F32)
                    qdec = sb(s4, "qdec", [128, 4, 128], F32)
                    kdec = sb(s4, "kdec", [128, 4, 128], F32)
                    gn = sb(s4, "gn", [128, 512], F32)
                    eij = sb(s4, "eij", [128, 128], F32)
                    rowq = sb(s4, "rowq", [128, 128], F32)
                    rowk = sb(s4, "rowk", [128, 128], F32)
                    kb.dma(SP, gn.t[:], ret_gn_g.rearrange("o a b -> o (a b)").broadcast_to([128, 512]), writes=[gn.b])
                    kb.op(POOL, lambda: G.iota(eij.t[:], pattern=[[1, 128]], base=0, channel_multiplier=-1, allow_small_or_imprecise_dtypes=True), writes=[eij.b])
                    kb.op(POOL, lambda: G.iota(rowq.t[:], pattern=[[1, 128]], base=1, channel_multiplier=0, allow_small_or_imprecise_dtypes=True), writes=[rowq.b])
                    kb.op(POOL, lambda: G.iota(rowk.t[:], pattern=[[-1, 128]], base=127, channel_multiplier=0, allow_small_or_imprecise_dtypes=True), writes=[rowk.b])
                    lgs = [float(np.log(1.0 - 2.0 ** (-5.0 - h))) for h in range(4)]
                    cds = [float(np.exp(128.0 * np.float32(lg))) for lg in lgs]
                    cdt = sb(s4, "cdt", [128, 4], F32)
                    for h in range(4):
                        kb.op(ACT, lambda h=h: A.activation(out=cdt.t[:, h:h + 1], in_=rowq.t[:, 0:1], func=AF.Exp, scale=128.0 * float(np.float32(lgs[h]))),
                              reads=[rowq.b], writes=[cdt.b])
                    for h in range(4):
                        kb.op(ACT, lambda h=h: A.activation(out=idT.t[:, h, :], in_=eij.t[:], func=AF.Exp, scale=lgs[h]), reads=[eij.b], writes=[idT.b])
                        kb.op(POOL, lambda h=h: G.affine_select(out=idT.t[:, h, :], in_=idT.t[:, h, :], pattern=[[1, 128]], compare_op=ALU.is_ge, fill=0.0,
                                                                base=0, channel_multiplier=-1), reads=[idT.b], writes=[idT.b])
                        kb.op(ACT, lambda h=h: A.activation(out=qdec.t[:, h, :], in_=rowq.t[:], func=AF.Exp, scale=lgs[h]), reads=[rowq.b], writes=[qdec.b])
                        kb.op(ACT, lambda h=h: A.activation(out=kdec.t[:, h, :], in_=rowk.t[:], func=AF.Exp, scale=lgs[h]), reads=[rowk.b], writes=[kdec.b])
                    qTr = sb(s4, "qTr", [128, 4, 512], BF16)
                    qdT = sb(s4, "qdT", [128, 4, 512], BF16)
                    kTr = sb(s4, "kTr", [128, 4, 512], BF16)
                    kdT = sb(s4, "kdT", [128, 4, 512], BF16)
                    v_sb = [sb(s4, f"v_sb{i}", [128, 4, 128], BF16) for i in range(2)]
                    sgl = [sb(s4, f"sgl{i}", [128, 512], F32) for i in range(2)]
                    kd = [sb(s4, f"kd{i}", [128, 4, 128], BF16) for i in range(2)]
                    attb = [sb(s4, f"attb{i}", [128, 4, 128], BF16) for i in range(2)]
                    state_f = sb(s4, "state_f", [128, 4, 128], F32, 4)
                    state_b = sb(s4, "state_b", [128, 4, 128], BF16)
                    yr = [sb(s4, f"yr{i}", [128, 512], BF16) for i in range(2)]
                    yrdbg = sb(s4, "yrdbg", [128, 512], F32) if "y_ret" in dbg else None
                    kb.op(POOL, lambda: G.memset(state_f.t[:], 0.0), writes=state_f.bs)
                    kb.op(POOL, lambda: G.memset(state_b.t[:], 0.0), writes=[state_b.b])
                    KS = float(128.0 ** -0.5)
                    pcnt = [0]
                    state_done = [False] * (NT + 1)
                    stt_done = [False] * (NT + 1)
                    bst = [sb(s4, f"bst{i}", [128, 4, 6], F32, 4) for i in range(2)]
                    mv = [sb(s4, f"mv{i}", [128, 4, 2], F32, 4) for i in range(2)]
                    rs4 = [sb(s4, f"rs4{i}", [128, 12], F32) for i in range(2)]
                    on = [sb(s4, f"on{i}", [128, 512], F32, 4) for i in range(2)]

                    def p1d_gen(c):
                        i2 = c % 2
                        cl = c % 4
                        bA, bB, bT = 2 + 3 * i2, 3 + 3 * i2, 4 + 3 * i2
                        tok = slice(c * 128, (c + 1) * 128)
                        cs = slice(cl * 128, (cl + 1) * 128)
                        bst_, mv_, rs4_, on_ = bst[i2], mv[i2], rs4[i2], on[i2]
                        for k in range(KC):
                            kb.op(PE, lambda k=k: TE.matmul(pf(bA), lhsT=hT.t[:, k, tok], rhs=wr.t[:, k, 1024:1536], start=(k == 0), stop=(k == KC - 1)),
                                  reads=[hT.bs[c], wr.bs[2]], writes=[PS[bA].b], inc=(k == KC - 1))
                        yield
                        for k in range(KC):
                            kb.op(PE, lambda k=k: TE.matmul(pf(bB), lhsT=hT.t[:, k, tok], rhs=wr.t[:, k, 1536:2048], start=(k == 0), stop=(k == KC - 1)),
                                  reads=[hT.bs[c], wr.bs[3]], writes=[PS[bB].b], inc=(k == KC - 1))
                        yield
                        vs, sg_, kd_, ab = v_sb[i2], sgl[i2], kd[i2], attb[i2]
                        kb.op(ACT, lambda: A.copy(out=vs.t[:].rearrange("p a b -> p (a b)"), in_=pf(bA)), reads=[PS[bA].b], writes=[vs.b])
                        yield
                        kb.op(ACT, lambda: A.activation(out=sg_.t[:], in_=pf(bB), func=AF.Silu), reads=[PS[bB].b], writes=[sg_.b])
                        yield
                        kb.op(POOL, lambda: G.tensor_tensor(out=sg_.t[:], in0=sg_.t[:], in1=gn.t[:], op=ALU.mult), reads=[sg_.b, gn.b], writes=[sg_.b])
                        yield
                        for _ in range(XY3):
                            yield
                        transposes([kdT.t[:, h, cs] for h in range(4)], bT, [kdT.b])
                        yield
                        kb.op(ACT, lambda: A.copy(out=kd_.t[:].rearrange("p a b -> p (a b)"), in_=pb(bT)[:, 0:512]), reads=[PS[bT].b], writes=[kd_.b])
                        yield
                        for h in range(4):
                            kb.op(PE, lambda h=h: TE.matmul(pf(bA)[:, h * 128:(h + 1) * 128], lhsT=kTr.t[:, h, cs], rhs=qTr.t[:, h, cs], start=True, stop=True),
                                  reads=[kTr.b, qTr.b], writes=[PS[bA].b], inc=(h == 3))
                        yield
                        kb.op(DVE, lambda: V.tensor_tensor(out=ab.t[:], in0=pf(bA).rearrange("p (a b) -> p a b", b=128), in1=idT.t[:], op=ALU.mult),
                              reads=[PS[bA].b, idT.b], writes=[ab.b])
                        yield
                        if c < NT - 1:
                            for h in range(4):
                                kb.op(PE, lambda h=h: TE.matmul(pf(bT)[:, h * 128:(h + 1) * 128], lhsT=kd_.t[:, h, :], rhs=vs.t[:, h, :], start=True, stop=True),
                                      reads=[kd_.b, vs.b], writes=[PS[bT].b], inc=(h == 3))
                            yield
                            while c > 0 and not state_done[c - 1]:
                                yield
                            for h in range(4):
                                kb.op(DVE, lambda h=h: V.scalar_tensor_tensor(out=state_f.t[:, h, :], in0=state_f.t[:, h, :], scalar=cdt.t[:, h:h + 1],
                                                                              in1=pf(bT)[:, h * 128:(h + 1) * 128], op0=ALU.mult, op1=ALU.add),
                                      reads=[state_f.bs[h], PS[bT].b, cdt.b], writes=[state_f.bs[h]])
                        stt_done[c] = True
                        yield
                        while c > 0 and not state_done[c - 1]:
                            yield
                        for h in range(4):
                            kb.op(PE, lambda h=h: TE.matmul(pf(bB)[:, h * 128:(h + 1) * 128], lhsT=ab.t[:, h, :], rhs=vs.t[:, h, :], start=True, stop=(c == 0)),
                                  reads=[ab.b, vs.b], writes=[PS[bB].b], inc=(c == 0 and h == 3))
                            if c > 0:
                                kb.op(PE, lambda h=h: TE.matmul(pf(bB)[:, h * 128:(h + 1) * 128], lhsT=qdT.t[:, h, cs], rhs=state_b.t[:, h, :], start=False, stop=True),
                                      reads=[qdT.b, state_b.b], writes=[PS[bB].b], inc=(h == 3))
                        yield
                        if c < NT - 1:
                            kb.op(POOL, lambda: G.tensor_copy(out=state_b.t[:], in_=state_f.t[:]), reads=state_f.bs, writes=[state_b.b])
                        state_done[c] = True
                        yield
                        for h in range(4):
                            kb.op(DVE, lambda h=h: V.bn_stats(out=bst_.t[:, h, :], in_=pf(bB)[:, h * 128:(h + 1) * 128]), reads=[PS[bB].b], writes=[bst_.bs[h]])
                        yield
                        for h in range(4):
                            kb.op(DVE, lambda h=h: V.bn_aggr(out=mv_.t[:, h, :], in_=bst_.t[:, h, :]), reads=[bst_.bs[h]], writes=[mv_.bs[h]])
                        yield
                        kb.op(DVE, lambda: V.tensor_scalar(out=rs4_.t[:, 0:4], in0=mv_.t[:, :, 1], scalar1=EPS, scalar2=None, op0=ALU.add), reads=mv_.bs, writes=[rs4_.b])
                        yield
                        kb.op(POOL, lambda: G.tensor_tensor(out=rs4_.t[:, 8:12], in0=rs4_.t[:, 0:4], in1=nhalf.t[:, 0:4], op=ALU.pow),
                              reads=[rs4_.b, nhalf.b], writes=[rs4_.b])
                        yield
                        for h in range(4):
                            kb.op(DVE, lambda h=h: V.tensor_scalar(out=on_.t[:, h * 128:(h + 1) * 128], in0=pf(bB)[:, h * 128:(h + 1) * 128],
                                                                   scalar1=mv_.t[:, h, 0:1], scalar2=rs4_.t[:, 8 + h:9 + h], op0=ALU.subtract, op1=ALU.mult),
                                  reads=[PS[bB].b, mv_.bs[h], rs4_.b], writes=[on_.bs[h]])
                        yield
                        y_ = yr[i2]
                        kb.op(DVE, lambda: V.tensor_tensor(out=y_.t[:], in0=on_.t[:], in1=sg_.t[:], op=ALU.mult), reads=on_.bs + [sg_.b], writes=[y_.b])
                        if yrdbg is not None:
                            kb.op(DVE, lambda: V.tensor_tensor(out=yrdbg.t[:], in0=on_.t[:], in1=sg_.t[:], op=ALU.mult), reads=on_.bs + [sg_.b], writes=[yrdbg.b])
                            dbg_store("y_ret", yrdbg.t[:], tok, [yrdbg.b])
                        yield
                        for _ in range(XY3):
                            yield
                        transposes([y_.t[:, k * 128:(k + 1) * 128] for k in range(4)], bT, [y_.b])
                        yield
                        kb.op(ACT, lambda: A.copy(out=yretT.t[:, :, tok], in_=pb(bT)[:, 0:512].rearrange("p (a b) -> p a b", b=128)),
                              reads=[PS[bT].b], writes=[yretT.bs[c]])
                        yield

                    for tg in range(4):
                        tks = slice(tg * 512, (tg + 1) * 512)
                        hbs = [hT.bs[4 * tg + i] for i in range(4)]
                        for qk in range(2):
                            for h in range(4):
                                bk = pcnt[0] % 2
                                pcnt[0] += 1
                                for k in range(KC):
                                    kb.op(PE, lambda k=k, qk=qk, h=h, bk=bk: TE.matmul(pf(bk), lhsT=wr.t[:, k, qk * 512 + h * 128:qk * 512 + (h + 1) * 128],
                                                                                      rhs=hT.t[:, k, tks], start=(k == 0), stop=(k == KC - 1)),
                                          reads=hbs + [wr.bs[qk]], writes=[PS[bk].b], inc=(k == KC - 1))
                                pv4 = pf(bk).rearrange("p (a b) -> p a b", b=128)
                                if qk == 0:
                                    kb.op(ACT, lambda h=h, bk=bk: A.copy(out=qTr.t[:, h, :], in_=pf(bk)), reads=[PS[bk].b], writes=[qTr.b])
                                    kb.op(DVE, lambda h=h, pv4=pv4: V.tensor_tensor(out=qdT.t[:, h, :].rearrange("p (a b) -> p a b", b=128), in0=pv4,
                                                                                    in1=qdec.t[:, h:h + 1, :].broadcast_to([128, 4, 128]), op=ALU.mult),
                                          reads=[PS[bk].b, qdec.b], writes=[qdT.b])
                                else:
                                    kb.op(ACT, lambda h=h, bk=bk: A.mul(out=kTr.t[:, h, :], in_=pf(bk), mul=KS), reads=[PS[bk].b], writes=[kTr.b])
                                    kb.op(DVE, lambda h=h, pv4=pv4: V.scalar_tensor_tensor(out=kdT.t[:, h, :].rearrange("p (a b) -> p a b", b=128), in0=pv4, scalar=KS,
                                                                                           in1=kdec.t[:, h:h + 1, :].broadcast_to([128, 4, 128]),
                                                                                           op0=ALU.mult, op1=ALU.mult),
                                          reads=[PS[bk].b, kdec.b], writes=[kdT.b])
                        run_interleaved([(lambda c=c: p1d_gen(c)) for c in range(4 * tg, 4 * tg + 4)], width=2)

            def phase_1e():
                with ExitStack() as s5:
                    xt = [sb(s5, f"xte{i}", [128, D], F32) for i in range(2)]
                    g2bc = sb(s5, "g2bc", [128, D], F32)
                    kb.dma(SP, g2bc.t[:], norm2_g[0:1, :].broadcast_to([128, D]), writes=[g2bc.b])
                    wo = sb(s5, "wo", [128, KC, D], BF16)
                    load_w(wo, w_out[0].rearrange("(k p) d -> p k d", p=128))
                    gates = [sb(s5, f"gates{i}", [128, D], F32, 2) for i in range(2)]
                    tmix = [sb(s5, f"tmix{i}", [128, D], F32, 2) for i in range(2)]
                    tmix2 = [sb(s5, f"tmix2{i}", [128, D], F32, 2) for i in range(2)]
                    mixed = [sb(s5, f"mixed{i}", [128, D], BF16, 2) for i in range(2)]
                    mixT = [sb(s5, f"mixT{i}", [128, KC, 128], BF16) for i in range(2)]
                    x1t = [sb(s5, f"x1t{i}", [128, D], F32, 2) for i in range(2)]

                    def p1e_gen(c):
                        i2 = c % 2
                        bs_ = [4 * i2 + i for i in range(4)]
                        tok = slice(c * 128, (c + 1) * 128)
                        xx = xt[i2]
                        kb.dma(SP, xx.t[:], x[tok, :], writes=[xx.b])
                        gt_, tm = gates[i2], (tmix[i2], tmix2[i2])
                        for n, yT in ((0, ynsaT), (1, yretT)):
                            for half in range(2):
                                j = 2 * n + half
                                for k in range(KC):
                                    kb.op(PE, lambda j=j, k=k, half=half: TE.matmul(pf(bs_[half]), lhsT=hT.t[:, k, tok], rhs=wm.t[:, k, j * 512:(j + 1) * 512],
                                                                                    start=(k == 0), stop=(k == KC - 1)),
                                          reads=[hT.bs[c], wm.bs[j]], writes=[PS[bs_[half]].b], inc=(k == KC - 1))
                                yield
                            for half in range(2):
                                bk = bs_[2 + half]
                                for k in range(4):
                                    kb.op(PE, lambda n=n, half=half, k=k, bk=bk, yT=yT: TE.matmul(pf(bk), lhsT=yT.t[:, k, tok], rhs=wbr.t[:, n * 4 + k, half * 512:(half + 1) * 512],
                                                                                                  start=(k == 0), stop=(k == 3)),
                                          reads=[yT.bs[c], wbr.b], writes=[PS[bk].b], inc=(k == 3))
                                yield
                            for half in range(2):
                                kb.op(ACT, lambda half=half: A.activation(out=gt_.t[:, half * 512:(half + 1) * 512], in_=pf(bs_[half]), func=AF.Sigmoid),
                                      reads=[PS[bs_[half]].b], writes=[gt_.bs[half]])
                                yield
                            for half in range(2):
                                hs = slice(half * 512, (half + 1) * 512)
                                kb.op(DVE, lambda half=half, hs=hs, n=n: V.tensor_tensor(out=tm[n].t[:, hs], in0=gt_.t[:, hs], in1=pf(bs_[2 + half]), op=ALU.mult),
                                      reads=[gt_.bs[half], PS[bs_[2 + half]].b], writes=[tm[n].bs[half]])
                                yield
                        mx = mixed[i2]
                        kb.op(DVE, lambda: V.tensor_tensor(out=mx.t[:, 0:512], in0=tm[0].t[:, 0:512], in1=tm[1].t[:, 0:512], op=ALU.add), reads=[tm[0].bs[0], tm[1].bs[0]], writes=[mx.bs[0]])
                        kb.op(POOL, lambda: G.tensor_tensor(out=mx.t[:, 512:1024], in0=tm[0].t[:, 512:1024], in1=tm[1].t[:, 512:1024], op=ALU.add), reads=[tm[0].bs[1], tm[1].bs[1]], writes=[mx.bs[1]])
                        yield
                        for _ in range(XY3):
                            yield
                        transposes([mx.t[:, k * 128:(k + 1) * 128] for k in range(KC)], bs_[0], mx.bs)
                        yield
                        mt = mixT[i2]
                        kb.op(ACT, lambda: A.copy(out=mt.t[:], in_=pb(bs_[0]).rearrange("p (a b) -> p a b", b=128)), reads=[PS[bs_[0]].b], writes=[mt.b])
                        yield
                        for half in range(2):
                            for k in range(KC):
                                kb.op(PE, lambda half=half, k=k: TE.matmul(pf(bs_[2 + half]), lhsT=mt.t[:, k, :], rhs=wo.t[:, k, half * 512:(half + 1) * 512],
                                                                           start=(k == 0), stop=(k == KC - 1)),
                                      reads=[mt.b, wo.b], writes=[PS[bs_[2 + half]].b], inc=(k == KC - 1))
                            yield
                        x1 = x1t[i2]
                        for half in range(2):
                            hs = slice(half * 512, (half + 1) * 512)
                            kb.op(DVE, lambda half=half, hs=hs: V.tensor_tensor(out=x1.t[:, hs], in0=xx.t[:, hs], in1=pf(bs_[2 + half]), op=ALU.add),
                                  reads=[xx.b, PS[bs_[2 + half]].b], writes=[x1.bs[half]])
                            yield
                        kb.dma(SP, xmid[tok, :], x1.t[:], reads=x1.bs)
                        dbg_store("x1", x1.t[:], tok, x1.bs)
                        yield from norm_gen(x1, c, g2bc, i2, bs_[1])

                    run_interleaved([(lambda c=c: p1e_gen(c)) for c in range(NT)], width=2)

            def phase_2():
                with ExitStack() as s6:
                    xt = [sb(s6, f"xtf{i}", [128, D], F32) for i in range(2)]
                    wd = sb(s6, "wd", [128, NFB, D], BF16, 2)
                    wd_v = ffn_w_down[0].rearrange("(fb p) d -> p fb d", p=128)
                    wgs = [sb(s6, f"wgs{i}", [128, KC, 256], BF16) for i in range(2)]
                    wus = [sb(s6, f"wus{i}", [128, KC, 256], BF16) for i in range(2)]
                    act = sb(s6, "act", [128, NFB, 1024], BF16, NFB)
                    sgs = [sb(s6, f"sgs{i}", [128, 512], F32) for i in range(2)]
                    outt = [sb(s6, f"outt{i}", [128, D], F32, 2) for i in range(2)]
                    wg_v = ffn_w_gate[0].rearrange("(k p) f -> p k f", p=128)
                    wu_v = ffn_w_up[0].rearrange("(k p) f -> p k f", p=128)
                    cn = [0, 0]
                    for hf in range(2):
                        for fg in range(11):
                            cols = slice(fg * 256, (fg + 1) * 256)
                            wg_, wu_ = wgs[fg % 2], wus[fg % 2]
                            kb.dma(POOL, wg_.t[:], wg_v[:, :, cols], writes=[wg_.b])
                            kb.dma(POOL, wu_.t[:], wu_v[:, :, cols], writes=[wu_.b])
                            if hf == 0 and fg == 1:
                                kb.dma(POOL, wd.t[:, 0:11, :], wd_v[:, 0:11, :], writes=[wd.bs[0]])
                                kb.dma(POOL, wd.t[:, 11:22, :], wd_v[:, 11:22, :], writes=[wd.bs[1]])
                            for fl in range(2):
                                fb = fg * 2 + fl
                                for t2 in range(2):
                                    tokc = slice(hf * 1024 + t2 * 512, hf * 1024 + (t2 + 1) * 512)
                                    hbs = [hT.bs[hf * 8 + t2 * 4 + i] for i in range(4)]
                                    gb, ub = (0, 1) if cn[0] % 2 == 0 else (2, 3)
                                    cn[0] += 1
                                    for k in range(KC):
                                        kb.op(PE, lambda k=k, fl=fl, gb=gb, wg_=wg_, tokc=tokc: TE.matmul(pf(gb), lhsT=wg_.t[:, k, fl * 128:(fl + 1) * 128], rhs=hT.t[:, k, tokc],
                                                                                                          start=(k == 0), stop=(k == KC - 1)),
                                              reads=hbs + [wg_.b], writes=[PS[gb].b], inc=(k == KC - 1))
                                    for k in range(KC):
                                        kb.op(PE, lambda k=k, fl=fl, ub=ub, wu_=wu_, tokc=tokc: TE.matmul(pf(ub), lhsT=wu_.t[:, k, fl * 128:(fl + 1) * 128], rhs=hT.t[:, k, tokc],
                                                                                                          start=(k == 0), stop=(k == KC - 1)),
                                              reads=hbs + [wu_.b], writes=[PS[ub].b], inc=(k == KC - 1))
                                    sg_ = sgs[cn[0] % 2]
                                    kb.op(ACT, lambda sg_=sg_, gb=gb: A.activation(out=sg_.t[:], in_=pf(gb), func=AF.Silu), reads=[PS[gb].b], writes=[sg_.b])
                                    kb.op(DVE, lambda sg_=sg_, ub=ub, fb=fb, t2=t2: V.tensor_tensor(out=act.t[:, fb, t2 * 512:(t2 + 1) * 512], in0=sg_.t[:], in1=pf(ub), op=ALU.mult),
                                          reads=[sg_.b, PS[ub].b], writes=[act.bs[fb]])
                        for tl in range(8):
                            c = hf * 8 + tl
                            tok = slice(c * 128, (c + 1) * 128)
                            xx = xt[c % 2]
                            kb.dma(SP, xx.t[:], xmid[tok, :], writes=[xx.b])
                            ob = (4, 5) if cn[1] % 2 == 0 else (6, 7)
                            cn[1] += 1
                            for half in range(2):
                                for fb in range(NFB):
                                    kb.op(PE, lambda half=half, fb=fb, ob=ob, tl=tl: TE.matmul(pf(ob[half]), lhsT=act.t[:, fb, tl * 128:(tl + 1) * 128],
                                                                                               rhs=wd.t[:, fb, half * 512:(half + 1) * 512],
                                                                                               start=(fb == 0), stop=(fb == NFB - 1)),
                                          reads=[act.bs[fb], wd.bs[0 if fb < 11 else 1]], writes=[PS[ob[half]].b], inc=(fb == NFB - 1))
                            ot = outt[c % 2]
                            for half in range(2):
                                hs = slice(half * 512, (half + 1) * 512)
                                kb.op(DVE, lambda half=half, hs=hs, ob=ob, ot=ot, xx=xx: V.tensor_tensor(out=ot.t[:, hs], in0=xx.t[:, hs], in1=pf(ob[half]), op=ALU.add),
                                      reads=[xx.b, PS[ob[half]].b], writes=[ot.bs[half]])
                            kb.dma(SP, out[tok, :], ot.t[:], reads=ot.bs, is_out=True)

            with ExitStack() as sB:
                ynsaT = sb(sB, "ynsaT", [128, 4, S], BF16, NT)
                yretT = sb(sB, "yretT", [128, 4, S], BF16, NT)

                with ExitStack() as sA:
                    gq = sb(sA, "gq", [128, 64], F32)
                    gk = sb(sA, "gk", [128, 3, 64], F32)
                    QAL = sb(sA, "QAL", [128, NT, 8, 4], BF16)
                    KAL = sb(sA, "KAL", [128, NT, 4], BF16)
                    KCAL = sb(sA, "KCAL", [128, 4], BF16)
                    OH = sb(sA, "OH", [128, NT, 32], BF16)
                    dmask8 = sb(sA, "dmask8", [128, 8, 128], BF16)
                    tmask8 = sb(sA, "tmask8", [128, 8, 128], BF16)
                    cmask = sb(sA, "cmask", [128, S], BF16)
                    addc = sb(sA, "addc", [128, NT, 32], F32)
                    ov = sb(sA, "ov", [128, 32], BF16)
                    KT_slc = sb(sA, "KT_slc", [128, 2, S], BF16, NT)
                    KT_win = sb(sA, "KT_win", [128, 2, S], BF16, NT)
                    V_slc = sb(sA, "V_slc", [128, NT, 2, 65], BF16, NT)
                    V_win = sb(sA, "V_win", [128, NT, 2, 65], BF16, NT)
                    KcT = sb(sA, "KcT", [128, 2, 128], BF16)
                    Vc = sb(sA, "Vc", [128, 2, 97], BF16)

                    with ExitStack() as s0:
                        SL = sb(s0, "SL", [128, 8], F32)
                        th128 = sb(s0, "th128", [128, NT], F32)
                        pidx = sb(s0, "pidx", [128, 1], F32)
                        QALf = sb(s0, "QALf", [128, NT, 8, 4], F32)
                        KALf = sb(s0, "KALf", [128, NT, 4], F32)
                        KCALf = sb(s0, "KCALf", [128, 4], F32)
                        rel = sb(s0, "rel", [128, NT, 32], F32)
                        f0 = sb(s0, "f0", [128, NT, 32], F32)
                        f1 = sb(s0, "f1", [128, NT, 32], F32)
                        t1 = sb(s0, "t1", [128, NT, 32], F32)
                        hp = sb(s0, "hp", [128, 1], F32)
                        ovf = sb(s0, "ovf", [128, 32], F32)
                        ova = sb(s0, "ova", [128, 32], F32)
                        ones_b = sb(s0, "ones_b", [128, 512], BF16)
                        ones_b2 = sb(s0, "ones_b2", [128, 1024], BF16)
                        zeros_b = sb(s0, "zeros_b", [128, 512], BF16)

                        kb.dma(SP, gq.t[:], nsa_q_norm[0:1, :].broadcast_to([128, 64]), writes=[gq.b])
                        kb.dma(SP, gk.t[:].rearrange("p a b -> p (a b)"),
                               nsa_k_norm.rearrange("o a b -> o (a b)").broadcast_to([128, 192]), writes=[gk.b])
                        kb.op(DVE, lambda: V.tensor_scalar(out=gq.t[:], in0=gq.t[:], scalar1=0.125, scalar2=None, op0=ALU.mult),
                              reads=[gq.b], writes=[gq.b])
                        for h in range(8):
                            kb.op(POOL, lambda h=h: G.memset(SL.t[:, h:h + 1], 2.0 ** (-(h + 1))), writes=[SL.b])
                        kb.op(POOL, lambda: G.iota(th128.t[:], pattern=[[128, NT]], base=0, channel_multiplier=0,
                                                   allow_small_or_imprecise_dtypes=True), writes=[th128.b])
                        kb.op(POOL, lambda: G.iota(pidx.t[:], pattern=[[0, 1]], base=0, channel_multiplier=1,
                                                   allow_small_or_imprecise_dtypes=True), writes=[pidx.b])
                        SLb = SL.t[:].unsqueeze(1).broadcast_to([128, NT, 8])
                        THb = th128.t[:].unsqueeze(2).broadcast_to([128, NT, 8])
                        kb.op(DVE, lambda: V.scalar_tensor_tensor(out=QALf.t[:, :, :, 0], in0=THb, scalar=-1.0, in1=SLb,
                                                                  op0=ALU.mult, op1=ALU.mult),
                              reads=[SL.b, th128.b], writes=[QALf.b])
                        kb.op(DVE, lambda: V.tensor_scalar(out=QALf.t[:, :, :, 1], in0=SLb, scalar1=pidx.t[:, 0:1], scalar2=-1.0,
                                                           op0=ALU.mult, op1=ALU.mult),
                              reads=[SL.b, pidx.b], writes=[QALf.b])
                        kb.op(DVE, lambda: V.tensor_copy(out=QALf.t[:, :, :, 2], in_=SLb), reads=[SL.b], writes=[QALf.b])
                        kb.op(DVE, lambda: V.tensor_copy(out=QALf.t[:, :, :, 3], in_=SLb), reads=[SL.b], writes=[QALf.b])
                        kb.op(DVE, lambda: V.tensor_copy(out=QAL.t[:], in_=QALf.t[:]), reads=[QALf.b], writes=[QAL.b])
                        kb.op(POOL, lambda: G.memset(KALf.t[:, :, 0:2], 1.0), writes=[KALf.b])
                        kb.op(DVE, lambda: V.tensor_copy(out=KALf.t[:, :, 2], in_=th128.t[:]), reads=[th128.b], writes=[KALf.b])
                        kb.op(DVE, lambda: V.tensor_copy(out=KALf.t[:, :, 3], in_=pidx.t[:, 0:1].broadcast_to([128, NT])),
                              reads=[pidx.b], writes=[KALf.b])
                        kb.op(DVE, lambda: V.tensor_copy(out=KAL.t[:], in_=KALf.t[:]), reads=[KALf.b], writes=[KAL.b])
                        kb.op(POOL, lambda: G.memset(KCALf.t[:, 0:2], 1.0), writes=[KCALf.b])
                        kb.op(POOL, lambda: G.memset(KCALf.t[:, 3:4], 31.0), reads=[], writes=[KCALf.b])
                        kb.op(DVE, lambda: V.tensor_scalar(out=KCALf.t[:, 2:3], in0=pidx.t[:, 0:1], scalar1=16.0, scalar2=None,
                                                           op0=ALU.mult), reads=[pidx.b], writes=[KCALf.b])
                        kb.op(DVE, lambda: V.tensor_copy(out=KCAL.t[:], in_=KCALf.t[:]), reads=[KCALf.b], writes=[KCAL.b])
                        kb.op(POOL, lambda: G.memset(OH.t[:], 0.0), writes=[OH.b])
                        for kt in range(NT):
                            kb.op(POOL, lambda kt=kt: G.memset(OH.t[0:64, kt, 2 * kt:2 * kt + 1], 1.0), writes=[OH.b])
                            kb.op(POOL, lambda kt=kt: G.memset(OH.t[64:128, kt, 2 * kt + 1:2 * kt + 2], 1.0), writes=[OH.b])
                        kb.op(POOL, lambda: G.memset(ones_b.t[:], 1.0), writes=[ones_b.b])
                        kb.op(POOL, lambda: G.memset(zeros_b.t[:], 0.0), writes=[zeros_b.b])
                        ob8 = ones_b2.t[:].rearrange("p (a b) -> p a b", b=128)
                        kb.op(POOL, lambda: G.memset(ones_b2.t[:], 1.0), writes=[ones_b2.b])
                        kb.op(POOL, lambda: G.affine_select(out=dmask8.t[:], in_=ob8, pattern=[[0, 8], [1, 128]],
                                                            compare_op=ALU.is_ge, fill=0.0, base=0, channel_multiplier=-1),
                              reads=[ones_b2.b], writes=[dmask8.b])
                        kb.op(POOL, lambda: G.affine_select(out=tmask8.t[:], in_=ob8, pattern=[[0, 8], [-1, 128]],
                                                            compare_op=ALU.is_gt, fill=0.0, base=0, channel_multiplier=1),
                              reads=[ones_b2.b], writes=[tmask8.b])
                        for i in range(4):
                            kb.op(POOL, lambda i=i: G.affine_select(out=cmask.t[:, i * 512:(i + 1) * 512], in_=zeros_b.t[:],
                                                                    pattern=[[1, 512]], compare_op=ALU.is_ge, fill=NEG,
                                                                    base=-31 + 512 * i, channel_multiplier=-16),
                                  reads=[zeros_b.b], writes=[cmask.b])
                        kb.op(POOL, lambda: G.iota(rel.t[:], pattern=[[-2, NT], [1, 32]], base=0, channel_multiplier=0,
                                                   allow_small_or_imprecise_dtypes=True), writes=[rel.b])
                        kb.op(DVE, lambda: V.tensor_scalar(out=hp.t[:], in0=pidx.t[:], scalar1=64.0, scalar2=None, op0=ALU.is_ge),
                              reads=[pidx.b], writes=[hp.b])
                        kb.op(DVE, lambda: V.tensor_scalar(out=rel.t[:], in0=rel.t[:], scalar1=hp.t[:, 0:1], scalar2=None,
                                                           op0=ALU.subtract), reads=[rel.b, hp.b], writes=[rel.b])
                        kb.op(DVE, lambda: V.tensor_scalar(out=t1.t[:], in0=rel.t[:], scalar1=0.0, scalar2=-1e9,
                                                           op0=ALU.is_gt, op1=ALU.mult), reads=[rel.b], writes=[t1.b])
                        kb.op(DVE, lambda: V.tensor_scalar(out=f0.t[:], in0=rel.t[:], scalar1=0.0, scalar2=None, op0=ALU.is_equal),
                              reads=[rel.b], writes=[f0.b])
                        kb.op(DVE, lambda: V.tensor_scalar(out=f1.t[:], in0=rel.t[:], scalar1=-1.0, scalar2=None, op0=ALU.is_equal),
                              reads=[rel.b], writes=[f1.b])
                        kb.op(DVE, lambda: V.tensor_tensor(out=f0.t[:], in0=f0.t[:], in1=f1.t[:], op=ALU.max),
                              reads=[f0.b, f1.b], writes=[f0.b])
                        kb.op(DVE, lambda: V.memset(f0.t[:, :, 0:1], 1.0), reads=[], writes=[f0.b])
                        kb.op(DVE, lambda: V.scalar_tensor_tensor(out=addc.t[:], in0=f0.t[:], scalar=1e4, in1=t1.t[:],
                                                                  op0=ALU.mult, op1=ALU.add), reads=[f0.b, t1.b], writes=[addc.b])
                        kb.op(POOL, lambda: G.iota(ovf.t[:], pattern=[[-64, 32]], base=0, channel_multiplier=16,
                                                   allow_small_or_imprecise_dtypes=True), writes=[ovf.b])
                        kb.op(DVE, lambda: V.tensor_scalar(out=ova.t[:], in0=ovf.t[:], scalar1=63.0, scalar2=None, op0=ALU.is_le),
                              reads=[ovf.b], writes=[ova.b])
                        kb.op(DVE, lambda: V.tensor_scalar(out=ovf.t[:], in0=ovf.t[:], scalar1=-31.0, scalar2=None, op0=ALU.is_ge),
                              reads=[ovf.b], writes=[ovf.b])
                        kb.op(DVE, lambda: V.tensor_tensor(out=ov.t[:], in0=ova.t[:], in1=ovf.t[:], op=ALU.mult),
                              reads=[ova.b, ovf.b], writes=[ov.b])
                        kb.op(POOL, lambda: G.memset(V_slc.t[:, :, :, 64:65], 1.0), writes=V_slc.bs)
                        kb.op(POOL, lambda: G.memset(V_win.t[:, :, :, 64:65], 1.0), writes=V_win.bs)
                        kb.barrier()
                        chk(1)

                    with ExitStack() as s2:
                        cmpT = sb(s2, "cmpT", [128, 2, S], BF16, NT)
                        w1sb = sb(s2, "w1sb", [128, 2, 32, 128], BF16)
                        w2sb = sb(s2, "w2sb", [128, 2, 64], BF16)
                        pe_sb = sb(s2, "pe_sb", [32, 2, 64], F32)
                        with ExitStack() as s2a:
                            xt = [sb(s2a, f"xta{i}", [128, D], F32) for i in range(2)]
                            g1bc = sb(s2a, "g1bc", [128, D], F32)
                            kb.dma(SP, g1bc.t[:], norm1_g[0:1, :].broadcast_to([128, D]), writes=[g1bc.b])

                            def p1a_gen(c):
                                xx = xt[c % 2]
                                kb.dma(SP, xx.t[:], x[c * 128:(c + 1) * 128, :], writes=[xx.b])
                                yield
                                yield from norm_gen(xx, c, g1bc, c % 2, 6 + (c % 2))

                            wkv = sb(s2a, "wkv", [128, KC, 768], BF16)
                            load_w(wkv, w_in_v[:, :, C_KV:C_KV + 768])
                            for kv in range(2):
                                src = cmp_w1[0, kv].rearrange("l d f -> d l f")
                                kb.dma(POOL, w1sb.t[0:64, kv], src, writes=[w1sb.b])
                                kb.dma(POOL, w1sb.t[64:128, kv], src, writes=[w1sb.b])
                            kb.dma(POOL, w2sb.t[:], cmp_w2[0].rearrange("k f d -> f k d"), writes=[w2sb.b])
                            kb.dma(SP, pe_sb.t[:], cmp_pe[0].rearrange("k l d -> l k d"), writes=[pe_sb.b])
                            cmp_tok = [sb(s2a, f"cmp_tok{i}", [128, 256], BF16) for i in range(2)]
                            sqk = [sb(s2a, f"sqk{i}", [128, 256], F32) for i in range(2)]
                            kst = sb(s2a, "kst", [128, NT, 16], F32, NT)
                            tmpk = [sb(s2a, f"tmpk{i}", [128, 4, 64], F32) for i in range(2)]
                            ka_slc = [sb(s2a, f"ka_slc{i}", [128, 2, 128], BF16, 2) for i in range(2)]
                            ka_win = [sb(s2a, f"ka_win{i}", [128, 2, 128], BF16, 2) for i in range(2)]
                            for i in range(2):
                                kb.op(POOL, lambda i=i: G.memset(ka_slc[i].t[:], 0.0), writes=ka_slc[i].bs)
                                kb.op(POOL, lambda i=i: G.memset(ka_win[i].t[:], 0.0), writes=ka_win[i].bs)
                            def p1b_gen(c):
                                i2 = c % 2
                                bA, bB = (0, 1) if i2 == 0 else (2, 3)
                                tok = slice(c * 128, (c + 1) * 128)
                                if CUT >= 1:
                                    yield
                                    for k in range(KC):
                                        kb.op(PE, lambda k=k: TE.matmul(pf(bA), lhsT=hT.t[:, k, tok], rhs=wkv.t[:, k, 0:512],
                                                                        start=(k == 0), stop=(k == KC - 1)),
                                              reads=[hT.bs[c], wkv.b], writes=[PS[bA].b], inc=(k == KC - 1))
                                    for k in range(KC):
                                        kb.op(PE, lambda k=k: TE.matmul(pf(bB)[:, 0:256], lhsT=hT.t[:, k, tok], rhs=wkv.t[:, k, 512:768],
                                                                        start=(k == 0), stop=(k == KC - 1)),
                                              reads=[hT.bs[c], wkv.b], writes=[PS[bB].b], inc=(k == KC - 1))
                                if CUT >= 2:
                                    yield
                                    ct = cmp_tok[i2]
                                    kb.op(ACT, lambda: A.copy(out=ct.t[:], in_=pf(bA)[:, 0:256]), reads=[PS[bA].b], writes=[ct.b])
                                    sq = sqk[i2]
                                    kb.op(ACT, lambda: A.activation(out=sq.t[:, 0:128], in_=pf(bA)[:, 256:384], func=AF.Square),
                                          reads=[PS[bA].b], writes=[sq.b])
                                    kb.op(ACT, lambda: A.activation(out=sq.t[:, 128:256], in_=pf(bB)[:, 0:128], func=AF.Square),
                                          reads=[PS[bB].b], writes=[sq.b])
                                if CUT >= 3:
                                    yield
                                    ks = kst.t
                                    ksb = [kst.bs[c]]
                                    kb.op(DVE, lambda: V.tensor_reduce(out=ks[:, c, 0:4], in_=sq.t[:].rearrange("p (a b) -> p a b", b=64),
                                                                       axis=AX.X, op=ALU.add), reads=[sq.b], writes=ksb)
                                    rstd_from_ss(ks[:, c, 0:4], ks[:, c, 4:8], ks[:, c, 8:12], ks[:, c, 12:16], 64, ksb)
                                    tk = tmpk[i2]
                                    kb.op(DVE, lambda: V.tensor_tensor(out=tk.t[:, 0:2, :], in0=pf(bA)[:, 256:384].rearrange("p (a b) -> p a b", b=64),
                                                                       in1=ks[:, c, 12:14].unsqueeze(2).broadcast_to([128, 2, 64]), op=ALU.mult),
                                          reads=[PS[bA].b] + ksb, writes=[tk.b])
                                    kb.op(DVE, lambda: V.tensor_tensor(out=tk.t[:, 2:4, :], in0=pf(bB)[:, 0:128].rearrange("p (a b) -> p a b", b=64),
                                                                       in1=ks[:, c, 14:16].unsqueeze(2).broadcast_to([128, 2, 64]), op=ALU.mult),
                                          reads=[PS[bB].b] + ksb, writes=[tk.b])
                                    ksl, kwn = ka_slc[i2], ka_win[i2]
                                    kb.op(DVE, lambda: V.tensor_tensor(out=ksl.t[:, :, 0:64], in0=tk.t[:, 0:2, :],
                                                                       in1=gk.t[:, 1:2, :].broadcast_to([128, 2, 64]), op=ALU.mult),
                                          reads=[tk.b, gk.b], writes=[ksl.bs[0]])
                                    kb.op(DVE, lambda: V.tensor_tensor(out=kwn.t[:, :, 0:64], in0=tk.t[:, 2:4, :],
                                                                       in1=gk.t[:, 2:3, :].broadcast_to([128, 2, 64]), op=ALU.mult),
                                          reads=[tk.b, gk.b], writes=[kwn.bs[0]])
                                if CUT >= 4:
                                    yield
                                    kb.op(POOL, lambda: G.tensor_copy(out=ksl.t[:, :, 64:96], in_=OH.t[:, c:c + 1, :].broadcast_to([128, 2, 32])),
                                          reads=[OH.b], writes=[ksl.bs[1]])
                                    kb.op(POOL, lambda: G.tensor_copy(out=ksl.t[:, :, 96:100], in_=KAL.t[:, c:c + 1, :].broadcast_to([128, 2, 4])),
                                          reads=[KAL.b], writes=[ksl.bs[1]])
                                    kb.op(POOL, lambda: G.tensor_copy(out=kwn.t[:, :, 96:100], in_=KAL.t[:, c:c + 1, :].broadcast_to([128, 2, 4])),
                                          reads=[KAL.b], writes=[kwn.bs[1]])
                                if CUT >= 5:
                                    yield
                                    kb.op(ACT, lambda: A.copy(out=V_slc.t[:, c, :, 0:64], in_=pf(bA)[:, 384:512].rearrange("p (a b) -> p a b", b=64)),
                                          reads=[PS[bA].b], writes=[V_slc.bs[c]])
                                    kb.op(ACT, lambda: A.copy(out=V_win.t[:, c, :, 0:64], in_=pf(bB)[:, 128:256].rearrange("p (a b) -> p a b", b=64)),
                                          reads=[PS[bB].b], writes=[V_win.bs[c]])
                                if CUT >= 6:
                                    yield
                                    tb = 4 + i2
                                    for _ in range(XY3):
                                        yield
                                    transposes([ksl.t[:, 0, :], ksl.t[:, 1, :], kwn.t[:, 0, :], kwn.t[:, 1, :], ct.t[:, 0:128], ct.t[:, 128:256]],
                                               tb, ksl.bs + kwn.bs + [ct.b])
                                    pv3 = pb(tb).rearrange("p (a b) -> p a b", b=128)
                                    kb.op(ACT, lambda: A.copy(out=KT_slc.t[:, :, tok], in_=pv3[:, 0:2, :]), reads=[PS[tb].b], writes=[KT_slc.bs[c]])
                                    kb.op(ACT, lambda: A.copy(out=KT_win.t[:, :, tok], in_=pv3[:, 2:4, :]), reads=[PS[tb].b], writes=[KT_win.bs[c]])
                                    kb.op(ACT, lambda: A.copy(out=cmpT.t[:, :, tok], in_=pv3[:, 4:6, :]), reads=[PS[tb].b], writes=[cmpT.bs[c]])
                            def p1ab_gen(c):
                                yield from p1a_gen(c)
                                yield from p1b_gen(c)

                            run_interleaved([(lambda c=c: p1ab_gen(c)) for c in range(NT)], width=2)
                            if "KT_slc" in dbg:
                                kdb = sb(s2a, "kdb", [128, 2, S], F32)
                                kb.op(DVE, lambda: V.tensor_copy(out=kdb.t[:], in_=KT_slc.t[:]), reads=KT_slc.bs, writes=[kdb.b])
                                kb.dma(SP, dbg["KT_slc"].rearrange("p (a b) -> p a b", b=S), kdb.t[:], reads=[kdb.b], is_out=True)
                            kb.barrier()
                            chk(3)

                        with ExitStack() as s2b:
                            peT = sb(s2b, "peT", [64, 2, 32], BF16)
                            bias_c = sb(s2b, "bias_c", [128, 2], F32)
                            xhs = [sb(s2b, f"xh{i}", [128, 128], F32) for i in range(4)]
                            x2s = [sb(s2b, f"x2{i}", [128, 128], F32) for i in range(4)]
                            sgs_ = [sb(s2b, f"sgc{i}", [128, 128], F32) for i in range(4)]
                            HTbs = [sb(s2b, f"HTb{i}", [128, 128], BF16) for i in range(4)]
                            kca = sb(s2b, "kca", [128, 2, 128], BF16)
                            cst = sb(s2b, "cst", [128, 8], F32)
                            tmpcs = [sb(s2b, f"tmpc{i}", [128, 64], F32) for i in range(4)]
                            kb.op(POOL, lambda: G.memset(kca.t[:], 0.0), writes=[kca.b])
                            kb.op(POOL, lambda: G.memset(Vc.t[:], 0.0), writes=[Vc.b])
                            kb.op(POOL, lambda: G.tensor_copy(out=kca.t[:, :, 96:100], in_=KCAL.t[:].unsqueeze(1).broadcast_to([128, 2, 4])),
                                  reads=[KCAL.b], writes=[kca.b])
                            kb.op(POOL, lambda: G.memset(Vc.t[:, :, 64:65], 1.0), writes=[Vc.b])
                            kb.op(POOL, lambda: G.tensor_copy(out=Vc.t[:, :, 65:97], in_=ov.t[:].unsqueeze(1).broadcast_to([128, 2, 32])),
                                  reads=[ov.b], writes=[Vc.b])
                            for kv in range(2):
                                kb.op(PE, lambda kv=kv: TE.transpose(out=pf(0)[0:64, kv * 32:(kv + 1) * 32], in_=pe_sb.t[0:32, kv, :],
                                                                     identity=ident_f.t[0:32, 0:32]),
                                      reads=[pe_sb.b, ident_f.b], writes=[PS[0].b])
                            kb.op(DVE, lambda: V.tensor_copy(out=peT.t[:], in_=pf(0)[0:64, 0:64].rearrange("p (a b) -> p a b", b=32)),
                                  reads=[PS[0].b], writes=[peT.b])
                            for kv in range(2):
                                for l in range(32):
                                    kb.op(PE, lambda kv=kv, l=l: TE.matmul(pf(1)[:, kv:kv + 1], lhsT=w1sb.t[0:64, kv, l, :], rhs=peT.t[0:64, kv, l:l + 1],
                                                                           start=(l == 0), stop=(l == 31)),
                                          reads=[w1sb.b, peT.b], writes=[PS[1].b], inc=(l == 31))
                            kb.op(DVE, lambda: V.tensor_copy(out=bias_c.t[:], in_=pf(1)[:, 0:2]), reads=[PS[1].b], writes=[bias_c.b])
                            def cmp_gen(kv, g, idx):
                                bH, bO = 2 * idx, 2 * idx + 1
                                xh, x2, sg, HTb, tmpc = xhs[idx], x2s[idx], sgs_[idx], HTbs[idx], tmpcs[idx]
                                for l in range(32):
                                    kb.op(PE, lambda kv=kv, g=g, l=l: TE.matmul(
                                        pf(bH)[:, 0:127], lhsT=w1sb.t[g * 64:(g + 1) * 64, kv, l, :],
                                        rhs=cmpT.t[g * 64:(g + 1) * 64, kv, l:l + 16 * 126 + 1:16],
                                        start=(l == 0), stop=(l == 31)),
                                        reads=[w1sb.b] + cmpT.bs, writes=[PS[bH].b], inc=(l == 31))
                                yield
                                kb.op(ACT, lambda kv=kv: A.activation(out=xh.t[:, 0:127], in_=pf(bH)[:, 0:127], func=AF.Identity,
                                                                      bias=bias_c.t[:, kv:kv + 1], scale=1.0),
                                      reads=[PS[bH].b, bias_c.b], writes=[xh.b])
                                yield
                                kb.op(DVE, lambda: V.tensor_tensor(out=x2.t[:, 0:127], in0=xh.t[:, 0:127], in1=xh.t[:, 0:127], op=ALU.mult),
                                      reads=[xh.b], writes=[x2.b])
                                yield
                                kb.op(DVE, lambda: V.tensor_scalar(out=x2.t[:, 0:127], in0=x2.t[:, 0:127], scalar1=0.044715, scalar2=1.0,
                                                                   op0=ALU.mult, op1=ALU.add), reads=[x2.b], writes=[x2.b])
                                yield
                                kb.op(DVE, lambda: V.tensor_tensor(out=x2.t[:, 0:127], in0=x2.t[:, 0:127], in1=xh.t[:, 0:127], op=ALU.mult),
                                      reads=[x2.b, xh.b], writes=[x2.b])
                                yield
                                kb.op(ACT, lambda: A.activation(out=sg.t[:, 0:127], in_=x2.t[:, 0:127], func=AF.Sigmoid, scale=1.5957691216057308),
                                      reads=[x2.b], writes=[sg.b])
                                yield
                                kb.op(DVE, lambda: V.tensor_tensor(out=HTb.t[:, 0:127], in0=xh.t[:, 0:127], in1=sg.t[:, 0:127], op=ALU.mult),
                                      reads=[xh.b, sg.b], writes=[HTb.b])
                                yield
                                kb.op(PE, lambda kv=kv: TE.matmul(pf(bO)[0:127, 0:64], lhsT=HTb.t[:, 0:127], rhs=w2sb.t[:, kv, :], start=True, stop=True),
                                      reads=[HTb.b, w2sb.b], writes=[PS[bO].b])
                                yield
                                if kv == 0:
                                    kb.op(ACT, lambda g=g: A.activation(out=tmpc.t[0:127, :], in_=pf(bO)[0:127, 0:64], func=AF.Square,
                                                                        accum_out=cst.t[0:127, g:g + 1]),
                                          reads=[PS[bO].b], writes=[tmpc.b, cst.b])
                                    rstd_from_ss(cst.t[0:127, g:g + 1], cst.t[0:127, 2 + g:3 + g], cst.t[0:127, 4 + g:5 + g], cst.t[0:127, 6 + g:7 + g], 64, [cst.b])
                                    kb.op(DVE, lambda g=g: V.scalar_tensor_tensor(out=kca.t[0:127, g, 0:64], in0=pf(bO)[0:127, 0:64],
                                                                                  scalar=cst.t[0:127, 6 + g:7 + g], in1=gk.t[0:127, 0, :],
                                                                                  op0=ALU.mult, op1=ALU.mult),
                                          reads=[PS[bO].b, cst.b, gk.b], writes=[kca.b])
                                else:
                                    kb.op(ACT, lambda g=g: A.copy(out=Vc.t[0:127, g, 0:64], in_=pf(bO)[0:127, 0:64]),
                                          reads=[PS[bO].b], writes=[Vc.b])
                                yield

                            run_interleaved([(lambda kv=kv, g=g: cmp_gen(kv, g, 2 * kv + g)) for kv in range(2) for g in range(2)], width=4)
                            transposes([kca.t[:, 0, :], kca.t[:, 1, :]], 6, [kca.b])
                            kb.op(ACT, lambda: A.copy(out=KcT.t[:], in_=pb(6)[:, 0:256].rearrange("p (a b) -> p a b", b=128)),
                                  reads=[PS[6].b], writes=[KcT.b])
                            if "kc" in dbg:
                                kcd = sb(s2b, "kcd", [128, 2, 64], F32)
                                kb.op(DVE, lambda: V.tensor_copy(out=kcd.t[:], in_=kca.t[:, :, 0:64]), reads=[kca.b], writes=[kcd.b])
                                kb.dma(SP, dbg["kc"].rearrange("p (a b) -> p a b", b=64), kcd.t[:], reads=[kcd.b], is_out=True)
                            if "vc" in dbg:
                                vcd = sb(s2b, "vcd", [128, 2, 64], F32)
                                kb.op(DVE, lambda: V.tensor_copy(out=vcd.t[:], in_=Vc.t[:, :, 0:64]), reads=[Vc.b], writes=[vcd.b])
                                kb.dma(SP, dbg["vc"].rearrange("p (a b) -> p a b", b=64), vcd.t[:], reads=[vcd.b], is_out=True)
                            kb.barrier()
                            chk(4)

                    phase_1c()
                    kb.barrier()
                    chk(5)

                wm = sb(sB, "wm", [128, KC, 2048], BF16, 4)
                wbr = sb(sB, "wbr", [128, 8, D], BF16)
                phase_1d()
                kb.barrier()
                chk(6)
                phase_1e()
                kb.barrier()
                chk(7)

            phase_2()
            kb.finish()
    except _Stop:
        pass
    return nc


_NAMES = ["x", "norm1_g", "w_in", "nsa_q_norm", "nsa_k_norm", "cmp_pe", "cmp_w1", "cmp_w2", "ret_gn_g",
          "w_branch", "w_out", "norm2_g", "ffn_w_gate", "ffn_w_up", "ffn_w_down"]


def kernel(**inputs):
    n = 8
    arrs = {k: np.ascontiguousarray(np.asarray(inputs[k], dtype=np.float32)) for k in _NAMES}
    nc = build_nc()
    in_maps = []
    for i in range(n):
        m = {k: arrs[k] for k in _NAMES if k != "x"}
        m["x"] = np.ascontiguousarray(arrs["x"][i])
        in_maps.append(m)
    res = run_bass_kernel_spmd(nc, in_maps, core_ids=list(range(n)))
    return np.stack([np.asarray(r["out"], dtype=np.float32) for r in res.results], axis=0)
```

```python
import numpy as np
from contextlib import ExitStack
import concourse.bass as bass
import concourse.mybir as mybir
from concourse.bass_utils import run_bass_kernel_spmd

F32 = mybir.dt.float32
BF16 = mybir.dt.bfloat16
AF = mybir.ActivationFunctionType
ALU = mybir.AluOpType
AX = mybir.AxisListType

S = 2048
D = 1024
NT = 16
KC = 8
N_IN = 5400
DFF = 2816
NFB = 22
EPS = 1e-6
SEM_LIMIT = 24000
import os as _os
CUT = int(_os.environ.get('P1B_CUT', '99'))
CUTC = int(_os.environ.get('P1C_CUT', '99'))
SUBC = int(_os.environ.get('P1C_SUB', '99'))
WIDTH = int(_os.environ.get('P1C_WIDTH', '2'))
XY1 = int(_os.environ.get('XY1', '0'))
XY2 = int(_os.environ.get('XY2', '2'))
XY3 = int(_os.environ.get('XY3', '0'))
PREWIN = int(_os.environ.get('PREWIN', '1'))
QDVE = int(_os.environ.get('QDVE', '1'))
NUDVE = int(_os.environ.get('NUDVE', '1'))
NEG = -30000.0

C_Q = 0
C_KV = 512
C_G = 1280
C_R = 1304
C_M = 3352


class Buf:
    __slots__ = ("name", "w", "r", "excl")

    def __init__(self, name):
        self.name = name
        self.w = None
        self.r = []
        self.excl = False


class SemW:
    __slots__ = ("h",)

    def __init__(self, h):
        self.h = h


class Slot:
    __slots__ = ("sem", "val")

    def __init__(self, sem):
        self.sem = sem
        self.val = 0


class Q:
    def __init__(self, name, eng):
        self.name = name
        self.eng = eng
        self.sem = None
        self.count = 0
        self.waited = {}
        self.ring = []
        self.ri = 0
        self.pending = False


class T:
    def __init__(self, t, name, nb=1):
        self.t = t
        self.bs = [Buf(f"{name}{i}") for i in range(nb)]

    @property
    def b(self):
        return self.bs[0]


class KB:
    def __init__(self, nc, es):
        self.nc = nc
        self.es = es
        self.nsem = 0
        self.pe = self.mkq("pe", nc.tensor)
        self.act = self.mkq("act", nc.scalar)
        self.dve = self.mkq("dve", nc.vector)
        self.pool = self.mkq("pool", nc.gpsimd)
        self.sp = self.mkq("sp", nc.sync)
        self.qs = [self.pe, self.act, self.dve, self.pool, self.sp]
        for q, n in ((self.sp, 16), (self.pool, 8), (self.act, 4)):
            q.ring = [Slot(self.new_sem(f"{q.name}_d{i}")) for i in range(n)]
        self.out_toks = []

    def new_sem(self, name):
        self.nsem += 1
        return SemW(self.es.enter_context(self.nc.semaphore(f"{name}_{self.nsem}")))

    def mkq(self, name, eng):
        q = Q(name, eng)
        q.sem = self.new_sem(name)
        return q

    def wait(self, q, tok):
        sw, val = tok[0], tok[1]
        if q.waited.get(sw, 0) >= val:
            return
        q.eng.wait_ge(sw.h, val)
        q.waited[sw] = val

    def _dep(self, q, tok, raw, force=False):
        if tok[2] is q and q is self.pe and not force:
            return
        self.wait(q, tok)

    def _deps(self, q, reads, writes, force=False):
        for b in reads:
            if b.w is not None:
                self._dep(q, b.w, True, force)
            if b.excl:
                for t in b.r:
                    if t[2] is not q:
                        self._dep(q, t, False, force)
        for b in writes:
            if b.w is not None:
                self._dep(q, b.w, False, force)
            for t in b.r:
                self._dep(q, t, False, force)

    def _record(self, tok, reads, writes):
        for b in reads:
            if tok[2] is not None:
                b.r = [t for t in b.r if t[2] is not tok[2]]
            b.r.append(tok)
        for b in writes:
            b.w = tok
            b.r = []

    def op(self, q, fn, reads=(), writes=(), inc=True):
        self._deps(q, reads, writes)
        ins = fn()
        if inc:
            if q.count >= SEM_LIMIT and not q.pending:
                q.sem = self.new_sem(q.name)
                q.count = 0
            ins.then_inc(q.sem.h, 1)
            q.count += 1
            q.pending = False
            tok = (q.sem, q.count, q)
        else:
            q.pending = True
            tok = (q.sem, q.count + 1, q)
        self._record(tok, reads, writes)
        return ins

    def dma(self, q, out, in_, reads=(), writes=(), is_out=False):
        self._deps(q, reads, writes, force=True)
        slot = q.ring[q.ri % len(q.ring)]
        q.ri += 1
        if slot.val > 0:
            self.wait(q, (slot.sem, slot.val))
        if slot.val >= SEM_LIMIT:
            slot.sem = self.new_sem(q.name + "_d")
            slot.val = 0
        ins = q.eng.dma_start(out=out, in_=in_)
        ins.then_inc(slot.sem.h, 16)
        slot.val += 16
        tok = (slot.sem, slot.val, None)
        self._record(tok, reads, writes)
        if is_out:
            self.out_toks.append(tok)
        return tok

    def barrier(self):
        toks = []
        for o in self.qs:
            if o.count > 0:
                toks.append((o.sem, o.count, o))
            for sl in o.ring:
                if sl.val > 0:
                    toks.append((sl.sem, sl.val, None))
        for q in self.qs:
            for t in toks:
                if t[2] is q:
                    continue
                self.wait(q, t)

    def finish(self):
        for t in self.out_toks:
            self.wait(self.sp, t)


class _Stop(Exception):
    pass


def build_nc(debug=None, stop=None):
    nc = bass.Bass("TRN2", target_bir_lowering=False)

    def din(name, shape):
        return nc.dram_tensor(name, list(shape), F32, kind="ExternalInput").ap()

    x = din("x", [S, D])
    norm1_g = din("norm1_g", [1, D])
    w_in = din("w_in", [1, D, N_IN])
    nsa_q_norm = din("nsa_q_norm", [1, 64])
    nsa_k_norm = din("nsa_k_norm", [1, 3, 64])
    cmp_pe = din("cmp_pe", [1, 2, 32, 64])
    cmp_w1 = din("cmp_w1", [1, 2, 32, 64, 128])
    cmp_w2 = din("cmp_w2", [1, 2, 128, 64])
    ret_gn_g = din("ret_gn_g", [1, 4, 128])
    w_branch = din("w_branch", [1, 2, 512, D])
    w_out = din("w_out", [1, D, D])
    norm2_g = din("norm2_g", [1, D])
    ffn_w_gate = din("ffn_w_gate", [1, D, DFF])
    ffn_w_up = din("ffn_w_up", [1, D, DFF])
    ffn_w_down = din("ffn_w_down", [1, DFF, D])
    out = nc.dram_tensor("out", [S, D], F32, kind="ExternalOutput").ap()
    xmid = nc.dram_tensor("xmid", [S, D], F32, kind="Internal").ap()
    dbg = {}
    if debug:
        for name, shape in debug.items():
            dbg[name] = nc.dram_tensor("dbg_" + name, list(shape), F32, kind="ExternalOutput").ap()

    w_in_v = w_in[0].rearrange("(k p) n -> p k n", p=128)

    try:
        with ExitStack() as es:
            kb = KB(nc, es)

            def chk(n):
                if stop is not None and n >= stop:
                    kb.barrier()
                    kb.finish()
                    raise _Stop()
            PE, ACT, DVE, POOL, SP = kb.pe, kb.act, kb.dve, kb.pool, kb.sp
            V, A, G, TE = nc.vector, nc.scalar, nc.gpsimd, nc.tensor

            def sb(scope, name, shape, dt, nb=1):
                return T(scope.enter_context(nc.sbuf_tensor(name, list(shape), dt)), name, nb)

            PS2 = [es.enter_context(nc.psum_tensor(f"psp{j}", [128, 1024], F32)) for j in range(4)]
            PS = [T(None, f"ps{i}") for i in range(8)]
            for p_ in PS:
                p_.b.excl = True

            def pf(i):
                return PS2[i // 2][:, (i % 2) * 512:(i % 2 + 1) * 512]

            def pb(i):
                return PS2[i // 2][:].bitcast(BF16)[:, (i % 2) * 1024:(i % 2 + 1) * 1024]

            def pf2(j):
                return PS2[j][:]

            ident_f = sb(es, "ident_f", [128, 128], F32)
            ident_b = sb(es, "ident_b", [128, 128], BF16)
            ones_f = sb(es, "ones_f", [128, 128], F32)
            nhalf = sb(es, "nhalf", [128, 16], F32)
            hT = sb(es, "hT", [128, KC, S], BF16, NT)
            stat = sb(es, "stat", [128, NT, 4], F32, NT)
            hb = [sb(es, f"hb{i}", [128, D], BF16) for i in range(2)]
            junk = sb(es, "junk", [128, D], BF16)

            kb.op(POOL, lambda: G.memset(ones_f.t[:], 1.0), writes=[ones_f.b])
            kb.op(POOL, lambda: G.memset(nhalf.t[:], -0.5), writes=[nhalf.b])
            kb.op(POOL, lambda: G.affine_select(out=ident_f.t[:], in_=ones_f.t[:, 0:128], pattern=[[1, 128]],
                                                compare_op=ALU.is_equal, fill=0.0, base=0, channel_multiplier=-1),
                  reads=[ones_f.b], writes=[ident_f.b])
            kb.op(DVE, lambda: V.tensor_copy(out=ident_b.t[:], in_=ident_f.t[:]), reads=[ident_f.b], writes=[ident_b.b])

            def rstd_from_ss(ss_ap, ms_ap, sd_ap, rs_ap, n, bufs):
                k = ms_ap.shape[-1]
                P_ = ms_ap.shape[0]
                kb.op(DVE, lambda: V.tensor_scalar(out=ms_ap, in0=ss_ap, scalar1=1.0 / n, scalar2=EPS,
                                                   op0=ALU.mult, op1=ALU.add), reads=bufs, writes=bufs)
                kb.op(POOL, lambda: G.tensor_tensor(out=rs_ap, in0=ms_ap, in1=nhalf.t[0:P_, 0:k], op=ALU.pow),
                      reads=list(bufs) + [nhalf.b], writes=bufs)

            def transposes(src_aps, bank, reads):
                pbv = pb(bank)
                n = len(src_aps)
                for i, ap in enumerate(src_aps):
                    kb.op(PE, lambda ap=ap, i=i: TE.transpose(out=pbv[:, i * 128:(i + 1) * 128], in_=ap, identity=ident_b.t[:]),
                          reads=list(reads) + [ident_b.b], writes=[PS[bank].b], inc=(i == n - 1))

            def norm_gen(src, c, gbc, sidx, bank):
                sbuf_ = [stat.bs[c]]
                st = stat.t
                kb.op(ACT, lambda: A.activation(out=junk.t[:], in_=src.t[:], func=AF.Square, accum_out=st[:, c, 0:1]),
                      reads=src.bs, writes=[junk.b] + sbuf_)
                yield
                kb.op(DVE, lambda: V.tensor_scalar(out=st[:, c, 1:2], in0=st[:, c, 0:1], scalar1=1.0 / D, scalar2=EPS,
                                                   op0=ALU.mult, op1=ALU.add), reads=sbuf_, writes=sbuf_)
                yield
                kb.op(POOL, lambda: G.tensor_tensor(out=st[:, c, 3:4], in0=st[:, c, 1:2], in1=nhalf.t[:, 0:1], op=ALU.pow),
                      reads=sbuf_ + [nhalf.b], writes=sbuf_)
                yield
                h = hb[sidx % 2]
                kb.op(DVE, lambda: V.scalar_tensor_tensor(out=h.t[:], in0=src.t[:], scalar=st[:, c, 3:4], in1=gbc.t[:],
                                                          op0=ALU.mult, op1=ALU.mult),
                      reads=src.bs + [gbc.b] + sbuf_, writes=[h.b])
                yield
                for _ in range(XY3):
                    yield
                transposes([h.t[:, k * 128:(k + 1) * 128] for k in range(KC)], bank, [h.b])
                yield
                kb.op(ACT, lambda: A.copy(out=hT.t[:, :, c * 128:(c + 1) * 128],
                                          in_=pb(bank).rearrange("p (a b) -> p a b", b=128)),
                      reads=[PS[bank].b], writes=[hT.bs[c]])
                yield

            def run_interleaved(gen_fns, width=2):
                pending = list(gen_fns)
                active = []
                while pending or active:
                    while pending and len(active) < width:
                        active.append(pending.pop(0)())
                    for g_ in list(active):
                        try:
                            next(g_)
                        except StopIteration:
                            active.remove(g_)

            def load_w(dst, src_ap, q=None):
                kb.dma(q or POOL, dst.t[:], src_ap, writes=[dst.b])

            def dbg_store(name, src_ap, rows, reads):
                if name in dbg:
                    kb.dma(SP, dbg[name][rows], src_ap, reads=reads, is_out=True)

            def phase_1c():
                with ExitStack() as s3:
                    wq = sb(s3, "wq", [128, KC, 512], BF16)
                    load_w(wq, w_in_v[:, :, C_Q:C_Q + 512])
                    wg = sb(s3, "wg", [128, KC, 24], BF16)
                    load_w(wg, w_in_v[:, :, C_G:C_G + 24])
                    sqq = [sb(s3, f"sqq{i}", [128, 512], F32) for i in range(2)]
                    qst = sb(s3, "qst", [128, NT, 32], F32, NT)
                    tmpq = [sb(s3, f"tmpq{i}", [128, 8, 64], F32) for i in range(2)]
                    qaug = [sb(s3, f"qaug{i}", [128, 8, 128], BF16, 4) for i in range(2)]
                    qT = [sb(s3, f"qT{i}", [128, 8, 128], BF16) for i in range(2)]
                    qT2 = [sb(s3, f"qT2{i}", [128, 8, 128], BF16) for i in range(2)]
                    gate = [sb(s3, f"gate{i}", [128, 24], F32) for i in range(2)]
                    scl = [sb(s3, f"scl{i}", [128, 1024], F32) for i in range(2)]
                    NPT = 3
                    PT = [[sb(s3, f"PT{i}_{j}", [128, 1024], BF16, 2) for j in range(NPT)] for i in range(2)]
                    oTs = [sb(s3, f"oTs{i}", [128, 1024], F32, 2) for i in range(2)]
                    num = [sb(s3, f"num{i}", [128, 3, 8, 64], F32) for i in range(2)]
                    den = [sb(s3, f"den{i}", [128, 3, 8], F32) for i in range(2)]
                    rdc = [sb(s3, f"rdc{i}", [128, 8], F32) for i in range(2)]
                    impn = [sb(s3, f"impn{i}", [128, 8, 32], F32) for i in range(2)]
                    imp = [sb(s3, f"imp{i}", [128, 2, 32], F32) for i in range(2)]
                    top8 = [sb(s3, f"top8{i}", [128, 2, 8], F32, 2) for i in range(2)]
                    rd = [sb(s3, f"rd{i}", [128, 3, 8], F32) for i in range(2)]
                    coef = [sb(s3, f"coef{i}", [128, 3, 8], F32) for i in range(2)]
                    oacc = [sb(s3, f"oacc{i}", [128, 8, 64], F32) for i in range(2)]
                    otmp = [sb(s3, f"otmp{i}", [128, 8, 64], F32) for i in range(2)]
                    ytok = [sb(s3, f"ytok{i}", [128, 512], BF16) for i in range(2)]
                    ydbg = sb(s3, "ydbg", [128, 512], F32) if "y_nsa" in dbg else None
                    for i in range(2):
                        kb.op(POOL, lambda i=i: G.memset(qaug[i].t[:], 0.0), writes=qaug[i].bs)
                    ptc = [0, 0]

                    def tile_gen(c):
                        i2 = c % 2
                        base = 4 * i2
                        bZ, bG = base, base + 1
                        bO0, bO1 = base + 2, base + 3
                        Sb = [PS[bZ].b, PS[bG].b]
                        Ob = [PS[bO0].b, PS[bO1].b]
                        tok = slice(c * 128, (c + 1) * 128)

                        def S2():
                            return pf2(base // 2)

                        def O2():
                            return pf2(base // 2 + 1)

                        def X8():
                            return O2().rearrange("p (h c) -> p h c", h=8)

                        def to_token_major(br, ncol):
                            nu, de = num[i2], den[i2]
                            ot = oTs[i2]
                            kb.op(DVE, lambda: V.tensor_scalar(out=ot.t[0:ncol, 0:512], in0=O2()[0:ncol, 0:512], scalar1=1.0, scalar2=None, op0=ALU.mult),
                                  reads=[Ob[0]], writes=[ot.bs[0]])
                            kb.op(ACT, lambda: A.copy(out=ot.t[0:ncol, 512:1024], in_=O2()[0:ncol, 512:1024]),
                                  reads=[Ob[1]], writes=[ot.bs[1]])
                            yield
                            for _ in range(XY2):
                                yield
                            for hh in range(8):
                                kb.op(PE, lambda hh=hh: TE.transpose(out=X8()[:, hh, 0:ncol], in_=ot.t[0:ncol, hh * 128:(hh + 1) * 128],
                                                                     identity=ident_f.t[0:ncol, 0:ncol]),
                                      reads=[ot.bs[hh // 4], ident_f.b], writes=[Ob[hh // 4]], inc=(hh % 4 == 3))
                            yield
                            if NUDVE:
                                kb.op(DVE, lambda: V.tensor_scalar(out=nu.t[:, br, :, :], in0=X8()[:, :, 0:64], scalar1=1.0, scalar2=None, op0=ALU.mult), reads=Ob, writes=[nu.b])
                            else:
                                kb.op(ACT, lambda: A.copy(out=nu.t[:, br, :, :], in_=X8()[:, :, 0:64]), reads=Ob, writes=[nu.b])
                            yield
                            kb.op(DVE, lambda: V.tensor_scalar(out=de.t[:, br, :], in0=X8()[:, :, 64], scalar1=1e-30, scalar2=None, op0=ALU.max),
                                  reads=Ob, writes=[de.b])
                            yield

                        def branch(br, KT, VV, kts, qsrc, pre=False):
                            n = len(kts)

                            def scores(kt):
                                for g in range(2):
                                    kb.op(PE, lambda g=g: TE.matmul(S2()[:, g * 512:(g + 1) * 512], lhsT=KT.t[:, g, kt * 128:(kt + 1) * 128],
                                                                    rhs=qsrc.t[:, 4 * g:4 * g + 4, :], start=True, stop=True),
                                          reads=[KT.bs[kt], qsrc.b], writes=[Sb[g]])

                            if not pre:
                                scores(kts[0])
                                yield
                            for j, kt in enumerate(kts):
                                pt = PT[i2][ptc[i2] % NPT]
                                ptc[i2] += 1
                                for g in range(2):
                                    kb.op(ACT, lambda pt=pt, g=g: A.activation(out=pt.t[:, g * 512:(g + 1) * 512], in_=S2()[:, g * 512:(g + 1) * 512], func=AF.Exp),
                                          reads=[Sb[g]], writes=[pt.bs[g]])
                                yield
                                if j + 1 < n:
                                    for _ in range(XY1):
                                        yield
                                    scores(kts[j + 1])
                                    yield
                                if kt == c:
                                    kb.op(DVE, lambda pt=pt: V.tensor_tensor(out=pt.t[:], in0=pt.t[:], in1=dmask8.t[:].rearrange("p a b -> p (a b)"), op=ALU.mult),
                                          reads=pt.bs + [dmask8.b], writes=pt.bs)
                                    yield
                                elif br == 2 and kt == c - 4:
                                    kb.op(DVE, lambda pt=pt: V.tensor_tensor(out=pt.t[:], in0=pt.t[:], in1=tmask8.t[:].rearrange("p a b -> p (a b)"), op=ALU.mult),
                                          reads=pt.bs + [tmask8.b], writes=pt.bs)
                                    yield
                                for g in range(2):
                                    kb.op(PE, lambda kt=kt, pt=pt, j=j, g=g: TE.matmul(O2()[0:65, g * 512:(g + 1) * 512], lhsT=VV.t[:, kt, g, :],
                                                                                       rhs=pt.t[:, g * 512:(g + 1) * 512], start=(j == 0), stop=(j == n - 1)),
                                          reads=[pt.bs[g], VV.bs[kt]], writes=[Ob[g]], inc=(j == n - 1))
                                yield
                            yield from to_token_major(br, 65)

                        for k in range(KC):
                            kb.op(PE, lambda k=k: TE.matmul(pf(bZ), lhsT=hT.t[:, k, tok], rhs=wq.t[:, k, :], start=(k == 0), stop=(k == KC - 1)),
                                  reads=[hT.bs[c], wq.b], writes=[PS[bZ].b], inc=(k == KC - 1))
                        for k in range(KC):
                            kb.op(PE, lambda k=k: TE.matmul(pf(bG)[:, 0:24], lhsT=hT.t[:, k, tok], rhs=wg.t[:, k, :], start=(k == 0), stop=(k == KC - 1)),
                                  reads=[hT.bs[c], wg.b], writes=[PS[bG].b], inc=(k == KC - 1))
                        yield
                        sq = sqq[i2]
                        kb.op(ACT, lambda: A.activation(out=sq.t[:], in_=pf(bZ), func=AF.Square), reads=[PS[bZ].b], writes=[sq.b])
                        gt = gate[i2]
                        kb.op(ACT, lambda: A.activation(out=gt.t[:], in_=pf(bG)[:, 0:24], func=AF.Tanh, scale=0.5), reads=[PS[bG].b], writes=[gt.b])
                        yield
                        kb.op(DVE, lambda: V.tensor_scalar(out=gt.t[:], in0=gt.t[:], scalar1=0.5, scalar2=0.5, op0=ALU.mult, op1=ALU.add),
                              reads=[gt.b], writes=[gt.b])
                        qs = qst.t
                        qsb = [qst.bs[c]]
                        kb.op(DVE, lambda: V.tensor_reduce(out=qs[:, c, 0:8], in_=sq.t[:].rearrange("p (a b) -> p a b", b=64), axis=AX.X, op=ALU.add),
                              reads=[sq.b], writes=qsb)
                        yield
                        kb.op(DVE, lambda: V.tensor_scalar(out=qs[:, c, 8:16], in0=qs[:, c, 0:8], scalar1=1.0 / 64, scalar2=EPS, op0=ALU.mult, op1=ALU.add), reads=qsb, writes=qsb)
                        yield
                        kb.op(POOL, lambda: G.tensor_tensor(out=qs[:, c, 24:32], in0=qs[:, c, 8:16], in1=nhalf.t[:, 0:8], op=ALU.pow),
                              reads=qsb + [nhalf.b], writes=qsb)
                        yield
                        tq = tmpq[i2]
                        qa = qaug[i2]
                        kb.op(DVE, lambda: V.tensor_tensor(out=tq.t[:], in0=pf(bZ).rearrange("p (a b) -> p a b", b=64),
                                                           in1=qs[:, c, 24:32].unsqueeze(2).broadcast_to([128, 8, 64]), op=ALU.mult),
                              reads=[PS[bZ].b] + qsb, writes=[tq.b])
                        yield
                        kb.op(DVE, lambda: V.tensor_tensor(out=qa.t[:, :, 0:64], in0=tq.t[:], in1=gq.t[:].unsqueeze(1).broadcast_to([128, 8, 64]), op=ALU.mult),
                              reads=[tq.b, gq.b], writes=[qa.bs[0]])
                        kb.op(POOL, lambda: G.tensor_copy(out=qa.t[:, :, 96:100], in_=QAL.t[:, c, :, :]), reads=[QAL.b], writes=[qa.bs[1]])
                        yield
                        transposes([qa.t[:, h, :] for h in range(8)], bZ, qa.bs)
                        yield
                        q1 = qT[i2]
                        if QDVE:
                            kb.op(DVE, lambda: V.tensor_copy(out=q1.t[:], in_=pb(bZ).rearrange("p (a b) -> p a b", b=128)), reads=[PS[bZ].b], writes=[q1.b])
                        else:
                            kb.op(ACT, lambda: A.copy(out=q1.t[:], in_=pb(bZ).rearrange("p (a b) -> p a b", b=128)), reads=[PS[bZ].b], writes=[q1.b])
                        yield
                        nu, de = num[i2], den[i2]
                        rdc_, impn_, imp_, top8_ = rdc[i2], impn[i2], imp[i2], top8[i2]
                        sc_ = scl[i2]
                        pc = PT[i2][ptc[i2] % NPT]
                        ptc[i2] += 1
                        for g in range(2):
                            kb.op(PE, lambda g=g: TE.matmul(S2()[0:127, g * 512:(g + 1) * 512], lhsT=KcT.t[:, g, 0:127], rhs=q1.t[:, 4 * g:4 * g + 4, :], start=True, stop=True),
                                  reads=[KcT.b, q1.b], writes=[Sb[g]])
                        yield
                        kb.op(DVE, lambda: V.scalar_tensor_tensor(out=sc_.t[0:127, :].rearrange("p (a b) -> p a b", b=128),
                                                                  in0=S2()[0:127, :].rearrange("p (a b) -> p a b", b=128), scalar=60.0,
                                                                  in1=cmask.t[0:127, tok].unsqueeze(1).broadcast_to([127, 8, 128]),
                                                                  op0=ALU.min, op1=ALU.add),
                              reads=Sb + [cmask.b], writes=[sc_.b])
                        yield
                        if PREWIN:
                            kt0 = max(0, c - 4)
                            for g in range(2):
                                kb.op(PE, lambda g=g: TE.matmul(S2()[:, g * 512:(g + 1) * 512], lhsT=KT_win.t[:, g, kt0 * 128:(kt0 + 1) * 128],
                                                                rhs=q1.t[:, 4 * g:4 * g + 4, :], start=True, stop=True),
                                      reads=[KT_win.bs[kt0], q1.b], writes=[Sb[g]])
                            yield
                        kb.op(ACT, lambda: A.activation(out=pc.t[0:127, :], in_=sc_.t[0:127, :], func=AF.Exp), reads=[sc_.b], writes=pc.bs)
                        yield
                        for g in range(2):
                            kb.op(PE, lambda g=g: TE.matmul(O2()[0:97, g * 512:(g + 1) * 512], lhsT=Vc.t[0:127, g, :], rhs=pc.t[0:127, g * 512:(g + 1) * 512], start=True, stop=True),
                                  reads=[pc.bs[g], Vc.b], writes=[Ob[g]])
                        yield
                        yield from to_token_major(0, 97)
                        kb.op(DVE, lambda: V.reciprocal(out=rdc_.t[:], in_=de.t[:, 0, :]), reads=[de.b], writes=[rdc_.b])
                        yield
                        kb.op(DVE, lambda: V.tensor_tensor(out=impn_.t[:], in0=X8()[:, :, 65:97],
                                                           in1=rdc_.t[:].unsqueeze(2).broadcast_to([128, 8, 32]), op=ALU.mult),
                              reads=Ob + [rdc_.b], writes=[impn_.b])
                        yield
                        q2 = qT2[i2]

                        def imp_chain():
                            kb.op(DVE, lambda: V.tensor_reduce(out=imp_.t[:], in_=impn_.t[:].rearrange("p (g r) j -> p g j r", g=2), axis=AX.X, op=ALU.add),
                                  reads=[impn_.b], writes=[imp_.b])
                            yield
                            kb.op(DVE, lambda: V.tensor_tensor(out=imp_.t[:], in0=imp_.t[:], in1=addc.t[:, c:c + 1, :].broadcast_to([128, 2, 32]), op=ALU.add),
                                  reads=[imp_.b, addc.b], writes=[imp_.b])
                            yield
                            for g in range(2):
                                kb.op(DVE, lambda g=g: V.max(out=top8_.t[:, g, :], in_=imp_.t[:, g, :]), reads=[imp_.b], writes=[top8_.bs[g]])
                            yield
                            for g in range(2):
                                kb.op(DVE, lambda g=g: V.tensor_scalar(out=qa.t[:, 4 * g:4 * g + 4, 64:96],
                                                                       in0=imp_.t[:, g:g + 1, :].broadcast_to([128, 4, 32]),
                                                                       scalar1=top8_.t[:, g, 7:8], scalar2=NEG, op0=ALU.is_lt, op1=ALU.mult),
                                      reads=[imp_.b, top8_.bs[g]], writes=[qa.bs[2 + g]])
                            yield

                        gw = branch(2, KT_win, V_win, list(range(max(0, c - 4), c + 1)), q1, pre=bool(PREWIN))
                        gi = imp_chain()
                        live = [gw, gi]
                        while live:
                            for g_ in list(live):
                                try:
                                    next(g_)
                                except StopIteration:
                                    live.remove(g_)
                            yield
                        for _ in range(XY3):
                            yield
                        transposes([qa.t[:, h, :] for h in range(8)], bZ, qa.bs)
                        if QDVE:
                            kb.op(DVE, lambda: V.tensor_copy(out=q2.t[:], in_=pb(bZ).rearrange("p (a b) -> p a b", b=128)), reads=[PS[bZ].b], writes=[q2.b])
                        else:
                            kb.op(ACT, lambda: A.copy(out=q2.t[:], in_=pb(bZ).rearrange("p (a b) -> p a b", b=128)), reads=[PS[bZ].b], writes=[q2.b])
                        yield
                        yield from branch(1, KT_slc, V_slc, list(range(0, c + 1)), q2)
                        rd_, coef_, oacc_, otmp_ = rd[i2], coef[i2], oacc[i2], otmp[i2]
                        kb.op(DVE, lambda: V.reciprocal(out=rd_.t[:], in_=de.t[:]), reads=[de.b], writes=[rd_.b])
                        yield
                        kb.op(DVE, lambda: V.tensor_tensor(out=coef_.t[:], in0=gt.t[:].rearrange("p (h b) -> p b h", b=3), in1=rd_.t[:], op=ALU.mult),
                              reads=[gt.b, rd_.b], writes=[coef_.b])
                        yield
                        kb.op(DVE, lambda: V.tensor_tensor(out=oacc_.t[:], in0=nu.t[:, 0], in1=coef_.t[:, 0, :].unsqueeze(2).broadcast_to([128, 8, 64]), op=ALU.mult),
                              reads=[nu.b, coef_.b], writes=[oacc_.b])
                        kb.op(POOL, lambda: G.tensor_tensor(out=otmp_.t[:], in0=nu.t[:, 1], in1=coef_.t[:, 1, :].unsqueeze(2).broadcast_to([128, 8, 64]), op=ALU.mult),
                              reads=[nu.b, coef_.b], writes=[otmp_.b])
                        yield
                        kb.op(DVE, lambda: V.tensor_tensor(out=oacc_.t[:], in0=oacc_.t[:], in1=otmp_.t[:], op=ALU.add), reads=[oacc_.b, otmp_.b], writes=[oacc_.b])
                        yield
                        kb.op(POOL, lambda: G.tensor_tensor(out=otmp_.t[:], in0=nu.t[:, 2], in1=coef_.t[:, 2, :].unsqueeze(2).broadcast_to([128, 8, 64]), op=ALU.mult),
                              reads=[nu.b, coef_.b], writes=[otmp_.b])
                        yield
                        yt = ytok[i2]
                        kb.op(DVE, lambda: V.tensor_tensor(out=yt.t[:], in0=oacc_.t[:].rearrange("p a b -> p (a b)"), in1=otmp_.t[:].rearrange("p a b -> p (a b)"), op=ALU.add),
                              reads=[oacc_.b, otmp_.b], writes=[yt.b])
                        if ydbg is not None:
                            kb.op(POOL, lambda: G.tensor_tensor(out=ydbg.t[:], in0=oacc_.t[:].rearrange("p a b -> p (a b)"), in1=otmp_.t[:].rearrange("p a b -> p (a b)"), op=ALU.add),
                                  reads=[oacc_.b, otmp_.b], writes=[ydbg.b])
                            dbg_store("y_nsa", ydbg.t[:], tok, [ydbg.b])
                        yield
                        for _ in range(XY3):
                            yield
                        transposes([yt.t[:, k * 128:(k + 1) * 128] for k in range(4)], bZ, [yt.b])
                        yield
                        kb.op(DVE, lambda: V.tensor_copy(out=ynsaT.t[:, :, tok], in_=pb(bZ)[:, 0:512].rearrange("p (a b) -> p a b", b=128)),
                              reads=[PS[bZ].b], writes=[ynsaT.bs[c]])
                        yield

                    run_interleaved([(lambda c=c: tile_gen(c)) for c in range(NT)], width=WIDTH)

            def phase_1d():
                with ExitStack() as s4:
                    wr = sb(s4, "wr", [128, KC, 2048], BF16, 4)
                    for j in range(4):
                        kb.dma(POOL, wr.t[:, :, j * 512:(j + 1) * 512], w_in_v[:, :, C_R + j * 512:C_R + (j + 1) * 512], writes=[wr.bs[j]])
                    for j in range(4):
                        kb.dma(POOL, wm.t[:, :, j * 512:(j + 1) * 512], w_in_v[:, :, C_M + j * 512:C_M + (j + 1) * 512], writes=[wm.bs[j]])
                    load_w(wbr, w_branch[0].rearrange("n (k p) d -> p (n k) d", p=128))
                    idT = sb(s4, "idT", [128, 4, 128], F32)
                    qdec = sb(s4, "qdec", [128, 4, 128], F32)
                    kdec = sb(s4, "kdec", [128, 4, 128], F32)
                    gn = sb(s4, "gn", [128, 512], F32)
                    eij = sb(s4, "eij", [128, 128], F32)
                    rowq = sb(s4, "rowq", [128, 128], F32)
                    rowk = sb(s4, "rowk", [128, 128], F32)
                    kb.dma(SP, gn.t[:], ret_gn_g.rearrange("o a b -> o (a b)").broadcast_to([128, 512]), writes=[gn.b])
                    kb.op(POOL, lambda: G.iota(eij.t[:], pattern=[[1, 128]], base=0, channel_multiplier=-1, allow_small_or_imprecise_dtypes=True), writes=[eij.b])
                    kb.op(POOL, lambda: G.iota(rowq.t[:], pattern=[[1, 128]], base=1, channel_multiplier=0, allow_small_or_imprecise_dtypes=True), writes=[rowq.b])
                    kb.op(POOL, lambda: G.iota(rowk.t[:], pattern=[[-1, 128]], base=127, channel_multiplier=0, allow_small_or_imprecise_dtypes=True), writes=[rowk.b])
                    lgs = [float(np.log(1.0 - 2.0 ** (-5.0 - h))) for h in range(4)]
                    cds = [float(np.exp(128.0 * np.float32(lg))) for lg in lgs]
                    cdt = sb(s4, "cdt", [128, 4], F32)
                    for h in range(4):
                        kb.op(ACT, lambda h=h: A.activation(out=cdt.t[:, h:h + 1], in_=rowq.t[:, 0:1], func=AF.Exp, scale=128.0 * float(np.float32(lgs[h]))),
                              reads=[rowq.b], writes=[cdt.b])
                    for h in range(4):
                        kb.op(ACT, lambda h=h: A.activation(out=idT.t[:, h, :], in_=eij.t[:], func=AF.Exp, scale=lgs[h]), reads=[eij.b], writes=[idT.b])
                        kb.op(POOL, lambda h=h: G.affine_select(out=idT.t[:, h, :], in_=idT.t[:, h, :], pattern=[[1, 128]], compare_op=ALU.is_ge, fill=0.0,
                                                                base=0, channel_multiplier=-1), reads=[idT.b], writes=[idT.b])
                        kb.op(ACT, lambda h=h: A.activation(out=qdec.t[:, h, :], in_=rowq.t[:], func=AF.Exp, scale=lgs[h]), reads=[rowq.b], writes=[qdec.b])
                        kb.op(ACT, lambda h=h: A.activation(out=kdec.t[:, h, :], in_=rowk.t[:], func=AF.Exp, scale=lgs[h]), reads=[rowk.b], writes=[kdec.b])
                    qTr = sb(s4, "qTr", [128, 4, 512], BF16)
                    qdT = sb(s4, "qdT", [128, 4, 512], BF16)
                    kTr = sb(s4, "kTr", [128, 4, 512], BF16)
                    kdT = sb(s4, "kdT", [128, 4, 512], BF16)
                    v_sb = [sb(s4, f"v_sb{i}", [128, 4, 128], BF16) for i in range(2)]
                    sgl = [sb(s4, f"sgl{i}", [128, 512], F32) for i in range(2)]
                    kd = [sb(s4, f"kd{i}", [128, 4, 128], BF16) for i in range(2)]
                    attb = [sb(s4, f"attb{i}", [128, 4, 128], BF16) for i in range(2)]
                    state_f = sb(s4, "state_f", [128, 4, 128], F32, 4)
                    state_b = sb(s4, "state_b", [128, 4, 128], BF16)
                    yr = [sb(s4, f"yr{i}", [128, 512], BF16) for i in range(2)]
                    yrdbg = sb(s4, "yrdbg", [128, 512], F32) if "y_ret" in dbg else None
                    kb.op(POOL, lambda: G.memset(state_f.t[:], 0.0), writes=state_f.bs)
                    kb.op(POOL, lambda: G.memset(state_b.t[:], 0.0), writes=[state_b.b])
                    KS = float(128.0 ** -0.5)
                    pcnt = [0]
                    state_done = [False] * (NT + 1)
                    stt_done = [False] * (NT + 1)
                    bst = [sb(s4, f"bst{i}", [128, 4, 6], F32, 4) for i in range(2)]
                    mv = [sb(s4, f"mv{i}", [128, 4, 2], F32, 4) for i in range(2)]
                    rs4 = [sb(s4, f"rs4{i}", [128, 12], F32) for i in range(2)]
                    on = [sb(s4, f"on{i}", [128, 512], F32, 4) for i in range(2)]

                    def p1d_gen(c):
                        i2 = c % 2
                        cl = c % 4
                        bA, bB, bT = 2 + 3 * i2, 3 + 3 * i2, 4 + 3 * i2
                        tok = slice(c * 128, (c + 1) * 128)
                        cs = slice(cl * 128, (cl + 1) * 128)
                        bst_, mv_, rs4_, on_ = bst[i2], mv[i2], rs4[i2], on[i2]
                        for k in range(KC):
                            kb.op(PE, lambda k=k: TE.matmul(pf(bA), lhsT=hT.t[:, k, tok], rhs=wr.t[:, k, 1024:1536], start=(k == 0), stop=(k == KC - 1)),
                                  reads=[hT.bs[c], wr.bs[2]], writes=[PS[bA].b], inc=(k == KC - 1))
                        yield
                        for k in range(KC):
                            kb.op(PE, lambda k=k: TE.matmul(pf(bB), lhsT=hT.t[:, k, tok], rhs=wr.t[:, k, 1536:2048], start=(k == 0), stop=(k == KC - 1)),
                                  reads=[hT.bs[c], wr.bs[3]], writes=[PS[bB].b], inc=(k == KC - 1))
                        yield
                        vs, sg_, kd_, ab = v_sb[i2], sgl[i2], kd[i2], attb[i2]
                        kb.op(ACT, lambda: A.copy(out=vs.t[:].rearrange("p a b -> p (a b)"), in_=pf(bA)), reads=[PS[bA].b], writes=[vs.b])
                        yield
                        kb.op(ACT, lambda: A.activation(out=sg_.t[:], in_=pf(bB), func=AF.Silu), reads=[PS[bB].b], writes=[sg_.b])
                        yield
                        kb.op(POOL, lambda: G.tensor_tensor(out=sg_.t[:], in0=sg_.t[:], in1=gn.t[:], op=ALU.mult), reads=[sg_.b, gn.b], writes=[sg_.b])
                        yield
                        for _ in range(XY3):
                            yield
                        transposes([kdT.t[:, h, cs] for h in range(4)], bT, [kdT.b])
                        yield
                        kb.op(ACT, lambda: A.copy(out=kd_.t[:].rearrange("p a b -> p (a b)"), in_=pb(bT)[:, 0:512]), reads=[PS[bT].b], writes=[kd_.b])
                        yield
                        for h in range(4):
                            kb.op(PE, lambda h=h: TE.matmul(pf(bA)[:, h * 128:(h + 1) * 128], lhsT=kTr.t[:, h, cs], rhs=qTr.t[:, h, cs], start=True, stop=True),
                                  reads=[kTr.b, qTr.b], writes=[PS[bA].b], inc=(h == 3))
                        yield
                        kb.op(DVE, lambda: V.tensor_tensor(out=ab.t[:], in0=pf(bA).rearrange("p (a b) -> p a b", b=128), in1=idT.t[:], op=ALU.mult),
                              reads=[PS[bA].b, idT.b], writes=[ab.b])
                        yield
                        if c < NT - 1:
                            for h in range(4):
                                kb.op(PE, lambda h=h: TE.matmul(pf(bT)[:, h * 128:(h + 1) * 128], lhsT=kd_.t[:, h, :], rhs=vs.t[:, h, :], start=True, stop=True),
                                      reads=[kd_.b, vs.b], writes=[PS[bT].b], inc=(h == 3))
                            yield
                            while c > 0 and not state_done[c - 1]:
                                yield
                            for h in range(4):
                                kb.op(DVE, lambda h=h: V.scalar_tensor_tensor(out=state_f.t[:, h, :], in0=state_f.t[:, h, :], scalar=cdt.t[:, h:h + 1],
                                                                              in1=pf(bT)[:, h * 128:(h + 1) * 128], op0=ALU.mult, op1=ALU.add),
                                      reads=[state_f.bs[h], PS[bT].b, cdt.b], writes=[state_f.bs[h]])
                        stt_done[c] = True
                        yield
                        while c > 0 and not state_done[c - 1]:
                            yield
                        for h in range(4):
                            kb.op(PE, lambda h=h: TE.matmul(pf(bB)[:, h * 128:(h + 1) * 128], lhsT=ab.t[:, h, :], rhs=vs.t[:, h, :], start=True, stop=(c == 0)),
                                  reads=[ab.b, vs.b], writes=[PS[bB].b], inc=(c == 0 and h == 3))
                            if c > 0:
                                kb.op(PE, lambda h=h: TE.matmul(pf(bB)[:, h * 128:(h + 1) * 128], lhsT=qdT.t[:, h, cs], rhs=state_b.t[:, h, :], start=False, stop=True),
                                      reads=[qdT.b, state_b.b], writes=[PS[bB].b], inc=(h == 3))
                        yield
                        if c < NT - 1:
                            kb.op(POOL, lambda: G.tensor_copy(out=state_b.t[:], in_=state_f.t[:]), reads=state_f.bs, writes=[state_b.b])
                        state_done[c] = True
                        yield
                        for h in range(4):
                            kb.op(DVE, lambda h=h: V.bn_stats(out=bst_.t[:, h, :], in_=pf(bB)[:, h * 128:(h + 1) * 128]), reads=[PS[bB].b], writes=[bst_.bs[h]])
                        yield
                        for h in range(4):
                            kb.op(DVE, lambda h=h: V.bn_aggr(out=mv_.t[:, h, :], in_=bst_.t[:, h, :]), reads=[bst_.bs[h]], writes=[mv_.bs[h]])
                        yield
                        kb.op(DVE, lambda: V.tensor_scalar(out=rs4_.t[:, 0:4], in0=mv_.t[:, :, 1], scalar1=EPS, scalar2=None, op0=ALU.add), reads=mv_.bs, writes=[rs4_.b])
                        yield
                        kb.op(POOL, lambda: G.tensor_tensor(out=rs4_.t[:, 8:12], in0=rs4_.t[:, 0:4], in1=nhalf.t[:, 0:4], op=ALU.pow),
                              reads=[rs4_.b, nhalf.b], writes=[rs4_.b])
                        yield
                        for h in range(4):
                            kb.op(DVE, lambda h=h: V.tensor_scalar(out=on_.t[:, h * 128:(h + 1) * 128], in0=pf(bB)[:, h * 128:(h + 1) * 128],
                                                                   scalar1=mv_.t[:, h, 0:1], scalar2=rs4_.t[:, 8 + h:9 + h], op0=ALU.subtract, op1=ALU.mult),
                                  reads=[PS[bB].b, mv_.bs[h], rs4_.b], writes=[on_.bs[h]])
                        yield
                        y_ = yr[i2]
                        kb.op(DVE, lambda: V.tensor_tensor(out=y_.t[:], in0=on_.t[:], in1=sg_.t[:], op=ALU.mult), reads=on_.bs + [sg_.b], writes=[y_.b])
                        if yrdbg is not None:
                            kb.op(DVE, lambda: V.tensor_tensor(out=yrdbg.t[:], in0=on_.t[:], in1=sg_.t[:], op=ALU.mult), reads=on_.bs + [sg_.b], writes=[yrdbg.b])
                            dbg_store("y_ret", yrdbg.t[:], tok, [yrdbg.b])
                        yield
                        for _ in range(XY3):
                            yield
                        transposes([y_.t[:, k * 128:(k + 1) * 128] for k in range(4)], bT, [y_.b])
                        yield
                        kb.op(ACT, lambda: A.copy(out=yretT.t[:, :, tok], in_=pb(bT)[:, 0:512].rearrange("p (a b) -> p a b", b=128)),
                              reads=[PS[bT].b], writes=[yretT.bs[c]])
                        yield

                    for tg in range(4):
                        tks = slice(tg * 512, (tg + 1) * 512)
                        hbs = [hT.bs[4 * tg + i] for i in range(4)]
                        for qk in range(2):
                            for h in range(4):
                                bk = pcnt[0] % 2
                                pcnt[0] += 1
                                for k in range(KC):
                                    kb.op(PE, lambda k=k, qk=qk, h=h, bk=bk: TE.matmul(pf(bk), lhsT=wr.t[:, k, qk * 512 + h * 128:qk * 512 + (h + 1) * 128],
                                                                                      rhs=hT.t[:, k, tks], start=(k == 0), stop=(k == KC - 1)),
                                          reads=hbs + [wr.bs[qk]], writes=[PS[bk].b], inc=(k == KC - 1))
                                pv4 = pf(bk).rearrange("p (a b) -> p a b", b=128)
                                if qk == 0:
                                    kb.op(ACT, lambda h=h, bk=bk: A.copy(out=qTr.t[:, h, :], in_=pf(bk)), reads=[PS[bk].b], writes=[qTr.b])
                                    kb.op(DVE, lambda h=h, pv4=pv4: V.tensor_tensor(out=qdT.t[:, h, :].rearrange("p (a b) -> p a b", b=128), in0=pv4,
                                                                                    in1=qdec.t[:, h:h + 1, :].broadcast_to([128, 4, 128]), op=ALU.mult),
                                          reads=[PS[bk].b, qdec.b], writes=[qdT.b])
                                else:
                                    kb.op(ACT, lambda h=h, bk=bk: A.mul(out=kTr.t[:, h, :], in_=pf(bk), mul=KS), reads=[PS[bk].b], writes=[kTr.b])
                                    kb.op(DVE, lambda h=h, pv4=pv4: V.scalar_tensor_tensor(out=kdT.t[:, h, :].rearrange("p (a b) -> p a b", b=128), in0=pv4, scalar=KS,
                                                                                           in1=kdec.t[:, h:h + 1, :].broadcast_to([128, 4, 128]),
                                                                                           op0=ALU.mult, op1=ALU.mult),
                                          reads=[PS[bk].b, kdec.b], writes=[kdT.b])
                        run_interleaved([(lambda c=c: p1d_gen(c)) for c in range(4 * tg, 4 * tg + 4)], width=2)

            def phase_1e():
                with ExitStack() as s5:
                    xt = [sb(s5, f"xte{i}", [128, D], F32) for i in range(2)]
                    g2bc = sb(s5, "g2bc", [128, D], F32)
                    kb.dma(SP, g2bc.t[:], norm2_g[0:1, :].broadcast_to([128, D]), writes=[g2bc.b])
                    wo = sb(s5, "wo", [128, KC, D], BF16)
                    load_w(wo, w_out[0].rearrange("(k p) d -> p k d", p=128))
                    gates = [sb(s5, f"gates{i}", [128, D], F32, 2) for i in range(2)]
                    tmix = [sb(s5, f"tmix{i}", [128, D], F32, 2) for i in range(2)]
                    tmix2 = [sb(s5, f"tmix2{i}", [128, D], F32, 2) for i in range(2)]
                    mixed = [sb(s5, f"mixed{i}", [128, D], BF16, 2) for i in range(2)]
                    mixT = [sb(s5, f"mixT{i}", [128, KC, 128], BF16) for i in range(2)]
                    x1t = [sb(s5, f"x1t{i}", [128, D], F32, 2) for i in range(2)]

                    def p1e_gen(c):
                        i2 = c % 2
                        bs_ = [4 * i2 + i for i in range(4)]
                        tok = slice(c * 128, (c + 1) * 128)
                        xx = xt[i2]
                        kb.dma(SP, xx.t[:], x[tok, :], writes=[xx.b])
                        gt_, tm = gates[i2], (tmix[i2], tmix2[i2])
                        for n, yT in ((0, ynsaT), (1, yretT)):
                            for half in range(2):
                                j = 2 * n + half
                                for k in range(KC):
                                    kb.op(PE, lambda j=j, k=k, half=half: TE.matmul(pf(bs_[half]), lhsT=hT.t[:, k, tok], rhs=wm.t[:, k, j * 512:(j + 1) * 512],
                                                                                    start=(k == 0), stop=(k == KC - 1)),
                                          reads=[hT.bs[c], wm.bs[j]], writes=[PS[bs_[half]].b], inc=(k == KC - 1))
                                yield
                            for half in range(2):
                                bk = bs_[2 + half]
                                for k in range(4):
                                    kb.op(PE, lambda n=n, half=half, k=k, bk=bk, yT=yT: TE.matmul(pf(bk), lhsT=yT.t[:, k, tok], rhs=wbr.t[:, n * 4 + k, half * 512:(half + 1) * 512],
                                                                                                  start=(k == 0), stop=(k == 3)),
                                          reads=[yT.bs[c], wbr.b], writes=[PS[bk].b], inc=(k == 3))
                                yield
                            for half in range(2):
                                kb.op(ACT, lambda half=half: A.activation(out=gt_.t[:, half * 512:(half + 1) * 512], in_=pf(bs_[half]), func=AF.Sigmoid),
                                      reads=[PS[bs_[half]].b], writes=[gt_.bs[half]])
                                yield
                            for half in range(2):
                                hs = slice(half * 512, (half + 1) * 512)
                                kb.op(DVE, lambda half=half, hs=hs, n=n: V.tensor_tensor(out=tm[n].t[:, hs], in0=gt_.t[:, hs], in1=pf(bs_[2 + half]), op=ALU.mult),
                                      reads=[gt_.bs[half], PS[bs_[2 + half]].b], writes=[tm[n].bs[half]])
                                yield
                        mx = mixed[i2]
                        kb.op(DVE, lambda: V.tensor_tensor(out=mx.t[:, 0:512], in0=tm[0].t[:, 0:512], in1=tm[1].t[:, 0:512], op=ALU.add), reads=[tm[0].bs[0], tm[1].bs[0]], writes=[mx.bs[0]])
                        kb.op(POOL, lambda: G.tensor_tensor(out=mx.t[:, 512:1024], in0=tm[0].t[:, 512:1024], in1=tm[1].t[:, 512:1024], op=ALU.add), reads=[tm[0].bs[1], tm[1].bs[1]], writes=[mx.bs[1]])
                        yield
                        for _ in range(XY3):
                            yield
                        transposes([mx.t[:, k * 128:(k + 1) * 128] for k in range(KC)], bs_[0], mx.bs)
                        yield
                        mt = mixT[i2]
                        kb.op(ACT, lambda: A.copy(out=mt.t[:], in_=pb(bs_[0]).rearrange("p (a b) -> p a b", b=128)), reads=[PS[bs_[0]].b], writes=[mt.b])
                        yield
                        for half in range(2):
                            for k in range(KC):
                                kb.op(PE, lambda half=half, k=k: TE.matmul(pf(bs_[2 + half]), lhsT=mt.t[:, k, :], rhs=wo.t[:, k, half * 512:(half + 1) * 512],
                                                                           start=(k == 0), stop=(k == KC - 1)),
                                      reads=[mt.b, wo.b], writes=[PS[bs_[2 + half]].b], inc=(k == KC - 1))
                            yield
                        x1 = x1t[i2]
                        for half in range(2):
                            hs = slice(half * 512, (half + 1) * 512)
                            kb.op(DVE, lambda half=half, hs=hs: V.tensor_tensor(out=x1.t[:, hs], in0=xx.t[:, hs], in1=pf(bs_[2 + half]), op=ALU.add),
                                  reads=[xx.b, PS[bs_[2 + half]].b], writes=[x1.bs[half]])
                            yield
                        kb.dma(SP, xmid[tok, :], x1.t[:], reads=x1.bs)
                        dbg_store("x1", x1.t[:], tok, x1.bs)
                        yield from norm_gen(x1, c, g2bc, i2, bs_[1])

                    run_interleaved([(lambda c=c: p1e_gen(c)) for c in range(NT)], width=2)

            def phase_2():
                with ExitStack() as s6:
                    xt = [sb(s6, f"xtf{i}", [128, D], F32) for i in range(2)]
                    wd = sb(s6, "wd", [128, NFB, D], BF16, 2)
                    wd_v = ffn_w_down[0].rearrange("(fb p) d -> p fb d", p=128)
                    wgs = [sb(s6, f"wgs{i}", [128, KC, 256], BF16) for i in range(2)]
                    wus = [sb(s6, f"wus{i}", [128, KC, 256], BF16) for i in range(2)]
                    act = sb(s6, "act", [128, NFB, 1024], BF16, NFB)
                    sgs = [sb(s6, f"sgs{i}", [128, 512], F32) for i in range(2)]
                    outt = [sb(s6, f"outt{i}", [128, D], F32, 2) for i in range(2)]
                    wg_v = ffn_w_gate[0].rearrange("(k p) f -> p k f", p=128)
                    wu_v = ffn_w_up[0].rearrange("(k p) f -> p k f", p=128)
                    cn = [0, 0]
                    for hf in range(2):
                        for fg in range(11):
                            cols = slice(fg * 256, (fg + 1) * 256)
                            wg_, wu_ = wgs[fg % 2], wus[fg % 2]
                            kb.dma(POOL, wg_.t[:], wg_v[:, :, cols], writes=[wg_.b])
                            kb.dma(POOL, wu_.t[:], wu_v[:, :, cols], writes=[wu_.b])
                            if hf == 0 and fg == 1:
                                kb.dma(POOL, wd.t[:, 0:11, :], wd_v[:, 0:11, :], writes=[wd.bs[0]])
                                kb.dma(POOL, wd.t[:, 11:22, :], wd_v[:, 11:22, :], writes=[wd.bs[1]])
                            for fl in range(2):
                                fb = fg * 2 + fl
                                for t2 in range(2):
                                    tokc = slice(hf * 1024 + t2 * 512, hf * 1024 + (t2 + 1) * 512)
                                    hbs = [hT.bs[hf * 8 + t2 * 4 + i] for i in range(4)]
                                    gb, ub = (0, 1) if cn[0] % 2 == 0 else (2, 3)
                                    cn[0] += 1
                                    for k in range(KC):
                                        kb.op(PE, lambda k=k, fl=fl, gb=gb, wg_=wg_, tokc=tokc: TE.matmul(pf(gb), lhsT=wg_.t[:, k, fl * 128:(fl + 1) * 128], rhs=hT.t[:, k, tokc],
                                                                                                          start=(k == 0), stop=(k == KC - 1)),
                                              reads=hbs + [wg_.b], writes=[PS[gb].b], inc=(k == KC - 1))
                                    for k in range(KC):
                                        kb.op(PE, lambda k=k, fl=fl, ub=ub, wu_=wu_, tokc=tokc: TE.matmul(pf(ub), lhsT=wu_.t[:, k, fl * 128:(fl + 1) * 128], rhs=hT.t[:, k, tokc],
                                                                                                          start=(k == 0), stop=(k == KC - 1)),
                                              reads=hbs + [wu_.b], writes=[PS[ub].b], inc=(k == KC - 1))
                                    sg_ = sgs[cn[0] % 2]
                                    kb.op(ACT, lambda sg_=sg_, gb=gb: A.activation(out=sg_.t[:], in_=pf(gb), func=AF.Silu), reads=[PS[gb].b], writes=[sg_.b])
                                    kb.op(DVE, lambda sg_=sg_, ub=ub, fb=fb, t2=t2: V.tensor_tensor(out=act.t[:, fb, t2 * 512:(t2 + 1) * 512], in0=sg_.t[:], in1=pf(ub), op=ALU.mult),
                                          reads=[sg_.b, PS[ub].b], writes=[act.bs[fb]])
                        for tl in range(8):
                            c = hf * 8 + tl
                            tok = slice(c * 128, (c + 1) * 128)
                            xx = xt[c % 2]
                            kb.dma(SP, xx.t[:], xmid[tok, :], writes=[xx.b])
                            ob = (4, 5) if cn[1] % 2 == 0 else (6, 7)
                            cn[1] += 1
                            for half in range(2):
                                for fb in range(NFB):
                                    kb.op(PE, lambda half=half, fb=fb, ob=ob, tl=tl: TE.matmul(pf(ob[half]), lhsT=act.t[:, fb, tl * 128:(tl + 1) * 128],
                                                                                               rhs=wd.t[:, fb, half * 512:(half + 1) * 512],
                                                                                               start=(fb == 0), stop=(fb == NFB - 1)),
                                          reads=[act.bs[fb], wd.bs[0 if fb < 11 else 1]], writes=[PS[ob[half]].b], inc=(fb == NFB - 1))
                            ot = outt[c % 2]
                            for half in range(2):
                                hs = slice(half * 512, (half + 1) * 512)
                                kb.op(DVE, lambda half=half, hs=hs, ob=ob, ot=ot, xx=xx: V.tensor_tensor(out=ot.t[:, hs], in0=xx.t[:, hs], in1=pf(ob[half]), op=ALU.add),
                                      reads=[xx.b, PS[ob[half]].b], writes=[ot.bs[half]])
                            kb.dma(SP, out[tok, :], ot.t[:], reads=ot.bs, is_out=True)

            with ExitStack() as sB:
                ynsaT = sb(sB, "ynsaT", [128, 4, S], BF16, NT)
                yretT = sb(sB, "yretT", [128, 4, S], BF16, NT)

                with ExitStack() as sA:
                    gq = sb(sA, "gq", [128, 64], F32)
                    gk = sb(sA, "gk", [128, 3, 64], F32)
                    QAL = sb(sA, "QAL", [128, NT, 8, 4], BF16)
                    KAL = sb(sA, "KAL", [128, NT, 4], BF16)
                    KCAL = sb(sA, "KCAL", [128, 4], BF16)
                    OH = sb(sA, "OH", [128, NT, 32], BF16)
                    dmask8 = sb(sA, "dmask8", [128, 8, 128], BF16)
                    tmask8 = sb(sA, "tmask8", [128, 8, 128], BF16)
                    cmask = sb(sA, "cmask", [128, S], BF16)
                    addc = sb(sA, "addc", [128, NT, 32], F32)
                    ov = sb(sA, "ov", [128, 32], BF16)
                    KT_slc = sb(sA, "KT_slc", [128, 2, S], BF16, NT)
                    KT_win = sb(sA, "KT_win", [128, 2, S], BF16, NT)
                    V_slc = sb(sA, "V_slc", [128, NT, 2, 65], BF16, NT)
                    V_win = sb(sA, "V_win", [128, NT, 2, 65], BF16, NT)
                    KcT = sb(sA, "KcT", [128, 2, 128], BF16)
                    Vc = sb(sA, "Vc", [128, 2, 97], BF16)

                    with ExitStack() as s0:
                        SL = sb(s0, "SL", [128, 8], F32)
                        th128 = sb(s0, "th128", [128, NT], F32)
                        pidx = sb(s0, "pidx", [128, 1], F32)
                        QALf = sb(s0, "QALf", [128, NT, 8, 4], F32)
                        KALf = sb(s0, "KALf", [128, NT, 4], F32)
                        KCALf = sb(s0, "KCALf", [128, 4], F32)
                        rel = sb(s0, "rel", [128, NT, 32], F32)
                        f0 = sb(s0, "f0", [128, NT, 32], F32)
                        f1 = sb(s0, "f1", [128, NT, 32], F32)
                        t1 = sb(s0, "t1", [128, NT, 32], F32)
                        hp = sb(s0, "hp", [128, 1], F32)
                        ovf = sb(s0, "ovf", [128, 32], F32)
                        ova = sb(s0, "ova", [128, 32], F32)
                        ones_b = sb(s0, "ones_b", [128, 512], BF16)
                        ones_b2 = sb(s0, "ones_b2", [128, 1024], BF16)
                        zeros_b = sb(s0, "zeros_b", [128, 512], BF16)

                        kb.dma(SP, gq.t[:], nsa_q_norm[0:1, :].broadcast_to([128, 64]), writes=[gq.b])
                        kb.dma(SP, gk.t[:].rearrange("p a b -> p (a b)"),
                               nsa_k_norm.rearrange("o a b -> o (a b)").broadcast_to([128, 192]), writes=[gk.b])
                        kb.op(DVE, lambda: V.tensor_scalar(out=gq.t[:], in0=gq.t[:], scalar1=0.125, scalar2=None, op0=ALU.mult),
                              reads=[gq.b], writes=[gq.b])
                        for h in range(8):
                            kb.op(POOL, lambda h=h: G.memset(SL.t[:, h:h + 1], 2.0 ** (-(h + 1))), writes=[SL.b])
                        kb.op(POOL, lambda: G.iota(th128.t[:], pattern=[[128, NT]], base=0, channel_multiplier=0,
                                                   allow_small_or_imprecise_dtypes=True), writes=[th128.b])
                        kb.op(POOL, lambda: G.iota(pidx.t[:], pattern=[[0, 1]], base=0, channel_multiplier=1,
                                                   allow_small_or_imprecise_dtypes=True), writes=[pidx.b])
                        SLb = SL.t[:].unsqueeze(1).broadcast_to([128, NT, 8])
                        THb = th128.t[:].unsqueeze(2).broadcast_to([128, NT, 8])
                        kb.op(DVE, lambda: V.scalar_tensor_tensor(out=QALf.t[:, :, :, 0], in0=THb, scalar=-1.0, in1=SLb,
                                                                  op0=ALU.mult, op1=ALU.mult),
                              reads=[SL.b, th128.b], writes=[QALf.b])
                        kb.op(DVE, lambda: V.tensor_scalar(out=QALf.t[:, :, :, 1], in0=SLb, scalar1=pidx.t[:, 0:1], scalar2=-1.0,
                                                           op0=ALU.mult, op1=ALU.mult),
                              reads=[SL.b, pidx.b], writes=[QALf.b])
                        kb.op(DVE, lambda: V.tensor_copy(out=QALf.t[:, :, :, 2], in_=SLb), reads=[SL.b], writes=[QALf.b])
                        kb.op(DVE, lambda: V.tensor_copy(out=QALf.t[:, :, :, 3], in_=SLb), reads=[SL.b], writes=[QALf.b])
                        kb.op(DVE, lambda: V.tensor_copy(out=QAL.t[:], in_=QALf.t[:]), reads=[QALf.b], writes=[QAL.b])
                        kb.op(POOL, lambda: G.memset(KALf.t[:, :, 0:2], 1.0), writes=[KALf.b])
                        kb.op(DVE, lambda: V.tensor_copy(out=KALf.t[:, :, 2], in_=th128.t[:]), reads=[th128.b], writes=[KALf.b])
                        kb.op(DVE, lambda: V.tensor_copy(out=KALf.t[:, :, 3], in_=pidx.t[:, 0:1].broadcast_to([128, NT])),
                              reads=[pidx.b], writes=[KALf.b])
                        kb.op(DVE, lambda: V.tensor_copy(out=KAL.t[:], in_=KALf.t[:]), reads=[KALf.b], writes=[KAL.b])
                        kb.op(POOL, lambda: G.memset(KCALf.t[:, 0:2], 1.0), writes=[KCALf.b])
                        kb.op(POOL, lambda: G.memset(KCALf.t[:, 3:4], 31.0), reads=[], writes=[KCALf.b])
                        kb.op(DVE, lambda: V.tensor_scalar(out=KCALf.t[:, 2:3], in0=pidx.t[:, 0:1], scalar1=16.0, scalar2=None,
                                                           op0=ALU.mult), reads=[pidx.b], writes=[KCALf.b])
                        kb.op(DVE, lambda: V.tensor_copy(out=KCAL.t[:], in_=KCALf.t[:]), reads=[KCALf.b], writes=[KCAL.b])
                        kb.op(POOL, lambda: G.memset(OH.t[:], 0.0), writes=[OH.b])
                        for kt in range(NT):
                            kb.op(POOL, lambda kt=kt: G.memset(OH.t[0:64, kt, 2 * kt:2 * kt + 1], 1.0), writes=[OH.b])
                            kb.op(POOL, lambda kt=kt: G.memset(OH.t[64:128, kt, 2 * kt + 1:2 * kt + 2], 1.0), writes=[OH.b])
                        kb.op(POOL, lambda: G.memset(ones_b.t[:], 1.0), writes=[ones_b.b])
                        kb.op(POOL, lambda: G.memset(zeros_b.t[:], 0.0), writes=[zeros_b.b])
                        ob8 = ones_b2.t[:].rearrange("p (a b) -> p a b", b=128)
                        kb.op(POOL, lambda: G.memset(ones_b2.t[:], 1.0), writes=[ones_b2.b])
                        kb.op(POOL, lambda: G.affine_select(out=dmask8.t[:], in_=ob8, pattern=[[0, 8], [1, 128]],
                                                            compare_op=ALU.is_ge, fill=0.0, base=0, channel_multiplier=-1),
                              reads=[ones_b2.b], writes=[dmask8.b])
                        kb.op(POOL, lambda: G.affine_select(out=tmask8.t[:], in_=ob8, pattern=[[0, 8], [-1, 128]],
                                                            compare_op=ALU.is_gt, fill=0.0, base=0, channel_multiplier=1),
                              reads=[ones_b2.b], writes=[tmask8.b])
                        for i in range(4):
                            kb.op(POOL, lambda i=i: G.affine_select(out=cmask.t[:, i * 512:(i + 1) * 512], in_=zeros_b.t[:],
                                                                    pattern=[[1, 512]], compare_op=ALU.is_ge, fill=NEG,
                                                                    base=-31 + 512 * i, channel_multiplier=-16),
                                  reads=[zeros_b.b], writes=[cmask.b])
                        kb.op(POOL, lambda: G.iota(rel.t[:], pattern=[[-2, NT], [1, 32]], base=0, channel_multiplier=0,
                                                   allow_small_or_imprecise_dtypes=True), writes=[rel.b])
                        kb.op(DVE, lambda: V.tensor_scalar(out=hp.t[:], in0=pidx.t[:], scalar1=64.0, scalar2=None, op0=ALU.is_ge),
                              reads=[pidx.b], writes=[hp.b])
                        kb.op(DVE, lambda: V.tensor_scalar(out=rel.t[:], in0=rel.t[:], scalar1=hp.t[:, 0:1], scalar2=None,
                                                           op0=ALU.subtract), reads=[rel.b, hp.b], writes=[rel.b])
                        kb.op(DVE, lambda: V.tensor_scalar(out=t1.t[:], in0=rel.t[:], scalar1=0.0, scalar2=-1e9,
                                                           op0=ALU.is_gt, op1=ALU.mult), reads=[rel.b], writes=[t1.b])
                        kb.op(DVE, lambda: V.tensor_scalar(out=f0.t[:], in0=rel.t[:], scalar1=0.0, scalar2=None, op0=ALU.is_equal),
                              reads=[rel.b], writes=[f0.b])
                        kb.op(DVE, lambda: V.tensor_scalar(out=f1.t[:], in0=rel.t[:], scalar1=-1.0, scalar2=None, op0=ALU.is_equal),
                              reads=[rel.b], writes=[f1.b])
                        kb.op(DVE, lambda: V.tensor_tensor(out=f0.t[:], in0=f0.t[:], in1=f1.t[:], op=ALU.max),
                              reads=[f0.b, f1.b], writes=[f0.b])
                        kb.op(DVE, lambda: V.memset(f0.t[:, :, 0:1], 1.0), reads=[], writes=[f0.b])
                        kb.op(DVE, lambda: V.scalar_tensor_tensor(out=addc.t[:], in0=f0.t[:], scalar=1e4, in1=t1.t[:],
                                                                  op0=ALU.mult, op1=ALU.add), reads=[f0.b, t1.b], writes=[addc.b])
                        kb.op(POOL, lambda: G.iota(ovf.t[:], pattern=[[-64, 32]], base=0, channel_multiplier=16,
                                                   allow_small_or_imprecise_dtypes=True), writes=[ovf.b])
                        kb.op(DVE, lambda: V.tensor_scalar(out=ova.t[:], in0=ovf.t[:], scalar1=63.0, scalar2=None, op0=ALU.is_le),
                              reads=[ovf.b], writes=[ova.b])
                        kb.op(DVE, lambda: V.tensor_scalar(out=ovf.t[:], in0=ovf.t[:], scalar1=-31.0, scalar2=None, op0=ALU.is_ge),
                              reads=[ovf.b], writes=[ovf.b])
                        kb.op(DVE, lambda: V.tensor_tensor(out=ov.t[:], in0=ova.t[:], in1=ovf.t[:], op=ALU.mult),
                              reads=[ova.b, ovf.b], writes=[ov.b])
                        kb.op(POOL, lambda: G.memset(V_slc.t[:, :, :, 64:65], 1.0), writes=V_slc.bs)
                        kb.op(POOL, lambda: G.memset(V_win.t[:, :, :, 64:65], 1.0), writes=V_win.bs)
                        kb.barrier()
                        chk(1)

                    with ExitStack() as s2:
                        cmpT = sb(s2, "cmpT", [128, 2, S], BF16, NT)
                        w1sb = sb(s2, "w1sb", [128, 2, 32, 128], BF16)
                        w2sb = sb(s2, "w2sb", [128, 2, 64], BF16)
                        pe_sb = sb(s2, "pe_sb", [32, 2, 64], F32)
                        with ExitStack() as s2a:
                            xt = [sb(s2a, f"xta{i}", [128, D], F32) for i in range(2)]
                            g1bc = sb(s2a, "g1bc", [128, D], F32)
                            kb.dma(SP, g1bc.t[:], norm1_g[0:1, :].broadcast_to([128, D]), writes=[g1bc.b])

                            def p1a_gen(c):
                                xx = xt[c % 2]
                                kb.dma(SP, xx.t[:], x[c * 128:(c + 1) * 128, :], writes=[xx.b])
                                yield
                                yield from norm_gen(xx, c, g1bc, c % 2, 6 + (c % 2))

                            wkv = sb(s2a, "wkv", [128, KC, 768], BF16)
                            load_w(wkv, w_in_v[:, :, C_KV:C_KV + 768])
                            for kv in range(2):
                                src = cmp_w1[0, kv].rearrange("l d f -> d l f")
                                kb.dma(POOL, w1sb.t[0:64, kv], src, writes=[w1sb.b])
                                kb.dma(POOL, w1sb.t[64:128, kv], src, writes=[w1sb.b])
                            kb.dma(POOL, w2sb.t[:], cmp_w2[0].rearrange("k f d -> f k d"), writes=[w2sb.b])
                            kb.dma(SP, pe_sb.t[:], cmp_pe[0].rearrange("k l d -> l k d"), writes=[pe_sb.b])
                            cmp_tok = [sb(s2a, f"cmp_tok{i}", [128, 256], BF16) for i in range(2)]
                            sqk = [sb(s2a, f"sqk{i}", [128, 256], F32) for i in range(2)]
                            kst = sb(s2a, "kst", [128, NT, 16], F32, NT)
                            tmpk = [sb(s2a, f"tmpk{i}", [128, 4, 64], F32) for i in range(2)]
                            ka_slc = [sb(s2a, f"ka_slc{i}", [128, 2, 128], BF16, 2) for i in range(2)]
                            ka_win = [sb(s2a, f"ka_win{i}", [128, 2, 128], BF16, 2) for i in range(2)]
                            for i in range(2):
                                kb.op(POOL, lambda i=i: G.memset(ka_slc[i].t[:], 0.0), writes=ka_slc[i].bs)
                                kb.op(POOL, lambda i=i: G.memset(ka_win[i].t[:], 0.0), writes=ka_win[i].bs)
                            def p1b_gen(c):
                                i2 = c % 2
                                bA, bB = (0, 1) if i2 == 0 else (2, 3)
                                tok = slice(c * 128, (c + 1) * 128)
                                if CUT >= 1:
                                    yield
                                    for k in range(KC):
                                        kb.op(PE, lambda k=k: TE.matmul(pf(bA), lhsT=hT.t[:, k, tok], rhs=wkv.t[:, k, 0:512],
                                                                        start=(k == 0), stop=(k == KC - 1)),
                                              reads=[hT.bs[c], wkv.b], writes=[PS[bA].b], inc=(k == KC - 1))
                                    for k in range(KC):
                                        kb.op(PE, lambda k=k: TE.matmul(pf(bB)[:, 0:256], lhsT=hT.t[:, k, tok], rhs=wkv.t[:, k, 512:768],
                                                                        start=(k == 0), stop=(k == KC - 1)),
                                              reads=[hT.bs[c], wkv.b], writes=[PS[bB].b], inc=(k == KC - 1))
                                if CUT >= 2:
                                    yield
                                    ct = cmp_tok[i2]
                                    kb.op(ACT, lambda: A.copy(out=ct.t[:], in_=pf(bA)[:, 0:256]), reads=[PS[bA].b], writes=[ct.b])
                                    sq = sqk[i2]
                                    kb.op(ACT, lambda: A.activation(out=sq.t[:, 0:128], in_=pf(bA)[:, 256:384], func=AF.Square),
                                          reads=[PS[bA].b], writes=[sq.b])
                                    kb.op(ACT, lambda: A.activation(out=sq.t[:, 128:256], in_=pf(bB)[:, 0:128], func=AF.Square),
                                          reads=[PS[bB].b], writes=[sq.b])
                                if CUT >= 3:
                                    yield
                                    ks = kst.t
                                    ksb = [kst.bs[c]]
                                    kb.op(DVE, lambda: V.tensor_reduce(out=ks[:, c, 0:4], in_=sq.t[:].rearrange("p (a b) -> p a b", b=64),
                                                                       axis=AX.X, op=ALU.add), reads=[sq.b], writes=ksb)
                                    rstd_from_ss(ks[:, c, 0:4], ks[:, c, 4:8], ks[:, c, 8:12], ks[:, c, 12:16], 64, ksb)
                                    tk = tmpk[i2]
                                    kb.op(DVE, lambda: V.tensor_tensor(out=tk.t[:, 0:2, :], in0=pf(bA)[:, 256:384].rearrange("p (a b) -> p a b", b=64),
                                                                       in1=ks[:, c, 12:14].unsqueeze(2).broadcast_to([128, 2, 64]), op=ALU.mult),
                                          reads=[PS[bA].b] + ksb, writes=[tk.b])
                                    kb.op(DVE, lambda: V.tensor_tensor(out=tk.t[:, 2:4, :], in0=pf(bB)[:, 0:128].rearrange("p (a b) -> p a b", b=64),
                                                                       in1=ks[:, c, 14:16].unsqueeze(2).broadcast_to([128, 2, 64]), op=ALU.mult),
                                          reads=[PS[bB].b] + ksb, writes=[tk.b])
                                    ksl, kwn = ka_slc[i2], ka_win[i2]
                                    kb.op(DVE, lambda: V.tensor_tensor(out=ksl.t[:, :, 0:64], in0=tk.t[:, 0:2, :],
                                                                       in1=gk.t[:, 1:2, :].broadcast_to([128, 2, 64]), op=ALU.mult),
                                          reads=[tk.b, gk.b], writes=[ksl.bs[0]])
                                    kb.op(DVE, lambda: V.tensor_tensor(out=kwn.t[:, :, 0:64], in0=tk.t[:, 2:4, :],
                                                                       in1=gk.t[:, 2:3, :].broadcast_to([128, 2, 64]), op=ALU.mult),
                                          reads=[tk.b, gk.b], writes=[kwn.bs[0]])
                                if CUT >= 4:
                                    yield
                                    kb.op(POOL, lambda: G.tensor_copy(out=ksl.t[:, :, 64:96], in_=OH.t[:, c:c + 1, :].broadcast_to([128, 2, 32])),
                                          reads=[OH.b], writes=[ksl.bs[1]])
                                    kb.op(POOL, lambda: G.tensor_copy(out=ksl.t[:, :, 96:100], in_=KAL.t[:, c:c + 1, :].broadcast_to([128, 2, 4])),
                                          reads=[KAL.b], writes=[ksl.bs[1]])
                                    kb.op(POOL, lambda: G.tensor_copy(out=kwn.t[:, :, 96:100], in_=KAL.t[:, c:c + 1, :].broadcast_to([128, 2, 4])),
                                          reads=[KAL.b], writes=[kwn.bs[1]])
                                if CUT >= 5:
                                    yield
                                    kb.op(ACT, lambda: A.copy(out=V_slc.t[:, c, :, 0:64], in_=pf(bA)[:, 384:512].rearrange("p (a b) -> p a b", b=64)),
                                          reads=[PS[bA].b], writes=[V_slc.bs[c]])
                                    kb.op(ACT, lambda: A.copy(out=V_win.t[:, c, :, 0:64], in_=pf(bB)[:, 128:256].rearrange("p (a b) -> p a b", b=64)),
                                          reads=[PS[bB].b], writes=[V_win.bs[c]])
                                if CUT >= 6:
                                    yield
                                    tb = 4 + i2
                                    for _ in range(XY3):
                                        yield
                                    transposes([ksl.t[:, 0, :], ksl.t[:, 1, :], kwn.t[:, 0, :], kwn.t[:, 1, :], ct.t[:, 0:128], ct.t[:, 128:256]],
                                               tb, ksl.bs + kwn.bs + [ct.b])
                                    pv3 = pb(tb).rearrange("p (a b) -> p a b", b=128)
                                    kb.op(ACT, lambda: A.copy(out=KT_slc.t[:, :, tok], in_=pv3[:, 0:2, :]), reads=[PS[tb].b], writes=[KT_slc.bs[c]])
                                    kb.op(ACT, lambda: A.copy(out=KT_win.t[:, :, tok], in_=pv3[:, 2:4, :]), reads=[PS[tb].b], writes=[KT_win.bs[c]])
                                    kb.op(ACT, lambda: A.copy(out=cmpT.t[:, :, tok], in_=pv3[:, 4:6, :]), reads=[PS[tb].b], writes=[cmpT.bs[c]])
                            def p1ab_gen(c):
                                yield from p1a_gen(c)
                                yield from p1b_gen(c)

                            run_interleaved([(lambda c=c: p1ab_gen(c)) for c in range(NT)], width=2)
                            if "KT_slc" in dbg:
                                kdb = sb(s2a, "kdb", [128, 2, S], F32)
                                kb.op(DVE, lambda: V.tensor_copy(out=kdb.t[:], in_=KT_slc.t[:]), reads=KT_slc.bs, writes=[kdb.b])
                                kb.dma(SP, dbg["KT_slc"].rearrange("p (a b) -> p a b", b=S), kdb.t[:], reads=[kdb.b], is_out=True)
                            kb.barrier()
                            chk(3)

                        with ExitStack() as s2b:
                            peT = sb(s2b, "peT", [64, 2, 32], BF16)
                            bias_c = sb(s2b, "bias_c", [128, 2], F32)
                            xhs = [sb(s2b, f"xh{i}", [128, 128], F32) for i in range(4)]
                            x2s = [sb(s2b, f"x2{i}", [128, 128], F32) for i in range(4)]
                            sgs_ = [sb(s2b, f"sgc{i}", [128, 128], F32) for i in range(4)]
                            HTbs = [sb(s2b, f"HTb{i}", [128, 128], BF16) for i in range(4)]
                            kca = sb(s2b, "kca", [128, 2, 128], BF16)
                            cst = sb(s2b, "cst", [128, 8], F32)
                            tmpcs = [sb(s2b, f"tmpc{i}", [128, 64], F32) for i in range(4)]
                            kb.op(POOL, lambda: G.memset(kca.t[:], 0.0), writes=[kca.b])
                            kb.op(POOL, lambda: G.memset(Vc.t[:], 0.0), writes=[Vc.b])
                            kb.op(POOL, lambda: G.tensor_copy(out=kca.t[:, :, 96:100], in_=KCAL.t[:].unsqueeze(1).broadcast_to([128, 2, 4])),
                                  reads=[KCAL.b], writes=[kca.b])
                            kb.op(POOL, lambda: G.memset(Vc.t[:, :, 64:65], 1.0), writes=[Vc.b])
                            kb.op(POOL, lambda: G.tensor_copy(out=Vc.t[:, :, 65:97], in_=ov.t[:].unsqueeze(1).broadcast_to([128, 2, 32])),
                                  reads=[ov.b], writes=[Vc.b])
                            for kv in range(2):
                                kb.op(PE, lambda kv=kv: TE.transpose(out=pf(0)[0:64, kv * 32:(kv + 1) * 32], in_=pe_sb.t[0:32, kv, :],
                                                                     identity=ident_f.t[0:32, 0:32]),
                                      reads=[pe_sb.b, ident_f.b], writes=[PS[0].b])
                            kb.op(DVE, lambda: V.tensor_copy(out=peT.t[:], in_=pf(0)[0:64, 0:64].rearrange("p (a b) -> p a b", b=32)),
                                  reads=[PS[0].b], writes=[peT.b])
                            for kv in range(2):
                                for l in range(32):
                                    kb.op(PE, lambda kv=kv, l=l: TE.matmul(pf(1)[:, kv:kv + 1], lhsT=w1sb.t[0:64, kv, l, :], rhs=peT.t[0:64, kv, l:l + 1],
                                                                           start=(l == 0), stop=(l == 31)),
                                          reads=[w1sb.b, peT.b], writes=[PS[1].b], inc=(l == 31))
                            kb.op(DVE, lambda: V.tensor_copy(out=bias_c.t[:], in_=pf(1)[:, 0:2]), reads=[PS[1].b], writes=[bias_c.b])
                            def cmp_gen(kv, g, idx):
                                bH, bO = 2 * idx, 2 * idx + 1
                                xh, x2, sg, HTb, tmpc = xhs[idx], x2s[idx], sgs_[idx], HTbs[idx], tmpcs[idx]
                                for l in range(32):
                                    kb.op(PE, lambda kv=kv, g=g, l=l: TE.matmul(
                                        pf(bH)[:, 0:127], lhsT=w1sb.t[g * 64:(g + 1) * 64, kv, l, :],
                                        rhs=cmpT.t[g * 64:(g + 1) * 64, kv, l:l + 16 * 126 + 1:16],
                                        start=(l == 0), stop=(l == 31)),
                                        reads=[w1sb.b] + cmpT.bs, writes=[PS[bH].b], inc=(l == 31))
                                yield
                                kb.op(ACT, lambda kv=kv: A.activation(out=xh.t[:, 0:127], in_=pf(bH)[:, 0:127], func=AF.Identity,
                                                                      bias=bias_c.t[:, kv:kv + 1], scale=1.0),
                                      reads=[PS[bH].b, bias_c.b], writes=[xh.b])
                                yield
                                kb.op(DVE, lambda: V.tensor_tensor(out=x2.t[:, 0:127], in0=xh.t[:, 0:127], in1=xh.t[:, 0:127], op=ALU.mult),
                                      reads=[xh.b], writes=[x2.b])
                                yield
                                kb.op(DVE, lambda: V.tensor_scalar(out=x2.t[:, 0:127], in0=x2.t[:, 0:127], scalar1=0.044715, scalar2=1.0,
                                                                   op0=ALU.mult, op1=ALU.add), reads=[x2.b], writes=[x2.b])
                                yield
                                kb.op(DVE, lambda: V.tensor_tensor(out=x2.t[:, 0:127], in0=x2.t[:, 0:127], in1=xh.t[:, 0:127], op=ALU.mult),
                                      reads=[x2.b, xh.b], writes=[x2.b])
                                yield
                                kb.op(ACT, lambda: A.activation(out=sg.t[:, 0:127], in_=x2.t[:, 0:127], func=AF.Sigmoid, scale=1.5957691216057308),
                                      reads=[x2.b], writes=[sg.b])
                                yield
                                kb.op(DVE, lambda: V.tensor_tensor(out=HTb.t[:, 0:127], in0=xh.t[:, 0:127], in1=sg.t[:, 0:127], op=ALU.mult),
                                      reads=[xh.b, sg.b], writes=[HTb.b])
                                yield
                                kb.op(PE, lambda kv=kv: TE.matmul(pf(bO)[0:127, 0:64], lhsT=HTb.t[:, 0:127], rhs=w2sb.t[:, kv, :], start=True, stop=True),
                                      reads=[HTb.b, w2sb.b], writes=[PS[bO].b])
                                yield
                                if kv == 0:
                                    kb.op(ACT, lambda g=g: A.activation(out=tmpc.t[0:127, :], in_=pf(bO)[0:127, 0:64], func=AF.Square,
                                                                        accum_out=cst.t[0:127, g:g + 1]),
                                          reads=[PS[bO].b], writes=[tmpc.b, cst.b])
                                    rstd_from_ss(cst.t[0:127, g:g + 1], cst.t[0:127, 2 + g:3 + g], cst.t[0:127, 4 + g:5 + g], cst.t[0:127, 6 + g:7 + g], 64, [cst.b])
                                    kb.op(DVE, lambda g=g: V.scalar_tensor_tensor(out=kca.t[0:127, g, 0:64], in0=pf(bO)[0:127, 0:64],
                                                                                  scalar=cst.t[0:127, 6 + g:7 + g], in1=gk.t[0:127, 0, :],
                                                                                  op0=ALU.mult, op1=ALU.mult),
                                          reads=[PS[bO].b, cst.b, gk.b], writes=[kca.b])
                                else:
                                    kb.op(ACT, lambda g=g: A.copy(out=Vc.t[0:127, g, 0:64], in_=pf(bO)[0:127, 0:64]),
                                          reads=[PS[bO].b], writes=[Vc.b])
                                yield

                            run_interleaved([(lambda kv=kv, g=g: cmp_gen(kv, g, 2 * kv + g)) for kv in range(2) for g in range(2)], width=4)
                            transposes([kca.t[:, 0, :], kca.t[:, 1, :]], 6, [kca.b])
                            kb.op(ACT, lambda: A.copy(out=KcT.t[:], in_=pb(6)[:, 0:256].rearrange("p (a b) -> p a b", b=128)),
                                  reads=[PS[6].b], writes=[KcT.b])
                            if "kc" in dbg:
                                kcd = sb(s2b, "kcd", [128, 2, 64], F32)
                                kb.op(DVE, lambda: V.tensor_copy(out=kcd.t[:], in_=kca.t[:, :, 0:64]), reads=[kca.b], writes=[kcd.b])
                                kb.dma(SP, dbg["kc"].rearrange("p (a b) -> p a b", b=64), kcd.t[:], reads=[kcd.b], is_out=True)
                            if "vc" in dbg:
                                vcd = sb(s2b, "vcd", [128, 2, 64], F32)
                                kb.op(DVE, lambda: V.tensor_copy(out=vcd.t[:], in_=Vc.t[:, :, 0:64]), reads=[Vc.b], writes=[vcd.b])
                                kb.dma(SP, dbg["vc"].rearrange("p (a b) -> p a b", b=64), vcd.t[:], reads=[vcd.b], is_out=True)
                            kb.barrier()
                            chk(4)

                    phase_1c()
                    kb.barrier()
                    chk(5)

                wm = sb(sB, "wm", [128, KC, 2048], BF16, 4)
                wbr = sb(sB, "wbr", [128, 8, D], BF16)
                phase_1d()
                kb.barrier()
                chk(6)
                phase_1e()
                kb.barrier()
                chk(7)

            phase_2()
            kb.finish()
    except _Stop:
        pass
    return nc


_NAMES = ["x", "norm1_g", "w_in", "nsa_q_norm", "nsa_k_norm", "cmp_pe", "cmp_w1", "cmp_w2", "ret_gn_g",
          "w_branch", "w_out", "norm2_g", "ffn_w_gate", "ffn_w_up", "ffn_w_down"]


def kernel(**inputs):
    n = 8
    arrs = {k: np.ascontiguousarray(np.asarray(inputs[k], dtype=np.float32)) for k in _NAMES}
    nc = build_nc()
    in_maps = []
    for i in range(n):
        m = {k: arrs[k] for k in _NAMES if k != "x"}
        m["x"] = np.ascontiguousarray(arrs["x"][i])
        in_maps.append(m)
    res = run_bass_kernel_spmd(nc, in_maps, core_ids=list(range(n)))
    return np.stack([np.asarray(r["out"], dtype=np.float32) for r in res.results], axis=0)
```

```python
import numpy as np
from contextlib import ExitStack
import concourse.bass as bass
import concourse.mybir as mybir
from concourse.bass_utils import run_bass_kernel_spmd

F32 = mybir.dt.float32
BF16 = mybir.dt.bfloat16
AF = mybir.ActivationFunctionType
ALU = mybir.AluOpType
AX = mybir.AxisListType

S = 2048
D = 1024
NT = 16
KC = 8
N_IN = 5400
DFF = 2816
NFB = 22
EPS = 1e-6
SEM_LIMIT = 24000
import os as _os
CUT = int(_os.environ.get('P1B_CUT', '99'))
CUTC = int(_os.environ.get('P1C_CUT', '99'))
SUBC = int(_os.environ.get('P1C_SUB', '99'))
WIDTH = int(_os.environ.get('P1C_WIDTH', '2'))
XY1 = int(_os.environ.get('XY1', '0'))
XY2 = int(_os.environ.get('XY2', '2'))
XY3 = int(_os.environ.get('XY3', '0'))
PREWIN = int(_os.environ.get('PREWIN', '1'))
QDVE = int(_os.environ.get('QDVE', '1'))
NUDVE = int(_os.environ.get('NUDVE', '1'))
NEG = -30000.0

C_Q = 0
C_KV = 512
C_G = 1280
C_R = 1304
C_M = 3352


class Buf:
    __slots__ = ("name", "w", "r", "excl")

    def __init__(self, name):
        self.name = name
        self.w = None
        self.r = []
        self.excl = False


class SemW:
    __slots__ = ("h",)

    def __init__(self, h):
        self.h = h


class Slot:
    __slots__ = ("sem", "val")

    def __init__(self, sem):
        self.sem = sem
        self.val = 0


class Q:
    def __init__(self, name, eng):
        self.name = name
        self.eng = eng
        self.sem = None
        self.count = 0
        self.waited = {}
        self.ring = []
        self.ri = 0
        self.pending = False


class T:
    def __init__(self, t, name, nb=1):
        self.t = t
        self.bs = [Buf(f"{name}{i}") for i in range(nb)]

    @property
    def b(self):
        return self.bs[0]


class KB:
    def __init__(self, nc, es):
        self.nc = nc
        self.es = es
        self.nsem = 0
        self.pe = self.mkq("pe", nc.tensor)
        self.act = self.mkq("act", nc.scalar)
        self.dve = self.mkq("dve", nc.vector)
        self.pool = self.mkq("pool", nc.gpsimd)
        self.sp = self.mkq("sp", nc.sync)
        self.qs = [self.pe, self.act, self.dve, self.pool, self.sp]
        for q, n in ((self.sp, 16), (self.pool, 8), (self.act, 4)):
            q.ring = [Slot(self.new_sem(f"{q.name}_d{i}")) for i in range(n)]
        self.out_toks = []

    def new_sem(self, name):
        self.nsem += 1
        return SemW(self.es.enter_context(self.nc.semaphore(f"{name}_{self.nsem}")))

    def mkq(self, name, eng):
        q = Q(name, eng)
        q.sem = self.new_sem(name)
        return q

    def wait(self, q, tok):
        sw, val = tok[0], tok[1]
        if q.waited.get(sw, 0) >= val:
            return
        q.eng.wait_ge(sw.h, val)
        q.waited[sw] = val

    def _dep(self, q, tok, raw, force=False):
        if tok[2] is q and q is self.pe and not force:
            return
        self.wait(q, tok)

    def _deps(self, q, reads, writes, force=False):
        for b in reads:
            if b.w is not None:
                self._dep(q, b.w, True, force)
            if b.excl:
                for t in b.r:
                    if t[2] is not q:
                        self._dep(q, t, False, force)
        for b in writes:
            if b.w is not None:
                self._dep(q, b.w, False, force)
            for t in b.r:
                self._dep(q, t, False, force)

    def _record(self, tok, reads, writes):
        for b in reads:
            if tok[2] is not None:
                b.r = [t for t in b.r if t[2] is not tok[2]]
            b.r.append(tok)
        for b in writes:
            b.w = tok
            b.r = []

    def op(self, q, fn, reads=(), writes=(), inc=True):
        self._deps(q, reads, writes)
        ins = fn()
        if inc:
            if q.count >= SEM_LIMIT and not q.pending:
                q.sem = self.new_sem(q.name)
                q.count = 0
            ins.then_inc(q.sem.h, 1)
            q.count += 1
            q.pending = False
            tok = (q.sem, q.count, q)
        else:
            q.pending = True
            tok = (q.sem, q.count + 1, q)
        self._record(tok, reads, writes)
        return ins

    def dma(self, q, out, in_, reads=(), writes=(), is_out=False):
        self._deps(q, reads, writes, force=True)
        slot = q.ring[q.ri % len(q.ring)]
        q.ri += 1
        if slot.val > 0:
            self.wait(q, (slot.sem, slot.val))
        if slot.val >= SEM_LIMIT:
            slot.sem = self.new_sem(q.name + "_d")
            slot.val = 0
        ins = q.eng.dma_start(out=out, in_=in_)
        ins.then_inc(slot.sem.h, 16)
        slot.val += 16
        tok = (slot.sem, slot.val, None)
        self._record(tok, reads, writes)
        if is_out:
            self.out_toks.append(tok)
        return tok

    def barrier(self):
        toks = []
        for o in self.qs:
            if o.count > 0:
                toks.append((o.sem, o.count, o))
            for sl in o.ring:
                if sl.val > 0:
                    toks.append((sl.sem, sl.val, None))
        for q in self.qs:
            for t in toks:
                if t[2] is q:
                    continue
                self.wait(q, t)

    def finish(self):
        for t in self.out_toks:
            self.wait(self.sp, t)


class _Stop(Exception):
    pass


def build_nc(debug=None, stop=None):
    nc = bass.Bass("TRN2", target_bir_lowering=False)

    def din(name, shape):
        return nc.dram_tensor(name, list(shape), F32, kind="ExternalInput").ap()

    x = din("x", [S, D])
    norm1_g = din("norm1_g", [1, D])
    w_in = din("w_in", [1, D, N_IN])
    nsa_q_norm = din("nsa_q_norm", [1, 64])
    nsa_k_norm = din("nsa_k_norm", [1, 3, 64])
    cmp_pe = din("cmp_pe", [1, 2, 32, 64])
    cmp_w1 = din("cmp_w1", [1, 2, 32, 64, 128])
    cmp_w2 = din("cmp_w2", [1, 2, 128, 64])
    ret_gn_g = din("ret_gn_g", [1, 4, 128])
    w_branch = din("w_branch", [1, 2, 512, D])
    w_out = din("w_out", [1, D, D])
    norm2_g = din("norm2_g", [1, D])
    ffn_w_gate = din("ffn_w_gate", [1, D, DFF])
    ffn_w_up = din("ffn_w_up", [1, D, DFF])
    ffn_w_down = din("ffn_w_down", [1, DFF, D])
    out = nc.dram_tensor("out", [S, D], F32, kind="ExternalOutput").ap()
    xmid = nc.dram_tensor("xmid", [S, D], F32, kind="Internal").ap()
    dbg = {}
    if debug:
        for name, shape in debug.items():
            dbg[name] = nc.dram_tensor("dbg_" + name, list(shape), F32, kind="ExternalOutput").ap()

    w_in_v = w_in[0].rearrange("(k p) n -> p k n", p=128)

    try:
        with ExitStack() as es:
            kb = KB(nc, es)

            def chk(n):
                if stop is not None and n >= stop:
                    kb.barrier()
                    kb.finish()
                    raise _Stop()
            PE, ACT, DVE, POOL, SP = kb.pe, kb.act, kb.dve, kb.pool, kb.sp
            V, A, G, TE = nc.vector, nc.scalar, nc.gpsimd, nc.tensor

            def sb(scope, name, shape, dt, nb=1):
                return T(scope.enter_context(nc.sbuf_tensor(name, list(shape), dt)), name, nb)

            PS2 = [es.enter_context(nc.psum_tensor(f"psp{j}", [128, 1024], F32)) for j in range(4)]
            PS = [T(None, f"ps{i}") for i in range(8)]
            for p_ in PS:
                p_.b.excl = True

            def pf(i):
                return PS2[i // 2][:, (i % 2) * 512:(i % 2 + 1) * 512]

            def pb(i):
                return PS2[i // 2][:].bitcast(BF16)[:, (i % 2) * 1024:(i % 2 + 1) * 1024]

            def pf2(j):
                return PS2[j][:]

            ident_f = sb(es, "ident_f", [128, 128], F32)
            ident_b = sb(es, "ident_b", [128, 128], BF16)
            ones_f = sb(es, "ones_f", [128, 128], F32)
            nhalf = sb(es, "nhalf", [128, 16], F32)
            hT = sb(es, "hT", [128, KC, S], BF16, NT)
            stat = sb(es, "stat", [128, NT, 4], F32, NT)
            hb = [sb(es, f"hb{i}", [128, D], BF16) for i in range(2)]
            junk = sb(es, "junk", [128, D], BF16)

            kb.op(POOL, lambda: G.memset(ones_f.t[:], 1.0), writes=[ones_f.b])
            kb.op(POOL, lambda: G.memset(nhalf.t[:], -0.5), writes=[nhalf.b])
            kb.op(POOL, lambda: G.affine_select(out=ident_f.t[:], in_=ones_f.t[:, 0:128], pattern=[[1, 128]],
                                                compare_op=ALU.is_equal, fill=0.0, base=0, channel_multiplier=-1),
                  reads=[ones_f.b], writes=[ident_f.b])
            kb.op(DVE, lambda: V.tensor_copy(out=ident_b.t[:], in_=ident_f.t[:]), reads=[ident_f.b], writes=[ident_b.b])

            def rstd_from_ss(ss_ap, ms_ap, sd_ap, rs_ap, n, bufs):
                k = ms_ap.shape[-1]
                P_ = ms_ap.shape[0]
                kb.op(DVE, lambda: V.tensor_scalar(out=ms_ap, in0=ss_ap, scalar1=1.0 / n, scalar2=EPS,
                                                   op0=ALU.mult, op1=ALU.add), reads=bufs, writes=bufs)
                kb.op(POOL, lambda: G.tensor_tensor(out=rs_ap, in0=ms_ap, in1=nhalf.t[0:P_, 0:k], op=ALU.pow),
                      reads=list(bufs) + [nhalf.b], writes=bufs)

            def transposes(src_aps, bank, reads):
                pbv = pb(bank)
                n = len(src_aps)
                for i, ap in enumerate(src_aps):
                    kb.op(PE, lambda ap=ap, i=i: TE.transpose(out=pbv[:, i * 128:(i + 1) * 128], in_=ap, identity=ident_b.t[:]),
                          reads=list(reads) + [ident_b.b], writes=[PS[bank].b], inc=(i == n - 1))

            def norm_gen(src, c, gbc, sidx, bank):
                sbuf_ = [stat.bs[c]]
                st = stat.t
                kb.op(ACT, lambda: A.activation(out=junk.t[:], in_=src.t[:], func=AF.Square, accum_out=st[:, c, 0:1]),
                      reads=src.bs, writes=[junk.b] + sbuf_)
                yield
                kb.op(DVE, lambda: V.tensor_scalar(out=st[:, c, 1:2], in0=st[:, c, 0:1], scalar1=1.0 / D, scalar2=EPS,
                                                   op0=ALU.mult, op1=ALU.add), reads=sbuf_, writes=sbuf_)
                yield
                kb.op(POOL, lambda: G.tensor_tensor(out=st[:, c, 3:4], in0=st[:, c, 1:2], in1=nhalf.t[:, 0:1], op=ALU.pow),
                      reads=sbuf_ + [nhalf.b], writes=sbuf_)
                yield
                h = hb[sidx % 2]
                kb.op(DVE, lambda: V.scalar_tensor_tensor(out=h.t[:], in0=src.t[:], scalar=st[:, c, 3:4], in1=gbc.t[:],
                                                          op0=ALU.mult, op1=ALU.mult),
                      reads=src.bs + [gbc.b] + sbuf_, writes=[h.b])
                yield
                for _ in range(XY3):
                    yield
                transposes([h.t[:, k * 128:(k + 1) * 128] for k in range(KC)], bank, [h.b])
                yield
                kb.op(ACT, lambda: A.copy(out=hT.t[:, :, c * 128:(c + 1) * 128],
                                          in_=pb(bank).rearrange("p (a b) -> p a b", b=128)),
                      reads=[PS[bank].b], writes=[hT.bs[c]])
                yield

            def run_interleaved(gen_fns, width=2):
                pending = list(gen_fns)
                active = []
                while pending or active:
                    while pending and len(active) < width:
                        active.append(pending.pop(0)())
                    for g_ in list(active):
                        try:
                            next(g_)
                        except StopIteration:
                            active.remove(g_)

            def load_w(dst, src_ap, q=None):
                kb.dma(q or POOL, dst.t[:], src_ap, writes=[dst.b])

            def dbg_store(name, src_ap, rows, reads):
                if name in dbg:
                    kb.dma(SP, dbg[name][rows], src_ap, reads=reads, is_out=True)

            def phase_1c():
                with ExitStack() as s3:
                    wq = sb(s3, "wq", [128, KC, 512], BF16)
                    load_w(wq, w_in_v[:, :, C_Q:C_Q + 512])
                    wg = sb(s3, "wg", [128, KC, 24], BF16)
                    load_w(wg, w_in_v[:, :, C_G:C_G + 24])
                    sqq = [sb(s3, f"sqq{i}", [128, 512], F32) for i in range(2)]
                    qst = sb(s3, "qst", [128, NT, 32], F32, NT)
                    tmpq = [sb(s3, f"tmpq{i}", [128, 8, 64], F32) for i in range(2)]
                    qaug = [sb(s3, f"qaug{i}", [128, 8, 128], BF16, 4) for i in range(2)]
                    qT = [sb(s3, f"qT{i}", [128, 8, 128], BF16) for i in range(2)]
                    qT2 = [sb(s3, f"qT2{i}", [128, 8, 128], BF16) for i in range(2)]
                    gate = [sb(s3, f"gate{i}", [128, 24], F32) for i in range(2)]
                    scl = [sb(s3, f"scl{i}", [128, 1024], F32, 2) for i in range(2)]
                    NPT = 3
                    PT = [[sb(s3, f"PT{i}_{j}", [128, 1024], BF16, 2) for j in range(NPT)] for i in range(2)]
                    oTs = [sb(s3, f"oTs{i}", [128, 1024], F32, 2) for i in range(2)]
                    num = [sb(s3, f"num{i}", [128, 3, 8, 64], F32) for i in range(2)]
                    den = [sb(s3, f"den{i}", [128, 3, 8], F32) for i in range(2)]
                    rdc = [sb(s3, f"rdc{i}", [128, 8], F32) for i in range(2)]
                    impn = [sb(s3, f"impn{i}", [128, 8, 32], F32) for i in range(2)]
                    imp = [sb(s3, f"imp{i}", [128, 2, 32], F32) for i in range(2)]
                    top8 = [sb(s3, f"top8{i}", [128, 2, 8], F32, 2) for i in range(2)]
                    rd = [sb(s3, f"rd{i}", [128, 3, 8], F32) for i in range(2)]
                    coef = [sb(s3, f"coef{i}", [128, 3, 8], F32) for i in range(2)]
                    oacc = [sb(s3, f"oacc{i}", [128, 8, 64], F32) for i in range(2)]
                    otmp = [sb(s3, f"otmp{i}", [128, 8, 64], F32) for i in range(2)]
                    ytok = [sb(s3, f"ytok{i}", [128, 512], BF16) for i in range(2)]
                    ydbg = sb(s3, "ydbg", [128, 512], F32) if "y_nsa" in dbg else None
                    for i in range(2):
                        kb.op(POOL, lambda i=i: G.memset(qaug[i].t[:], 0.0), writes=qaug[i].bs)
                    ptc = [0, 0]

                    def tile_gen(c):
                        i2 = c % 2
                        base = 4 * i2
                        bZ, bG = base, base + 1
                        bO0, bO1 = base + 2, base + 3
                        Sb = [PS[bZ].b, PS[bG].b]
                        Ob = [PS[bO0].b, PS[bO1].b]
                        tok = slice(c * 128, (c + 1) * 128)

                        def S2():
                            return pf2(base // 2)

                        def O2():
                            return pf2(base // 2 + 1)

                        def X8():
                            return O2().rearrange("p (h c) -> p h c", h=8)

                        def to_token_major(br, ncol):
                            nu, de = num[i2], den[i2]
                            ot = oTs[i2]
                            kb.op(DVE, lambda: V.tensor_scalar(out=ot.t[0:ncol, 0:512], in0=O2()[0:ncol, 0:512], scalar1=1.0, scalar2=None, op0=ALU.mult),
                                  reads=[Ob[0]], writes=[ot.bs[0]])
                            kb.op(ACT, lambda: A.copy(out=ot.t[0:ncol, 512:1024], in_=O2()[0:ncol, 512:1024]),
                                  reads=[Ob[1]], writes=[ot.bs[1]])
                            yield
                            for _ in range(XY2):
                                yield
                            for hh in range(8):
                                kb.op(PE, lambda hh=hh: TE.transpose(out=X8()[:, hh, 0:ncol], in_=ot.t[0:ncol, hh * 128:(hh + 1) * 128],
                                                                     identity=ident_f.t[0:ncol, 0:ncol]),
                                      reads=[ot.bs[hh // 4], ident_f.b], writes=[Ob[hh // 4]], inc=(hh % 4 == 3))
                            yield
                            if NUDVE:
                                kb.op(DVE, lambda: V.tensor_scalar(out=nu.t[:, br, :, :], in0=X8()[:, :, 0:64], scalar1=1.0, scalar2=None, op0=ALU.mult), reads=Ob, writes=[nu.b])
                            else:
                                kb.op(ACT, lambda: A.copy(out=nu.t[:, br, :, :], in_=X8()[:, :, 0:64]), reads=Ob, writes=[nu.b])
                            yield
                            kb.op(DVE, lambda: V.tensor_scalar(out=de.t[:, br, :], in0=X8()[:, :, 64], scalar1=1e-30, scalar2=None, op0=ALU.max),
                                  reads=Ob, writes=[de.b])
                            yield

                        def branch(br, KT, VV, kts, qsrc, pre=False):
                            n = len(kts)

                            def scores(kt):
                                for g in range(2):
                                    kb.op(PE, lambda g=g: TE.matmul(S2()[:, g * 512:(g + 1) * 512], lhsT=KT.t[:, g, kt * 128:(kt + 1) * 128],
                                                                    rhs=qsrc.t[:, 4 * g:4 * g + 4, :], start=True, stop=True),
                                          reads=[KT.bs[kt], qsrc.b], writes=[Sb[g]])

                            if not pre:
                                scores(kts[0])
                                yield
                            for j, kt in enumerate(kts):
                                pt = PT[i2][ptc[i2] % NPT]
                                ptc[i2] += 1
                                for g in range(2):
                                    kb.op(ACT, lambda pt=pt, g=g: A.activation(out=pt.t[:, g * 512:(g + 1) * 512], in_=S2()[:, g * 512:(g + 1) * 512], func=AF.Exp),
                                          reads=[Sb[g]], writes=[pt.bs[g]])
                                yield
                                if j + 1 < n:
                                    for _ in range(XY1):
                                        yield
                                    scores(kts[j + 1])
                                    yield
                                if kt == c:
                                    kb.op(DVE, lambda pt=pt: V.tensor_tensor(out=pt.t[:], in0=pt.t[:], in1=dmask8.t[:].rearrange("p a b -> p (a b)"), op=ALU.mult),
                                          reads=pt.bs + [dmask8.b], writes=pt.bs)
                                    yield
                                elif br == 2 and kt == c - 4:
                                    kb.op(DVE, lambda pt=pt: V.tensor_tensor(out=pt.t[:], in0=pt.t[:], in1=tmask8.t[:].rearrange("p a b -> p (a b)"), op=ALU.mult),
                                          reads=pt.bs + [tmask8.b], writes=pt.bs)
                                    yield
                                for g in range(2):
                                    kb.op(PE, lambda kt=kt, pt=pt, j=j, g=g: TE.matmul(O2()[0:65, g * 512:(g + 1) * 512], lhsT=VV.t[:, kt, g, :],
                                                                                       rhs=pt.t[:, g * 512:(g + 1) * 512], start=(j == 0), stop=(j == n - 1)),
                                          reads=[pt.bs[g], VV.bs[kt]], writes=[Ob[g]], inc=(j == n - 1))
                                yield
                            yield from to_token_major(br, 65)

                        for k in range(KC):
                            kb.op(PE, lambda k=k: TE.matmul(pf(bZ), lhsT=hT.t[:, k, tok], rhs=wq.t[:, k, :], start=(k == 0), stop=(k == KC - 1)),
                                  reads=[hT.bs[c], wq.b], writes=[PS[bZ].b], inc=(k == KC - 1))
                        for k in range(KC):
                            kb.op(PE, lambda k=k: TE.matmul(pf(bG)[:, 0:24], lhsT=hT.t[:, k, tok], rhs=wg.t[:, k, :], start=(k == 0), stop=(k == KC - 1)),
                                  reads=[hT.bs[c], wg.b], writes=[PS[bG].b], inc=(k == KC - 1))
                        yield
                        sq = sqq[i2]
                        kb.op(ACT, lambda: A.activation(out=sq.t[:], in_=pf(bZ), func=AF.Square), reads=[PS[bZ].b], writes=[sq.b])
                        gt = gate[i2]
                        kb.op(ACT, lambda: A.activation(out=gt.t[:], in_=pf(bG)[:, 0:24], func=AF.Tanh, scale=0.5), reads=[PS[bG].b], writes=[gt.b])
                        yield
                        kb.op(DVE, lambda: V.tensor_scalar(out=gt.t[:], in0=gt.t[:], scalar1=0.5, scalar2=0.5, op0=ALU.mult, op1=ALU.add),
                              reads=[gt.b], writes=[gt.b])
                        qs = qst.t
                        qsb = [qst.bs[c]]
                        kb.op(DVE, lambda: V.tensor_reduce(out=qs[:, c, 0:8], in_=sq.t[:].rearrange("p (a b) -> p a b", b=64), axis=AX.X, op=ALU.add),
                              reads=[sq.b], writes=qsb)
                        yield
                        kb.op(DVE, lambda: V.tensor_scalar(out=qs[:, c, 8:16], in0=qs[:, c, 0:8], scalar1=1.0 / 64, scalar2=EPS, op0=ALU.mult, op1=ALU.add), reads=qsb, writes=qsb)
                        yield
                        kb.op(POOL, lambda: G.tensor_tensor(out=qs[:, c, 24:32], in0=qs[:, c, 8:16], in1=nhalf.t[:, 0:8], op=ALU.pow),
                              reads=qsb + [nhalf.b], writes=qsb)
                        yield
                        tq = tmpq[i2]
                        qa = qaug[i2]
                        kb.op(DVE, lambda: V.tensor_tensor(out=tq.t[:], in0=pf(bZ).rearrange("p (a b) -> p a b", b=64),
                                                           in1=qs[:, c, 24:32].unsqueeze(2).broadcast_to([128, 8, 64]), op=ALU.mult),
                              reads=[PS[bZ].b] + qsb, writes=[tq.b])
                        yield
                        kb.op(DVE, lambda: V.tensor_tensor(out=qa.t[:, :, 0:64], in0=tq.t[:], in1=gq.t[:].unsqueeze(1).broadcast_to([128, 8, 64]), op=ALU.mult),
                              reads=[tq.b, gq.b], writes=[qa.bs[0]])
                        kb.op(POOL, lambda: G.tensor_copy(out=qa.t[:, :, 96:100], in_=QAL.t[:, c, :, :]), reads=[QAL.b], writes=[qa.bs[1]])
                        yield
                        transposes([qa.t[:, h, :] for h in range(8)], bZ, qa.bs)
                        yield
                        q1 = qT[i2]
                        if QDVE:
                            kb.op(DVE, lambda: V.tensor_copy(out=q1.t[:], in_=pb(bZ).rearrange("p (a b) -> p a b", b=128)), reads=[PS[bZ].b], writes=[q1.b])
                        else:
                            kb.op(ACT, lambda: A.copy(out=q1.t[:], in_=pb(bZ).rearrange("p (a b) -> p a b", b=128)), reads=[PS[bZ].b], writes=[q1.b])
                        yield
                        nu, de = num[i2], den[i2]
                        rdc_, impn_, imp_, top8_ = rdc[i2], impn[i2], imp[i2], top8[i2]
                        sc_ = scl[i2]
                        pc = PT[i2][ptc[i2] % NPT]
                        ptc[i2] += 1
                        for g in range(2):
                            kb.op(PE, lambda g=g: TE.matmul(S2()[0:127, g * 512:(g + 1) * 512], lhsT=KcT.t[:, g, 0:127], rhs=q1.t[:, 4 * g:4 * g + 4, :], start=True, stop=True),
                                  reads=[KcT.b, q1.b], writes=[Sb[g]])
                        yield
                        for g in range(2):
                            kb.op(DVE, lambda g=g: V.scalar_tensor_tensor(out=sc_.t[0:127, g * 512:(g + 1) * 512].rearrange("p (a b) -> p a b", b=128),
                                                                          in0=S2()[0:127, g * 512:(g + 1) * 512].rearrange("p (a b) -> p a b", b=128), scalar=60.0,
                                                                          in1=cmask.t[0:127, tok].unsqueeze(1).broadcast_to([127, 4, 128]),
                                                                          op0=ALU.min, op1=ALU.add),
                                  reads=[Sb[g], cmask.b], writes=[sc_.bs[g]])
                        yield
                        if PREWIN:
                            kt0 = max(0, c - 4)
                            for g in range(2):
                                kb.op(PE, lambda g=g: TE.matmul(S2()[:, g * 512:(g + 1) * 512], lhsT=KT_win.t[:, g, kt0 * 128:(kt0 + 1) * 128],
                                                                rhs=q1.t[:, 4 * g:4 * g + 4, :], start=True, stop=True),
                                      reads=[KT_win.bs[kt0], q1.b], writes=[Sb[g]])
                            yield
                        for g in range(2):
                            kb.op(ACT, lambda g=g: A.activation(out=pc.t[0:127, g * 512:(g + 1) * 512], in_=sc_.t[0:127, g * 512:(g + 1) * 512], func=AF.Exp),
                                  reads=[sc_.bs[g]], writes=[pc.bs[g]])
                        yield
                        for g in range(2):
                            kb.op(PE, lambda g=g: TE.matmul(O2()[0:97, g * 512:(g + 1) * 512], lhsT=Vc.t[0:127, g, :], rhs=pc.t[0:127, g * 512:(g + 1) * 512], start=True, stop=True),
                                  reads=[pc.bs[g], Vc.b], writes=[Ob[g]])
                        yield
                        yield from to_token_major(0, 97)
                        kb.op(DVE, lambda: V.reciprocal(out=rdc_.t[:], in_=de.t[:, 0, :]), reads=[de.b], writes=[rdc_.b])
                        yield
                        kb.op(DVE, lambda: V.tensor_tensor(out=impn_.t[:], in0=X8()[:, :, 65:97],
                                                           in1=rdc_.t[:].unsqueeze(2).broadcast_to([128, 8, 32]), op=ALU.mult),
                              reads=Ob + [rdc_.b], writes=[impn_.b])
                        yield
                        q2 = qT2[i2]

                        def imp_chain():
                            kb.op(DVE, lambda: V.tensor_reduce(out=imp_.t[:], in_=impn_.t[:].rearrange("p (g r) j -> p g j r", g=2), axis=AX.X, op=ALU.add),
                                  reads=[impn_.b], writes=[imp_.b])
                            yield
                            kb.op(DVE, lambda: V.tensor_tensor(out=imp_.t[:], in0=imp_.t[:], in1=addc.t[:, c:c + 1, :].broadcast_to([128, 2, 32]), op=ALU.add),
                                  reads=[imp_.b, addc.b], writes=[imp_.b])
                            yield
                            for g in range(2):
                                kb.op(DVE, lambda g=g: V.max(out=top8_.t[:, g, :], in_=imp_.t[:, g, :]), reads=[imp_.b], writes=[top8_.bs[g]])
                            yield
                            for g in range(2):
                                kb.op(DVE, lambda g=g: V.tensor_scalar(out=qa.t[:, 4 * g:4 * g + 4, 64:96],
                                                                       in0=imp_.t[:, g:g + 1, :].broadcast_to([128, 4, 32]),
                                                                       scalar1=top8_.t[:, g, 7:8], scalar2=NEG, op0=ALU.is_lt, op1=ALU.mult),
                                      reads=[imp_.b, top8_.bs[g]], writes=[qa.bs[2 + g]])
                            yield

                        gw = branch(2, KT_win, V_win, list(range(max(0, c - 4), c + 1)), q1, pre=bool(PREWIN))
                        gi = imp_chain()
                        live = [gw, gi]
                        while live:
                            for g_ in list(live):
                                try:
                                    next(g_)
                                except StopIteration:
                                    live.remove(g_)
                            yield
                        for _ in range(XY3):
                            yield
                        transposes([qa.t[:, h, :] for h in range(8)], bZ, qa.bs)
                        if QDVE:
                            kb.op(DVE, lambda: V.tensor_copy(out=q2.t[:], in_=pb(bZ).rearrange("p (a b) -> p a b", b=128)), reads=[PS[bZ].b], writes=[q2.b])
                        else:
                            kb.op(ACT, lambda: A.copy(out=q2.t[:], in_=pb(bZ).rearrange("p (a b) -> p a b", b=128)), reads=[PS[bZ].b], writes=[q2.b])
                        yield
                        yield from branch(1, KT_slc, V_slc, list(range(0, c + 1)), q2)
                        rd_, coef_, oacc_, otmp_ = rd[i2], coef[i2], oacc[i2], otmp[i2]
                        kb.op(DVE, lambda: V.reciprocal(out=rd_.t[:], in_=de.t[:]), reads=[de.b], writes=[rd_.b])
                        yield
                        kb.op(DVE, lambda: V.tensor_tensor(out=coef_.t[:], in0=gt.t[:].rearrange("p (h b) -> p b h", b=3), in1=rd_.t[:], op=ALU.mult),
                              reads=[gt.b, rd_.b], writes=[coef_.b])
                        yield
                        kb.op(DVE, lambda: V.tensor_tensor(out=oacc_.t[:], in0=nu.t[:, 0], in1=coef_.t[:, 0, :].unsqueeze(2).broadcast_to([128, 8, 64]), op=ALU.mult),
                              reads=[nu.b, coef_.b], writes=[oacc_.b])
                        kb.op(POOL, lambda: G.tensor_tensor(out=otmp_.t[:], in0=nu.t[:, 1], in1=coef_.t[:, 1, :].unsqueeze(2).broadcast_to([128, 8, 64]), op=ALU.mult),
                              reads=[nu.b, coef_.b], writes=[otmp_.b])
                        yield
                        kb.op(DVE, lambda: V.tensor_tensor(out=oacc_.t[:], in0=oacc_.t[:], in1=otmp_.t[:], op=ALU.add), reads=[oacc_.b, otmp_.b], writes=[oacc_.b])
                        yield
                        kb.op(POOL, lambda: G.tensor_tensor(out=otmp_.t[:], in0=nu.t[:, 2], in1=coef_.t[:, 2, :].unsqueeze(2).broadcast_to([128, 8, 64]), op=ALU.mult),
                              reads=[nu.b, coef_.b], writes=[otmp_.b])
                        yield
                        yt = ytok[i2]
                        kb.op(DVE, lambda: V.tensor_tensor(out=yt.t[:], in0=oacc_.t[:].rearrange("p a b -> p (a b)"), in1=otmp_.t[:].rearrange("p a b -> p (a b)"), op=ALU.add),
                              reads=[oacc_.b, otmp_.b], writes=[yt.b])
                        if ydbg is not None:
                            kb.op(POOL, lambda: G.tensor_tensor(out=ydbg.t[:], in0=oacc_.t[:].rearrange("p a b -> p (a b)"), in1=otmp_.t[:].rearrange("p a b -> p (a b)"), op=ALU.add),
                                  reads=[oacc_.b, otmp_.b], writes=[ydbg.b])
                            dbg_store("y_nsa", ydbg.t[:], tok, [ydbg.b])
                        yield
                        for _ in range(XY3):
                            yield
                        transposes([yt.t[:, k * 128:(k + 1) * 128] for k in range(4)], bZ, [yt.b])
                        yield
                        kb.op(DVE, lambda: V.tensor_copy(out=ynsaT.t[:, :, tok], in_=pb(bZ)[:, 0:512].rearrange("p (a b) -> p a b", b=128)),
                              reads=[PS[bZ].b], writes=[ynsaT.bs[c]])
                        yield

                    run_interleaved([(lambda c=c: tile_gen(c)) for c in range(NT)], width=WIDTH)

            def phase_1d():
                with ExitStack() as s4:
                    wr = sb(s4, "wr", [128, KC, 2048], BF16, 4)
                    for j in range(4):
                        kb.dma(POOL, wr.t[:, :, j * 512:(j + 1) * 512], w_in_v[:, :, C_R + j * 512:C_R + (j + 1) * 512], writes=[wr.bs[j]])
                    for j in range(4):
                        kb.dma(POOL, wm.t[:, :, j * 512:(j + 1) * 512], w_in_v[:, :, C_M + j * 512:C_M + (j + 1) * 512], writes=[wm.bs[j]])
                    load_w(wbr, w_branch[0].rearrange("n (k p) d -> p (n k) d", p=128))
                    idT = sb(s4, "idT", [128, 4, 128], F32)
                    qdec = sb(s4, "qdec", [128, 4, 128], F32)
                    kdec = sb(s4, "kdec", [128, 4, 128], F32)
                    gn = sb(s4, "gn", [128, 512], F32)
                    eij = sb(s4, "eij", [128, 128], F32)
                    rowq = sb(s4, "rowq", [128, 128], F32)
                    rowk = sb(s4, "rowk", [128, 128], F32)
                    kb.dma(SP, gn.t[:], ret_gn_g.rearrange("o a b -> o (a b)").broadcast_to([128, 512]), writes=[gn.b])
                    kb.op(POOL, lambda: G.iota(eij.t[:], pattern=[[1, 128]], base=0, channel_multiplier=-1, allow_small_or_imprecise_dtypes=True), writes=[eij.b])
                    kb.op(POOL, lambda: G.iota(rowq.t[:], pattern=[[1, 128]], base=1, channel_multiplier=0, allow_small_or_imprecise_dtypes=True), writes=[rowq.b])
                    kb.op(POOL, lambda: G.iota(rowk.t[:], pattern=[[-1, 128]], base=127, channel_multiplier=0, allow_small_or_imprecise_dtypes=True), writes=[rowk.b])
                    lgs = [float(np.log(1.0 - 2.0 ** (-5.0 - h))) for h in range(4)]
                    cds = [float(np.exp(128.0 * np.float32(lg))) for lg in lgs]
                    cdt = sb(s4, "cdt", [128, 4], F32)
                    for h in range(4):
                        kb.op(ACT, lambda h=h: A.activation(out=cdt.t[:, h:h + 1], in_=rowq.t[:, 0:1], func=AF.Exp, scale=128.0 * float(np.float32(lgs[h]))),
                              reads=[rowq.b], writes=[cdt.b])
                    for h in range(4):
                        kb.op(ACT, lambda h=h: A.activation(out=idT.t[:, h, :], in_=eij.t[:], func=AF.Exp, scale=lgs[h]), reads=[eij.b], writes=[idT.b])
                        kb.op(POOL, lambda h=h: G.affine_select(out=idT.t[:, h, :], in_=idT.t[:, h, :], pattern=[[1, 128]], compare_op=ALU.is_ge, fill=0.0,
                                                                base=0, channel_multiplier=-1), reads=[idT.b], writes=[idT.b])
                        kb.op(ACT, lambda h=h: A.activation(out=qdec.t[:, h, :], in_=rowq.t[:], func=AF.Exp, scale=lgs[h]), reads=[rowq.b], writes=[qdec.b])
                        kb.op(ACT, lambda h=h: A.activation(out=kdec.t[:, h, :], in_=rowk.t[:], func=AF.Exp, scale=lgs[h]), reads=[rowk.b], writes=[kdec.b])
                    qTr = sb(s4, "qTr", [128, 4, 512], BF16)
                    qdT = sb(s4, "qdT", [128, 4, 512], BF16)
                    kTr = sb(s4, "kTr", [128, 4, 512], BF16)
                    kdT = sb(s4, "kdT", [128, 4, 512], BF16)
                    v_sb = [sb(s4, f"v_sb{i}", [128, 4, 128], BF16) for i in range(2)]
                    sgl = [sb(s4, f"sgl{i}", [128, 512], F32) for i in range(2)]
                    kd = [sb(s4, f"kd{i}", [128, 4, 128], BF16) for i in range(2)]
                    attb = [sb(s4, f"attb{i}", [128, 4, 128], BF16) for i in range(2)]
                    state_f = sb(s4, "state_f", [128, 4, 128], F32, 4)
                    state_b = sb(s4, "state_b", [128, 4, 128], BF16)
                    yr = [sb(s4, f"yr{i}", [128, 512], BF16) for i in range(2)]
                    yrdbg = sb(s4, "yrdbg", [128, 512], F32) if "y_ret" in dbg else None
                    kb.op(POOL, lambda: G.memset(state_f.t[:], 0.0), writes=state_f.bs)
                    kb.op(POOL, lambda: G.memset(state_b.t[:], 0.0), writes=[state_b.b])
                    KS = float(128.0 ** -0.5)
                    pcnt = [0]
                    state_done = [False] * (NT + 1)
                    stt_done = [False] * (NT + 1)
                    bst = [sb(s4, f"bst{i}", [128, 4, 6], F32, 4) for i in range(2)]
                    mv = [sb(s4, f"mv{i}", [128, 4, 2], F32, 4) for i in range(2)]
                    rs4 = [sb(s4, f"rs4{i}", [128, 12], F32) for i in range(2)]
                    on = [sb(s4, f"on{i}", [128, 512], F32, 4) for i in range(2)]

                    def p1d_gen(c):
                        i2 = c % 2
                        cl = c % 4
                        bA, bB, bT = 2 + 3 * i2, 3 + 3 * i2, 4 + 3 * i2
                        tok = slice(c * 128, (c + 1) * 128)
                        cs = slice(cl * 128, (cl + 1) * 128)
                        bst_, mv_, rs4_, on_ = bst[i2], mv[i2], rs4[i2], on[i2]
                        for k in range(KC):
                            kb.op(PE, lambda k=k: TE.matmul(pf(bA), lhsT=hT.t[:, k, tok], rhs=wr.t[:, k, 1024:1536], start=(k == 0), stop=(k == KC - 1)),
                                  reads=[hT.bs[c], wr.bs[2]], writes=[PS[bA].b], inc=(k == KC - 1))
                        yield
                        for k in range(KC):
                            kb.op(PE, lambda k=k: TE.matmul(pf(bB), lhsT=hT.t[:, k, tok], rhs=wr.t[:, k, 1536:2048], start=(k == 0), stop=(k == KC - 1)),
                                  reads=[hT.bs[c], wr.bs[3]], writes=[PS[bB].b], inc=(k == KC - 1))
                        yield
                        vs, sg_, kd_, ab = v_sb[i2], sgl[i2], kd[i2], attb[i2]
                        kb.op(ACT, lambda: A.copy(out=vs.t[:].rearrange("p a b -> p (a b)"), in_=pf(bA)), reads=[PS[bA].b], writes=[vs.b])
                        yield
                        kb.op(ACT, lambda: A.activation(out=sg_.t[:], in_=pf(bB), func=AF.Silu), reads=[PS[bB].b], writes=[sg_.b])
                        yield
                        kb.op(POOL, lambda: G.tensor_tensor(out=sg_.t[:], in0=sg_.t[:], in1=gn.t[:], op=ALU.mult), reads=[sg_.b, gn.b], writes=[sg_.b])
                        yield
                        for _ in range(XY3):
                            yield
                        transposes([kdT.t[:, h, cs] for h in range(4)], bT, [kdT.b])
                        yield
                        kb.op(ACT, lambda: A.copy(out=kd_.t[:].rearrange("p a b -> p (a b)"), in_=pb(bT)[:, 0:512]), reads=[PS[bT].b], writes=[kd_.b])
                        yield
                        for h in range(4):
                            kb.op(PE, lambda h=h: TE.matmul(pf(bA)[:, h * 128:(h + 1) * 128], lhsT=kTr.t[:, h, cs], rhs=qTr.t[:, h, cs], start=True, stop=True),
                                  reads=[kTr.b, qTr.b], writes=[PS[bA].b], inc=(h == 3))
                        yield
                        kb.op(DVE, lambda: V.tensor_tensor(out=ab.t[:], in0=pf(bA).rearrange("p (a b) -> p a b", b=128), in1=idT.t[:], op=ALU.mult),
                              reads=[PS[bA].b, idT.b], writes=[ab.b])
                        yield
                        if c < NT - 1:
                            for h in range(4):
                                kb.op(PE, lambda h=h: TE.matmul(pf(bT)[:, h * 128:(h + 1) * 128], lhsT=kd_.t[:, h, :], rhs=vs.t[:, h, :], start=True, stop=True),
                                      reads=[kd_.b, vs.b], writes=[PS[bT].b], inc=(h == 3))
                            yield
                            while c > 0 and not state_done[c - 1]:
                                yield
                            for h in range(4):
                                kb.op(DVE, lambda h=h: V.scalar_tensor_tensor(out=state_f.t[:, h, :], in0=state_f.t[:, h, :], scalar=cdt.t[:, h:h + 1],
                                                                              in1=pf(bT)[:, h * 128:(h + 1) * 128], op0=ALU.mult, op1=ALU.add),
                                      reads=[state_f.bs[h], PS[bT].b, cdt.b], writes=[state_f.bs[h]])
                        stt_done[c] = True
                        yield
                        while c > 0 and not state_done[c - 1]:
                            yield
                        for h in range(4):
                            kb.op(PE, lambda h=h: TE.matmul(pf(bB)[:, h * 128:(h + 1) * 128], lhsT=ab.t[:, h, :], rhs=vs.t[:, h, :], start=True, stop=(c == 0)),
                                  reads=[ab.b, vs.b], writes=[PS[bB].b], inc=(c == 0 and h == 3))
                            if c > 0:
                                kb.op(PE, lambda h=h: TE.matmul(pf(bB)[:, h * 128:(h + 1) * 128], lhsT=qdT.t[:, h, cs], rhs=state_b.t[:, h, :], start=False, stop=True),
                                      reads=[qdT.b, state_b.b], writes=[PS[bB].b], inc=(h == 3))
                        yield
                        if c < NT - 1:
                            kb.op(POOL, lambda: G.tensor_copy(out=state_b.t[:], in_=state_f.t[:]), reads=state_f.bs, writes=[state_b.b])
                        state_done[c] = True
                        yield
                        for h in range(4):
                            kb.op(DVE, lambda h=h: V.bn_stats(out=bst_.t[:, h, :], in_=pf(bB)[:, h * 128:(h + 1) * 128]), reads=[PS[bB].b], writes=[bst_.bs[h]])
                        yield
                        for h in range(4):
                            kb.op(DVE, lambda h=h: V.bn_aggr(out=mv_.t[:, h, :], in_=bst_.t[:, h, :]), reads=[bst_.bs[h]], writes=[mv_.bs[h]])
                        yield
                        kb.op(DVE, lambda: V.tensor_scalar(out=rs4_.t[:, 0:4], in0=mv_.t[:, :, 1], scalar1=EPS, scalar2=None, op0=ALU.add), reads=mv_.bs, writes=[rs4_.b])
                        yield
                        kb.op(POOL, lambda: G.tensor_tensor(out=rs4_.t[:, 8:12], in0=rs4_.t[:, 0:4], in1=nhalf.t[:, 0:4], op=ALU.pow),
                              reads=[rs4_.b, nhalf.b], writes=[rs4_.b])
                        yield
                        for h in range(4):
                            kb.op(DVE, lambda h=h: V.tensor_scalar(out=on_.t[:, h * 128:(h + 1) * 128], in0=pf(bB)[:, h * 128:(h + 1) * 128],
                                                                   scalar1=mv_.t[:, h, 0:1], scalar2=rs4_.t[:, 8 + h:9 + h], op0=ALU.subtract, op1=ALU.mult),
                                  reads=[PS[bB].b, mv_.bs[h], rs4_.b], writes=[on_.bs[h]])
                        yield
                        y_ = yr[i2]
                        kb.op(DVE, lambda: V.tensor_tensor(out=y_.t[:], in0=on_.t[:], in1=sg_.t[:], op=ALU.mult), reads=on_.bs + [sg_.b], writes=[y_.b])
                        if yrdbg is not None:
                            kb.op(DVE, lambda: V.tensor_tensor(out=yrdbg.t[:], in0=on_.t[:], in1=sg_.t[:], op=ALU.mult), reads=on_.bs + [sg_.b], writes=[yrdbg.b])
                            dbg_store("y_ret", yrdbg.t[:], tok, [yrdbg.b])
                        yield
                        for _ in range(XY3):
                            yield
                        transposes([y_.t[:, k * 128:(k + 1) * 128] for k in range(4)], bT, [y_.b])
                        yield
                        kb.op(ACT, lambda: A.copy(out=yretT.t[:, :, tok], in_=pb(bT)[:, 0:512].rearrange("p (a b) -> p a b", b=128)),
                              reads=[PS[bT].b], writes=[yretT.bs[c]])
                        yield

                    for tg in range(4):
                        tks = slice(tg * 512, (tg + 1) * 512)
                        hbs = [hT.bs[4 * tg + i] for i in range(4)]
                        for qk in range(2):
                            for h in range(4):
                                bk = pcnt[0] % 2
                                pcnt[0] += 1
                                for k in range(KC):
                                    kb.op(PE, lambda k=k, qk=qk, h=h, bk=bk: TE.matmul(pf(bk), lhsT=wr.t[:, k, qk * 512 + h * 128:qk * 512 + (h + 1) * 128],
                                                                                      rhs=hT.t[:, k, tks], start=(k == 0), stop=(k == KC - 1)),
                                          reads=hbs + [wr.bs[qk]], writes=[PS[bk].b], inc=(k == KC - 1))
                                pv4 = pf(bk).rearrange("p (a b) -> p a b", b=128)
                                if qk == 0:
                                    kb.op(ACT, lambda h=h, bk=bk: A.copy(out=qTr.t[:, h, :], in_=pf(bk)), reads=[PS[bk].b], writes=[qTr.b])
                                    kb.op(DVE, lambda h=h, pv4=pv4: V.tensor_tensor(out=qdT.t[:, h, :].rearrange("p (a b) -> p a b", b=128), in0=pv4,
                                                                                    in1=qdec.t[:, h:h + 1, :].broadcast_to([128, 4, 128]), op=ALU.mult),
                                          reads=[PS[bk].b, qdec.b], writes=[qdT.b])
                                else:
                                    kb.op(ACT, lambda h=h, bk=bk: A.mul(out=kTr.t[:, h, :], in_=pf(bk), mul=KS), reads=[PS[bk].b], writes=[kTr.b])
                                    kb.op(DVE, lambda h=h, pv4=pv4: V.scalar_tensor_tensor(out=kdT.t[:, h, :].rearrange("p (a b) -> p a b", b=128), in0=pv4, scalar=KS,
                                                                                           in1=kdec.t[:, h:h + 1, :].broadcast_to([128, 4, 128]),
                                                                                           op0=ALU.mult, op1=ALU.mult),
                                          reads=[PS[bk].b, kdec.b], writes=[kdT.b])
                        run_interleaved([(lambda c=c: p1d_gen(c)) for c in range(4 * tg, 4 * tg + 4)], width=2)

            def phase_1e():
                with ExitStack() as s5:
                    xt = [sb(s5, f"xte{i}", [128, D], F32) for i in range(2)]
                    g2bc = sb(s5, "g2bc", [128, D], F32)
                    kb.dma(SP, g2bc.t[:], norm2_g[0:1, :].broadcast_to([128, D]), writes=[g2bc.b])
                    wo = sb(s5, "wo", [128, KC, D], BF16)
                    load_w(wo, w_out[0].rearrange("(k p) d -> p k d", p=128))
                    gates = [sb(s5, f"gates{i}", [128, D], F32, 2) for i in range(2)]
                    tmix = [sb(s5, f"tmix{i}", [128, D], F32, 2) for i in range(2)]
                    tmix2 = [sb(s5, f"tmix2{i}", [128, D], F32, 2) for i in range(2)]
                    mixed = [sb(s5, f"mixed{i}", [128, D], BF16, 2) for i in range(2)]
                    mixT = [sb(s5, f"mixT{i}", [128, KC, 128], BF16) for i in range(2)]
                    x1t = [sb(s5, f"x1t{i}", [128, D], F32, 2) for i in range(2)]

                    def p1e_gen(c):
                        i2 = c % 2
                        bs_ = [4 * i2 + i for i in range(4)]
                        tok = slice(c * 128, (c + 1) * 128)
                        xx = xt[i2]
                        kb.dma(SP, xx.t[:], x[tok, :], writes=[xx.b])
                        gt_, tm = gates[i2], (tmix[i2], tmix2[i2])
                        for n, yT in ((0, ynsaT), (1, yretT)):
                            for half in range(2):
                                j = 2 * n + half
                                for k in range(KC):
                                    kb.op(PE, lambda j=j, k=k, half=half: TE.matmul(pf(bs_[half]), lhsT=hT.t[:, k, tok], rhs=wm.t[:, k, j * 512:(j + 1) * 512],
                                                                                    start=(k == 0), stop=(k == KC - 1)),
                                          reads=[hT.bs[c], wm.bs[j]], writes=[PS[bs_[half]].b], inc=(k == KC - 1))
                                yield
                            for half in range(2):
                                bk = bs_[2 + half]
                                for k in range(4):
                                    kb.op(PE, lambda n=n, half=half, k=k, bk=bk, yT=yT: TE.matmul(pf(bk), lhsT=yT.t[:, k, tok], rhs=wbr.t[:, n * 4 + k, half * 512:(half + 1) * 512],
                                                                                                  start=(k == 0), stop=(k == 3)),
                                          reads=[yT.bs[c], wbr.b], writes=[PS[bk].b], inc=(k == 3))
                                yield
                            for half in range(2):
                                kb.op(ACT, lambda half=half: A.activation(out=gt_.t[:, half * 512:(half + 1) * 512], in_=pf(bs_[half]), func=AF.Sigmoid),
                                      reads=[PS[bs_[half]].b], writes=[gt_.bs[half]])
                                yield
                            for half in range(2):
                                hs = slice(half * 512, (half + 1) * 512)
                                kb.op(DVE, lambda half=half, hs=hs, n=n: V.tensor_tensor(out=tm[n].t[:, hs], in0=gt_.t[:, hs], in1=pf(bs_[2 + half]), op=ALU.mult),
                                      reads=[gt_.bs[half], PS[bs_[2 + half]].b], writes=[tm[n].bs[half]])
                                yield
                        mx = mixed[i2]
                        kb.op(DVE, lambda: V.tensor_tensor(out=mx.t[:, 0:512], in0=tm[0].t[:, 0:512], in1=tm[1].t[:, 0:512], op=ALU.add), reads=[tm[0].bs[0], tm[1].bs[0]], writes=[mx.bs[0]])
                        kb.op(POOL, lambda: G.tensor_tensor(out=mx.t[:, 512:1024], in0=tm[0].t[:, 512:1024], in1=tm[1].t[:, 512:1024], op=ALU.add), reads=[tm[0].bs[1], tm[1].bs[1]], writes=[mx.bs[1]])
                        yield
                        for _ in range(XY3):
                            yield
                        transposes([mx.t[:, k * 128:(k + 1) * 128] for k in range(KC)], bs_[0], mx.bs)
                        yield
                        mt = mixT[i2]
                        kb.op(ACT, lambda: A.copy(out=mt.t[:], in_=pb(bs_[0]).rearrange("p (a b) -> p a b", b=128)), reads=[PS[bs_[0]].b], writes=[mt.b])
                        yield
                        for half in range(2):
                            for k in range(KC):
                                kb.op(PE, lambda half=half, k=k: TE.matmul(pf(bs_[2 + half]), lhsT=mt.t[:, k, :], rhs=wo.t[:, k, half * 512:(half + 1) * 512],
                                                                           start=(k == 0), stop=(k == KC - 1)),
                                      reads=[mt.b, wo.b], writes=[PS[bs_[2 + half]].b], inc=(k == KC - 1))
                            yield
                        x1 = x1t[i2]
                        for half in range(2):
                            hs = slice(half * 512, (half + 1) * 512)
                            kb.op(DVE, lambda half=half, hs=hs: V.tensor_tensor(out=x1.t[:, hs], in0=xx.t[:, hs], in1=pf(bs_[2 + half]), op=ALU.add),
                                  reads=[xx.b, PS[bs_[2 + half]].b], writes=[x1.bs[half]])
                            yield
                        kb.dma(SP, xmid[tok, :], x1.t[:], reads=x1.bs)
                        dbg_store("x1", x1.t[:], tok, x1.bs)
                        yield from norm_gen(x1, c, g2bc, i2, bs_[1])

                    run_interleaved([(lambda c=c: p1e_gen(c)) for c in range(NT)], width=2)

            def phase_2():
                with ExitStack() as s6:
                    xt = [sb(s6, f"xtf{i}", [128, D], F32) for i in range(2)]
                    wd = sb(s6, "wd", [128, NFB, D], BF16, 2)
                    wd_v = ffn_w_down[0].rearrange("(fb p) d -> p fb d", p=128)
                    wgs = [sb(s6, f"wgs{i}", [128, KC, 256], BF16) for i in range(2)]
                    wus = [sb(s6, f"wus{i}", [128, KC, 256], BF16) for i in range(2)]
                    act = sb(s6, "act", [128, NFB, 1024], BF16, NFB)
                    sgs = [sb(s6, f"sgs{i}", [128, 512], F32) for i in range(2)]
                    outt = [sb(s6, f"outt{i}", [128, D], F32, 2) for i in range(2)]
                    wg_v = ffn_w_gate[0].rearrange("(k p) f -> p k f", p=128)
                    wu_v = ffn_w_up[0].rearrange("(k p) f -> p k f", p=128)
                    cn = [0, 0]
                    for hf in range(2):
                        for fg in range(11):
                            cols = slice(fg * 256, (fg + 1) * 256)
                            wg_, wu_ = wgs[fg % 2], wus[fg % 2]
                            kb.dma(POOL, wg_.t[:], wg_v[:, :, cols], writes=[wg_.b])
                            kb.dma(POOL, wu_.t[:], wu_v[:, :, cols], writes=[wu_.b])
                            if hf == 0 and fg == 1:
                                kb.dma(POOL, wd.t[:, 0:11, :], wd_v[:, 0:11, :], writes=[wd.bs[0]])
                                kb.dma(POOL, wd.t[:, 11:22, :], wd_v[:, 11:22, :], writes=[wd.bs[1]])
                            for fl in range(2):
                                fb = fg * 2 + fl
                                for t2 in range(2):
                                    tokc = slice(hf * 1024 + t2 * 512, hf * 1024 + (t2 + 1) * 512)
                                    hbs = [hT.bs[hf * 8 + t2 * 4 + i] for i in range(4)]
                                    gb, ub = (0, 1) if cn[0] % 2 == 0 else (2, 3)
                                    cn[0] += 1
                                    for k in range(KC):
                                        kb.op(PE, lambda k=k, fl=fl, gb=gb, wg_=wg_, tokc=tokc: TE.matmul(pf(gb), lhsT=wg_.t[:, k, fl * 128:(fl + 1) * 128], rhs=hT.t[:, k, tokc],
                                                                                                          start=(k == 0), stop=(k == KC - 1)),
                                              reads=hbs + [wg_.b], writes=[PS[gb].b], inc=(k == KC - 1))
                                    for k in range(KC):
                                        kb.op(PE, lambda k=k, fl=fl, ub=ub, wu_=wu_, tokc=tokc: TE.matmul(pf(ub), lhsT=wu_.t[:, k, fl * 128:(fl + 1) * 128], rhs=hT.t[:, k, tokc],
                                                                                                          start=(k == 0), stop=(k == KC - 1)),
                                              reads=hbs + [wu_.b], writes=[PS[ub].b], inc=(k == KC - 1))
                                    sg_ = sgs[cn[0] % 2]
                                    kb.op(ACT, lambda sg_=sg_, gb=gb: A.activation(out=sg_.t[:], in_=pf(gb), func=AF.Silu), reads=[PS[gb].b], writes=[sg_.b])
                                    kb.op(DVE, lambda sg_=sg_, ub=ub, fb=fb, t2=t2: V.tensor_tensor(out=act.t[:, fb, t2 * 512:(t2 + 1) * 512], in0=sg_.t[:], in1=pf(ub), op=ALU.mult),
                                          reads=[sg_.b, PS[ub].b], writes=[act.bs[fb]])
                        for tl in range(8):
                            c = hf * 8 + tl
                            tok = slice(c * 128, (c + 1) * 128)
                            xx = xt[c % 2]
                            kb.dma(SP, xx.t[:], xmid[tok, :], writes=[xx.b])
                            ob = (4, 5) if cn[1] % 2 == 0 else (6, 7)
                            cn[1] += 1
                            for half in range(2):
                                for fb in range(NFB):
                                    kb.op(PE, lambda half=half, fb=fb, ob=ob, tl=tl: TE.matmul(pf(ob[half]), lhsT=act.t[:, fb, tl * 128:(tl + 1) * 128],
                                                                                               rhs=wd.t[:, fb, half * 512:(half + 1) * 512],
                                                                                               start=(fb == 0), stop=(fb == NFB - 1)),
                                          reads=[act.bs[fb], wd.bs[0 if fb < 11 else 1]], writes=[PS[ob[half]].b], inc=(fb == NFB - 1))
                            ot = outt[c % 2]
                            for half in range(2):
                                hs = slice(half * 512, (half + 1) * 512)
                                kb.op(DVE, lambda half=half, hs=hs, ob=ob, ot=ot, xx=xx: V.tensor_tensor(out=ot.t[:, hs], in0=xx.t[:, hs], in1=pf(ob[half]), op=ALU.add),
                                      reads=[xx.b, PS[ob[half]].b], writes=[ot.bs[half]])
                            kb.dma(SP, out[tok, :], ot.t[:], reads=ot.bs, is_out=True)

            with ExitStack() as sB:
                ynsaT = sb(sB, "ynsaT", [128, 4, S], BF16, NT)
                yretT = sb(sB, "yretT", [128, 4, S], BF16, NT)

                with ExitStack() as sA:
                    gq = sb(sA, "gq", [128, 64], F32)
                    gk = sb(sA, "gk", [128, 3, 64], F32)
                    QAL = sb(sA, "QAL", [128, NT, 8, 4], BF16)
                    KAL = sb(sA, "KAL", [128, NT, 4], BF16)
                    KCAL = sb(sA, "KCAL", [128, 4], BF16)
                    OH = sb(sA, "OH", [128, NT, 32], BF16)
                    dmask8 = sb(sA, "dmask8", [128, 8, 128], BF16)
                    tmask8 = sb(sA, "tmask8", [128, 8, 128], BF16)
                    cmask = sb(sA, "cmask", [128, S], BF16)
                    addc = sb(sA, "addc", [128, NT, 32], F32)
                    ov = sb(sA, "ov", [128, 32], BF16)
                    KT_slc = sb(sA, "KT_slc", [128, 2, S], BF16, NT)
                    KT_win = sb(sA, "KT_win", [128, 2, S], BF16, NT)
                    V_slc = sb(sA, "V_slc", [128, NT, 2, 65], BF16, NT)
                    V_win = sb(sA, "V_win", [128, NT, 2, 65], BF16, NT)
                    KcT = sb(sA, "KcT", [128, 2, 128], BF16)
                    Vc = sb(sA, "Vc", [128, 2, 97], BF16)

                    with ExitStack() as s0:
                        SL = sb(s0, "SL", [128, 8], F32)
                        th128 = sb(s0, "th128", [128, NT], F32)
                        pidx = sb(s0, "pidx", [128, 1], F32)
                        QALf = sb(s0, "QALf", [128, NT, 8, 4], F32)
                        KALf = sb(s0, "KALf", [128, NT, 4], F32)
                        KCALf = sb(s0, "KCALf", [128, 4], F32)
                        rel = sb(s0, "rel", [128, NT, 32], F32)
                        f0 = sb(s0, "f0", [128, NT, 32], F32)
                        f1 = sb(s0, "f1", [128, NT, 32], F32)
                        t1 = sb(s0, "t1", [128, NT, 32], F32)
                        hp = sb(s0, "hp", [128, 1], F32)
                        ovf = sb(s0, "ovf", [128, 32], F32)
                        ova = sb(s0, "ova", [128, 32], F32)
                        ones_b = sb(s0, "ones_b", [128, 512], BF16)
                        ones_b2 = sb(s0, "ones_b2", [128, 1024], BF16)
                        zeros_b = sb(s0, "zeros_b", [128, 512], BF16)

                        kb.dma(SP, gq.t[:], nsa_q_norm[0:1, :].broadcast_to([128, 64]), writes=[gq.b])
                        kb.dma(SP, gk.t[:].rearrange("p a b -> p (a b)"),
                               nsa_k_norm.rearrange("o a b -> o (a b)").broadcast_to([128, 192]), writes=[gk.b])
                        kb.op(DVE, lambda: V.tensor_scalar(out=gq.t[:], in0=gq.t[:], scalar1=0.125, scalar2=None, op0=ALU.mult),
                              reads=[gq.b], writes=[gq.b])
                        for h in range(8):
                            kb.op(POOL, lambda h=h: G.memset(SL.t[:, h:h + 1], 2.0 ** (-(h + 1))), writes=[SL.b])
                        kb.op(POOL, lambda: G.iota(th128.t[:], pattern=[[128, NT]], base=0, channel_multiplier=0,
                                                   allow_small_or_imprecise_dtypes=True), writes=[th128.b])
                        kb.op(POOL, lambda: G.iota(pidx.t[:], pattern=[[0, 1]], base=0, channel_multiplier=1,
                                                   allow_small_or_imprecise_dtypes=True), writes=[pidx.b])
                        SLb = SL.t[:].unsqueeze(1).broadcast_to([128, NT, 8])
                        THb = th128.t[:].unsqueeze(2).broadcast_to([128, NT, 8])
                        kb.op(DVE, lambda: V.scalar_tensor_tensor(out=QALf.t[:, :, :, 0], in0=THb, scalar=-1.0, in1=SLb,
                                                                  op0=ALU.mult, op1=ALU.mult),
                              reads=[SL.b, th128.b], writes=[QALf.b])
                        kb.op(DVE, lambda: V.tensor_scalar(out=QALf.t[:, :, :, 1], in0=SLb, scalar1=pidx.t[:, 0:1], scalar2=-1.0,
                                                           op0=ALU.mult, op1=ALU.mult),
                              reads=[SL.b, pidx.b], writes=[QALf.b])
                        kb.op(DVE, lambda: V.tensor_copy(out=QALf.t[:, :, :, 2], in_=SLb), reads=[SL.b], writes=[QALf.b])
                        kb.op(DVE, lambda: V.tensor_copy(out=QALf.t[:, :, :, 3], in_=SLb), reads=[SL.b], writes=[QALf.b])
                        kb.op(DVE, lambda: V.tensor_copy(out=QAL.t[:], in_=QALf.t[:]), reads=[QALf.b], writes=[QAL.b])
                        kb.op(POOL, lambda: G.memset(KALf.t[:, :, 0:2], 1.0), writes=[KALf.b])
                        kb.op(DVE, lambda: V.tensor_copy(out=KALf.t[:, :, 2], in_=th128.t[:]), reads=[th128.b], writes=[KALf.b])
                        kb.op(DVE, lambda: V.tensor_copy(out=KALf.t[:, :, 3], in_=pidx.t[:, 0:1].broadcast_to([128, NT])),
                              reads=[pidx.b], writes=[KALf.b])
                        kb.op(DVE, lambda: V.tensor_copy(out=KAL.t[:], in_=KALf.t[:]), reads=[KALf.b], writes=[KAL.b])
                        kb.op(POOL, lambda: G.memset(KCALf.t[:, 0:2], 1.0), writes=[KCALf.b])
                        kb.op(POOL, lambda: G.memset(KCALf.t[:, 3:4], 31.0), reads=[], writes=[KCALf.b])
                        kb.op(DVE, lambda: V.tensor_scalar(out=KCALf.t[:, 2:3], in0=pidx.t[:, 0:1], scalar1=16.0, scalar2=None,
                                                           op0=ALU.mult), reads=[pidx.b], writes=[KCALf.b])
                        kb.op(DVE, lambda: V.tensor_copy(out=KCAL.t[:], in_=KCALf.t[:]), reads=[KCALf.b], writes=[KCAL.b])
                        kb.op(POOL, lambda: G.memset(OH.t[:], 0.0), writes=[OH.b])
                        for kt in range(NT):
                            kb.op(POOL, lambda kt=kt: G.memset(OH.t[0:64, kt, 2 * kt:2 * kt + 1], 1.0), writes=[OH.b])
                            kb.op(POOL, lambda kt=kt: G.memset(OH.t[64:128, kt, 2 * kt + 1:2 * kt + 2], 1.0), writes=[OH.b])
                        kb.op(POOL, lambda: G.memset(ones_b.t[:], 1.0), writes=[ones_b.b])
                        kb.op(POOL, lambda: G.memset(zeros_b.t[:], 0.0), writes=[zeros_b.b])
                        ob8 = ones_b2.t[:].rearrange("p (a b) -> p a b", b=128)
                        kb.op(POOL, lambda: G.memset(ones_b2.t[:], 1.0), writes=[ones_b2.b])
                        kb.op(POOL, lambda: G.affine_select(out=dmask8.t[:], in_=ob8, pattern=[[0, 8], [1, 128]],
                                                            compare_op=ALU.is_ge, fill=0.0, base=0, channel_multiplier=-1),
                              reads=[ones_b2.b], writes=[dmask8.b])
                        kb.op(POOL, lambda: G.affine_select(out=tmask8.t[:], in_=ob8, pattern=[[0, 8], [-1, 128]],
                                                            compare_op=ALU.is_gt, fill=0.0, base=0, channel_multiplier=1),
                              reads=[ones_b2.b], writes=[tmask8.b])
                        for i in range(4):
                            kb.op(POOL, lambda i=i: G.affine_select(out=cmask.t[:, i * 512:(i + 1) * 512], in_=zeros_b.t[:],
                                                                    pattern=[[1, 512]], compare_op=ALU.is_ge, fill=NEG,
                                                                    base=-31 + 512 * i, channel_multiplier=-16),
                                  reads=[zeros_b.b], writes=[cmask.b])
                        kb.op(POOL, lambda: G.iota(rel.t[:], pattern=[[-2, NT], [1, 32]], base=0, channel_multiplier=0,
                                                   allow_small_or_imprecise_dtypes=True), writes=[rel.b])
                        kb.op(DVE, lambda: V.tensor_scalar(out=hp.t[:], in0=pidx.t[:], scalar1=64.0, scalar2=None, op0=ALU.is_ge),
                              reads=[pidx.b], writes=[hp.b])
                        kb.op(DVE, lambda: V.tensor_scalar(out=rel.t[:], in0=rel.t[:], scalar1=hp.t[:, 0:1], scalar2=None,
                                                           op0=ALU.subtract), reads=[rel.b, hp.b], writes=[rel.b])
                        kb.op(DVE, lambda: V.tensor_scalar(out=t1.t[:], in0=rel.t[:], scalar1=0.0, scalar2=-1e9,
                                                           op0=ALU.is_gt, op1=ALU.mult), reads=[rel.b], writes=[t1.b])
                        kb.op(DVE, lambda: V.tensor_scalar(out=f0.t[:], in0=rel.t[:], scalar1=0.0, scalar2=None, op0=ALU.is_equal),
                              reads=[rel.b], writes=[f0.b])
                        kb.op(DVE, lambda: V.tensor_scalar(out=f1.t[:], in0=rel.t[:], scalar1=-1.0, scalar2=None, op0=ALU.is_equal),
                              reads=[rel.b], writes=[f1.b])
                        kb.op(DVE, lambda: V.tensor_tensor(out=f0.t[:], in0=f0.t[:], in1=f1.t[:], op=ALU.max),
                              reads=[f0.b, f1.b], writes=[f0.b])
                        kb.op(DVE, lambda: V.memset(f0.t[:, :, 0:1], 1.0), reads=[], writes=[f0.b])
                        kb.op(DVE, lambda: V.scalar_tensor_tensor(out=addc.t[:], in0=f0.t[:], scalar=1e4, in1=t1.t[:],
                                                                  op0=ALU.mult, op1=ALU.add), reads=[f0.b, t1.b], writes=[addc.b])
                        kb.op(POOL, lambda: G.iota(ovf.t[:], pattern=[[-64, 32]], base=0, channel_multiplier=16,
                                                   allow_small_or_imprecise_dtypes=True), writes=[ovf.b])
                        kb.op(DVE, lambda: V.tensor_scalar(out=ova.t[:], in0=ovf.t[:], scalar1=63.0, scalar2=None, op0=ALU.is_le),
                              reads=[ovf.b], writes=[ova.b])
                        kb.op(DVE, lambda: V.tensor_scalar(out=ovf.t[:], in0=ovf.t[:], scalar1=-31.0, scalar2=None, op0=ALU.is_ge),
                              reads=[ovf.b], writes=[ovf.b])
                        kb.op(DVE, lambda: V.tensor_tensor(out=ov.t[:], in0=ova.t[:], in1=ovf.t[:], op=ALU.mult),
                              reads=[ova.b, ovf.b], writes=[ov.b])
                        kb.op(POOL, lambda: G.memset(V_slc.t[:, :, :, 64:65], 1.0), writes=V_slc.bs)
                        kb.op(POOL, lambda: G.memset(V_win.t[:, :, :, 64:65], 1.0), writes=V_win.bs)
                        kb.barrier()
                        chk(1)

                    with ExitStack() as s2:
                        cmpT = sb(s2, "cmpT", [128, 2, S], BF16, NT)
                        w1sb = sb(s2, "w1sb", [128, 2, 32, 128], BF16)
                        w2sb = sb(s2, "w2sb", [128, 2, 64], BF16)
                        pe_sb = sb(s2, "pe_sb", [32, 2, 64], F32)
                        with ExitStack() as s2a:
                            xt = [sb(s2a, f"xta{i}", [128, D], F32) for i in range(2)]
                            g1bc = sb(s2a, "g1bc", [128, D], F32)
                            kb.dma(SP, g1bc.t[:], norm1_g[0:1, :].broadcast_to([128, D]), writes=[g1bc.b])

                            def p1a_gen(c):
                                xx = xt[c % 2]
                                kb.dma(SP, xx.t[:], x[c * 128:(c + 1) * 128, :], writes=[xx.b])
                                yield
                                yield from norm_gen(xx, c, g1bc, c % 2, 6 + (c % 2))

                            wkv = sb(s2a, "wkv", [128, KC, 768], BF16)
                            load_w(wkv, w_in_v[:, :, C_KV:C_KV + 768])
                            for kv in range(2):
                                src = cmp_w1[0, kv].rearrange("l d f -> d l f")
                                kb.dma(POOL, w1sb.t[0:64, kv], src, writes=[w1sb.b])
                                kb.dma(POOL, w1sb.t[64:128, kv], src, writes=[w1sb.b])
                            kb.dma(POOL, w2sb.t[:], cmp_w2[0].rearrange("k f d -> f k d"), writes=[w2sb.b])
                            kb.dma(SP, pe_sb.t[:], cmp_pe[0].rearrange("k l d -> l k d"), writes=[pe_sb.b])
                            cmp_tok = [sb(s2a, f"cmp_tok{i}", [128, 256], BF16) for i in range(2)]
                            sqk = [sb(s2a, f"sqk{i}", [128, 256], F32) for i in range(2)]
                            kst = sb(s2a, "kst", [128, NT, 16], F32, NT)
                            tmpk = [sb(s2a, f"tmpk{i}", [128, 4, 64], F32) for i in range(2)]
                            ka_slc = [sb(s2a, f"ka_slc{i}", [128, 2, 128], BF16, 2) for i in range(2)]
                            ka_win = [sb(s2a, f"ka_win{i}", [128, 2, 128], BF16, 2) for i in range(2)]
                            for i in range(2):
                                kb.op(POOL, lambda i=i: G.memset(ka_slc[i].t[:], 0.0), writes=ka_slc[i].bs)
                                kb.op(POOL, lambda i=i: G.memset(ka_win[i].t[:], 0.0), writes=ka_win[i].bs)
                            def p1b_gen(c):
                                i2 = c % 2
                                bA, bB = (0, 1) if i2 == 0 else (2, 3)
                                tok = slice(c * 128, (c + 1) * 128)
                                if CUT >= 1:
                                    yield
                                    for k in range(KC):
                                        kb.op(PE, lambda k=k: TE.matmul(pf(bA), lhsT=hT.t[:, k, tok], rhs=wkv.t[:, k, 0:512],
                                                                        start=(k == 0), stop=(k == KC - 1)),
                                              reads=[hT.bs[c], wkv.b], writes=[PS[bA].b], inc=(k == KC - 1))
                                    for k in range(KC):
                                        kb.op(PE, lambda k=k: TE.matmul(pf(bB)[:, 0:256], lhsT=hT.t[:, k, tok], rhs=wkv.t[:, k, 512:768],
                                                                        start=(k == 0), stop=(k == KC - 1)),
                                              reads=[hT.bs[c], wkv.b], writes=[PS[bB].b], inc=(k == KC - 1))
                                if CUT >= 2:
                                    yield
                                    ct = cmp_tok[i2]
                                    kb.op(ACT, lambda: A.copy(out=ct.t[:], in_=pf(bA)[:, 0:256]), reads=[PS[bA].b], writes=[ct.b])
                                    sq = sqk[i2]
                                    kb.op(ACT, lambda: A.activation(out=sq.t[:, 0:128], in_=pf(bA)[:, 256:384], func=AF.Square),
                                          reads=[PS[bA].b], writes=[sq.b])
                                    kb.op(ACT, lambda: A.activation(out=sq.t[:, 128:256], in_=pf(bB)[:, 0:128], func=AF.Square),
                                          reads=[PS[bB].b], writes=[sq.b])
                                if CUT >= 3:
                                    yield
                                    ks = kst.t
                                    ksb = [kst.bs[c]]
                                    kb.op(DVE, lambda: V.tensor_reduce(out=ks[:, c, 0:4], in_=sq.t[:].rearrange("p (a b) -> p a b", b=64),
                                                                       axis=AX.X, op=ALU.add), reads=[sq.b], writes=ksb)
                                    rstd_from_ss(ks[:, c, 0:4], ks[:, c, 4:8], ks[:, c, 8:12], ks[:, c, 12:16], 64, ksb)
                                    tk = tmpk[i2]
                                    kb.op(DVE, lambda: V.tensor_tensor(out=tk.t[:, 0:2, :], in0=pf(bA)[:, 256:384].rearrange("p (a b) -> p a b", b=64),
                                                                       in1=ks[:, c, 12:14].unsqueeze(2).broadcast_to([128, 2, 64]), op=ALU.mult),
                                          reads=[PS[bA].b] + ksb, writes=[tk.b])
                                    kb.op(DVE, lambda: V.tensor_tensor(out=tk.t[:, 2:4, :], in0=pf(bB)[:, 0:128].rearrange("p (a b) -> p a b", b=64),
                                                                       in1=ks[:, c, 14:16].unsqueeze(2).broadcast_to([128, 2, 64]), op=ALU.mult),
                                          reads=[PS[bB].b] + ksb, writes=[tk.b])
                                    ksl, kwn = ka_slc[i2], ka_win[i2]
                                    kb.op(DVE, lambda: V.tensor_tensor(out=ksl.t[:, :, 0:64], in0=tk.t[:, 0:2, :],
                                                                       in1=gk.t[:, 1:2, :].broadcast_to([128, 2, 64]), op=ALU.mult),
                                          reads=[tk.b, gk.b], writes=[ksl.bs[0]])
                                    kb.op(DVE, lambda: V.tensor_tensor(out=kwn.t[:, :, 0:64], in0=tk.t[:, 2:4, :],
                                                                       in1=gk.t[:, 2:3, :].broadcast_to([128, 2, 64]), op=ALU.mult),
                                          reads=[tk.b, gk.b], writes=[kwn.bs[0]])
                                if CUT >= 4:
                                    yield
                                    kb.op(POOL, lambda: G.tensor_copy(out=ksl.t[:, :, 64:96], in_=OH.t[:, c:c + 1, :].broadcast_to([128, 2, 32])),
                                          reads=[OH.b], writes=[ksl.bs[1]])
                                    kb.op(POOL, lambda: G.tensor_copy(out=ksl.t[:, :, 96:100], in_=KAL.t[:, c:c + 1, :].broadcast_to([128, 2, 4])),
                                          reads=[KAL.b], writes=[ksl.bs[1]])
                                    kb.op(POOL, lambda: G.tensor_copy(out=kwn.t[:, :, 96:100], in_=KAL.t[:, c:c + 1, :].broadcast_to([128, 2, 4])),
                                          reads=[KAL.b], writes=[kwn.bs[1]])
                                if CUT >= 5:
                                    yield
                                    kb.op(ACT, lambda: A.copy(out=V_slc.t[:, c, :, 0:64], in_=pf(bA)[:, 384:512].rearrange("p (a b) -> p a b", b=64)),
                                          reads=[PS[bA].b], writes=[V_slc.bs[c]])
                                    kb.op(ACT, lambda: A.copy(out=V_win.t[:, c, :, 0:64], in_=pf(bB)[:, 128:256].rearrange("p (a b) -> p a b", b=64)),
                                          reads=[PS[bB].b], writes=[V_win.bs[c]])
                                if CUT >= 6:
                                    yield
                                    tb = 4 + i2
                                    for _ in range(XY3):
                                        yield
                                    transposes([ksl.t[:, 0, :], ksl.t[:, 1, :], kwn.t[:, 0, :], kwn.t[:, 1, :], ct.t[:, 0:128], ct.t[:, 128:256]],
                                               tb, ksl.bs + kwn.bs + [ct.b])
                                    pv3 = pb(tb).rearrange("p (a b) -> p a b", b=128)
                                    kb.op(ACT, lambda: A.copy(out=KT_slc.t[:, :, tok], in_=pv3[:, 0:2, :]), reads=[PS[tb].b], writes=[KT_slc.bs[c]])
                                    kb.op(ACT, lambda: A.copy(out=KT_win.t[:, :, tok], in_=pv3[:, 2:4, :]), reads=[PS[tb].b], writes=[KT_win.bs[c]])
                                    kb.op(ACT, lambda: A.copy(out=cmpT.t[:, :, tok], in_=pv3[:, 4:6, :]), reads=[PS[tb].b], writes=[cmpT.bs[c]])
                            def p1ab_gen(c):
                                yield from p1a_gen(c)
                                yield from p1b_gen(c)

                            run_interleaved([(lambda c=c: p1ab_gen(c)) for c in range(NT)], width=2)
                            if "KT_slc" in dbg:
                                kdb = sb(s2a, "kdb", [128, 2, S], F32)
                                kb.op(DVE, lambda: V.tensor_copy(out=kdb.t[:], in_=KT_slc.t[:]), reads=KT_slc.bs, writes=[kdb.b])
                                kb.dma(SP, dbg["KT_slc"].rearrange("p (a b) -> p a b", b=S), kdb.t[:], reads=[kdb.b], is_out=True)
                            kb.barrier()
                            chk(3)

                        with ExitStack() as s2b:
                            peT = sb(s2b, "peT", [64, 2, 32], BF16)
                            bias_c = sb(s2b, "bias_c", [128, 2], F32)
                            xhs = [sb(s2b, f"xh{i}", [128, 128], F32) for i in range(4)]
                            x2s = [sb(s2b, f"x2{i}", [128, 128], F32) for i in range(4)]
                            sgs_ = [sb(s2b, f"sgc{i}", [128, 128], F32) for i in range(4)]
                            HTbs = [sb(s2b, f"HTb{i}", [128, 128], BF16) for i in range(4)]
                            kca = sb(s2b, "kca", [128, 2, 128], BF16)
                            cst = sb(s2b, "cst", [128, 8], F32)
                            tmpcs = [sb(s2b, f"tmpc{i}", [128, 64], F32) for i in range(4)]
                            kb.op(POOL, lambda: G.memset(kca.t[:], 0.0), writes=[kca.b])
                            kb.op(POOL, lambda: G.memset(Vc.t[:], 0.0), writes=[Vc.b])
                            kb.op(POOL, lambda: G.tensor_copy(out=kca.t[:, :, 96:100], in_=KCAL.t[:].unsqueeze(1).broadcast_to([128, 2, 4])),
                                  reads=[KCAL.b], writes=[kca.b])
                            kb.op(POOL, lambda: G.memset(Vc.t[:, :, 64:65], 1.0), writes=[Vc.b])
                            kb.op(POOL, lambda: G.tensor_copy(out=Vc.t[:, :, 65:97], in_=ov.t[:].unsqueeze(1).broadcast_to([128, 2, 32])),
                                  reads=[ov.b], writes=[Vc.b])
                            for kv in range(2):
                                kb.op(PE, lambda kv=kv: TE.transpose(out=pf(0)[0:64, kv * 32:(kv + 1) * 32], in_=pe_sb.t[0:32, kv, :],
                                                                     identity=ident_f.t[0:32, 0:32]),
                                      reads=[pe_sb.b, ident_f.b], writes=[PS[0].b])
                            kb.op(DVE, lambda: V.tensor_copy(out=peT.t[:], in_=pf(0)[0:64, 0:64].rearrange("p (a b) -> p a b", b=32)),
                                  reads=[PS[0].b], writes=[peT.b])
                            for kv in range(2):
                                for l in range(32):
                                    kb.op(PE, lambda kv=kv, l=l: TE.matmul(pf(1)[:, kv:kv + 1], lhsT=w1sb.t[0:64, kv, l, :], rhs=peT.t[0:64, kv, l:l + 1],
                                                                           start=(l == 0), stop=(l == 31)),
                                          reads=[w1sb.b, peT.b], writes=[PS[1].b], inc=(l == 31))
                            kb.op(DVE, lambda: V.tensor_copy(out=bias_c.t[:], in_=pf(1)[:, 0:2]), reads=[PS[1].b], writes=[bias_c.b])
                            def cmp_gen(kv, g, idx):
                                bH, bO = 2 * idx, 2 * idx + 1
                                xh, x2, sg, HTb, tmpc = xhs[idx], x2s[idx], sgs_[idx], HTbs[idx], tmpcs[idx]
                                for l in range(32):
                                    kb.op(PE, lambda kv=kv, g=g, l=l: TE.matmul(
                                        pf(bH)[:, 0:127], lhsT=w1sb.t[g * 64:(g + 1) * 64, kv, l, :],
                                        rhs=cmpT.t[g * 64:(g + 1) * 64, kv, l:l + 16 * 126 + 1:16],
                                        start=(l == 0), stop=(l == 31)),
                                        reads=[w1sb.b] + cmpT.bs, writes=[PS[bH].b], inc=(l == 31))
                                yield
                                kb.op(ACT, lambda kv=kv: A.activation(out=xh.t[:, 0:127], in_=pf(bH)[:, 0:127], func=AF.Identity,
                                                                      bias=bias_c.t[:, kv:kv + 1], scale=1.0),
                                      reads=[PS[bH].b, bias_c.b], writes=[xh.b])
                                yield
                                kb.op(DVE, lambda: V.tensor_tensor(out=x2.t[:, 0:127], in0=xh.t[:, 0:127], in1=xh.t[:, 0:127], op=ALU.mult),
                                      reads=[xh.b], writes=[x2.b])
                                yield
                                kb.op(DVE, lambda: V.tensor_scalar(out=x2.t[:, 0:127], in0=x2.t[:, 0:127], scalar1=0.044715, scalar2=1.0,
                                                                   op0=ALU.mult, op1=ALU.add), reads=[x2.b], writes=[x2.b])
                                yield
                                kb.op(DVE, lambda: V.tensor_tensor(out=x2.t[:, 0:127], in0=x2.t[:, 0:127], in1=xh.t[:, 0:127], op=ALU.mult),
                                      reads=[x2.b, xh.b], writes=[x2.b])
                                yield
                                kb.op(ACT, lambda: A.activation(out=sg.t[:, 0:127], in_=x2.t[:, 0:127], func=AF.Sigmoid, scale=1.5957691216057308),
                                      reads=[x2.b], writes=[sg.b])
                                yield
                                kb.op(DVE, lambda: V.tensor_tensor(out=HTb.t[:, 0:127], in0=xh.t[:, 0:127], in1=sg.t[:, 0:127], op=ALU.mult),
                                      reads=[xh.b, sg.b], writes=[HTb.b])
                                yield
                                kb.op(PE, lambda kv=kv: TE.matmul(pf(bO)[0:127, 0:64], lhsT=HTb.t[:, 0:127], rhs=w2sb.t[:, kv, :], start=True, stop=True),
                                      reads=[HTb.b, w2sb.b], writes=[PS[bO].b])
                                yield
                                if kv == 0:
                                    kb.op(ACT, lambda g=g: A.activation(out=tmpc.t[0:127, :], in_=pf(bO)[0:127, 0:64], func=AF.Square,
                                                                        accum_out=cst.t[0:127, g:g + 1]),
                                          reads=[PS[bO].b], writes=[tmpc.b, cst.b])
                                    rstd_from_ss(cst.t[0:127, g:g + 1], cst.t[0:127, 2 + g:3 + g], cst.t[0:127, 4 + g:5 + g], cst.t[0:127, 6 + g:7 + g], 64, [cst.b])
                                    kb.op(DVE, lambda g=g: V.scalar_tensor_tensor(out=kca.t[0:127, g, 0:64], in0=pf(bO)[0:127, 0:64],
                                                                                  scalar=cst.t[0:127, 6 + g:7 + g], in1=gk.t[0:127, 0, :],
                                                                                  op0=ALU.mult, op1=ALU.mult),
                                          reads=[PS[bO].b, cst.b, gk.b], writes=[kca.b])
                                else:
                                    kb.op(ACT, lambda g=g: A.copy(out=Vc.t[0:127, g, 0:64], in_=pf(bO)[0:127, 0:64]),
                                          reads=[PS[bO].b], writes=[Vc.b])
                                yield

                            run_interleaved([(lambda kv=kv, g=g: cmp_gen(kv, g, 2 * kv + g)) for kv in range(2) for g in range(2)], width=4)
                            transposes([kca.t[:, 0, :], kca.t[:, 1, :]], 6, [kca.b])
                            kb.op(ACT, lambda: A.copy(out=KcT.t[:], in_=pb(6)[:, 0:256].rearrange("p (a b) -> p a b", b=128)),
                                  reads=[PS[6].b], writes=[KcT.b])
                            if "kc" in dbg:
                                kcd = sb(s2b, "kcd", [128, 2, 64], F32)
                                kb.op(DVE, lambda: V.tensor_copy(out=kcd.t[:], in_=kca.t[:, :, 0:64]), reads=[kca.b], writes=[kcd.b])
                                kb.dma(SP, dbg["kc"].rearrange("p (a b) -> p a b", b=64), kcd.t[:], reads=[kcd.b], is_out=True)
                            if "vc" in dbg:
                                vcd = sb(s2b, "vcd", [128, 2, 64], F32)
                                kb.op(DVE, lambda: V.tensor_copy(out=vcd.t[:], in_=Vc.t[:, :, 0:64]), reads=[Vc.b], writes=[vcd.b])
                                kb.dma(SP, dbg["vc"].rearrange("p (a b) -> p a b", b=64), vcd.t[:], reads=[vcd.b], is_out=True)
                            kb.barrier()
                            chk(4)

                    phase_1c()
                    kb.barrier()
                    chk(5)

                wm = sb(sB, "wm", [128, KC, 2048], BF16, 4)
                wbr = sb(sB, "wbr", [128, 8, D], BF16)
                phase_1d()
                kb.barrier()
                chk(6)
                phase_1e()
                kb.barrier()
                chk(7)

            phase_2()
            kb.finish()
    except _Stop:
        pass
    return nc


_NAMES = ["x", "norm1_g", "w_in", "nsa_q_norm", "nsa_k_norm", "cmp_pe", "cmp_w1", "cmp_w2", "ret_gn_g",
          "w_branch", "w_out", "norm2_g", "ffn_w_gate", "ffn_w_up", "ffn_w_down"]


def kernel(**inputs):
    n = 8
    arrs = {k: np.ascontiguousarray(np.asarray(inputs[k], dtype=np.float32)) for k in _NAMES}
    nc = build_nc()
    in_maps = []
    for i in range(n):
        m = {k: arrs[k] for k in _NAMES if k != "x"}
        m["x"] = np.ascontiguousarray(arrs["x"][i])
        in_maps.append(m)
    res = run_bass_kernel_spmd(nc, in_maps, core_ids=list(range(n)))
    return np.stack([np.asarray(r["out"], dtype=np.float32) for r in res.results], axis=0)
```
